# Optimizing a Trainium2 kernel written in Bass

```python
import math
import jax, jax.numpy as jnp
from jax import lax
import numpy as np

D_MODEL = 1024
BATCH = 1
SEQ = 16384
DEPTH = 2
DEC_BATCH = 8
DEC_SEQ = 2048
PAST_LEN = 128

N_META = 16
D_HYENA = D_MODEL // 2
D_CONF = D_MODEL - D_HYENA
D_IN_EVEN = 3 * D_HYENA + 2 * D_CONF
SHORT_K = 3
CONF_K = 31
FILTER_EMB = 33
FILTER_HIDDEN = 64
N_HEADS = 16
QK_NOPE = 64
QK_ROPE = 32
V_HEAD = 64
Q_RANK = 384
KV_RANK = 256
ROPE_THETA = 10000.0
D_FF = 4 * D_MODEL
Q_BLOCK = 128
N_EVEN = (DEPTH + 1) // 2
N_ODD = DEPTH // 2
DN_ALPHA = (2 * DEPTH) ** 0.25
DN_BETA = (8 * DEPTH) ** -0.25
LN_EPS = 1e-5
RMS_EPS = 1e-6
DECAY_TARGET = 1e-2
FAST_DECAY_PCT = 0.3
SLOW_DECAY_PCT = 1.5

kernel_name = "hybrid_hyena_conformer_mla_encoder"


def layer_norm(x, g, b):
    xf = x.astype(jnp.float32)
    mu = jnp.mean(xf, axis=-1, keepdims=True)
    var = jnp.mean(jnp.square(xf - mu), axis=-1, keepdims=True)
    y = (xf - mu) * lax.rsqrt(var + LN_EPS) * g.astype(jnp.float32) + b.astype(jnp.float32)
    return y.astype(x.dtype)


def rms_norm(x, g):
    xf = x.astype(jnp.float32)
    y = xf * lax.rsqrt(jnp.mean(jnp.square(xf), axis=-1, keepdims=True) + RMS_EPS) * g.astype(jnp.float32)
    return y.astype(x.dtype)


def depthwise_conv(x, w, b):
    k = w.shape[0]
    pad = (k - 1) // 2
    y = lax.conv_general_dilated(
        x, w[:, None, :].astype(x.dtype), window_strides=(1,), padding=[(pad, pad)],
        dimension_numbers=("NWC", "WIO", "NWC"), feature_group_count=x.shape[-1])
    return y + b.astype(x.dtype)


def hyena_filters(L, w1, b1, f1, w2, b2, f2, w3, decay):
    f32 = jnp.float32
    t = jnp.arange(L, dtype=f32) / max(L - 1, 1)
    bands = (FILTER_EMB - 1) // 2
    freqs = jnp.linspace(1e-4, bands - 1, bands, dtype=f32)
    w = 2.0 * math.pi * jnp.arange(L, dtype=f32) / L
    ang = w[:, None] * freqs[None, :]
    z = jnp.concatenate([t[:, None], jnp.cos(ang), -jnp.sin(ang)], axis=-1)
    h = jnp.sin(f1.astype(f32)[:, None, :] * (jnp.einsum('le,def->dlf', z, w1.astype(f32)) + b1.astype(f32)[:, None, :]))
    h = jnp.sin(f2.astype(f32)[:, None, :] * (jnp.einsum('dlf,dfg->dlg', h, w2.astype(f32)) + b2.astype(f32)[:, None, :]))
    h = jnp.einsum('dlf,dfc->dlc', h, w3.astype(f32))
    h = h * jnp.exp(-t[None, :, None] * decay.astype(f32)[:, None, :])
    c = h.shape[-1]
    taps = jnp.concatenate([h[0], jnp.zeros((1, c), f32), h[1, 1:][::-1]], axis=0)
    return taps * lax.rsqrt(jnp.sum(jnp.square(taps), axis=0, keepdims=True))


def hyena_mix(u, taps, skip_d):
    x0, x1, v = jnp.split(u, 3, axis=-1)
    L = u.shape[1]
    zf = (x1 * v).astype(jnp.float32)
    zspec = jnp.fft.rfft(zf, n=2 * L, axis=1)
    hspec = jnp.fft.rfft(taps, n=2 * L, axis=0)
    y = jnp.fft.irfft(zspec * hspec[None], n=2 * L, axis=1)[:, :L]
    y = y + zf * skip_d.astype(jnp.float32)
    return (x0.astype(jnp.float32) * y).astype(u.dtype)


def conformer_conv(u, dw_w, dw_b, ln_g, ln_b):
    a, g = jnp.split(u, 2, axis=-1)
    h = a * jax.nn.sigmoid(g)
    h = depthwise_conv(h, dw_w, dw_b)
    h = layer_norm(h, ln_g, ln_b)
    return jax.nn.silu(h)


def even_mixer(x, w_in, b_in, short_w, short_b, f_w1, f_b1, f_fr1, f_w2, f_b2, f_fr2, f_w3,
               decay, skip_d, dw_w, dw_b, cln_g, cln_b, w_out, b_out):
    L = x.shape[1]
    proj = x @ w_in + b_in
    hy_in = depthwise_conv(proj[..., :3 * D_HYENA], short_w, short_b)
    taps = hyena_filters(L, f_w1, f_b1, f_fr1, f_w2, f_b2, f_fr2, f_w3, decay)
    y_a = hyena_mix(hy_in, taps, skip_d)
    y_b = conformer_conv(proj[..., 3 * D_HYENA:], dw_w, dw_b, cln_g, cln_b)
    return jnp.concatenate([y_a, y_b], axis=-1) @ w_out + b_out


def rope_tables(L):
    pos = jnp.arange(L, dtype=jnp.float32)
    inv = 1.0 / (ROPE_THETA ** (jnp.arange(0, QK_ROPE, 2, dtype=jnp.float32) / QK_ROPE))
    ang = pos[:, None] * inv[None, :]
    return jnp.cos(ang), jnp.sin(ang)


def apply_rope(x, cos, sin):
    x1, x2 = jnp.split(x, 2, axis=-1)
    cos = cos.astype(x.dtype)
    sin = sin.astype(x.dtype)
    return jnp.concatenate([x1 * cos - x2 * sin, x1 * sin + x2 * cos], axis=-1)


def mla_mixer(x, wq_a, q_norm, wq_b, wkv_a, kv_norm, wkv_b, wo):
    B, L, _ = x.shape
    cq = rms_norm(x @ wq_a, q_norm)
    q = (cq @ wq_b).reshape(B, L, N_HEADS, QK_NOPE + QK_ROPE)
    q_nope, q_pe = q[..., :QK_NOPE], q[..., QK_NOPE:]
    kv = x @ wkv_a
    ckv = rms_norm(kv[..., :KV_RANK], kv_norm)
    cos, sin = rope_tables(L)
    q_pe = apply_rope(q_pe, cos[None, :, None, :], sin[None, :, None, :])
    k_pe = apply_rope(kv[..., KV_RANK:], cos[None], sin[None])
    kvb = (ckv @ wkv_b).reshape(B, L, N_HEADS, QK_NOPE + V_HEAD)
    k_nope, v = kvb[..., :QK_NOPE], kvb[..., QK_NOPE:]
    n_blk = -(-L // Q_BLOCK)
    Lp = n_blk * Q_BLOCK
    pad = ((0, 0), (0, Lp - L), (0, 0), (0, 0))
    qn = jnp.pad(q_nope, pad).reshape(B, n_blk, Q_BLOCK, N_HEADS, QK_NOPE).transpose(1, 0, 2, 3, 4)
    qp = jnp.pad(q_pe, pad).reshape(B, n_blk, Q_BLOCK, N_HEADS, QK_ROPE).transpose(1, 0, 2, 3, 4)
    scale = (QK_NOPE + QK_ROPE) ** -0.5

    def attend(blk):
        qn_b, qp_b = blk
        s = jnp.einsum('bqhd,bkhd->bhqk', qn_b, k_nope) + jnp.einsum('bqhr,bkr->bhqk', qp_b, k_pe)
        p = jax.nn.softmax(s.astype(jnp.float32) * scale, axis=-1).astype(v.dtype)
        return jnp.einsum('bhqk,bkhd->bqhd', p, v)

    o = lax.map(attend, (qn, qp))
    o = o.transpose(1, 0, 2, 3, 4).reshape(B, Lp, N_HEADS * V_HEAD)[:, :L]
    return o @ wo


def trunk(x, p):
    B = x.shape[0]
    meta = jnp.broadcast_to(p['meta_tokens'][None].astype(x.dtype), (B, N_META, D_MODEL))
    h = jnp.concatenate([meta, x], axis=1)
    for layer in range(DEPTH):
        i = layer // 2
        if layer % 2 == 0:
            mix = even_mixer(h, p['ev_w_in'][i], p['ev_b_in'][i], p['ev_short_w'][i], p['ev_short_b'][i],
                             p['hy_w1'][i], p['hy_b1'][i], p['hy_freq1'][i], p['hy_w2'][i], p['hy_b2'][i],
                             p['hy_freq2'][i], p['hy_w3'][i], p['hy_decay'][i], p['hy_skip_d'][i],
                             p['cf_dw_w'][i], p['cf_dw_b'][i], p['cf_ln_g'][i], p['cf_ln_b'][i],
                             p['ev_w_out'][i], p['ev_b_out'][i])
        else:
            mix = mla_mixer(h, p['mla_wq_a'][i], p['mla_q_norm'][i], p['mla_wq_b'][i], p['mla_wkv_a'][i],
                            p['mla_kv_norm'][i], p['mla_wkv_b'][i], p['mla_wo'][i])
        h = layer_norm(DN_ALPHA * h + mix, p['ln1_g'][layer], p['ln1_b'][layer])
        f = jnp.square(jax.nn.relu(h @ p['mlp_w1'][layer])) @ p['mlp_w2'][layer]
        h = layer_norm(DN_ALPHA * h + f, p['ln2_g'][layer], p['ln2_b'][layer])
    return h[:, N_META:]


def setup_inputs(seed: int = 0) -> dict:
    key = jax.random.key(seed)
    ks = iter(jax.random.split(key, 48))

    def nrm(shape, scale=1.0):
        return jax.random.normal(next(ks), shape, jnp.float32) * scale

    def gain(shape):
        return 1.0 + nrm(shape, 0.01)

    fast = abs(math.log(DECAY_TARGET) / FAST_DECAY_PCT)
    slow = abs(math.log(DECAY_TARGET) / SLOW_DECAY_PCT)
    base_decay = jnp.linspace(slow, fast, D_HYENA, dtype=jnp.float32)
    d = {}
    d['x_prompt'] = nrm((BATCH, SEQ, D_MODEL))
    d['x_sample'] = nrm((DEC_BATCH, DEC_SEQ, D_MODEL))
    d['meta_tokens'] = nrm((N_META, D_MODEL))
    d['ev_w_in'] = nrm((N_EVEN, D_MODEL, D_IN_EVEN), D_MODEL ** -0.5)
    d['ev_b_in'] = nrm((N_EVEN, D_IN_EVEN), 0.01)
    d['ev_short_w'] = nrm((N_EVEN, SHORT_K, 3 * D_HYENA), SHORT_K ** -0.5)
    d['ev_short_b'] = nrm((N_EVEN, 3 * D_HYENA), 0.01)
    d['hy_w1'] = nrm((N_EVEN, 2, FILTER_EMB, FILTER_HIDDEN), FILTER_EMB ** -0.5)
    d['hy_b1'] = nrm((N_EVEN, 2, FILTER_HIDDEN), 0.01)
    d['hy_freq1'] = gain((N_EVEN, 2, FILTER_HIDDEN))
    d['hy_w2'] = nrm((N_EVEN, 2, FILTER_HIDDEN, FILTER_HIDDEN), FILTER_HIDDEN ** -0.5)
    d['hy_b2'] = nrm((N_EVEN, 2, FILTER_HIDDEN), 0.01)
    d['hy_freq2'] = gain((N_EVEN, 2, FILTER_HIDDEN))
    d['hy_w3'] = nrm((N_EVEN, 2, FILTER_HIDDEN, D_HYENA), FILTER_HIDDEN ** -0.5)
    d['hy_decay'] = base_decay * (1.0 + nrm((N_EVEN, 2, D_HYENA), 0.05))
    d['hy_skip_d'] = nrm((N_EVEN, D_HYENA))
    d['cf_dw_w'] = nrm((N_EVEN, CONF_K, D_CONF), CONF_K ** -0.5)
    d['cf_dw_b'] = nrm((N_EVEN, D_CONF), 0.01)
    d['cf_ln_g'] = gain((N_EVEN, D_CONF))
    d['cf_ln_b'] = nrm((N_EVEN, D_CONF), 0.01)
    d['ev_w_out'] = nrm((N_EVEN, D_HYENA + D_CONF, D_MODEL), DN_BETA * (D_HYENA + D_CONF) ** -0.5)
    d['ev_b_out'] = nrm((N_EVEN, D_MODEL), 0.01)
    d['mla_wq_a'] = nrm((N_ODD, D_MODEL, Q_RANK), D_MODEL ** -0.5)
    d['mla_q_norm'] = gain((N_ODD, Q_RANK))
    d['mla_wq_b'] = nrm((N_ODD, Q_RANK, N_HEADS * (QK_NOPE + QK_ROPE)), Q_RANK ** -0.5)
    d['mla_wkv_a'] = nrm((N_ODD, D_MODEL, KV_RANK + QK_ROPE), D_MODEL ** -0.5)
    d['mla_kv_norm'] = gain((N_ODD, KV_RANK))
    d['mla_wkv_b'] = nrm((N_ODD, KV_RANK, N_HEADS * (QK_NOPE + V_HEAD)), KV_RANK ** -0.5)
    d['mla_wo'] = nrm((N_ODD, N_HEADS * V_HEAD, D_MODEL), DN_BETA * (N_HEADS * V_HEAD) ** -0.5)
    d['ln1_g'] = gain((DEPTH, D_MODEL))
    d['ln1_b'] = nrm((DEPTH, D_MODEL), 0.01)
    d['mlp_w1'] = nrm((DEPTH, D_MODEL, D_FF), D_MODEL ** -0.5)
    d['mlp_w2'] = nrm((DEPTH, D_FF, D_MODEL), DN_BETA * D_FF ** -0.5)
    d['ln2_g'] = gain((DEPTH, D_MODEL))
    d['ln2_b'] = nrm((DEPTH, D_MODEL), 0.01)
    return d


def reference(x_prompt, x_sample, meta_tokens, ev_w_in, ev_b_in, ev_short_w, ev_short_b,
              hy_w1, hy_b1, hy_freq1, hy_w2, hy_b2, hy_freq2, hy_w3, hy_decay, hy_skip_d,
              cf_dw_w, cf_dw_b, cf_ln_g, cf_ln_b, ev_w_out, ev_b_out,
              mla_wq_a, mla_q_norm, mla_wq_b, mla_wkv_a, mla_kv_norm, mla_wkv_b, mla_wo,
              ln1_g, ln1_b, mlp_w1, mlp_w2, ln2_g, ln2_b):
    params = dict(meta_tokens=meta_tokens, ev_w_in=ev_w_in, ev_b_in=ev_b_in, ev_short_w=ev_short_w,
                  ev_short_b=ev_short_b, hy_w1=hy_w1, hy_b1=hy_b1, hy_freq1=hy_freq1, hy_w2=hy_w2,
                  hy_b2=hy_b2, hy_freq2=hy_freq2, hy_w3=hy_w3, hy_decay=hy_decay, hy_skip_d=hy_skip_d,
                  cf_dw_w=cf_dw_w, cf_dw_b=cf_dw_b, cf_ln_g=cf_ln_g, cf_ln_b=cf_ln_b,
                  ev_w_out=ev_w_out, ev_b_out=ev_b_out, mla_wq_a=mla_wq_a, mla_q_norm=mla_q_norm,
                  mla_wq_b=mla_wq_b, mla_wkv_a=mla_wkv_a, mla_kv_norm=mla_kv_norm, mla_wkv_b=mla_wkv_b,
                  mla_wo=mla_wo, ln1_g=ln1_g, ln1_b=ln1_b, mlp_w1=mlp_w1, mlp_w2=mlp_w2,
                  ln2_g=ln2_g, ln2_b=ln2_b)
    y_prompt = trunk(x_prompt, params)
    y_sample = trunk(x_sample, params)
    return (y_prompt, y_sample)
```

```python
import math
from contextlib import ExitStack
import numpy as np
import ml_dtypes
import concourse.bass as bass
import concourse.mybir as mybir
from concourse.bass_utils import run_bass_kernel_spmd

F32 = mybir.dt.float32
BF16 = mybir.dt.bfloat16
AF = mybir.ActivationFunctionType
ALU = mybir.AluOpType
AX = mybir.AxisListType

D = 1024
NMETA = 16
DFF = 4096
ALPHA = 4 ** 0.25
LN_EPS = 1e-5
RMS_EPS = 1e-6
NH = 16


SEM_MAX = 24000


class Dep:
    __slots__ = ("w", "r")

    def __init__(self):
        self.w = None
        self.r = {}


class KB:
    def __init__(self, nc):
        self.nc = nc
        self.stack = ExitStack()
        self.raw = dict(pe=nc.tensor, act=nc.scalar, dve=nc.vector, pool=nc.gpsimd, sp=nc.sync)
        self.sem = {}
        self.cnt = {}
        self.seen = {e: {} for e in self.raw}
        self.semobj = []
        for e in ("pe", "act", "dve", "pool"):
            self.sem[e] = self._newsem("s_" + e)
            self.cnt[e] = 0
        self.dq = {}
        for q, n in (("sp", 20), ("act", 8), ("pool", 8)):
            self.dq[q] = dict(sems=[self._newsem(f"d_{q}{i}") for i in range(n)], vals=[0] * n, nxt=0)
        self.uid = 0

    def _newsem(self, name):
        s = self.stack.enter_context(self.nc.semaphore(name))
        self.semobj.append(s)
        return len(self.semobj) - 1

    def name(self, p):
        self.uid += 1
        return f"{p}{self.uid}"

    def sb(self, st, shape, dt, name="t"):
        return st.enter_context(self.nc.sbuf_tensor(self.name(name), list(shape), dt))

    def ps(self, st, shape, dt, name="p"):
        return st.enter_context(self.nc.psum_tensor(self.name(name), list(shape), dt))

    def _waits(self, eng, reads, writes, extra=None):
        need = {}

        def add(tok):
            if tok is None:
                return
            s, v, src = tok
            if src == "pe" and eng == "pe":
                return
            if need.get(s, 0) < v:
                need[s] = v

        for d in reads:
            add(d.w)
        for d in writes:
            add(d.w)
            for t in d.r.values():
                add(t)
        if extra:
            for t in extra:
                add(t)
        seen = self.seen[eng]
        for s, v in need.items():
            if seen.get(s, 0) < v:
                self.raw[eng].wait_ge(self.semobj[s], v)
                seen[s] = v

    def op(self, eng, fn, reads=(), writes=()):
        self._waits(eng, reads, writes)
        ins = fn(self.raw[eng])
        if self.cnt[eng] >= SEM_MAX:
            self.sem[eng] = self._newsem(self.name("s_" + eng))
            self.cnt[eng] = 0
        self.cnt[eng] += 1
        ins.then_inc(self.semobj[self.sem[eng]], 1)
        tok = (self.sem[eng], self.cnt[eng], eng)
        for d in reads:
            d.r[tok[0]] = tok
        for d in writes:
            d.w = tok
            d.r = {}
        return ins

    def dma(self, q, out, in_, reads=(), writes=(), **kw):
        dq = self.dq[q]
        i = dq["nxt"]
        dq["nxt"] = (i + 1) % len(dq["sems"])
        s = dq["sems"][i]
        extra = [(s, dq["vals"][i], "dma")] if dq["vals"][i] else None
        self._waits(q, reads, writes, extra)
        ins = self.raw[q].dma_start(out=out, in_=in_, **kw)
        dq["vals"][i] += 16
        ins.then_inc(self.semobj[s], 16)
        tok = (s, dq["vals"][i], "dma")
        for d in reads:
            d.r[s] = tok
        for d in writes:
            d.w = tok
            d.r = {}
        return ins

    def all_gather(self, in_ap, out_ap, reads=(), writes=()):
        if not hasattr(self, "ccsem"):
            self.ccsem = self._newsem("ccsem")
            self.ccval = 0
        self._waits("pool", reads, writes)
        ins = self.raw["pool"].collective_compute("AllGather", ALU.bypass, replica_groups=[list(range(8))],
                                                  ins=[in_ap.opt()], outs=[out_ap.opt()])
        self.ccval += 1
        ins.then_inc(self.semobj[self.ccsem], 1)
        tok = (self.ccsem, self.ccval, "cc")
        for d in reads:
            d.r[self.ccsem] = tok
        for d in writes:
            d.w = tok
            d.r = {}
        return ins

    def gather_rows(self, out, in_rows, idx, reads=(), writes=()):
        dq = self.dq["pool"]
        i = dq["nxt"]
        dq["nxt"] = (i + 1) % len(dq["sems"])
        s = dq["sems"][i]
        extra = [(s, dq["vals"][i], "dma")] if dq["vals"][i] else None
        self._waits("pool", reads, writes, extra)
        ins = self.raw["pool"].indirect_dma_start(out=out, out_offset=None, in_=in_rows,
                                                  in_offset=bass.IndirectOffsetOnAxis(ap=idx, axis=0))
        dq["vals"][i] += 16
        ins.then_inc(self.semobj[s], 16)
        tok = (s, dq["vals"][i], "dma")
        for d in reads:
            d.r[s] = tok
        for d in writes:
            d.w = tok
            d.r = {}
        return ins

    def barrier(self):
        toks = [(self.sem[e], self.cnt[e], e) for e in ("pe", "act", "dve", "pool") if self.cnt[e]]
        for q in self.dq.values():
            for s, v in zip(q["sems"], q["vals"]):
                if v:
                    toks.append((s, v, "dma"))
        if getattr(self, "ccval", 0):
            toks.append((self.ccsem, self.ccval, "cc"))
        for eng in ("pe", "act", "dve", "pool", "sp"):
            seen = self.seen[eng]
            for s, v, src in toks:
                if seen.get(s, 0) < v and not (s == self.sem.get(eng)):
                    self.raw[eng].wait_ge(self.semobj[s], v)
                    seen[s] = v

    def finish_wait(self):
        for q in self.dq.values():
            for s, v in zip(q["sems"], q["vals"]):
                if v and self.seen["sp"].get(s, 0) < v:
                    self.raw["sp"].wait_ge(self.semobj[s], v)
                    self.seen["sp"][s] = v


class Glob:
    pass


def setup_globals(kb, st):
    g = Glob()
    nc = kb.nc
    g.pall = kb.ps(st, [128, 8, 512], F32, "banks")
    g.psum = [g.pall[:, b, :] for b in range(8)]
    g.pd = [Dep() for _ in range(8)]
    g.ident_f = kb.sb(st, [128, 128], F32, "identf")
    g.ident_b = kb.sb(st, [128, 128], BF16, "identb")
    g.ident_d = Dep()
    g.ones_b = kb.sb(st, [128, 128], BF16, "onesb")
    g.ones_d = Dep()
    g.bk = -1
    return g


def load_ident(kb, g, ident_dram):
    kb.dma("sp", g.ident_f[:], ident_dram, writes=[g.ident_d])
    kb.op("dve", lambda e: e.tensor_copy(out=g.ident_b[:], in_=g.ident_f[:]), reads=[g.ident_d], writes=[g.ident_d])
    kb.op("pool", lambda e: e.memset(g.ones_b[:], 1.0), writes=[g.ones_d])


_rr = [0]


def cast_eng():
    _rr[0] += 1
    return ("dve", "pool", "act")[_rr[0] % 3]


def copy_op(kb, eng, out, in_, reads, writes):
    if eng == "act":
        return kb.op("act", lambda e: e.copy(out=out, in_=in_), reads=reads, writes=writes)
    return kb.op(eng, lambda e: e.tensor_copy(out=out, in_=in_), reads=reads, writes=writes)


def load_weight_bf16(kb, st_phase, dst, dst_dep, src, kc, ncols, stage_cols=2048):
    with ExitStack() as st:
        stg = [kb.sb(st, [128, stage_cols], F32, "wstg") for _ in range(3)]
        sd = [Dep() for _ in range(3)]
        i = 0
        for k in range(kc):
            for c0 in range(0, ncols, stage_cols):
                cn = min(stage_cols, ncols - c0)
                j = i % 3
                kb.dma("sp" if i % 2 == 0 else "pool", stg[j][:, :cn], src[k * 128:(k + 1) * 128, c0:c0 + cn], writes=[sd[j]])
                copy_op(kb, ("dve", "act")[i % 2], dst[:, k, c0:c0 + cn], stg[j][:, :cn], [sd[j]], [dst_dep])
                i += 1
        kb.barrier()


def load_bcast(kb, dst, dep, src_row):
    kb.dma("sp", dst, src_row.partition_broadcast(128) if len(src_row.shape) == 1 else src_row.broadcast_to([128, src_row.shape[-1]]), writes=[dep])


def layer_norm_tile(kb, r, rd, n, gt, bt, gbd, out, outd, small, smd, junk, junkd):
    s1, s2 = small[:, 0:1], small[:, 1:2]
    kb.op("act", lambda e: e.activation(out=junk[:n, :], in_=r[:n, :], func=AF.Identity, accum_out=s1[:n, :]), reads=[rd], writes=[junkd, smd])
    kb.op("act", lambda e: e.activation(out=junk[:n, :], in_=r[:n, :], func=AF.Square, accum_out=s2[:n, :]), reads=[rd], writes=[junkd, smd])
    mean, var, rstd = small[:, 2:3], small[:, 3:4], small[:, 4:5]
    kb.op("dve", lambda e: e.tensor_scalar(out=mean[:n, :], in0=s1[:n, :], scalar1=1.0 / D, scalar2=None, op0=ALU.mult), reads=[smd], writes=[smd])
    kb.op("dve", lambda e: e.tensor_tensor(out=var[:n, :], in0=mean[:n, :], in1=mean[:n, :], op=ALU.mult), reads=[smd], writes=[smd])
    kb.op("dve", lambda e: e.scalar_tensor_tensor(out=var[:n, :], in0=s2[:n, :], scalar=1.0 / D, in1=var[:n, :], op0=ALU.mult, op1=ALU.subtract), reads=[smd], writes=[smd])
    kb.op("act", lambda e: e.activation(out=rstd[:n, :], in_=var[:n, :], func=AF.Sqrt, bias=LN_EPS, scale=1.0), reads=[smd], writes=[smd])
    kb.op("dve", lambda e: e.reciprocal(out=rstd[:n, :], in_=rstd[:n, :]), reads=[smd], writes=[smd])
    kb.op("dve", lambda e: e.tensor_scalar(out=r[:n, :], in0=r[:n, :], scalar1=mean[:n, :], scalar2=rstd[:n, :], op0=ALU.subtract, op1=ALU.mult), reads=[smd, rd], writes=[rd])
    kb.op("pool", lambda e: e.tensor_tensor(out=r[:n, :], in0=r[:n, :], in1=gt[:n, :], op=ALU.mult), reads=[rd, gbd], writes=[rd])
    kb.op("pool", lambda e: e.tensor_tensor(out=out[:n, :], in0=r[:n, :], in1=bt[:n, :], op=ALU.add), reads=[rd, gbd], writes=[outd])


def mm(kb, out, lhsT, rhs, start, stop, reads, writes):
    return kb.op("pe", lambda e: e.matmul(out, lhsT=lhsT, rhs=rhs, start=start, stop=stop), reads, writes)


def tt(kb, eng, out, in0, in1, op, reads, writes):
    return kb.op(eng, lambda e: e.tensor_tensor(out=out, in0=in0, in1=in1, op=op), reads, writes)


def ts(kb, eng, out, in0, s1, s2, op0, op1, reads, writes):
    if s2 is None:
        return kb.op(eng, lambda e: e.tensor_scalar(out=out, in0=in0, scalar1=s1, scalar2=None, op0=op0), reads, writes)
    return kb.op(eng, lambda e: e.tensor_scalar(out=out, in0=in0, scalar1=s1, scalar2=s2, op0=op0, op1=op1), reads, writes)


def stt(kb, eng, out, in0, scalar, in1, op0, op1, reads, writes):
    return kb.op("dve", lambda e: e.scalar_tensor_tensor(out=out, in0=in0, scalar=scalar, in1=in1, op0=op0, op1=op1), reads, writes)


def act(kb, out, in_, func, reads, writes, **kw):
    return kb.op("act", lambda e: e.activation(out=out, in_=in_, func=func, **kw), reads, writes)


def nextbank(g):
    g.bk = (g.bk + 1) % 8
    return g.bk


def transpose_tile(kb, g, src, srcd, n, dstT, dstd, col0, kc=8):
    for k0 in range(0, kc, 4):
        b = nextbank(g)
        kn = min(4, kc - k0)
        pv = g.psum[b][:, :].rearrange("p (k t) -> p k t", k=4)
        for k in range(kn):
            kb.op("pe", lambda e, k=k: e.transpose(pv[:, k, :n], src[:n, (k0 + k) * 128:(k0 + k + 1) * 128], g.ident_f[:n, :n]),
                  reads=[srcd, g.ident_d], writes=[g.pd[b]])
        copy_op(kb, ("dve", "act")[b % 2], dstT[:, k0:k0 + kn, col0:col0 + n], pv[:, 0:kn, :n], [g.pd[b]], [dstd])


def phase_proj_ln(kb, g, tiles, fm, W, bias, lng, lnb):
    with ExitStack() as st:
        Wb = kb.sb(st, [128, 8, D], BF16, "Wb")
        Wd = Dep()
        load_weight_bf16(kb, st, Wb, Wd, W, 8, D)
        gt = kb.sb(st, [128, D], F32, "g")
        bt = kb.sb(st, [128, D], F32, "b")
        gbd = Dep()
        load_bcast(kb, gt[:], gbd, lng)
        load_bcast(kb, bt[:], gbd, lnb)
        if bias is not None:
            bi = kb.sb(st, [128, D], F32, "bias")
            load_bcast(kb, bi[:], gbd, bias)
        NB = 2
        hb = [kb.sb(st, [128, D], F32, "h") for _ in range(NB)]
        hd = [Dep() for _ in range(NB)]
        yT = [kb.sb(st, [128, 8, 128], BF16, "yT") for _ in range(NB)]
        yTd = [Dep() for _ in range(NB)]
        if not fm:
            yb = [kb.sb(st, [128, D], F32, "y") for _ in range(NB)]
            yd = [Dep() for _ in range(NB)]
        rb = [kb.sb(st, [128, D], F32, "r") for _ in range(NB)]
        rd = [Dep() for _ in range(NB)]
        junk = kb.sb(st, [128, D], F32, "junk")
        junkd = Dep()
        small = [kb.sb(st, [128, 8], F32, "small") for _ in range(NB)]
        smd = [Dep() for _ in range(NB)]
        for i, (hap, yap, oap, n) in enumerate(tiles):
            j = i % NB
            kb.dma("sp", hb[j][:n, :], hap, writes=[hd[j]])
            if fm:
                for qi, (ksl, src, dep) in enumerate(yap):
                    kb.dma(("pool", "sp")[qi % 2], yT[j][:, ksl, :n], src, reads=[dep] if dep is not None else [], writes=[yTd[j]])
            else:
                kb.dma("pool", yb[j][:n, :], yap, writes=[yd[j]])
                transpose_tile(kb, g, yb[j], yd[j], n, yT[j], yTd[j], 0)
            bks = (nextbank(g), nextbank(g))
            for half, bk in enumerate(bks):
                for k in range(8):
                    mm(kb, g.psum[bk][:n, :], yT[j][:, k, :n], Wb[:, k, half * 512:(half + 1) * 512], k == 0, k == 7,
                       [yTd[j], Wd], [g.pd[bk]])
            for half, bk in enumerate(bks):
                sl = slice(half * 512, (half + 1) * 512)
                if bias is not None:
                    tt(kb, "dve", rb[j][:n, sl], g.psum[bk][:n, :], bi[:n, sl], ALU.add, [g.pd[bk], gbd], [rd[j]])
                else:
                    copy_op(kb, "act", rb[j][:n, sl], g.psum[bk][:n, :], [g.pd[bk]], [rd[j]])
            stt(kb, "pool", rb[j][:n, :], hb[j][:n, :], ALPHA, rb[j][:n, :], ALU.mult, ALU.add, [hd[j], rd[j]], [rd[j]])
            layer_norm_tile(kb, rb[j], rd[j], n, gt, bt, gbd, rb[j], rd[j], small[j], smd[j], junk, junkd)
            kb.dma("sp", oap, rb[j][:n, :], reads=[rd[j]])
        kb.barrier()


def phase_mlp_ln(kb, g, tiles, W1, W2, lng, lnb):
    with ExitStack() as st:
        W1b = kb.sb(st, [128, 8, DFF], BF16, "W1b")
        W2b = kb.sb(st, [128, 32, D], BF16, "W2b")
        Wd = Dep()
        load_weight_bf16(kb, st, W1b, Wd, W1, 8, DFF)
        load_weight_bf16(kb, st, W2b, Wd, W2, 32, D, stage_cols=1024)
        gt = kb.sb(st, [128, D], F32, "g")
        bt = kb.sb(st, [128, D], F32, "b")
        gbd = Dep()
        load_bcast(kb, gt[:], gbd, lng)
        load_bcast(kb, bt[:], gbd, lnb)
        hb = [kb.sb(st, [128, D], F32, "h") for _ in range(4)]
        hd = [Dep() for _ in range(4)]
        hT = kb.sb(st, [128, 8, 512], BF16, "hT")
        hTd = Dep()
        uT = kb.sb(st, [128, 32, 512], BF16, "uT")
        uTd = [Dep() for _ in range(32)]
        rl = [kb.sb(st, [128, 512], F32, "relu") for _ in range(2)]
        rld = [Dep() for _ in range(2)]
        junk = kb.sb(st, [128, D], BF16, "junk")
        junkd = Dep()
        small = [kb.sb(st, [128, 8], F32, "small") for _ in range(4)]
        smd = [Dep() for _ in range(4)]
        for s0 in range(0, len(tiles), 4):
            grp = tiles[s0:s0 + 4]
            offs = []
            tot = 0
            for i, (iap, oap, n) in enumerate(grp):
                kb.dma("sp" if i % 2 == 0 else "pool", hb[i][:n, :], iap, writes=[hd[i]])
                offs.append(tot)
                tot += n
            for i, (iap, oap, n) in enumerate(grp):
                transpose_tile(kb, g, hb[i], hd[i], n, hT, hTd, offs[i])
            for j in range(32):
                bk = nextbank(g)
                for k in range(8):
                    mm(kb, g.psum[bk][:, :tot], W1b[:, k, j * 128:(j + 1) * 128], hT[:, k, :tot], k == 0, k == 7, [Wd, hTd], [g.pd[bk]])
                q = j % 2
                act(kb, rl[q][:, :tot], g.psum[bk][:, :tot], AF.Relu, [g.pd[bk]], [rld[q]])
                tt(kb, "pool" if j % 4 < 3 else "dve", uT[:, j, :tot], rl[q][:, :tot], rl[q][:, :tot], ALU.mult, [rld[q]], [uTd[j]])
            for i, (iap, oap, n) in enumerate(grp):
                bks = (nextbank(g), nextbank(g))
                for half, bk in enumerate(bks):
                    for j in range(32):
                        mm(kb, g.psum[bk][:n, :], uT[:, j, offs[i]:offs[i] + n], W2b[:, j, half * 512:(half + 1) * 512], j == 0, j == 31,
                           [uTd[j], Wd], [g.pd[bk]])
                for half, bk in enumerate(bks):
                    sl = slice(half * 512, (half + 1) * 512)
                    stt(kb, "dve", hb[i][:n, sl], hb[i][:n, sl], ALPHA, g.psum[bk][:n, :], ALU.mult, ALU.add, [hd[i], g.pd[bk]], [hd[i]])
                layer_norm_tile(kb, hb[i], hd[i], n, gt, bt, gbd, hb[i], hd[i], small[i], smd[i], junk, junkd)
                kb.dma("sp", oap, hb[i][:n, :], reads=[hd[i]])
        kb.barrier()


def rms_rstd(kb, src, srcd, n, width, small, smd, junk, junkd, col):
    ss, rs = small[:, col:col + 1], small[:, col + 1:col + 2]
    act(kb, junk[:n, :width], src[:n, :width], AF.Square, [srcd], [junkd, smd], accum_out=ss[:n, :])
    act(kb, rs[:n, :], ss[:n, :], AF.Sqrt, [smd], [smd], bias=RMS_EPS, scale=1.0 / width)
    kb.op("dve", lambda e: e.reciprocal(out=rs[:n, :], in_=rs[:n, :]), [smd], [smd])
    return rs


def phase_qkv(kb, g, seqs, wqa, qg, WqH, WqS, wkva, kvg):
    with ExitStack() as st:
        wqa_b = kb.sb(st, [128, 8, 384], BF16, "wqa")
        wkva_b = kb.sb(st, [128, 8, 288], BF16, "wkva")
        wqh_b = kb.sb(st, [128, 3, NH * 128], BF16, "wqh")
        wqs_b = kb.sb(st, [128, 3, NH * 32], BF16, "wqs")
        Wd = Dep()
        load_weight_bf16(kb, st, wqa_b, Wd, wqa, 8, 384)
        load_weight_bf16(kb, st, wkva_b, Wd, wkva, 8, 288)
        load_weight_bf16(kb, st, wqh_b, Wd, WqH, 3, NH * 128)
        load_weight_bf16(kb, st, wqs_b, Wd, WqS, 3, NH * 32)
        qgt = kb.sb(st, [128, 384], F32, "qg")
        kvgt = kb.sb(st, [128, 256], F32, "kvg")
        gd = Dep()
        load_bcast(kb, qgt[:], gd, qg)
        load_bcast(kb, kvgt[:], gd, kvg)
        hb = [kb.sb(st, [128, D], F32, "h") for _ in range(4)]
        hd = [Dep() for _ in range(4)]
        hT = kb.sb(st, [128, 8, 512], BF16, "hT")
        hTd = Dep()
        cq = [kb.sb(st, [128, 384], F32, "cq") for _ in range(2)]
        cqd = [Dep() for _ in range(2)]
        cqT = kb.sb(st, [128, 3, 512], BF16, "cqT")
        cqTd = Dep()
        kvr = [kb.sb(st, [128, 288], F32, "kvr") for _ in range(2)]
        kvrd = [Dep() for _ in range(2)]
        kvo = [kb.sb(st, [128, 288], F32, "kvo") for _ in range(2)]
        kvod = [Dep() for _ in range(2)]
        cst = [kb.sb(st, [128, 32], F32, "cs") for _ in range(2)]
        csd = [Dep() for _ in range(2)]
        tmp = [kb.sb(st, [128, 64], F32, "tmp") for _ in range(2)]
        tmpd = [Dep() for _ in range(2)]
        junk = kb.sb(st, [128, 384], F32, "junk")
        junkd = Dep()
        small = [kb.sb(st, [128, 8], F32, "small") for _ in range(2)]
        smd = [Dep() for _ in range(2)]
        Ct = kb.sb(st, [32, 2048], F32, "C")
        St = kb.sb(st, [32, 2048], F32, "S")
        CSd = Dep()
        qsw = [kb.sb(st, [32, 512], F32, "qsw") for _ in range(2)]
        qswd = [Dep() for _ in range(2)]
        qo = [kb.sb(st, [128, 512], BF16, "qo") for _ in range(2)]
        qod = [Dep() for _ in range(2)]
        it = 0
        for sq in seqs:
            if not sq.get("kv_only"):
                kb.dma("sp", Ct[:], sq["CS"][0], writes=[CSd])
                kb.dma("sp", St[:], sq["CS"][1], writes=[CSd])
            tiles = sq["tiles"]
            for s0 in range(0, len(tiles), 4):
                grp = tiles[s0:s0 + 4]
                offs, tot = [], 0
                for i, (hap, kvap, csap, n) in enumerate(grp):
                    kb.dma("sp" if i % 2 == 0 else "pool", hb[i][:n, :], hap, writes=[hd[i]])
                    offs.append(tot)
                    tot += n
                for i, (hap, kvap, csap, n) in enumerate(grp):
                    transpose_tile(kb, g, hb[i], hd[i], n, hT, hTd, offs[i])
                is_main = (tot == 512) and not sq.get("kv_only")
                for i, (hap, kvap, csap, n) in enumerate(grp):
                    j = it % 2
                    it += 1
                    kb.dma("pool", cst[j][:n, :], csap, writes=[csd[j]])
                    bk = nextbank(g)
                    for k in range(8):
                        mm(kb, g.psum[bk][:n, :288], hT[:, k, offs[i]:offs[i] + n], wkva_b[:, k, :], k == 0, k == 7, [hTd, Wd], [g.pd[bk]])
                    copy_op(kb, "act", kvr[j][:n, :], g.psum[bk][:n, :288], [g.pd[bk]], [kvrd[j]])
                    rs = rms_rstd(kb, kvr[j], kvrd[j], n, 256, small[j], smd[j], junk, junkd, 0)
                    stt(kb, "dve", kvo[j][:n, 0:256], kvr[j][:n, 0:256], rs[:n, :], kvgt[:n, :], ALU.mult, ALU.mult, [kvrd[j], smd[j], gd], [kvod[j]])
                    x1, x2 = kvr[j][:n, 256:272], kvr[j][:n, 272:288]
                    co, si = cst[j][:n, 0:16], cst[j][:n, 16:32]
                    t = tmp[j]
                    tt(kb, "pool", t[:n, 0:16], x1, co, ALU.mult, [kvrd[j], csd[j]], [tmpd[j]])
                    tt(kb, "pool", t[:n, 16:32], x2, si, ALU.mult, [kvrd[j], csd[j]], [tmpd[j]])
                    tt(kb, "pool", t[:n, 32:48], x1, si, ALU.mult, [kvrd[j], csd[j]], [tmpd[j]])
                    tt(kb, "pool", t[:n, 48:64], x2, co, ALU.mult, [kvrd[j], csd[j]], [tmpd[j]])
                    tt(kb, "dve", kvo[j][:n, 256:272], t[:n, 0:16], t[:n, 16:32], ALU.subtract, [tmpd[j]], [kvod[j]])
                    tt(kb, "dve", kvo[j][:n, 272:288], t[:n, 32:48], t[:n, 48:64], ALU.add, [tmpd[j]], [kvod[j]])
                    kb.dma("sp", kvap, kvo[j][:n, :], reads=[kvod[j]])
                    if not is_main:
                        continue
                    bk = nextbank(g)
                    for k in range(8):
                        mm(kb, g.psum[bk][:n, :384], hT[:, k, offs[i]:offs[i] + n], wqa_b[:, k, :], k == 0, k == 7, [hTd, Wd], [g.pd[bk]])
                    copy_op(kb, "act", cq[j][:n, :], g.psum[bk][:n, :384], [g.pd[bk]], [cqd[j]])
                    rs = rms_rstd(kb, cq[j], cqd[j], n, 384, small[j], smd[j], junk, junkd, 2)
                    stt(kb, "dve", cq[j][:n, :], cq[j][:n, :], rs[:n, :], qgt[:n, :], ALU.mult, ALU.mult, [cqd[j], smd[j], gd], [cqd[j]])
                    transpose_tile(kb, g, cq[j], cqd[j], n, cqT, cqTd, offs[i], kc=3)
                if not is_main:
                    continue
                q0 = (s0 // 4) * 512
                for h in range(NH):
                    j = h % 2
                    bka, bkb = nextbank(g), nextbank(g)
                    for k in range(3):
                        mm(kb, g.psum[bka][:, :], wqh_b[:, k, h * 128:(h + 1) * 128], cqT[:, k, :], k == 0, k == 2, [Wd, cqTd], [g.pd[bka]])
                    for k in range(3):
                        mm(kb, g.psum[bkb][:32, :], wqs_b[:, k, h * 32:(h + 1) * 32], cqT[:, k, :], k == 0, k == 2, [Wd, cqTd], [g.pd[bkb]])
                    tt(kb, "dve", qsw[j][:, :], g.psum[bkb][:32, :], St[:, q0:q0 + 512], ALU.mult, [g.pd[bkb], CSd], [qswd[j]])
                    rope_q(kb, g, qo[j], qod[j], bka, qsw[j], qswd[j], Ct, CSd, q0, st, small)
                    copy_op(kb, "act", qo[j][32:64, :], g.psum[bka][32:64, :], [g.pd[bka]], [qod[j]])
                    copy_op(kb, "act", qo[j][64:128, :], g.psum[bka][64:128, :], [g.pd[bka]], [qod[j]])
                    kb.dma("pool", sq["qt"](h, q0), qo[j][:, :], reads=[qod[j]])
        kb.barrier()


_ropetmp = {}


def rope_q(kb, g, qo, qod, bka, qsw, qswd, Ct, CSd, q0, st, small):
    key = id(st)
    if key not in _ropetmp:
        _ropetmp[key] = (kb.sb(st, [32, 512], F32, "rq"), Dep())
    t, td = _ropetmp[key]
    tt(kb, "dve", t[:, :], g.psum[bka][0:32, :], Ct[:, q0:q0 + 512], ALU.mult, [g.pd[bka], CSd], [td])
    tt(kb, "pool", qo[0:32, :], t[:, :], qsw[:, :], ALU.add, [td, qswd], [qod])


QK_SCALE = 96 ** -0.5


def phase_attn(kb, g, seqs, WkH, WvH):
    NKmax = max(sum(n for _, n in sq["kchunks"]) for sq in seqs)
    NCH = max(len(sq["kchunks"]) for sq in seqs)
    with ExitStack() as st:
        wk_b = kb.sb(st, [128, 2, NH * 128], BF16, "wk")
        wv_b = kb.sb(st, [128, 2, NH * 64], BF16, "wv")
        Wd = Dep()
        load_weight_bf16(kb, st, wk_b, Wd, WkH, 2, NH * 128)
        load_weight_bf16(kb, st, wv_b, Wd, WvH, 2, NH * 64, stage_cols=1024)
        ckvT = kb.sb(st, [128, 3, NKmax], BF16, "ckvT")
        ckvTd = Dep()
        KT = kb.sb(st, [128, NKmax], BF16, "KT")
        KTd = Dep()
        V = kb.sb(st, [128, NCH, 66], BF16, "V")
        Vd = Dep()
        QT = [kb.sb(st, [128, 2048], BF16, "QT") for _ in range(2)]
        QTd = [Dep() for _ in range(2)]
        P = [kb.sb(st, [128, 1024], BF16, "P") for _ in range(2)]
        Pd = [Dep() for _ in range(2)]
        oT = kb.sb(st, [128, 1024], F32, "oT")
        oTd = Dep()
        osm = [kb.sb(st, [128, 4, 64], F32, "osm") for _ in range(2)]
        osmd = [Dep() for _ in range(2)]
        rec = [kb.sb(st, [128, 4, 1], F32, "rec") for _ in range(2)]
        recd = [Dep() for _ in range(2)]
        kvin = [kb.sb(st, [128, 288], F32, "kvin") for _ in range(2)]
        kvind = [Dep() for _ in range(2)]
        kb.op("pool", lambda e: e.memset(V[:, :, 64:66], 1.0), [], [Vd])
        for sq in seqs:
            chunks = sq["kchunks"]
            NK = sum(n for _, n in chunks)
            coff = []
            c0 = 0
            for ci, (kvap, n) in enumerate(chunks):
                j = ci % 2
                kb.dma("sp" if ci % 2 == 0 else "pool", kvin[j][:n, :], kvap, writes=[kvind[j]])
                b = nextbank(g)
                pv = g.psum[b].rearrange("p (k t) -> p k t", k=4)
                for k, w in ((0, 128), (1, 128), (2, 32)):
                    kb.op("pe", lambda e, k=k, w=w: e.transpose(pv[:w, k, :n], kvin[j][:n, k * 128:k * 128 + w], g.ident_f[:n, :n]),
                          [kvind[j], g.ident_d], [g.pd[b]])
                copy_op(kb, "dve", ckvT[:, 0:2, c0:c0 + n], pv[:, 0:2, :n], [g.pd[b]], [ckvTd])
                copy_op(kb, "act", ckvT[0:32, 2, c0:c0 + n], pv[0:32, 2, :n], [g.pd[b]], [ckvTd])
                coff.append(c0)
                c0 += n
            for h in range(NH):
                qj = h % 2
                kb.dma("sp", QT[qj][:, :], sq["qt"](h), writes=[QTd[qj]])
                for bi, k0 in enumerate(range(0, NK, 512)):
                    kn = min(512, NK - k0)
                    b = 6 + bi % 2
                    mm(kb, g.psum[b][:, :kn], wk_b[:, 0, h * 128:(h + 1) * 128], ckvT[:, 0, k0:k0 + kn], True, False, [Wd, ckvTd], [g.pd[b]])
                    mm(kb, g.psum[b][:, :kn], wk_b[:, 1, h * 128:(h + 1) * 128], ckvT[:, 1, k0:k0 + kn], False, False, [Wd, ckvTd], [g.pd[b]])
                    mm(kb, g.psum[b][:, :kn], g.ident_b[0:32, :], ckvT[0:32, 2, k0:k0 + kn], False, True, [g.ident_d, ckvTd], [g.pd[b]])
                    copy_op(kb, ("dve", "pool")[bi % 2] if False else "dve", KT[:, k0:k0 + kn], g.psum[b][:, :kn], [g.pd[b]], [KTd])
                for gi, cg in enumerate(range(0, len(chunks), 8)):
                    cn = min(8, len(chunks) - cg)
                    b = 6 + gi % 2
                    for ci in range(cn):
                        n = chunks[cg + ci][1]
                        o = coff[cg + ci]
                        for k in range(2):
                            mm(kb, g.psum[b][:n, ci * 64:(ci + 1) * 64], ckvT[:, k, o:o + n], wv_b[:, k, h * 64:(h + 1) * 64], k == 0, k == 1,
                               [ckvTd, Wd], [g.pd[b]])
                    copy_op(kb, "act", V[:, cg:cg + cn, 0:64], g.psum[b][:, :cn * 64].rearrange("p (c d) -> p c d", d=64), [g.pd[b]], [Vd])
                for qsb in range(2):
                    for ci, (kvap, n) in enumerate(chunks):
                        o = coff[ci]
                        sb0 = 2 + 2 * (ci % 2)
                        pj = ci % 2
                        for i in range(2):
                            mm(kb, g.psum[sb0 + i][:n, :], KT[:, o:o + n], QT[qj][:, qsb * 1024 + i * 512:qsb * 1024 + (i + 1) * 512], True, True,
                               [KTd, QTd[qj]], [g.pd[sb0 + i]])
                        act(kb, P[pj][:n, :].rearrange("p (a b) -> p a b", a=2), g.pall[:n, sb0:sb0 + 2, :], AF.Exp,
                            [g.pd[sb0], g.pd[sb0 + 1]], [Pd[pj]], scale=QK_SCALE)
                        for i in range(2):
                            mm(kb, g.psum[i][:65, :], V[:n, ci, 0:65], P[pj][:n, i * 512:(i + 1) * 512], ci == 0, ci == len(chunks) - 1,
                               [Vd, Pd[pj]], [g.pd[i]])
                    copy_op(kb, "dve", oT[:65, :].rearrange("p (a b) -> p a b", a=2), g.pall[:65, 0:2, :], [g.pd[0], g.pd[1]], [oTd])
                    for half in range(2):
                        b = 6 + half
                        oj = half
                        pv = g.psum[b][:, 0:4 * 65].rearrange("p (t c) -> p t c", c=65)
                        for t in range(4):
                            q0 = half * 512 + t * 128
                            kb.op("pe", lambda e, t=t, q0=q0: e.transpose(pv[:, t, :], oT[:65, q0:q0 + 128], g.ident_f[:65, :65]),
                                  [oTd, g.ident_d], [g.pd[b]])
                        kb.op("dve", lambda e: e.reciprocal(out=rec[oj][:, :, :], in_=pv[:, :, 64:65]), [g.pd[b]], [recd[oj]])
                        tt(kb, "dve", osm[oj][:, :, :], pv[:, :, 0:64], rec[oj][:, :, :].broadcast_to([128, 4, 64]), ALU.mult,
                           [g.pd[b], recd[oj]], [osmd[oj]])
                        kb.dma("pool", sq["o"](qsb, half, h), osm[oj][:, :, :], reads=[osmd[oj]])
        kb.barrier()


XC = 2124


def phase_conf(kb, g, seqs, w_conf, cols_ap):
    with ExitStack() as st:
        wb = kb.sb(st, [128, 8, 1024], BF16, "wconf")
        Wd = Dep()
        load_weight_bf16(kb, st, wb, Wd, w_conf, 8, 1024)
        cols = kb.sb(st, [128, 20 + 124], F32, "cols")
        cd = Dep()
        kb.dma("sp", cols[:], cols_ap, writes=[cd])
        Dg = kb.sb(st, [128, 4, 31, 128], BF16, "Dg")
        Dgd = Dep()
        for j in range(4):
            for k in range(31):
                ts(kb, ("dve", "pool")[k % 2], Dg[:, j, k, :], g.ident_f[:, :], cols[:, 20 + j * 31 + k:20 + j * 31 + k + 1], None, ALU.mult, None,
                   [g.ident_d, cd], [Dgd])
        xin = [kb.sb(st, [128, D], F32, "xin") for _ in range(2)]
        xind = [Dep() for _ in range(2)]
        xT = kb.sb(st, [128, 8, XC], BF16, "xT")
        xTd = Dep()
        hT = kb.sb(st, [128, 4, XC], BF16, "hT")
        hTd = Dep()
        mask = kb.sb(st, [128, XC], F32, "mask")
        maskd = Dep()
        sg = [kb.sb(st, [128, 512], F32, "sg") for _ in range(2)]
        sgd = [Dep() for _ in range(2)]
        cc = kb.sb(st, [128, 4, 512], F32, "cc")
        ccd = Dep()
        cb = kb.sb(st, [128, 4, 512], BF16, "cb")
        cbd = Dep()
        sq = kb.sb(st, [128, 4, 512], BF16, "sq")
        sqd = Dep()
        mean = kb.sb(st, [128, 512], F32, "mean")
        rstd = kb.sb(st, [128, 512], F32, "rstd")
        std = Dep()
        yt = [kb.sb(st, [128, 512], F32, "yt") for _ in range(2)]
        ytd = [Dep() for _ in range(2)]
        yo = [kb.sb(st, [128, 512], BF16, "yo") for _ in range(2)]
        yod = [Dep() for _ in range(2)]
        for s_ in seqs:
            NC = s_.get("ncols", XC)
            kb.dma("sp", mask[:, :NC], s_["mask"].broadcast_to([128, NC]), writes=[maskd])
            for ti, t0 in enumerate(range(0, NC, 128)):
                n = min(128, NC - t0)
                j = ti % 2
                kb.dma("sp" if ti % 2 == 0 else "pool", xin[j][:n, :], s_["x"][t0:t0 + n, :], writes=[xind[j]])
                transpose_tile(kb, g, xin[j], xind[j], n, xT, xTd, t0)
            for bi, c0 in enumerate(range(0, NC, 512)):
                cn = min(512, NC - c0)
                for j in range(4):
                    ba, bg = nextbank(g), nextbank(g)
                    for k in range(8):
                        mm(kb, g.psum[ba][:, :cn], wb[:, k, j * 128:(j + 1) * 128], xT[:, k, c0:c0 + cn], k == 0, k == 7, [Wd, xTd], [g.pd[ba]])
                    for k in range(8):
                        mm(kb, g.psum[bg][:, :cn], wb[:, k, 512 + j * 128:512 + (j + 1) * 128], xT[:, k, c0:c0 + cn], k == 0, k == 7, [Wd, xTd], [g.pd[bg]])
                    q = j % 2
                    act(kb, sg[q][:, :cn], g.psum[bg][:, :cn], AF.Sigmoid, [g.pd[bg], cd], [sgd[q]], bias=cols[:, 4 + j:5 + j], scale=1.0)
                    stt(kb, "dve", sg[q][:, :cn], g.psum[ba][:, :cn], cols[:, j:j + 1], sg[q][:, :cn], ALU.add, ALU.mult, [g.pd[ba], cd, sgd[q]], [sgd[q]])
                    tt(kb, "pool", hT[:, j, c0:c0 + cn], sg[q][:, :cn], mask[:, c0:c0 + cn], ALU.mult, [sgd[q], maskd], [hTd])
            blocks = s_.get("blocks") or ([(15, 16)] + [(61 + 512 * i, 512) for i in range(4)])
            for bi, (c0, cn) in enumerate(blocks):
                for j in range(4):
                    b = nextbank(g)
                    for k in range(31):
                        mm(kb, g.psum[b][:, :cn], Dg[:, j, k, :], hT[:, j, c0 + k - 15:c0 + k - 15 + cn], k == 0, k == 30, [Dgd, hTd], [g.pd[b]])
                    act(kb, cc[:, j, :cn], g.psum[b][:, :cn], AF.Identity, [g.pd[b], cd], [ccd], bias=cols[:, 8 + j:9 + j], scale=1.0)
                    copy_op(kb, "pool", cb[:, j, :cn], cc[:, j, :cn], [ccd], [cbd])
                    tt(kb, "dve", sq[:, j, :cn], cc[:, j, :cn], cc[:, j, :cn], ALU.mult, [ccd], [sqd])
                b1, b2 = nextbank(g), nextbank(g)
                for j in range(4):
                    mm(kb, g.psum[b1][:, :cn], g.ones_b[:, :], cb[:, j, :cn], j == 0, j == 3, [g.ones_d, cbd], [g.pd[b1]])
                for j in range(4):
                    mm(kb, g.psum[b2][:, :cn], g.ones_b[:, :], sq[:, j, :cn], j == 0, j == 3, [g.ones_d, sqd], [g.pd[b2]])
                act(kb, mean[:, :cn], g.psum[b1][:, :cn], AF.Copy, [g.pd[b1]], [std], scale=1.0 / 512)
                tt(kb, "pool", rstd[:, :cn], mean[:, :cn], mean[:, :cn], ALU.mult, [std], [std])
                stt(kb, "dve", rstd[:, :cn], g.psum[b2][:, :cn], 1.0 / 512, rstd[:, :cn], ALU.mult, ALU.subtract, [g.pd[b2], std], [std])
                act(kb, rstd[:, :cn], rstd[:, :cn], AF.Sqrt, [std], [std], bias=LN_EPS, scale=1.0)
                kb.op("dve", lambda e: e.reciprocal(out=rstd[:, :cn], in_=rstd[:, :cn]), [std], [std])
                for j in range(4):
                    q = j % 2
                    tt(kb, "pool", yt[q][:, :cn], cc[:, j, :cn], mean[:, :cn], ALU.subtract, [ccd, std], [ytd[q]])
                    tt(kb, "dve", yt[q][:, :cn], yt[q][:, :cn], rstd[:, :cn], ALU.mult, [ytd[q], std], [ytd[q]])
                    ts(kb, "dve", yt[q][:, :cn], yt[q][:, :cn], cols[:, 12 + j:13 + j], cols[:, 16 + j:17 + j], ALU.mult, ALU.add, [ytd[q], cd], [ytd[q]])
                    act(kb, yo[q][:, :cn], yt[q][:, :cn], AF.Silu, [ytd[q]], [yod[q]])
                    kb.dma("sp", s_["out"](j, bi, cn), yo[q][:, :cn], reads=[yod[q]])
        kb.barrier()


I32 = mybir.dt.int32
TWO_PI = 2.0 * math.pi


class FCfg:
    def __init__(self, L, rows, N1, nq, CB):
        self.L, self.rows, self.N1, self.nq, self.CB = L, rows, N1, nq, CB
        self.N2 = 86 * nq
        self.N = N1 * self.N2
        assert self.N >= 2 * L - 1 and rows * self.N2 >= L


CFG_P = FCfg(16400, 64, 128, 3, 4)
CFG_S = FCfg(2064, 24, 48, 1, 16)


def fft_tables(cfg):
    N1, N2, N, rows, nq = cfg.N1, cfg.N2, cfg.N, cfg.rows, cfg.nq
    n1 = np.arange(rows)[:, None].astype(np.float64)
    k1 = np.arange(N1)[None, :].astype(np.float64)
    a = 2 * np.pi * n1 * k1 / N1
    F1 = np.concatenate([np.cos(a), -np.sin(a)], 1)
    n2 = np.arange(N2)[:, None].astype(np.float64)
    a = 2 * np.pi * n2 * k1 / N
    tw = np.stack([np.cos(a), -np.sin(a)], 1)
    tw = tw.reshape(nq, 86, 2, N1).transpose(1, 0, 2, 3)
    m = np.arange(N2)[None, :].astype(np.float64)
    a = 2 * np.pi * n2 * m / N2
    F2 = np.stack([np.cos(a), -np.sin(a), np.sin(a)], 0)
    F2 = F2.reshape(3, nq, 86, N2).transpose(2, 0, 1, 3)
    kk = np.arange(N1)[:, None].astype(np.float64)
    a = 2 * np.pi * kk * np.arange(N2)[None, :] / N
    twc = np.stack([np.cos(a), np.sin(a)], 1)
    a = 2 * np.pi * kk * np.arange(rows)[None, :] / N1
    G1 = np.stack([np.cos(a) / N, -np.sin(a) / N], 1)
    bf = ml_dtypes.bfloat16
    return dict(F1=F1.astype(np.float32).astype(bf), tw=np.ascontiguousarray(tw).astype(np.float32),
                F2=np.ascontiguousarray(F2).astype(np.float32).astype(bf), twc=twc.astype(np.float32),
                G1=G1.astype(np.float32).astype(bf))


class FTab:
    pass


def fft_load_tables(kb, st, cfg, tabs):
    t = FTab()
    t.d = Dep()
    t.F1 = kb.sb(st, [cfg.rows, 2 * cfg.N1], BF16, "F1")
    t.tw = kb.sb(st, [86, cfg.nq, 2, cfg.N1], F32, "tw")
    t.F2 = kb.sb(st, [86, 3, cfg.nq, cfg.N2], BF16, "F2")
    t.twc = kb.sb(st, [cfg.N1, 2, cfg.N2], F32, "twc")
    t.G1 = kb.sb(st, [cfg.N1, 2, cfg.rows], BF16, "G1")
    for nm in ("F1", "tw", "F2", "twc", "G1"):
        kb.dma("sp", getattr(t, nm)[:], tabs[nm], writes=[t.d])
    return t


class FBuf:
    pass


def fft_alloc(kb, st, cfg):
    b = FBuf()
    CB, nq, N1, N2, rows = cfg.CB, cfg.nq, cfg.N1, cfg.N2, cfg.rows
    E = CB * nq * N1
    E2 = CB * N2
    tn = max(E, E2)
    b.src_f = kb.sb(st, [rows, CB, N2], F32, "srcf")
    b.src_fd = Dep()
    b.src_b = kb.sb(st, [rows, CB, N2], BF16, "srcb")
    b.src_bd = Dep()
    b.As = kb.sb(st, [86, CB * nq, 2, N1], F32, "As")
    b.Asd = Dep()
    b.Ab = kb.sb(st, [86, CB * nq, 2, N1], BF16, "Ab")
    b.Abd = Dep()
    b.Xs = kb.sb(st, [86, CB * nq, 2, N1], F32, "Xs")
    b.Xsd = Dep()
    b.t = [kb.sb(st, [128, tn], F32, "ft") for _ in range(4)]
    b.td = [Dep() for _ in range(4)]
    return b


def cmul_batched(kb, cfg, b, P, shape, Are, Aim, Br, Bi, out_re, out_im, rdeps, wdep, conj=False):
    n = int(np.prod(shape))
    pat = {2: "p (a b) -> p a b", 3: "p (a b c) -> p a b c"}[len(shape)]
    kw = dict(zip("abc", shape))
    kw.pop("a")
    tv = [b.t[i][:P, :n].rearrange(pat, **kw) for i in range(4)]
    tt(kb, "dve", tv[0], Are, Br, ALU.mult, rdeps, [b.td[0]])
    tt(kb, "pool", tv[1], Aim, Bi, ALU.mult, rdeps, [b.td[1]])
    tt(kb, "pool", tv[2], Are, Bi, ALU.mult, rdeps, [b.td[2]])
    tt(kb, "dve", tv[3], Aim, Br, ALU.mult, rdeps, [b.td[3]])
    tt(kb, "dve", out_re, tv[0], tv[1], ALU.subtract, [b.td[0], b.td[1]], [wdep])
    tt(kb, "pool", out_im, tv[2], tv[3], ALU.add, [b.td[2], b.td[3]], [wdep])


def fft_fwd(kb, g, cfg, tb, b, cb):
    nq, N1, N2, rows = cfg.nq, cfg.N1, cfg.N2, cfg.rows
    per = 512 // (2 * N1)
    tot = cb * nq
    for i0 in range(0, tot, per):
        cnt = min(per, tot - i0)
        bk = nextbank(g)
        for i in range(i0, i0 + cnt):
            c, q = divmod(i, nq)
            mm(kb, g.psum[bk][:86, (i - i0) * 2 * N1:(i - i0 + 1) * 2 * N1], b.src_b[:rows, c, q * 86:(q + 1) * 86], tb.F1[:rows, :], True, True,
               [b.src_bd, tb.d], [g.pd[bk]])
        copy_op(kb, "act", b.As[:, i0:i0 + cnt, :, :], g.psum[bk][:86, :cnt * 2 * N1].rearrange("p (i r k) -> p i r k", r=2, k=N1), [g.pd[bk]], [b.Asd])
    Av = b.As[:, :tot, :, :].rearrange("p (c q) r k -> p c q r k", q=nq)
    Abv = b.Ab[:, :tot, :, :].rearrange("p (c q) r k -> p c q r k", q=nq)
    twr = tb.tw[:, :, 0, :].unsqueeze(1).broadcast_to([86, cb, nq, N1])
    twi = tb.tw[:, :, 1, :].unsqueeze(1).broadcast_to([86, cb, nq, N1])
    cmul_batched(kb, cfg, b, 86, (cb, nq, N1), Av[:, :, :, 0, :], Av[:, :, :, 1, :], twr, twi, Abv[:, :, :, 0, :], Abv[:, :, :, 1, :],
                 [b.Asd, tb.d], b.Abd)
    for i0 in range(0, tot, per):
        cnt = min(per, tot - i0)
        bk = nextbank(g)
        for i in range(i0, i0 + cnt):
            c, p = divmod(i, nq)
            reg = g.psum[bk][:86, (i - i0) * 2 * N1:(i - i0 + 1) * 2 * N1]
            for q in range(nq):
                blk = slice(p * 86, (p + 1) * 86)
                mm(kb, reg, tb.F2[:, 0, q, blk], b.Ab[:, c * nq + q, :, :].rearrange("p r k -> p (r k)"), q == 0, False, [tb.d, b.Abd], [g.pd[bk]])
                mm(kb, reg[:, 0:N1], tb.F2[:, 2, q, blk], b.Ab[:, c * nq + q, 1, :], False, False, [tb.d, b.Abd], [g.pd[bk]])
                mm(kb, reg[:, N1:2 * N1], tb.F2[:, 1, q, blk], b.Ab[:, c * nq + q, 0, :], False, q == nq - 1, [tb.d, b.Abd], [g.pd[bk]])
        copy_op(kb, "act", b.Xs[:, i0:i0 + cnt, :, :], g.psum[bk][:86, :cnt * 2 * N1].rearrange("p (i r k) -> p i r k", r=2, k=N1), [g.pd[bk]], [b.Xsd])


def fft_layout_dma(kb, q, cfg, tile, tiled, dram2d, c0, cb, to_sbuf):
    L, N2, rows = cfg.L, cfg.N2, cfg.rows
    full = L // N2
    rem = L - full * N2
    dv = dram2d[c0:c0 + cb, 0:full * N2].rearrange("c (a b) -> a c b", b=N2)
    if to_sbuf:
        kb.dma(q, tile[:full, :cb, :], dv, writes=[tiled])
        if rem:
            kb.dma(q, tile[full:full + 1, :cb, :rem], dram2d[c0:c0 + cb, full * N2:L].unsqueeze(0), writes=[tiled])
    else:
        kb.dma(q, dv, tile[:full, :cb, :], reads=[tiled])
        if rem:
            kb.dma(q, dram2d[c0:c0 + cb, full * N2:L].unsqueeze(0), tile[full:full + 1, :cb, :rem], reads=[tiled])


def phase_hy_conv(kb, g, cfg, tabs, taps, Hs, seqs, dskip):
    CB, nq, N1, N2, rows, L = cfg.CB, cfg.nq, cfg.N1, cfg.N2, cfg.rows, cfg.L
    with ExitStack() as st:
        tb = fft_load_tables(kb, st, cfg, tabs)
        b = fft_alloc(kb, st, cfg)
        kb.op("pool", lambda e: e.memset(b.src_f[:, :, :], 0.0), [], [b.src_fd])
        X0 = kb.sb(st, [86, CB * nq, 2, N1], F32, "X0")
        X0d = Dep()
        Hb = kb.sb(st, [86, CB * nq, 2, N1], F32, "Hb")
        Hbd = Dep()
        for c0 in range(0, 64, CB):
            for d in range(2):
                fft_layout_dma(kb, "sp", cfg, b.src_f, b.src_fd, taps[d], c0, CB, True)
                copy_op(kb, "dve", b.src_b[:, :, :], b.src_f[:, :, :], [b.src_fd], [b.src_bd])
                fft_fwd(kb, g, cfg, tb, b, CB)
                if d == 0:
                    copy_op(kb, "pool", X0[:, :, :, :], b.Xs[:, :, :, :], [b.Xsd], [X0d])
            tt(kb, "dve", Hb[:, :, 0, :], X0[:, :, 0, :], b.Xs[:, :, 0, :], ALU.add, [X0d, b.Xsd], [Hbd])
            tt(kb, "pool", Hb[:, :, 1, :], X0[:, :, 1, :], b.Xs[:, :, 1, :], ALU.subtract, [X0d, b.Xsd], [Hbd])
            kb.dma("sp", Hs[:, c0 * nq:(c0 + CB) * nq, :, :], Hb[:, :, :, :], reads=[Hbd])
        kb.barrier()
        Yb = kb.sb(st, [86, CB * nq, 2, N1], BF16, "Yb")
        Ybd = Dep()
        Bs = kb.sb(st, [N1, CB, 2, N2], F32, "Bs")
        Bsd = Dep()
        Bb = kb.sb(st, [N1, CB, 2, N2], BF16, "Bb")
        Bbd = Dep()
        x0f = kb.sb(st, [rows, CB, N2], F32, "x0f")
        x0d = Dep()
        cv = kb.sb(st, [rows, CB, N2], F32, "cv")
        cvd = Dep()
        yo = kb.sb(st, [rows, CB, N2], BF16, "yo")
        yod = Dep()
        dsk = kb.sb(st, [128, 64], F32, "dsk")
        dskd = Dep()
        kb.dma("sp", dsk[:, :], dskip.broadcast_to([128, 64]), writes=[dskd])
        perb = 512 // N2
        for sq in seqs:
            for c0 in range(0, 64, CB):
                fft_layout_dma(kb, "sp", cfg, b.src_f, b.src_fd, sq["z"], c0, CB, True)
                fft_layout_dma(kb, "pool", cfg, x0f, x0d, sq["x0"], c0, CB, True)
                kb.dma("sp", Hb[:, :, :, :], Hs[:, c0 * nq:(c0 + CB) * nq, :, :], writes=[Hbd])
                copy_op(kb, "dve", b.src_b[:, :, :], b.src_f[:, :, :], [b.src_fd], [b.src_bd])
                fft_fwd(kb, g, cfg, tb, b, CB)
                cmul_batched(kb, cfg, b, 86, (CB * nq, N1), b.Xs[:, :, 0, :], b.Xs[:, :, 1, :], Hb[:, :, 0, :], Hb[:, :, 1, :],
                             Yb[:, :, 0, :], Yb[:, :, 1, :], [b.Xsd, Hbd], Ybd)
                tot = CB * 2
                for i0 in range(0, tot, perb):
                    cnt = min(perb, tot - i0)
                    bk = nextbank(g)
                    for i in range(i0, i0 + cnt):
                        c, ri = divmod(i, 2)
                        reg = g.psum[bk][:N1, (i - i0) * N2:(i - i0 + 1) * N2]
                        for p in range(nq):
                            ya_re, ya_im = Yb[:, c * nq + p, 0, :], Yb[:, c * nq + p, 1, :]
                            if ri == 0:
                                mm(kb, reg, ya_re, tb.F2[:, 0, p, :], p == 0, False, [Ybd, tb.d], [g.pd[bk]])
                                mm(kb, reg, ya_im, tb.F2[:, 1, p, :], False, p == nq - 1, [Ybd, tb.d], [g.pd[bk]])
                            else:
                                mm(kb, reg, ya_re, tb.F2[:, 2, p, :], p == 0, False, [Ybd, tb.d], [g.pd[bk]])
                                mm(kb, reg, ya_im, tb.F2[:, 0, p, :], False, p == nq - 1, [Ybd, tb.d], [g.pd[bk]])
                    copy_op(kb, "act", Bs[:, :, :, :].rearrange("p c r n -> p (c r) n")[:, i0:i0 + cnt, :],
                            g.psum[bk][:N1, :cnt * N2].rearrange("p (i n) -> p i n", n=N2), [g.pd[bk]], [Bsd])
                twr = tb.twc[:, 0, :].unsqueeze(1).broadcast_to([N1, CB, N2])
                twi = tb.twc[:, 1, :].unsqueeze(1).broadcast_to([N1, CB, N2])
                cmul_batched(kb, cfg, b, N1, (CB, N2), Bs[:, :, 0, :], Bs[:, :, 1, :], twr, twi, Bb[:, :, 0, :], Bb[:, :, 1, :], [Bsd, tb.d], Bbd)
                for i0 in range(0, CB, perb):
                    cnt = min(perb, CB - i0)
                    bk = nextbank(g)
                    for c in range(i0, i0 + cnt):
                        reg = g.psum[bk][:rows, (c - i0) * N2:(c - i0 + 1) * N2]
                        mm(kb, reg, tb.G1[:, 0, :], Bb[:, c, 0, :], True, False, [tb.d, Bbd], [g.pd[bk]])
                        mm(kb, reg, tb.G1[:, 1, :], Bb[:, c, 1, :], False, True, [tb.d, Bbd], [g.pd[bk]])
                    copy_op(kb, "act", cv[:, i0:i0 + cnt, :], g.psum[bk][:rows, :cnt * N2].rearrange("p (i n) -> p i n", n=N2), [g.pd[bk]], [cvd])
                tt(kb, "pool", b.src_f[:, :, :], b.src_f[:, :, :], dsk[:rows, c0:c0 + CB].unsqueeze(2).broadcast_to([rows, CB, N2]), ALU.mult,
                   [b.src_fd, dskd, b.src_bd], [b.src_fd])
                tt(kb, "dve", cv[:, :, :], cv[:, :, :], b.src_f[:, :, :], ALU.add, [cvd, b.src_fd], [cvd])
                tt(kb, "pool", yo[:, :, :], cv[:, :, :], x0f[:, :, :], ALU.mult, [cvd, x0d], [yod])
                fft_layout_dma(kb, "sp", cfg, yo, yod, sq["ya"], c0, CB, False)
        kb.barrier()


def phase_hy_inproj(kb, g, seqs, w_hy, brow, hcols, G=1):
    with ExitStack() as st:
        wb = kb.sb(st, [128, 8, G * 192], BF16, "why")
        Wd = Dep()
        load_weight_bf16(kb, st, wb, Wd, w_hy, 8, G * 192, stage_cols=1536)
        hc = kb.sb(st, [64, G * 12], F32, "hc")
        hcd = Dep()
        kb.dma("sp", hc[:, :], hcols, writes=[hcd])
        brf = kb.sb(st, [1, G * 192], F32, "brf")
        brb = kb.sb(st, [1, G * 192], BF16, "brb")
        brd = Dep()
        kb.dma("sp", brf[:, :], brow, writes=[brd])
        copy_op(kb, "dve", brb[:, :], brf[:, :], [brd], [brd])
        xin = [kb.sb(st, [128, D], F32, "xin") for _ in range(4)]
        xind = [Dep() for _ in range(4)]
        xT = [kb.sb(st, [128, 8, 512], BF16, "xT") for _ in range(2)]
        xTd = [Dep() for _ in range(2)]
        vf = [kb.sb(st, [1, 512], F32, "vf") for _ in range(2)]
        vb = [kb.sb(st, [1, 512], BF16, "vb") for _ in range(2)]
        vd = [Dep() for _ in range(2)]
        o3 = [[kb.sb(st, [64, 512], F32, "o3") for _ in range(3)] for _ in range(2)]
        o3d = [[Dep() for _ in range(3)] for _ in range(2)]
        bi = 0
        oi = 0
        for sq in seqs:
            L = sq["L"]
            for t0 in range(0, L, 510):
                no = min(510, L - t0)
                ni = no + 2
                j = bi % 2
                bi += 1
                for ti, r0 in enumerate(range(0, ni, 128)):
                    n = min(128, ni - r0)
                    kb.dma("sp" if ti % 2 == 0 else "pool", xin[ti][:n, :], sq["xh"][t0 + r0:t0 + r0 + n, :], writes=[xind[ti]])
                    transpose_tile(kb, g, xin[ti], xind[ti], n, xT[j], xTd[j], r0)
                kb.dma("pool", vf[j][:, :ni], sq["valid"][:, t0:t0 + ni], writes=[vd[j]])
                copy_op(kb, "dve", vb[j][:, :ni], vf[j][:, :ni], [vd[j]], [vd[j]])
                for gg in range(G):
                    oj = oi % 2
                    oi += 1
                    for gi in range(3):
                        c0 = gg * 192 + gi * 64
                        h0 = gg * 12 + gi * 4
                        bk = nextbank(g)
                        for k in range(8):
                            mm(kb, g.psum[bk][:64, :ni], wb[:, k, c0:c0 + 64], xT[j][:, k, :ni], k == 0, False, [Wd, xTd[j]], [g.pd[bk]])
                        mm(kb, g.psum[bk][:64, :ni], brb[:, c0:c0 + 64], vb[j][:, :ni], False, True, [brd, vd[j]], [g.pd[bk]])
                        o = o3[oj][gi]
                        od = o3d[oj][gi]
                        ts(kb, "dve", o[:, :no], g.psum[bk][:64, 1:1 + no], hc[:, h0 + 1:h0 + 2], hc[:, h0 + 3:h0 + 4], ALU.mult, ALU.add,
                           [g.pd[bk], hcd], [od])
                        stt(kb, "dve", o[:, :no], g.psum[bk][:64, 0:no], hc[:, h0:h0 + 1], o[:, :no], ALU.mult, ALU.add, [g.pd[bk], hcd, od], [od])
                        stt(kb, "dve", o[:, :no], g.psum[bk][:64, 2:2 + no], hc[:, h0 + 2:h0 + 3], o[:, :no], ALU.mult, ALU.add, [g.pd[bk], hcd, od], [od])
                    tt(kb, "pool", o3[oj][1][:, :no], o3[oj][1][:, :no], o3[oj][2][:, :no], ALU.mult, [o3d[oj][1], o3d[oj][2]], [o3d[oj][1]])
                    kb.dma("sp", sq["x0"][gg][:, t0:t0 + no], o3[oj][0][:, :no], reads=[o3d[oj][0]])
                    kb.dma("pool", sq["z"][gg][:, t0:t0 + no], o3[oj][1][:, :no], reads=[o3d[oj][1]])
        kb.barrier()


def sin_reduced(kb, out, outd, src_ps, fcol, fbcol, tmps, tmpd, ki, kid, n, reads):
    a, r = tmps
    ts(kb, "dve", a[:, :n], src_ps, fcol, fbcol, ALU.mult, ALU.add, reads, [tmpd[0]])
    ts(kb, "pool", r[:, :n], a[:, :n], 1.0 / TWO_PI, None, ALU.mult, None, [tmpd[0]], [tmpd[1]])
    copy_op(kb, "dve", ki[:, :n], r[:, :n], [tmpd[1]], [kid])
    copy_op(kb, "pool", r[:, :n], ki[:, :n], [kid], [tmpd[1]])
    stt(kb, "dve", r[:, :n], r[:, :n], -TWO_PI, a[:, :n], ALU.mult, ALU.add, [tmpd[0], tmpd[1]], [tmpd[1]])
    ts(kb, "pool", r[:, :n], r[:, :n], -3.1415925, 3.1415925, ALU.max, ALU.min, [tmpd[1]], [tmpd[1]])
    return act(kb, out, r[:, :n], AF.Sin, [tmpd[1]], [outd])


def phase_hy_filters(kb, g, L, zposT, fw, taps_out):
    with ExitStack() as st:
        w1 = kb.sb(st, [33, 2, 64], F32, "fw1")
        w2 = kb.sb(st, [64, 2, 64], F32, "fw2")
        w3 = kb.sb(st, [64, 2, 64], F32, "fw3")
        fc = kb.sb(st, [64, 2, 8], F32, "fc")
        Wd = Dep()
        kb.dma("sp", w1[:, :, :], fw["w1"].rearrange("d e f -> e d f"), writes=[Wd])
        kb.dma("sp", w2[:, :, :], fw["w2"].rearrange("d e f -> e d f"), writes=[Wd])
        kb.dma("sp", w3[:, :, :], fw["w3"].rearrange("d e f -> e d f"), writes=[Wd])
        kb.dma("sp", fc[:, :, 0:5], fw["fcols"], writes=[Wd])
        tt(kb, "dve", fc[:, :, 5:6], fc[:, :, 0:1], fc[:, :, 1:2], ALU.mult, [Wd], [Wd])
        tt(kb, "dve", fc[:, :, 6:7], fc[:, :, 2:3], fc[:, :, 3:4], ALU.mult, [Wd], [Wd])
        ts(kb, "dve", fc[:, :, 7:8], fc[:, :, 4:5], -1.0, None, ALU.mult, None, [Wd], [Wd])
        taps = kb.sb(st, [64, 2, L], F32, "taps")
        tapsd = Dep()
        zp = [kb.sb(st, [33, 512], F32, "zp") for _ in range(2)]
        zpd = [Dep() for _ in range(2)]
        tb_ = [kb.sb(st, [64, 512], F32, "tbc") for _ in range(2)]
        tbd = [Dep() for _ in range(2)]
        tmps = [kb.sb(st, [64, 512], F32, "ftmp") for _ in range(2)]
        tmpd = [Dep(), Dep()]
        ki = kb.sb(st, [64, 512], I32, "ki")
        kid = Dep()
        h1 = kb.sb(st, [64, 512], F32, "h1")
        h1d = Dep()
        h2 = kb.sb(st, [64, 512], F32, "h2")
        h2d = Dep()
        ex = kb.sb(st, [64, 512], F32, "ex")
        exd = Dep()
        ss = kb.sb(st, [64, 2 * ((L + 511) // 512) + 4], F32, "ss")
        ssd = Dep()
        junk = kb.sb(st, [64, 512], F32, "fjunk")
        junkd = Dep()
        nb = (L + 511) // 512
        for bi, l0 in enumerate(range(0, L, 512)):
            n = min(512, L - l0)
            j = bi % 2
            kb.dma("sp", zp[j][:, :n], zposT[:, l0:l0 + n], writes=[zpd[j]])
            kb.dma("pool", tb_[j][:, :n], zposT[0:1, l0:l0 + n].broadcast_to([64, n]), writes=[tbd[j]])
            for d in range(2):
                bk = nextbank(g)
                mm(kb, g.psum[bk][:64, :n], w1[:, d, :], zp[j][:, :n], True, True, [Wd, zpd[j]], [g.pd[bk]])
                sin_reduced(kb, h1[:, :n], h1d, g.psum[bk][:64, :n], fc[:, d, 0:1], fc[:, d, 5:6], tmps, tmpd, ki, kid, n, [g.pd[bk], Wd])
                bk = nextbank(g)
                mm(kb, g.psum[bk][:64, :n], w2[:, d, :], h1[:, :n], True, True, [Wd, h1d], [g.pd[bk]])
                sin_reduced(kb, h2[:, :n], h2d, g.psum[bk][:64, :n], fc[:, d, 2:3], fc[:, d, 6:7], tmps, tmpd, ki, kid, n, [g.pd[bk], Wd])
                bk = nextbank(g)
                mm(kb, g.psum[bk][:64, :n], w3[:, d, :], h2[:, :n], True, True, [Wd, h2d], [g.pd[bk]])
                act(kb, ex[:, :n], tb_[j][:, :n], AF.Exp, [tbd[j], Wd], [exd], scale=fc[:, d, 7:8])
                tt(kb, "dve", taps[:, d, l0:l0 + n], g.psum[bk][:64, :n], ex[:, :n], ALU.mult, [g.pd[bk], exd], [tapsd])
                if d == 1 and l0 == 0:
                    kb.op("pool", lambda e: e.memset(taps[:, 1, 0:1], 0.0), [], [tapsd])
                act(kb, junk[:, :n], taps[:, d, l0:l0 + n], AF.Square, [tapsd], [junkd, ssd], accum_out=ss[:, 2 * bi + d:2 * bi + d + 1])
        tot, nrm = ss[:, 2 * nb:2 * nb + 1], ss[:, 2 * nb + 1:2 * nb + 2]
        kb.op("dve", lambda e: e.tensor_reduce(out=tot, in_=ss[:, 0:2 * nb], axis=AX.X, op=ALU.add), [ssd], [ssd])
        act(kb, nrm, tot, AF.Sqrt, [ssd], [ssd])
        kb.op("dve", lambda e: e.reciprocal(out=nrm, in_=nrm), [ssd], [ssd])
        for d in range(2):
            for l0 in range(0, L, 4096):
                n = min(4096, L - l0)
                ts(kb, ("dve", "pool")[d], taps[:, d, l0:l0 + n], taps[:, d, l0:l0 + n], nrm, None, ALU.mult, None, [tapsd, ssd], [tapsd])
            kb.dma("sp", taps_out[d], taps[:, d, :], reads=[tapsd])
        kb.barrier()


LP, LS = 16400, 2064
NCORES = 8
BF = ml_dtypes.bfloat16


class Prog:
    def __init__(self):
        self.nc = bass.Bass("TRN2", target_bir_lowering=False)
        self.kb = KB(self.nc)
        self.ins = {}

    def din(self, name, shape, dt=F32):
        self.ins[name] = (tuple(shape), dt)
        return self.nc.dram_tensor(name, list(shape), dt, kind="ExternalInput").ap()

    def dout(self, name, shape, dt=F32):
        return self.nc.dram_tensor(name, list(shape), dt, kind="ExternalOutput").ap()

    def scr(self, name, shape, dt=F32):
        return self.nc.dram_tensor(name, list(shape), dt).ap()


def chunk_tiles():
    return [(t0, 128, 61 + t0) for t0 in range(0, 2048, 128)] + [(2048, 16, 15)]


def declare_tabs(P, cfg, pre):
    t = fft_tables(cfg)
    return {k: P.din(pre + k, v.shape, F32 if v.dtype == np.float32 else BF16) for k, v in t.items()}, {pre + k: v for k, v in t.items()}


def build_l1():
    P = Prog()
    kb = P.kb
    ident = P.din("ident", [128, 128])
    xh_p = P.din("xh_p", [LP + 2, D])
    xh_s = P.din("xh_s", [LS + 2, D])
    valid_p = P.din("valid_p", [1, LP + 2])
    valid_s = P.din("valid_s", [1, LS + 2])
    zpos_p = P.din("zpos_p", [33, LP])
    zpos_s = P.din("zpos_s", [33, LS])
    tabsP, _ = declare_tabs(P, CFG_P, "tp_")
    tabsS, _ = declare_tabs(P, CFG_S, "ts_")
    fw1 = P.din("fw1", [2, 33, 64])
    fw2 = P.din("fw2", [2, 64, 64])
    fw3 = P.din("fw3", [9, 2, 64, 64])
    fcols = P.din("fcols", [9, 64, 2, 5])
    why = P.din("why", [9, D, 192])
    brow = P.din("brow", [9, 1, 192])
    hcols = P.din("hcols", [9, 64, 12])
    dskip = P.din("dskip", [9, 1, 64])
    xc = P.din("xc", [2, XC, D])
    mask = P.din("mask", [2, 1, XC])
    wconf = P.din("wconf", [D, 1024])
    ccols = P.din("ccols", [128, 144])
    yaP = P.dout("yaP", [64, LP], BF16)
    yaS = P.dout("yaS", [8, 64, LS], BF16)
    ybT = P.dout("ybT", [2, 512, LS], BF16)
    taps_p = P.scr("taps_p", [2, 64, LP])
    Hs_p = P.scr("Hs_p", [86, 64 * CFG_P.nq, 2, CFG_P.N1])
    z_p = P.scr("z_p", [64, LP])
    x0_p = P.scr("x0_p", [64, LP])
    taps_s = P.scr("taps_s", [8, 2, 64, LS])
    Hs_s = P.scr("Hs_s", [8, 86, 64 * CFG_S.nq, 2, CFG_S.N1])
    z_s = P.scr("z_s", [8, 64, LS])
    x0_s = P.scr("x0_s", [8, 64, LS])
    with ExitStack() as st:
        g = setup_globals(kb, st)
        load_ident(kb, g, ident)
        fwd = lambda i: dict(w1=fw1, w2=fw2, w3=fw3[i], fcols=fcols[i])
        phase_hy_filters(kb, g, LP, zpos_p, fwd(0), taps_p)
        phase_hy_inproj(kb, g, [dict(xh=xh_p, valid=valid_p, L=LP, z=[z_p], x0=[x0_p])], why[0], brow[0], hcols[0])
        phase_hy_conv(kb, g, CFG_P, tabsP, taps_p, Hs_p, [dict(z=z_p, x0=x0_p, ya=yaP)], dskip[0])
        for gi in range(8):
            phase_hy_filters(kb, g, LS, zpos_s, fwd(1 + gi), taps_s[gi])
            phase_hy_inproj(kb, g, [dict(xh=xh_s, valid=valid_s, L=LS, z=[z_s[gi]], x0=[x0_s[gi]])], why[1 + gi], brow[1 + gi], hcols[1 + gi])
            phase_hy_conv(kb, g, CFG_S, tabsS, taps_s[gi], Hs_s[gi], [dict(z=z_s[gi], x0=x0_s[gi], ya=yaS[gi])], dskip[1 + gi])

        def outf(s_):
            def f(j, bi, cn):
                if bi == 0:
                    return ybT[s_, j * 128:(j + 1) * 128, 2048:2064]
                return ybT[s_, j * 128:(j + 1) * 128, (bi - 1) * 512:bi * 512]
            return f
        phase_conf(kb, g, [dict(x=xc[s_], mask=mask[s_], out=outf(s_)) for s_ in range(2)], wconf, ccols)
        kb.finish_wait()
    return P


def build_l2():
    P = Prog()
    kb = P.kb
    ident = P.din("ident", [128, 128])
    xc = P.din("xc", [2, XC, D])
    ycT = P.din("ycT", [2, D, LS], BF16)
    wout = P.din("wout", [D, D])
    bout = P.din("bout", [1, D])
    ln1g = P.din("ln1g", [1, D]); ln1b = P.din("ln1b", [1, D]); ln2g = P.din("ln2g", [1, D]); ln2b = P.din("ln2b", [1, D])
    w1 = P.din("w1", [D, DFF]); w2 = P.din("w2", [DFF, D])
    wqa = P.din("wqa", [D, 384]); qg = P.din("qg", [1, 384]); WqH = P.din("WqH", [384, NH * 128]); WqS = P.din("WqS", [384, NH * 32])
    wkva = P.din("wkva", [D, 288]); kvg = P.din("kvg", [1, 256])
    cs = P.din("cs", [2, LS, 32]); Cq = P.din("Cq", [2, 32, 2048]); Sq = P.din("Sq", [2, 32, 2048])
    h2 = P.dout("h2", [2, LS, D])
    kvlat = P.dout("kvlat", [2, LS, 288])
    QT = P.dout("QT", [2, NH, 128, 2048], BF16)
    h1 = P.scr("h1", [2, LS, D])
    tl = chunk_tiles()
    with ExitStack() as st:
        g = setup_globals(kb, st)
        load_ident(kb, g, ident)
        ycv = ycT.rearrange("s (k p) t -> s p k t", p=128)
        phase_proj_ln(kb, g, [(xc[s_, xr:xr + n, :], [(slice(0, 8), ycv[s_, :, :, t0:t0 + n], None)], h1[s_, t0:t0 + n, :], n) for s_ in range(2) for t0, n, xr in tl],
                      True, wout, bout, ln1g, ln1b)
        phase_mlp_ln(kb, g, [(h1[s_, t0:t0 + n, :], h2[s_, t0:t0 + n, :], n) for s_ in range(2) for t0, n, xr in tl], w1, w2, ln2g, ln2b)
        seqs = []
        for s_ in range(2):
            seqs.append(dict(tiles=[(h2[s_, t0:t0 + n, :], kvlat[s_, t0:t0 + n, :], cs[s_, t0:t0 + n, :], n) for t0, n, xr in tl],
                             CS=(Cq[s_], Sq[s_]), qt=(lambda s_: (lambda h, q0: QT[s_, h, :, q0:q0 + 512]))(s_)))
        phase_qkv(kb, g, seqs, wqa, qg, WqH, WqS, wkva, kvg)
        kb.finish_wait()
    return P


def build_l3():
    P = Prog()
    kb = P.kb
    ident = P.din("ident", [128, 128])
    h2 = P.din("h2", [2, LS, D])
    kvp = P.din("kvp", [LP, 288])
    kvs = P.din("kvs", [LS, 288])
    QT = P.din("QT", [2, NH, 128, 2048], BF16)
    WkH = P.din("WkH", [256, NH * 128]); WvH = P.din("WvH", [256, NH * 64])
    wo = P.din("wo", [D, D])
    ln1g = P.din("ln1g", [1, D]); ln1b = P.din("ln1b", [1, D]); ln2g = P.din("ln2g", [1, D]); ln2b = P.din("ln2b", [1, D])
    w1 = P.din("w1", [D, DFF]); w2 = P.din("w2", [DFF, D])
    out = P.dout("out", [2, 2048, D])
    otok = P.scr("otok", [2, 2048, D])
    h3 = P.scr("h3", [2, 2048, D])
    with ExitStack() as st:
        g = setup_globals(kb, st)
        load_ident(kb, g, ident)
        seqs = []
        for s_, kv, L in ((0, kvp, LP), (1, kvs, LS)):
            otv = otok[s_].rearrange("(a t p) (h c) -> a p t h c", p=128, t=4, c=64)
            seqs.append(dict(kchunks=[(kv[t0:min(t0 + 128, L), :], min(128, L - t0)) for t0 in range(0, L, 128)],
                             qt=(lambda s_: (lambda h: QT[s_, h, :, :]))(s_),
                             o=(lambda otv: (lambda qsb, half, h: otv[qsb * 2 + half, :, :, h, :]))(otv)))
        phase_attn(kb, g, seqs, WkH, WvH)
        tl2 = [(s_, t0) for s_ in range(2) for t0 in range(0, 2048, 128)]
        phase_proj_ln(kb, g, [(h2[s_, t0:t0 + 128, :], otok[s_, t0:t0 + 128, :], h3[s_, t0:t0 + 128, :], 128) for s_, t0 in tl2], False, wo, None, ln1g, ln1b)
        phase_mlp_ln(kb, g, [(h3[s_, t0:t0 + 128, :], out[s_, t0:t0 + 128, :], 128) for s_, t0 in tl2], w1, w2, ln2g, ln2b)
        kb.finish_wait()
    return P


def zpos_table(L):
    t = np.arange(L, dtype=np.float32) / max(L - 1, 1)
    freqs = np.linspace(1e-4, 15, 16, dtype=np.float32)
    w = (np.float32(2.0 * math.pi) * np.arange(L, dtype=np.float32) / np.float32(L)).astype(np.float32)
    ang = w[:, None] * freqs[None, :]
    return np.ascontiguousarray(np.concatenate([t[:, None], np.cos(ang), -np.sin(ang)], -1).T.astype(np.float32))


def rope_cs(pos):
    inv = (1.0 / (10000.0 ** (np.arange(0, 32, 2, dtype=np.float32) / 32))).astype(np.float32)
    ang = pos.astype(np.float32)[:, None] * inv[None, :]
    return np.cos(ang).astype(np.float32), np.sin(ang).astype(np.float32)


def make_xc(hfull, m0, L):
    x = np.zeros((XC, D), np.float32)
    mk = np.zeros((1, XC), np.float32)
    x[15:46] = hfull[0:31]
    mk[0, 15:46] = 1
    lo, hi = m0 - 15, min(m0 + 2048 + 15, L)
    x[46:46 + (hi - lo)] = hfull[lo:hi]
    mk[0, 46:46 + (hi - lo)] = 1
    return x, mk


def colpack(v):
    return np.ascontiguousarray(v.reshape(4, 128).T)


def check_inputs(P, im):
    for k, (shape, dt) in P.ins.items():
        assert k in im, k
        assert tuple(im[k].shape) == shape, (k, im[k].shape, shape)
    return {k: np.ascontiguousarray(im[k]) for k in P.ins}


def kernel_unfused(x_prompt, x_sample, meta_tokens, ev_w_in, ev_b_in, ev_short_w, ev_short_b,
           hy_w1, hy_b1, hy_freq1, hy_w2, hy_b2, hy_freq2, hy_w3, hy_decay, hy_skip_d,
           cf_dw_w, cf_dw_b, cf_ln_g, cf_ln_b, ev_w_out, ev_b_out,
           mla_wq_a, mla_q_norm, mla_wq_b, mla_wkv_a, mla_kv_norm, mla_wkv_b, mla_wo,
           ln1_g, ln1_b, mlp_w1, mlp_w2, ln2_g, ln2_b):
    f = lambda a: np.asarray(a, dtype=np.float32)
    x_prompt, x_sample, meta = f(x_prompt), f(x_sample), f(meta_tokens)
    win, bin_, sw, sb = f(ev_w_in)[0], f(ev_b_in)[0], f(ev_short_w)[0], f(ev_short_b)[0]
    ident = np.eye(128, dtype=np.float32)
    hp = np.concatenate([meta, x_prompt[0]], 0)
    hs = [np.concatenate([meta, x_sample[c]], 0) for c in range(8)]
    z1 = np.zeros((1, D), np.float32)
    xh_p = np.concatenate([z1, hp, z1], 0)
    valid_p = np.ones((1, LP + 2), np.float32); valid_p[0, 0] = 0; valid_p[0, -1] = 0
    valid_s = np.ones((1, LS + 2), np.float32); valid_s[0, 0] = 0; valid_s[0, -1] = 0
    tabP, tabS = fft_tables(CFG_P), fft_tables(CFG_S)
    def grp(gi):
        ch = slice(gi * 64, gi * 64 + 64)
        gcols = [np.arange(k * 512 + gi * 64, k * 512 + gi * 64 + 64) for k in range(3)]
        allc = np.concatenate(gcols)
        return dict(fw3=np.ascontiguousarray(f(hy_w3)[0][:, :, ch]),
                    fcols=np.ascontiguousarray(np.stack([f(hy_freq1)[0], f(hy_b1)[0], f(hy_freq2)[0], f(hy_b2)[0], f(hy_decay)[0][:, ch]], -1).transpose(1, 0, 2)),
                    why=np.ascontiguousarray(win[:, allc]), brow=bin_[allc][None, :].copy(),
                    hcols=np.concatenate([np.stack([sw[0, gc], sw[1, gc], sw[2, gc], sb[gc]], 1) for gc in gcols], 1).astype(np.float32),
                    dskip=f(hy_skip_d)[0][ch][None, :].copy())
    G = [grp(gi) for gi in range(8)]
    ccols = np.concatenate([colpack(bin_[1536:2048]), colpack(bin_[2048:2560]), colpack(f(cf_dw_b)[0]), colpack(f(cf_ln_g)[0]), colpack(f(cf_ln_b)[0]),
                            np.ascontiguousarray(f(cf_dw_w)[0].T.reshape(4, 128, 31).transpose(1, 0, 2).reshape(128, 124))], 1).astype(np.float32)
    xcs, masks = [], []
    for c in range(8):
        a, ma = make_xc(hp, 16 + 2048 * c, LP)
        b, mb = make_xc(hs[c], 16, LS)
        xcs.append(np.stack([a, b], 0))
        masks.append(np.stack([ma, mb], 0))
    P1 = build_l1()
    ims = []
    for c in range(8):
        order = [c] + list(range(8))
        im = dict(ident=ident, xh_p=xh_p, xh_s=np.concatenate([z1, hs[c], z1], 0), valid_p=valid_p, valid_s=valid_s,
                  zpos_p=zpos_table(LP), zpos_s=zpos_table(LS), fw1=f(hy_w1)[0], fw2=f(hy_w2)[0],
                  fw3=np.stack([G[i]["fw3"] for i in order], 0), fcols=np.stack([G[i]["fcols"] for i in order], 0).astype(np.float32),
                  why=np.stack([G[i]["why"] for i in order], 0), brow=np.stack([G[i]["brow"] for i in order], 0),
                  hcols=np.stack([G[i]["hcols"] for i in order], 0), dskip=np.stack([G[i]["dskip"] for i in order], 0),
                  xc=xcs[c], mask=masks[c], wconf=np.ascontiguousarray(win[:, 1536:2560]), ccols=ccols)
        for k, v in tabP.items():
            im["tp_" + k] = v
        for k, v in tabS.items():
            im["ts_" + k] = v
        ims.append(check_inputs(P1, im))
    r1 = run_bass_kernel_spmd(P1.nc, ims, core_ids=list(range(8))).results
    yaP_all = np.concatenate([np.asarray(r1[c]["yaP"]) for c in range(8)], 0)
    P2 = build_l2()
    wqb = f(mla_wq_b)[0].reshape(384, NH, 96)
    WqH = np.concatenate([wqb[:, :, 64:96], np.zeros((384, NH, 32), np.float32), wqb[:, :, 0:64]], -1).reshape(384, NH * 128)
    WqS = np.concatenate([wqb[:, :, 80:96], wqb[:, :, 64:80]], -1).reshape(384, NH * 32)
    wkvb = f(mla_wkv_b)[0].reshape(256, NH, 128)
    WkH = np.concatenate([np.zeros((256, NH, 64), np.float32), wkvb[:, :, 0:64]], -1).reshape(256, NH * 128)
    WvH = np.ascontiguousarray(wkvb[:, :, 64:128]).reshape(256, NH * 64)
    ims = []
    for c in range(8):
        m0 = 16 + 2048 * c
        ya_p = np.concatenate([yaP_all[:, m0:m0 + 2048], yaP_all[:, 0:16]], 1)
        ya_s = np.asarray(r1[c]["yaS"]).reshape(512, LS)
        ya_s = np.concatenate([ya_s[:, 16:], ya_s[:, 0:16]], 1)
        yb = np.asarray(r1[c]["ybT"])
        ycT = np.stack([np.concatenate([ya_p, yb[0]], 0), np.concatenate([ya_s, yb[1]], 0)], 0)
        css, Cqs, Sqs = [], [], []
        for pos in (np.concatenate([np.arange(m0, m0 + 2048), np.arange(16)]), np.concatenate([np.arange(16, LS), np.arange(16)])):
            co, si = rope_cs(pos)
            css.append(np.concatenate([co, si], 1))
            Cqs.append(np.concatenate([co[:2048].T, co[:2048].T], 0))
            Sqs.append(np.concatenate([-si[:2048].T, si[:2048].T], 0))
        im = dict(ident=ident, xc=xcs[c], ycT=ycT, wout=f(ev_w_out)[0], bout=f(ev_b_out)[0:1], ln1g=f(ln1_g)[0:1], ln1b=f(ln1_b)[0:1],
                  ln2g=f(ln2_g)[0:1], ln2b=f(ln2_b)[0:1], w1=f(mlp_w1)[0], w2=f(mlp_w2)[0], wqa=f(mla_wq_a)[0], qg=f(mla_q_norm)[0:1],
                  WqH=WqH, WqS=WqS, wkva=f(mla_wkv_a)[0], kvg=f(mla_kv_norm)[0:1], cs=np.stack(css, 0), Cq=np.stack(Cqs, 0), Sq=np.stack(Sqs, 0))
        ims.append(check_inputs(P2, im))
    r2 = run_bass_kernel_spmd(P2.nc, ims, core_ids=list(range(8))).results
    kvp = np.concatenate([np.asarray(r2[c]["kvlat"])[0, :2048] for c in range(8)] + [np.asarray(r2[0]["kvlat"])[0, 2048:]], 0)
    P3 = build_l3()
    ims = []
    for c in range(8):
        im = dict(ident=ident, h2=np.asarray(r2[c]["h2"]), kvp=kvp, kvs=np.asarray(r2[c]["kvlat"])[1], QT=np.asarray(r2[c]["QT"]), WkH=WkH, WvH=WvH,
                  wo=f(mla_wo)[0], ln1g=f(ln1_g)[1:2], ln1b=f(ln1_b)[1:2], ln2g=f(ln2_g)[1:2], ln2b=f(ln2_b)[1:2], w1=f(mlp_w1)[1], w2=f(mlp_w2)[1])
        ims.append(check_inputs(P3, im))
    r3 = run_bass_kernel_spmd(P3.nc, ims, core_ids=list(range(8))).results
    y_prompt = np.concatenate([np.asarray(r3[c]["out"])[0] for c in range(8)], 0)[None].astype(np.float32)
    y_sample = np.stack([np.asarray(r3[c]["out"])[1] for c in range(8)], 0).astype(np.float32)
    return (y_prompt, y_sample)


U32 = mybir.dt.uint32
YAW = 18432


def build_fused(stop=10 ** 9, trace_steps=None):
    P = Prog()
    step = [0]

    def run(fn, *a):
        if step[0] < stop:
            fn(*a)
        step[0] += 1

    kb = P.kb
    nc = P.nc
    ident = P.din("ident", [128, 128])
    xh_p = P.din("xh_p", [LP + 2, D]); xh_s = P.din("xh_s", [LS + 2, D])
    valid_p = P.din("valid_p", [1, LP + 2]); valid_s = P.din("valid_s", [1, LS + 2])
    zpos_p = P.din("zpos_p", [33, LP]); zpos_s = P.din("zpos_s", [33, LS])
    tabsP, _ = declare_tabs(P, CFG_P, "tp_")
    tabsS, _ = declare_tabs(P, CFG_S, "ts_")
    fw1 = P.din("fw1", [2, 33, 64]); fw2 = P.din("fw2", [2, 64, 64])
    fw3 = P.din("fw3", [9, 2, 64, 64]); fcols = P.din("fcols", [9, 64, 2, 5])
    why = P.din("why", [9, D, 192]); brow = P.din("brow", [9, 1, 192]); hcols = P.din("hcols", [9, 64, 12]); dskip = P.din("dskip", [9, 1, 64])
    xc = P.din("xc", [2, XC, D]); mask = P.din("mask", [2, 1, XC])
    wconf = P.din("wconf", [D, 1024]); ccols = P.din("ccols", [128, 144])
    gidx = P.din("gidx", [128, 4], U32)
    wout = P.din("wout", [D, D]); bout = P.din("bout", [1, D])
    ln1g = P.din("ln1g", [2, 1, D]); ln1b = P.din("ln1b", [2, 1, D]); ln2g = P.din("ln2g", [2, 1, D]); ln2b = P.din("ln2b", [2, 1, D])
    w1 = P.din("w1", [2, D, DFF]); w2 = P.din("w2", [2, DFF, D])
    wqa = P.din("wqa", [D, 384]); qg = P.din("qg", [1, 384]); WqH = P.din("WqH", [384, NH * 128]); WqS = P.din("WqS", [384, NH * 32])
    wkva = P.din("wkva", [D, 288]); kvg = P.din("kvg", [1, 256])
    cs = P.din("cs", [2, LS, 32]); Cq = P.din("Cq", [2, 32, 2048]); Sq = P.din("Sq", [2, 32, 2048])
    WkH = P.din("WkH", [256, NH * 128]); WvH = P.din("WvH", [256, NH * 64]); wo = P.din("wo", [D, D])
    out = P.dout("out", [2, 2048, D])
    yaP = P.scr("yaP", [64, YAW], BF16)
    yaP_all = P.scr("yaP_all", [512, YAW], BF16)
    yaS = P.scr("yaS", [8, 64, LS], BF16)
    ybT = P.scr("ybT", [2, 512, LS], BF16)
    taps_p = P.scr("taps_p", [2, 64, LP]); Hs_p = P.scr("Hs_p", [86, 64 * CFG_P.nq, 2, CFG_P.N1])
    z_p = P.scr("z_p", [64, LP]); x0_p = P.scr("x0_p", [64, LP])
    taps_s = P.scr("taps_s", [8, 2, 64, LS]); Hs_s = P.scr("Hs_s", [8, 86, 64 * CFG_S.nq, 2, CFG_S.N1])
    z_s = P.scr("z_s", [8, 64, LS]); x0_s = P.scr("x0_s", [8, 64, LS])
    h1 = P.scr("h1", [2, LS, D]); h2 = P.scr("h2", [2, LS, D])
    kvlat = P.scr("kvlat", [2, LS, 288]); kv_all = P.scr("kv_all", [8 * LS, 288])
    QT = P.scr("QT", [2, NH, 128, 2048], BF16)
    otok = P.scr("otok", [2, 2048, D]); h3 = P.scr("h3", [2, 2048, D])
    tl = chunk_tiles()
    with ExitStack() as st:
        g = setup_globals(kb, st)
        load_ident(kb, g, ident)
        fwd = lambda i: dict(w1=fw1, w2=fw2, w3=fw3[i], fcols=fcols[i])
        run(phase_hy_filters, kb, g, LP, zpos_p, fwd(0), taps_p)
        run(phase_hy_inproj, kb, g, [dict(xh=xh_p, valid=valid_p, L=LP, z=[z_p], x0=[x0_p])], why[0], brow[0], hcols[0])
        run(phase_hy_conv, kb, g, CFG_P, tabsP, taps_p, Hs_p, [dict(z=z_p, x0=x0_p, ya=yaP[:, 2032:2032 + LP])], dskip[0])
        agd = Dep()
        run(lambda: kb.all_gather(yaP, yaP_all, reads=[], writes=[agd]))
        kb.barrier()
        for gi in range(8):
            run(phase_hy_filters, kb, g, LS, zpos_s, fwd(1 + gi), taps_s[gi])
            run(phase_hy_inproj, kb, g, [dict(xh=xh_s, valid=valid_s, L=LS, z=[z_s[gi]], x0=[x0_s[gi]])], why[1 + gi], brow[1 + gi], hcols[1 + gi])
            run(phase_hy_conv, kb, g, CFG_S, tabsS, taps_s[gi], Hs_s[gi], [dict(z=z_s[gi], x0=x0_s[gi], ya=yaS[gi])], dskip[1 + gi])

        def outf(s_):
            def f(j, bi, cn):
                if bi == 0:
                    return ybT[s_, j * 128:(j + 1) * 128, 2048:2064]
                return ybT[s_, j * 128:(j + 1) * 128, (bi - 1) * 512:bi * 512]
            return f
        run(phase_conf, kb, g, [dict(x=xc[s_], mask=mask[s_], out=outf(s_)) for s_ in range(2)], wconf, ccols)
        with ExitStack() as st2:
            yaG = kb.sb(st2, [128, 4, 2048], BF16, "yaG")
            yaGd = Dep()
            ix = kb.sb(st2, [128, 4], U32, "gix")
            ixd = Dep()
            kb.dma("sp", ix[:, :], gidx[:, :], writes=[ixd])
            rows = yaP_all.rearrange("c (b t) -> (c b) t", t=2048)
            for k in range(4):
                run(lambda k=k: kb.gather_rows(yaG[:, k, :], rows[:, :], ix[:, k:k + 1], reads=[agd, ixd], writes=[yaGd]))
            ybv = ybT.rearrange("s (k p) t -> s p k t", p=128)
            yav_meta = yaP_all.rearrange("(k p) c -> p k c", p=128)
            yas = yaS.rearrange("g c t -> (g c) t").rearrange("(k p) t -> p k t", p=128)
            tiles = []
            for t0, n, xr in tl:
                if n == 128:
                    yl = [(slice(0, 4), yaG[:, :, t0:t0 + n], yaGd), (slice(4, 8), ybv[0, :, :, t0:t0 + n], None)]
                else:
                    yl = [(slice(0, 4), yav_meta[:, :, 2032:2048], agd), (slice(4, 8), ybv[0, :, :, 2048:2064], None)]
                tiles.append((xc[0, xr:xr + n, :], yl, h1[0, t0:t0 + n, :], n))
            for t0, n, xr in tl:
                tok0 = 16 + t0 if n == 128 else 0
                yl = [(slice(0, 4), yas[:, :, tok0:tok0 + n], None), (slice(4, 8), ybv[1, :, :, t0:t0 + n], None)]
                tiles.append((xc[1, xr:xr + n, :], yl, h1[1, t0:t0 + n, :], n))
            run(phase_proj_ln, kb, g, tiles, True, wout, bout, ln1g[0], ln1b[0])
        run(phase_mlp_ln, kb, g, [(h1[s_, t0:t0 + n, :], h2[s_, t0:t0 + n, :], n) for s_ in range(2) for t0, n, xr in tl], w1[0], w2[0], ln2g[0], ln2b[0])
        seqs = []
        for s_ in range(2):
            seqs.append(dict(tiles=[(h2[s_, t0:t0 + n, :], kvlat[s_, t0:t0 + n, :], cs[s_, t0:t0 + n, :], n) for t0, n, xr in tl],
                             CS=(Cq[s_], Sq[s_]), qt=(lambda s_: (lambda h, q0: QT[s_, h, :, q0:q0 + 512]))(s_)))
        run(phase_qkv, kb, g, seqs, wqa, qg, WqH, WqS, wkva, kvg)
        kvd = Dep()
        run(lambda: kb.all_gather(kvlat[0], kv_all, reads=[], writes=[kvd]))
        kb.barrier()
        seqs = []
        pch = [(kv_all[r * LS + t0:r * LS + t0 + 128, :], 128) for r in range(8) for t0 in range(0, 2048, 128)] + [(kv_all[2048:2064, :], 16)]
        sch = [(kvlat[1, t0:min(t0 + 128, LS), :], min(128, LS - t0)) for t0 in range(0, LS, 128)]
        for s_, ch in ((0, pch), (1, sch)):
            otv = otok[s_].rearrange("(a t p) (h c) -> a p t h c", p=128, t=4, c=64)
            seqs.append(dict(kchunks=ch, qt=(lambda s_: (lambda h: QT[s_, h, :, :]))(s_),
                             o=(lambda otv: (lambda qsb, half, h: otv[qsb * 2 + half, :, :, h, :]))(otv)))
        run(phase_attn, kb, g, seqs, WkH, WvH)
        tl2 = [(s_, t0) for s_ in range(2) for t0 in range(0, 2048, 128)]
        run(phase_proj_ln, kb, g, [(h2[s_, t0:t0 + 128, :], otok[s_, t0:t0 + 128, :], h3[s_, t0:t0 + 128, :], 128) for s_, t0 in tl2], False, wo, None, ln1g[1], ln1b[1])
        run(phase_mlp_ln, kb, g, [(h3[s_, t0:t0 + 128, :], out[s_, t0:t0 + 128, :], 128) for s_, t0 in tl2], w1[1], w2[1], ln2g[1], ln2b[1])
        kb.finish_wait()
    P.nsteps = step[0]
    return P


def build_nc():
    P = Prog()
    kb = P.kb
    ident = P.din("ident", [128, 128])
    xpad_p = P.din("xpad_p", [LP + 30, D]); maskpad = P.din("maskpad", [1, LP + 30])
    xh_s = P.din("xh_s", [LS + 2, D]); valid_s = P.din("valid_s", [1, LS + 2])
    xc_s = P.din("xc_s", [XC, D]); mask_s = P.din("mask_s", [1, XC])
    zpos_p = P.din("zpos_p", [33, LP]); zpos_s = P.din("zpos_s", [33, LS])
    tabsP, _ = declare_tabs(P, CFG_P, "tp_")
    tabsS, _ = declare_tabs(P, CFG_S, "ts_")
    fw1 = P.din("fw1", [2, 33, 64]); fw2 = P.din("fw2", [2, 64, 64])
    fw3 = P.din("fw3", [8, 2, 64, 64]); fcols = P.din("fcols", [8, 64, 2, 5])
    why = P.din("why", [D, 8 * 192]); brow = P.din("brow", [1, 8 * 192]); hcols = P.din("hcols", [64, 8 * 12]); dskip = P.din("dskip", [8, 1, 64])
    wconf = P.din("wconf", [D, 1024]); ccols = P.din("ccols", [128, 144])
    tokidx = P.din("tokidx", [128, 16], U32)
    wout = P.din("wout", [D, D]); bout = P.din("bout", [1, D])
    ln1g = P.din("ln1g", [2, 1, D]); ln1b = P.din("ln1b", [2, 1, D]); ln2g = P.din("ln2g", [2, 1, D]); ln2b = P.din("ln2b", [2, 1, D])
    w1 = P.din("w1", [2, D, DFF]); w2 = P.din("w2", [2, DFF, D])
    wqa = P.din("wqa", [D, 384]); qg = P.din("qg", [1, 384]); WqH = P.din("WqH", [384, NH * 128]); WqS = P.din("WqS", [384, NH * 32])
    wkva = P.din("wkva", [D, 288]); kvg = P.din("kvg", [1, 256])
    cs_all = P.din("cs_all", [LP, 32])
    cs = P.din("cs", [2, LS, 32]); Cq = P.din("Cq", [2, 32, 2048]); Sq = P.din("Sq", [2, 32, 2048])
    WkH = P.din("WkH", [256, NH * 128]); WvH = P.din("WvH", [256, NH * 64]); wo = P.din("wo", [D, D])
    out = P.dout("out", [2, 2048, D])
    yaP_all = P.scr("yaP_all", [512, YAW], BF16)
    yaS = P.scr("yaS", [8, 64, LS], BF16)
    ybT_p = P.scr("ybT_p", [512, LP], BF16); ybT_s = P.scr("ybT_s", [512, LS], BF16)
    taps_p = P.scr("taps_p", [2, 64, LP]); Hs_p = P.scr("Hs_p", [86, 64 * CFG_P.nq, 2, CFG_P.N1])
    z_p = P.scr("z_p", [8, 64, LP]); x0_p = P.scr("x0_p", [8, 64, LP])
    taps_s = P.scr("taps_s", [2, 64, LS]); Hs_s = P.scr("Hs_s", [86, 64 * CFG_S.nq, 2, CFG_S.N1])
    z_s = P.scr("z_s", [8, 64, LS]); x0_s = P.scr("x0_s", [8, 64, LS])
    h1_all = P.scr("h1_all", [LP, D]); h2_all = P.scr("h2_all", [LP, D])
    h1_s = P.scr("h1_s", [LS, D]); h2_s = P.scr("h2_s", [LS, D]); h2_own = P.scr("h2_own", [LS, D])
    kv_all = P.scr("kv_all", [LP, 288]); kv_dummy = P.scr("kv_dummy", [LS, 288]); kvlat_s = P.scr("kvlat_s", [LS, 288])
    QT = P.scr("QT", [2, NH, 128, 2048], BF16)
    otok = P.scr("otok", [2, 2048, D]); h3 = P.scr("h3", [2, 2048, D])
    tl = chunk_tiles()
    with ExitStack() as st:
        g = setup_globals(kb, st)
        load_ident(kb, g, ident)
        fwd = lambda i: dict(w1=fw1, w2=fw2, w3=fw3[i], fcols=fcols[i])
        phase_hy_inproj(kb, g, [dict(xh=xpad_p[14:14 + LP + 2, :], valid=maskpad[:, 14:14 + LP + 2], L=LP,
                                     z=[z_p[gi] for gi in range(8)], x0=[x0_p[gi] for gi in range(8)])], why, brow, hcols, G=8)
        for gi in range(8):
            phase_hy_filters(kb, g, LP, zpos_p, fwd(gi), taps_p)
            phase_hy_conv(kb, g, CFG_P, tabsP, taps_p, Hs_p, [dict(z=z_p[gi], x0=x0_p[gi], ya=yaP_all[gi * 64:(gi + 1) * 64, 2032:2032 + LP])], dskip[gi])
        phase_hy_inproj(kb, g, [dict(xh=xh_s, valid=valid_s, L=LS, z=[z_s[gi] for gi in range(8)], x0=[x0_s[gi] for gi in range(8)])],
                        why, brow, hcols, G=8)
        for gi in range(8):
            phase_hy_filters(kb, g, LS, zpos_s, fwd(gi), taps_s)
            phase_hy_conv(kb, g, CFG_S, tabsS, taps_s, Hs_s, [dict(z=z_s[gi], x0=x0_s[gi], ya=yaS[gi])], dskip[gi])
        cseqs = []
        for j in range(8):
            r0 = 16 + 2048 * j
            cseqs.append(dict(x=xpad_p[r0:r0 + 2078, :], mask=maskpad[:, r0:r0 + 2078], ncols=2078, blocks=[(15 + 512 * i, 512) for i in range(4)],
                              out=(lambda j: (lambda jj, bi, cn: ybT_p[jj * 128:(jj + 1) * 128, 2048 * j + 512 * bi:2048 * j + 512 * bi + cn]))(j)))
        cseqs.append(dict(x=xpad_p[0:46, :], mask=maskpad[:, 0:46], ncols=46, blocks=[(15, 16)],
                          out=lambda jj, bi, cn: ybT_p[jj * 128:(jj + 1) * 128, 16384:16400]))

        def outf_s(jj, bi, cn):
            if bi == 0:
                return ybT_s[jj * 128:(jj + 1) * 128, 2048:2064]
            return ybT_s[jj * 128:(jj + 1) * 128, (bi - 1) * 512:bi * 512]
        cseqs.append(dict(x=xc_s, mask=mask_s, out=outf_s))
        phase_conf(kb, g, cseqs, wconf, ccols)
        yav = yaP_all.rearrange("(k p) c -> p k c", p=128)
        ybv_p = ybT_p.rearrange("(k p) t -> p k t", p=128)
        ybv_s = ybT_s.rearrange("(k p) t -> p k t", p=128)
        yas = yaS.rearrange("g c t -> (g c) t").rearrange("(k p) t -> p k t", p=128)
        tiles = []
        for j in range(8):
            for t0 in range(0, 2048, 128):
                tok = 16 + 2048 * j + t0
                gr = 2048 * j + t0
                tiles.append((xpad_p[15 + tok:15 + tok + 128, :],
                              [(slice(0, 4), yav[:, :, 2032 + tok:2032 + tok + 128], None), (slice(4, 8), ybv_p[:, :, gr:gr + 128], None)],
                              h1_all[gr:gr + 128, :], 128))
        tiles.append((xpad_p[15:31, :], [(slice(0, 4), yav[:, :, 2032:2048], None), (slice(4, 8), ybv_p[:, :, 16384:16400], None)],
                      h1_all[16384:16400, :], 16))
        for t0, n, xr in tl:
            tok0 = 16 + t0 if n == 128 else 0
            tiles.append((xc_s[xr:xr + n, :], [(slice(0, 4), yas[:, :, tok0:tok0 + n], None), (slice(4, 8), ybv_s[:, :, t0:t0 + n], None)],
                          h1_s[t0:t0 + n, :], n))
        phase_proj_ln(kb, g, tiles, True, wout, bout, ln1g[0], ln1b[0])
        ptl = [(r0, min(128, LP - r0)) for r0 in range(0, LP, 128)]
        phase_mlp_ln(kb, g, [(h1_all[r0:r0 + n, :], h2_all[r0:r0 + n, :], n) for r0, n in ptl] +
                     [(h1_s[t0:t0 + n, :], h2_s[t0:t0 + n, :], n) for t0, n, xr in tl], w1[0], w2[0], ln2g[0], ln2b[0])
        with ExitStack() as st2:
            ix = kb.sb(st2, [128, 16], U32, "tokix")
            ixd = Dep()
            kb.dma("sp", ix[:, :], tokidx[:, :], writes=[ixd])
            gb = [kb.sb(st2, [128, D], F32, "gb") for _ in range(2)]
            gd = [Dep(), Dep()]
            for i in range(16):
                j = i % 2
                kb.gather_rows(gb[j][:, :], h2_all[:, :], ix[:, i:i + 1], reads=[ixd], writes=[gd[j]])
                kb.dma("sp", h2_own[128 * i:128 * i + 128, :], gb[j][:, :], reads=[gd[j]])
            kb.dma("sp", h2_own[2048:2064, :], h2_all[16384:16400, :])
            kb.barrier()
        seqs = [dict(tiles=[(h2_all[r0:r0 + n, :], kv_all[r0:r0 + n, :], cs_all[r0:r0 + n, :], n) for r0, n in ptl], kv_only=True),
                dict(tiles=[(h2_own[t0:t0 + n, :], kv_dummy[t0:t0 + n, :], cs[0, t0:t0 + n, :], n) for t0, n, xr in tl],
                     CS=(Cq[0], Sq[0]), qt=lambda h, q0: QT[0, h, :, q0:q0 + 512]),
                dict(tiles=[(h2_s[t0:t0 + n, :], kvlat_s[t0:t0 + n, :], cs[1, t0:t0 + n, :], n) for t0, n, xr in tl],
                     CS=(Cq[1], Sq[1]), qt=lambda h, q0: QT[1, h, :, q0:q0 + 512])]
        phase_qkv(kb, g, seqs, wqa, qg, WqH, WqS, wkva, kvg)
        aseqs = []
        for s_, ch in ((0, [(kv_all[r0:r0 + n, :], n) for r0, n in ptl]),
                       (1, [(kvlat_s[t0:min(t0 + 128, LS), :], min(128, LS - t0)) for t0 in range(0, LS, 128)])):
            otv = otok[s_].rearrange("(a t p) (h c) -> a p t h c", p=128, t=4, c=64)
            aseqs.append(dict(kchunks=ch, qt=(lambda s_: (lambda h: QT[s_, h, :, :]))(s_),
                              o=(lambda otv: (lambda qsb, half, h: otv[qsb * 2 + half, :, :, h, :]))(otv)))
        phase_attn(kb, g, aseqs, WkH, WvH)
        hres = (h2_own, h2_s)
        tl2 = [(s_, t0) for s_ in range(2) for t0 in range(0, 2048, 128)]
        phase_proj_ln(kb, g, [(hres[s_][t0:t0 + 128, :], otok[s_, t0:t0 + 128, :], h3[s_, t0:t0 + 128, :], 128) for s_, t0 in tl2], False, wo, None, ln1g[1], ln1b[1])
        phase_mlp_ln(kb, g, [(h3[s_, t0:t0 + 128, :], out[s_, t0:t0 + 128, :], 128) for s_, t0 in tl2], w1[1], w2[1], ln2g[1], ln2b[1])
        kb.finish_wait()
    return P


def kernel(x_prompt, x_sample, meta_tokens, ev_w_in, ev_b_in, ev_short_w, ev_short_b,
           hy_w1, hy_b1, hy_freq1, hy_w2, hy_b2, hy_freq2, hy_w3, hy_decay, hy_skip_d,
           cf_dw_w, cf_dw_b, cf_ln_g, cf_ln_b, ev_w_out, ev_b_out,
           mla_wq_a, mla_q_norm, mla_wq_b, mla_wkv_a, mla_kv_norm, mla_wkv_b, mla_wo,
           ln1_g, ln1_b, mlp_w1, mlp_w2, ln2_g, ln2_b):
    f = lambda a: np.asarray(a, dtype=np.float32)
    x_prompt, x_sample, meta = f(x_prompt), f(x_sample), f(meta_tokens)
    win, bin_, sw, sb = f(ev_w_in)[0], f(ev_b_in)[0], f(ev_short_w)[0], f(ev_short_b)[0]
    ident = np.eye(128, dtype=np.float32)
    hp = np.concatenate([meta, x_prompt[0]], 0)
    hs = [np.concatenate([meta, x_sample[c]], 0) for c in range(8)]
    z1 = np.zeros((1, D), np.float32)
    z15 = np.zeros((15, D), np.float32)
    xpad_p = np.concatenate([z15, hp, z15], 0)
    maskpad = np.zeros((1, LP + 30), np.float32); maskpad[0, 15:15 + LP] = 1
    valid_s = np.ones((1, LS + 2), np.float32); valid_s[0, 0] = 0; valid_s[0, -1] = 0
    tabP, tabS = fft_tables(CFG_P), fft_tables(CFG_S)
    gcols = [[np.arange(k * 512 + gi * 64, k * 512 + gi * 64 + 64) for k in range(3)] for gi in range(8)]
    allc = np.concatenate([np.concatenate(gc) for gc in gcols])
    why = np.ascontiguousarray(win[:, allc])
    brow = bin_[allc][None, :].copy()
    hcols = np.concatenate([np.stack([sw[0, c_], sw[1, c_], sw[2, c_], sb[c_]], 1) for gc in gcols for c_ in gc], 1).astype(np.float32)
    fw3 = np.stack([np.ascontiguousarray(f(hy_w3)[0][:, :, gi * 64:gi * 64 + 64]) for gi in range(8)], 0)
    fcols = np.stack([np.stack([f(hy_freq1)[0], f(hy_b1)[0], f(hy_freq2)[0], f(hy_b2)[0], f(hy_decay)[0][:, gi * 64:gi * 64 + 64]], -1).transpose(1, 0, 2)
                      for gi in range(8)], 0).astype(np.float32)
    dskip = np.stack([f(hy_skip_d)[0][gi * 64:gi * 64 + 64][None, :] for gi in range(8)], 0)
    ccols = np.concatenate([colpack(bin_[1536:2048]), colpack(bin_[2048:2560]), colpack(f(cf_dw_b)[0]), colpack(f(cf_ln_g)[0]), colpack(f(cf_ln_b)[0]),
                            np.ascontiguousarray(f(cf_dw_w)[0].T.reshape(4, 128, 31).transpose(1, 0, 2).reshape(128, 124))], 1).astype(np.float32)
    wqb = f(mla_wq_b)[0].reshape(384, NH, 96)
    WqH = np.concatenate([wqb[:, :, 64:96], np.zeros((384, NH, 32), np.float32), wqb[:, :, 0:64]], -1).reshape(384, NH * 128)
    WqS = np.concatenate([wqb[:, :, 80:96], wqb[:, :, 64:80]], -1).reshape(384, NH * 32)
    wkvb = f(mla_wkv_b)[0].reshape(256, NH, 128)
    WkH = np.concatenate([np.zeros((256, NH, 64), np.float32), wkvb[:, :, 0:64]], -1).reshape(256, NH * 128)
    WvH = np.ascontiguousarray(wkvb[:, :, 64:128]).reshape(256, NH * 64)
    zp_p, zp_s = zpos_table(LP), zpos_table(LS)
    co, si = rope_cs(np.concatenate([np.arange(16, LP), np.arange(16)]))
    cs_all = np.concatenate([co, si], 1)
    shared = dict(ident=ident, xpad_p=xpad_p, maskpad=maskpad, valid_s=valid_s, zpos_p=zp_p, zpos_s=zp_s, fw1=f(hy_w1)[0], fw2=f(hy_w2)[0],
                  fw3=fw3, fcols=fcols, why=why, brow=brow, hcols=hcols, dskip=dskip, wconf=np.ascontiguousarray(win[:, 1536:2560]), ccols=ccols,
                  wout=f(ev_w_out)[0], bout=f(ev_b_out)[0:1], ln1g=f(ln1_g)[:, None, :], ln1b=f(ln1_b)[:, None, :],
                  ln2g=f(ln2_g)[:, None, :], ln2b=f(ln2_b)[:, None, :], w1=f(mlp_w1), w2=f(mlp_w2), wqa=f(mla_wq_a)[0], qg=f(mla_q_norm)[0:1],
                  WqH=WqH, WqS=WqS, wkva=f(mla_wkv_a)[0], kvg=f(mla_kv_norm)[0:1], cs_all=cs_all, WkH=WkH, WvH=WvH, wo=f(mla_wo)[0])
    for k, v in tabP.items():
        shared["tp_" + k] = v
    for k, v in tabS.items():
        shared["ts_" + k] = v
    P = build_nc()
    ims = []
    for c in range(8):
        m0 = 16 + 2048 * c
        xb, mb = make_xc(hs[c], 16, LS)
        css, Cqs, Sqs = [], [], []
        for pos in (np.concatenate([np.arange(m0, m0 + 2048), np.arange(16)]), np.concatenate([np.arange(16, LS), np.arange(16)])):
            co, si = rope_cs(pos)
            css.append(np.concatenate([co, si], 1))
            Cqs.append(np.concatenate([co[:2048].T, co[:2048].T], 0))
            Sqs.append(np.concatenate([-si[:2048].T, si[:2048].T], 0))
        tix = (2048 * c + 128 * np.arange(16)[None, :] + np.arange(128)[:, None]).astype(np.uint32)
        im = dict(shared)
        im.update(xh_s=np.concatenate([z1, hs[c], z1], 0), xc_s=xb, mask_s=mb, tokidx=tix,
                  cs=np.stack(css, 0), Cq=np.stack(Cqs, 0), Sq=np.stack(Sqs, 0))
        ims.append(check_inputs(P, im))
    r = run_bass_kernel_spmd(P.nc, ims, core_ids=list(range(8))).results
    y_prompt = np.concatenate([np.asarray(r[c]["out"])[0] for c in range(8)], 0)[None].astype(np.float32)
    y_sample = np.stack([np.asarray(r[c]["out"])[1] for c in range(8)], 0).astype(np.float32)
    return (y_prompt, y_sample)
```

```python
import math
from contextlib import ExitStack
import numpy as np
import ml_dtypes
import concourse.bass as bass
import concourse.mybir as mybir
from concourse.bass_utils import run_bass_kernel_spmd

F32 = mybir.dt.float32
BF16 = mybir.dt.bfloat16
AF = mybir.ActivationFunctionType
ALU = mybir.AluOpType
AX = mybir.AxisListType

D = 1024
NMETA = 16
DFF = 4096
ALPHA = 4 ** 0.25
LN_EPS = 1e-5
RMS_EPS = 1e-6
NH = 16


SEM_MAX = 24000


class Dep:
    __slots__ = ("w", "r")

    def __init__(self):
        self.w = None
        self.r = {}


class KB:
    def __init__(self, nc):
        self.nc = nc
        self.stack = ExitStack()
        self.raw = dict(pe=nc.tensor, act=nc.scalar, dve=nc.vector, pool=nc.gpsimd, sp=nc.sync)
        self.sem = {}
        self.cnt = {}
        self.seen = {e: {} for e in self.raw}
        self.semobj = []
        for e in ("pe", "act", "dve", "pool"):
            self.sem[e] = self._newsem("s_" + e)
            self.cnt[e] = 0
        self.dq = {}
        for q, n in (("sp", 20), ("act", 8), ("pool", 8)):
            self.dq[q] = dict(sems=[self._newsem(f"d_{q}{i}") for i in range(n)], vals=[0] * n, nxt=0)
        self.uid = 0

    def _newsem(self, name):
        s = self.stack.enter_context(self.nc.semaphore(name))
        self.semobj.append(s)
        return len(self.semobj) - 1

    def name(self, p):
        self.uid += 1
        return f"{p}{self.uid}"

    def sb(self, st, shape, dt, name="t"):
        return st.enter_context(self.nc.sbuf_tensor(self.name(name), list(shape), dt))

    def ps(self, st, shape, dt, name="p"):
        return st.enter_context(self.nc.psum_tensor(self.name(name), list(shape), dt))

    def _waits(self, eng, reads, writes, extra=None):
        need = {}

        def add(tok):
            if tok is None:
                return
            s, v, src = tok
            if src == "pe" and eng == "pe":
                return
            if need.get(s, 0) < v:
                need[s] = v

        for d in reads:
            add(d.w)
        for d in writes:
            add(d.w)
            for t in d.r.values():
                add(t)
        if extra:
            for t in extra:
                add(t)
        seen = self.seen[eng]
        for s, v in need.items():
            if seen.get(s, 0) < v:
                self.raw[eng].wait_ge(self.semobj[s], v)
                seen[s] = v

    def op(self, eng, fn, reads=(), writes=()):
        self._waits(eng, reads, writes)
        ins = fn(self.raw[eng])
        if self.cnt[eng] >= SEM_MAX:
            self.sem[eng] = self._newsem(self.name("s_" + eng))
            self.cnt[eng] = 0
        self.cnt[eng] += 1
        ins.then_inc(self.semobj[self.sem[eng]], 1)
        tok = (self.sem[eng], self.cnt[eng], eng)
        for d in reads:
            d.r[tok[0]] = tok
        for d in writes:
            d.w = tok
            d.r = {}
        return ins

    def dma(self, q, out, in_, reads=(), writes=(), **kw):
        dq = self.dq[q]
        i = dq["nxt"]
        dq["nxt"] = (i + 1) % len(dq["sems"])
        s = dq["sems"][i]
        extra = [(s, dq["vals"][i], "dma")] if dq["vals"][i] else None
        self._waits(q, reads, writes, extra)
        ins = self.raw[q].dma_start(out=out, in_=in_, **kw)
        dq["vals"][i] += 16
        ins.then_inc(self.semobj[s], 16)
        tok = (s, dq["vals"][i], "dma")
        for d in reads:
            d.r[s] = tok
        for d in writes:
            d.w = tok
            d.r = {}
        return ins

    def all_gather(self, in_ap, out_ap, reads=(), writes=()):
        if not hasattr(self, "ccsem"):
            self.ccsem = self._newsem("ccsem")
            self.ccval = 0
        self._waits("pool", reads, writes)
        ins = self.raw["pool"].collective_compute("AllGather", ALU.bypass, replica_groups=[list(range(8))],
                                                  ins=[in_ap.opt()], outs=[out_ap.opt()])
        self.ccval += 1
        ins.then_inc(self.semobj[self.ccsem], 1)
        tok = (self.ccsem, self.ccval, "cc")
        for d in reads:
            d.r[self.ccsem] = tok
        for d in writes:
            d.w = tok
            d.r = {}
        return ins

    def gather_rows(self, out, in_rows, idx, reads=(), writes=()):
        dq = self.dq["pool"]
        i = dq["nxt"]
        dq["nxt"] = (i + 1) % len(dq["sems"])
        s = dq["sems"][i]
        extra = [(s, dq["vals"][i], "dma")] if dq["vals"][i] else None
        self._waits("pool", reads, writes, extra)
        ins = self.raw["pool"].indirect_dma_start(out=out, out_offset=None, in_=in_rows,
                                                  in_offset=bass.IndirectOffsetOnAxis(ap=idx, axis=0))
        dq["vals"][i] += 16
        ins.then_inc(self.semobj[s], 16)
        tok = (s, dq["vals"][i], "dma")
        for d in reads:
            d.r[s] = tok
        for d in writes:
            d.w = tok
            d.r = {}
        return ins

    def barrier(self):
        toks = [(self.sem[e], self.cnt[e], e) for e in ("pe", "act", "dve", "pool") if self.cnt[e]]
        for q in self.dq.values():
            for s, v in zip(q["sems"], q["vals"]):
                if v:
                    toks.append((s, v, "dma"))
        if getattr(self, "ccval", 0):
            toks.append((self.ccsem, self.ccval, "cc"))
        for eng in ("pe", "act", "dve", "pool", "sp"):
            seen = self.seen[eng]
            for s, v, src in toks:
                if seen.get(s, 0) < v and not (s == self.sem.get(eng)):
                    self.raw[eng].wait_ge(self.semobj[s], v)
                    seen[s] = v

    def finish_wait(self):
        for q in self.dq.values():
            for s, v in zip(q["sems"], q["vals"]):
                if v and self.seen["sp"].get(s, 0) < v:
                    self.raw["sp"].wait_ge(self.semobj[s], v)
                    self.seen["sp"][s] = v


class Glob:
    pass


def setup_globals(kb, st):
    g = Glob()
    nc = kb.nc
    g.pall = kb.ps(st, [128, 8, 512], F32, "banks")
    g.psum = [g.pall[:, b, :] for b in range(8)]
    g.pd = [Dep() for _ in range(8)]
    g.ident_f = kb.sb(st, [128, 128], F32, "identf")
    g.ident_b = kb.sb(st, [128, 128], BF16, "identb")
    g.ident_d = Dep()
    g.ones_b = kb.sb(st, [128, 128], BF16, "onesb")
    g.ones_d = Dep()
    g.bk = -1
    return g


def load_ident(kb, g, ident_dram):
    kb.dma("sp", g.ident_f[:], ident_dram, writes=[g.ident_d])
    kb.op("dve", lambda e: e.tensor_copy(out=g.ident_b[:], in_=g.ident_f[:]), reads=[g.ident_d], writes=[g.ident_d])
    kb.op("pool", lambda e: e.memset(g.ones_b[:], 1.0), writes=[g.ones_d])


_rr = [0]


def cast_eng():
    _rr[0] += 1
    return ("dve", "pool", "act")[_rr[0] % 3]


def copy_op(kb, eng, out, in_, reads, writes):
    if eng == "act":
        return kb.op("act", lambda e: e.copy(out=out, in_=in_), reads=reads, writes=writes)
    return kb.op(eng, lambda e: e.tensor_copy(out=out, in_=in_), reads=reads, writes=writes)


def load_weight_bf16(kb, st_phase, dst, dst_dep, src, kc, ncols, stage_cols=2048):
    with ExitStack() as st:
        stg = [kb.sb(st, [128, stage_cols], F32, "wstg") for _ in range(3)]
        sd = [Dep() for _ in range(3)]
        i = 0
        for k in range(kc):
            for c0 in range(0, ncols, stage_cols):
                cn = min(stage_cols, ncols - c0)
                j = i % 3
                kb.dma("sp" if i % 2 == 0 else "pool", stg[j][:, :cn], src[k * 128:(k + 1) * 128, c0:c0 + cn], writes=[sd[j]])
                copy_op(kb, ("dve", "act")[i % 2], dst[:, k, c0:c0 + cn], stg[j][:, :cn], [sd[j]], [dst_dep])
                i += 1
        kb.barrier()


def load_bcast(kb, dst, dep, src_row):
    kb.dma("sp", dst, src_row.partition_broadcast(128) if len(src_row.shape) == 1 else src_row.broadcast_to([128, src_row.shape[-1]]), writes=[dep])


def layer_norm_tile(kb, r, rd, n, gt, bt, gbd, out, outd, small, smd, junk, junkd):
    s1, s2 = small[:, 0:1], small[:, 1:2]
    kb.op("act", lambda e: e.activation(out=junk[:n, :], in_=r[:n, :], func=AF.Identity, accum_out=s1[:n, :]), reads=[rd], writes=[junkd, smd])
    kb.op("act", lambda e: e.activation(out=junk[:n, :], in_=r[:n, :], func=AF.Square, accum_out=s2[:n, :]), reads=[rd], writes=[junkd, smd])
    mean, var, rstd = small[:, 2:3], small[:, 3:4], small[:, 4:5]
    kb.op("dve", lambda e: e.tensor_scalar(out=mean[:n, :], in0=s1[:n, :], scalar1=1.0 / D, scalar2=None, op0=ALU.mult), reads=[smd], writes=[smd])
    kb.op("dve", lambda e: e.tensor_tensor(out=var[:n, :], in0=mean[:n, :], in1=mean[:n, :], op=ALU.mult), reads=[smd], writes=[smd])
    kb.op("dve", lambda e: e.scalar_tensor_tensor(out=var[:n, :], in0=s2[:n, :], scalar=1.0 / D, in1=var[:n, :], op0=ALU.mult, op1=ALU.subtract), reads=[smd], writes=[smd])
    kb.op("act", lambda e: e.activation(out=rstd[:n, :], in_=var[:n, :], func=AF.Sqrt, bias=LN_EPS, scale=1.0), reads=[smd], writes=[smd])
    kb.op("dve", lambda e: e.reciprocal(out=rstd[:n, :], in_=rstd[:n, :]), reads=[smd], writes=[smd])
    kb.op("dve", lambda e: e.tensor_scalar(out=r[:n, :], in0=r[:n, :], scalar1=mean[:n, :], scalar2=rstd[:n, :], op0=ALU.subtract, op1=ALU.mult), reads=[smd, rd], writes=[rd])
    kb.op("pool", lambda e: e.tensor_tensor(out=r[:n, :], in0=r[:n, :], in1=gt[:n, :], op=ALU.mult), reads=[rd, gbd], writes=[rd])
    kb.op("pool", lambda e: e.tensor_tensor(out=out[:n, :], in0=r[:n, :], in1=bt[:n, :], op=ALU.add), reads=[rd, gbd], writes=[outd])


def mm(kb, out, lhsT, rhs, start, stop, reads, writes):
    return kb.op("pe", lambda e: e.matmul(out, lhsT=lhsT, rhs=rhs, start=start, stop=stop), reads, writes)


def tt(kb, eng, out, in0, in1, op, reads, writes):
    return kb.op(eng, lambda e: e.tensor_tensor(out=out, in0=in0, in1=in1, op=op), reads, writes)


def ts(kb, eng, out, in0, s1, s2, op0, op1, reads, writes):
    if s2 is None:
        return kb.op(eng, lambda e: e.tensor_scalar(out=out, in0=in0, scalar1=s1, scalar2=None, op0=op0), reads, writes)
    return kb.op(eng, lambda e: e.tensor_scalar(out=out, in0=in0, scalar1=s1, scalar2=s2, op0=op0, op1=op1), reads, writes)


def stt(kb, eng, out, in0, scalar, in1, op0, op1, reads, writes):
    return kb.op("dve", lambda e: e.scalar_tensor_tensor(out=out, in0=in0, scalar=scalar, in1=in1, op0=op0, op1=op1), reads, writes)


def act(kb, out, in_, func, reads, writes, **kw):
    return kb.op("act", lambda e: e.activation(out=out, in_=in_, func=func, **kw), reads, writes)


def nextbank(g):
    g.bk = (g.bk + 1) % 8
    return g.bk


def transpose_tile(kb, g, src, srcd, n, dstT, dstd, col0, kc=8):
    for k0 in range(0, kc, 4):
        b = nextbank(g)
        kn = min(4, kc - k0)
        pv = g.psum[b][:, :].rearrange("p (k t) -> p k t", k=4)
        for k in range(kn):
            kb.op("pe", lambda e, k=k: e.transpose(pv[:, k, :n], src[:n, (k0 + k) * 128:(k0 + k + 1) * 128], g.ident_f[:n, :n]),
                  reads=[srcd, g.ident_d], writes=[g.pd[b]])
        copy_op(kb, ("dve", "act")[b % 2], dstT[:, k0:k0 + kn, col0:col0 + n], pv[:, 0:kn, :n], [g.pd[b]], [dstd])


def phase_proj_ln(kb, g, tiles, fm, W, bias, lng, lnb):
    with ExitStack() as st:
        Wb = kb.sb(st, [128, 8, D], BF16, "Wb")
        Wd = Dep()
        load_weight_bf16(kb, st, Wb, Wd, W, 8, D)
        gt = kb.sb(st, [128, D], F32, "g")
        bt = kb.sb(st, [128, D], F32, "b")
        gbd = Dep()
        load_bcast(kb, gt[:], gbd, lng)
        load_bcast(kb, bt[:], gbd, lnb)
        if bias is not None:
            bi = kb.sb(st, [128, D], F32, "bias")
            load_bcast(kb, bi[:], gbd, bias)
        NB = 2
        hb = [kb.sb(st, [128, D], F32, "h") for _ in range(NB)]
        hd = [Dep() for _ in range(NB)]
        yT = [kb.sb(st, [128, 8, 128], BF16, "yT") for _ in range(NB)]
        yTd = [Dep() for _ in range(NB)]
        if not fm:
            yb = [kb.sb(st, [128, D], F32, "y") for _ in range(NB)]
            yd = [Dep() for _ in range(NB)]
        rb = [kb.sb(st, [128, D], F32, "r") for _ in range(NB)]
        rd = [Dep() for _ in range(NB)]
        junk = kb.sb(st, [128, D], F32, "junk")
        junkd = Dep()
        small = [kb.sb(st, [128, 8], F32, "small") for _ in range(NB)]
        smd = [Dep() for _ in range(NB)]
        for i, (hap, yap, oap, n) in enumerate(tiles):
            j = i % NB
            kb.dma("sp", hb[j][:n, :], hap, writes=[hd[j]])
            if fm:
                for qi, (ksl, src, dep) in enumerate(yap):
                    kb.dma(("pool", "sp")[qi % 2], yT[j][:, ksl, :n], src, reads=[dep] if dep is not None else [], writes=[yTd[j]])
            else:
                kb.dma("pool", yb[j][:n, :], yap, writes=[yd[j]])
                transpose_tile(kb, g, yb[j], yd[j], n, yT[j], yTd[j], 0)
            bks = (nextbank(g), nextbank(g))
            for half, bk in enumerate(bks):
                for k in range(8):
                    mm(kb, g.psum[bk][:n, :], yT[j][:, k, :n], Wb[:, k, half * 512:(half + 1) * 512], k == 0, k == 7,
                       [yTd[j], Wd], [g.pd[bk]])
            for half, bk in enumerate(bks):
                sl = slice(half * 512, (half + 1) * 512)
                if bias is not None:
                    tt(kb, "dve", rb[j][:n, sl], g.psum[bk][:n, :], bi[:n, sl], ALU.add, [g.pd[bk], gbd], [rd[j]])
                else:
                    copy_op(kb, "act", rb[j][:n, sl], g.psum[bk][:n, :], [g.pd[bk]], [rd[j]])
            stt(kb, "pool", rb[j][:n, :], hb[j][:n, :], ALPHA, rb[j][:n, :], ALU.mult, ALU.add, [hd[j], rd[j]], [rd[j]])
            layer_norm_tile(kb, rb[j], rd[j], n, gt, bt, gbd, rb[j], rd[j], small[j], smd[j], junk, junkd)
            kb.dma("sp", oap, rb[j][:n, :], reads=[rd[j]])
        kb.barrier()


def phase_mlp_ln(kb, g, tiles, W1, W2, lng, lnb):
    with ExitStack() as st:
        W1b = kb.sb(st, [128, 8, DFF], BF16, "W1b")
        W2b = kb.sb(st, [128, 32, D], BF16, "W2b")
        Wd = Dep()
        load_weight_bf16(kb, st, W1b, Wd, W1, 8, DFF)
        load_weight_bf16(kb, st, W2b, Wd, W2, 32, D, stage_cols=1024)
        gt = kb.sb(st, [128, D], F32, "g")
        bt = kb.sb(st, [128, D], F32, "b")
        gbd = Dep()
        load_bcast(kb, gt[:], gbd, lng)
        load_bcast(kb, bt[:], gbd, lnb)
        hb = [kb.sb(st, [128, D], F32, "h") for _ in range(4)]
        hd = [Dep() for _ in range(4)]
        hT = kb.sb(st, [128, 8, 512], BF16, "hT")
        hTd = Dep()
        uT = kb.sb(st, [128, 32, 512], BF16, "uT")
        uTd = [Dep() for _ in range(32)]
        rl = [kb.sb(st, [128, 512], F32, "relu") for _ in range(2)]
        rld = [Dep() for _ in range(2)]
        junk = kb.sb(st, [128, D], BF16, "junk")
        junkd = Dep()
        small = [kb.sb(st, [128, 8], F32, "small") for _ in range(4)]
        smd = [Dep() for _ in range(4)]
        for s0 in range(0, len(tiles), 4):
            grp = tiles[s0:s0 + 4]
            offs = []
            tot = 0
            for i, (iap, oap, n) in enumerate(grp):
                kb.dma("sp" if i % 2 == 0 else "pool", hb[i][:n, :], iap, writes=[hd[i]])
                offs.append(tot)
                tot += n
            for i, (iap, oap, n) in enumerate(grp):
                transpose_tile(kb, g, hb[i], hd[i], n, hT, hTd, offs[i])
            for j in range(32):
                bk = nextbank(g)
                for k in range(8):
                    mm(kb, g.psum[bk][:, :tot], W1b[:, k, j * 128:(j + 1) * 128], hT[:, k, :tot], k == 0, k == 7, [Wd, hTd], [g.pd[bk]])
                q = j % 2
                act(kb, rl[q][:, :tot], g.psum[bk][:, :tot], AF.Relu, [g.pd[bk]], [rld[q]])
                tt(kb, "pool" if j % 4 < 3 else "dve", uT[:, j, :tot], rl[q][:, :tot], rl[q][:, :tot], ALU.mult, [rld[q]], [uTd[j]])
            for i, (iap, oap, n) in enumerate(grp):
                bks = (nextbank(g), nextbank(g))
                for half, bk in enumerate(bks):
                    for j in range(32):
                        mm(kb, g.psum[bk][:n, :], uT[:, j, offs[i]:offs[i] + n], W2b[:, j, half * 512:(half + 1) * 512], j == 0, j == 31,
                           [uTd[j], Wd], [g.pd[bk]])
                for half, bk in enumerate(bks):
                    sl = slice(half * 512, (half + 1) * 512)
                    stt(kb, "dve", hb[i][:n, sl], hb[i][:n, sl], ALPHA, g.psum[bk][:n, :], ALU.mult, ALU.add, [hd[i], g.pd[bk]], [hd[i]])
                layer_norm_tile(kb, hb[i], hd[i], n, gt, bt, gbd, hb[i], hd[i], small[i], smd[i], junk, junkd)
                kb.dma("sp", oap, hb[i][:n, :], reads=[hd[i]])
        kb.barrier()


def rms_rstd(kb, src, srcd, n, width, small, smd, junk, junkd, col):
    ss, rs = small[:, col:col + 1], small[:, col + 1:col + 2]
    act(kb, junk[:n, :width], src[:n, :width], AF.Square, [srcd], [junkd, smd], accum_out=ss[:n, :])
    act(kb, rs[:n, :], ss[:n, :], AF.Sqrt, [smd], [smd], bias=RMS_EPS, scale=1.0 / width)
    kb.op("dve", lambda e: e.reciprocal(out=rs[:n, :], in_=rs[:n, :]), [smd], [smd])
    return rs


def phase_qkv(kb, g, seqs, wqa, qg, WqH, WqS, wkva, kvg):
    with ExitStack() as st:
        wqa_b = kb.sb(st, [128, 8, 384], BF16, "wqa")
        wkva_b = kb.sb(st, [128, 8, 288], BF16, "wkva")
        wqh_b = kb.sb(st, [128, 3, NH * 128], BF16, "wqh")
        wqs_b = kb.sb(st, [128, 3, NH * 32], BF16, "wqs")
        Wd = Dep()
        load_weight_bf16(kb, st, wqa_b, Wd, wqa, 8, 384)
        load_weight_bf16(kb, st, wkva_b, Wd, wkva, 8, 288)
        load_weight_bf16(kb, st, wqh_b, Wd, WqH, 3, NH * 128)
        load_weight_bf16(kb, st, wqs_b, Wd, WqS, 3, NH * 32)
        qgt = kb.sb(st, [128, 384], F32, "qg")
        kvgt = kb.sb(st, [128, 256], F32, "kvg")
        gd = Dep()
        load_bcast(kb, qgt[:], gd, qg)
        load_bcast(kb, kvgt[:], gd, kvg)
        hb = [kb.sb(st, [128, D], F32, "h") for _ in range(4)]
        hd = [Dep() for _ in range(4)]
        hT = kb.sb(st, [128, 8, 512], BF16, "hT")
        hTd = Dep()
        cq = [kb.sb(st, [128, 384], F32, "cq") for _ in range(2)]
        cqd = [Dep() for _ in range(2)]
        cqT = kb.sb(st, [128, 3, 512], BF16, "cqT")
        cqTd = Dep()
        kvr = [kb.sb(st, [128, 288], F32, "kvr") for _ in range(2)]
        kvrd = [Dep() for _ in range(2)]
        kvo = [kb.sb(st, [128, 288], F32, "kvo") for _ in range(2)]
        kvod = [Dep() for _ in range(2)]
        cst = [kb.sb(st, [128, 32], F32, "cs") for _ in range(2)]
        csd = [Dep() for _ in range(2)]
        tmp = [kb.sb(st, [128, 64], F32, "tmp") for _ in range(2)]
        tmpd = [Dep() for _ in range(2)]
        junk = kb.sb(st, [128, 384], F32, "junk")
        junkd = Dep()
        small = [kb.sb(st, [128, 8], F32, "small") for _ in range(2)]
        smd = [Dep() for _ in range(2)]
        Ct = kb.sb(st, [32, 2048], F32, "C")
        St = kb.sb(st, [32, 2048], F32, "S")
        CSd = Dep()
        qsw = [kb.sb(st, [32, 512], F32, "qsw") for _ in range(2)]
        qswd = [Dep() for _ in range(2)]
        qo = [kb.sb(st, [128, 512], BF16, "qo") for _ in range(2)]
        qod = [Dep() for _ in range(2)]
        it = 0
        for sq in seqs:
            if not sq.get("kv_only"):
                kb.dma("sp", Ct[:], sq["CS"][0], writes=[CSd])
                kb.dma("sp", St[:], sq["CS"][1], writes=[CSd])
            tiles = sq["tiles"]
            for s0 in range(0, len(tiles), 4):
                grp = tiles[s0:s0 + 4]
                offs, tot = [], 0
                for i, (hap, kvap, csap, n) in enumerate(grp):
                    kb.dma("sp" if i % 2 == 0 else "pool", hb[i][:n, :], hap, writes=[hd[i]])
                    offs.append(tot)
                    tot += n
                for i, (hap, kvap, csap, n) in enumerate(grp):
                    transpose_tile(kb, g, hb[i], hd[i], n, hT, hTd, offs[i])
                is_main = (tot == 512) and not sq.get("kv_only")
                for i, (hap, kvap, csap, n) in enumerate(grp):
                    j = it % 2
                    it += 1
                    kb.dma("pool", cst[j][:n, :], csap, writes=[csd[j]])
                    bk = nextbank(g)
                    for k in range(8):
                        mm(kb, g.psum[bk][:n, :288], hT[:, k, offs[i]:offs[i] + n], wkva_b[:, k, :], k == 0, k == 7, [hTd, Wd], [g.pd[bk]])
                    copy_op(kb, "act", kvr[j][:n, :], g.psum[bk][:n, :288], [g.pd[bk]], [kvrd[j]])
                    rs = rms_rstd(kb, kvr[j], kvrd[j], n, 256, small[j], smd[j], junk, junkd, 0)
                    stt(kb, "dve", kvo[j][:n, 0:256], kvr[j][:n, 0:256], rs[:n, :], kvgt[:n, :], ALU.mult, ALU.mult, [kvrd[j], smd[j], gd], [kvod[j]])
                    x1, x2 = kvr[j][:n, 256:272], kvr[j][:n, 272:288]
                    co, si = cst[j][:n, 0:16], cst[j][:n, 16:32]
                    t = tmp[j]
                    tt(kb, "pool", t[:n, 0:16], x1, co, ALU.mult, [kvrd[j], csd[j]], [tmpd[j]])
                    tt(kb, "pool", t[:n, 16:32], x2, si, ALU.mult, [kvrd[j], csd[j]], [tmpd[j]])
                    tt(kb, "pool", t[:n, 32:48], x1, si, ALU.mult, [kvrd[j], csd[j]], [tmpd[j]])
                    tt(kb, "pool", t[:n, 48:64], x2, co, ALU.mult, [kvrd[j], csd[j]], [tmpd[j]])
                    tt(kb, "dve", kvo[j][:n, 256:272], t[:n, 0:16], t[:n, 16:32], ALU.subtract, [tmpd[j]], [kvod[j]])
                    tt(kb, "dve", kvo[j][:n, 272:288], t[:n, 32:48], t[:n, 48:64], ALU.add, [tmpd[j]], [kvod[j]])
                    kb.dma("sp", kvap, kvo[j][:n, :], reads=[kvod[j]])
                    if not is_main:
                        continue
                    bk = nextbank(g)
                    for k in range(8):
                        mm(kb, g.psum[bk][:n, :384], hT[:, k, offs[i]:offs[i] + n], wqa_b[:, k, :], k == 0, k == 7, [hTd, Wd], [g.pd[bk]])
                    copy_op(kb, "act", cq[j][:n, :], g.psum[bk][:n, :384], [g.pd[bk]], [cqd[j]])
                    rs = rms_rstd(kb, cq[j], cqd[j], n, 384, small[j], smd[j], junk, junkd, 2)
                    stt(kb, "dve", cq[j][:n, :], cq[j][:n, :], rs[:n, :], qgt[:n, :], ALU.mult, ALU.mult, [cqd[j], smd[j], gd], [cqd[j]])
                    transpose_tile(kb, g, cq[j], cqd[j], n, cqT, cqTd, offs[i], kc=3)
                if not is_main:
                    continue
                q0 = (s0 // 4) * 512
                for h in range(NH):
                    j = h % 2
                    bka, bkb = nextbank(g), nextbank(g)
                    for k in range(3):
                        mm(kb, g.psum[bka][:, :], wqh_b[:, k, h * 128:(h + 1) * 128], cqT[:, k, :], k == 0, k == 2, [Wd, cqTd], [g.pd[bka]])
                    for k in range(3):
                        mm(kb, g.psum[bkb][:32, :], wqs_b[:, k, h * 32:(h + 1) * 32], cqT[:, k, :], k == 0, k == 2, [Wd, cqTd], [g.pd[bkb]])
                    tt(kb, "dve", qsw[j][:, :], g.psum[bkb][:32, :], St[:, q0:q0 + 512], ALU.mult, [g.pd[bkb], CSd], [qswd[j]])
                    rope_q(kb, g, qo[j], qod[j], bka, qsw[j], qswd[j], Ct, CSd, q0, st, small)
                    copy_op(kb, "act", qo[j][32:64, :], g.psum[bka][32:64, :], [g.pd[bka]], [qod[j]])
                    copy_op(kb, "act", qo[j][64:128, :], g.psum[bka][64:128, :], [g.pd[bka]], [qod[j]])
                    kb.dma("pool", sq["qt"](h, q0), qo[j][:, :], reads=[qod[j]])
        kb.barrier()


_ropetmp = {}


def rope_q(kb, g, qo, qod, bka, qsw, qswd, Ct, CSd, q0, st, small):
    key = id(st)
    if key not in _ropetmp:
        _ropetmp[key] = (kb.sb(st, [32, 512], F32, "rq"), Dep())
    t, td = _ropetmp[key]
    tt(kb, "dve", t[:, :], g.psum[bka][0:32, :], Ct[:, q0:q0 + 512], ALU.mult, [g.pd[bka], CSd], [td])
    tt(kb, "pool", qo[0:32, :], t[:, :], qsw[:, :], ALU.add, [td, qswd], [qod])


QK_SCALE = 96 ** -0.5


def phase_attn(kb, g, seqs, WkH, WvH):
    NKmax = max(sum(n for _, n in sq["kchunks"]) for sq in seqs)
    NCH = max(len(sq["kchunks"]) for sq in seqs)
    with ExitStack() as st:
        wk_b = kb.sb(st, [128, 2, NH * 128], BF16, "wk")
        wv_b = kb.sb(st, [128, 2, NH * 64], BF16, "wv")
        Wd = Dep()
        load_weight_bf16(kb, st, wk_b, Wd, WkH, 2, NH * 128)
        load_weight_bf16(kb, st, wv_b, Wd, WvH, 2, NH * 64, stage_cols=1024)
        ckvT = kb.sb(st, [128, 3, NKmax], BF16, "ckvT")
        ckvTd = Dep()
        KT = kb.sb(st, [128, NKmax], BF16, "KT")
        KTd = Dep()
        V = kb.sb(st, [128, NCH, 66], BF16, "V")
        Vd = Dep()
        QT = [kb.sb(st, [128, 2048], BF16, "QT") for _ in range(2)]
        QTd = [Dep() for _ in range(2)]
        P = [kb.sb(st, [128, 1024], BF16, "P") for _ in range(2)]
        Pd = [Dep() for _ in range(2)]
        oT = kb.sb(st, [128, 1024], F32, "oT")
        oTd = Dep()
        osm = [kb.sb(st, [128, 4, 64], F32, "osm") for _ in range(2)]
        osmd = [Dep() for _ in range(2)]
        rec = [kb.sb(st, [128, 4, 1], F32, "rec") for _ in range(2)]
        recd = [Dep() for _ in range(2)]
        kvin = [kb.sb(st, [128, 288], F32, "kvin") for _ in range(2)]
        kvind = [Dep() for _ in range(2)]
        kb.op("pool", lambda e: e.memset(V[:, :, 64:66], 1.0), [], [Vd])
        for sq in seqs:
            chunks = sq["kchunks"]
            NK = sum(n for _, n in chunks)
            coff = []
            c0 = 0
            for ci, (kvap, n) in enumerate(chunks):
                j = ci % 2
                kb.dma("sp" if ci % 2 == 0 else "pool", kvin[j][:n, :], kvap, writes=[kvind[j]])
                b = nextbank(g)
                pv = g.psum[b].rearrange("p (k t) -> p k t", k=4)
                for k, w in ((0, 128), (1, 128), (2, 32)):
                    kb.op("pe", lambda e, k=k, w=w: e.transpose(pv[:w, k, :n], kvin[j][:n, k * 128:k * 128 + w], g.ident_f[:n, :n]),
                          [kvind[j], g.ident_d], [g.pd[b]])
                copy_op(kb, "dve", ckvT[:, 0:2, c0:c0 + n], pv[:, 0:2, :n], [g.pd[b]], [ckvTd])
                copy_op(kb, "act", ckvT[0:32, 2, c0:c0 + n], pv[0:32, 2, :n], [g.pd[b]], [ckvTd])
                coff.append(c0)
                c0 += n
            for h in range(NH):
                qj = h % 2
                kb.dma("sp", QT[qj][:, :], sq["qt"](h), writes=[QTd[qj]])
                for bi, k0 in enumerate(range(0, NK, 512)):
                    kn = min(512, NK - k0)
                    b = 6 + bi % 2
                    mm(kb, g.psum[b][:, :kn], wk_b[:, 0, h * 128:(h + 1) * 128], ckvT[:, 0, k0:k0 + kn], True, False, [Wd, ckvTd], [g.pd[b]])
                    mm(kb, g.psum[b][:, :kn], wk_b[:, 1, h * 128:(h + 1) * 128], ckvT[:, 1, k0:k0 + kn], False, False, [Wd, ckvTd], [g.pd[b]])
                    mm(kb, g.psum[b][:, :kn], g.ident_b[0:32, :], ckvT[0:32, 2, k0:k0 + kn], False, True, [g.ident_d, ckvTd], [g.pd[b]])
                    copy_op(kb, ("dve", "pool")[bi % 2] if False else "dve", KT[:, k0:k0 + kn], g.psum[b][:, :kn], [g.pd[b]], [KTd])
                for gi, cg in enumerate(range(0, len(chunks), 8)):
                    cn = min(8, len(chunks) - cg)
                    b = 6 + gi % 2
                    for ci in range(cn):
                        n = chunks[cg + ci][1]
                        o = coff[cg + ci]
                        for k in range(2):
                            mm(kb, g.psum[b][:n, ci * 64:(ci + 1) * 64], ckvT[:, k, o:o + n], wv_b[:, k, h * 64:(h + 1) * 64], k == 0, k == 1,
                               [ckvTd, Wd], [g.pd[b]])
                    copy_op(kb, "act", V[:, cg:cg + cn, 0:64], g.psum[b][:, :cn * 64].rearrange("p (c d) -> p c d", d=64), [g.pd[b]], [Vd])
                for qsb in range(2):
                    for ci, (kvap, n) in enumerate(chunks):
                        o = coff[ci]
                        sb0 = 2 + 2 * (ci % 2)
                        pj = ci % 2
                        for i in range(2):
                            mm(kb, g.psum[sb0 + i][:n, :], KT[:, o:o + n], QT[qj][:, qsb * 1024 + i * 512:qsb * 1024 + (i + 1) * 512], True, True,
                               [KTd, QTd[qj]], [g.pd[sb0 + i]])
                        act(kb, P[pj][:n, :].rearrange("p (a b) -> p a b", a=2), g.pall[:n, sb0:sb0 + 2, :], AF.Exp,
                            [g.pd[sb0], g.pd[sb0 + 1]], [Pd[pj]], scale=QK_SCALE)
                        for i in range(2):
                            mm(kb, g.psum[i][:65, :], V[:n, ci, 0:65], P[pj][:n, i * 512:(i + 1) * 512], ci == 0, ci == len(chunks) - 1,
                               [Vd, Pd[pj]], [g.pd[i]])
                    copy_op(kb, "dve", oT[:65, :].rearrange("p (a b) -> p a b", a=2), g.pall[:65, 0:2, :], [g.pd[0], g.pd[1]], [oTd])
                    for half in range(2):
                        b = 6 + half
                        oj = half
                        pv = g.psum[b][:, 0:4 * 65].rearrange("p (t c) -> p t c", c=65)
                        for t in range(4):
                            q0 = half * 512 + t * 128
                            kb.op("pe", lambda e, t=t, q0=q0: e.transpose(pv[:, t, :], oT[:65, q0:q0 + 128], g.ident_f[:65, :65]),
                                  [oTd, g.ident_d], [g.pd[b]])
                        kb.op("dve", lambda e: e.reciprocal(out=rec[oj][:, :, :], in_=pv[:, :, 64:65]), [g.pd[b]], [recd[oj]])
                        tt(kb, "dve", osm[oj][:, :, :], pv[:, :, 0:64], rec[oj][:, :, :].broadcast_to([128, 4, 64]), ALU.mult,
                           [g.pd[b], recd[oj]], [osmd[oj]])
                        kb.dma("pool", sq["o"](qsb, half, h), osm[oj][:, :, :], reads=[osmd[oj]])
        kb.barrier()


XC = 2124


def phase_conf(kb, g, seqs, w_conf, cols_ap):
    with ExitStack() as st:
        wb = kb.sb(st, [128, 8, 1024], BF16, "wconf")
        Wd = Dep()
        load_weight_bf16(kb, st, wb, Wd, w_conf, 8, 1024)
        cols = kb.sb(st, [128, 20 + 124], F32, "cols")
        cd = Dep()
        kb.dma("sp", cols[:], cols_ap, writes=[cd])
        Dg = kb.sb(st, [128, 4, 31, 128], BF16, "Dg")
        Dgd = Dep()
        for j in range(4):
            for k in range(31):
                ts(kb, ("dve", "pool")[k % 2], Dg[:, j, k, :], g.ident_f[:, :], cols[:, 20 + j * 31 + k:20 + j * 31 + k + 1], None, ALU.mult, None,
                   [g.ident_d, cd], [Dgd])
        xin = [kb.sb(st, [128, D], F32, "xin") for _ in range(2)]
        xind = [Dep() for _ in range(2)]
        xT = kb.sb(st, [128, 8, XC], BF16, "xT")
        xTd = Dep()
        hT = kb.sb(st, [128, 4, XC], BF16, "hT")
        hTd = Dep()
        mask = kb.sb(st, [128, XC], F32, "mask")
        maskd = Dep()
        sg = [kb.sb(st, [128, 512], F32, "sg") for _ in range(2)]
        sgd = [Dep() for _ in range(2)]
        cc = kb.sb(st, [128, 4, 512], F32, "cc")
        ccd = Dep()
        cb = kb.sb(st, [128, 4, 512], BF16, "cb")
        cbd = Dep()
        sq = kb.sb(st, [128, 4, 512], BF16, "sq")
        sqd = Dep()
        mean = kb.sb(st, [128, 512], F32, "mean")
        rstd = kb.sb(st, [128, 512], F32, "rstd")
        std = Dep()
        yt = [kb.sb(st, [128, 512], F32, "yt") for _ in range(2)]
        ytd = [Dep() for _ in range(2)]
        yo = [kb.sb(st, [128, 512], BF16, "yo") for _ in range(2)]
        yod = [Dep() for _ in range(2)]
        for s_ in seqs:
            NC = s_.get("ncols", XC)
            kb.dma("sp", mask[:, :NC], s_["mask"].broadcast_to([128, NC]), writes=[maskd])
            for ti, t0 in enumerate(range(0, NC, 128)):
                n = min(128, NC - t0)
                j = ti % 2
                kb.dma("sp" if ti % 2 == 0 else "pool", xin[j][:n, :], s_["x"][t0:t0 + n, :], writes=[xind[j]])
                transpose_tile(kb, g, xin[j], xind[j], n, xT, xTd, t0)
            for bi, c0 in enumerate(range(0, NC, 512)):
                cn = min(512, NC - c0)
                for j in range(4):
                    ba, bg = nextbank(g), nextbank(g)
                    for k in range(8):
                        mm(kb, g.psum[ba][:, :cn], wb[:, k, j * 128:(j + 1) * 128], xT[:, k, c0:c0 + cn], k == 0, k == 7, [Wd, xTd], [g.pd[ba]])
                    for k in range(8):
                        mm(kb, g.psum[bg][:, :cn], wb[:, k, 512 + j * 128:512 + (j + 1) * 128], xT[:, k, c0:c0 + cn], k == 0, k == 7, [Wd, xTd], [g.pd[bg]])
                    q = j % 2
                    act(kb, sg[q][:, :cn], g.psum[bg][:, :cn], AF.Sigmoid, [g.pd[bg], cd], [sgd[q]], bias=cols[:, 4 + j:5 + j], scale=1.0)
                    stt(kb, "dve", sg[q][:, :cn], g.psum[ba][:, :cn], cols[:, j:j + 1], sg[q][:, :cn], ALU.add, ALU.mult, [g.pd[ba], cd, sgd[q]], [sgd[q]])
                    tt(kb, "pool", hT[:, j, c0:c0 + cn], sg[q][:, :cn], mask[:, c0:c0 + cn], ALU.mult, [sgd[q], maskd], [hTd])
            blocks = s_.get("blocks") or ([(15, 16)] + [(61 + 512 * i, 512) for i in range(4)])
            for bi, (c0, cn) in enumerate(blocks):
                for j in range(4):
                    b = nextbank(g)
                    for k in range(31):
                        mm(kb, g.psum[b][:, :cn], Dg[:, j, k, :], hT[:, j, c0 + k - 15:c0 + k - 15 + cn], k == 0, k == 30, [Dgd, hTd], [g.pd[b]])
                    act(kb, cc[:, j, :cn], g.psum[b][:, :cn], AF.Identity, [g.pd[b], cd], [ccd], bias=cols[:, 8 + j:9 + j], scale=1.0)
                    copy_op(kb, "pool", cb[:, j, :cn], cc[:, j, :cn], [ccd], [cbd])
                    tt(kb, "dve", sq[:, j, :cn], cc[:, j, :cn], cc[:, j, :cn], ALU.mult, [ccd], [sqd])
                b1, b2 = nextbank(g), nextbank(g)
                for j in range(4):
                    mm(kb, g.psum[b1][:, :cn], g.ones_b[:, :], cb[:, j, :cn], j == 0, j == 3, [g.ones_d, cbd], [g.pd[b1]])
                for j in range(4):
                    mm(kb, g.psum[b2][:, :cn], g.ones_b[:, :], sq[:, j, :cn], j == 0, j == 3, [g.ones_d, sqd], [g.pd[b2]])
                act(kb, mean[:, :cn], g.psum[b1][:, :cn], AF.Copy, [g.pd[b1]], [std], scale=1.0 / 512)
                tt(kb, "pool", rstd[:, :cn], mean[:, :cn], mean[:, :cn], ALU.mult, [std], [std])
                stt(kb, "dve", rstd[:, :cn], g.psum[b2][:, :cn], 1.0 / 512, rstd[:, :cn], ALU.mult, ALU.subtract, [g.pd[b2], std], [std])
                act(kb, rstd[:, :cn], rstd[:, :cn], AF.Sqrt, [std], [std], bias=LN_EPS, scale=1.0)
                kb.op("dve", lambda e: e.reciprocal(out=rstd[:, :cn], in_=rstd[:, :cn]), [std], [std])
                for j in range(4):
                    q = j % 2
                    tt(kb, "pool", yt[q][:, :cn], cc[:, j, :cn], mean[:, :cn], ALU.subtract, [ccd, std], [ytd[q]])
                    tt(kb, "dve", yt[q][:, :cn], yt[q][:, :cn], rstd[:, :cn], ALU.mult, [ytd[q], std], [ytd[q]])
                    ts(kb, "dve", yt[q][:, :cn], yt[q][:, :cn], cols[:, 12 + j:13 + j], cols[:, 16 + j:17 + j], ALU.mult, ALU.add, [ytd[q], cd], [ytd[q]])
                    act(kb, yo[q][:, :cn], yt[q][:, :cn], AF.Silu, [ytd[q]], [yod[q]])
                    kb.dma("sp", s_["out"](j, bi, cn), yo[q][:, :cn], reads=[yod[q]])
        kb.barrier()


I32 = mybir.dt.int32
TWO_PI = 2.0 * math.pi


class FCfg:
    def __init__(self, L, rows, N1, nq, CB):
        self.L, self.rows, self.N1, self.nq, self.CB = L, rows, N1, nq, CB
        self.N2 = 86 * nq
        self.N = N1 * self.N2
        assert self.N >= 2 * L - 1 and rows * self.N2 >= L


CFG_P = FCfg(16400, 64, 128, 3, 4)
CFG_S = FCfg(2064, 24, 48, 1, 16)


def fft_tables(cfg):
    N1, N2, N, rows, nq = cfg.N1, cfg.N2, cfg.N, cfg.rows, cfg.nq
    n1 = np.arange(rows)[:, None].astype(np.float64)
    k1 = np.arange(N1)[None, :].astype(np.float64)
    a = 2 * np.pi * n1 * k1 / N1
    F1 = np.concatenate([np.cos(a), -np.sin(a)], 1)
    n2 = np.arange(N2)[:, None].astype(np.float64)
    a = 2 * np.pi * n2 * k1 / N
    tw = np.stack([np.cos(a), -np.sin(a)], 1)
    tw = tw.reshape(nq, 86, 2, N1).transpose(1, 0, 2, 3)
    m = np.arange(N2)[None, :].astype(np.float64)
    a = 2 * np.pi * n2 * m / N2
    F2 = np.stack([np.cos(a), -np.sin(a), np.sin(a)], 0)
    F2 = F2.reshape(3, nq, 86, N2).transpose(2, 0, 1, 3)
    kk = np.arange(N1)[:, None].astype(np.float64)
    a = 2 * np.pi * kk * np.arange(N2)[None, :] / N
    twc = np.stack([np.cos(a), np.sin(a)], 1)
    a = 2 * np.pi * kk * np.arange(rows)[None, :] / N1
    G1 = np.stack([np.cos(a) / N, -np.sin(a) / N], 1)
    bf = ml_dtypes.bfloat16
    return dict(F1=F1.astype(np.float32).astype(bf), tw=np.ascontiguousarray(tw).astype(np.float32),
                F2=np.ascontiguousarray(F2).astype(np.float32).astype(bf), twc=twc.astype(np.float32),
                G1=G1.astype(np.float32).astype(bf))


class FTab:
    pass


def fft_load_tables(kb, st, cfg, tabs):
    t = FTab()
    t.d = Dep()
    t.F1 = kb.sb(st, [cfg.rows, 2 * cfg.N1], BF16, "F1")
    t.tw = kb.sb(st, [86, cfg.nq, 2, cfg.N1], F32, "tw")
    t.F2 = kb.sb(st, [86, 3, cfg.nq, cfg.N2], BF16, "F2")
    t.twc = kb.sb(st, [cfg.N1, 2, cfg.N2], F32, "twc")
    t.G1 = kb.sb(st, [cfg.N1, 2, cfg.rows], BF16, "G1")
    for nm in ("F1", "tw", "F2", "twc", "G1"):
        kb.dma("sp", getattr(t, nm)[:], tabs[nm], writes=[t.d])
    return t


class FBuf:
    pass


def fft_alloc(kb, st, cfg):
    b = FBuf()
    CB, nq, N1, N2, rows = cfg.CB, cfg.nq, cfg.N1, cfg.N2, cfg.rows
    E = CB * nq * N1
    E2 = CB * N2
    tn = max(E, E2)
    b.src_f = kb.sb(st, [rows, CB, N2], F32, "srcf")
    b.src_fd = Dep()
    b.src_b = kb.sb(st, [rows, CB, N2], BF16, "srcb")
    b.src_bd = Dep()
    b.As = kb.sb(st, [86, CB * nq, 2, N1], F32, "As")
    b.Asd = Dep()
    b.Ab = kb.sb(st, [86, CB * nq, 2, N1], BF16, "Ab")
    b.Abd = Dep()
    b.Xs = kb.sb(st, [86, CB * nq, 2, N1], F32, "Xs")
    b.Xsd = Dep()
    b.t = [kb.sb(st, [128, tn], F32, "ft") for _ in range(4)]
    b.td = [Dep() for _ in range(4)]
    return b


def cmul_batched(kb, cfg, b, P, shape, Are, Aim, Br, Bi, out_re, out_im, rdeps, wdep, conj=False):
    n = int(np.prod(shape))
    pat = {2: "p (a b) -> p a b", 3: "p (a b c) -> p a b c"}[len(shape)]
    kw = dict(zip("abc", shape))
    kw.pop("a")
    tv = [b.t[i][:P, :n].rearrange(pat, **kw) for i in range(4)]
    tt(kb, "dve", tv[0], Are, Br, ALU.mult, rdeps, [b.td[0]])
    tt(kb, "pool", tv[1], Aim, Bi, ALU.mult, rdeps, [b.td[1]])
    tt(kb, "pool", tv[2], Are, Bi, ALU.mult, rdeps, [b.td[2]])
    tt(kb, "dve", tv[3], Aim, Br, ALU.mult, rdeps, [b.td[3]])
    tt(kb, "dve", out_re, tv[0], tv[1], ALU.subtract, [b.td[0], b.td[1]], [wdep])
    tt(kb, "pool", out_im, tv[2], tv[3], ALU.add, [b.td[2], b.td[3]], [wdep])


def fft_fwd(kb, g, cfg, tb, b, cb):
    nq, N1, N2, rows = cfg.nq, cfg.N1, cfg.N2, cfg.rows
    per = 512 // (2 * N1)
    tot = cb * nq
    for i0 in range(0, tot, per):
        cnt = min(per, tot - i0)
        bk = nextbank(g)
        for i in range(i0, i0 + cnt):
            c, q = divmod(i, nq)
            mm(kb, g.psum[bk][:86, (i - i0) * 2 * N1:(i - i0 + 1) * 2 * N1], b.src_b[:rows, c, q * 86:(q + 1) * 86], tb.F1[:rows, :], True, True,
               [b.src_bd, tb.d], [g.pd[bk]])
        copy_op(kb, "act", b.As[:, i0:i0 + cnt, :, :], g.psum[bk][:86, :cnt * 2 * N1].rearrange("p (i r k) -> p i r k", r=2, k=N1), [g.pd[bk]], [b.Asd])
    Av = b.As[:, :tot, :, :].rearrange("p (c q) r k -> p c q r k", q=nq)
    Abv = b.Ab[:, :tot, :, :].rearrange("p (c q) r k -> p c q r k", q=nq)
    twr = tb.tw[:, :, 0, :].unsqueeze(1).broadcast_to([86, cb, nq, N1])
    twi = tb.tw[:, :, 1, :].unsqueeze(1).broadcast_to([86, cb, nq, N1])
    cmul_batched(kb, cfg, b, 86, (cb, nq, N1), Av[:, :, :, 0, :], Av[:, :, :, 1, :], twr, twi, Abv[:, :, :, 0, :], Abv[:, :, :, 1, :],
                 [b.Asd, tb.d], b.Abd)
    for i0 in range(0, tot, per):
        cnt = min(per, tot - i0)
        bk = nextbank(g)
        for i in range(i0, i0 + cnt):
            c, p = divmod(i, nq)
            reg = g.psum[bk][:86, (i - i0) * 2 * N1:(i - i0 + 1) * 2 * N1]
            for q in range(nq):
                blk = slice(p * 86, (p + 1) * 86)
                mm(kb, reg, tb.F2[:, 0, q, blk], b.Ab[:, c * nq + q, :, :].rearrange("p r k -> p (r k)"), q == 0, False, [tb.d, b.Abd], [g.pd[bk]])
                mm(kb, reg[:, 0:N1], tb.F2[:, 2, q, blk], b.Ab[:, c * nq + q, 1, :], False, False, [tb.d, b.Abd], [g.pd[bk]])
                mm(kb, reg[:, N1:2 * N1], tb.F2[:, 1, q, blk], b.Ab[:, c * nq + q, 0, :], False, q == nq - 1, [tb.d, b.Abd], [g.pd[bk]])
        copy_op(kb, "act", b.Xs[:, i0:i0 + cnt, :, :], g.psum[bk][:86, :cnt * 2 * N1].rearrange("p (i r k) -> p i r k", r=2, k=N1), [g.pd[bk]], [b.Xsd])


def fft_layout_dma(kb, q, cfg, tile, tiled, dram2d, c0, cb, to_sbuf):
    L, N2, rows = cfg.L, cfg.N2, cfg.rows
    full = L // N2
    rem = L - full * N2
    dv = dram2d[c0:c0 + cb, 0:full * N2].rearrange("c (a b) -> a c b", b=N2)
    if to_sbuf:
        kb.dma(q, tile[:full, :cb, :], dv, writes=[tiled])
        if rem:
            kb.dma(q, tile[full:full + 1, :cb, :rem], dram2d[c0:c0 + cb, full * N2:L].unsqueeze(0), writes=[tiled])
    else:
        kb.dma(q, dv, tile[:full, :cb, :], reads=[tiled])
        if rem:
            kb.dma(q, dram2d[c0:c0 + cb, full * N2:L].unsqueeze(0), tile[full:full + 1, :cb, :rem], reads=[tiled])


def phase_hy_conv(kb, g, cfg, tabs, taps, Hs, seqs, dskip):
    CB, nq, N1, N2, rows, L = cfg.CB, cfg.nq, cfg.N1, cfg.N2, cfg.rows, cfg.L
    with ExitStack() as st:
        tb = fft_load_tables(kb, st, cfg, tabs)
        b = fft_alloc(kb, st, cfg)
        kb.op("pool", lambda e: e.memset(b.src_f[:, :, :], 0.0), [], [b.src_fd])
        X0 = kb.sb(st, [86, CB * nq, 2, N1], F32, "X0")
        X0d = Dep()
        Hb = kb.sb(st, [86, CB * nq, 2, N1], F32, "Hb")
        Hbd = Dep()
        for c0 in range(0, 64, CB):
            for d in range(2):
                fft_layout_dma(kb, "sp", cfg, b.src_f, b.src_fd, taps[d], c0, CB, True)
                copy_op(kb, "dve", b.src_b[:, :, :], b.src_f[:, :, :], [b.src_fd], [b.src_bd])
                fft_fwd(kb, g, cfg, tb, b, CB)
                if d == 0:
                    copy_op(kb, "pool", X0[:, :, :, :], b.Xs[:, :, :, :], [b.Xsd], [X0d])
            tt(kb, "dve", Hb[:, :, 0, :], X0[:, :, 0, :], b.Xs[:, :, 0, :], ALU.add, [X0d, b.Xsd], [Hbd])
            tt(kb, "pool", Hb[:, :, 1, :], X0[:, :, 1, :], b.Xs[:, :, 1, :], ALU.subtract, [X0d, b.Xsd], [Hbd])
            kb.dma("sp", Hs[:, c0 * nq:(c0 + CB) * nq, :, :], Hb[:, :, :, :], reads=[Hbd])
        kb.barrier()
        Yb = kb.sb(st, [86, CB * nq, 2, N1], BF16, "Yb")
        Ybd = Dep()
        Bs = kb.sb(st, [N1, CB, 2, N2], F32, "Bs")
        Bsd = Dep()
        Bb = kb.sb(st, [N1, CB, 2, N2], BF16, "Bb")
        Bbd = Dep()
        x0f = kb.sb(st, [rows, CB, N2], F32, "x0f")
        x0d = Dep()
        cv = kb.sb(st, [rows, CB, N2], F32, "cv")
        cvd = Dep()
        yo = kb.sb(st, [rows, CB, N2], BF16, "yo")
        yod = Dep()
        dsk = kb.sb(st, [128, 64], F32, "dsk")
        dskd = Dep()
        kb.dma("sp", dsk[:, :], dskip.broadcast_to([128, 64]), writes=[dskd])
        perb = 512 // N2
        for sq in seqs:
            for c0 in range(0, 64, CB):
                fft_layout_dma(kb, "sp", cfg, b.src_f, b.src_fd, sq["z"], c0, CB, True)
                fft_layout_dma(kb, "pool", cfg, x0f, x0d, sq["x0"], c0, CB, True)
                kb.dma("sp", Hb[:, :, :, :], Hs[:, c0 * nq:(c0 + CB) * nq, :, :], writes=[Hbd])
                copy_op(kb, "dve", b.src_b[:, :, :], b.src_f[:, :, :], [b.src_fd], [b.src_bd])
                fft_fwd(kb, g, cfg, tb, b, CB)
                cmul_batched(kb, cfg, b, 86, (CB * nq, N1), b.Xs[:, :, 0, :], b.Xs[:, :, 1, :], Hb[:, :, 0, :], Hb[:, :, 1, :],
                             Yb[:, :, 0, :], Yb[:, :, 1, :], [b.Xsd, Hbd], Ybd)
                tot = CB * 2
                for i0 in range(0, tot, perb):
                    cnt = min(perb, tot - i0)
                    bk = nextbank(g)
                    for i in range(i0, i0 + cnt):
                        c, ri = divmod(i, 2)
                        reg = g.psum[bk][:N1, (i - i0) * N2:(i - i0 + 1) * N2]
                        for p in range(nq):
                            ya_re, ya_im = Yb[:, c * nq + p, 0, :], Yb[:, c * nq + p, 1, :]
                            if ri == 0:
                                mm(kb, reg, ya_re, tb.F2[:, 0, p, :], p == 0, False, [Ybd, tb.d], [g.pd[bk]])
                                mm(kb, reg, ya_im, tb.F2[:, 1, p, :], False, p == nq - 1, [Ybd, tb.d], [g.pd[bk]])
                            else:
                                mm(kb, reg, ya_re, tb.F2[:, 2, p, :], p == 0, False, [Ybd, tb.d], [g.pd[bk]])
                                mm(kb, reg, ya_im, tb.F2[:, 0, p, :], False, p == nq - 1, [Ybd, tb.d], [g.pd[bk]])
                    copy_op(kb, "act", Bs[:, :, :, :].rearrange("p c r n -> p (c r) n")[:, i0:i0 + cnt, :],
                            g.psum[bk][:N1, :cnt * N2].rearrange("p (i n) -> p i n", n=N2), [g.pd[bk]], [Bsd])
                twr = tb.twc[:, 0, :].unsqueeze(1).broadcast_to([N1, CB, N2])
                twi = tb.twc[:, 1, :].unsqueeze(1).broadcast_to([N1, CB, N2])
                cmul_batched(kb, cfg, b, N1, (CB, N2), Bs[:, :, 0, :], Bs[:, :, 1, :], twr, twi, Bb[:, :, 0, :], Bb[:, :, 1, :], [Bsd, tb.d], Bbd)
                for i0 in range(0, CB, perb):
                    cnt = min(perb, CB - i0)
                    bk = nextbank(g)
                    for c in range(i0, i0 + cnt):
                        reg = g.psum[bk][:rows, (c - i0) * N2:(c - i0 + 1) * N2]
                        mm(kb, reg, tb.G1[:, 0, :], Bb[:, c, 0, :], True, False, [tb.d, Bbd], [g.pd[bk]])
                        mm(kb, reg, tb.G1[:, 1, :], Bb[:, c, 1, :], False, True, [tb.d, Bbd], [g.pd[bk]])
                    copy_op(kb, "act", cv[:, i0:i0 + cnt, :], g.psum[bk][:rows, :cnt * N2].rearrange("p (i n) -> p i n", n=N2), [g.pd[bk]], [cvd])
                tt(kb, "pool", b.src_f[:, :, :], b.src_f[:, :, :], dsk[:rows, c0:c0 + CB].unsqueeze(2).broadcast_to([rows, CB, N2]), ALU.mult,
                   [b.src_fd, dskd, b.src_bd], [b.src_fd])
                tt(kb, "dve", cv[:, :, :], cv[:, :, :], b.src_f[:, :, :], ALU.add, [cvd, b.src_fd], [cvd])
                tt(kb, "pool", yo[:, :, :], cv[:, :, :], x0f[:, :, :], ALU.mult, [cvd, x0d], [yod])
                fft_layout_dma(kb, "sp", cfg, yo, yod, sq["ya"], c0, CB, False)
        kb.barrier()


def phase_hy_inproj(kb, g, seqs, w_hy, brow, hcols, G=1):
    with ExitStack() as st:
        wb = kb.sb(st, [128, 8, G * 192], BF16, "why")
        Wd = Dep()
        load_weight_bf16(kb, st, wb, Wd, w_hy, 8, G * 192, stage_cols=1536)
        hc = kb.sb(st, [64, G * 12], F32, "hc")
        hcd = Dep()
        kb.dma("sp", hc[:, :], hcols, writes=[hcd])
        brf = kb.sb(st, [1, G * 192], F32, "brf")
        brb = kb.sb(st, [1, G * 192], BF16, "brb")
        brd = Dep()
        kb.dma("sp", brf[:, :], brow, writes=[brd])
        copy_op(kb, "dve", brb[:, :], brf[:, :], [brd], [brd])
        xin = [kb.sb(st, [128, D], F32, "xin") for _ in range(4)]
        xind = [Dep() for _ in range(4)]
        xT = [kb.sb(st, [128, 8, 512], BF16, "xT") for _ in range(2)]
        xTd = [Dep() for _ in range(2)]
        vf = [kb.sb(st, [1, 512], F32, "vf") for _ in range(2)]
        vb = [kb.sb(st, [1, 512], BF16, "vb") for _ in range(2)]
        vd = [Dep() for _ in range(2)]
        o3 = [[kb.sb(st, [64, 512], F32, "o3") for _ in range(3)] for _ in range(2)]
        o3d = [[Dep() for _ in range(3)] for _ in range(2)]
        bi = 0
        oi = 0
        for sq in seqs:
            L = sq["L"]
            for t0 in range(0, L, 510):
                no = min(510, L - t0)
                ni = no + 2
                j = bi % 2
                bi += 1
                for ti, r0 in enumerate(range(0, ni, 128)):
                    n = min(128, ni - r0)
                    kb.dma("sp" if ti % 2 == 0 else "pool", xin[ti][:n, :], sq["xh"][t0 + r0:t0 + r0 + n, :], writes=[xind[ti]])
                    transpose_tile(kb, g, xin[ti], xind[ti], n, xT[j], xTd[j], r0)
                kb.dma("pool", vf[j][:, :ni], sq["valid"][:, t0:t0 + ni], writes=[vd[j]])
                copy_op(kb, "dve", vb[j][:, :ni], vf[j][:, :ni], [vd[j]], [vd[j]])
                for gg in range(G):
                    oj = oi % 2
                    oi += 1
                    for gi in range(3):
                        c0 = gg * 192 + gi * 64
                        h0 = gg * 12 + gi * 4
                        bk = nextbank(g)
                        for k in range(8):
                            mm(kb, g.psum[bk][:64, :ni], wb[:, k, c0:c0 + 64], xT[j][:, k, :ni], k == 0, False, [Wd, xTd[j]], [g.pd[bk]])
                        mm(kb, g.psum[bk][:64, :ni], brb[:, c0:c0 + 64], vb[j][:, :ni], False, True, [brd, vd[j]], [g.pd[bk]])
                        o = o3[oj][gi]
                        od = o3d[oj][gi]
                        ts(kb, "dve", o[:, :no], g.psum[bk][:64, 1:1 + no], hc[:, h0 + 1:h0 + 2], hc[:, h0 + 3:h0 + 4], ALU.mult, ALU.add,
                           [g.pd[bk], hcd], [od])
                        stt(kb, "dve", o[:, :no], g.psum[bk][:64, 0:no], hc[:, h0:h0 + 1], o[:, :no], ALU.mult, ALU.add, [g.pd[bk], hcd, od], [od])
                        stt(kb, "dve", o[:, :no], g.psum[bk][:64, 2:2 + no], hc[:, h0 + 2:h0 + 3], o[:, :no], ALU.mult, ALU.add, [g.pd[bk], hcd, od], [od])
                    tt(kb, "pool", o3[oj][1][:, :no], o3[oj][1][:, :no], o3[oj][2][:, :no], ALU.mult, [o3d[oj][1], o3d[oj][2]], [o3d[oj][1]])
                    kb.dma("sp", sq["x0"][gg][:, t0:t0 + no], o3[oj][0][:, :no], reads=[o3d[oj][0]])
                    kb.dma("pool", sq["z"][gg][:, t0:t0 + no], o3[oj][1][:, :no], reads=[o3d[oj][1]])
        kb.barrier()


def sin_reduced(kb, out, outd, src_ps, fcol, fbcol, tmps, tmpd, ki, kid, n, reads):
    a, r = tmps
    ts(kb, "dve", a[:, :n], src_ps, fcol, fbcol, ALU.mult, ALU.add, reads, [tmpd[0]])
    ts(kb, "pool", r[:, :n], a[:, :n], 1.0 / TWO_PI, None, ALU.mult, None, [tmpd[0]], [tmpd[1]])
    copy_op(kb, "dve", ki[:, :n], r[:, :n], [tmpd[1]], [kid])
    copy_op(kb, "pool", r[:, :n], ki[:, :n], [kid], [tmpd[1]])
    stt(kb, "dve", r[:, :n], r[:, :n], -TWO_PI, a[:, :n], ALU.mult, ALU.add, [tmpd[0], tmpd[1]], [tmpd[1]])
    ts(kb, "pool", r[:, :n], r[:, :n], -3.1415925, 3.1415925, ALU.max, ALU.min, [tmpd[1]], [tmpd[1]])
    return act(kb, out, r[:, :n], AF.Sin, [tmpd[1]], [outd])


def phase_hy_filters(kb, g, L, zposT, fw, taps_out):
    with ExitStack() as st:
        w1 = kb.sb(st, [33, 2, 64], F32, "fw1")
        w2 = kb.sb(st, [64, 2, 64], F32, "fw2")
        w3 = kb.sb(st, [64, 2, 64], F32, "fw3")
        fc = kb.sb(st, [64, 2, 8], F32, "fc")
        Wd = Dep()
        kb.dma("sp", w1[:, :, :], fw["w1"].rearrange("d e f -> e d f"), writes=[Wd])
        kb.dma("sp", w2[:, :, :], fw["w2"].rearrange("d e f -> e d f"), writes=[Wd])
        kb.dma("sp", w3[:, :, :], fw["w3"].rearrange("d e f -> e d f"), writes=[Wd])
        kb.dma("sp", fc[:, :, 0:5], fw["fcols"], writes=[Wd])
        tt(kb, "dve", fc[:, :, 5:6], fc[:, :, 0:1], fc[:, :, 1:2], ALU.mult, [Wd], [Wd])
        tt(kb, "dve", fc[:, :, 6:7], fc[:, :, 2:3], fc[:, :, 3:4], ALU.mult, [Wd], [Wd])
        ts(kb, "dve", fc[:, :, 7:8], fc[:, :, 4:5], -1.0, None, ALU.mult, None, [Wd], [Wd])
        taps = kb.sb(st, [64, 2, L], F32, "taps")
        tapsd = Dep()
        zp = [kb.sb(st, [33, 512], F32, "zp") for _ in range(2)]
        zpd = [Dep() for _ in range(2)]
        tb_ = [kb.sb(st, [64, 512], F32, "tbc") for _ in range(2)]
        tbd = [Dep() for _ in range(2)]
        tmps = [kb.sb(st, [64, 512], F32, "ftmp") for _ in range(2)]
        tmpd = [Dep(), Dep()]
        ki = kb.sb(st, [64, 512], I32, "ki")
        kid = Dep()
        h1 = kb.sb(st, [64, 512], F32, "h1")
        h1d = Dep()
        h2 = kb.sb(st, [64, 512], F32, "h2")
        h2d = Dep()
        ex = kb.sb(st, [64, 512], F32, "ex")
        exd = Dep()
        ss = kb.sb(st, [64, 2 * ((L + 511) // 512) + 4], F32, "ss")
        ssd = Dep()
        junk = kb.sb(st, [64, 512], F32, "fjunk")
        junkd = Dep()
        nb = (L + 511) // 512
        for bi, l0 in enumerate(range(0, L, 512)):
            n = min(512, L - l0)
            j = bi % 2
            kb.dma("sp", zp[j][:, :n], zposT[:, l0:l0 + n], writes=[zpd[j]])
            kb.dma("pool", tb_[j][:, :n], zposT[0:1, l0:l0 + n].broadcast_to([64, n]), writes=[tbd[j]])
            for d in range(2):
                bk = nextbank(g)
                mm(kb, g.psum[bk][:64, :n], w1[:, d, :], zp[j][:, :n], True, True, [Wd, zpd[j]], [g.pd[bk]])
                sin_reduced(kb, h1[:, :n], h1d, g.psum[bk][:64, :n], fc[:, d, 0:1], fc[:, d, 5:6], tmps, tmpd, ki, kid, n, [g.pd[bk], Wd])
                bk = nextbank(g)
                mm(kb, g.psum[bk][:64, :n], w2[:, d, :], h1[:, :n], True, True, [Wd, h1d], [g.pd[bk]])
                sin_reduced(kb, h2[:, :n], h2d, g.psum[bk][:64, :n], fc[:, d, 2:3], fc[:, d, 6:7], tmps, tmpd, ki, kid, n, [g.pd[bk], Wd])
                bk = nextbank(g)
                mm(kb, g.psum[bk][:64, :n], w3[:, d, :], h2[:, :n], True, True, [Wd, h2d], [g.pd[bk]])
                act(kb, ex[:, :n], tb_[j][:, :n], AF.Exp, [tbd[j], Wd], [exd], scale=fc[:, d, 7:8])
                tt(kb, "dve", taps[:, d, l0:l0 + n], g.psum[bk][:64, :n], ex[:, :n], ALU.mult, [g.pd[bk], exd], [tapsd])
                if d == 1 and l0 == 0:
                    kb.op("pool", lambda e: e.memset(taps[:, 1, 0:1], 0.0), [], [tapsd])
                act(kb, junk[:, :n], taps[:, d, l0:l0 + n], AF.Square, [tapsd], [junkd, ssd], accum_out=ss[:, 2 * bi + d:2 * bi + d + 1])
        tot, nrm = ss[:, 2 * nb:2 * nb + 1], ss[:, 2 * nb + 1:2 * nb + 2]
        kb.op("dve", lambda e: e.tensor_reduce(out=tot, in_=ss[:, 0:2 * nb], axis=AX.X, op=ALU.add), [ssd], [ssd])
        act(kb, nrm, tot, AF.Sqrt, [ssd], [ssd])
        kb.op("dve", lambda e: e.reciprocal(out=nrm, in_=nrm), [ssd], [ssd])
        for d in range(2):
            for l0 in range(0, L, 4096):
                n = min(4096, L - l0)
                ts(kb, ("dve", "pool")[d], taps[:, d, l0:l0 + n], taps[:, d, l0:l0 + n], nrm, None, ALU.mult, None, [tapsd, ssd], [tapsd])
            kb.dma("sp", taps_out[d], taps[:, d, :], reads=[tapsd])
        kb.barrier()


def phase_hy_filter_h2(kb, g, L, zposT, fw, h2_out):
    with ExitStack() as st:
        w1 = kb.sb(st, [33, 2, 64], F32, "fw1")
        w2 = kb.sb(st, [64, 2, 64], F32, "fw2")
        fc = kb.sb(st, [64, 2, 8], F32, "fc")
        Wd = Dep()
        kb.dma("sp", w1[:, :, :], fw["w1"].rearrange("d e f -> e d f"), writes=[Wd])
        kb.dma("sp", w2[:, :, :], fw["w2"].rearrange("d e f -> e d f"), writes=[Wd])
        kb.dma("sp", fc[:, :, 0:5], fw["fcols"], writes=[Wd])
        tt(kb, "dve", fc[:, :, 5:6], fc[:, :, 0:1], fc[:, :, 1:2], ALU.mult, [Wd], [Wd])
        tt(kb, "dve", fc[:, :, 6:7], fc[:, :, 2:3], fc[:, :, 3:4], ALU.mult, [Wd], [Wd])
        zp = [kb.sb(st, [33, 512], F32, "zp") for _ in range(2)]
        zpd = [Dep() for _ in range(2)]
        NQ = 3
        tmps = [[kb.sb(st, [64, 512], F32, "ftmp") for _ in range(2)] for _ in range(NQ)]
        tmpd = [[Dep(), Dep()] for _ in range(NQ)]
        ki = [kb.sb(st, [64, 512], I32, "ki") for _ in range(NQ)]
        kid = [Dep() for _ in range(NQ)]
        h1 = [kb.sb(st, [64, 512], F32, "h1") for _ in range(NQ)]
        h1d = [Dep() for _ in range(NQ)]
        h2 = [kb.sb(st, [64, 512], F32, "h2") for _ in range(NQ)]
        h2d = [Dep() for _ in range(NQ)]
        it = 0
        for bi, l0 in enumerate(range(0, L, 512)):
            n = min(512, L - l0)
            j = bi % 2
            kb.dma("sp", zp[j][:, :n], zposT[:, l0:l0 + n], writes=[zpd[j]])
            for d in range(2):
                q = it % NQ
                it += 1
                bk = nextbank(g)
                mm(kb, g.psum[bk][:64, :n], w1[:, d, :], zp[j][:, :n], True, True, [Wd, zpd[j]], [g.pd[bk]])
                sin_reduced(kb, h1[q][:, :n], h1d[q], g.psum[bk][:64, :n], fc[:, d, 0:1], fc[:, d, 5:6], tmps[q], tmpd[q], ki[q], kid[q], n, [g.pd[bk], Wd])
                bk = nextbank(g)
                mm(kb, g.psum[bk][:64, :n], w2[:, d, :], h1[q][:, :n], True, True, [Wd, h1d[q]], [g.pd[bk]])
                sin_reduced(kb, h2[q][:, :n], h2d[q], g.psum[bk][:64, :n], fc[:, d, 2:3], fc[:, d, 6:7], tmps[q], tmpd[q], ki[q], kid[q], n, [g.pd[bk], Wd])
                kb.dma("pool", h2_out[d][:, l0:l0 + n], h2[q][:, :n], reads=[h2d[q]])
        kb.barrier()


def phase_hy_filter_taps(kb, g, L, zposT, h2_in, w3_ap, fcols_ap, taps_out):
    with ExitStack() as st:
        w3 = kb.sb(st, [64, 2, 64], F32, "fw3")
        fc = kb.sb(st, [64, 2, 8], F32, "fc")
        Wd = Dep()
        kb.dma("sp", w3[:, :, :], w3_ap.rearrange("d e f -> e d f"), writes=[Wd])
        kb.dma("sp", fc[:, :, 0:5], fcols_ap, writes=[Wd])
        ts(kb, "dve", fc[:, :, 7:8], fc[:, :, 4:5], -1.0, None, ALU.mult, None, [Wd], [Wd])
        taps = kb.sb(st, [64, 2, L], F32, "taps")
        tapsd = [Dep(), Dep()]
        NQ = 3
        tb_ = [kb.sb(st, [64, 512], F32, "tbc") for _ in range(2)]
        tbd = [Dep() for _ in range(2)]
        hin = [kb.sb(st, [64, 512], F32, "h2in") for _ in range(NQ)]
        hind = [Dep() for _ in range(NQ)]
        ex = [kb.sb(st, [64, 512], F32, "ex") for _ in range(NQ)]
        exd = [Dep() for _ in range(NQ)]
        junk = [kb.sb(st, [64, 512], F32, "fjunk") for _ in range(2)]
        junkd = [Dep(), Dep()]
        nb = (L + 511) // 512
        ss = kb.sb(st, [64, 2 * nb + 4], F32, "ss")
        ssd = Dep()
        it = 0
        for bi, l0 in enumerate(range(0, L, 512)):
            n = min(512, L - l0)
            j = bi % 2
            kb.dma("pool", tb_[j][:, :n], zposT[0:1, l0:l0 + n].broadcast_to([64, n]), writes=[tbd[j]])
            for d in range(2):
                q = it % NQ
                it += 1
                kb.dma("sp", hin[q][:, :n], h2_in[d][:, l0:l0 + n], writes=[hind[q]])
                bk = nextbank(g)
                mm(kb, g.psum[bk][:64, :n], w3[:, d, :], hin[q][:, :n], True, True, [Wd, hind[q]], [g.pd[bk]])
                act(kb, ex[q][:, :n], tb_[j][:, :n], AF.Exp, [tbd[j], Wd], [exd[q]], scale=fc[:, d, 7:8])
                tt(kb, "dve", taps[:, d, l0:l0 + n], g.psum[bk][:64, :n], ex[q][:, :n], ALU.mult, [g.pd[bk], exd[q]], [tapsd[d]])
                if d == 1 and l0 == 0:
                    kb.op("pool", lambda e: e.memset(taps[:, 1, 0:1], 0.0), [], [tapsd[d]])
                act(kb, junk[d][:, :n], taps[:, d, l0:l0 + n], AF.Square, [tapsd[d]], [junkd[d], ssd], accum_out=ss[:, 2 * bi + d:2 * bi + d + 1])
        tot, nrm = ss[:, 2 * nb:2 * nb + 1], ss[:, 2 * nb + 1:2 * nb + 2]
        kb.op("dve", lambda e: e.tensor_reduce(out=tot, in_=ss[:, 0:2 * nb], axis=AX.X, op=ALU.add), [ssd], [ssd])
        act(kb, nrm, tot, AF.Sqrt, [ssd], [ssd])
        kb.op("dve", lambda e: e.reciprocal(out=nrm, in_=nrm), [ssd], [ssd])
        for d in range(2):
            for l0 in range(0, L, 4096):
                n = min(4096, L - l0)
                ts(kb, ("dve", "pool")[d], taps[:, d, l0:l0 + n], taps[:, d, l0:l0 + n], nrm, None, ALU.mult, None, [tapsd[d], ssd], [tapsd[d]])
            kb.dma(("sp", "pool")[d], taps_out[d], taps[:, d, :], reads=[tapsd[d]])
        kb.barrier()


LP, LS = 16400, 2064
NCORES = 8
BF = ml_dtypes.bfloat16


class Prog:
    def __init__(self):
        self.nc = bass.Bass("TRN2", target_bir_lowering=False)
        self.kb = KB(self.nc)
        self.ins = {}

    def din(self, name, shape, dt=F32):
        self.ins[name] = (tuple(shape), dt)
        return self.nc.dram_tensor(name, list(shape), dt, kind="ExternalInput").ap()

    def dout(self, name, shape, dt=F32):
        return self.nc.dram_tensor(name, list(shape), dt, kind="ExternalOutput").ap()

    def scr(self, name, shape, dt=F32):
        return self.nc.dram_tensor(name, list(shape), dt).ap()


def chunk_tiles():
    return [(t0, 128, 61 + t0) for t0 in range(0, 2048, 128)] + [(2048, 16, 15)]


def declare_tabs(P, cfg, pre):
    t = fft_tables(cfg)
    return {k: P.din(pre + k, v.shape, F32 if v.dtype == np.float32 else BF16) for k, v in t.items()}, {pre + k: v for k, v in t.items()}


def build_l1():
    P = Prog()
    kb = P.kb
    ident = P.din("ident", [128, 128])
    xh_p = P.din("xh_p", [LP + 2, D])
    xh_s = P.din("xh_s", [LS + 2, D])
    valid_p = P.din("valid_p", [1, LP + 2])
    valid_s = P.din("valid_s", [1, LS + 2])
    zpos_p = P.din("zpos_p", [33, LP])
    zpos_s = P.din("zpos_s", [33, LS])
    tabsP, _ = declare_tabs(P, CFG_P, "tp_")
    tabsS, _ = declare_tabs(P, CFG_S, "ts_")
    fw1 = P.din("fw1", [2, 33, 64])
    fw2 = P.din("fw2", [2, 64, 64])
    fw3 = P.din("fw3", [9, 2, 64, 64])
    fcols = P.din("fcols", [9, 64, 2, 5])
    why = P.din("why", [9, D, 192])
    brow = P.din("brow", [9, 1, 192])
    hcols = P.din("hcols", [9, 64, 12])
    dskip = P.din("dskip", [9, 1, 64])
    xc = P.din("xc", [2, XC, D])
    mask = P.din("mask", [2, 1, XC])
    wconf = P.din("wconf", [D, 1024])
    ccols = P.din("ccols", [128, 144])
    yaP = P.dout("yaP", [64, LP], BF16)
    yaS = P.dout("yaS", [8, 64, LS], BF16)
    ybT = P.dout("ybT", [2, 512, LS], BF16)
    taps_p = P.scr("taps_p", [2, 64, LP])
    Hs_p = P.scr("Hs_p", [86, 64 * CFG_P.nq, 2, CFG_P.N1])
    z_p = P.scr("z_p", [64, LP])
    x0_p = P.scr("x0_p", [64, LP])
    taps_s = P.scr("taps_s", [8, 2, 64, LS])
    Hs_s = P.scr("Hs_s", [8, 86, 64 * CFG_S.nq, 2, CFG_S.N1])
    z_s = P.scr("z_s", [8, 64, LS])
    x0_s = P.scr("x0_s", [8, 64, LS])
    with ExitStack() as st:
        g = setup_globals(kb, st)
        load_ident(kb, g, ident)
        fwd = lambda i: dict(w1=fw1, w2=fw2, w3=fw3[i], fcols=fcols[i])
        phase_hy_filters(kb, g, LP, zpos_p, fwd(0), taps_p)
        phase_hy_inproj(kb, g, [dict(xh=xh_p, valid=valid_p, L=LP, z=[z_p], x0=[x0_p])], why[0], brow[0], hcols[0])
        phase_hy_conv(kb, g, CFG_P, tabsP, taps_p, Hs_p, [dict(z=z_p, x0=x0_p, ya=yaP)], dskip[0])
        for gi in range(8):
            phase_hy_filters(kb, g, LS, zpos_s, fwd(1 + gi), taps_s[gi])
            phase_hy_inproj(kb, g, [dict(xh=xh_s, valid=valid_s, L=LS, z=[z_s[gi]], x0=[x0_s[gi]])], why[1 + gi], brow[1 + gi], hcols[1 + gi])
            phase_hy_conv(kb, g, CFG_S, tabsS, taps_s[gi], Hs_s[gi], [dict(z=z_s[gi], x0=x0_s[gi], ya=yaS[gi])], dskip[1 + gi])

        def outf(s_):
            def f(j, bi, cn):
                if bi == 0:
                    return ybT[s_, j * 128:(j + 1) * 128, 2048:2064]
                return ybT[s_, j * 128:(j + 1) * 128, (bi - 1) * 512:bi * 512]
            return f
        phase_conf(kb, g, [dict(x=xc[s_], mask=mask[s_], out=outf(s_)) for s_ in range(2)], wconf, ccols)
        kb.finish_wait()
    return P


def build_l2():
    P = Prog()
    kb = P.kb
    ident = P.din("ident", [128, 128])
    xc = P.din("xc", [2, XC, D])
    ycT = P.din("ycT", [2, D, LS], BF16)
    wout = P.din("wout", [D, D])
    bout = P.din("bout", [1, D])
    ln1g = P.din("ln1g", [1, D]); ln1b = P.din("ln1b", [1, D]); ln2g = P.din("ln2g", [1, D]); ln2b = P.din("ln2b", [1, D])
    w1 = P.din("w1", [D, DFF]); w2 = P.din("w2", [DFF, D])
    wqa = P.din("wqa", [D, 384]); qg = P.din("qg", [1, 384]); WqH = P.din("WqH", [384, NH * 128]); WqS = P.din("WqS", [384, NH * 32])
    wkva = P.din("wkva", [D, 288]); kvg = P.din("kvg", [1, 256])
    cs = P.din("cs", [2, LS, 32]); Cq = P.din("Cq", [2, 32, 2048]); Sq = P.din("Sq", [2, 32, 2048])
    h2 = P.dout("h2", [2, LS, D])
    kvlat = P.dout("kvlat", [2, LS, 288])
    QT = P.dout("QT", [2, NH, 128, 2048], BF16)
    h1 = P.scr("h1", [2, LS, D])
    tl = chunk_tiles()
    with ExitStack() as st:
        g = setup_globals(kb, st)
        load_ident(kb, g, ident)
        ycv = ycT.rearrange("s (k p) t -> s p k t", p=128)
        phase_proj_ln(kb, g, [(xc[s_, xr:xr + n, :], [(slice(0, 8), ycv[s_, :, :, t0:t0 + n], None)], h1[s_, t0:t0 + n, :], n) for s_ in range(2) for t0, n, xr in tl],
                      True, wout, bout, ln1g, ln1b)
        phase_mlp_ln(kb, g, [(h1[s_, t0:t0 + n, :], h2[s_, t0:t0 + n, :], n) for s_ in range(2) for t0, n, xr in tl], w1, w2, ln2g, ln2b)
        seqs = []
        for s_ in range(2):
            seqs.append(dict(tiles=[(h2[s_, t0:t0 + n, :], kvlat[s_, t0:t0 + n, :], cs[s_, t0:t0 + n, :], n) for t0, n, xr in tl],
                             CS=(Cq[s_], Sq[s_]), qt=(lambda s_: (lambda h, q0: QT[s_, h, :, q0:q0 + 512]))(s_)))
        phase_qkv(kb, g, seqs, wqa, qg, WqH, WqS, wkva, kvg)
        kb.finish_wait()
    return P


def build_l3():
    P = Prog()
    kb = P.kb
    ident = P.din("ident", [128, 128])
    h2 = P.din("h2", [2, LS, D])
    kvp = P.din("kvp", [LP, 288])
    kvs = P.din("kvs", [LS, 288])
    QT = P.din("QT", [2, NH, 128, 2048], BF16)
    WkH = P.din("WkH", [256, NH * 128]); WvH = P.din("WvH", [256, NH * 64])
    wo = P.din("wo", [D, D])
    ln1g = P.din("ln1g", [1, D]); ln1b = P.din("ln1b", [1, D]); ln2g = P.din("ln2g", [1, D]); ln2b = P.din("ln2b", [1, D])
    w1 = P.din("w1", [D, DFF]); w2 = P.din("w2", [DFF, D])
    out = P.dout("out", [2, 2048, D])
    otok = P.scr("otok", [2, 2048, D])
    h3 = P.scr("h3", [2, 2048, D])
    with ExitStack() as st:
        g = setup_globals(kb, st)
        load_ident(kb, g, ident)
        seqs = []
        for s_, kv, L in ((0, kvp, LP), (1, kvs, LS)):
            otv = otok[s_].rearrange("(a t p) (h c) -> a p t h c", p=128, t=4, c=64)
            seqs.append(dict(kchunks=[(kv[t0:min(t0 + 128, L), :], min(128, L - t0)) for t0 in range(0, L, 128)],
                             qt=(lambda s_: (lambda h: QT[s_, h, :, :]))(s_),
                             o=(lambda otv: (lambda qsb, half, h: otv[qsb * 2 + half, :, :, h, :]))(otv)))
        phase_attn(kb, g, seqs, WkH, WvH)
        tl2 = [(s_, t0) for s_ in range(2) for t0 in range(0, 2048, 128)]
        phase_proj_ln(kb, g, [(h2[s_, t0:t0 + 128, :], otok[s_, t0:t0 + 128, :], h3[s_, t0:t0 + 128, :], 128) for s_, t0 in tl2], False, wo, None, ln1g, ln1b)
        phase_mlp_ln(kb, g, [(h3[s_, t0:t0 + 128, :], out[s_, t0:t0 + 128, :], 128) for s_, t0 in tl2], w1, w2, ln2g, ln2b)
        kb.finish_wait()
    return P


def zpos_table(L):
    t = np.arange(L, dtype=np.float32) / max(L - 1, 1)
    freqs = np.linspace(1e-4, 15, 16, dtype=np.float32)
    w = (np.float32(2.0 * math.pi) * np.arange(L, dtype=np.float32) / np.float32(L)).astype(np.float32)
    ang = w[:, None] * freqs[None, :]
    return np.ascontiguousarray(np.concatenate([t[:, None], np.cos(ang), -np.sin(ang)], -1).T.astype(np.float32))


def rope_cs(pos):
    inv = (1.0 / (10000.0 ** (np.arange(0, 32, 2, dtype=np.float32) / 32))).astype(np.float32)
    ang = pos.astype(np.float32)[:, None] * inv[None, :]
    return np.cos(ang).astype(np.float32), np.sin(ang).astype(np.float32)


def make_xc(hfull, m0, L):
    x = np.zeros((XC, D), np.float32)
    mk = np.zeros((1, XC), np.float32)
    x[15:46] = hfull[0:31]
    mk[0, 15:46] = 1
    lo, hi = m0 - 15, min(m0 + 2048 + 15, L)
    x[46:46 + (hi - lo)] = hfull[lo:hi]
    mk[0, 46:46 + (hi - lo)] = 1
    return x, mk


def colpack(v):
    return np.ascontiguousarray(v.reshape(4, 128).T)


def check_inputs(P, im):
    for k, (shape, dt) in P.ins.items():
        assert k in im, k
        assert tuple(im[k].shape) == shape, (k, im[k].shape, shape)
    return {k: np.ascontiguousarray(im[k]) for k in P.ins}


def kernel_unfused(x_prompt, x_sample, meta_tokens, ev_w_in, ev_b_in, ev_short_w, ev_short_b,
           hy_w1, hy_b1, hy_freq1, hy_w2, hy_b2, hy_freq2, hy_w3, hy_decay, hy_skip_d,
           cf_dw_w, cf_dw_b, cf_ln_g, cf_ln_b, ev_w_out, ev_b_out,
           mla_wq_a, mla_q_norm, mla_wq_b, mla_wkv_a, mla_kv_norm, mla_wkv_b, mla_wo,
           ln1_g, ln1_b, mlp_w1, mlp_w2, ln2_g, ln2_b):
    f = lambda a: np.asarray(a, dtype=np.float32)
    x_prompt, x_sample, meta = f(x_prompt), f(x_sample), f(meta_tokens)
    win, bin_, sw, sb = f(ev_w_in)[0], f(ev_b_in)[0], f(ev_short_w)[0], f(ev_short_b)[0]
    ident = np.eye(128, dtype=np.float32)
    hp = np.concatenate([meta, x_prompt[0]], 0)
    hs = [np.concatenate([meta, x_sample[c]], 0) for c in range(8)]
    z1 = np.zeros((1, D), np.float32)
    xh_p = np.concatenate([z1, hp, z1], 0)
    valid_p = np.ones((1, LP + 2), np.float32); valid_p[0, 0] = 0; valid_p[0, -1] = 0
    valid_s = np.ones((1, LS + 2), np.float32); valid_s[0, 0] = 0; valid_s[0, -1] = 0
    tabP, tabS = fft_tables(CFG_P), fft_tables(CFG_S)
    def grp(gi):
        ch = slice(gi * 64, gi * 64 + 64)
        gcols = [np.arange(k * 512 + gi * 64, k * 512 + gi * 64 + 64) for k in range(3)]
        allc = np.concatenate(gcols)
        return dict(fw3=np.ascontiguousarray(f(hy_w3)[0][:, :, ch]),
                    fcols=np.ascontiguousarray(np.stack([f(hy_freq1)[0], f(hy_b1)[0], f(hy_freq2)[0], f(hy_b2)[0], f(hy_decay)[0][:, ch]], -1).transpose(1, 0, 2)),
                    why=np.ascontiguousarray(win[:, allc]), brow=bin_[allc][None, :].copy(),
                    hcols=np.concatenate([np.stack([sw[0, gc], sw[1, gc], sw[2, gc], sb[gc]], 1) for gc in gcols], 1).astype(np.float32),
                    dskip=f(hy_skip_d)[0][ch][None, :].copy())
    G = [grp(gi) for gi in range(8)]
    ccols = np.concatenate([colpack(bin_[1536:2048]), colpack(bin_[2048:2560]), colpack(f(cf_dw_b)[0]), colpack(f(cf_ln_g)[0]), colpack(f(cf_ln_b)[0]),
                            np.ascontiguousarray(f(cf_dw_w)[0].T.reshape(4, 128, 31).transpose(1, 0, 2).reshape(128, 124))], 1).astype(np.float32)
    xcs, masks = [], []
    for c in range(8):
        a, ma = make_xc(hp, 16 + 2048 * c, LP)
        b, mb = make_xc(hs[c], 16, LS)
        xcs.append(np.stack([a, b], 0))
        masks.append(np.stack([ma, mb], 0))
    P1 = build_l1()
    ims = []
    for c in range(8):
        order = [c] + list(range(8))
        im = dict(ident=ident, xh_p=xh_p, xh_s=np.concatenate([z1, hs[c], z1], 0), valid_p=valid_p, valid_s=valid_s,
                  zpos_p=zpos_table(LP), zpos_s=zpos_table(LS), fw1=f(hy_w1)[0], fw2=f(hy_w2)[0],
                  fw3=np.stack([G[i]["fw3"] for i in order], 0), fcols=np.stack([G[i]["fcols"] for i in order], 0).astype(np.float32),
                  why=np.stack([G[i]["why"] for i in order], 0), brow=np.stack([G[i]["brow"] for i in order], 0),
                  hcols=np.stack([G[i]["hcols"] for i in order], 0), dskip=np.stack([G[i]["dskip"] for i in order], 0),
                  xc=xcs[c], mask=masks[c], wconf=np.ascontiguousarray(win[:, 1536:2560]), ccols=ccols)
        for k, v in tabP.items():
            im["tp_" + k] = v
        for k, v in tabS.items():
            im["ts_" + k] = v
        ims.append(check_inputs(P1, im))
    r1 = run_bass_kernel_spmd(P1.nc, ims, core_ids=list(range(8))).results
    yaP_all = np.concatenate([np.asarray(r1[c]["yaP"]) for c in range(8)], 0)
    P2 = build_l2()
    wqb = f(mla_wq_b)[0].reshape(384, NH, 96)
    WqH = np.concatenate([wqb[:, :, 64:96], np.zeros((384, NH, 32), np.float32), wqb[:, :, 0:64]], -1).reshape(384, NH * 128)
    WqS = np.concatenate([wqb[:, :, 80:96], wqb[:, :, 64:80]], -1).reshape(384, NH * 32)
    wkvb = f(mla_wkv_b)[0].reshape(256, NH, 128)
    WkH = np.concatenate([np.zeros((256, NH, 64), np.float32), wkvb[:, :, 0:64]], -1).reshape(256, NH * 128)
    WvH = np.ascontiguousarray(wkvb[:, :, 64:128]).reshape(256, NH * 64)
    ims = []
    for c in range(8):
        m0 = 16 + 2048 * c
        ya_p = np.concatenate([yaP_all[:, m0:m0 + 2048], yaP_all[:, 0:16]], 1)
        ya_s = np.asarray(r1[c]["yaS"]).reshape(512, LS)
        ya_s = np.concatenate([ya_s[:, 16:], ya_s[:, 0:16]], 1)
        yb = np.asarray(r1[c]["ybT"])
        ycT = np.stack([np.concatenate([ya_p, yb[0]], 0), np.concatenate([ya_s, yb[1]], 0)], 0)
        css, Cqs, Sqs = [], [], []
        for pos in (np.concatenate([np.arange(m0, m0 + 2048), np.arange(16)]), np.concatenate([np.arange(16, LS), np.arange(16)])):
            co, si = rope_cs(pos)
            css.append(np.concatenate([co, si], 1))
            Cqs.append(np.concatenate([co[:2048].T, co[:2048].T], 0))
            Sqs.append(np.concatenate([-si[:2048].T, si[:2048].T], 0))
        im = dict(ident=ident, xc=xcs[c], ycT=ycT, wout=f(ev_w_out)[0], bout=f(ev_b_out)[0:1], ln1g=f(ln1_g)[0:1], ln1b=f(ln1_b)[0:1],
                  ln2g=f(ln2_g)[0:1], ln2b=f(ln2_b)[0:1], w1=f(mlp_w1)[0], w2=f(mlp_w2)[0], wqa=f(mla_wq_a)[0], qg=f(mla_q_norm)[0:1],
                  WqH=WqH, WqS=WqS, wkva=f(mla_wkv_a)[0], kvg=f(mla_kv_norm)[0:1], cs=np.stack(css, 0), Cq=np.stack(Cqs, 0), Sq=np.stack(Sqs, 0))
        ims.append(check_inputs(P2, im))
    r2 = run_bass_kernel_spmd(P2.nc, ims, core_ids=list(range(8))).results
    kvp = np.concatenate([np.asarray(r2[c]["kvlat"])[0, :2048] for c in range(8)] + [np.asarray(r2[0]["kvlat"])[0, 2048:]], 0)
    P3 = build_l3()
    ims = []
    for c in range(8):
        im = dict(ident=ident, h2=np.asarray(r2[c]["h2"]), kvp=kvp, kvs=np.asarray(r2[c]["kvlat"])[1], QT=np.asarray(r2[c]["QT"]), WkH=WkH, WvH=WvH,
                  wo=f(mla_wo)[0], ln1g=f(ln1_g)[1:2], ln1b=f(ln1_b)[1:2], ln2g=f(ln2_g)[1:2], ln2b=f(ln2_b)[1:2], w1=f(mlp_w1)[1], w2=f(mlp_w2)[1])
        ims.append(check_inputs(P3, im))
    r3 = run_bass_kernel_spmd(P3.nc, ims, core_ids=list(range(8))).results
    y_prompt = np.concatenate([np.asarray(r3[c]["out"])[0] for c in range(8)], 0)[None].astype(np.float32)
    y_sample = np.stack([np.asarray(r3[c]["out"])[1] for c in range(8)], 0).astype(np.float32)
    return (y_prompt, y_sample)


U32 = mybir.dt.uint32
YAW = 18432


def build_fused(stop=10 ** 9, trace_steps=None):
    P = Prog()
    step = [0]

    def run(fn, *a):
        if step[0] < stop:
            fn(*a)
        step[0] += 1

    kb = P.kb
    nc = P.nc
    ident = P.din("ident", [128, 128])
    xh_p = P.din("xh_p", [LP + 2, D]); xh_s = P.din("xh_s", [LS + 2, D])
    valid_p = P.din("valid_p", [1, LP + 2]); valid_s = P.din("valid_s", [1, LS + 2])
    zpos_p = P.din("zpos_p", [33, LP]); zpos_s = P.din("zpos_s", [33, LS])
    tabsP, _ = declare_tabs(P, CFG_P, "tp_")
    tabsS, _ = declare_tabs(P, CFG_S, "ts_")
    fw1 = P.din("fw1", [2, 33, 64]); fw2 = P.din("fw2", [2, 64, 64])
    fw3 = P.din("fw3", [9, 2, 64, 64]); fcols = P.din("fcols", [9, 64, 2, 5])
    why = P.din("why", [9, D, 192]); brow = P.din("brow", [9, 1, 192]); hcols = P.din("hcols", [9, 64, 12]); dskip = P.din("dskip", [9, 1, 64])
    xc = P.din("xc", [2, XC, D]); mask = P.din("mask", [2, 1, XC])
    wconf = P.din("wconf", [D, 1024]); ccols = P.din("ccols", [128, 144])
    gidx = P.din("gidx", [128, 4], U32)
    wout = P.din("wout", [D, D]); bout = P.din("bout", [1, D])
    ln1g = P.din("ln1g", [2, 1, D]); ln1b = P.din("ln1b", [2, 1, D]); ln2g = P.din("ln2g", [2, 1, D]); ln2b = P.din("ln2b", [2, 1, D])
    w1 = P.din("w1", [2, D, DFF]); w2 = P.din("w2", [2, DFF, D])
    wqa = P.din("wqa", [D, 384]); qg = P.din("qg", [1, 384]); WqH = P.din("WqH", [384, NH * 128]); WqS = P.din("WqS", [384, NH * 32])
    wkva = P.din("wkva", [D, 288]); kvg = P.din("kvg", [1, 256])
    cs = P.din("cs", [2, LS, 32]); Cq = P.din("Cq", [2, 32, 2048]); Sq = P.din("Sq", [2, 32, 2048])
    WkH = P.din("WkH", [256, NH * 128]); WvH = P.din("WvH", [256, NH * 64]); wo = P.din("wo", [D, D])
    out = P.dout("out", [2, 2048, D])
    yaP = P.scr("yaP", [64, YAW], BF16)
    yaP_all = P.scr("yaP_all", [512, YAW], BF16)
    yaS = P.scr("yaS", [8, 64, LS], BF16)
    ybT = P.scr("ybT", [2, 512, LS], BF16)
    taps_p = P.scr("taps_p", [2, 64, LP]); Hs_p = P.scr("Hs_p", [86, 64 * CFG_P.nq, 2, CFG_P.N1])
    z_p = P.scr("z_p", [64, LP]); x0_p = P.scr("x0_p", [64, LP])
    taps_s = P.scr("taps_s", [8, 2, 64, LS]); Hs_s = P.scr("Hs_s", [8, 86, 64 * CFG_S.nq, 2, CFG_S.N1])
    z_s = P.scr("z_s", [8, 64, LS]); x0_s = P.scr("x0_s", [8, 64, LS])
    h1 = P.scr("h1", [2, LS, D]); h2 = P.scr("h2", [2, LS, D])
    kvlat = P.scr("kvlat", [2, LS, 288]); kv_all = P.scr("kv_all", [8 * LS, 288])
    QT = P.scr("QT", [2, NH, 128, 2048], BF16)
    otok = P.scr("otok", [2, 2048, D]); h3 = P.scr("h3", [2, 2048, D])
    tl = chunk_tiles()
    with ExitStack() as st:
        g = setup_globals(kb, st)
        load_ident(kb, g, ident)
        fwd = lambda i: dict(w1=fw1, w2=fw2, w3=fw3[i], fcols=fcols[i])
        run(phase_hy_filters, kb, g, LP, zpos_p, fwd(0), taps_p)
        run(phase_hy_inproj, kb, g, [dict(xh=xh_p, valid=valid_p, L=LP, z=[z_p], x0=[x0_p])], why[0], brow[0], hcols[0])
        run(phase_hy_conv, kb, g, CFG_P, tabsP, taps_p, Hs_p, [dict(z=z_p, x0=x0_p, ya=yaP[:, 2032:2032 + LP])], dskip[0])
        agd = Dep()
        run(lambda: kb.all_gather(yaP, yaP_all, reads=[], writes=[agd]))
        kb.barrier()
        for gi in range(8):
            run(phase_hy_filters, kb, g, LS, zpos_s, fwd(1 + gi), taps_s[gi])
            run(phase_hy_inproj, kb, g, [dict(xh=xh_s, valid=valid_s, L=LS, z=[z_s[gi]], x0=[x0_s[gi]])], why[1 + gi], brow[1 + gi], hcols[1 + gi])
            run(phase_hy_conv, kb, g, CFG_S, tabsS, taps_s[gi], Hs_s[gi], [dict(z=z_s[gi], x0=x0_s[gi], ya=yaS[gi])], dskip[1 + gi])

        def outf(s_):
            def f(j, bi, cn):
                if bi == 0:
                    return ybT[s_, j * 128:(j + 1) * 128, 2048:2064]
                return ybT[s_, j * 128:(j + 1) * 128, (bi - 1) * 512:bi * 512]
            return f
        run(phase_conf, kb, g, [dict(x=xc[s_], mask=mask[s_], out=outf(s_)) for s_ in range(2)], wconf, ccols)
        with ExitStack() as st2:
            yaG = kb.sb(st2, [128, 4, 2048], BF16, "yaG")
            yaGd = Dep()
            ix = kb.sb(st2, [128, 4], U32, "gix")
            ixd = Dep()
            kb.dma("sp", ix[:, :], gidx[:, :], writes=[ixd])
            rows = yaP_all.rearrange("c (b t) -> (c b) t", t=2048)
            for k in range(4):
                run(lambda k=k: kb.gather_rows(yaG[:, k, :], rows[:, :], ix[:, k:k + 1], reads=[agd, ixd], writes=[yaGd]))
            ybv = ybT.rearrange("s (k p) t -> s p k t", p=128)
            yav_meta = yaP_all.rearrange("(k p) c -> p k c", p=128)
            yas = yaS.rearrange("g c t -> (g c) t").rearrange("(k p) t -> p k t", p=128)
            tiles = []
            for t0, n, xr in tl:
                if n == 128:
                    yl = [(slice(0, 4), yaG[:, :, t0:t0 + n], yaGd), (slice(4, 8), ybv[0, :, :, t0:t0 + n], None)]
                else:
                    yl = [(slice(0, 4), yav_meta[:, :, 2032:2048], agd), (slice(4, 8), ybv[0, :, :, 2048:2064], None)]
                tiles.append((xc[0, xr:xr + n, :], yl, h1[0, t0:t0 + n, :], n))
            for t0, n, xr in tl:
                tok0 = 16 + t0 if n == 128 else 0
                yl = [(slice(0, 4), yas[:, :, tok0:tok0 + n], None), (slice(4, 8), ybv[1, :, :, t0:t0 + n], None)]
                tiles.append((xc[1, xr:xr + n, :], yl, h1[1, t0:t0 + n, :], n))
            run(phase_proj_ln, kb, g, tiles, True, wout, bout, ln1g[0], ln1b[0])
        run(phase_mlp_ln, kb, g, [(h1[s_, t0:t0 + n, :], h2[s_, t0:t0 + n, :], n) for s_ in range(2) for t0, n, xr in tl], w1[0], w2[0], ln2g[0], ln2b[0])
        seqs = []
        for s_ in range(2):
            seqs.append(dict(tiles=[(h2[s_, t0:t0 + n, :], kvlat[s_, t0:t0 + n, :], cs[s_, t0:t0 + n, :], n) for t0, n, xr in tl],
                             CS=(Cq[s_], Sq[s_]), qt=(lambda s_: (lambda h, q0: QT[s_, h, :, q0:q0 + 512]))(s_)))
        run(phase_qkv, kb, g, seqs, wqa, qg, WqH, WqS, wkva, kvg)
        kvd = Dep()
        run(lambda: kb.all_gather(kvlat[0], kv_all, reads=[], writes=[kvd]))
        kb.barrier()
        seqs = []
        pch = [(kv_all[r * LS + t0:r * LS + t0 + 128, :], 128) for r in range(8) for t0 in range(0, 2048, 128)] + [(kv_all[2048:2064, :], 16)]
        sch = [(kvlat[1, t0:min(t0 + 128, LS), :], min(128, LS - t0)) for t0 in range(0, LS, 128)]
        for s_, ch in ((0, pch), (1, sch)):
            otv = otok[s_].rearrange("(a t p) (h c) -> a p t h c", p=128, t=4, c=64)
            seqs.append(dict(kchunks=ch, qt=(lambda s_: (lambda h: QT[s_, h, :, :]))(s_),
                             o=(lambda otv: (lambda qsb, half, h: otv[qsb * 2 + half, :, :, h, :]))(otv)))
        run(phase_attn, kb, g, seqs, WkH, WvH)
        tl2 = [(s_, t0) for s_ in range(2) for t0 in range(0, 2048, 128)]
        run(phase_proj_ln, kb, g, [(h2[s_, t0:t0 + 128, :], otok[s_, t0:t0 + 128, :], h3[s_, t0:t0 + 128, :], 128) for s_, t0 in tl2], False, wo, None, ln1g[1], ln1b[1])
        run(phase_mlp_ln, kb, g, [(h3[s_, t0:t0 + 128, :], out[s_, t0:t0 + 128, :], 128) for s_, t0 in tl2], w1[1], w2[1], ln2g[1], ln2b[1])
        kb.finish_wait()
    P.nsteps = step[0]
    return P


def build_nc():
    P = Prog()
    kb = P.kb
    ident = P.din("ident", [128, 128])
    xpad_p = P.din("xpad_p", [LP + 30, D]); maskpad = P.din("maskpad", [1, LP + 30])
    xh_s = P.din("xh_s", [LS + 2, D]); valid_s = P.din("valid_s", [1, LS + 2])
    xc_s = P.din("xc_s", [XC, D]); mask_s = P.din("mask_s", [1, XC])
    zpos_p = P.din("zpos_p", [33, LP]); zpos_s = P.din("zpos_s", [33, LS])
    tabsP, _ = declare_tabs(P, CFG_P, "tp_")
    tabsS, _ = declare_tabs(P, CFG_S, "ts_")
    fw1 = P.din("fw1", [2, 33, 64]); fw2 = P.din("fw2", [2, 64, 64])
    fw3 = P.din("fw3", [8, 2, 64, 64]); fcols = P.din("fcols", [8, 64, 2, 5])
    why = P.din("why", [D, 8 * 192]); brow = P.din("brow", [1, 8 * 192]); hcols = P.din("hcols", [64, 8 * 12]); dskip = P.din("dskip", [8, 1, 64])
    wconf = P.din("wconf", [D, 1024]); ccols = P.din("ccols", [128, 144])
    tokidx = P.din("tokidx", [128, 16], U32)
    wout = P.din("wout", [D, D]); bout = P.din("bout", [1, D])
    ln1g = P.din("ln1g", [2, 1, D]); ln1b = P.din("ln1b", [2, 1, D]); ln2g = P.din("ln2g", [2, 1, D]); ln2b = P.din("ln2b", [2, 1, D])
    w1 = P.din("w1", [2, D, DFF]); w2 = P.din("w2", [2, DFF, D])
    wqa = P.din("wqa", [D, 384]); qg = P.din("qg", [1, 384]); WqH = P.din("WqH", [384, NH * 128]); WqS = P.din("WqS", [384, NH * 32])
    wkva = P.din("wkva", [D, 288]); kvg = P.din("kvg", [1, 256])
    cs_all = P.din("cs_all", [LP, 32])
    cs = P.din("cs", [2, LS, 32]); Cq = P.din("Cq", [2, 32, 2048]); Sq = P.din("Sq", [2, 32, 2048])
    WkH = P.din("WkH", [256, NH * 128]); WvH = P.din("WvH", [256, NH * 64]); wo = P.din("wo", [D, D])
    out = P.dout("out", [2, 2048, D])
    yaP_all = P.scr("yaP_all", [512, YAW], BF16)
    yaS = P.scr("yaS", [8, 64, LS], BF16)
    ybT_p = P.scr("ybT_p", [512, LP], BF16); ybT_s = P.scr("ybT_s", [512, LS], BF16)
    h2f_p = P.scr("h2f_p", [2, 64, LP]); h2f_s = P.scr("h2f_s", [2, 64, LS])
    taps_p = P.scr("taps_p", [2, 64, LP]); Hs_p = P.scr("Hs_p", [86, 64 * CFG_P.nq, 2, CFG_P.N1])
    z_p = P.scr("z_p", [8, 64, LP]); x0_p = P.scr("x0_p", [8, 64, LP])
    taps_s = P.scr("taps_s", [2, 64, LS]); Hs_s = P.scr("Hs_s", [86, 64 * CFG_S.nq, 2, CFG_S.N1])
    z_s = P.scr("z_s", [8, 64, LS]); x0_s = P.scr("x0_s", [8, 64, LS])
    h1_all = P.scr("h1_all", [LP, D]); h2_all = P.scr("h2_all", [LP, D])
    h1_s = P.scr("h1_s", [LS, D]); h2_s = P.scr("h2_s", [LS, D]); h2_own = P.scr("h2_own", [LS, D])
    kv_all = P.scr("kv_all", [LP, 288]); kv_dummy = P.scr("kv_dummy", [LS, 288]); kvlat_s = P.scr("kvlat_s", [LS, 288])
    QT = P.scr("QT", [2, NH, 128, 2048], BF16)
    otok = P.scr("otok", [2, 2048, D]); h3 = P.scr("h3", [2, 2048, D])
    tl = chunk_tiles()
    with ExitStack() as st:
        g = setup_globals(kb, st)
        load_ident(kb, g, ident)
        fwd = lambda i: dict(w1=fw1, w2=fw2, w3=fw3[i], fcols=fcols[i])
        phase_hy_inproj(kb, g, [dict(xh=xpad_p[14:14 + LP + 2, :], valid=maskpad[:, 14:14 + LP + 2], L=LP,
                                     z=[z_p[gi] for gi in range(8)], x0=[x0_p[gi] for gi in range(8)])], why, brow, hcols, G=8)
        phase_hy_filter_h2(kb, g, LP, zpos_p, fwd(0), h2f_p)
        phase_hy_filter_h2(kb, g, LS, zpos_s, fwd(0), h2f_s)
        for gi in range(8):
            phase_hy_filter_taps(kb, g, LP, zpos_p, h2f_p, fw3[gi], fcols[gi], taps_p)
            phase_hy_conv(kb, g, CFG_P, tabsP, taps_p, Hs_p, [dict(z=z_p[gi], x0=x0_p[gi], ya=yaP_all[gi * 64:(gi + 1) * 64, 2032:2032 + LP])], dskip[gi])
        phase_hy_inproj(kb, g, [dict(xh=xh_s, valid=valid_s, L=LS, z=[z_s[gi] for gi in range(8)], x0=[x0_s[gi] for gi in range(8)])],
                        why, brow, hcols, G=8)
        for gi in range(8):
            phase_hy_filter_taps(kb, g, LS, zpos_s, h2f_s, fw3[gi], fcols[gi], taps_s)
            phase_hy_conv(kb, g, CFG_S, tabsS, taps_s, Hs_s, [dict(z=z_s[gi], x0=x0_s[gi], ya=yaS[gi])], dskip[gi])
        cseqs = []
        for j in range(8):
            r0 = 16 + 2048 * j
            cseqs.append(dict(x=xpad_p[r0:r0 + 2078, :], mask=maskpad[:, r0:r0 + 2078], ncols=2078, blocks=[(15 + 512 * i, 512) for i in range(4)],
                              out=(lambda j: (lambda jj, bi, cn: ybT_p[jj * 128:(jj + 1) * 128, 2048 * j + 512 * bi:2048 * j + 512 * bi + cn]))(j)))
        cseqs.append(dict(x=xpad_p[0:46, :], mask=maskpad[:, 0:46], ncols=46, blocks=[(15, 16)],
                          out=lambda jj, bi, cn: ybT_p[jj * 128:(jj + 1) * 128, 16384:16400]))

        def outf_s(jj, bi, cn):
            if bi == 0:
                return ybT_s[jj * 128:(jj + 1) * 128, 2048:2064]
            return ybT_s[jj * 128:(jj + 1) * 128, (bi - 1) * 512:bi * 512]
        cseqs.append(dict(x=xc_s, mask=mask_s, out=outf_s))
        phase_conf(kb, g, cseqs, wconf, ccols)
        yav = yaP_all.rearrange("(k p) c -> p k c", p=128)
        ybv_p = ybT_p.rearrange("(k p) t -> p k t", p=128)
        ybv_s = ybT_s.rearrange("(k p) t -> p k t", p=128)
        yas = yaS.rearrange("g c t -> (g c) t").rearrange("(k p) t -> p k t", p=128)
        tiles = []
        for j in range(8):
            for t0 in range(0, 2048, 128):
                tok = 16 + 2048 * j + t0
                gr = 2048 * j + t0
                tiles.append((xpad_p[15 + tok:15 + tok + 128, :],
                              [(slice(0, 4), yav[:, :, 2032 + tok:2032 + tok + 128], None), (slice(4, 8), ybv_p[:, :, gr:gr + 128], None)],
                              h1_all[gr:gr + 128, :], 128))
        tiles.append((xpad_p[15:31, :], [(slice(0, 4), yav[:, :, 2032:2048], None), (slice(4, 8), ybv_p[:, :, 16384:16400], None)],
                      h1_all[16384:16400, :], 16))
        for t0, n, xr in tl:
            tok0 = 16 + t0 if n == 128 else 0
            tiles.append((xc_s[xr:xr + n, :], [(slice(0, 4), yas[:, :, tok0:tok0 + n], None), (slice(4, 8), ybv_s[:, :, t0:t0 + n], None)],
                          h1_s[t0:t0 + n, :], n))
        phase_proj_ln(kb, g, tiles, True, wout, bout, ln1g[0], ln1b[0])
        ptl = [(r0, min(128, LP - r0)) for r0 in range(0, LP, 128)]
        phase_mlp_ln(kb, g, [(h1_all[r0:r0 + n, :], h2_all[r0:r0 + n, :], n) for r0, n in ptl] +
                     [(h1_s[t0:t0 + n, :], h2_s[t0:t0 + n, :], n) for t0, n, xr in tl], w1[0], w2[0], ln2g[0], ln2b[0])
        with ExitStack() as st2:
            ix = kb.sb(st2, [128, 16], U32, "tokix")
            ixd = Dep()
            kb.dma("sp", ix[:, :], tokidx[:, :], writes=[ixd])
            gb = [kb.sb(st2, [128, D], F32, "gb") for _ in range(2)]
            gd = [Dep(), Dep()]
            for i in range(16):
                j = i % 2
                kb.gather_rows(gb[j][:, :], h2_all[:, :], ix[:, i:i + 1], reads=[ixd], writes=[gd[j]])
                kb.dma("sp", h2_own[128 * i:128 * i + 128, :], gb[j][:, :], reads=[gd[j]])
            kb.dma("sp", h2_own[2048:2064, :], h2_all[16384:16400, :])
            kb.barrier()
        seqs = [dict(tiles=[(h2_all[r0:r0 + n, :], kv_all[r0:r0 + n, :], cs_all[r0:r0 + n, :], n) for r0, n in ptl], kv_only=True),
                dict(tiles=[(h2_own[t0:t0 + n, :], kv_dummy[t0:t0 + n, :], cs[0, t0:t0 + n, :], n) for t0, n, xr in tl],
                     CS=(Cq[0], Sq[0]), qt=lambda h, q0: QT[0, h, :, q0:q0 + 512]),
                dict(tiles=[(h2_s[t0:t0 + n, :], kvlat_s[t0:t0 + n, :], cs[1, t0:t0 + n, :], n) for t0, n, xr in tl],
                     CS=(Cq[1], Sq[1]), qt=lambda h, q0: QT[1, h, :, q0:q0 + 512])]
        phase_qkv(kb, g, seqs, wqa, qg, WqH, WqS, wkva, kvg)
        aseqs = []
        for s_, ch in ((0, [(kv_all[r0:r0 + n, :], n) for r0, n in ptl]),
                       (1, [(kvlat_s[t0:min(t0 + 128, LS), :], min(128, LS - t0)) for t0 in range(0, LS, 128)])):
            otv = otok[s_].rearrange("(a t p) (h c) -> a p t h c", p=128, t=4, c=64)
            aseqs.append(dict(kchunks=ch, qt=(lambda s_: (lambda h: QT[s_, h, :, :]))(s_),
                              o=(lambda otv: (lambda qsb, half, h: otv[qsb * 2 + half, :, :, h, :]))(otv)))
        phase_attn(kb, g, aseqs, WkH, WvH)
        hres = (h2_own, h2_s)
        tl2 = [(s_, t0) for s_ in range(2) for t0 in range(0, 2048, 128)]
        phase_proj_ln(kb, g, [(hres[s_][t0:t0 + 128, :], otok[s_, t0:t0 + 128, :], h3[s_, t0:t0 + 128, :], 128) for s_, t0 in tl2], False, wo, None, ln1g[1], ln1b[1])
        phase_mlp_ln(kb, g, [(h3[s_, t0:t0 + 128, :], out[s_, t0:t0 + 128, :], 128) for s_, t0 in tl2], w1[1], w2[1], ln2g[1], ln2b[1])
        kb.finish_wait()
    return P


def kernel(x_prompt, x_sample, meta_tokens, ev_w_in, ev_b_in, ev_short_w, ev_short_b,
           hy_w1, hy_b1, hy_freq1, hy_w2, hy_b2, hy_freq2, hy_w3, hy_decay, hy_skip_d,
           cf_dw_w, cf_dw_b, cf_ln_g, cf_ln_b, ev_w_out, ev_b_out,
           mla_wq_a, mla_q_norm, mla_wq_b, mla_wkv_a, mla_kv_norm, mla_wkv_b, mla_wo,
           ln1_g, ln1_b, mlp_w1, mlp_w2, ln2_g, ln2_b):
    f = lambda a: np.asarray(a, dtype=np.float32)
    x_prompt, x_sample, meta = f(x_prompt), f(x_sample), f(meta_tokens)
    win, bin_, sw, sb = f(ev_w_in)[0], f(ev_b_in)[0], f(ev_short_w)[0], f(ev_short_b)[0]
    ident = np.eye(128, dtype=np.float32)
    hp = np.concatenate([meta, x_prompt[0]], 0)
    hs = [np.concatenate([meta, x_sample[c]], 0) for c in range(8)]
    z1 = np.zeros((1, D), np.float32)
    z15 = np.zeros((15, D), np.float32)
    xpad_p = np.concatenate([z15, hp, z15], 0)
    maskpad = np.zeros((1, LP + 30), np.float32); maskpad[0, 15:15 + LP] = 1
    valid_s = np.ones((1, LS + 2), np.float32); valid_s[0, 0] = 0; valid_s[0, -1] = 0
    tabP, tabS = fft_tables(CFG_P), fft_tables(CFG_S)
    gcols = [[np.arange(k * 512 + gi * 64, k * 512 + gi * 64 + 64) for k in range(3)] for gi in range(8)]
    allc = np.concatenate([np.concatenate(gc) for gc in gcols])
    why = np.ascontiguousarray(win[:, allc])
    brow = bin_[allc][None, :].copy()
    hcols = np.concatenate([np.stack([sw[0, c_], sw[1, c_], sw[2, c_], sb[c_]], 1) for gc in gcols for c_ in gc], 1).astype(np.float32)
    fw3 = np.stack([np.ascontiguousarray(f(hy_w3)[0][:, :, gi * 64:gi * 64 + 64]) for gi in range(8)], 0)
    fcols = np.stack([np.stack([f(hy_freq1)[0], f(hy_b1)[0], f(hy_freq2)[0], f(hy_b2)[0], f(hy_decay)[0][:, gi * 64:gi * 64 + 64]], -1).transpose(1, 0, 2)
                      for gi in range(8)], 0).astype(np.float32)
    dskip = np.stack([f(hy_skip_d)[0][gi * 64:gi * 64 + 64][None, :] for gi in range(8)], 0)
    ccols = np.concatenate([colpack(bin_[1536:2048]), colpack(bin_[2048:2560]), colpack(f(cf_dw_b)[0]), colpack(f(cf_ln_g)[0]), colpack(f(cf_ln_b)[0]),
                            np.ascontiguousarray(f(cf_dw_w)[0].T.reshape(4, 128, 31).transpose(1, 0, 2).reshape(128, 124))], 1).astype(np.float32)
    wqb = f(mla_wq_b)[0].reshape(384, NH, 96)
    WqH = np.concatenate([wqb[:, :, 64:96], np.zeros((384, NH, 32), np.float32), wqb[:, :, 0:64]], -1).reshape(384, NH * 128)
    WqS = np.concatenate([wqb[:, :, 80:96], wqb[:, :, 64:80]], -1).reshape(384, NH * 32)
    wkvb = f(mla_wkv_b)[0].reshape(256, NH, 128)
    WkH = np.concatenate([np.zeros((256, NH, 64), np.float32), wkvb[:, :, 0:64]], -1).reshape(256, NH * 128)
    WvH = np.ascontiguousarray(wkvb[:, :, 64:128]).reshape(256, NH * 64)
    zp_p, zp_s = zpos_table(LP), zpos_table(LS)
    co, si = rope_cs(np.concatenate([np.arange(16, LP), np.arange(16)]))
    cs_all = np.concatenate([co, si], 1)
    shared = dict(ident=ident, xpad_p=xpad_p, maskpad=maskpad, valid_s=valid_s, zpos_p=zp_p, zpos_s=zp_s, fw1=f(hy_w1)[0], fw2=f(hy_w2)[0],
                  fw3=fw3, fcols=fcols, why=why, brow=brow, hcols=hcols, dskip=dskip, wconf=np.ascontiguousarray(win[:, 1536:2560]), ccols=ccols,
                  wout=f(ev_w_out)[0], bout=f(ev_b_out)[0:1], ln1g=f(ln1_g)[:, None, :], ln1b=f(ln1_b)[:, None, :],
                  ln2g=f(ln2_g)[:, None, :], ln2b=f(ln2_b)[:, None, :], w1=f(mlp_w1), w2=f(mlp_w2), wqa=f(mla_wq_a)[0], qg=f(mla_q_norm)[0:1],
                  WqH=WqH, WqS=WqS, wkva=f(mla_wkv_a)[0], kvg=f(mla_kv_norm)[0:1], cs_all=cs_all, WkH=WkH, WvH=WvH, wo=f(mla_wo)[0])
    for k, v in tabP.items():
        shared["tp_" + k] = v
    for k, v in tabS.items():
        shared["ts_" + k] = v
    P = build_nc()
    ims = []
    for c in range(8):
        m0 = 16 + 2048 * c
        xb, mb = make_xc(hs[c], 16, LS)
        css, Cqs, Sqs = [], [], []
        for pos in (np.concatenate([np.arange(m0, m0 + 2048), np.arange(16)]), np.concatenate([np.arange(16, LS), np.arange(16)])):
            co, si = rope_cs(pos)
            css.append(np.concatenate([co, si], 1))
            Cqs.append(np.concatenate([co[:2048].T, co[:2048].T], 0))
            Sqs.append(np.concatenate([-si[:2048].T, si[:2048].T], 0))
        tix = (2048 * c + 128 * np.arange(16)[None, :] + np.arange(128)[:, None]).astype(np.uint32)
        im = dict(shared)
        im.update(xh_s=np.concatenate([z1, hs[c], z1], 0), xc_s=xb, mask_s=mb, tokidx=tix,
                  cs=np.stack(css, 0), Cq=np.stack(Cqs, 0), Sq=np.stack(Sqs, 0))
        ims.append(check_inputs(P, im))
    r = run_bass_kernel_spmd(P.nc, ims, core_ids=list(range(8))).results
    y_prompt = np.concatenate([np.asarray(r[c]["out"])[0] for c in range(8)], 0)[None].astype(np.float32)
    y_sample = np.stack([np.asarray(r[c]["out"])[1] for c in range(8)], 0).astype(np.float32)
    return (y_prompt, y_sample)
```

```python
import math
from contextlib import ExitStack
import numpy as np
import ml_dtypes
import concourse.bass as bass
import concourse.mybir as mybir
from concourse.bass_utils import run_bass_kernel_spmd

F32 = mybir.dt.float32
BF16 = mybir.dt.bfloat16
AF = mybir.ActivationFunctionType
ALU = mybir.AluOpType
AX = mybir.AxisListType

D = 1024
NMETA = 16
DFF = 4096
ALPHA = 4 ** 0.25
LN_EPS = 1e-5
RMS_EPS = 1e-6
NH = 16


SEM_MAX = 24000


class Dep:
    __slots__ = ("w", "r")

    def __init__(self):
        self.w = None
        self.r = {}


class KB:
    def __init__(self, nc):
        self.nc = nc
        self.stack = ExitStack()
        self.raw = dict(pe=nc.tensor, act=nc.scalar, dve=nc.vector, pool=nc.gpsimd, sp=nc.sync)
        self.sem = {}
        self.cnt = {}
        self.seen = {e: {} for e in self.raw}
        self.semobj = []
        for e in ("pe", "act", "dve", "pool"):
            self.sem[e] = self._newsem("s_" + e)
            self.cnt[e] = 0
        self.dq = {}
        for q, n in (("sp", 20), ("act", 8), ("pool", 8)):
            self.dq[q] = dict(sems=[self._newsem(f"d_{q}{i}") for i in range(n)], vals=[0] * n, nxt=0)
        self.uid = 0

    def _newsem(self, name):
        s = self.stack.enter_context(self.nc.semaphore(name))
        self.semobj.append(s)
        return len(self.semobj) - 1

    def name(self, p):
        self.uid += 1
        return f"{p}{self.uid}"

    def sb(self, st, shape, dt, name="t"):
        return st.enter_context(self.nc.sbuf_tensor(self.name(name), list(shape), dt))

    def ps(self, st, shape, dt, name="p"):
        return st.enter_context(self.nc.psum_tensor(self.name(name), list(shape), dt))

    def _waits(self, eng, reads, writes, extra=None):
        need = {}

        def add(tok):
            if tok is None:
                return
            s, v, src = tok
            if src == "pe" and eng == "pe":
                return
            if need.get(s, 0) < v:
                need[s] = v

        for d in reads:
            add(d.w)
        for d in writes:
            add(d.w)
            for t in d.r.values():
                add(t)
        if extra:
            for t in extra:
                add(t)
        seen = self.seen[eng]
        for s, v in need.items():
            if seen.get(s, 0) < v:
                self.raw[eng].wait_ge(self.semobj[s], v)
                seen[s] = v

    def op(self, eng, fn, reads=(), writes=()):
        self._waits(eng, reads, writes)
        ins = fn(self.raw[eng])
        if self.cnt[eng] >= SEM_MAX:
            self.sem[eng] = self._newsem(self.name("s_" + eng))
            self.cnt[eng] = 0
        self.cnt[eng] += 1
        ins.then_inc(self.semobj[self.sem[eng]], 1)
        tok = (self.sem[eng], self.cnt[eng], eng)
        for d in reads:
            d.r[tok[0]] = tok
        for d in writes:
            d.w = tok
            d.r = {}
        return ins

    def dma(self, q, out, in_, reads=(), writes=(), **kw):
        dq = self.dq[q]
        i = dq["nxt"]
        dq["nxt"] = (i + 1) % len(dq["sems"])
        s = dq["sems"][i]
        extra = [(s, dq["vals"][i], "dma")] if dq["vals"][i] else None
        self._waits(q, reads, writes, extra)
        ins = self.raw[q].dma_start(out=out, in_=in_, **kw)
        dq["vals"][i] += 16
        ins.then_inc(self.semobj[s], 16)
        tok = (s, dq["vals"][i], "dma")
        for d in reads:
            d.r[s] = tok
        for d in writes:
            d.w = tok
            d.r = {}
        return ins

    def all_gather(self, in_ap, out_ap, reads=(), writes=()):
        if not hasattr(self, "ccsem"):
            self.ccsem = self._newsem("ccsem")
            self.ccval = 0
        self._waits("pool", reads, writes)
        ins = self.raw["pool"].collective_compute("AllGather", ALU.bypass, replica_groups=[list(range(8))],
                                                  ins=[in_ap.opt()], outs=[out_ap.opt()])
        self.ccval += 1
        ins.then_inc(self.semobj[self.ccsem], 1)
        tok = (self.ccsem, self.ccval, "cc")
        for d in reads:
            d.r[self.ccsem] = tok
        for d in writes:
            d.w = tok
            d.r = {}
        return ins

    def gather_rows(self, out, in_rows, idx, reads=(), writes=()):
        dq = self.dq["pool"]
        i = dq["nxt"]
        dq["nxt"] = (i + 1) % len(dq["sems"])
        s = dq["sems"][i]
        extra = [(s, dq["vals"][i], "dma")] if dq["vals"][i] else None
        self._waits("pool", reads, writes, extra)
        ins = self.raw["pool"].indirect_dma_start(out=out, out_offset=None, in_=in_rows,
                                                  in_offset=bass.IndirectOffsetOnAxis(ap=idx, axis=0))
        dq["vals"][i] += 16
        ins.then_inc(self.semobj[s], 16)
        tok = (s, dq["vals"][i], "dma")
        for d in reads:
            d.r[s] = tok
        for d in writes:
            d.w = tok
            d.r = {}
        return ins

    def barrier(self):
        toks = [(self.sem[e], self.cnt[e], e) for e in ("pe", "act", "dve", "pool") if self.cnt[e]]
        for q in self.dq.values():
            for s, v in zip(q["sems"], q["vals"]):
                if v:
                    toks.append((s, v, "dma"))
        if getattr(self, "ccval", 0):
            toks.append((self.ccsem, self.ccval, "cc"))
        for eng in ("pe", "act", "dve", "pool", "sp"):
            seen = self.seen[eng]
            for s, v, src in toks:
                if seen.get(s, 0) < v and not (s == self.sem.get(eng)):
                    self.raw[eng].wait_ge(self.semobj[s], v)
                    seen[s] = v

    def finish_wait(self):
        for q in self.dq.values():
            for s, v in zip(q["sems"], q["vals"]):
                if v and self.seen["sp"].get(s, 0) < v:
                    self.raw["sp"].wait_ge(self.semobj[s], v)
                    self.seen["sp"][s] = v


class Glob:
    pass


def setup_globals(kb, st):
    g = Glob()
    nc = kb.nc
    g.pall = kb.ps(st, [128, 8, 512], F32, "banks")
    g.psum = [g.pall[:, b, :] for b in range(8)]
    g.pd = [Dep() for _ in range(8)]
    g.ident_f = kb.sb(st, [128, 128], F32, "identf")
    g.ident_b = kb.sb(st, [128, 128], BF16, "identb")
    g.ident_d = Dep()
    g.ones_b = kb.sb(st, [128, 128], BF16, "onesb")
    g.ones_d = Dep()
    g.bk = -1
    return g


def load_ident(kb, g, ident_dram):
    kb.dma("sp", g.ident_f[:], ident_dram, writes=[g.ident_d])
    kb.op("dve", lambda e: e.tensor_copy(out=g.ident_b[:], in_=g.ident_f[:]), reads=[g.ident_d], writes=[g.ident_d])
    kb.op("pool", lambda e: e.memset(g.ones_b[:], 1.0), writes=[g.ones_d])


_rr = [0]


def cast_eng():
    _rr[0] += 1
    return ("dve", "pool", "act")[_rr[0] % 3]


def copy_op(kb, eng, out, in_, reads, writes):
    if eng == "act":
        return kb.op("act", lambda e: e.copy(out=out, in_=in_), reads=reads, writes=writes)
    return kb.op(eng, lambda e: e.tensor_copy(out=out, in_=in_), reads=reads, writes=writes)


def load_weight_bf16(kb, st_phase, dst, dst_dep, src, kc, ncols, stage_cols=2048):
    with ExitStack() as st:
        stg = [kb.sb(st, [128, stage_cols], F32, "wstg") for _ in range(3)]
        sd = [Dep() for _ in range(3)]
        i = 0
        for k in range(kc):
            for c0 in range(0, ncols, stage_cols):
                cn = min(stage_cols, ncols - c0)
                j = i % 3
                kb.dma("sp" if i % 2 == 0 else "pool", stg[j][:, :cn], src[k * 128:(k + 1) * 128, c0:c0 + cn], writes=[sd[j]])
                copy_op(kb, ("dve", "act")[i % 2], dst[:, k, c0:c0 + cn], stg[j][:, :cn], [sd[j]], [dst_dep])
                i += 1
        kb.barrier()


def load_bcast(kb, dst, dep, src_row):
    kb.dma("sp", dst, src_row.partition_broadcast(128) if len(src_row.shape) == 1 else src_row.broadcast_to([128, src_row.shape[-1]]), writes=[dep])


def layer_norm_tile(kb, r, rd, n, gt, bt, gbd, out, outd, small, smd, junk, junkd):
    s1, s2 = small[:, 0:1], small[:, 1:2]
    kb.op("act", lambda e: e.activation(out=junk[:n, :], in_=r[:n, :], func=AF.Identity, accum_out=s1[:n, :]), reads=[rd], writes=[junkd, smd])
    kb.op("act", lambda e: e.activation(out=junk[:n, :], in_=r[:n, :], func=AF.Square, accum_out=s2[:n, :]), reads=[rd], writes=[junkd, smd])
    mean, var, rstd = small[:, 2:3], small[:, 3:4], small[:, 4:5]
    kb.op("dve", lambda e: e.tensor_scalar(out=mean[:n, :], in0=s1[:n, :], scalar1=1.0 / D, scalar2=None, op0=ALU.mult), reads=[smd], writes=[smd])
    kb.op("dve", lambda e: e.tensor_tensor(out=var[:n, :], in0=mean[:n, :], in1=mean[:n, :], op=ALU.mult), reads=[smd], writes=[smd])
    kb.op("dve", lambda e: e.scalar_tensor_tensor(out=var[:n, :], in0=s2[:n, :], scalar=1.0 / D, in1=var[:n, :], op0=ALU.mult, op1=ALU.subtract), reads=[smd], writes=[smd])
    kb.op("act", lambda e: e.activation(out=rstd[:n, :], in_=var[:n, :], func=AF.Sqrt, bias=LN_EPS, scale=1.0), reads=[smd], writes=[smd])
    kb.op("dve", lambda e: e.reciprocal(out=rstd[:n, :], in_=rstd[:n, :]), reads=[smd], writes=[smd])
    kb.op("dve", lambda e: e.tensor_scalar(out=r[:n, :], in0=r[:n, :], scalar1=mean[:n, :], scalar2=rstd[:n, :], op0=ALU.subtract, op1=ALU.mult), reads=[smd, rd], writes=[rd])
    kb.op("pool", lambda e: e.tensor_tensor(out=r[:n, :], in0=r[:n, :], in1=gt[:n, :], op=ALU.mult), reads=[rd, gbd], writes=[rd])
    kb.op("pool", lambda e: e.tensor_tensor(out=out[:n, :], in0=r[:n, :], in1=bt[:n, :], op=ALU.add), reads=[rd, gbd], writes=[outd])


def mm(kb, out, lhsT, rhs, start, stop, reads, writes):
    return kb.op("pe", lambda e: e.matmul(out, lhsT=lhsT, rhs=rhs, start=start, stop=stop), reads, writes)


def tt(kb, eng, out, in0, in1, op, reads, writes):
    return kb.op(eng, lambda e: e.tensor_tensor(out=out, in0=in0, in1=in1, op=op), reads, writes)


def ts(kb, eng, out, in0, s1, s2, op0, op1, reads, writes):
    if s2 is None:
        return kb.op(eng, lambda e: e.tensor_scalar(out=out, in0=in0, scalar1=s1, scalar2=None, op0=op0), reads, writes)
    return kb.op(eng, lambda e: e.tensor_scalar(out=out, in0=in0, scalar1=s1, scalar2=s2, op0=op0, op1=op1), reads, writes)


def stt(kb, eng, out, in0, scalar, in1, op0, op1, reads, writes):
    return kb.op("dve", lambda e: e.scalar_tensor_tensor(out=out, in0=in0, scalar=scalar, in1=in1, op0=op0, op1=op1), reads, writes)


def act(kb, out, in_, func, reads, writes, **kw):
    return kb.op("act", lambda e: e.activation(out=out, in_=in_, func=func, **kw), reads, writes)


def nextbank(g):
    g.bk = (g.bk + 1) % 8
    return g.bk


def transpose_tile(kb, g, src, srcd, n, dstT, dstd, col0, kc=8):
    for k0 in range(0, kc, 4):
        b = nextbank(g)
        kn = min(4, kc - k0)
        pv = g.psum[b][:, :].rearrange("p (k t) -> p k t", k=4)
        for k in range(kn):
            kb.op("pe", lambda e, k=k: e.transpose(pv[:, k, :n], src[:n, (k0 + k) * 128:(k0 + k + 1) * 128], g.ident_f[:n, :n]),
                  reads=[srcd, g.ident_d], writes=[g.pd[b]])
        copy_op(kb, ("dve", "act")[b % 2], dstT[:, k0:k0 + kn, col0:col0 + n], pv[:, 0:kn, :n], [g.pd[b]], [dstd])


def phase_proj_ln(kb, g, tiles, fm, W, bias, lng, lnb):
    with ExitStack() as st:
        Wb = kb.sb(st, [128, 8, D], BF16, "Wb")
        Wd = Dep()
        load_weight_bf16(kb, st, Wb, Wd, W, 8, D)
        gt = kb.sb(st, [128, D], F32, "g")
        bt = kb.sb(st, [128, D], F32, "b")
        gbd = Dep()
        load_bcast(kb, gt[:], gbd, lng)
        load_bcast(kb, bt[:], gbd, lnb)
        if bias is not None:
            bi = kb.sb(st, [128, D], F32, "bias")
            load_bcast(kb, bi[:], gbd, bias)
        NB = 2
        hb = [kb.sb(st, [128, D], F32, "h") for _ in range(NB)]
        hd = [Dep() for _ in range(NB)]
        yT = [kb.sb(st, [128, 8, 128], BF16, "yT") for _ in range(NB)]
        yTd = [Dep() for _ in range(NB)]
        if not fm:
            yb = [kb.sb(st, [128, D], F32, "y") for _ in range(NB)]
            yd = [Dep() for _ in range(NB)]
        rb = [kb.sb(st, [128, D], F32, "r") for _ in range(NB)]
        rd = [Dep() for _ in range(NB)]
        junk = kb.sb(st, [128, D], F32, "junk")
        junkd = Dep()
        small = [kb.sb(st, [128, 8], F32, "small") for _ in range(NB)]
        smd = [Dep() for _ in range(NB)]
        for i, (hap, yap, oap, n) in enumerate(tiles):
            j = i % NB
            kb.dma("sp", hb[j][:n, :], hap, writes=[hd[j]])
            if fm:
                for qi, (ksl, src, dep) in enumerate(yap):
                    kb.dma(("pool", "sp")[qi % 2], yT[j][:, ksl, :n], src, reads=[dep] if dep is not None else [], writes=[yTd[j]])
            else:
                kb.dma("pool", yb[j][:n, :], yap, writes=[yd[j]])
                transpose_tile(kb, g, yb[j], yd[j], n, yT[j], yTd[j], 0)
            bks = (nextbank(g), nextbank(g))
            for half, bk in enumerate(bks):
                for k in range(8):
                    mm(kb, g.psum[bk][:n, :], yT[j][:, k, :n], Wb[:, k, half * 512:(half + 1) * 512], k == 0, k == 7,
                       [yTd[j], Wd], [g.pd[bk]])
            for half, bk in enumerate(bks):
                sl = slice(half * 512, (half + 1) * 512)
                if bias is not None:
                    tt(kb, "dve", rb[j][:n, sl], g.psum[bk][:n, :], bi[:n, sl], ALU.add, [g.pd[bk], gbd], [rd[j]])
                else:
                    copy_op(kb, "act", rb[j][:n, sl], g.psum[bk][:n, :], [g.pd[bk]], [rd[j]])
            stt(kb, "pool", rb[j][:n, :], hb[j][:n, :], ALPHA, rb[j][:n, :], ALU.mult, ALU.add, [hd[j], rd[j]], [rd[j]])
            layer_norm_tile(kb, rb[j], rd[j], n, gt, bt, gbd, rb[j], rd[j], small[j], smd[j], junk, junkd)
            kb.dma("sp", oap, rb[j][:n, :], reads=[rd[j]])
        kb.barrier()


def phase_mlp_ln(kb, g, tiles, W1, W2, lng, lnb):
    with ExitStack() as st:
        W1b = kb.sb(st, [128, 8, DFF], BF16, "W1b")
        W2b = kb.sb(st, [128, 32, D], BF16, "W2b")
        Wd = Dep()
        load_weight_bf16(kb, st, W1b, Wd, W1, 8, DFF)
        load_weight_bf16(kb, st, W2b, Wd, W2, 32, D, stage_cols=1024)
        gt = kb.sb(st, [128, D], F32, "g")
        bt = kb.sb(st, [128, D], F32, "b")
        gbd = Dep()
        load_bcast(kb, gt[:], gbd, lng)
        load_bcast(kb, bt[:], gbd, lnb)
        hb = [kb.sb(st, [128, D], F32, "h") for _ in range(4)]
        hd = [Dep() for _ in range(4)]
        hT = kb.sb(st, [128, 8, 512], BF16, "hT")
        hTd = Dep()
        uT = kb.sb(st, [128, 32, 512], BF16, "uT")
        uTd = [Dep() for _ in range(32)]
        rl = [kb.sb(st, [128, 512], F32, "relu") for _ in range(2)]
        rld = [Dep() for _ in range(2)]
        junk = kb.sb(st, [128, D], BF16, "junk")
        junkd = Dep()
        small = [kb.sb(st, [128, 8], F32, "small") for _ in range(4)]
        smd = [Dep() for _ in range(4)]
        for s0 in range(0, len(tiles), 4):
            grp = tiles[s0:s0 + 4]
            offs = []
            tot = 0
            for i, (iap, oap, n) in enumerate(grp):
                kb.dma("sp" if i % 2 == 0 else "pool", hb[i][:n, :], iap, writes=[hd[i]])
                offs.append(tot)
                tot += n
            for i, (iap, oap, n) in enumerate(grp):
                transpose_tile(kb, g, hb[i], hd[i], n, hT, hTd, offs[i])
            for j in range(32):
                bk = nextbank(g)
                for k in range(8):
                    mm(kb, g.psum[bk][:, :tot], W1b[:, k, j * 128:(j + 1) * 128], hT[:, k, :tot], k == 0, k == 7, [Wd, hTd], [g.pd[bk]])
                q = j % 2
                act(kb, rl[q][:, :tot], g.psum[bk][:, :tot], AF.Relu, [g.pd[bk]], [rld[q]])
                tt(kb, "pool" if j % 4 < 3 else "dve", uT[:, j, :tot], rl[q][:, :tot], rl[q][:, :tot], ALU.mult, [rld[q]], [uTd[j]])
            for i, (iap, oap, n) in enumerate(grp):
                bks = (nextbank(g), nextbank(g))
                for half, bk in enumerate(bks):
                    for j in range(32):
                        mm(kb, g.psum[bk][:n, :], uT[:, j, offs[i]:offs[i] + n], W2b[:, j, half * 512:(half + 1) * 512], j == 0, j == 31,
                           [uTd[j], Wd], [g.pd[bk]])
                for half, bk in enumerate(bks):
                    sl = slice(half * 512, (half + 1) * 512)
                    stt(kb, "dve", hb[i][:n, sl], hb[i][:n, sl], ALPHA, g.psum[bk][:n, :], ALU.mult, ALU.add, [hd[i], g.pd[bk]], [hd[i]])
                layer_norm_tile(kb, hb[i], hd[i], n, gt, bt, gbd, hb[i], hd[i], small[i], smd[i], junk, junkd)
                kb.dma("sp", oap, hb[i][:n, :], reads=[hd[i]])
        kb.barrier()


def rms_rstd(kb, src, srcd, n, width, small, smd, junk, junkd, col):
    ss, rs = small[:, col:col + 1], small[:, col + 1:col + 2]
    act(kb, junk[:n, :width], src[:n, :width], AF.Square, [srcd], [junkd, smd], accum_out=ss[:n, :])
    act(kb, rs[:n, :], ss[:n, :], AF.Sqrt, [smd], [smd], bias=RMS_EPS, scale=1.0 / width)
    kb.op("dve", lambda e: e.reciprocal(out=rs[:n, :], in_=rs[:n, :]), [smd], [smd])
    return rs


def phase_qkv(kb, g, seqs, wqa, qg, WqH, WqS, wkva, kvg):
    with ExitStack() as st:
        wqa_b = kb.sb(st, [128, 8, 384], BF16, "wqa")
        wkva_b = kb.sb(st, [128, 8, 288], BF16, "wkva")
        wqh_b = kb.sb(st, [128, 3, NH * 128], BF16, "wqh")
        wqs_b = kb.sb(st, [128, 3, NH * 32], BF16, "wqs")
        Wd = Dep()
        load_weight_bf16(kb, st, wqa_b, Wd, wqa, 8, 384)
        load_weight_bf16(kb, st, wkva_b, Wd, wkva, 8, 288)
        load_weight_bf16(kb, st, wqh_b, Wd, WqH, 3, NH * 128)
        load_weight_bf16(kb, st, wqs_b, Wd, WqS, 3, NH * 32)
        qgt = kb.sb(st, [128, 384], F32, "qg")
        kvgt = kb.sb(st, [128, 256], F32, "kvg")
        gd = Dep()
        load_bcast(kb, qgt[:], gd, qg)
        load_bcast(kb, kvgt[:], gd, kvg)
        hb = [kb.sb(st, [128, D], F32, "h") for _ in range(4)]
        hd = [Dep() for _ in range(4)]
        hT = kb.sb(st, [128, 8, 512], BF16, "hT")
        hTd = Dep()
        cq = [kb.sb(st, [128, 384], F32, "cq") for _ in range(2)]
        cqd = [Dep() for _ in range(2)]
        cqT = kb.sb(st, [128, 3, 512], BF16, "cqT")
        cqTd = Dep()
        kvr = [kb.sb(st, [128, 288], F32, "kvr") for _ in range(2)]
        kvrd = [Dep() for _ in range(2)]
        kvo = [kb.sb(st, [128, 288], F32, "kvo") for _ in range(2)]
        kvod = [Dep() for _ in range(2)]
        cst = [kb.sb(st, [128, 32], F32, "cs") for _ in range(2)]
        csd = [Dep() for _ in range(2)]
        tmp = [kb.sb(st, [128, 64], F32, "tmp") for _ in range(2)]
        tmpd = [Dep() for _ in range(2)]
        junk = kb.sb(st, [128, 384], F32, "junk")
        junkd = Dep()
        small = [kb.sb(st, [128, 8], F32, "small") for _ in range(2)]
        smd = [Dep() for _ in range(2)]
        Ct = kb.sb(st, [32, 2048], F32, "C")
        St = kb.sb(st, [32, 2048], F32, "S")
        CSd = Dep()
        qsw = [kb.sb(st, [32, 512], F32, "qsw") for _ in range(2)]
        qswd = [Dep() for _ in range(2)]
        qo = [kb.sb(st, [128, 512], BF16, "qo") for _ in range(2)]
        qod = [Dep() for _ in range(2)]
        it = 0
        for sq in seqs:
            if not sq.get("kv_only"):
                kb.dma("sp", Ct[:], sq["CS"][0], writes=[CSd])
                kb.dma("sp", St[:], sq["CS"][1], writes=[CSd])
            tiles = sq["tiles"]
            for s0 in range(0, len(tiles), 4):
                grp = tiles[s0:s0 + 4]
                offs, tot = [], 0
                for i, (hap, kvap, csap, n) in enumerate(grp):
                    kb.dma("sp" if i % 2 == 0 else "pool", hb[i][:n, :], hap, writes=[hd[i]])
                    offs.append(tot)
                    tot += n
                for i, (hap, kvap, csap, n) in enumerate(grp):
                    transpose_tile(kb, g, hb[i], hd[i], n, hT, hTd, offs[i])
                is_main = (tot == 512) and not sq.get("kv_only")
                for i, (hap, kvap, csap, n) in enumerate(grp):
                    j = it % 2
                    it += 1
                    kb.dma("pool", cst[j][:n, :], csap, writes=[csd[j]])
                    bk = nextbank(g)
                    for k in range(8):
                        mm(kb, g.psum[bk][:n, :288], hT[:, k, offs[i]:offs[i] + n], wkva_b[:, k, :], k == 0, k == 7, [hTd, Wd], [g.pd[bk]])
                    copy_op(kb, "act", kvr[j][:n, :], g.psum[bk][:n, :288], [g.pd[bk]], [kvrd[j]])
                    rs = rms_rstd(kb, kvr[j], kvrd[j], n, 256, small[j], smd[j], junk, junkd, 0)
                    stt(kb, "dve", kvo[j][:n, 0:256], kvr[j][:n, 0:256], rs[:n, :], kvgt[:n, :], ALU.mult, ALU.mult, [kvrd[j], smd[j], gd], [kvod[j]])
                    x1, x2 = kvr[j][:n, 256:272], kvr[j][:n, 272:288]
                    co, si = cst[j][:n, 0:16], cst[j][:n, 16:32]
                    t = tmp[j]
                    tt(kb, "pool", t[:n, 0:16], x1, co, ALU.mult, [kvrd[j], csd[j]], [tmpd[j]])
                    tt(kb, "pool", t[:n, 16:32], x2, si, ALU.mult, [kvrd[j], csd[j]], [tmpd[j]])
                    tt(kb, "pool", t[:n, 32:48], x1, si, ALU.mult, [kvrd[j], csd[j]], [tmpd[j]])
                    tt(kb, "pool", t[:n, 48:64], x2, co, ALU.mult, [kvrd[j], csd[j]], [tmpd[j]])
                    tt(kb, "dve", kvo[j][:n, 256:272], t[:n, 0:16], t[:n, 16:32], ALU.subtract, [tmpd[j]], [kvod[j]])
                    tt(kb, "dve", kvo[j][:n, 272:288], t[:n, 32:48], t[:n, 48:64], ALU.add, [tmpd[j]], [kvod[j]])
                    kb.dma("sp", kvap, kvo[j][:n, :], reads=[kvod[j]])
                    if not is_main:
                        continue
                    bk = nextbank(g)
                    for k in range(8):
                        mm(kb, g.psum[bk][:n, :384], hT[:, k, offs[i]:offs[i] + n], wqa_b[:, k, :], k == 0, k == 7, [hTd, Wd], [g.pd[bk]])
                    copy_op(kb, "act", cq[j][:n, :], g.psum[bk][:n, :384], [g.pd[bk]], [cqd[j]])
                    rs = rms_rstd(kb, cq[j], cqd[j], n, 384, small[j], smd[j], junk, junkd, 2)
                    stt(kb, "dve", cq[j][:n, :], cq[j][:n, :], rs[:n, :], qgt[:n, :], ALU.mult, ALU.mult, [cqd[j], smd[j], gd], [cqd[j]])
                    transpose_tile(kb, g, cq[j], cqd[j], n, cqT, cqTd, offs[i], kc=3)
                if not is_main:
                    continue
                q0 = (s0 // 4) * 512
                for h in range(NH):
                    j = h % 2
                    bka, bkb = nextbank(g), nextbank(g)
                    for k in range(3):
                        mm(kb, g.psum[bka][:, :], wqh_b[:, k, h * 128:(h + 1) * 128], cqT[:, k, :], k == 0, k == 2, [Wd, cqTd], [g.pd[bka]])
                    for k in range(3):
                        mm(kb, g.psum[bkb][:32, :], wqs_b[:, k, h * 32:(h + 1) * 32], cqT[:, k, :], k == 0, k == 2, [Wd, cqTd], [g.pd[bkb]])
                    tt(kb, "dve", qsw[j][:, :], g.psum[bkb][:32, :], St[:, q0:q0 + 512], ALU.mult, [g.pd[bkb], CSd], [qswd[j]])
                    rope_q(kb, g, qo[j], qod[j], bka, qsw[j], qswd[j], Ct, CSd, q0, st, small)
                    copy_op(kb, "act", qo[j][32:64, :], g.psum[bka][32:64, :], [g.pd[bka]], [qod[j]])
                    copy_op(kb, "act", qo[j][64:128, :], g.psum[bka][64:128, :], [g.pd[bka]], [qod[j]])
                    kb.dma("pool", sq["qt"](h, q0), qo[j][:, :], reads=[qod[j]])
        kb.barrier()


_ropetmp = {}


def rope_q(kb, g, qo, qod, bka, qsw, qswd, Ct, CSd, q0, st, small):
    key = id(st)
    if key not in _ropetmp:
        _ropetmp[key] = (kb.sb(st, [32, 512], F32, "rq"), Dep())
    t, td = _ropetmp[key]
    tt(kb, "dve", t[:, :], g.psum[bka][0:32, :], Ct[:, q0:q0 + 512], ALU.mult, [g.pd[bka], CSd], [td])
    tt(kb, "pool", qo[0:32, :], t[:, :], qsw[:, :], ALU.add, [td, qswd], [qod])


QK_SCALE = 96 ** -0.5


def phase_attn(kb, g, seqs, WkH, WvH):
    NKmax = max(sum(n for _, n in sq["kchunks"]) for sq in seqs)
    NCH = max(len(sq["kchunks"]) for sq in seqs)
    with ExitStack() as st:
        wk_b = kb.sb(st, [128, 2, NH * 128], BF16, "wk")
        wv_b = kb.sb(st, [128, 2, NH * 64], BF16, "wv")
        Wd = Dep()
        load_weight_bf16(kb, st, wk_b, Wd, WkH, 2, NH * 128)
        load_weight_bf16(kb, st, wv_b, Wd, WvH, 2, NH * 64, stage_cols=1024)
        ckvT = kb.sb(st, [128, 3, NKmax], BF16, "ckvT")
        ckvTd = Dep()
        KT = kb.sb(st, [128, NKmax], BF16, "KT")
        KTd = Dep()
        V = kb.sb(st, [128, NCH, 66], BF16, "V")
        Vd = Dep()
        QT = [kb.sb(st, [128, 2048], BF16, "QT") for _ in range(2)]
        QTd = [Dep() for _ in range(2)]
        P = [kb.sb(st, [128, 1024], BF16, "P") for _ in range(2)]
        Pd = [Dep() for _ in range(2)]
        oT = kb.sb(st, [128, 1024], F32, "oT")
        oTd = Dep()
        osm = [kb.sb(st, [128, 4, 64], F32, "osm") for _ in range(2)]
        osmd = [Dep() for _ in range(2)]
        rec = [kb.sb(st, [128, 4, 1], F32, "rec") for _ in range(2)]
        recd = [Dep() for _ in range(2)]
        kvin = [kb.sb(st, [128, 288], F32, "kvin") for _ in range(2)]
        kvind = [Dep() for _ in range(2)]
        kb.op("pool", lambda e: e.memset(V[:, :, 64:66], 1.0), [], [Vd])
        for sq in seqs:
            chunks = sq["kchunks"]
            NK = sum(n for _, n in chunks)
            coff = []
            c0 = 0
            for ci, (kvap, n) in enumerate(chunks):
                j = ci % 2
                kb.dma("sp" if ci % 2 == 0 else "pool", kvin[j][:n, :], kvap, writes=[kvind[j]])
                b = nextbank(g)
                pv = g.psum[b].rearrange("p (k t) -> p k t", k=4)
                for k, w in ((0, 128), (1, 128), (2, 32)):
                    kb.op("pe", lambda e, k=k, w=w: e.transpose(pv[:w, k, :n], kvin[j][:n, k * 128:k * 128 + w], g.ident_f[:n, :n]),
                          [kvind[j], g.ident_d], [g.pd[b]])
                copy_op(kb, "dve", ckvT[:, 0:2, c0:c0 + n], pv[:, 0:2, :n], [g.pd[b]], [ckvTd])
                copy_op(kb, "act", ckvT[0:32, 2, c0:c0 + n], pv[0:32, 2, :n], [g.pd[b]], [ckvTd])
                coff.append(c0)
                c0 += n
            for h in range(NH):
                qj = h % 2
                kb.dma("sp", QT[qj][:, :], sq["qt"](h), writes=[QTd[qj]])
                for bi, k0 in enumerate(range(0, NK, 512)):
                    kn = min(512, NK - k0)
                    b = 6 + bi % 2
                    mm(kb, g.psum[b][:, :kn], wk_b[:, 0, h * 128:(h + 1) * 128], ckvT[:, 0, k0:k0 + kn], True, False, [Wd, ckvTd], [g.pd[b]])
                    mm(kb, g.psum[b][:, :kn], wk_b[:, 1, h * 128:(h + 1) * 128], ckvT[:, 1, k0:k0 + kn], False, False, [Wd, ckvTd], [g.pd[b]])
                    mm(kb, g.psum[b][:, :kn], g.ident_b[0:32, :], ckvT[0:32, 2, k0:k0 + kn], False, True, [g.ident_d, ckvTd], [g.pd[b]])
                    copy_op(kb, ("dve", "pool")[bi % 2] if False else "dve", KT[:, k0:k0 + kn], g.psum[b][:, :kn], [g.pd[b]], [KTd])
                for gi, cg in enumerate(range(0, len(chunks), 8)):
                    cn = min(8, len(chunks) - cg)
                    b = 6 + gi % 2
                    for ci in range(cn):
                        n = chunks[cg + ci][1]
                        o = coff[cg + ci]
                        for k in range(2):
                            mm(kb, g.psum[b][:n, ci * 64:(ci + 1) * 64], ckvT[:, k, o:o + n], wv_b[:, k, h * 64:(h + 1) * 64], k == 0, k == 1,
                               [ckvTd, Wd], [g.pd[b]])
                    copy_op(kb, "act", V[:, cg:cg + cn, 0:64], g.psum[b][:, :cn * 64].rearrange("p (c d) -> p c d", d=64), [g.pd[b]], [Vd])
                for qsb in range(2):
                    for ci, (kvap, n) in enumerate(chunks):
                        o = coff[ci]
                        sb0 = 2 + 2 * (ci % 2)
                        pj = ci % 2
                        for i in range(2):
                            mm(kb, g.psum[sb0 + i][:n, :], KT[:, o:o + n], QT[qj][:, qsb * 1024 + i * 512:qsb * 1024 + (i + 1) * 512], True, True,
                               [KTd, QTd[qj]], [g.pd[sb0 + i]])
                        act(kb, P[pj][:n, :].rearrange("p (a b) -> p a b", a=2), g.pall[:n, sb0:sb0 + 2, :], AF.Exp,
                            [g.pd[sb0], g.pd[sb0 + 1]], [Pd[pj]], scale=QK_SCALE)
                        for i in range(2):
                            mm(kb, g.psum[i][:65, :], V[:n, ci, 0:65], P[pj][:n, i * 512:(i + 1) * 512], ci == 0, ci == len(chunks) - 1,
                               [Vd, Pd[pj]], [g.pd[i]])
                    copy_op(kb, "dve", oT[:65, :].rearrange("p (a b) -> p a b", a=2), g.pall[:65, 0:2, :], [g.pd[0], g.pd[1]], [oTd])
                    for half in range(2):
                        b = 6 + half
                        oj = half
                        pv = g.psum[b][:, 0:4 * 65].rearrange("p (t c) -> p t c", c=65)
                        for t in range(4):
                            q0 = half * 512 + t * 128
                            kb.op("pe", lambda e, t=t, q0=q0: e.transpose(pv[:, t, :], oT[:65, q0:q0 + 128], g.ident_f[:65, :65]),
                                  [oTd, g.ident_d], [g.pd[b]])
                        kb.op("dve", lambda e: e.reciprocal(out=rec[oj][:, :, :], in_=pv[:, :, 64:65]), [g.pd[b]], [recd[oj]])
                        tt(kb, "dve", osm[oj][:, :, :], pv[:, :, 0:64], rec[oj][:, :, :].broadcast_to([128, 4, 64]), ALU.mult,
                           [g.pd[b], recd[oj]], [osmd[oj]])
                        kb.dma("pool", sq["o"](qsb, half, h), osm[oj][:, :, :], reads=[osmd[oj]])
        kb.barrier()


XC = 2124


def phase_conf(kb, g, seqs, w_conf, cols_ap):
    with ExitStack() as st:
        wb = kb.sb(st, [128, 8, 1024], BF16, "wconf")
        Wd = Dep()
        load_weight_bf16(kb, st, wb, Wd, w_conf, 8, 1024)
        cols = kb.sb(st, [128, 20 + 124], F32, "cols")
        cd = Dep()
        kb.dma("sp", cols[:], cols_ap, writes=[cd])
        Dg = kb.sb(st, [128, 4, 31, 128], BF16, "Dg")
        Dgd = Dep()
        for j in range(4):
            for k in range(31):
                ts(kb, ("dve", "pool")[k % 2], Dg[:, j, k, :], g.ident_f[:, :], cols[:, 20 + j * 31 + k:20 + j * 31 + k + 1], None, ALU.mult, None,
                   [g.ident_d, cd], [Dgd])
        xin = [kb.sb(st, [128, D], F32, "xin") for _ in range(2)]
        xind = [Dep() for _ in range(2)]
        xT = kb.sb(st, [128, 8, XC], BF16, "xT")
        xTd = Dep()
        hT = kb.sb(st, [128, 4, XC], BF16, "hT")
        hTd = Dep()
        mask = kb.sb(st, [128, XC], F32, "mask")
        maskd = Dep()
        sg = [kb.sb(st, [128, 512], F32, "sg") for _ in range(2)]
        sgd = [Dep() for _ in range(2)]
        cc = kb.sb(st, [128, 4, 512], F32, "cc")
        ccd = Dep()
        cb = kb.sb(st, [128, 4, 512], BF16, "cb")
        cbd = Dep()
        sq = kb.sb(st, [128, 4, 512], BF16, "sq")
        sqd = Dep()
        mean = kb.sb(st, [128, 512], F32, "mean")
        rstd = kb.sb(st, [128, 512], F32, "rstd")
        std = Dep()
        yt = [kb.sb(st, [128, 512], F32, "yt") for _ in range(2)]
        ytd = [Dep() for _ in range(2)]
        yo = [kb.sb(st, [128, 512], BF16, "yo") for _ in range(2)]
        yod = [Dep() for _ in range(2)]
        for s_ in seqs:
            NC = s_.get("ncols", XC)
            kb.dma("sp", mask[:, :NC], s_["mask"].broadcast_to([128, NC]), writes=[maskd])
            for ti, t0 in enumerate(range(0, NC, 128)):
                n = min(128, NC - t0)
                j = ti % 2
                kb.dma("sp" if ti % 2 == 0 else "pool", xin[j][:n, :], s_["x"][t0:t0 + n, :], writes=[xind[j]])
                transpose_tile(kb, g, xin[j], xind[j], n, xT, xTd, t0)
            for bi, c0 in enumerate(range(0, NC, 512)):
                cn = min(512, NC - c0)
                for j in range(4):
                    ba, bg = nextbank(g), nextbank(g)
                    for k in range(8):
                        mm(kb, g.psum[ba][:, :cn], wb[:, k, j * 128:(j + 1) * 128], xT[:, k, c0:c0 + cn], k == 0, k == 7, [Wd, xTd], [g.pd[ba]])
                    for k in range(8):
                        mm(kb, g.psum[bg][:, :cn], wb[:, k, 512 + j * 128:512 + (j + 1) * 128], xT[:, k, c0:c0 + cn], k == 0, k == 7, [Wd, xTd], [g.pd[bg]])
                    q = j % 2
                    act(kb, sg[q][:, :cn], g.psum[bg][:, :cn], AF.Sigmoid, [g.pd[bg], cd], [sgd[q]], bias=cols[:, 4 + j:5 + j], scale=1.0)
                    stt(kb, "dve", sg[q][:, :cn], g.psum[ba][:, :cn], cols[:, j:j + 1], sg[q][:, :cn], ALU.add, ALU.mult, [g.pd[ba], cd, sgd[q]], [sgd[q]])
                    tt(kb, "pool", hT[:, j, c0:c0 + cn], sg[q][:, :cn], mask[:, c0:c0 + cn], ALU.mult, [sgd[q], maskd], [hTd])
            blocks = s_.get("blocks") or ([(15, 16)] + [(61 + 512 * i, 512) for i in range(4)])
            for bi, (c0, cn) in enumerate(blocks):
                for j in range(4):
                    b = nextbank(g)
                    for k in range(31):
                        mm(kb, g.psum[b][:, :cn], Dg[:, j, k, :], hT[:, j, c0 + k - 15:c0 + k - 15 + cn], k == 0, k == 30, [Dgd, hTd], [g.pd[b]])
                    act(kb, cc[:, j, :cn], g.psum[b][:, :cn], AF.Identity, [g.pd[b], cd], [ccd], bias=cols[:, 8 + j:9 + j], scale=1.0)
                    copy_op(kb, "pool", cb[:, j, :cn], cc[:, j, :cn], [ccd], [cbd])
                    tt(kb, "dve", sq[:, j, :cn], cc[:, j, :cn], cc[:, j, :cn], ALU.mult, [ccd], [sqd])
                b1, b2 = nextbank(g), nextbank(g)
                for j in range(4):
                    mm(kb, g.psum[b1][:, :cn], g.ones_b[:, :], cb[:, j, :cn], j == 0, j == 3, [g.ones_d, cbd], [g.pd[b1]])
                for j in range(4):
                    mm(kb, g.psum[b2][:, :cn], g.ones_b[:, :], sq[:, j, :cn], j == 0, j == 3, [g.ones_d, sqd], [g.pd[b2]])
                act(kb, mean[:, :cn], g.psum[b1][:, :cn], AF.Copy, [g.pd[b1]], [std], scale=1.0 / 512)
                tt(kb, "pool", rstd[:, :cn], mean[:, :cn], mean[:, :cn], ALU.mult, [std], [std])
                stt(kb, "dve", rstd[:, :cn], g.psum[b2][:, :cn], 1.0 / 512, rstd[:, :cn], ALU.mult, ALU.subtract, [g.pd[b2], std], [std])
                act(kb, rstd[:, :cn], rstd[:, :cn], AF.Sqrt, [std], [std], bias=LN_EPS, scale=1.0)
                kb.op("dve", lambda e: e.reciprocal(out=rstd[:, :cn], in_=rstd[:, :cn]), [std], [std])
                for j in range(4):
                    q = j % 2
                    tt(kb, "pool", yt[q][:, :cn], cc[:, j, :cn], mean[:, :cn], ALU.subtract, [ccd, std], [ytd[q]])
                    tt(kb, "dve", yt[q][:, :cn], yt[q][:, :cn], rstd[:, :cn], ALU.mult, [ytd[q], std], [ytd[q]])
                    ts(kb, "dve", yt[q][:, :cn], yt[q][:, :cn], cols[:, 12 + j:13 + j], cols[:, 16 + j:17 + j], ALU.mult, ALU.add, [ytd[q], cd], [ytd[q]])
                    act(kb, yo[q][:, :cn], yt[q][:, :cn], AF.Silu, [ytd[q]], [yod[q]])
                    kb.dma("sp", s_["out"](j, bi, cn), yo[q][:, :cn], reads=[yod[q]])
        kb.barrier()


I32 = mybir.dt.int32
TWO_PI = 2.0 * math.pi


class FCfg:
    def __init__(self, L, rows, N1, nq, CB):
        self.L, self.rows, self.N1, self.nq, self.CB = L, rows, N1, nq, CB
        self.N2 = 86 * nq
        self.N = N1 * self.N2
        self.NF = N1 // 2 + 1
        assert self.N >= 2 * L - 1 and rows * self.N2 >= L


CFG_P = FCfg(16400, 64, 128, 3, 8)
CFG_S = FCfg(2064, 24, 48, 1, 32)


def fft_tables(cfg):
    N1, N2, N, rows, nq, NF = cfg.N1, cfg.N2, cfg.N, cfg.rows, cfg.nq, cfg.NF
    n1 = np.arange(rows)[:, None].astype(np.float64)
    k1 = np.arange(NF)[None, :].astype(np.float64)
    a = 2 * np.pi * n1 * k1 / N1
    F1 = np.concatenate([np.cos(a), -np.sin(a)], 1)
    n2 = np.arange(N2)[:, None].astype(np.float64)
    a = 2 * np.pi * n2 * k1 / N
    tw = np.stack([np.cos(a), -np.sin(a)], 1)
    tw = tw.reshape(nq, 86, 2, NF).transpose(1, 0, 2, 3)
    m = np.arange(N2)[None, :].astype(np.float64)
    a = 2 * np.pi * n2 * m / N2
    F2 = np.stack([np.cos(a), -np.sin(a), np.sin(a)], 0)
    F2 = F2.reshape(3, nq, 86, N2).transpose(2, 0, 1, 3)
    kk = np.arange(NF)[:, None].astype(np.float64)
    a = 2 * np.pi * kk * np.arange(N2)[None, :] / N
    twc = np.stack([np.cos(a), np.sin(a)], 1)
    a = 2 * np.pi * kk * np.arange(rows)[None, :] / N1
    wgt = np.full((NF, 1), 2.0)
    wgt[0, 0] = 1.0
    wgt[NF - 1, 0] = 1.0
    G1 = np.stack([wgt * np.cos(a) / N, -wgt * np.sin(a) / N], 1)
    bf = ml_dtypes.bfloat16
    return dict(F1=F1.astype(np.float32).astype(bf), tw=np.ascontiguousarray(tw).astype(np.float32),
                F2=np.ascontiguousarray(F2).astype(np.float32).astype(bf), twc=twc.astype(np.float32),
                G1=G1.astype(np.float32).astype(bf))


class FTab:
    pass


def fft_load_tables(kb, st, cfg, tabs):
    t = FTab()
    t.d = Dep()
    t.F1 = kb.sb(st, [cfg.rows, 2 * cfg.NF], BF16, "F1")
    t.tw = kb.sb(st, [86, cfg.nq, 2, cfg.NF], F32, "tw")
    t.F2 = kb.sb(st, [86, 3, cfg.nq, cfg.N2], BF16, "F2")
    t.twc = kb.sb(st, [cfg.NF, 2, cfg.N2], F32, "twc")
    t.G1 = kb.sb(st, [cfg.NF, 2, cfg.rows], BF16, "G1")
    for nm in ("F1", "tw", "F2", "twc", "G1"):
        kb.dma("sp", getattr(t, nm)[:], tabs[nm], writes=[t.d])
    return t


class FBuf:
    pass


def fft_alloc(kb, st, cfg, nsets=1):
    CB, nq, N1, N2, rows = cfg.CB, cfg.nq, cfg.NF, cfg.N2, cfg.rows
    E = CB * nq * N1
    E2 = CB * N2
    tn = max(E, E2)
    Ab = kb.sb(st, [86, CB * nq, 2, N1], BF16, "Ab")
    Abd = Dep()
    Xs = kb.sb(st, [86, CB * nq, 2, N1], F32, "Xs")
    Xsd = Dep()
    t = [kb.sb(st, [128, tn], F32, "ft") for _ in range(4)]
    td = [Dep() for _ in range(4)]
    sets = []
    for _ in range(nsets):
        b = FBuf()
        b.src_f = kb.sb(st, [rows, CB, N2], F32, "srcf")
        b.src_fd = Dep()
        b.src_b = kb.sb(st, [rows, CB, N2], BF16, "srcb")
        b.src_bd = Dep()
        b.As = kb.sb(st, [86, CB * nq, 2, N1], F32, "As")
        b.Asd = Dep()
        b.Ab, b.Abd, b.Xs, b.Xsd, b.t, b.td = Ab, Abd, Xs, Xsd, t, td
        sets.append(b)
    return sets if nsets > 1 else sets[0]


def cmul_batched(kb, cfg, b, P, shape, Are, Aim, Br, Bi, out_re, out_im, rdeps, wdep, conj=False):
    n = int(np.prod(shape))
    pat = {2: "p (a b) -> p a b", 3: "p (a b c) -> p a b c"}[len(shape)]
    kw = dict(zip("abc", shape))
    kw.pop("a")
    tv = [b.t[i][:P, :n].rearrange(pat, **kw) for i in range(4)]
    tt(kb, "dve", tv[0], Are, Br, ALU.mult, rdeps, [b.td[0]])
    tt(kb, "pool", tv[1], Aim, Bi, ALU.mult, rdeps, [b.td[1]])
    tt(kb, "pool", tv[2], Are, Bi, ALU.mult, rdeps, [b.td[2]])
    tt(kb, "dve", tv[3], Aim, Br, ALU.mult, rdeps, [b.td[3]])
    tt(kb, "dve", out_re, tv[0], tv[1], ALU.subtract, [b.td[0], b.td[1]], [wdep])
    tt(kb, "pool", out_im, tv[2], tv[3], ALU.add, [b.td[2], b.td[3]], [wdep])


def fft_s1(kb, g, cfg, tb, b, cb):
    nq, N1, N2, rows = cfg.nq, cfg.NF, cfg.N2, cfg.rows
    per = 512 // (2 * N1)
    tot = cb * nq
    for i0 in range(0, tot, per):
        cnt = min(per, tot - i0)
        bk = nextbank(g)
        for i in range(i0, i0 + cnt):
            c, q = divmod(i, nq)
            mm(kb, g.psum[bk][:86, (i - i0) * 2 * N1:(i - i0 + 1) * 2 * N1], b.src_b[:rows, c, q * 86:(q + 1) * 86], tb.F1[:rows, :], True, True,
               [b.src_bd, tb.d], [g.pd[bk]])
        copy_op(kb, "act", b.As[:, i0:i0 + cnt, :, :], g.psum[bk][:86, :cnt * 2 * N1].rearrange("p (i r k) -> p i r k", r=2, k=N1), [g.pd[bk]], [b.Asd])


def fft_s2(kb, g, cfg, tb, b, cb):
    nq, N1, N2, rows = cfg.nq, cfg.NF, cfg.N2, cfg.rows
    per = 512 // (2 * N1)
    tot = cb * nq
    Av = b.As[:, :tot, :, :].rearrange("p (c q) r k -> p c q r k", q=nq)
    Abv = b.Ab[:, :tot, :, :].rearrange("p (c q) r k -> p c q r k", q=nq)
    twr = tb.tw[:, :, 0, :].unsqueeze(1).broadcast_to([86, cb, nq, N1])
    twi = tb.tw[:, :, 1, :].unsqueeze(1).broadcast_to([86, cb, nq, N1])
    cmul_batched(kb, cfg, b, 86, (cb, nq, N1), Av[:, :, :, 0, :], Av[:, :, :, 1, :], twr, twi, Abv[:, :, :, 0, :], Abv[:, :, :, 1, :],
                 [b.Asd, tb.d], b.Abd)
    for i0 in range(0, tot, per):
        cnt = min(per, tot - i0)
        bk = nextbank(g)
        for i in range(i0, i0 + cnt):
            c, p = divmod(i, nq)
            reg = g.psum[bk][:86, (i - i0) * 2 * N1:(i - i0 + 1) * 2 * N1]
            for q in range(nq):
                blk = slice(p * 86, (p + 1) * 86)
                mm(kb, reg, tb.F2[:, 0, q, blk], b.Ab[:, c * nq + q, :, :].rearrange("p r k -> p (r k)"), q == 0, False, [tb.d, b.Abd], [g.pd[bk]])
                mm(kb, reg[:, 0:N1], tb.F2[:, 2, q, blk], b.Ab[:, c * nq + q, 1, :], False, False, [tb.d, b.Abd], [g.pd[bk]])
                mm(kb, reg[:, N1:2 * N1], tb.F2[:, 1, q, blk], b.Ab[:, c * nq + q, 0, :], False, q == nq - 1, [tb.d, b.Abd], [g.pd[bk]])
        copy_op(kb, "act", b.Xs[:, i0:i0 + cnt, :, :], g.psum[bk][:86, :cnt * 2 * N1].rearrange("p (i r k) -> p i r k", r=2, k=N1), [g.pd[bk]], [b.Xsd])


def fft_fwd(kb, g, cfg, tb, b, cb):
    fft_s1(kb, g, cfg, tb, b, cb)
    fft_s2(kb, g, cfg, tb, b, cb)


def pipeline2(items, stage_a, stage_b, depth=2):
    if depth < 2:
        for it in items:
            stage_a(it)
            stage_b(it)
        return
    prev = None
    for it in items:
        stage_a(it)
        if prev is not None:
            stage_b(prev)
        prev = it
    if prev is not None:
        stage_b(prev)


def fft_layout_dma(kb, q, cfg, tile, tiled, dram2d, c0, cb, to_sbuf):
    L, N2, rows = cfg.L, cfg.N2, cfg.rows
    full = L // N2
    rem = L - full * N2
    dv = dram2d[c0:c0 + cb, 0:full * N2].rearrange("c (a b) -> a c b", b=N2)
    if to_sbuf:
        kb.dma(q, tile[:full, :cb, :], dv, writes=[tiled])
        if rem:
            kb.dma(q, tile[full:full + 1, :cb, :rem], dram2d[c0:c0 + cb, full * N2:L].unsqueeze(0), writes=[tiled])
    else:
        kb.dma(q, dv, tile[:full, :cb, :], reads=[tiled])
        if rem:
            kb.dma(q, dram2d[c0:c0 + cb, full * N2:L].unsqueeze(0), tile[full:full + 1, :cb, :rem], reads=[tiled])


def phase_hy_conv(kb, g, cfg, tabs, taps, Hs, seqs, dskip):
    CB, nq, N1, N2, rows, L = cfg.CB, cfg.nq, cfg.NF, cfg.N2, cfg.rows, cfg.L
    with ExitStack() as st:
        tb = fft_load_tables(kb, st, cfg, tabs)
        bs = [fft_alloc(kb, st, cfg, nsets=1)]
        for b in bs:
            kb.op("pool", lambda e, b=b: e.memset(b.src_f[:, :, :], 0.0), [], [b.src_fd])
        X0 = kb.sb(st, [86, CB * nq, 2, N1], F32, "X0")
        X0d = Dep()
        Hb = [kb.sb(st, [86, CB * nq, 2, N1], F32, "Hb") for _ in range(len(bs))]
        Hbd = [Dep() for _ in range(len(bs))]
        items = [(c0, d, bs[i % len(bs)]) for i, (c0, d) in enumerate((c0, d) for c0 in range(0, 64, CB) for d in range(2))]

        def sp_a(it):
            c0, d, b = it
            fft_layout_dma(kb, "sp", cfg, b.src_f, b.src_fd, taps[d], c0, CB, True)
            copy_op(kb, "dve", b.src_b[:, :, :], b.src_f[:, :, :], [b.src_fd], [b.src_bd])
            fft_s1(kb, g, cfg, tb, b, CB)

        def sp_b(it):
            c0, d, b = it
            fft_s2(kb, g, cfg, tb, b, CB)
            if d == 0:
                copy_op(kb, "pool", X0[:, :, :, :], b.Xs[:, :, :, :], [b.Xsd], [X0d])
            else:
                tt(kb, "dve", Hb[0][:, :, 0, :], X0[:, :, 0, :], b.Xs[:, :, 0, :], ALU.add, [X0d, b.Xsd], [Hbd[0]])
                tt(kb, "pool", Hb[0][:, :, 1, :], X0[:, :, 1, :], b.Xs[:, :, 1, :], ALU.subtract, [X0d, b.Xsd], [Hbd[0]])
                kb.dma("sp", Hs[:, c0 * nq:(c0 + CB) * nq, :, :], Hb[0][:, :, :, :], reads=[Hbd[0]])
        pipeline2(items, sp_a, sp_b, depth=len(bs))
        kb.barrier()
        Yb = kb.sb(st, [86, CB * nq, 2, N1], BF16, "Yb")
        Ybd = Dep()
        Bs = kb.sb(st, [N1, CB, 2, N2], F32, "Bs")
        Bsd = Dep()
        Bb = kb.sb(st, [N1, CB, 2, N2], BF16, "Bb")
        Bbd = Dep()
        x0f = [kb.sb(st, [rows, CB, N2], F32, "x0f") for _ in range(len(bs))]
        x0d = [Dep() for _ in range(len(bs))]
        cv = kb.sb(st, [rows, CB, N2], F32, "cv")
        cvd = Dep()
        yo = kb.sb(st, [rows, CB, N2], BF16, "yo")
        yod = Dep()
        dsk = kb.sb(st, [128, 64], F32, "dsk")
        dskd = Dep()
        kb.dma("sp", dsk[:, :], dskip.broadcast_to([128, 64]), writes=[dskd])
        perb = 512 // N2
        citems = [(sq, c0, i % len(bs)) for i, (sq, c0) in enumerate((sq, c0) for sq in seqs for c0 in range(0, 64, CB))]

        def cv_a(it):
            sq, c0, k = it
            b = bs[k]
            fft_layout_dma(kb, "sp", cfg, b.src_f, b.src_fd, sq["z"], c0, CB, True)
            fft_layout_dma(kb, "pool", cfg, x0f[k], x0d[k], sq["x0"], c0, CB, True)
            kb.dma("sp", Hb[k][:, :, :, :], Hs[:, c0 * nq:(c0 + CB) * nq, :, :], writes=[Hbd[k]])
            copy_op(kb, "dve", b.src_b[:, :, :], b.src_f[:, :, :], [b.src_fd], [b.src_bd])
            fft_s1(kb, g, cfg, tb, b, CB)

        def cv_b(it):
            sq, c0, k = it
            b = bs[k]
            fft_s2(kb, g, cfg, tb, b, CB)
            cmul_batched(kb, cfg, b, 86, (CB * nq, N1), b.Xs[:, :, 0, :], b.Xs[:, :, 1, :], Hb[k][:, :, 0, :], Hb[k][:, :, 1, :],
                         Yb[:, :, 0, :], Yb[:, :, 1, :], [b.Xsd, Hbd[k]], Ybd)
            tot = CB * 2
            for i0 in range(0, tot, perb):
                cnt = min(perb, tot - i0)
                bk = nextbank(g)
                for i in range(i0, i0 + cnt):
                    c, ri = divmod(i, 2)
                    reg = g.psum[bk][:N1, (i - i0) * N2:(i - i0 + 1) * N2]
                    for p in range(nq):
                        ya_re, ya_im = Yb[:, c * nq + p, 0, :], Yb[:, c * nq + p, 1, :]
                        if ri == 0:
                            mm(kb, reg, ya_re, tb.F2[:, 0, p, :], p == 0, False, [Ybd, tb.d], [g.pd[bk]])
                            mm(kb, reg, ya_im, tb.F2[:, 1, p, :], False, p == nq - 1, [Ybd, tb.d], [g.pd[bk]])
                        else:
                            mm(kb, reg, ya_re, tb.F2[:, 2, p, :], p == 0, False, [Ybd, tb.d], [g.pd[bk]])
                            mm(kb, reg, ya_im, tb.F2[:, 0, p, :], False, p == nq - 1, [Ybd, tb.d], [g.pd[bk]])
                copy_op(kb, "act", Bs[:, :, :, :].rearrange("p c r n -> p (c r) n")[:, i0:i0 + cnt, :],
                        g.psum[bk][:N1, :cnt * N2].rearrange("p (i n) -> p i n", n=N2), [g.pd[bk]], [Bsd])
            twr = tb.twc[:, 0, :].unsqueeze(1).broadcast_to([N1, CB, N2])
            twi = tb.twc[:, 1, :].unsqueeze(1).broadcast_to([N1, CB, N2])
            cmul_batched(kb, cfg, b, N1, (CB, N2), Bs[:, :, 0, :], Bs[:, :, 1, :], twr, twi, Bb[:, :, 0, :], Bb[:, :, 1, :], [Bsd, tb.d], Bbd)
            for i0 in range(0, CB, perb):
                cnt = min(perb, CB - i0)
                bk = nextbank(g)
                for c in range(i0, i0 + cnt):
                    reg = g.psum[bk][:rows, (c - i0) * N2:(c - i0 + 1) * N2]
                    mm(kb, reg, tb.G1[:, 0, :], Bb[:, c, 0, :], True, False, [tb.d, Bbd], [g.pd[bk]])
                    mm(kb, reg, tb.G1[:, 1, :], Bb[:, c, 1, :], False, True, [tb.d, Bbd], [g.pd[bk]])
                copy_op(kb, "act", cv[:, i0:i0 + cnt, :], g.psum[bk][:rows, :cnt * N2].rearrange("p (i n) -> p i n", n=N2), [g.pd[bk]], [cvd])
            tt(kb, "pool", b.src_f[:, :, :], b.src_f[:, :, :], dsk[:rows, c0:c0 + CB].unsqueeze(2).broadcast_to([rows, CB, N2]), ALU.mult,
               [b.src_fd, dskd], [b.src_fd])
            tt(kb, "dve", cv[:, :, :], cv[:, :, :], b.src_f[:, :, :], ALU.add, [cvd, b.src_fd], [cvd])
            tt(kb, "pool", yo[:, :, :], cv[:, :, :], x0f[k][:, :, :], ALU.mult, [cvd, x0d[k]], [yod])
            fft_layout_dma(kb, "sp", cfg, yo, yod, sq["ya"], c0, CB, False)
        pipeline2(citems, cv_a, cv_b, depth=len(bs))
        kb.barrier()


def phase_hy_inproj(kb, g, seqs, w_hy, brow, hcols, G=1):
    with ExitStack() as st:
        wb = kb.sb(st, [128, 8, G * 192], BF16, "why")
        Wd = Dep()
        load_weight_bf16(kb, st, wb, Wd, w_hy, 8, G * 192, stage_cols=1536)
        hc = kb.sb(st, [64, G * 12], F32, "hc")
        hcd = Dep()
        kb.dma("sp", hc[:, :], hcols, writes=[hcd])
        brf = kb.sb(st, [1, G * 192], F32, "brf")
        brb = kb.sb(st, [1, G * 192], BF16, "brb")
        brd = Dep()
        kb.dma("sp", brf[:, :], brow, writes=[brd])
        copy_op(kb, "dve", brb[:, :], brf[:, :], [brd], [brd])
        xin = [kb.sb(st, [128, D], F32, "xin") for _ in range(4)]
        xind = [Dep() for _ in range(4)]
        xT = [kb.sb(st, [128, 8, 512], BF16, "xT") for _ in range(2)]
        xTd = [Dep() for _ in range(2)]
        vf = [kb.sb(st, [1, 512], F32, "vf") for _ in range(2)]
        vb = [kb.sb(st, [1, 512], BF16, "vb") for _ in range(2)]
        vd = [Dep() for _ in range(2)]
        o3 = [[kb.sb(st, [64, 512], F32, "o3") for _ in range(3)] for _ in range(2)]
        o3d = [[Dep() for _ in range(3)] for _ in range(2)]
        bi = 0
        oi = 0
        for sq in seqs:
            L = sq["L"]
            for t0 in range(0, L, 510):
                no = min(510, L - t0)
                ni = no + 2
                j = bi % 2
                bi += 1
                for ti, r0 in enumerate(range(0, ni, 128)):
                    n = min(128, ni - r0)
                    kb.dma("sp" if ti % 2 == 0 else "pool", xin[ti][:n, :], sq["xh"][t0 + r0:t0 + r0 + n, :], writes=[xind[ti]])
                    transpose_tile(kb, g, xin[ti], xind[ti], n, xT[j], xTd[j], r0)
                kb.dma("pool", vf[j][:, :ni], sq["valid"][:, t0:t0 + ni], writes=[vd[j]])
                copy_op(kb, "dve", vb[j][:, :ni], vf[j][:, :ni], [vd[j]], [vd[j]])
                for gg in range(G):
                    oj = oi % 2
                    oi += 1
                    for gi in range(3):
                        c0 = gg * 192 + gi * 64
                        h0 = gg * 12 + gi * 4
                        bk = nextbank(g)
                        for k in range(8):
                            mm(kb, g.psum[bk][:64, :ni], wb[:, k, c0:c0 + 64], xT[j][:, k, :ni], k == 0, False, [Wd, xTd[j]], [g.pd[bk]])
                        mm(kb, g.psum[bk][:64, :ni], brb[:, c0:c0 + 64], vb[j][:, :ni], False, True, [brd, vd[j]], [g.pd[bk]])
                        o = o3[oj][gi]
                        od = o3d[oj][gi]
                        act(kb, o[:, :no], g.psum[bk][:64, 1:1 + no], AF.Identity, [g.pd[bk], hcd], [od],
                            scale=hc[:, h0 + 1:h0 + 2], bias=hc[:, h0 + 3:h0 + 4])
                        stt(kb, "dve", o[:, :no], g.psum[bk][:64, 0:no], hc[:, h0:h0 + 1], o[:, :no], ALU.mult, ALU.add, [g.pd[bk], hcd, od], [od])
                        stt(kb, "dve", o[:, :no], g.psum[bk][:64, 2:2 + no], hc[:, h0 + 2:h0 + 3], o[:, :no], ALU.mult, ALU.add, [g.pd[bk], hcd, od], [od])
                    tt(kb, "pool", o3[oj][1][:, :no], o3[oj][1][:, :no], o3[oj][2][:, :no], ALU.mult, [o3d[oj][1], o3d[oj][2]], [o3d[oj][1]])
                    kb.dma("sp", sq["x0"][gg][:, t0:t0 + no], o3[oj][0][:, :no], reads=[o3d[oj][0]])
                    kb.dma("pool", sq["z"][gg][:, t0:t0 + no], o3[oj][1][:, :no], reads=[o3d[oj][1]])
        kb.barrier()


def sin_reduced(kb, out, outd, src_ps, fcol, fbcol, tmps, tmpd, ki, kid, n, reads):
    a, r = tmps
    ts(kb, "dve", a[:, :n], src_ps, fcol, fbcol, ALU.mult, ALU.add, reads, [tmpd[0]])
    ts(kb, "pool", r[:, :n], a[:, :n], 1.0 / TWO_PI, None, ALU.mult, None, [tmpd[0]], [tmpd[1]])
    copy_op(kb, "dve", ki[:, :n], r[:, :n], [tmpd[1]], [kid])
    copy_op(kb, "pool", r[:, :n], ki[:, :n], [kid], [tmpd[1]])
    stt(kb, "dve", r[:, :n], r[:, :n], -TWO_PI, a[:, :n], ALU.mult, ALU.add, [tmpd[0], tmpd[1]], [tmpd[1]])
    ts(kb, "pool", r[:, :n], r[:, :n], -3.1415925, 3.1415925, ALU.max, ALU.min, [tmpd[1]], [tmpd[1]])
    return act(kb, out, r[:, :n], AF.Sin, [tmpd[1]], [outd])


def phase_hy_filters(kb, g, L, zposT, fw, taps_out):
    with ExitStack() as st:
        w1 = kb.sb(st, [33, 2, 64], F32, "fw1")
        w2 = kb.sb(st, [64, 2, 64], F32, "fw2")
        w3 = kb.sb(st, [64, 2, 64], F32, "fw3")
        fc = kb.sb(st, [64, 2, 8], F32, "fc")
        Wd = Dep()
        kb.dma("sp", w1[:, :, :], fw["w1"].rearrange("d e f -> e d f"), writes=[Wd])
        kb.dma("sp", w2[:, :, :], fw["w2"].rearrange("d e f -> e d f"), writes=[Wd])
        kb.dma("sp", w3[:, :, :], fw["w3"].rearrange("d e f -> e d f"), writes=[Wd])
        kb.dma("sp", fc[:, :, 0:5], fw["fcols"], writes=[Wd])
        tt(kb, "dve", fc[:, :, 5:6], fc[:, :, 0:1], fc[:, :, 1:2], ALU.mult, [Wd], [Wd])
        tt(kb, "dve", fc[:, :, 6:7], fc[:, :, 2:3], fc[:, :, 3:4], ALU.mult, [Wd], [Wd])
        ts(kb, "dve", fc[:, :, 7:8], fc[:, :, 4:5], -1.0, None, ALU.mult, None, [Wd], [Wd])
        taps = kb.sb(st, [64, 2, L], F32, "taps")
        tapsd = Dep()
        zp = [kb.sb(st, [33, 512], F32, "zp") for _ in range(2)]
        zpd = [Dep() for _ in range(2)]
        tb_ = [kb.sb(st, [64, 512], F32, "tbc") for _ in range(2)]
        tbd = [Dep() for _ in range(2)]
        tmps = [kb.sb(st, [64, 512], F32, "ftmp") for _ in range(2)]
        tmpd = [Dep(), Dep()]
        ki = kb.sb(st, [64, 512], I32, "ki")
        kid = Dep()
        h1 = kb.sb(st, [64, 512], F32, "h1")
        h1d = Dep()
        h2 = kb.sb(st, [64, 512], F32, "h2")
        h2d = Dep()
        ex = kb.sb(st, [64, 512], F32, "ex")
        exd = Dep()
        ss = kb.sb(st, [64, 2 * ((L + 511) // 512) + 4], F32, "ss")
        ssd = Dep()
        junk = kb.sb(st, [64, 512], F32, "fjunk")
        junkd = Dep()
        nb = (L + 511) // 512
        for bi, l0 in enumerate(range(0, L, 512)):
            n = min(512, L - l0)
            j = bi % 2
            kb.dma("sp", zp[j][:, :n], zposT[:, l0:l0 + n], writes=[zpd[j]])
            kb.dma("pool", tb_[j][:, :n], zposT[0:1, l0:l0 + n].broadcast_to([64, n]), writes=[tbd[j]])
            for d in range(2):
                bk = nextbank(g)
                mm(kb, g.psum[bk][:64, :n], w1[:, d, :], zp[j][:, :n], True, True, [Wd, zpd[j]], [g.pd[bk]])
                sin_reduced(kb, h1[:, :n], h1d, g.psum[bk][:64, :n], fc[:, d, 0:1], fc[:, d, 5:6], tmps, tmpd, ki, kid, n, [g.pd[bk], Wd])
                bk = nextbank(g)
                mm(kb, g.psum[bk][:64, :n], w2[:, d, :], h1[:, :n], True, True, [Wd, h1d], [g.pd[bk]])
                sin_reduced(kb, h2[:, :n], h2d, g.psum[bk][:64, :n], fc[:, d, 2:3], fc[:, d, 6:7], tmps, tmpd, ki, kid, n, [g.pd[bk], Wd])
                bk = nextbank(g)
                mm(kb, g.psum[bk][:64, :n], w3[:, d, :], h2[:, :n], True, True, [Wd, h2d], [g.pd[bk]])
                act(kb, ex[:, :n], tb_[j][:, :n], AF.Exp, [tbd[j], Wd], [exd], scale=fc[:, d, 7:8])
                tt(kb, "dve", taps[:, d, l0:l0 + n], g.psum[bk][:64, :n], ex[:, :n], ALU.mult, [g.pd[bk], exd], [tapsd])
                if d == 1 and l0 == 0:
                    kb.op("pool", lambda e: e.memset(taps[:, 1, 0:1], 0.0), [], [tapsd])
                act(kb, junk[:, :n], taps[:, d, l0:l0 + n], AF.Square, [tapsd], [junkd, ssd], accum_out=ss[:, 2 * bi + d:2 * bi + d + 1])
        tot, nrm = ss[:, 2 * nb:2 * nb + 1], ss[:, 2 * nb + 1:2 * nb + 2]
        kb.op("dve", lambda e: e.tensor_reduce(out=tot, in_=ss[:, 0:2 * nb], axis=AX.X, op=ALU.add), [ssd], [ssd])
        act(kb, nrm, tot, AF.Sqrt, [ssd], [ssd])
        kb.op("dve", lambda e: e.reciprocal(out=nrm, in_=nrm), [ssd], [ssd])
        for d in range(2):
            for l0 in range(0, L, 4096):
                n = min(4096, L - l0)
                ts(kb, ("dve", "pool")[d], taps[:, d, l0:l0 + n], taps[:, d, l0:l0 + n], nrm, None, ALU.mult, None, [tapsd, ssd], [tapsd])
            kb.dma("sp", taps_out[d], taps[:, d, :], reads=[tapsd])
        kb.barrier()


def phase_hy_filter_h2(kb, g, L, zposT, fw, h2_out):
    with ExitStack() as st:
        w1 = kb.sb(st, [33, 2, 64], F32, "fw1")
        w2 = kb.sb(st, [64, 2, 64], F32, "fw2")
        fc = kb.sb(st, [64, 2, 8], F32, "fc")
        Wd = Dep()
        kb.dma("sp", w1[:, :, :], fw["w1"].rearrange("d e f -> e d f"), writes=[Wd])
        kb.dma("sp", w2[:, :, :], fw["w2"].rearrange("d e f -> e d f"), writes=[Wd])
        kb.dma("sp", fc[:, :, 0:5], fw["fcols"], writes=[Wd])
        tt(kb, "dve", fc[:, :, 5:6], fc[:, :, 0:1], fc[:, :, 1:2], ALU.mult, [Wd], [Wd])
        tt(kb, "dve", fc[:, :, 6:7], fc[:, :, 2:3], fc[:, :, 3:4], ALU.mult, [Wd], [Wd])
        zp = [kb.sb(st, [33, 512], F32, "zp") for _ in range(2)]
        zpd = [Dep() for _ in range(2)]
        NQ = 3
        tmps = [[kb.sb(st, [64, 512], F32, "ftmp") for _ in range(2)] for _ in range(NQ)]
        tmpd = [[Dep(), Dep()] for _ in range(NQ)]
        ki = [kb.sb(st, [64, 512], I32, "ki") for _ in range(NQ)]
        kid = [Dep() for _ in range(NQ)]
        h1 = [kb.sb(st, [64, 512], F32, "h1") for _ in range(NQ)]
        h1d = [Dep() for _ in range(NQ)]
        h2 = [kb.sb(st, [64, 512], F32, "h2") for _ in range(NQ)]
        h2d = [Dep() for _ in range(NQ)]
        it = 0
        for bi, l0 in enumerate(range(0, L, 512)):
            n = min(512, L - l0)
            j = bi % 2
            kb.dma("sp", zp[j][:, :n], zposT[:, l0:l0 + n], writes=[zpd[j]])
            for d in range(2):
                q = it % NQ
                it += 1
                bk = nextbank(g)
                mm(kb, g.psum[bk][:64, :n], w1[:, d, :], zp[j][:, :n], True, True, [Wd, zpd[j]], [g.pd[bk]])
                sin_reduced(kb, h1[q][:, :n], h1d[q], g.psum[bk][:64, :n], fc[:, d, 0:1], fc[:, d, 5:6], tmps[q], tmpd[q], ki[q], kid[q], n, [g.pd[bk], Wd])
                bk = nextbank(g)
                mm(kb, g.psum[bk][:64, :n], w2[:, d, :], h1[q][:, :n], True, True, [Wd, h1d[q]], [g.pd[bk]])
                sin_reduced(kb, h2[q][:, :n], h2d[q], g.psum[bk][:64, :n], fc[:, d, 2:3], fc[:, d, 6:7], tmps[q], tmpd[q], ki[q], kid[q], n, [g.pd[bk], Wd])
                kb.dma("pool", h2_out[d][:, l0:l0 + n], h2[q][:, :n], reads=[h2d[q]])
        kb.barrier()


def phase_hy_filter_taps(kb, g, L, zposT, h2_in, w3_ap, fcols_ap, taps_out):
    with ExitStack() as st:
        w3 = kb.sb(st, [64, 2, 64], F32, "fw3")
        fc = kb.sb(st, [64, 2, 8], F32, "fc")
        Wd = Dep()
        kb.dma("sp", w3[:, :, :], w3_ap.rearrange("d e f -> e d f"), writes=[Wd])
        kb.dma("sp", fc[:, :, 0:5], fcols_ap, writes=[Wd])
        ts(kb, "dve", fc[:, :, 7:8], fc[:, :, 4:5], -1.0, None, ALU.mult, None, [Wd], [Wd])
        taps = kb.sb(st, [64, 2, L], F32, "taps")
        tapsd = [Dep(), Dep()]
        NQ = 3
        tb_ = [kb.sb(st, [64, 512], F32, "tbc") for _ in range(2)]
        tbd = [Dep() for _ in range(2)]
        hin = [kb.sb(st, [64, 512], F32, "h2in") for _ in range(NQ)]
        hind = [Dep() for _ in range(NQ)]
        ex = [kb.sb(st, [64, 512], F32, "ex") for _ in range(NQ)]
        exd = [Dep() for _ in range(NQ)]
        junk = [kb.sb(st, [64, 512], F32, "fjunk") for _ in range(2)]
        junkd = [Dep(), Dep()]
        nb = (L + 511) // 512
        ss = kb.sb(st, [64, 2 * nb + 4], F32, "ss")
        ssd = Dep()
        it = 0
        for bi, l0 in enumerate(range(0, L, 512)):
            n = min(512, L - l0)
            j = bi % 2
            kb.dma("pool", tb_[j][:, :n], zposT[0:1, l0:l0 + n].broadcast_to([64, n]), writes=[tbd[j]])
            for d in range(2):
                q = it % NQ
                it += 1
                kb.dma("sp", hin[q][:, :n], h2_in[d][:, l0:l0 + n], writes=[hind[q]])
                bk = nextbank(g)
                mm(kb, g.psum[bk][:64, :n], w3[:, d, :], hin[q][:, :n], True, True, [Wd, hind[q]], [g.pd[bk]])
                act(kb, ex[q][:, :n], tb_[j][:, :n], AF.Exp, [tbd[j], Wd], [exd[q]], scale=fc[:, d, 7:8])
                tt(kb, "dve", taps[:, d, l0:l0 + n], g.psum[bk][:64, :n], ex[q][:, :n], ALU.mult, [g.pd[bk], exd[q]], [tapsd[d]])
                if d == 1 and l0 == 0:
                    kb.op("pool", lambda e: e.memset(taps[:, 1, 0:1], 0.0), [], [tapsd[d]])
                act(kb, junk[d][:, :n], taps[:, d, l0:l0 + n], AF.Square, [tapsd[d]], [junkd[d], ssd], accum_out=ss[:, 2 * bi + d:2 * bi + d + 1])
        tot, nrm = ss[:, 2 * nb:2 * nb + 1], ss[:, 2 * nb + 1:2 * nb + 2]
        kb.op("dve", lambda e: e.tensor_reduce(out=tot, in_=ss[:, 0:2 * nb], axis=AX.X, op=ALU.add), [ssd], [ssd])
        act(kb, nrm, tot, AF.Sqrt, [ssd], [ssd])
        kb.op("dve", lambda e: e.reciprocal(out=nrm, in_=nrm), [ssd], [ssd])
        for d in range(2):
            for l0 in range(0, L, 4096):
                n = min(4096, L - l0)
                ts(kb, ("dve", "pool")[d], taps[:, d, l0:l0 + n], taps[:, d, l0:l0 + n], nrm, None, ALU.mult, None, [tapsd[d], ssd], [tapsd[d]])
            kb.dma(("sp", "pool")[d], taps_out[d], taps[:, d, :], reads=[tapsd[d]])
        kb.barrier()


LP, LS = 16400, 2064
NCORES = 8
BF = ml_dtypes.bfloat16


class Prog:
    def __init__(self):
        self.nc = bass.Bass("TRN2", target_bir_lowering=False)
        self.kb = KB(self.nc)
        self.ins = {}

    def din(self, name, shape, dt=F32):
        self.ins[name] = (tuple(shape), dt)
        return self.nc.dram_tensor(name, list(shape), dt, kind="ExternalInput").ap()

    def dout(self, name, shape, dt=F32):
        return self.nc.dram_tensor(name, list(shape), dt, kind="ExternalOutput").ap()

    def scr(self, name, shape, dt=F32):
        return self.nc.dram_tensor(name, list(shape), dt).ap()


def chunk_tiles():
    return [(t0, 128, 61 + t0) for t0 in range(0, 2048, 128)] + [(2048, 16, 15)]


def declare_tabs(P, cfg, pre):
    t = fft_tables(cfg)
    return {k: P.din(pre + k, v.shape, F32 if v.dtype == np.float32 else BF16) for k, v in t.items()}, {pre + k: v for k, v in t.items()}


def build_l1():
    P = Prog()
    kb = P.kb
    ident = P.din("ident", [128, 128])
    xh_p = P.din("xh_p", [LP + 2, D])
    xh_s = P.din("xh_s", [LS + 2, D])
    valid_p = P.din("valid_p", [1, LP + 2])
    valid_s = P.din("valid_s", [1, LS + 2])
    zpos_p = P.din("zpos_p", [33, LP])
    zpos_s = P.din("zpos_s", [33, LS])
    tabsP, _ = declare_tabs(P, CFG_P, "tp_")
    tabsS, _ = declare_tabs(P, CFG_S, "ts_")
    fw1 = P.din("fw1", [2, 33, 64])
    fw2 = P.din("fw2", [2, 64, 64])
    fw3 = P.din("fw3", [9, 2, 64, 64])
    fcols = P.din("fcols", [9, 64, 2, 5])
    why = P.din("why", [9, D, 192])
    brow = P.din("brow", [9, 1, 192])
    hcols = P.din("hcols", [9, 64, 12])
    dskip = P.din("dskip", [9, 1, 64])
    xc = P.din("xc", [2, XC, D])
    mask = P.din("mask", [2, 1, XC])
    wconf = P.din("wconf", [D, 1024])
    ccols = P.din("ccols", [128, 144])
    yaP = P.dout("yaP", [64, LP], BF16)
    yaS = P.dout("yaS", [8, 64, LS], BF16)
    ybT = P.dout("ybT", [2, 512, LS], BF16)
    taps_p = P.scr("taps_p", [2, 64, LP])
    Hs_p = P.scr("Hs_p", [86, 64 * CFG_P.nq, 2, CFG_P.N1])
    z_p = P.scr("z_p", [64, LP])
    x0_p = P.scr("x0_p", [64, LP])
    taps_s = P.scr("taps_s", [8, 2, 64, LS])
    Hs_s = P.scr("Hs_s", [8, 86, 64 * CFG_S.nq, 2, CFG_S.N1])
    z_s = P.scr("z_s", [8, 64, LS])
    x0_s = P.scr("x0_s", [8, 64, LS])
    with ExitStack() as st:
        g = setup_globals(kb, st)
        load_ident(kb, g, ident)
        fwd = lambda i: dict(w1=fw1, w2=fw2, w3=fw3[i], fcols=fcols[i])
        phase_hy_filters(kb, g, LP, zpos_p, fwd(0), taps_p)
        phase_hy_inproj(kb, g, [dict(xh=xh_p, valid=valid_p, L=LP, z=[z_p], x0=[x0_p])], why[0], brow[0], hcols[0])
        phase_hy_conv(kb, g, CFG_P, tabsP, taps_p, Hs_p, [dict(z=z_p, x0=x0_p, ya=yaP)], dskip[0])
        for gi in range(8):
            phase_hy_filters(kb, g, LS, zpos_s, fwd(1 + gi), taps_s[gi])
            phase_hy_inproj(kb, g, [dict(xh=xh_s, valid=valid_s, L=LS, z=[z_s[gi]], x0=[x0_s[gi]])], why[1 + gi], brow[1 + gi], hcols[1 + gi])
            phase_hy_conv(kb, g, CFG_S, tabsS, taps_s[gi], Hs_s[gi], [dict(z=z_s[gi], x0=x0_s[gi], ya=yaS[gi])], dskip[1 + gi])

        def outf(s_):
            def f(j, bi, cn):
                if bi == 0:
                    return ybT[s_, j * 128:(j + 1) * 128, 2048:2064]
                return ybT[s_, j * 128:(j + 1) * 128, (bi - 1) * 512:bi * 512]
            return f
        phase_conf(kb, g, [dict(x=xc[s_], mask=mask[s_], out=outf(s_)) for s_ in range(2)], wconf, ccols)
        kb.finish_wait()
    return P


def build_l2():
    P = Prog()
    kb = P.kb
    ident = P.din("ident", [128, 128])
    xc = P.din("xc", [2, XC, D])
    ycT = P.din("ycT", [2, D, LS], BF16)
    wout = P.din("wout", [D, D])
    bout = P.din("bout", [1, D])
    ln1g = P.din("ln1g", [1, D]); ln1b = P.din("ln1b", [1, D]); ln2g = P.din("ln2g", [1, D]); ln2b = P.din("ln2b", [1, D])
    w1 = P.din("w1", [D, DFF]); w2 = P.din("w2", [DFF, D])
    wqa = P.din("wqa", [D, 384]); qg = P.din("qg", [1, 384]); WqH = P.din("WqH", [384, NH * 128]); WqS = P.din("WqS", [384, NH * 32])
    wkva = P.din("wkva", [D, 288]); kvg = P.din("kvg", [1, 256])
    cs = P.din("cs", [2, LS, 32]); Cq = P.din("Cq", [2, 32, 2048]); Sq = P.din("Sq", [2, 32, 2048])
    h2 = P.dout("h2", [2, LS, D])
    kvlat = P.dout("kvlat", [2, LS, 288])
    QT = P.dout("QT", [2, NH, 128, 2048], BF16)
    h1 = P.scr("h1", [2, LS, D])
    tl = chunk_tiles()
    with ExitStack() as st:
        g = setup_globals(kb, st)
        load_ident(kb, g, ident)
        ycv = ycT.rearrange("s (k p) t -> s p k t", p=128)
        phase_proj_ln(kb, g, [(xc[s_, xr:xr + n, :], [(slice(0, 8), ycv[s_, :, :, t0:t0 + n], None)], h1[s_, t0:t0 + n, :], n) for s_ in range(2) for t0, n, xr in tl],
                      True, wout, bout, ln1g, ln1b)
        phase_mlp_ln(kb, g, [(h1[s_, t0:t0 + n, :], h2[s_, t0:t0 + n, :], n) for s_ in range(2) for t0, n, xr in tl], w1, w2, ln2g, ln2b)
        seqs = []
        for s_ in range(2):
            seqs.append(dict(tiles=[(h2[s_, t0:t0 + n, :], kvlat[s_, t0:t0 + n, :], cs[s_, t0:t0 + n, :], n) for t0, n, xr in tl],
                             CS=(Cq[s_], Sq[s_]), qt=(lambda s_: (lambda h, q0: QT[s_, h, :, q0:q0 + 512]))(s_)))
        phase_qkv(kb, g, seqs, wqa, qg, WqH, WqS, wkva, kvg)
        kb.finish_wait()
    return P


def build_l3():
    P = Prog()
    kb = P.kb
    ident = P.din("ident", [128, 128])
    h2 = P.din("h2", [2, LS, D])
    kvp = P.din("kvp", [LP, 288])
    kvs = P.din("kvs", [LS, 288])
    QT = P.din("QT", [2, NH, 128, 2048], BF16)
    WkH = P.din("WkH", [256, NH * 128]); WvH = P.din("WvH", [256, NH * 64])
    wo = P.din("wo", [D, D])
    ln1g = P.din("ln1g", [1, D]); ln1b = P.din("ln1b", [1, D]); ln2g = P.din("ln2g", [1, D]); ln2b = P.din("ln2b", [1, D])
    w1 = P.din("w1", [D, DFF]); w2 = P.din("w2", [DFF, D])
    out = P.dout("out", [2, 2048, D])
    otok = P.scr("otok", [2, 2048, D])
    h3 = P.scr("h3", [2, 2048, D])
    with ExitStack() as st:
        g = setup_globals(kb, st)
        load_ident(kb, g, ident)
        seqs = []
        for s_, kv, L in ((0, kvp, LP), (1, kvs, LS)):
            otv = otok[s_].rearrange("(a t p) (h c) -> a p t h c", p=128, t=4, c=64)
            seqs.append(dict(kchunks=[(kv[t0:min(t0 + 128, L), :], min(128, L - t0)) for t0 in range(0, L, 128)],
                             qt=(lambda s_: (lambda h: QT[s_, h, :, :]))(s_),
                             o=(lambda otv: (lambda qsb, half, h: otv[qsb * 2 + half, :, :, h, :]))(otv)))
        phase_attn(kb, g, seqs, WkH, WvH)
        tl2 = [(s_, t0) for s_ in range(2) for t0 in range(0, 2048, 128)]
        phase_proj_ln(kb, g, [(h2[s_, t0:t0 + 128, :], otok[s_, t0:t0 + 128, :], h3[s_, t0:t0 + 128, :], 128) for s_, t0 in tl2], False, wo, None, ln1g, ln1b)
        phase_mlp_ln(kb, g, [(h3[s_, t0:t0 + 128, :], out[s_, t0:t0 + 128, :], 128) for s_, t0 in tl2], w1, w2, ln2g, ln2b)
        kb.finish_wait()
    return P


def zpos_table(L):
    t = np.arange(L, dtype=np.float32) / max(L - 1, 1)
    freqs = np.linspace(1e-4, 15, 16, dtype=np.float32)
    w = (np.float32(2.0 * math.pi) * np.arange(L, dtype=np.float32) / np.float32(L)).astype(np.float32)
    ang = w[:, None] * freqs[None, :]
    return np.ascontiguousarray(np.concatenate([t[:, None], np.cos(ang), -np.sin(ang)], -1).T.astype(np.float32))


def rope_cs(pos):
    inv = (1.0 / (10000.0 ** (np.arange(0, 32, 2, dtype=np.float32) / 32))).astype(np.float32)
    ang = pos.astype(np.float32)[:, None] * inv[None, :]
    return np.cos(ang).astype(np.float32), np.sin(ang).astype(np.float32)


def make_xc(hfull, m0, L):
    x = np.zeros((XC, D), np.float32)
    mk = np.zeros((1, XC), np.float32)
    x[15:46] = hfull[0:31]
    mk[0, 15:46] = 1
    lo, hi = m0 - 15, min(m0 + 2048 + 15, L)
    x[46:46 + (hi - lo)] = hfull[lo:hi]
    mk[0, 46:46 + (hi - lo)] = 1
    return x, mk


def colpack(v):
    return np.ascontiguousarray(v.reshape(4, 128).T)


def check_inputs(P, im):
    for k, (shape, dt) in P.ins.items():
        assert k in im, k
        assert tuple(im[k].shape) == shape, (k, im[k].shape, shape)
    return {k: np.ascontiguousarray(im[k]) for k in P.ins}


def kernel_unfused(x_prompt, x_sample, meta_tokens, ev_w_in, ev_b_in, ev_short_w, ev_short_b,
           hy_w1, hy_b1, hy_freq1, hy_w2, hy_b2, hy_freq2, hy_w3, hy_decay, hy_skip_d,
           cf_dw_w, cf_dw_b, cf_ln_g, cf_ln_b, ev_w_out, ev_b_out,
           mla_wq_a, mla_q_norm, mla_wq_b, mla_wkv_a, mla_kv_norm, mla_wkv_b, mla_wo,
           ln1_g, ln1_b, mlp_w1, mlp_w2, ln2_g, ln2_b):
    f = lambda a: np.asarray(a, dtype=np.float32)
    x_prompt, x_sample, meta = f(x_prompt), f(x_sample), f(meta_tokens)
    win, bin_, sw, sb = f(ev_w_in)[0], f(ev_b_in)[0], f(ev_short_w)[0], f(ev_short_b)[0]
    ident = np.eye(128, dtype=np.float32)
    hp = np.concatenate([meta, x_prompt[0]], 0)
    hs = [np.concatenate([meta, x_sample[c]], 0) for c in range(8)]
    z1 = np.zeros((1, D), np.float32)
    xh_p = np.concatenate([z1, hp, z1], 0)
    valid_p = np.ones((1, LP + 2), np.float32); valid_p[0, 0] = 0; valid_p[0, -1] = 0
    valid_s = np.ones((1, LS + 2), np.float32); valid_s[0, 0] = 0; valid_s[0, -1] = 0
    tabP, tabS = fft_tables(CFG_P), fft_tables(CFG_S)
    def grp(gi):
        ch = slice(gi * 64, gi * 64 + 64)
        gcols = [np.arange(k * 512 + gi * 64, k * 512 + gi * 64 + 64) for k in range(3)]
        allc = np.concatenate(gcols)
        return dict(fw3=np.ascontiguousarray(f(hy_w3)[0][:, :, ch]),
                    fcols=np.ascontiguousarray(np.stack([f(hy_freq1)[0], f(hy_b1)[0], f(hy_freq2)[0], f(hy_b2)[0], f(hy_decay)[0][:, ch]], -1).transpose(1, 0, 2)),
                    why=np.ascontiguousarray(win[:, allc]), brow=bin_[allc][None, :].copy(),
                    hcols=np.concatenate([np.stack([sw[0, gc], sw[1, gc], sw[2, gc], sb[gc]], 1) for gc in gcols], 1).astype(np.float32),
                    dskip=f(hy_skip_d)[0][ch][None, :].copy())
    G = [grp(gi) for gi in range(8)]
    ccols = np.concatenate([colpack(bin_[1536:2048]), colpack(bin_[2048:2560]), colpack(f(cf_dw_b)[0]), colpack(f(cf_ln_g)[0]), colpack(f(cf_ln_b)[0]),
                            np.ascontiguousarray(f(cf_dw_w)[0].T.reshape(4, 128, 31).transpose(1, 0, 2).reshape(128, 124))], 1).astype(np.float32)
    xcs, masks = [], []
    for c in range(8):
        a, ma = make_xc(hp, 16 + 2048 * c, LP)
        b, mb = make_xc(hs[c], 16, LS)
        xcs.append(np.stack([a, b], 0))
        masks.append(np.stack([ma, mb], 0))
    P1 = build_l1()
    ims = []
    for c in range(8):
        order = [c] + list(range(8))
        im = dict(ident=ident, xh_p=xh_p, xh_s=np.concatenate([z1, hs[c], z1], 0), valid_p=valid_p, valid_s=valid_s,
                  zpos_p=zpos_table(LP), zpos_s=zpos_table(LS), fw1=f(hy_w1)[0], fw2=f(hy_w2)[0],
                  fw3=np.stack([G[i]["fw3"] for i in order], 0), fcols=np.stack([G[i]["fcols"] for i in order], 0).astype(np.float32),
                  why=np.stack([G[i]["why"] for i in order], 0), brow=np.stack([G[i]["brow"] for i in order], 0),
                  hcols=np.stack([G[i]["hcols"] for i in order], 0), dskip=np.stack([G[i]["dskip"] for i in order], 0),
                  xc=xcs[c], mask=masks[c], wconf=np.ascontiguousarray(win[:, 1536:2560]), ccols=ccols)
        for k, v in tabP.items():
            im["tp_" + k] = v
        for k, v in tabS.items():
            im["ts_" + k] = v
        ims.append(check_inputs(P1, im))
    r1 = run_bass_kernel_spmd(P1.nc, ims, core_ids=list(range(8))).results
    yaP_all = np.concatenate([np.asarray(r1[c]["yaP"]) for c in range(8)], 0)
    P2 = build_l2()
    wqb = f(mla_wq_b)[0].reshape(384, NH, 96)
    WqH = np.concatenate([wqb[:, :, 64:96], np.zeros((384, NH, 32), np.float32), wqb[:, :, 0:64]], -1).reshape(384, NH * 128)
    WqS = np.concatenate([wqb[:, :, 80:96], wqb[:, :, 64:80]], -1).reshape(384, NH * 32)
    wkvb = f(mla_wkv_b)[0].reshape(256, NH, 128)
    WkH = np.concatenate([np.zeros((256, NH, 64), np.float32), wkvb[:, :, 0:64]], -1).reshape(256, NH * 128)
    WvH = np.ascontiguousarray(wkvb[:, :, 64:128]).reshape(256, NH * 64)
    ims = []
    for c in range(8):
        m0 = 16 + 2048 * c
        ya_p = np.concatenate([yaP_all[:, m0:m0 + 2048], yaP_all[:, 0:16]], 1)
        ya_s = np.asarray(r1[c]["yaS"]).reshape(512, LS)
        ya_s = np.concatenate([ya_s[:, 16:], ya_s[:, 0:16]], 1)
        yb = np.asarray(r1[c]["ybT"])
        ycT = np.stack([np.concatenate([ya_p, yb[0]], 0), np.concatenate([ya_s, yb[1]], 0)], 0)
        css, Cqs, Sqs = [], [], []
        for pos in (np.concatenate([np.arange(m0, m0 + 2048), np.arange(16)]), np.concatenate([np.arange(16, LS), np.arange(16)])):
            co, si = rope_cs(pos)
            css.append(np.concatenate([co, si], 1))
            Cqs.append(np.concatenate([co[:2048].T, co[:2048].T], 0))
            Sqs.append(np.concatenate([-si[:2048].T, si[:2048].T], 0))
        im = dict(ident=ident, xc=xcs[c], ycT=ycT, wout=f(ev_w_out)[0], bout=f(ev_b_out)[0:1], ln1g=f(ln1_g)[0:1], ln1b=f(ln1_b)[0:1],
                  ln2g=f(ln2_g)[0:1], ln2b=f(ln2_b)[0:1], w1=f(mlp_w1)[0], w2=f(mlp_w2)[0], wqa=f(mla_wq_a)[0], qg=f(mla_q_norm)[0:1],
                  WqH=WqH, WqS=WqS, wkva=f(mla_wkv_a)[0], kvg=f(mla_kv_norm)[0:1], cs=np.stack(css, 0), Cq=np.stack(Cqs, 0), Sq=np.stack(Sqs, 0))
        ims.append(check_inputs(P2, im))
    r2 = run_bass_kernel_spmd(P2.nc, ims, core_ids=list(range(8))).results
    kvp = np.concatenate([np.asarray(r2[c]["kvlat"])[0, :2048] for c in range(8)] + [np.asarray(r2[0]["kvlat"])[0, 2048:]], 0)
    P3 = build_l3()
    ims = []
    for c in range(8):
        im = dict(ident=ident, h2=np.asarray(r2[c]["h2"]), kvp=kvp, kvs=np.asarray(r2[c]["kvlat"])[1], QT=np.asarray(r2[c]["QT"]), WkH=WkH, WvH=WvH,
                  wo=f(mla_wo)[0], ln1g=f(ln1_g)[1:2], ln1b=f(ln1_b)[1:2], ln2g=f(ln2_g)[1:2], ln2b=f(ln2_b)[1:2], w1=f(mlp_w1)[1], w2=f(mlp_w2)[1])
        ims.append(check_inputs(P3, im))
    r3 = run_bass_kernel_spmd(P3.nc, ims, core_ids=list(range(8))).results
    y_prompt = np.concatenate([np.asarray(r3[c]["out"])[0] for c in range(8)], 0)[None].astype(np.float32)
    y_sample = np.stack([np.asarray(r3[c]["out"])[1] for c in range(8)], 0).astype(np.float32)
    return (y_prompt, y_sample)


U32 = mybir.dt.uint32
YAW = 18432


def build_fused(stop=10 ** 9, trace_steps=None):
    P = Prog()
    step = [0]

    def run(fn, *a):
        if step[0] < stop:
            fn(*a)
        step[0] += 1

    kb = P.kb
    nc = P.nc
    ident = P.din("ident", [128, 128])
    xh_p = P.din("xh_p", [LP + 2, D]); xh_s = P.din("xh_s", [LS + 2, D])
    valid_p = P.din("valid_p", [1, LP + 2]); valid_s = P.din("valid_s", [1, LS + 2])
    zpos_p = P.din("zpos_p", [33, LP]); zpos_s = P.din("zpos_s", [33, LS])
    tabsP, _ = declare_tabs(P, CFG_P, "tp_")
    tabsS, _ = declare_tabs(P, CFG_S, "ts_")
    fw1 = P.din("fw1", [2, 33, 64]); fw2 = P.din("fw2", [2, 64, 64])
    fw3 = P.din("fw3", [9, 2, 64, 64]); fcols = P.din("fcols", [9, 64, 2, 5])
    why = P.din("why", [9, D, 192]); brow = P.din("brow", [9, 1, 192]); hcols = P.din("hcols", [9, 64, 12]); dskip = P.din("dskip", [9, 1, 64])
    xc = P.din("xc", [2, XC, D]); mask = P.din("mask", [2, 1, XC])
    wconf = P.din("wconf", [D, 1024]); ccols = P.din("ccols", [128, 144])
    gidx = P.din("gidx", [128, 4], U32)
    wout = P.din("wout", [D, D]); bout = P.din("bout", [1, D])
    ln1g = P.din("ln1g", [2, 1, D]); ln1b = P.din("ln1b", [2, 1, D]); ln2g = P.din("ln2g", [2, 1, D]); ln2b = P.din("ln2b", [2, 1, D])
    w1 = P.din("w1", [2, D, DFF]); w2 = P.din("w2", [2, DFF, D])
    wqa = P.din("wqa", [D, 384]); qg = P.din("qg", [1, 384]); WqH = P.din("WqH", [384, NH * 128]); WqS = P.din("WqS", [384, NH * 32])
    wkva = P.din("wkva", [D, 288]); kvg = P.din("kvg", [1, 256])
    cs = P.din("cs", [2, LS, 32]); Cq = P.din("Cq", [2, 32, 2048]); Sq = P.din("Sq", [2, 32, 2048])
    WkH = P.din("WkH", [256, NH * 128]); WvH = P.din("WvH", [256, NH * 64]); wo = P.din("wo", [D, D])
    out = P.dout("out", [2, 2048, D])
    yaP = P.scr("yaP", [64, YAW], BF16)
    yaP_all = P.scr("yaP_all", [512, YAW], BF16)
    yaS = P.scr("yaS", [8, 64, LS], BF16)
    ybT = P.scr("ybT", [2, 512, LS], BF16)
    taps_p = P.scr("taps_p", [2, 64, LP]); Hs_p = P.scr("Hs_p", [86, 64 * CFG_P.nq, 2, CFG_P.N1])
    z_p = P.scr("z_p", [64, LP]); x0_p = P.scr("x0_p", [64, LP])
    taps_s = P.scr("taps_s", [8, 2, 64, LS]); Hs_s = P.scr("Hs_s", [8, 86, 64 * CFG_S.nq, 2, CFG_S.N1])
    z_s = P.scr("z_s", [8, 64, LS]); x0_s = P.scr("x0_s", [8, 64, LS])
    h1 = P.scr("h1", [2, LS, D]); h2 = P.scr("h2", [2, LS, D])
    kvlat = P.scr("kvlat", [2, LS, 288]); kv_all = P.scr("kv_all", [8 * LS, 288])
    QT = P.scr("QT", [2, NH, 128, 2048], BF16)
    otok = P.scr("otok", [2, 2048, D]); h3 = P.scr("h3", [2, 2048, D])
    tl = chunk_tiles()
    with ExitStack() as st:
        g = setup_globals(kb, st)
        load_ident(kb, g, ident)
        fwd = lambda i: dict(w1=fw1, w2=fw2, w3=fw3[i], fcols=fcols[i])
        run(phase_hy_filters, kb, g, LP, zpos_p, fwd(0), taps_p)
        run(phase_hy_inproj, kb, g, [dict(xh=xh_p, valid=valid_p, L=LP, z=[z_p], x0=[x0_p])], why[0], brow[0], hcols[0])
        run(phase_hy_conv, kb, g, CFG_P, tabsP, taps_p, Hs_p, [dict(z=z_p, x0=x0_p, ya=yaP[:, 2032:2032 + LP])], dskip[0])
        agd = Dep()
        run(lambda: kb.all_gather(yaP, yaP_all, reads=[], writes=[agd]))
        kb.barrier()
        for gi in range(8):
            run(phase_hy_filters, kb, g, LS, zpos_s, fwd(1 + gi), taps_s[gi])
            run(phase_hy_inproj, kb, g, [dict(xh=xh_s, valid=valid_s, L=LS, z=[z_s[gi]], x0=[x0_s[gi]])], why[1 + gi], brow[1 + gi], hcols[1 + gi])
            run(phase_hy_conv, kb, g, CFG_S, tabsS, taps_s[gi], Hs_s[gi], [dict(z=z_s[gi], x0=x0_s[gi], ya=yaS[gi])], dskip[1 + gi])

        def outf(s_):
            def f(j, bi, cn):
                if bi == 0:
                    return ybT[s_, j * 128:(j + 1) * 128, 2048:2064]
                return ybT[s_, j * 128:(j + 1) * 128, (bi - 1) * 512:bi * 512]
            return f
        run(phase_conf, kb, g, [dict(x=xc[s_], mask=mask[s_], out=outf(s_)) for s_ in range(2)], wconf, ccols)
        with ExitStack() as st2:
            yaG = kb.sb(st2, [128, 4, 2048], BF16, "yaG")
            yaGd = Dep()
            ix = kb.sb(st2, [128, 4], U32, "gix")
            ixd = Dep()
            kb.dma("sp", ix[:, :], gidx[:, :], writes=[ixd])
            rows = yaP_all.rearrange("c (b t) -> (c b) t", t=2048)
            for k in range(4):
                run(lambda k=k: kb.gather_rows(yaG[:, k, :], rows[:, :], ix[:, k:k + 1], reads=[agd, ixd], writes=[yaGd]))
            ybv = ybT.rearrange("s (k p) t -> s p k t", p=128)
            yav_meta = yaP_all.rearrange("(k p) c -> p k c", p=128)
            yas = yaS.rearrange("g c t -> (g c) t").rearrange("(k p) t -> p k t", p=128)
            tiles = []
            for t0, n, xr in tl:
                if n == 128:
                    yl = [(slice(0, 4), yaG[:, :, t0:t0 + n], yaGd), (slice(4, 8), ybv[0, :, :, t0:t0 + n], None)]
                else:
                    yl = [(slice(0, 4), yav_meta[:, :, 2032:2048], agd), (slice(4, 8), ybv[0, :, :, 2048:2064], None)]
                tiles.append((xc[0, xr:xr + n, :], yl, h1[0, t0:t0 + n, :], n))
            for t0, n, xr in tl:
                tok0 = 16 + t0 if n == 128 else 0
                yl = [(slice(0, 4), yas[:, :, tok0:tok0 + n], None), (slice(4, 8), ybv[1, :, :, t0:t0 + n], None)]
                tiles.append((xc[1, xr:xr + n, :], yl, h1[1, t0:t0 + n, :], n))
            run(phase_proj_ln, kb, g, tiles, True, wout, bout, ln1g[0], ln1b[0])
        run(phase_mlp_ln, kb, g, [(h1[s_, t0:t0 + n, :], h2[s_, t0:t0 + n, :], n) for s_ in range(2) for t0, n, xr in tl], w1[0], w2[0], ln2g[0], ln2b[0])
        seqs = []
        for s_ in range(2):
            seqs.append(dict(tiles=[(h2[s_, t0:t0 + n, :], kvlat[s_, t0:t0 + n, :], cs[s_, t0:t0 + n, :], n) for t0, n, xr in tl],
                             CS=(Cq[s_], Sq[s_]), qt=(lambda s_: (lambda h, q0: QT[s_, h, :, q0:q0 + 512]))(s_)))
        run(phase_qkv, kb, g, seqs, wqa, qg, WqH, WqS, wkva, kvg)
        kvd = Dep()
        run(lambda: kb.all_gather(kvlat[0], kv_all, reads=[], writes=[kvd]))
        kb.barrier()
        seqs = []
        pch = [(kv_all[r * LS + t0:r * LS + t0 + 128, :], 128) for r in range(8) for t0 in range(0, 2048, 128)] + [(kv_all[2048:2064, :], 16)]
        sch = [(kvlat[1, t0:min(t0 + 128, LS), :], min(128, LS - t0)) for t0 in range(0, LS, 128)]
        for s_, ch in ((0, pch), (1, sch)):
            otv = otok[s_].rearrange("(a t p) (h c) -> a p t h c", p=128, t=4, c=64)
            seqs.append(dict(kchunks=ch, qt=(lambda s_: (lambda h: QT[s_, h, :, :]))(s_),
                             o=(lambda otv: (lambda qsb, half, h: otv[qsb * 2 + half, :, :, h, :]))(otv)))
        run(phase_attn, kb, g, seqs, WkH, WvH)
        tl2 = [(s_, t0) for s_ in range(2) for t0 in range(0, 2048, 128)]
        run(phase_proj_ln, kb, g, [(h2[s_, t0:t0 + 128, :], otok[s_, t0:t0 + 128, :], h3[s_, t0:t0 + 128, :], 128) for s_, t0 in tl2], False, wo, None, ln1g[1], ln1b[1])
        run(phase_mlp_ln, kb, g, [(h3[s_, t0:t0 + 128, :], out[s_, t0:t0 + 128, :], 128) for s_, t0 in tl2], w1[1], w2[1], ln2g[1], ln2b[1])
        kb.finish_wait()
    P.nsteps = step[0]
    return P


def build_nc():
    P = Prog()
    kb = P.kb
    ident = P.din("ident", [128, 128])
    xpad_p = P.din("xpad_p", [LP + 30, D]); maskpad = P.din("maskpad", [1, LP + 30])
    xh_s = P.din("xh_s", [LS + 2, D]); valid_s = P.din("valid_s", [1, LS + 2])
    xc_s = P.din("xc_s", [XC, D]); mask_s = P.din("mask_s", [1, XC])
    zpos_p = P.din("zpos_p", [33, LP]); zpos_s = P.din("zpos_s", [33, LS])
    tabsP, _ = declare_tabs(P, CFG_P, "tp_")
    tabsS, _ = declare_tabs(P, CFG_S, "ts_")
    fw1 = P.din("fw1", [2, 33, 64]); fw2 = P.din("fw2", [2, 64, 64])
    fw3 = P.din("fw3", [8, 2, 64, 64]); fcols = P.din("fcols", [8, 64, 2, 5])
    why = P.din("why", [D, 8 * 192]); brow = P.din("brow", [1, 8 * 192]); hcols = P.din("hcols", [64, 8 * 12]); dskip = P.din("dskip", [8, 1, 64])
    wconf = P.din("wconf", [D, 1024]); ccols = P.din("ccols", [128, 144])
    tokidx = P.din("tokidx", [128, 16], U32)
    wout = P.din("wout", [D, D]); bout = P.din("bout", [1, D])
    ln1g = P.din("ln1g", [2, 1, D]); ln1b = P.din("ln1b", [2, 1, D]); ln2g = P.din("ln2g", [2, 1, D]); ln2b = P.din("ln2b", [2, 1, D])
    w1 = P.din("w1", [2, D, DFF]); w2 = P.din("w2", [2, DFF, D])
    wqa = P.din("wqa", [D, 384]); qg = P.din("qg", [1, 384]); WqH = P.din("WqH", [384, NH * 128]); WqS = P.din("WqS", [384, NH * 32])
    wkva = P.din("wkva", [D, 288]); kvg = P.din("kvg", [1, 256])
    cs_all = P.din("cs_all", [LP, 32])
    cs = P.din("cs", [2, LS, 32]); Cq = P.din("Cq", [2, 32, 2048]); Sq = P.din("Sq", [2, 32, 2048])
    WkH = P.din("WkH", [256, NH * 128]); WvH = P.din("WvH", [256, NH * 64]); wo = P.din("wo", [D, D])
    out = P.dout("out", [2, 2048, D])
    yaP_all = P.scr("yaP_all", [512, YAW], BF16)
    yaS = P.scr("yaS", [8, 64, LS], BF16)
    ybT_p = P.scr("ybT_p", [512, LP], BF16); ybT_s = P.scr("ybT_s", [512, LS], BF16)
    h2f_p = P.scr("h2f_p", [2, 64, LP]); h2f_s = P.scr("h2f_s", [2, 64, LS])
    taps_p = P.scr("taps_p", [2, 64, LP]); Hs_p = P.scr("Hs_p", [86, 64 * CFG_P.nq, 2, CFG_P.NF])
    z_p = P.scr("z_p", [8, 64, LP]); x0_p = P.scr("x0_p", [8, 64, LP])
    taps_s = P.scr("taps_s", [2, 64, LS]); Hs_s = P.scr("Hs_s", [86, 64 * CFG_S.nq, 2, CFG_S.NF])
    z_s = P.scr("z_s", [8, 64, LS]); x0_s = P.scr("x0_s", [8, 64, LS])
    h1_all = P.scr("h1_all", [LP, D]); h2_all = P.scr("h2_all", [LP, D])
    h1_s = P.scr("h1_s", [LS, D]); h2_s = P.scr("h2_s", [LS, D]); h2_own = P.scr("h2_own", [LS, D])
    kv_all = P.scr("kv_all", [LP, 288]); kv_dummy = P.scr("kv_dummy", [LS, 288]); kvlat_s = P.scr("kvlat_s", [LS, 288])
    QT = P.scr("QT", [2, NH, 128, 2048], BF16)
    otok = P.scr("otok", [2, 2048, D]); h3 = P.scr("h3", [2, 2048, D])
    tl = chunk_tiles()
    with ExitStack() as st:
        g = setup_globals(kb, st)
        load_ident(kb, g, ident)
        fwd = lambda i: dict(w1=fw1, w2=fw2, w3=fw3[i], fcols=fcols[i])
        phase_hy_inproj(kb, g, [dict(xh=xpad_p[14:14 + LP + 2, :], valid=maskpad[:, 14:14 + LP + 2], L=LP,
                                     z=[z_p[gi] for gi in range(8)], x0=[x0_p[gi] for gi in range(8)])], why, brow, hcols, G=8)
        phase_hy_filter_h2(kb, g, LP, zpos_p, fwd(0), h2f_p)
        phase_hy_filter_h2(kb, g, LS, zpos_s, fwd(0), h2f_s)
        for gi in range(8):
            phase_hy_filter_taps(kb, g, LP, zpos_p, h2f_p, fw3[gi], fcols[gi], taps_p)
            phase_hy_conv(kb, g, CFG_P, tabsP, taps_p, Hs_p, [dict(z=z_p[gi], x0=x0_p[gi], ya=yaP_all[gi * 64:(gi + 1) * 64, 2032:2032 + LP])], dskip[gi])
        phase_hy_inproj(kb, g, [dict(xh=xh_s, valid=valid_s, L=LS, z=[z_s[gi] for gi in range(8)], x0=[x0_s[gi] for gi in range(8)])],
                        why, brow, hcols, G=8)
        for gi in range(8):
            phase_hy_filter_taps(kb, g, LS, zpos_s, h2f_s, fw3[gi], fcols[gi], taps_s)
            phase_hy_conv(kb, g, CFG_S, tabsS, taps_s, Hs_s, [dict(z=z_s[gi], x0=x0_s[gi], ya=yaS[gi])], dskip[gi])
        cseqs = []
        for j in range(8):
            r0 = 16 + 2048 * j
            cseqs.append(dict(x=xpad_p[r0:r0 + 2078, :], mask=maskpad[:, r0:r0 + 2078], ncols=2078, blocks=[(15 + 512 * i, 512) for i in range(4)],
                              out=(lambda j: (lambda jj, bi, cn: ybT_p[jj * 128:(jj + 1) * 128, 2048 * j + 512 * bi:2048 * j + 512 * bi + cn]))(j)))
        cseqs.append(dict(x=xpad_p[0:46, :], mask=maskpad[:, 0:46], ncols=46, blocks=[(15, 16)],
                          out=lambda jj, bi, cn: ybT_p[jj * 128:(jj + 1) * 128, 16384:16400]))

        def outf_s(jj, bi, cn):
            if bi == 0:
                return ybT_s[jj * 128:(jj + 1) * 128, 2048:2064]
            return ybT_s[jj * 128:(jj + 1) * 128, (bi - 1) * 512:bi * 512]
        cseqs.append(dict(x=xc_s, mask=mask_s, out=outf_s))
        phase_conf(kb, g, cseqs, wconf, ccols)
        yav = yaP_all.rearrange("(k p) c -> p k c", p=128)
        ybv_p = ybT_p.rearrange("(k p) t -> p k t", p=128)
        ybv_s = ybT_s.rearrange("(k p) t -> p k t", p=128)
        yas = yaS.rearrange("g c t -> (g c) t").rearrange("(k p) t -> p k t", p=128)
        tiles = []
        for j in range(8):
            for t0 in range(0, 2048, 128):
                tok = 16 + 2048 * j + t0
                gr = 2048 * j + t0
                tiles.append((xpad_p[15 + tok:15 + tok + 128, :],
                              [(slice(0, 4), yav[:, :, 2032 + tok:2032 + tok + 128], None), (slice(4, 8), ybv_p[:, :, gr:gr + 128], None)],
                              h1_all[gr:gr + 128, :], 128))
        tiles.append((xpad_p[15:31, :], [(slice(0, 4), yav[:, :, 2032:2048], None), (slice(4, 8), ybv_p[:, :, 16384:16400], None)],
                      h1_all[16384:16400, :], 16))
        for t0, n, xr in tl:
            tok0 = 16 + t0 if n == 128 else 0
            tiles.append((xc_s[xr:xr + n, :], [(slice(0, 4), yas[:, :, tok0:tok0 + n], None), (slice(4, 8), ybv_s[:, :, t0:t0 + n], None)],
                          h1_s[t0:t0 + n, :], n))
        phase_proj_ln(kb, g, tiles, True, wout, bout, ln1g[0], ln1b[0])
        ptl = [(r0, min(128, LP - r0)) for r0 in range(0, LP, 128)]
        phase_mlp_ln(kb, g, [(h1_all[r0:r0 + n, :], h2_all[r0:r0 + n, :], n) for r0, n in ptl] +
                     [(h1_s[t0:t0 + n, :], h2_s[t0:t0 + n, :], n) for t0, n, xr in tl], w1[0], w2[0], ln2g[0], ln2b[0])
        with ExitStack() as st2:
            ix = kb.sb(st2, [128, 16], U32, "tokix")
            ixd = Dep()
            kb.dma("sp", ix[:, :], tokidx[:, :], writes=[ixd])
            gb = [kb.sb(st2, [128, D], F32, "gb") for _ in range(2)]
            gd = [Dep(), Dep()]
            for i in range(16):
                j = i % 2
                kb.gather_rows(gb[j][:, :], h2_all[:, :], ix[:, i:i + 1], reads=[ixd], writes=[gd[j]])
                kb.dma("sp", h2_own[128 * i:128 * i + 128, :], gb[j][:, :], reads=[gd[j]])
            kb.dma("sp", h2_own[2048:2064, :], h2_all[16384:16400, :])
            kb.barrier()
        seqs = [dict(tiles=[(h2_all[r0:r0 + n, :], kv_all[r0:r0 + n, :], cs_all[r0:r0 + n, :], n) for r0, n in ptl], kv_only=True),
                dict(tiles=[(h2_own[t0:t0 + n, :], kv_dummy[t0:t0 + n, :], cs[0, t0:t0 + n, :], n) for t0, n, xr in tl],
                     CS=(Cq[0], Sq[0]), qt=lambda h, q0: QT[0, h, :, q0:q0 + 512]),
                dict(tiles=[(h2_s[t0:t0 + n, :], kvlat_s[t0:t0 + n, :], cs[1, t0:t0 + n, :], n) for t0, n, xr in tl],
                     CS=(Cq[1], Sq[1]), qt=lambda h, q0: QT[1, h, :, q0:q0 + 512])]
        phase_qkv(kb, g, seqs, wqa, qg, WqH, WqS, wkva, kvg)
        aseqs = []
        for s_, ch in ((0, [(kv_all[r0:r0 + n, :], n) for r0, n in ptl]),
                       (1, [(kvlat_s[t0:min(t0 + 128, LS), :], min(128, LS - t0)) for t0 in range(0, LS, 128)])):
            otv = otok[s_].rearrange("(a t p) (h c) -> a p t h c", p=128, t=4, c=64)
            aseqs.append(dict(kchunks=ch, qt=(lambda s_: (lambda h: QT[s_, h, :, :]))(s_),
                              o=(lambda otv: (lambda qsb, half, h: otv[qsb * 2 + half, :, :, h, :]))(otv)))
        phase_attn(kb, g, aseqs, WkH, WvH)
        hres = (h2_own, h2_s)
        tl2 = [(s_, t0) for s_ in range(2) for t0 in range(0, 2048, 128)]
        phase_proj_ln(kb, g, [(hres[s_][t0:t0 + 128, :], otok[s_, t0:t0 + 128, :], h3[s_, t0:t0 + 128, :], 128) for s_, t0 in tl2], False, wo, None, ln1g[1], ln1b[1])
        phase_mlp_ln(kb, g, [(h3[s_, t0:t0 + 128, :], out[s_, t0:t0 + 128, :], 128) for s_, t0 in tl2], w1[1], w2[1], ln2g[1], ln2b[1])
        kb.finish_wait()
    return P


def kernel(x_prompt, x_sample, meta_tokens, ev_w_in, ev_b_in, ev_short_w, ev_short_b,
           hy_w1, hy_b1, hy_freq1, hy_w2, hy_b2, hy_freq2, hy_w3, hy_decay, hy_skip_d,
           cf_dw_w, cf_dw_b, cf_ln_g, cf_ln_b, ev_w_out, ev_b_out,
           mla_wq_a, mla_q_norm, mla_wq_b, mla_wkv_a, mla_kv_norm, mla_wkv_b, mla_wo,
           ln1_g, ln1_b, mlp_w1, mlp_w2, ln2_g, ln2_b):
    f = lambda a: np.asarray(a, dtype=np.float32)
    x_prompt, x_sample, meta = f(x_prompt), f(x_sample), f(meta_tokens)
    win, bin_, sw, sb = f(ev_w_in)[0], f(ev_b_in)[0], f(ev_short_w)[0], f(ev_short_b)[0]
    ident = np.eye(128, dtype=np.float32)
    hp = np.concatenate([meta, x_prompt[0]], 0)
    hs = [np.concatenate([meta, x_sample[c]], 0) for c in range(8)]
    z1 = np.zeros((1, D), np.float32)
    z15 = np.zeros((15, D), np.float32)
    xpad_p = np.concatenate([z15, hp, z15], 0)
    maskpad = np.zeros((1, LP + 30), np.float32); maskpad[0, 15:15 + LP] = 1
    valid_s = np.ones((1, LS + 2), np.float32); valid_s[0, 0] = 0; valid_s[0, -1] = 0
    tabP, tabS = fft_tables(CFG_P), fft_tables(CFG_S)
    gcols = [[np.arange(k * 512 + gi * 64, k * 512 + gi * 64 + 64) for k in range(3)] for gi in range(8)]
    allc = np.concatenate([np.concatenate(gc) for gc in gcols])
    why = np.ascontiguousarray(win[:, allc])
    brow = bin_[allc][None, :].copy()
    hcols = np.concatenate([np.stack([sw[0, c_], sw[1, c_], sw[2, c_], sb[c_]], 1) for gc in gcols for c_ in gc], 1).astype(np.float32)
    fw3 = np.stack([np.ascontiguousarray(f(hy_w3)[0][:, :, gi * 64:gi * 64 + 64]) for gi in range(8)], 0)
    fcols = np.stack([np.stack([f(hy_freq1)[0], f(hy_b1)[0], f(hy_freq2)[0], f(hy_b2)[0], f(hy_decay)[0][:, gi * 64:gi * 64 + 64]], -1).transpose(1, 0, 2)
                      for gi in range(8)], 0).astype(np.float32)
    dskip = np.stack([f(hy_skip_d)[0][gi * 64:gi * 64 + 64][None, :] for gi in range(8)], 0)
    ccols = np.concatenate([colpack(bin_[1536:2048]), colpack(bin_[2048:2560]), colpack(f(cf_dw_b)[0]), colpack(f(cf_ln_g)[0]), colpack(f(cf_ln_b)[0]),
                            np.ascontiguousarray(f(cf_dw_w)[0].T.reshape(4, 128, 31).transpose(1, 0, 2).reshape(128, 124))], 1).astype(np.float32)
    wqb = f(mla_wq_b)[0].reshape(384, NH, 96)
    WqH = np.concatenate([wqb[:, :, 64:96], np.zeros((384, NH, 32), np.float32), wqb[:, :, 0:64]], -1).reshape(384, NH * 128)
    WqS = np.concatenate([wqb[:, :, 80:96], wqb[:, :, 64:80]], -1).reshape(384, NH * 32)
    wkvb = f(mla_wkv_b)[0].reshape(256, NH, 128)
    WkH = np.concatenate([np.zeros((256, NH, 64), np.float32), wkvb[:, :, 0:64]], -1).reshape(256, NH * 128)
    WvH = np.ascontiguousarray(wkvb[:, :, 64:128]).reshape(256, NH * 64)
    zp_p, zp_s = zpos_table(LP), zpos_table(LS)
    co, si = rope_cs(np.concatenate([np.arange(16, LP), np.arange(16)]))
    cs_all = np.concatenate([co, si], 1)
    shared = dict(ident=ident, xpad_p=xpad_p, maskpad=maskpad, valid_s=valid_s, zpos_p=zp_p, zpos_s=zp_s, fw1=f(hy_w1)[0], fw2=f(hy_w2)[0],
                  fw3=fw3, fcols=fcols, why=why, brow=brow, hcols=hcols, dskip=dskip, wconf=np.ascontiguousarray(win[:, 1536:2560]), ccols=ccols,
                  wout=f(ev_w_out)[0], bout=f(ev_b_out)[0:1], ln1g=f(ln1_g)[:, None, :], ln1b=f(ln1_b)[:, None, :],
                  ln2g=f(ln2_g)[:, None, :], ln2b=f(ln2_b)[:, None, :], w1=f(mlp_w1), w2=f(mlp_w2), wqa=f(mla_wq_a)[0], qg=f(mla_q_norm)[0:1],
                  WqH=WqH, WqS=WqS, wkva=f(mla_wkv_a)[0], kvg=f(mla_kv_norm)[0:1], cs_all=cs_all, WkH=WkH, WvH=WvH, wo=f(mla_wo)[0])
    for k, v in tabP.items():
        shared["tp_" + k] = v
    for k, v in tabS.items():
        shared["ts_" + k] = v
    P = build_nc()
    ims = []
    for c in range(8):
        m0 = 16 + 2048 * c
        xb, mb = make_xc(hs[c], 16, LS)
        css, Cqs, Sqs = [], [], []
        for pos in (np.concatenate([np.arange(m0, m0 + 2048), np.arange(16)]), np.concatenate([np.arange(16, LS), np.arange(16)])):
            co, si = rope_cs(pos)
            css.append(np.concatenate([co, si], 1))
            Cqs.append(np.concatenate([co[:2048].T, co[:2048].T], 0))
            Sqs.append(np.concatenate([-si[:2048].T, si[:2048].T], 0))
        tix = (2048 * c + 128 * np.arange(16)[None, :] + np.arange(128)[:, None]).astype(np.uint32)
        im = dict(shared)
        im.update(xh_s=np.concatenate([z1, hs[c], z1], 0), xc_s=xb, mask_s=mb, tokidx=tix,
                  cs=np.stack(css, 0), Cq=np.stack(Cqs, 0), Sq=np.stack(Sqs, 0))
        ims.append(check_inputs(P, im))
    r = run_bass_kernel_spmd(P.nc, ims, core_ids=list(range(8))).results
    y_prompt = np.concatenate([np.asarray(r[c]["out"])[0] for c in range(8)], 0)[None].astype(np.float32)
    y_sample = np.stack([np.asarray(r[c]["out"])[1] for c in range(8)], 0).astype(np.float32)
    return (y_prompt, y_sample)
```

```python
import math
from contextlib import ExitStack
import numpy as np
import ml_dtypes
import concourse.bass as bass
import concourse.mybir as mybir
from concourse.bass_utils import run_bass_kernel_spmd

F32 = mybir.dt.float32
BF16 = mybir.dt.bfloat16
AF = mybir.ActivationFunctionType
ALU = mybir.AluOpType
AX = mybir.AxisListType

D = 1024
NMETA = 16
DFF = 4096
ALPHA = 4 ** 0.25
LN_EPS = 1e-5
RMS_EPS = 1e-6
NH = 16


SEM_MAX = 24000


class Dep:
    __slots__ = ("w", "r")

    def __init__(self):
        self.w = None
        self.r = {}


class KB:
    def __init__(self, nc):
        self.nc = nc
        self.stack = ExitStack()
        self.raw = dict(pe=nc.tensor, act=nc.scalar, dve=nc.vector, pool=nc.gpsimd, sp=nc.sync)
        self.sem = {}
        self.cnt = {}
        self.seen = {e: {} for e in self.raw}
        self.semobj = []
        for e in ("pe", "act", "dve", "pool"):
            self.sem[e] = self._newsem("s_" + e)
            self.cnt[e] = 0
        self.dq = {}
        for q, n in (("sp", 20), ("act", 8), ("pool", 8)):
            self.dq[q] = dict(sems=[self._newsem(f"d_{q}{i}") for i in range(n)], vals=[0] * n, nxt=0)
        self.uid = 0

    def _newsem(self, name):
        s = self.stack.enter_context(self.nc.semaphore(name))
        self.semobj.append(s)
        return len(self.semobj) - 1

    def name(self, p):
        self.uid += 1
        return f"{p}{self.uid}"

    def sb(self, st, shape, dt, name="t"):
        return st.enter_context(self.nc.sbuf_tensor(self.name(name), list(shape), dt))

    def ps(self, st, shape, dt, name="p"):
        return st.enter_context(self.nc.psum_tensor(self.name(name), list(shape), dt))

    def _waits(self, eng, reads, writes, extra=None):
        need = {}

        def add(tok):
            if tok is None:
                return
            s, v, src = tok
            if src == "pe" and eng == "pe":
                return
            if need.get(s, 0) < v:
                need[s] = v

        for d in reads:
            add(d.w)
        for d in writes:
            add(d.w)
            for t in d.r.values():
                add(t)
        if extra:
            for t in extra:
                add(t)
        seen = self.seen[eng]
        for s, v in need.items():
            if seen.get(s, 0) < v:
                self.raw[eng].wait_ge(self.semobj[s], v)
                seen[s] = v

    def op(self, eng, fn, reads=(), writes=()):
        self._waits(eng, reads, writes)
        ins = fn(self.raw[eng])
        if self.cnt[eng] >= SEM_MAX:
            self.sem[eng] = self._newsem(self.name("s_" + eng))
            self.cnt[eng] = 0
        self.cnt[eng] += 1
        ins.then_inc(self.semobj[self.sem[eng]], 1)
        tok = (self.sem[eng], self.cnt[eng], eng)
        for d in reads:
            d.r[tok[0]] = tok
        for d in writes:
            d.w = tok
            d.r = {}
        return ins

    def dma(self, q, out, in_, reads=(), writes=(), **kw):
        dq = self.dq[q]
        i = dq["nxt"]
        dq["nxt"] = (i + 1) % len(dq["sems"])
        s = dq["sems"][i]
        extra = [(s, dq["vals"][i], "dma")] if dq["vals"][i] else None
        self._waits(q, reads, writes, extra)
        ins = self.raw[q].dma_start(out=out, in_=in_, **kw)
        dq["vals"][i] += 16
        ins.then_inc(self.semobj[s], 16)
        tok = (s, dq["vals"][i], "dma")
        for d in reads:
            d.r[s] = tok
        for d in writes:
            d.w = tok
            d.r = {}
        return ins

    def all_gather(self, in_ap, out_ap, reads=(), writes=()):
        if not hasattr(self, "ccsem"):
            self.ccsem = self._newsem("ccsem")
            self.ccval = 0
        self._waits("pool", reads, writes)
        ins = self.raw["pool"].collective_compute("AllGather", ALU.bypass, replica_groups=[list(range(8))],
                                                  ins=[in_ap.opt()], outs=[out_ap.opt()])
        self.ccval += 1
        ins.then_inc(self.semobj[self.ccsem], 1)
        tok = (self.ccsem, self.ccval, "cc")
        for d in reads:
            d.r[self.ccsem] = tok
        for d in writes:
            d.w = tok
            d.r = {}
        return ins

    def gather_rows(self, out, in_rows, idx, reads=(), writes=()):
        dq = self.dq["pool"]
        i = dq["nxt"]
        dq["nxt"] = (i + 1) % len(dq["sems"])
        s = dq["sems"][i]
        extra = [(s, dq["vals"][i], "dma")] if dq["vals"][i] else None
        self._waits("pool", reads, writes, extra)
        ins = self.raw["pool"].indirect_dma_start(out=out, out_offset=None, in_=in_rows,
                                                  in_offset=bass.IndirectOffsetOnAxis(ap=idx, axis=0))
        dq["vals"][i] += 16
        ins.then_inc(self.semobj[s], 16)
        tok = (s, dq["vals"][i], "dma")
        for d in reads:
            d.r[s] = tok
        for d in writes:
            d.w = tok
            d.r = {}
        return ins

    def barrier(self):
        toks = [(self.sem[e], self.cnt[e], e) for e in ("pe", "act", "dve", "pool") if self.cnt[e]]
        for q in self.dq.values():
            for s, v in zip(q["sems"], q["vals"]):
                if v:
                    toks.append((s, v, "dma"))
        if getattr(self, "ccval", 0):
            toks.append((self.ccsem, self.ccval, "cc"))
        for eng in ("pe", "act", "dve", "pool", "sp"):
            seen = self.seen[eng]
            for s, v, src in toks:
                if seen.get(s, 0) < v and not (s == self.sem.get(eng)):
                    self.raw[eng].wait_ge(self.semobj[s], v)
                    seen[s] = v

    def finish_wait(self):
        for q in self.dq.values():
            for s, v in zip(q["sems"], q["vals"]):
                if v and self.seen["sp"].get(s, 0) < v:
                    self.raw["sp"].wait_ge(self.semobj[s], v)
                    self.seen["sp"][s] = v


class Glob:
    pass


def setup_globals(kb, st):
    g = Glob()
    nc = kb.nc
    g.pall = kb.ps(st, [128, 8, 512], F32, "banks")
    g.psum = [g.pall[:, b, :] for b in range(8)]
    g.pd = [Dep() for _ in range(8)]
    g.ident_f = kb.sb(st, [128, 128], F32, "identf")
    g.ident_b = kb.sb(st, [128, 128], BF16, "identb")
    g.ident_d = Dep()
    g.ones_b = kb.sb(st, [128, 128], BF16, "onesb")
    g.ones_d = Dep()
    g.bk = -1
    return g


def load_ident(kb, g, ident_dram):
    kb.dma("sp", g.ident_f[:], ident_dram, writes=[g.ident_d])
    kb.op("dve", lambda e: e.tensor_copy(out=g.ident_b[:], in_=g.ident_f[:]), reads=[g.ident_d], writes=[g.ident_d])
    kb.op("pool", lambda e: e.memset(g.ones_b[:], 1.0), writes=[g.ones_d])


_rr = [0]


def cast_eng():
    _rr[0] += 1
    return ("dve", "pool", "act")[_rr[0] % 3]


def copy_op(kb, eng, out, in_, reads, writes):
    if eng == "act":
        return kb.op("act", lambda e: e.copy(out=out, in_=in_), reads=reads, writes=writes)
    return kb.op(eng, lambda e: e.tensor_copy(out=out, in_=in_), reads=reads, writes=writes)


def load_weight_bf16(kb, st_phase, dst, dst_dep, src, kc, ncols, stage_cols=2048):
    with ExitStack() as st:
        stg = [kb.sb(st, [128, stage_cols], F32, "wstg") for _ in range(3)]
        sd = [Dep() for _ in range(3)]
        i = 0
        for k in range(kc):
            for c0 in range(0, ncols, stage_cols):
                cn = min(stage_cols, ncols - c0)
                j = i % 3
                kb.dma("sp" if i % 2 == 0 else "pool", stg[j][:, :cn], src[k * 128:(k + 1) * 128, c0:c0 + cn], writes=[sd[j]])
                copy_op(kb, ("dve", "act")[i % 2], dst[:, k, c0:c0 + cn], stg[j][:, :cn], [sd[j]], [dst_dep])
                i += 1
        kb.barrier()


def load_bcast(kb, dst, dep, src_row):
    kb.dma("sp", dst, src_row.partition_broadcast(128) if len(src_row.shape) == 1 else src_row.broadcast_to([128, src_row.shape[-1]]), writes=[dep])


def layer_norm_tile(kb, r, rd, n, gt, bt, gbd, out, outd, small, smd, junk, junkd):
    s1, s2 = small[:, 0:1], small[:, 1:2]
    kb.op("act", lambda e: e.activation(out=junk[:n, :], in_=r[:n, :], func=AF.Identity, accum_out=s1[:n, :]), reads=[rd], writes=[junkd, smd])
    kb.op("act", lambda e: e.activation(out=junk[:n, :], in_=r[:n, :], func=AF.Square, accum_out=s2[:n, :]), reads=[rd], writes=[junkd, smd])
    mean, var, rstd = small[:, 2:3], small[:, 3:4], small[:, 4:5]
    kb.op("dve", lambda e: e.tensor_scalar(out=mean[:n, :], in0=s1[:n, :], scalar1=1.0 / D, scalar2=None, op0=ALU.mult), reads=[smd], writes=[smd])
    kb.op("dve", lambda e: e.tensor_tensor(out=var[:n, :], in0=mean[:n, :], in1=mean[:n, :], op=ALU.mult), reads=[smd], writes=[smd])
    kb.op("dve", lambda e: e.scalar_tensor_tensor(out=var[:n, :], in0=s2[:n, :], scalar=1.0 / D, in1=var[:n, :], op0=ALU.mult, op1=ALU.subtract), reads=[smd], writes=[smd])
    kb.op("act", lambda e: e.activation(out=rstd[:n, :], in_=var[:n, :], func=AF.Sqrt, bias=LN_EPS, scale=1.0), reads=[smd], writes=[smd])
    kb.op("dve", lambda e: e.reciprocal(out=rstd[:n, :], in_=rstd[:n, :]), reads=[smd], writes=[smd])
    kb.op("dve", lambda e: e.tensor_scalar(out=r[:n, :], in0=r[:n, :], scalar1=mean[:n, :], scalar2=rstd[:n, :], op0=ALU.subtract, op1=ALU.mult), reads=[smd, rd], writes=[rd])
    kb.op("pool", lambda e: e.tensor_tensor(out=r[:n, :], in0=r[:n, :], in1=gt[:n, :], op=ALU.mult), reads=[rd, gbd], writes=[rd])
    kb.op("pool", lambda e: e.tensor_tensor(out=out[:n, :], in0=r[:n, :], in1=bt[:n, :], op=ALU.add), reads=[rd, gbd], writes=[outd])


def mm(kb, out, lhsT, rhs, start, stop, reads, writes):
    return kb.op("pe", lambda e: e.matmul(out, lhsT=lhsT, rhs=rhs, start=start, stop=stop), reads, writes)


def tt(kb, eng, out, in0, in1, op, reads, writes):
    return kb.op(eng, lambda e: e.tensor_tensor(out=out, in0=in0, in1=in1, op=op), reads, writes)


def ts(kb, eng, out, in0, s1, s2, op0, op1, reads, writes):
    if s2 is None:
        return kb.op(eng, lambda e: e.tensor_scalar(out=out, in0=in0, scalar1=s1, scalar2=None, op0=op0), reads, writes)
    return kb.op(eng, lambda e: e.tensor_scalar(out=out, in0=in0, scalar1=s1, scalar2=s2, op0=op0, op1=op1), reads, writes)


def stt(kb, eng, out, in0, scalar, in1, op0, op1, reads, writes):
    return kb.op("dve", lambda e: e.scalar_tensor_tensor(out=out, in0=in0, scalar=scalar, in1=in1, op0=op0, op1=op1), reads, writes)


def act(kb, out, in_, func, reads, writes, **kw):
    return kb.op("act", lambda e: e.activation(out=out, in_=in_, func=func, **kw), reads, writes)


def nextbank(g):
    g.bk = (g.bk + 1) % 8
    return g.bk


def transpose_tile(kb, g, src, srcd, n, dstT, dstd, col0, kc=8):
    for k0 in range(0, kc, 4):
        b = nextbank(g)
        kn = min(4, kc - k0)
        pv = g.psum[b][:, :].rearrange("p (k t) -> p k t", k=4)
        for k in range(kn):
            kb.op("pe", lambda e, k=k: e.transpose(pv[:, k, :n], src[:n, (k0 + k) * 128:(k0 + k + 1) * 128], g.ident_f[:n, :n]),
                  reads=[srcd, g.ident_d], writes=[g.pd[b]])
        copy_op(kb, ("dve", "act")[b % 2], dstT[:, k0:k0 + kn, col0:col0 + n], pv[:, 0:kn, :n], [g.pd[b]], [dstd])


def phase_proj_ln(kb, g, tiles, fm, W, bias, lng, lnb):
    with ExitStack() as st:
        Wb = kb.sb(st, [128, 8, D], BF16, "Wb")
        Wd = Dep()
        load_weight_bf16(kb, st, Wb, Wd, W, 8, D)
        gt = kb.sb(st, [128, D], F32, "g")
        bt = kb.sb(st, [128, D], F32, "b")
        gbd = Dep()
        load_bcast(kb, gt[:], gbd, lng)
        load_bcast(kb, bt[:], gbd, lnb)
        if bias is not None:
            bi = kb.sb(st, [128, D], F32, "bias")
            load_bcast(kb, bi[:], gbd, bias)
        NB = 4
        hb = [kb.sb(st, [128, D], F32, "h") for _ in range(NB)]
        hd = [Dep() for _ in range(NB)]
        yT = [kb.sb(st, [128, 8, 128], BF16, "yT") for _ in range(NB)]
        yTd = [Dep() for _ in range(NB)]
        if not fm:
            yb = [kb.sb(st, [128, D], F32, "y") for _ in range(NB)]
            yd = [Dep() for _ in range(NB)]
        rb = [kb.sb(st, [128, D], F32, "r") for _ in range(NB)]
        rd = [Dep() for _ in range(NB)]
        junk = kb.sb(st, [128, D], F32, "junk")
        junkd = Dep()
        small = [kb.sb(st, [128, 8], F32, "small") for _ in range(NB)]
        smd = [Dep() for _ in range(NB)]
        for i, (hap, yap, oap, n) in enumerate(tiles):
            j = i % NB
            kb.dma("sp", hb[j][:n, :], hap, writes=[hd[j]])
            if fm:
                for qi, (ksl, src, dep) in enumerate(yap):
                    kb.dma(("pool", "sp")[qi % 2], yT[j][:, ksl, :n], src, reads=[dep] if dep is not None else [], writes=[yTd[j]])
            else:
                kb.dma("pool", yb[j][:n, :], yap, writes=[yd[j]])
                transpose_tile(kb, g, yb[j], yd[j], n, yT[j], yTd[j], 0)
            bks = (nextbank(g), nextbank(g))
            for half, bk in enumerate(bks):
                for k in range(8):
                    mm(kb, g.psum[bk][:n, :], yT[j][:, k, :n], Wb[:, k, half * 512:(half + 1) * 512], k == 0, k == 7,
                       [yTd[j], Wd], [g.pd[bk]])
            for half, bk in enumerate(bks):
                sl = slice(half * 512, (half + 1) * 512)
                if bias is not None:
                    tt(kb, "dve", rb[j][:n, sl], g.psum[bk][:n, :], bi[:n, sl], ALU.add, [g.pd[bk], gbd], [rd[j]])
                else:
                    copy_op(kb, "act", rb[j][:n, sl], g.psum[bk][:n, :], [g.pd[bk]], [rd[j]])
            stt(kb, "pool", rb[j][:n, :], hb[j][:n, :], ALPHA, rb[j][:n, :], ALU.mult, ALU.add, [hd[j], rd[j]], [rd[j]])
            layer_norm_tile(kb, rb[j], rd[j], n, gt, bt, gbd, rb[j], rd[j], small[j], smd[j], junk, junkd)
            kb.dma("sp", oap, rb[j][:n, :], reads=[rd[j]])
        kb.barrier()


def phase_mlp_ln(kb, g, tiles, W1, W2, lng, lnb):
    with ExitStack() as st:
        W1b = kb.sb(st, [128, 8, DFF], BF16, "W1b")
        W2b = kb.sb(st, [128, 32, D], BF16, "W2b")
        Wd = Dep()
        load_weight_bf16(kb, st, W1b, Wd, W1, 8, DFF)
        load_weight_bf16(kb, st, W2b, Wd, W2, 32, D, stage_cols=1024)
        gt = kb.sb(st, [128, D], F32, "g")
        bt = kb.sb(st, [128, D], F32, "b")
        gbd = Dep()
        load_bcast(kb, gt[:], gbd, lng)
        load_bcast(kb, bt[:], gbd, lnb)
        hb = [kb.sb(st, [128, D], F32, "h") for _ in range(4)]
        hd = [Dep() for _ in range(4)]
        hT = kb.sb(st, [128, 8, 512], BF16, "hT")
        hTd = Dep()
        uT = kb.sb(st, [128, 32, 512], BF16, "uT")
        uTd = [Dep() for _ in range(32)]
        rl = [kb.sb(st, [128, 512], F32, "relu") for _ in range(2)]
        rld = [Dep() for _ in range(2)]
        junk = kb.sb(st, [128, D], BF16, "junk")
        junkd = Dep()
        small = [kb.sb(st, [128, 8], F32, "small") for _ in range(4)]
        smd = [Dep() for _ in range(4)]
        for s0 in range(0, len(tiles), 4):
            grp = tiles[s0:s0 + 4]
            offs = []
            tot = 0
            for i, (iap, oap, n) in enumerate(grp):
                kb.dma("sp" if i % 2 == 0 else "pool", hb[i][:n, :], iap, writes=[hd[i]])
                offs.append(tot)
                tot += n
            for i, (iap, oap, n) in enumerate(grp):
                transpose_tile(kb, g, hb[i], hd[i], n, hT, hTd, offs[i])
            for j in range(32):
                bk = nextbank(g)
                for k in range(8):
                    mm(kb, g.psum[bk][:, :tot], W1b[:, k, j * 128:(j + 1) * 128], hT[:, k, :tot], k == 0, k == 7, [Wd, hTd], [g.pd[bk]])
                q = j % 2
                act(kb, rl[q][:, :tot], g.psum[bk][:, :tot], AF.Relu, [g.pd[bk]], [rld[q]])
                tt(kb, "pool" if j % 4 < 3 else "dve", uT[:, j, :tot], rl[q][:, :tot], rl[q][:, :tot], ALU.mult, [rld[q]], [uTd[j]])
            for i, (iap, oap, n) in enumerate(grp):
                bks = (nextbank(g), nextbank(g))
                for half, bk in enumerate(bks):
                    for j in range(32):
                        mm(kb, g.psum[bk][:n, :], uT[:, j, offs[i]:offs[i] + n], W2b[:, j, half * 512:(half + 1) * 512], j == 0, j == 31,
                           [uTd[j], Wd], [g.pd[bk]])
                for half, bk in enumerate(bks):
                    sl = slice(half * 512, (half + 1) * 512)
                    stt(kb, "dve", hb[i][:n, sl], hb[i][:n, sl], ALPHA, g.psum[bk][:n, :], ALU.mult, ALU.add, [hd[i], g.pd[bk]], [hd[i]])
                layer_norm_tile(kb, hb[i], hd[i], n, gt, bt, gbd, hb[i], hd[i], small[i], smd[i], junk, junkd)
                kb.dma("sp", oap, hb[i][:n, :], reads=[hd[i]])
        kb.barrier()


def rms_rstd(kb, src, srcd, n, width, small, smd, junk, junkd, col):
    ss, rs = small[:, col:col + 1], small[:, col + 1:col + 2]
    act(kb, junk[:n, :width], src[:n, :width], AF.Square, [srcd], [junkd, smd], accum_out=ss[:n, :])
    act(kb, rs[:n, :], ss[:n, :], AF.Sqrt, [smd], [smd], bias=RMS_EPS, scale=1.0 / width)
    kb.op("dve", lambda e: e.reciprocal(out=rs[:n, :], in_=rs[:n, :]), [smd], [smd])
    return rs


def phase_qkv(kb, g, seqs, wqa, qg, WqH, WqS, wkva, kvg):
    with ExitStack() as st:
        wqa_b = kb.sb(st, [128, 8, 384], BF16, "wqa")
        wkva_b = kb.sb(st, [128, 8, 288], BF16, "wkva")
        wqh_b = kb.sb(st, [128, 3, NH * 128], BF16, "wqh")
        wqs_b = kb.sb(st, [128, 3, NH * 32], BF16, "wqs")
        Wd = Dep()
        load_weight_bf16(kb, st, wqa_b, Wd, wqa, 8, 384)
        load_weight_bf16(kb, st, wkva_b, Wd, wkva, 8, 288)
        load_weight_bf16(kb, st, wqh_b, Wd, WqH, 3, NH * 128)
        load_weight_bf16(kb, st, wqs_b, Wd, WqS, 3, NH * 32)
        qgt = kb.sb(st, [128, 384], F32, "qg")
        kvgt = kb.sb(st, [128, 256], F32, "kvg")
        gd = Dep()
        load_bcast(kb, qgt[:], gd, qg)
        load_bcast(kb, kvgt[:], gd, kvg)
        hb = [kb.sb(st, [128, D], F32, "h") for _ in range(4)]
        hd = [Dep() for _ in range(4)]
        hT = kb.sb(st, [128, 8, 512], BF16, "hT")
        hTd = Dep()
        cq = [kb.sb(st, [128, 384], F32, "cq") for _ in range(2)]
        cqd = [Dep() for _ in range(2)]
        cqT = kb.sb(st, [128, 3, 512], BF16, "cqT")
        cqTd = Dep()
        kvr = [kb.sb(st, [128, 288], F32, "kvr") for _ in range(2)]
        kvrd = [Dep() for _ in range(2)]
        kvo = [kb.sb(st, [128, 288], F32, "kvo") for _ in range(2)]
        kvod = [Dep() for _ in range(2)]
        cst = [kb.sb(st, [128, 32], F32, "cs") for _ in range(2)]
        csd = [Dep() for _ in range(2)]
        tmp = [kb.sb(st, [128, 64], F32, "tmp") for _ in range(2)]
        tmpd = [Dep() for _ in range(2)]
        junk = kb.sb(st, [128, 384], F32, "junk")
        junkd = Dep()
        small = [kb.sb(st, [128, 8], F32, "small") for _ in range(2)]
        smd = [Dep() for _ in range(2)]
        Ct = kb.sb(st, [32, 2048], F32, "C")
        St = kb.sb(st, [32, 2048], F32, "S")
        CSd = Dep()
        qsw = [kb.sb(st, [32, 512], F32, "qsw") for _ in range(2)]
        qswd = [Dep() for _ in range(2)]
        qo = [kb.sb(st, [128, 512], BF16, "qo") for _ in range(2)]
        qod = [Dep() for _ in range(2)]
        it = 0
        for sq in seqs:
            if not sq.get("kv_only"):
                kb.dma("sp", Ct[:], sq["CS"][0], writes=[CSd])
                kb.dma("sp", St[:], sq["CS"][1], writes=[CSd])
            tiles = sq["tiles"]
            for s0 in range(0, len(tiles), 4):
                grp = tiles[s0:s0 + 4]
                offs, tot = [], 0
                for i, (hap, kvap, csap, n) in enumerate(grp):
                    kb.dma("sp" if i % 2 == 0 else "pool", hb[i][:n, :], hap, writes=[hd[i]])
                    offs.append(tot)
                    tot += n
                for i, (hap, kvap, csap, n) in enumerate(grp):
                    transpose_tile(kb, g, hb[i], hd[i], n, hT, hTd, offs[i])
                is_main = (tot == 512) and not sq.get("kv_only")
                for i, (hap, kvap, csap, n) in enumerate(grp):
                    j = it % 2
                    it += 1
                    kb.dma("pool", cst[j][:n, :], csap, writes=[csd[j]])
                    bk = nextbank(g)
                    for k in range(8):
                        mm(kb, g.psum[bk][:n, :288], hT[:, k, offs[i]:offs[i] + n], wkva_b[:, k, :], k == 0, k == 7, [hTd, Wd], [g.pd[bk]])
                    copy_op(kb, "act", kvr[j][:n, :], g.psum[bk][:n, :288], [g.pd[bk]], [kvrd[j]])
                    rs = rms_rstd(kb, kvr[j], kvrd[j], n, 256, small[j], smd[j], junk, junkd, 0)
                    stt(kb, "dve", kvo[j][:n, 0:256], kvr[j][:n, 0:256], rs[:n, :], kvgt[:n, :], ALU.mult, ALU.mult, [kvrd[j], smd[j], gd], [kvod[j]])
                    x1, x2 = kvr[j][:n, 256:272], kvr[j][:n, 272:288]
                    co, si = cst[j][:n, 0:16], cst[j][:n, 16:32]
                    t = tmp[j]
                    tt(kb, "pool", t[:n, 0:16], x1, co, ALU.mult, [kvrd[j], csd[j]], [tmpd[j]])
                    tt(kb, "pool", t[:n, 16:32], x2, si, ALU.mult, [kvrd[j], csd[j]], [tmpd[j]])
                    tt(kb, "pool", t[:n, 32:48], x1, si, ALU.mult, [kvrd[j], csd[j]], [tmpd[j]])
                    tt(kb, "pool", t[:n, 48:64], x2, co, ALU.mult, [kvrd[j], csd[j]], [tmpd[j]])
                    tt(kb, "dve", kvo[j][:n, 256:272], t[:n, 0:16], t[:n, 16:32], ALU.subtract, [tmpd[j]], [kvod[j]])
                    tt(kb, "dve", kvo[j][:n, 272:288], t[:n, 32:48], t[:n, 48:64], ALU.add, [tmpd[j]], [kvod[j]])
                    kb.dma("sp", kvap, kvo[j][:n, :], reads=[kvod[j]])
                    if not is_main:
                        continue
                    bk = nextbank(g)
                    for k in range(8):
                        mm(kb, g.psum[bk][:n, :384], hT[:, k, offs[i]:offs[i] + n], wqa_b[:, k, :], k == 0, k == 7, [hTd, Wd], [g.pd[bk]])
                    copy_op(kb, "act", cq[j][:n, :], g.psum[bk][:n, :384], [g.pd[bk]], [cqd[j]])
                    rs = rms_rstd(kb, cq[j], cqd[j], n, 384, small[j], smd[j], junk, junkd, 2)
                    stt(kb, "dve", cq[j][:n, :], cq[j][:n, :], rs[:n, :], qgt[:n, :], ALU.mult, ALU.mult, [cqd[j], smd[j], gd], [cqd[j]])
                    transpose_tile(kb, g, cq[j], cqd[j], n, cqT, cqTd, offs[i], kc=3)
                if not is_main:
                    continue
                q0 = (s0 // 4) * 512
                for h in range(NH):
                    j = h % 2
                    bka, bkb = nextbank(g), nextbank(g)
                    for k in range(3):
                        mm(kb, g.psum[bka][:, :], wqh_b[:, k, h * 128:(h + 1) * 128], cqT[:, k, :], k == 0, k == 2, [Wd, cqTd], [g.pd[bka]])
                    for k in range(3):
                        mm(kb, g.psum[bkb][:32, :], wqs_b[:, k, h * 32:(h + 1) * 32], cqT[:, k, :], k == 0, k == 2, [Wd, cqTd], [g.pd[bkb]])
                    tt(kb, "dve", qsw[j][:, :], g.psum[bkb][:32, :], St[:, q0:q0 + 512], ALU.mult, [g.pd[bkb], CSd], [qswd[j]])
                    rope_q(kb, g, qo[j], qod[j], bka, qsw[j], qswd[j], Ct, CSd, q0, st, small)
                    copy_op(kb, "act", qo[j][32:64, :], g.psum[bka][32:64, :], [g.pd[bka]], [qod[j]])
                    copy_op(kb, "act", qo[j][64:128, :], g.psum[bka][64:128, :], [g.pd[bka]], [qod[j]])
                    kb.dma("pool", sq["qt"](h, q0), qo[j][:, :], reads=[qod[j]])
        kb.barrier()


_ropetmp = {}


def rope_q(kb, g, qo, qod, bka, qsw, qswd, Ct, CSd, q0, st, small):
    key = id(st)
    if key not in _ropetmp:
        _ropetmp[key] = (kb.sb(st, [32, 512], F32, "rq"), Dep())
    t, td = _ropetmp[key]
    tt(kb, "dve", t[:, :], g.psum[bka][0:32, :], Ct[:, q0:q0 + 512], ALU.mult, [g.pd[bka], CSd], [td])
    tt(kb, "pool", qo[0:32, :], t[:, :], qsw[:, :], ALU.add, [td, qswd], [qod])


QK_SCALE = 96 ** -0.5


def phase_attn(kb, g, seqs, WkH, WvH):
    NKmax = max(sum(n for _, n in sq["kchunks"]) for sq in seqs)
    NCH = max(len(sq["kchunks"]) for sq in seqs)
    with ExitStack() as st:
        wk_b = kb.sb(st, [128, 2, NH * 128], BF16, "wk")
        wv_b = kb.sb(st, [128, 2, NH * 64], BF16, "wv")
        Wd = Dep()
        load_weight_bf16(kb, st, wk_b, Wd, WkH, 2, NH * 128)
        load_weight_bf16(kb, st, wv_b, Wd, WvH, 2, NH * 64, stage_cols=1024)
        ckvT = kb.sb(st, [128, 3, NKmax], BF16, "ckvT")
        ckvTd = Dep()
        KT = kb.sb(st, [128, NKmax], BF16, "KT")
        KTd = Dep()
        V = kb.sb(st, [128, NCH, 66], BF16, "V")
        Vd = Dep()
        QT = [kb.sb(st, [128, 2048], BF16, "QT") for _ in range(2)]
        QTd = [Dep() for _ in range(2)]
        P = [kb.sb(st, [128, 1024], BF16, "P") for _ in range(2)]
        Pd = [Dep() for _ in range(2)]
        oT = kb.sb(st, [128, 1024], F32, "oT")
        oTd = Dep()
        osm = [kb.sb(st, [128, 4, 64], F32, "osm") for _ in range(2)]
        osmd = [Dep() for _ in range(2)]
        rec = [kb.sb(st, [128, 4, 1], F32, "rec") for _ in range(2)]
        recd = [Dep() for _ in range(2)]
        kvin = [kb.sb(st, [128, 288], F32, "kvin") for _ in range(2)]
        kvind = [Dep() for _ in range(2)]
        kb.op("pool", lambda e: e.memset(V[:, :, 64:66], 1.0), [], [Vd])
        for sq in seqs:
            chunks = sq["kchunks"]
            NK = sum(n for _, n in chunks)
            coff = []
            c0 = 0
            for ci, (kvap, n) in enumerate(chunks):
                j = ci % 2
                kb.dma("sp" if ci % 2 == 0 else "pool", kvin[j][:n, :], kvap, writes=[kvind[j]])
                b = nextbank(g)
                pv = g.psum[b].rearrange("p (k t) -> p k t", k=4)
                for k, w in ((0, 128), (1, 128), (2, 32)):
                    kb.op("pe", lambda e, k=k, w=w: e.transpose(pv[:w, k, :n], kvin[j][:n, k * 128:k * 128 + w], g.ident_f[:n, :n]),
                          [kvind[j], g.ident_d], [g.pd[b]])
                copy_op(kb, "dve", ckvT[:, 0:2, c0:c0 + n], pv[:, 0:2, :n], [g.pd[b]], [ckvTd])
                copy_op(kb, "act", ckvT[0:32, 2, c0:c0 + n], pv[0:32, 2, :n], [g.pd[b]], [ckvTd])
                coff.append(c0)
                c0 += n
            for h in range(NH):
                qj = h % 2
                kb.dma("sp", QT[qj][:, :], sq["qt"](h), writes=[QTd[qj]])
                for bi, k0 in enumerate(range(0, NK, 512)):
                    kn = min(512, NK - k0)
                    b = 6 + bi % 2
                    mm(kb, g.psum[b][:, :kn], wk_b[:, 0, h * 128:(h + 1) * 128], ckvT[:, 0, k0:k0 + kn], True, False, [Wd, ckvTd], [g.pd[b]])
                    mm(kb, g.psum[b][:, :kn], wk_b[:, 1, h * 128:(h + 1) * 128], ckvT[:, 1, k0:k0 + kn], False, False, [Wd, ckvTd], [g.pd[b]])
                    mm(kb, g.psum[b][:, :kn], g.ident_b[0:32, :], ckvT[0:32, 2, k0:k0 + kn], False, True, [g.ident_d, ckvTd], [g.pd[b]])
                    copy_op(kb, ("dve", "pool")[bi % 2] if False else "dve", KT[:, k0:k0 + kn], g.psum[b][:, :kn], [g.pd[b]], [KTd])
                for gi, cg in enumerate(range(0, len(chunks), 8)):
                    cn = min(8, len(chunks) - cg)
                    b = 6 + gi % 2
                    for ci in range(cn):
                        n = chunks[cg + ci][1]
                        o = coff[cg + ci]
                        for k in range(2):
                            mm(kb, g.psum[b][:n, ci * 64:(ci + 1) * 64], ckvT[:, k, o:o + n], wv_b[:, k, h * 64:(h + 1) * 64], k == 0, k == 1,
                               [ckvTd, Wd], [g.pd[b]])
                    copy_op(kb, "act", V[:, cg:cg + cn, 0:64], g.psum[b][:, :cn * 64].rearrange("p (c d) -> p c d", d=64), [g.pd[b]], [Vd])
                for qsb in range(2):
                    for ci, (kvap, n) in enumerate(chunks):
                        o = coff[ci]
                        sb0 = 2 + 2 * (ci % 2)
                        pj = ci % 2
                        for i in range(2):
                            mm(kb, g.psum[sb0 + i][:n, :], KT[:, o:o + n], QT[qj][:, qsb * 1024 + i * 512:qsb * 1024 + (i + 1) * 512], True, True,
                               [KTd, QTd[qj]], [g.pd[sb0 + i]])
                        act(kb, P[pj][:n, :].rearrange("p (a b) -> p a b", a=2), g.pall[:n, sb0:sb0 + 2, :], AF.Exp,
                            [g.pd[sb0], g.pd[sb0 + 1]], [Pd[pj]], scale=QK_SCALE)
                        for i in range(2):
                            mm(kb, g.psum[i][:65, :], V[:n, ci, 0:65], P[pj][:n, i * 512:(i + 1) * 512], ci == 0, ci == len(chunks) - 1,
                               [Vd, Pd[pj]], [g.pd[i]])
                    copy_op(kb, "dve", oT[:65, :].rearrange("p (a b) -> p a b", a=2), g.pall[:65, 0:2, :], [g.pd[0], g.pd[1]], [oTd])
                    for half in range(2):
                        b = 6 + half
                        oj = half
                        pv = g.psum[b][:, 0:4 * 65].rearrange("p (t c) -> p t c", c=65)
                        for t in range(4):
                            q0 = half * 512 + t * 128
                            kb.op("pe", lambda e, t=t, q0=q0: e.transpose(pv[:, t, :], oT[:65, q0:q0 + 128], g.ident_f[:65, :65]),
                                  [oTd, g.ident_d], [g.pd[b]])
                        kb.op("dve", lambda e: e.reciprocal(out=rec[oj][:, :, :], in_=pv[:, :, 64:65]), [g.pd[b]], [recd[oj]])
                        tt(kb, "dve", osm[oj][:, :, :], pv[:, :, 0:64], rec[oj][:, :, :].broadcast_to([128, 4, 64]), ALU.mult,
                           [g.pd[b], recd[oj]], [osmd[oj]])
                        kb.dma("pool", sq["o"](qsb, half, h), osm[oj][:, :, :], reads=[osmd[oj]])
        kb.barrier()


XC = 2124


def phase_conf(kb, g, seqs, w_conf, cols_ap):
    with ExitStack() as st:
        wb = kb.sb(st, [128, 8, 1024], BF16, "wconf")
        Wd = Dep()
        load_weight_bf16(kb, st, wb, Wd, w_conf, 8, 1024)
        cols = kb.sb(st, [128, 20 + 124], F32, "cols")
        cd = Dep()
        kb.dma("sp", cols[:], cols_ap, writes=[cd])
        Dg = kb.sb(st, [128, 4, 31, 128], BF16, "Dg")
        Dgd = Dep()
        for j in range(4):
            for k in range(31):
                ts(kb, ("dve", "pool")[k % 2], Dg[:, j, k, :], g.ident_f[:, :], cols[:, 20 + j * 31 + k:20 + j * 31 + k + 1], None, ALU.mult, None,
                   [g.ident_d, cd], [Dgd])
        xin = [kb.sb(st, [128, D], F32, "xin") for _ in range(2)]
        xind = [Dep() for _ in range(2)]
        xT = kb.sb(st, [128, 8, XC], BF16, "xT")
        xTd = Dep()
        hT = kb.sb(st, [128, 4, XC], BF16, "hT")
        hTd = Dep()
        mask = kb.sb(st, [128, XC], F32, "mask")
        maskd = Dep()
        sg = [kb.sb(st, [128, 512], F32, "sg") for _ in range(2)]
        sgd = [Dep() for _ in range(2)]
        cc = kb.sb(st, [128, 4, 512], F32, "cc")
        ccd = Dep()
        cb = kb.sb(st, [128, 4, 512], BF16, "cb")
        cbd = Dep()
        sq = kb.sb(st, [128, 4, 512], BF16, "sq")
        sqd = Dep()
        mean = kb.sb(st, [128, 512], F32, "mean")
        rstd = kb.sb(st, [128, 512], F32, "rstd")
        std = Dep()
        yt = [kb.sb(st, [128, 512], F32, "yt") for _ in range(2)]
        ytd = [Dep() for _ in range(2)]
        yo = [kb.sb(st, [128, 512], BF16, "yo") for _ in range(2)]
        yod = [Dep() for _ in range(2)]
        for s_ in seqs:
            NC = s_.get("ncols", XC)
            kb.dma("sp", mask[:, :NC], s_["mask"].broadcast_to([128, NC]), writes=[maskd])
            for ti, t0 in enumerate(range(0, NC, 128)):
                n = min(128, NC - t0)
                j = ti % 2
                kb.dma("sp" if ti % 2 == 0 else "pool", xin[j][:n, :], s_["x"][t0:t0 + n, :], writes=[xind[j]])
                transpose_tile(kb, g, xin[j], xind[j], n, xT, xTd, t0)
            for bi, c0 in enumerate(range(0, NC, 512)):
                cn = min(512, NC - c0)
                for j in range(4):
                    ba, bg = nextbank(g), nextbank(g)
                    for k in range(8):
                        mm(kb, g.psum[ba][:, :cn], wb[:, k, j * 128:(j + 1) * 128], xT[:, k, c0:c0 + cn], k == 0, k == 7, [Wd, xTd], [g.pd[ba]])
                    for k in range(8):
                        mm(kb, g.psum[bg][:, :cn], wb[:, k, 512 + j * 128:512 + (j + 1) * 128], xT[:, k, c0:c0 + cn], k == 0, k == 7, [Wd, xTd], [g.pd[bg]])
                    q = j % 2
                    act(kb, sg[q][:, :cn], g.psum[bg][:, :cn], AF.Sigmoid, [g.pd[bg], cd], [sgd[q]], bias=cols[:, 4 + j:5 + j], scale=1.0)
                    stt(kb, "dve", sg[q][:, :cn], g.psum[ba][:, :cn], cols[:, j:j + 1], sg[q][:, :cn], ALU.add, ALU.mult, [g.pd[ba], cd, sgd[q]], [sgd[q]])
                    tt(kb, "pool", hT[:, j, c0:c0 + cn], sg[q][:, :cn], mask[:, c0:c0 + cn], ALU.mult, [sgd[q], maskd], [hTd])
            blocks = s_.get("blocks") or ([(15, 16)] + [(61 + 512 * i, 512) for i in range(4)])
            for bi, (c0, cn) in enumerate(blocks):
                for j in range(4):
                    b = nextbank(g)
                    for k in range(31):
                        mm(kb, g.psum[b][:, :cn], Dg[:, j, k, :], hT[:, j, c0 + k - 15:c0 + k - 15 + cn], k == 0, k == 30, [Dgd, hTd], [g.pd[b]])
                    act(kb, cc[:, j, :cn], g.psum[b][:, :cn], AF.Identity, [g.pd[b], cd], [ccd], bias=cols[:, 8 + j:9 + j], scale=1.0)
                    copy_op(kb, "pool", cb[:, j, :cn], cc[:, j, :cn], [ccd], [cbd])
                    tt(kb, "dve", sq[:, j, :cn], cc[:, j, :cn], cc[:, j, :cn], ALU.mult, [ccd], [sqd])
                b1, b2 = nextbank(g), nextbank(g)
                for j in range(4):
                    mm(kb, g.psum[b1][:, :cn], g.ones_b[:, :], cb[:, j, :cn], j == 0, j == 3, [g.ones_d, cbd], [g.pd[b1]])
                for j in range(4):
                    mm(kb, g.psum[b2][:, :cn], g.ones_b[:, :], sq[:, j, :cn], j == 0, j == 3, [g.ones_d, sqd], [g.pd[b2]])
                act(kb, mean[:, :cn], g.psum[b1][:, :cn], AF.Copy, [g.pd[b1]], [std], scale=1.0 / 512)
                tt(kb, "pool", rstd[:, :cn], mean[:, :cn], mean[:, :cn], ALU.mult, [std], [std])
                stt(kb, "dve", rstd[:, :cn], g.psum[b2][:, :cn], 1.0 / 512, rstd[:, :cn], ALU.mult, ALU.subtract, [g.pd[b2], std], [std])
                act(kb, rstd[:, :cn], rstd[:, :cn], AF.Sqrt, [std], [std], bias=LN_EPS, scale=1.0)
                kb.op("dve", lambda e: e.reciprocal(out=rstd[:, :cn], in_=rstd[:, :cn]), [std], [std])
                for j in range(4):
                    q = j % 2
                    tt(kb, "pool", yt[q][:, :cn], cc[:, j, :cn], mean[:, :cn], ALU.subtract, [ccd, std], [ytd[q]])
                    tt(kb, "dve", yt[q][:, :cn], yt[q][:, :cn], rstd[:, :cn], ALU.mult, [ytd[q], std], [ytd[q]])
                    ts(kb, "dve", yt[q][:, :cn], yt[q][:, :cn], cols[:, 12 + j:13 + j], cols[:, 16 + j:17 + j], ALU.mult, ALU.add, [ytd[q], cd], [ytd[q]])
                    act(kb, yo[q][:, :cn], yt[q][:, :cn], AF.Silu, [ytd[q]], [yod[q]])
                    kb.dma("sp", s_["out"](j, bi, cn), yo[q][:, :cn], reads=[yod[q]])
        kb.barrier()


I32 = mybir.dt.int32
TWO_PI = 2.0 * math.pi


class FCfg:
    def __init__(self, L, rows, N1, nq, CB):
        self.L, self.rows, self.N1, self.nq, self.CB = L, rows, N1, nq, CB
        self.N2 = 86 * nq
        self.N = N1 * self.N2
        self.NF = N1 // 2 + 1
        assert self.N >= 2 * L - 1 and rows * self.N2 >= L


CFG_P = FCfg(16400, 64, 128, 3, 8)
CFG_S = FCfg(2064, 24, 48, 1, 32)


def fft_tables(cfg):
    N1, N2, N, rows, nq, NF = cfg.N1, cfg.N2, cfg.N, cfg.rows, cfg.nq, cfg.NF
    n1 = np.arange(rows)[:, None].astype(np.float64)
    k1 = np.arange(NF)[None, :].astype(np.float64)
    a = 2 * np.pi * n1 * k1 / N1
    F1 = np.concatenate([np.cos(a), -np.sin(a)], 1)
    n2 = np.arange(N2)[:, None].astype(np.float64)
    a = 2 * np.pi * n2 * k1 / N
    tw = np.stack([np.cos(a), -np.sin(a)], 1)
    tw = tw.reshape(nq, 86, 2, NF).transpose(1, 0, 2, 3)
    m = np.arange(N2)[None, :].astype(np.float64)
    a = 2 * np.pi * n2 * m / N2
    F2 = np.stack([np.cos(a), -np.sin(a), np.sin(a)], 0)
    F2 = F2.reshape(3, nq, 86, N2).transpose(2, 0, 1, 3)
    kk = np.arange(NF)[:, None].astype(np.float64)
    a = 2 * np.pi * kk * np.arange(N2)[None, :] / N
    twc = np.stack([np.cos(a), np.sin(a)], 1)
    a = 2 * np.pi * kk * np.arange(rows)[None, :] / N1
    wgt = np.full((NF, 1), 2.0)
    wgt[0, 0] = 1.0
    wgt[NF - 1, 0] = 1.0
    G1 = np.stack([wgt * np.cos(a) / N, -wgt * np.sin(a) / N], 1)
    bf = ml_dtypes.bfloat16
    return dict(F1=F1.astype(np.float32).astype(bf), tw=np.ascontiguousarray(tw).astype(np.float32),
                F2=np.ascontiguousarray(F2).astype(np.float32).astype(bf), twc=twc.astype(np.float32),
                G1=G1.astype(np.float32).astype(bf))


class FTab:
    pass


def fft_load_tables(kb, st, cfg, tabs):
    t = FTab()
    t.d = Dep()
    t.F1 = kb.sb(st, [cfg.rows, 2 * cfg.NF], BF16, "F1")
    t.tw = kb.sb(st, [86, cfg.nq, 2, cfg.NF], F32, "tw")
    t.F2 = kb.sb(st, [86, 3, cfg.nq, cfg.N2], BF16, "F2")
    t.twc = kb.sb(st, [cfg.NF, 2, cfg.N2], F32, "twc")
    t.G1 = kb.sb(st, [cfg.NF, 2, cfg.rows], BF16, "G1")
    for nm in ("F1", "tw", "F2", "twc", "G1"):
        kb.dma("sp", getattr(t, nm)[:], tabs[nm], writes=[t.d])
    return t


class FBuf:
    pass


def fft_alloc(kb, st, cfg, nsets=1):
    CB, nq, N1, N2, rows = cfg.CB, cfg.nq, cfg.NF, cfg.N2, cfg.rows
    E = CB * nq * N1
    E2 = CB * N2
    tn = max(E, E2)
    Ab = kb.sb(st, [86, CB * nq, 2, N1], BF16, "Ab")
    Abd = Dep()
    Xs = kb.sb(st, [86, CB * nq, 2, N1], F32, "Xs")
    Xsd = Dep()
    t = [kb.sb(st, [128, tn], F32, "ft") for _ in range(4)]
    td = [Dep() for _ in range(4)]
    sets = []
    for _ in range(nsets):
        b = FBuf()
        b.src_f = kb.sb(st, [rows, CB, N2], F32, "srcf")
        b.src_fd = Dep()
        b.src_b = kb.sb(st, [rows, CB, N2], BF16, "srcb")
        b.src_bd = Dep()
        b.As = kb.sb(st, [86, CB * nq, 2, N1], F32, "As")
        b.Asd = Dep()
        b.Ab, b.Abd, b.Xs, b.Xsd, b.t, b.td = Ab, Abd, Xs, Xsd, t, td
        sets.append(b)
    return sets if nsets > 1 else sets[0]


import os
CMUL_ENG = os.environ.get("CMUL_ENG", "dve,dve,dve,dve,dve,dve").split(",")


def cmul_batched(kb, cfg, b, P, shape, Are, Aim, Br, Bi, out_re, out_im, rdeps, wdep, conj=False):
    n = int(np.prod(shape))
    pat = {2: "p (a b) -> p a b", 3: "p (a b c) -> p a b c"}[len(shape)]
    kw = dict(zip("abc", shape))
    kw.pop("a")
    tv = [b.t[i][:P, :n].rearrange(pat, **kw) for i in range(4)]
    e = CMUL_ENG
    tt(kb, e[0], tv[0], Are, Br, ALU.mult, rdeps, [b.td[0]])
    tt(kb, e[1], tv[1], Aim, Bi, ALU.mult, rdeps, [b.td[1]])
    tt(kb, e[2], tv[2], Are, Bi, ALU.mult, rdeps, [b.td[2]])
    tt(kb, e[3], tv[3], Aim, Br, ALU.mult, rdeps, [b.td[3]])
    tt(kb, e[4], out_re, tv[0], tv[1], ALU.subtract, [b.td[0], b.td[1]], [wdep])
    tt(kb, e[5], out_im, tv[2], tv[3], ALU.add, [b.td[2], b.td[3]], [wdep])


def fft_s1(kb, g, cfg, tb, b, cb):
    nq, N1, N2, rows = cfg.nq, cfg.NF, cfg.N2, cfg.rows
    per = 512 // (2 * N1)
    tot = cb * nq
    for i0 in range(0, tot, per):
        cnt = min(per, tot - i0)
        bk = nextbank(g)
        for i in range(i0, i0 + cnt):
            c, q = divmod(i, nq)
            mm(kb, g.psum[bk][:86, (i - i0) * 2 * N1:(i - i0 + 1) * 2 * N1], b.src_b[:rows, c, q * 86:(q + 1) * 86], tb.F1[:rows, :], True, True,
               [b.src_bd, tb.d], [g.pd[bk]])
        copy_op(kb, "act", b.As[:, i0:i0 + cnt, :, :], g.psum[bk][:86, :cnt * 2 * N1].rearrange("p (i r k) -> p i r k", r=2, k=N1), [g.pd[bk]], [b.Asd])


def fft_s2(kb, g, cfg, tb, b, cb):
    nq, N1, N2, rows = cfg.nq, cfg.NF, cfg.N2, cfg.rows
    per = 512 // (2 * N1)
    tot = cb * nq
    Av = b.As[:, :tot, :, :].rearrange("p (c q) r k -> p c q r k", q=nq)
    Abv = b.Ab[:, :tot, :, :].rearrange("p (c q) r k -> p c q r k", q=nq)
    twr = tb.tw[:, :, 0, :].unsqueeze(1).broadcast_to([86, cb, nq, N1])
    twi = tb.tw[:, :, 1, :].unsqueeze(1).broadcast_to([86, cb, nq, N1])
    cmul_batched(kb, cfg, b, 86, (cb, nq, N1), Av[:, :, :, 0, :], Av[:, :, :, 1, :], twr, twi, Abv[:, :, :, 0, :], Abv[:, :, :, 1, :],
                 [b.Asd, tb.d], b.Abd)
    for i0 in range(0, tot, per):
        cnt = min(per, tot - i0)
        bk = nextbank(g)
        for i in range(i0, i0 + cnt):
            c, p = divmod(i, nq)
            reg = g.psum[bk][:86, (i - i0) * 2 * N1:(i - i0 + 1) * 2 * N1]
            for q in range(nq):
                blk = slice(p * 86, (p + 1) * 86)
                mm(kb, reg, tb.F2[:, 0, q, blk], b.Ab[:, c * nq + q, :, :].rearrange("p r k -> p (r k)"), q == 0, False, [tb.d, b.Abd], [g.pd[bk]])
                mm(kb, reg[:, 0:N1], tb.F2[:, 2, q, blk], b.Ab[:, c * nq + q, 1, :], False, False, [tb.d, b.Abd], [g.pd[bk]])
                mm(kb, reg[:, N1:2 * N1], tb.F2[:, 1, q, blk], b.Ab[:, c * nq + q, 0, :], False, q == nq - 1, [tb.d, b.Abd], [g.pd[bk]])
        copy_op(kb, "act", b.Xs[:, i0:i0 + cnt, :, :], g.psum[bk][:86, :cnt * 2 * N1].rearrange("p (i r k) -> p i r k", r=2, k=N1), [g.pd[bk]], [b.Xsd])


def fft_fwd(kb, g, cfg, tb, b, cb):
    fft_s1(kb, g, cfg, tb, b, cb)
    fft_s2(kb, g, cfg, tb, b, cb)


def pipeline2(items, stage_a, stage_b, depth=2):
    if depth < 2:
        for it in items:
            stage_a(it)
            stage_b(it)
        return
    prev = None
    for it in items:
        stage_a(it)
        if prev is not None:
            stage_b(prev)
        prev = it
    if prev is not None:
        stage_b(prev)


def fft_layout_dma(kb, q, cfg, tile, tiled, dram2d, c0, cb, to_sbuf):
    L, N2, rows = cfg.L, cfg.N2, cfg.rows
    full = L // N2
    rem = L - full * N2
    dv = dram2d[c0:c0 + cb, 0:full * N2].rearrange("c (a b) -> a c b", b=N2)
    if to_sbuf:
        kb.dma(q, tile[:full, :cb, :], dv, writes=[tiled])
        if rem:
            kb.dma(q, tile[full:full + 1, :cb, :rem], dram2d[c0:c0 + cb, full * N2:L].unsqueeze(0), writes=[tiled])
    else:
        kb.dma(q, dv, tile[:full, :cb, :], reads=[tiled])
        if rem:
            kb.dma(q, dram2d[c0:c0 + cb, full * N2:L].unsqueeze(0), tile[full:full + 1, :cb, :rem], reads=[tiled])


def phase_hy_conv(kb, g, cfg, tabs, taps, Hs, seqs, dskip):
    CB, nq, N1, N2, rows, L = cfg.CB, cfg.nq, cfg.NF, cfg.N2, cfg.rows, cfg.L
    with ExitStack() as st:
        tb = fft_load_tables(kb, st, cfg, tabs)
        bs = [fft_alloc(kb, st, cfg, nsets=1)]
        for b in bs:
            kb.op("pool", lambda e, b=b: e.memset(b.src_f[:, :, :], 0.0), [], [b.src_fd])
        X0 = kb.sb(st, [86, CB * nq, 2, N1], F32, "X0")
        X0d = Dep()
        Hb = [kb.sb(st, [86, CB * nq, 2, N1], F32, "Hb") for _ in range(len(bs))]
        Hbd = [Dep() for _ in range(len(bs))]
        items = [(c0, d, bs[i % len(bs)]) for i, (c0, d) in enumerate((c0, d) for c0 in range(0, 64, CB) for d in range(2))]

        def sp_a(it):
            c0, d, b = it
            fft_layout_dma(kb, "sp", cfg, b.src_f, b.src_fd, taps[d], c0, CB, True)
            copy_op(kb, "dve", b.src_b[:, :, :], b.src_f[:, :, :], [b.src_fd], [b.src_bd])
            fft_s1(kb, g, cfg, tb, b, CB)

        def sp_b(it):
            c0, d, b = it
            fft_s2(kb, g, cfg, tb, b, CB)
            if d == 0:
                copy_op(kb, "act", X0[:, :, :, :], b.Xs[:, :, :, :], [b.Xsd], [X0d])
            else:
                tt(kb, "dve", Hb[0][:, :, 0, :], X0[:, :, 0, :], b.Xs[:, :, 0, :], ALU.add, [X0d, b.Xsd], [Hbd[0]])
                tt(kb, "dve", Hb[0][:, :, 1, :], X0[:, :, 1, :], b.Xs[:, :, 1, :], ALU.subtract, [X0d, b.Xsd], [Hbd[0]])
                kb.dma("sp", Hs[:, c0 * nq:(c0 + CB) * nq, :, :], Hb[0][:, :, :, :], reads=[Hbd[0]])
        pipeline2(items, sp_a, sp_b, depth=len(bs))
        kb.barrier()
        Yb = kb.sb(st, [86, CB * nq, 2, N1], BF16, "Yb")
        Ybd = Dep()
        Bs = kb.sb(st, [N1, CB, 2, N2], F32, "Bs")
        Bsd = Dep()
        Bb = kb.sb(st, [N1, CB, 2, N2], BF16, "Bb")
        Bbd = Dep()
        x0f = [kb.sb(st, [rows, CB, N2], F32, "x0f") for _ in range(len(bs))]
        x0d = [Dep() for _ in range(len(bs))]
        cv = kb.sb(st, [rows, CB, N2], F32, "cv")
        cvd = Dep()
        yo = kb.sb(st, [rows, CB, N2], BF16, "yo")
        yod = Dep()
        dsk = kb.sb(st, [128, 64], F32, "dsk")
        dskd = Dep()
        kb.dma("sp", dsk[:, :], dskip.broadcast_to([128, 64]), writes=[dskd])
        perb = 512 // N2
        citems = [(sq, c0, i % len(bs)) for i, (sq, c0) in enumerate((sq, c0) for sq in seqs for c0 in range(0, 64, CB))]

        def cv_a(it):
            sq, c0, k = it
            b = bs[k]
            fft_layout_dma(kb, "sp", cfg, b.src_f, b.src_fd, sq["z"], c0, CB, True)
            fft_layout_dma(kb, "pool", cfg, x0f[k], x0d[k], sq["x0"], c0, CB, True)
            kb.dma("sp", Hb[k][:, :, :, :], Hs[:, c0 * nq:(c0 + CB) * nq, :, :], writes=[Hbd[k]])
            copy_op(kb, "dve", b.src_b[:, :, :], b.src_f[:, :, :], [b.src_fd], [b.src_bd])
            fft_s1(kb, g, cfg, tb, b, CB)

        def cv_b(it):
            sq, c0, k = it
            b = bs[k]
            fft_s2(kb, g, cfg, tb, b, CB)
            cmul_batched(kb, cfg, b, 86, (CB * nq, N1), b.Xs[:, :, 0, :], b.Xs[:, :, 1, :], Hb[k][:, :, 0, :], Hb[k][:, :, 1, :],
                         Yb[:, :, 0, :], Yb[:, :, 1, :], [b.Xsd, Hbd[k]], Ybd)
            tot = CB * 2
            for i0 in range(0, tot, perb):
                cnt = min(perb, tot - i0)
                bk = nextbank(g)
                for i in range(i0, i0 + cnt):
                    c, ri = divmod(i, 2)
                    reg = g.psum[bk][:N1, (i - i0) * N2:(i - i0 + 1) * N2]
                    for p in range(nq):
                        ya_re, ya_im = Yb[:, c * nq + p, 0, :], Yb[:, c * nq + p, 1, :]
                        if ri == 0:
                            mm(kb, reg, ya_re, tb.F2[:, 0, p, :], p == 0, False, [Ybd, tb.d], [g.pd[bk]])
                            mm(kb, reg, ya_im, tb.F2[:, 1, p, :], False, p == nq - 1, [Ybd, tb.d], [g.pd[bk]])
                        else:
                            mm(kb, reg, ya_re, tb.F2[:, 2, p, :], p == 0, False, [Ybd, tb.d], [g.pd[bk]])
                            mm(kb, reg, ya_im, tb.F2[:, 0, p, :], False, p == nq - 1, [Ybd, tb.d], [g.pd[bk]])
                copy_op(kb, "act", Bs[:, :, :, :].rearrange("p c r n -> p (c r) n")[:, i0:i0 + cnt, :],
                        g.psum[bk][:N1, :cnt * N2].rearrange("p (i n) -> p i n", n=N2), [g.pd[bk]], [Bsd])
            twr = tb.twc[:, 0, :].unsqueeze(1).broadcast_to([N1, CB, N2])
            twi = tb.twc[:, 1, :].unsqueeze(1).broadcast_to([N1, CB, N2])
            cmul_batched(kb, cfg, b, N1, (CB, N2), Bs[:, :, 0, :], Bs[:, :, 1, :], twr, twi, Bb[:, :, 0, :], Bb[:, :, 1, :], [Bsd, tb.d], Bbd)
            for i0 in range(0, CB, perb):
                cnt = min(perb, CB - i0)
                bk = nextbank(g)
                for c in range(i0, i0 + cnt):
                    reg = g.psum[bk][:rows, (c - i0) * N2:(c - i0 + 1) * N2]
                    mm(kb, reg, tb.G1[:, 0, :], Bb[:, c, 0, :], True, False, [tb.d, Bbd], [g.pd[bk]])
                    mm(kb, reg, tb.G1[:, 1, :], Bb[:, c, 1, :], False, True, [tb.d, Bbd], [g.pd[bk]])
                copy_op(kb, "act", cv[:, i0:i0 + cnt, :], g.psum[bk][:rows, :cnt * N2].rearrange("p (i n) -> p i n", n=N2), [g.pd[bk]], [cvd])
            tt(kb, "dve", b.src_f[:, :, :], b.src_f[:, :, :], dsk[:rows, c0:c0 + CB].unsqueeze(2).broadcast_to([rows, CB, N2]), ALU.mult,
               [b.src_fd, dskd], [b.src_fd])
            tt(kb, "dve", cv[:, :, :], cv[:, :, :], b.src_f[:, :, :], ALU.add, [cvd, b.src_fd], [cvd])
            tt(kb, "dve", yo[:, :, :], cv[:, :, :], x0f[k][:, :, :], ALU.mult, [cvd, x0d[k]], [yod])
            fft_layout_dma(kb, "sp", cfg, yo, yod, sq["ya"], c0, CB, False)
        pipeline2(citems, cv_a, cv_b, depth=len(bs))
        kb.barrier()


def phase_hy_inproj(kb, g, seqs, w_hy, brow, hcols, G=1):
    with ExitStack() as st:
        wb = kb.sb(st, [128, 8, G * 192], BF16, "why")
        Wd = Dep()
        load_weight_bf16(kb, st, wb, Wd, w_hy, 8, G * 192, stage_cols=1536)
        hc = kb.sb(st, [64, G * 12], F32, "hc")
        hcd = Dep()
        kb.dma("sp", hc[:, :], hcols, writes=[hcd])
        brf = kb.sb(st, [1, G * 192], F32, "brf")
        brb = kb.sb(st, [1, G * 192], BF16, "brb")
        brd = Dep()
        kb.dma("sp", brf[:, :], brow, writes=[brd])
        copy_op(kb, "dve", brb[:, :], brf[:, :], [brd], [brd])
        xin = [kb.sb(st, [128, D], F32, "xin") for _ in range(4)]
        xind = [Dep() for _ in range(4)]
        xT = [kb.sb(st, [128, 8, 512], BF16, "xT") for _ in range(2)]
        xTd = [Dep() for _ in range(2)]
        vf = [kb.sb(st, [1, 512], F32, "vf") for _ in range(2)]
        vb = [kb.sb(st, [1, 512], BF16, "vb") for _ in range(2)]
        vd = [Dep() for _ in range(2)]
        o3 = [[kb.sb(st, [64, 512], F32, "o3") for _ in range(3)] for _ in range(2)]
        o3d = [[Dep() for _ in range(3)] for _ in range(2)]
        bi = 0
        oi = 0
        for sq in seqs:
            L = sq["L"]
            for t0 in range(0, L, 510):
                no = min(510, L - t0)
                ni = no + 2
                j = bi % 2
                bi += 1
                for ti, r0 in enumerate(range(0, ni, 128)):
                    n = min(128, ni - r0)
                    kb.dma("sp" if ti % 2 == 0 else "pool", xin[ti][:n, :], sq["xh"][t0 + r0:t0 + r0 + n, :], writes=[xind[ti]])
                    transpose_tile(kb, g, xin[ti], xind[ti], n, xT[j], xTd[j], r0)
                kb.dma("pool", vf[j][:, :ni], sq["valid"][:, t0:t0 + ni], writes=[vd[j]])
                copy_op(kb, "dve", vb[j][:, :ni], vf[j][:, :ni], [vd[j]], [vd[j]])
                for gg in range(G):
                    oj = oi % 2
                    oi += 1
                    for gi in range(3):
                        c0 = gg * 192 + gi * 64
                        h0 = gg * 12 + gi * 4
                        bk = nextbank(g)
                        for k in range(8):
                            mm(kb, g.psum[bk][:64, :ni], wb[:, k, c0:c0 + 64], xT[j][:, k, :ni], k == 0, False, [Wd, xTd[j]], [g.pd[bk]])
                        mm(kb, g.psum[bk][:64, :ni], brb[:, c0:c0 + 64], vb[j][:, :ni], False, True, [brd, vd[j]], [g.pd[bk]])
                        o = o3[oj][gi]
                        od = o3d[oj][gi]
                        act(kb, o[:, :no], g.psum[bk][:64, 1:1 + no], AF.Identity, [g.pd[bk], hcd], [od],
                            scale=hc[:, h0 + 1:h0 + 2], bias=hc[:, h0 + 3:h0 + 4])
                        stt(kb, "dve", o[:, :no], g.psum[bk][:64, 0:no], hc[:, h0:h0 + 1], o[:, :no], ALU.mult, ALU.add, [g.pd[bk], hcd, od], [od])
                        stt(kb, "dve", o[:, :no], g.psum[bk][:64, 2:2 + no], hc[:, h0 + 2:h0 + 3], o[:, :no], ALU.mult, ALU.add, [g.pd[bk], hcd, od], [od])
                    tt(kb, "pool", o3[oj][1][:, :no], o3[oj][1][:, :no], o3[oj][2][:, :no], ALU.mult, [o3d[oj][1], o3d[oj][2]], [o3d[oj][1]])
                    kb.dma("sp", sq["x0"][gg][:, t0:t0 + no], o3[oj][0][:, :no], reads=[o3d[oj][0]])
                    kb.dma("pool", sq["z"][gg][:, t0:t0 + no], o3[oj][1][:, :no], reads=[o3d[oj][1]])
        kb.barrier()


def sin_reduced(kb, out, outd, src_ps, fcol, fbcol, tmps, tmpd, ki, kid, n, reads):
    a, r = tmps
    ts(kb, "dve", a[:, :n], src_ps, fcol, fbcol, ALU.mult, ALU.add, reads, [tmpd[0]])
    ts(kb, "pool", r[:, :n], a[:, :n], 1.0 / TWO_PI, None, ALU.mult, None, [tmpd[0]], [tmpd[1]])
    copy_op(kb, "dve", ki[:, :n], r[:, :n], [tmpd[1]], [kid])
    copy_op(kb, "pool", r[:, :n], ki[:, :n], [kid], [tmpd[1]])
    stt(kb, "dve", r[:, :n], r[:, :n], -TWO_PI, a[:, :n], ALU.mult, ALU.add, [tmpd[0], tmpd[1]], [tmpd[1]])
    ts(kb, "pool", r[:, :n], r[:, :n], -3.1415925, 3.1415925, ALU.max, ALU.min, [tmpd[1]], [tmpd[1]])
    return act(kb, out, r[:, :n], AF.Sin, [tmpd[1]], [outd])


def phase_hy_filters(kb, g, L, zposT, fw, taps_out):
    with ExitStack() as st:
        w1 = kb.sb(st, [33, 2, 64], F32, "fw1")
        w2 = kb.sb(st, [64, 2, 64], F32, "fw2")
        w3 = kb.sb(st, [64, 2, 64], F32, "fw3")
        fc = kb.sb(st, [64, 2, 8], F32, "fc")
        Wd = Dep()
        kb.dma("sp", w1[:, :, :], fw["w1"].rearrange("d e f -> e d f"), writes=[Wd])
        kb.dma("sp", w2[:, :, :], fw["w2"].rearrange("d e f -> e d f"), writes=[Wd])
        kb.dma("sp", w3[:, :, :], fw["w3"].rearrange("d e f -> e d f"), writes=[Wd])
        kb.dma("sp", fc[:, :, 0:5], fw["fcols"], writes=[Wd])
        tt(kb, "dve", fc[:, :, 5:6], fc[:, :, 0:1], fc[:, :, 1:2], ALU.mult, [Wd], [Wd])
        tt(kb, "dve", fc[:, :, 6:7], fc[:, :, 2:3], fc[:, :, 3:4], ALU.mult, [Wd], [Wd])
        ts(kb, "dve", fc[:, :, 7:8], fc[:, :, 4:5], -1.0, None, ALU.mult, None, [Wd], [Wd])
        taps = kb.sb(st, [64, 2, L], F32, "taps")
        tapsd = Dep()
        zp = [kb.sb(st, [33, 512], F32, "zp") for _ in range(2)]
        zpd = [Dep() for _ in range(2)]
        tb_ = [kb.sb(st, [64, 512], F32, "tbc") for _ in range(2)]
        tbd = [Dep() for _ in range(2)]
        tmps = [kb.sb(st, [64, 512], F32, "ftmp") for _ in range(2)]
        tmpd = [Dep(), Dep()]
        ki = kb.sb(st, [64, 512], I32, "ki")
        kid = Dep()
        h1 = kb.sb(st, [64, 512], F32, "h1")
        h1d = Dep()
        h2 = kb.sb(st, [64, 512], F32, "h2")
        h2d = Dep()
        ex = kb.sb(st, [64, 512], F32, "ex")
        exd = Dep()
        ss = kb.sb(st, [64, 2 * ((L + 511) // 512) + 4], F32, "ss")
        ssd = Dep()
        junk = kb.sb(st, [64, 512], F32, "fjunk")
        junkd = Dep()
        nb = (L + 511) // 512
        for bi, l0 in enumerate(range(0, L, 512)):
            n = min(512, L - l0)
            j = bi % 2
            kb.dma("sp", zp[j][:, :n], zposT[:, l0:l0 + n], writes=[zpd[j]])
            kb.dma("pool", tb_[j][:, :n], zposT[0:1, l0:l0 + n].broadcast_to([64, n]), writes=[tbd[j]])
            for d in range(2):
                bk = nextbank(g)
                mm(kb, g.psum[bk][:64, :n], w1[:, d, :], zp[j][:, :n], True, True, [Wd, zpd[j]], [g.pd[bk]])
                sin_reduced(kb, h1[:, :n], h1d, g.psum[bk][:64, :n], fc[:, d, 0:1], fc[:, d, 5:6], tmps, tmpd, ki, kid, n, [g.pd[bk], Wd])
                bk = nextbank(g)
                mm(kb, g.psum[bk][:64, :n], w2[:, d, :], h1[:, :n], True, True, [Wd, h1d], [g.pd[bk]])
                sin_reduced(kb, h2[:, :n], h2d, g.psum[bk][:64, :n], fc[:, d, 2:3], fc[:, d, 6:7], tmps, tmpd, ki, kid, n, [g.pd[bk], Wd])
                bk = nextbank(g)
                mm(kb, g.psum[bk][:64, :n], w3[:, d, :], h2[:, :n], True, True, [Wd, h2d], [g.pd[bk]])
                act(kb, ex[:, :n], tb_[j][:, :n], AF.Exp, [tbd[j], Wd], [exd], scale=fc[:, d, 7:8])
                tt(kb, "dve", taps[:, d, l0:l0 + n], g.psum[bk][:64, :n], ex[:, :n], ALU.mult, [g.pd[bk], exd], [tapsd])
                if d == 1 and l0 == 0:
                    kb.op("pool", lambda e: e.memset(taps[:, 1, 0:1], 0.0), [], [tapsd])
                act(kb, junk[:, :n], taps[:, d, l0:l0 + n], AF.Square, [tapsd], [junkd, ssd], accum_out=ss[:, 2 * bi + d:2 * bi + d + 1])
        tot, nrm = ss[:, 2 * nb:2 * nb + 1], ss[:, 2 * nb + 1:2 * nb + 2]
        kb.op("dve", lambda e: e.tensor_reduce(out=tot, in_=ss[:, 0:2 * nb], axis=AX.X, op=ALU.add), [ssd], [ssd])
        act(kb, nrm, tot, AF.Sqrt, [ssd], [ssd])
        kb.op("dve", lambda e: e.reciprocal(out=nrm, in_=nrm), [ssd], [ssd])
        for d in range(2):
            for l0 in range(0, L, 4096):
                n = min(4096, L - l0)
                ts(kb, ("dve", "pool")[d], taps[:, d, l0:l0 + n], taps[:, d, l0:l0 + n], nrm, None, ALU.mult, None, [tapsd, ssd], [tapsd])
            kb.dma("sp", taps_out[d], taps[:, d, :], reads=[tapsd])
        kb.barrier()


def phase_hy_filter_h2(kb, g, L, zposT, fw, h2_out):
    with ExitStack() as st:
        w1 = kb.sb(st, [33, 2, 64], F32, "fw1")
        w2 = kb.sb(st, [64, 2, 64], F32, "fw2")
        fc = kb.sb(st, [64, 2, 8], F32, "fc")
        Wd = Dep()
        kb.dma("sp", w1[:, :, :], fw["w1"].rearrange("d e f -> e d f"), writes=[Wd])
        kb.dma("sp", w2[:, :, :], fw["w2"].rearrange("d e f -> e d f"), writes=[Wd])
        kb.dma("sp", fc[:, :, 0:5], fw["fcols"], writes=[Wd])
        tt(kb, "dve", fc[:, :, 5:6], fc[:, :, 0:1], fc[:, :, 1:2], ALU.mult, [Wd], [Wd])
        tt(kb, "dve", fc[:, :, 6:7], fc[:, :, 2:3], fc[:, :, 3:4], ALU.mult, [Wd], [Wd])
        zp = [kb.sb(st, [33, 512], F32, "zp") for _ in range(2)]
        zpd = [Dep() for _ in range(2)]
        NQ = 3
        tmps = [[kb.sb(st, [64, 512], F32, "ftmp") for _ in range(2)] for _ in range(NQ)]
        tmpd = [[Dep(), Dep()] for _ in range(NQ)]
        ki = [kb.sb(st, [64, 512], I32, "ki") for _ in range(NQ)]
        kid = [Dep() for _ in range(NQ)]
        h1 = [kb.sb(st, [64, 512], F32, "h1") for _ in range(NQ)]
        h1d = [Dep() for _ in range(NQ)]
        h2 = [kb.sb(st, [64, 512], F32, "h2") for _ in range(NQ)]
        h2d = [Dep() for _ in range(NQ)]
        it = 0
        for bi, l0 in enumerate(range(0, L, 512)):
            n = min(512, L - l0)
            j = bi % 2
            kb.dma("sp", zp[j][:, :n], zposT[:, l0:l0 + n], writes=[zpd[j]])
            for d in range(2):
                q = it % NQ
                it += 1
                bk = nextbank(g)
                mm(kb, g.psum[bk][:64, :n], w1[:, d, :], zp[j][:, :n], True, True, [Wd, zpd[j]], [g.pd[bk]])
                sin_reduced(kb, h1[q][:, :n], h1d[q], g.psum[bk][:64, :n], fc[:, d, 0:1], fc[:, d, 5:6], tmps[q], tmpd[q], ki[q], kid[q], n, [g.pd[bk], Wd])
                bk = nextbank(g)
                mm(kb, g.psum[bk][:64, :n], w2[:, d, :], h1[q][:, :n], True, True, [Wd, h1d[q]], [g.pd[bk]])
                sin_reduced(kb, h2[q][:, :n], h2d[q], g.psum[bk][:64, :n], fc[:, d, 2:3], fc[:, d, 6:7], tmps[q], tmpd[q], ki[q], kid[q], n, [g.pd[bk], Wd])
                kb.dma("pool", h2_out[d][:, l0:l0 + n], h2[q][:, :n], reads=[h2d[q]])
        kb.barrier()


def phase_hy_filter_taps(kb, g, L, zposT, h2_in, w3_ap, fcols_ap, taps_out):
    with ExitStack() as st:
        w3 = kb.sb(st, [64, 2, 64], F32, "fw3")
        fc = kb.sb(st, [64, 2, 8], F32, "fc")
        Wd = Dep()
        kb.dma("sp", w3[:, :, :], w3_ap.rearrange("d e f -> e d f"), writes=[Wd])
        kb.dma("sp", fc[:, :, 0:5], fcols_ap, writes=[Wd])
        ts(kb, "dve", fc[:, :, 7:8], fc[:, :, 4:5], -1.0, None, ALU.mult, None, [Wd], [Wd])
        taps = kb.sb(st, [64, 2, L], F32, "taps")
        tapsd = [Dep(), Dep()]
        NQ = 3
        tb_ = [kb.sb(st, [64, 512], F32, "tbc") for _ in range(2)]
        tbd = [Dep() for _ in range(2)]
        hin = [kb.sb(st, [64, 512], F32, "h2in") for _ in range(NQ)]
        hind = [Dep() for _ in range(NQ)]
        ex = [kb.sb(st, [64, 512], F32, "ex") for _ in range(NQ)]
        exd = [Dep() for _ in range(NQ)]
        junk = [kb.sb(st, [64, 512], F32, "fjunk") for _ in range(2)]
        junkd = [Dep(), Dep()]
        nb = (L + 511) // 512
        ss = kb.sb(st, [64, 2 * nb + 4], F32, "ss")
        ssd = Dep()
        it = 0
        for bi, l0 in enumerate(range(0, L, 512)):
            n = min(512, L - l0)
            j = bi % 2
            kb.dma("pool", tb_[j][:, :n], zposT[0:1, l0:l0 + n].broadcast_to([64, n]), writes=[tbd[j]])
            for d in range(2):
                q = it % NQ
                it += 1
                kb.dma("sp", hin[q][:, :n], h2_in[d][:, l0:l0 + n], writes=[hind[q]])
                bk = nextbank(g)
                mm(kb, g.psum[bk][:64, :n], w3[:, d, :], hin[q][:, :n], True, True, [Wd, hind[q]], [g.pd[bk]])
                act(kb, ex[q][:, :n], tb_[j][:, :n], AF.Exp, [tbd[j], Wd], [exd[q]], scale=fc[:, d, 7:8])
                tt(kb, "dve", taps[:, d, l0:l0 + n], g.psum[bk][:64, :n], ex[q][:, :n], ALU.mult, [g.pd[bk], exd[q]], [tapsd[d]])
                if d == 1 and l0 == 0:
                    kb.op("pool", lambda e: e.memset(taps[:, 1, 0:1], 0.0), [], [tapsd[d]])
                act(kb, junk[d][:, :n], taps[:, d, l0:l0 + n], AF.Square, [tapsd[d]], [junkd[d], ssd], accum_out=ss[:, 2 * bi + d:2 * bi + d + 1])
        tot, nrm = ss[:, 2 * nb:2 * nb + 1], ss[:, 2 * nb + 1:2 * nb + 2]
        kb.op("dve", lambda e: e.tensor_reduce(out=tot, in_=ss[:, 0:2 * nb], axis=AX.X, op=ALU.add), [ssd], [ssd])
        act(kb, nrm, tot, AF.Sqrt, [ssd], [ssd])
        kb.op("dve", lambda e: e.reciprocal(out=nrm, in_=nrm), [ssd], [ssd])
        for d in range(2):
            for l0 in range(0, L, 4096):
                n = min(4096, L - l0)
                ts(kb, ("dve", "pool")[d], taps[:, d, l0:l0 + n], taps[:, d, l0:l0 + n], nrm, None, ALU.mult, None, [tapsd[d], ssd], [tapsd[d]])
            kb.dma(("sp", "pool")[d], taps_out[d], taps[:, d, :], reads=[tapsd[d]])
        kb.barrier()


LP, LS = 16400, 2064
NCORES = 8
BF = ml_dtypes.bfloat16


class Prog:
    def __init__(self):
        self.nc = bass.Bass("TRN2", target_bir_lowering=False)
        self.kb = KB(self.nc)
        self.ins = {}

    def din(self, name, shape, dt=F32):
        self.ins[name] = (tuple(shape), dt)
        return self.nc.dram_tensor(name, list(shape), dt, kind="ExternalInput").ap()

    def dout(self, name, shape, dt=F32):
        return self.nc.dram_tensor(name, list(shape), dt, kind="ExternalOutput").ap()

    def scr(self, name, shape, dt=F32):
        return self.nc.dram_tensor(name, list(shape), dt).ap()


def chunk_tiles():
    return [(t0, 128, 61 + t0) for t0 in range(0, 2048, 128)] + [(2048, 16, 15)]


def declare_tabs(P, cfg, pre):
    t = fft_tables(cfg)
    return {k: P.din(pre + k, v.shape, F32 if v.dtype == np.float32 else BF16) for k, v in t.items()}, {pre + k: v for k, v in t.items()}


def build_l1():
    P = Prog()
    kb = P.kb
    ident = P.din("ident", [128, 128])
    xh_p = P.din("xh_p", [LP + 2, D])
    xh_s = P.din("xh_s", [LS + 2, D])
    valid_p = P.din("valid_p", [1, LP + 2])
    valid_s = P.din("valid_s", [1, LS + 2])
    zpos_p = P.din("zpos_p", [33, LP])
    zpos_s = P.din("zpos_s", [33, LS])
    tabsP, _ = declare_tabs(P, CFG_P, "tp_")
    tabsS, _ = declare_tabs(P, CFG_S, "ts_")
    fw1 = P.din("fw1", [2, 33, 64])
    fw2 = P.din("fw2", [2, 64, 64])
    fw3 = P.din("fw3", [9, 2, 64, 64])
    fcols = P.din("fcols", [9, 64, 2, 5])
    why = P.din("why", [9, D, 192])
    brow = P.din("brow", [9, 1, 192])
    hcols = P.din("hcols", [9, 64, 12])
    dskip = P.din("dskip", [9, 1, 64])
    xc = P.din("xc", [2, XC, D])
    mask = P.din("mask", [2, 1, XC])
    wconf = P.din("wconf", [D, 1024])
    ccols = P.din("ccols", [128, 144])
    yaP = P.dout("yaP", [64, LP], BF16)
    yaS = P.dout("yaS", [8, 64, LS], BF16)
    ybT = P.dout("ybT", [2, 512, LS], BF16)
    taps_p = P.scr("taps_p", [2, 64, LP])
    Hs_p = P.scr("Hs_p", [86, 64 * CFG_P.nq, 2, CFG_P.N1])
    z_p = P.scr("z_p", [64, LP])
    x0_p = P.scr("x0_p", [64, LP])
    taps_s = P.scr("taps_s", [8, 2, 64, LS])
    Hs_s = P.scr("Hs_s", [8, 86, 64 * CFG_S.nq, 2, CFG_S.N1])
    z_s = P.scr("z_s", [8, 64, LS])
    x0_s = P.scr("x0_s", [8, 64, LS])
    with ExitStack() as st:
        g = setup_globals(kb, st)
        load_ident(kb, g, ident)
        fwd = lambda i: dict(w1=fw1, w2=fw2, w3=fw3[i], fcols=fcols[i])
        phase_hy_filters(kb, g, LP, zpos_p, fwd(0), taps_p)
        phase_hy_inproj(kb, g, [dict(xh=xh_p, valid=valid_p, L=LP, z=[z_p], x0=[x0_p])], why[0], brow[0], hcols[0])
        phase_hy_conv(kb, g, CFG_P, tabsP, taps_p, Hs_p, [dict(z=z_p, x0=x0_p, ya=yaP)], dskip[0])
        for gi in range(8):
            phase_hy_filters(kb, g, LS, zpos_s, fwd(1 + gi), taps_s[gi])
            phase_hy_inproj(kb, g, [dict(xh=xh_s, valid=valid_s, L=LS, z=[z_s[gi]], x0=[x0_s[gi]])], why[1 + gi], brow[1 + gi], hcols[1 + gi])
            phase_hy_conv(kb, g, CFG_S, tabsS, taps_s[gi], Hs_s[gi], [dict(z=z_s[gi], x0=x0_s[gi], ya=yaS[gi])], dskip[1 + gi])

        def outf(s_):
            def f(j, bi, cn):
                if bi == 0:
                    return ybT[s_, j * 128:(j + 1) * 128, 2048:2064]
                return ybT[s_, j * 128:(j + 1) * 128, (bi - 1) * 512:bi * 512]
            return f
        phase_conf(kb, g, [dict(x=xc[s_], mask=mask[s_], out=outf(s_)) for s_ in range(2)], wconf, ccols)
        kb.finish_wait()
    return P


def build_l2():
    P = Prog()
    kb = P.kb
    ident = P.din("ident", [128, 128])
    xc = P.din("xc", [2, XC, D])
    ycT = P.din("ycT", [2, D, LS], BF16)
    wout = P.din("wout", [D, D])
    bout = P.din("bout", [1, D])
    ln1g = P.din("ln1g", [1, D]); ln1b = P.din("ln1b", [1, D]); ln2g = P.din("ln2g", [1, D]); ln2b = P.din("ln2b", [1, D])
    w1 = P.din("w1", [D, DFF]); w2 = P.din("w2", [DFF, D])
    wqa = P.din("wqa", [D, 384]); qg = P.din("qg", [1, 384]); WqH = P.din("WqH", [384, NH * 128]); WqS = P.din("WqS", [384, NH * 32])
    wkva = P.din("wkva", [D, 288]); kvg = P.din("kvg", [1, 256])
    cs = P.din("cs", [2, LS, 32]); Cq = P.din("Cq", [2, 32, 2048]); Sq = P.din("Sq", [2, 32, 2048])
    h2 = P.dout("h2", [2, LS, D])
    kvlat = P.dout("kvlat", [2, LS, 288])
    QT = P.dout("QT", [2, NH, 128, 2048], BF16)
    h1 = P.scr("h1", [2, LS, D])
    tl = chunk_tiles()
    with ExitStack() as st:
        g = setup_globals(kb, st)
        load_ident(kb, g, ident)
        ycv = ycT.rearrange("s (k p) t -> s p k t", p=128)
        phase_proj_ln(kb, g, [(xc[s_, xr:xr + n, :], [(slice(0, 8), ycv[s_, :, :, t0:t0 + n], None)], h1[s_, t0:t0 + n, :], n) for s_ in range(2) for t0, n, xr in tl],
                      True, wout, bout, ln1g, ln1b)
        phase_mlp_ln(kb, g, [(h1[s_, t0:t0 + n, :], h2[s_, t0:t0 + n, :], n) for s_ in range(2) for t0, n, xr in tl], w1, w2, ln2g, ln2b)
        seqs = []
        for s_ in range(2):
            seqs.append(dict(tiles=[(h2[s_, t0:t0 + n, :], kvlat[s_, t0:t0 + n, :], cs[s_, t0:t0 + n, :], n) for t0, n, xr in tl],
                             CS=(Cq[s_], Sq[s_]), qt=(lambda s_: (lambda h, q0: QT[s_, h, :, q0:q0 + 512]))(s_)))
        phase_qkv(kb, g, seqs, wqa, qg, WqH, WqS, wkva, kvg)
        kb.finish_wait()
    return P


def build_l3():
    P = Prog()
    kb = P.kb
    ident = P.din("ident", [128, 128])
    h2 = P.din("h2", [2, LS, D])
    kvp = P.din("kvp", [LP, 288])
    kvs = P.din("kvs", [LS, 288])
    QT = P.din("QT", [2, NH, 128, 2048], BF16)
    WkH = P.din("WkH", [256, NH * 128]); WvH = P.din("WvH", [256, NH * 64])
    wo = P.din("wo", [D, D])
    ln1g = P.din("ln1g", [1, D]); ln1b = P.din("ln1b", [1, D]); ln2g = P.din("ln2g", [1, D]); ln2b = P.din("ln2b", [1, D])
    w1 = P.din("w1", [D, DFF]); w2 = P.din("w2", [DFF, D])
    out = P.dout("out", [2, 2048, D])
    otok = P.scr("otok", [2, 2048, D])
    h3 = P.scr("h3", [2, 2048, D])
    with ExitStack() as st:
        g = setup_globals(kb, st)
        load_ident(kb, g, ident)
        seqs = []
        for s_, kv, L in ((0, kvp, LP), (1, kvs, LS)):
            otv = otok[s_].rearrange("(a t p) (h c) -> a p t h c", p=128, t=4, c=64)
            seqs.append(dict(kchunks=[(kv[t0:min(t0 + 128, L), :], min(128, L - t0)) for t0 in range(0, L, 128)],
                             qt=(lambda s_: (lambda h: QT[s_, h, :, :]))(s_),
                             o=(lambda otv: (lambda qsb, half, h: otv[qsb * 2 + half, :, :, h, :]))(otv)))
        phase_attn(kb, g, seqs, WkH, WvH)
        tl2 = [(s_, t0) for s_ in range(2) for t0 in range(0, 2048, 128)]
        phase_proj_ln(kb, g, [(h2[s_, t0:t0 + 128, :], otok[s_, t0:t0 + 128, :], h3[s_, t0:t0 + 128, :], 128) for s_, t0 in tl2], False, wo, None, ln1g, ln1b)
        phase_mlp_ln(kb, g, [(h3[s_, t0:t0 + 128, :], out[s_, t0:t0 + 128, :], 128) for s_, t0 in tl2], w1, w2, ln2g, ln2b)
        kb.finish_wait()
    return P


def zpos_table(L):
    t = np.arange(L, dtype=np.float32) / max(L - 1, 1)
    freqs = np.linspace(1e-4, 15, 16, dtype=np.float32)
    w = (np.float32(2.0 * math.pi) * np.arange(L, dtype=np.float32) / np.float32(L)).astype(np.float32)
    ang = w[:, None] * freqs[None, :]
    return np.ascontiguousarray(np.concatenate([t[:, None], np.cos(ang), -np.sin(ang)], -1).T.astype(np.float32))


def rope_cs(pos):
    inv = (1.0 / (10000.0 ** (np.arange(0, 32, 2, dtype=np.float32) / 32))).astype(np.float32)
    ang = pos.astype(np.float32)[:, None] * inv[None, :]
    return np.cos(ang).astype(np.float32), np.sin(ang).astype(np.float32)


def make_xc(hfull, m0, L):
    x = np.zeros((XC, D), np.float32)
    mk = np.zeros((1, XC), np.float32)
    x[15:46] = hfull[0:31]
    mk[0, 15:46] = 1
    lo, hi = m0 - 15, min(m0 + 2048 + 15, L)
    x[46:46 + (hi - lo)] = hfull[lo:hi]
    mk[0, 46:46 + (hi - lo)] = 1
    return x, mk


def colpack(v):
    return np.ascontiguousarray(v.reshape(4, 128).T)


def check_inputs(P, im):
    for k, (shape, dt) in P.ins.items():
        assert k in im, k
        assert tuple(im[k].shape) == shape, (k, im[k].shape, shape)
    return {k: np.ascontiguousarray(im[k]) for k in P.ins}


def kernel_unfused(x_prompt, x_sample, meta_tokens, ev_w_in, ev_b_in, ev_short_w, ev_short_b,
           hy_w1, hy_b1, hy_freq1, hy_w2, hy_b2, hy_freq2, hy_w3, hy_decay, hy_skip_d,
           cf_dw_w, cf_dw_b, cf_ln_g, cf_ln_b, ev_w_out, ev_b_out,
           mla_wq_a, mla_q_norm, mla_wq_b, mla_wkv_a, mla_kv_norm, mla_wkv_b, mla_wo,
           ln1_g, ln1_b, mlp_w1, mlp_w2, ln2_g, ln2_b):
    f = lambda a: np.asarray(a, dtype=np.float32)
    x_prompt, x_sample, meta = f(x_prompt), f(x_sample), f(meta_tokens)
    win, bin_, sw, sb = f(ev_w_in)[0], f(ev_b_in)[0], f(ev_short_w)[0], f(ev_short_b)[0]
    ident = np.eye(128, dtype=np.float32)
    hp = np.concatenate([meta, x_prompt[0]], 0)
    hs = [np.concatenate([meta, x_sample[c]], 0) for c in range(8)]
    z1 = np.zeros((1, D), np.float32)
    xh_p = np.concatenate([z1, hp, z1], 0)
    valid_p = np.ones((1, LP + 2), np.float32); valid_p[0, 0] = 0; valid_p[0, -1] = 0
    valid_s = np.ones((1, LS + 2), np.float32); valid_s[0, 0] = 0; valid_s[0, -1] = 0
    tabP, tabS = fft_tables(CFG_P), fft_tables(CFG_S)
    def grp(gi):
        ch = slice(gi * 64, gi * 64 + 64)
        gcols = [np.arange(k * 512 + gi * 64, k * 512 + gi * 64 + 64) for k in range(3)]
        allc = np.concatenate(gcols)
        return dict(fw3=np.ascontiguousarray(f(hy_w3)[0][:, :, ch]),
                    fcols=np.ascontiguousarray(np.stack([f(hy_freq1)[0], f(hy_b1)[0], f(hy_freq2)[0], f(hy_b2)[0], f(hy_decay)[0][:, ch]], -1).transpose(1, 0, 2)),
                    why=np.ascontiguousarray(win[:, allc]), brow=bin_[allc][None, :].copy(),
                    hcols=np.concatenate([np.stack([sw[0, gc], sw[1, gc], sw[2, gc], sb[gc]], 1) for gc in gcols], 1).astype(np.float32),
                    dskip=f(hy_skip_d)[0][ch][None, :].copy())
    G = [grp(gi) for gi in range(8)]
    ccols = np.concatenate([colpack(bin_[1536:2048]), colpack(bin_[2048:2560]), colpack(f(cf_dw_b)[0]), colpack(f(cf_ln_g)[0]), colpack(f(cf_ln_b)[0]),
                            np.ascontiguousarray(f(cf_dw_w)[0].T.reshape(4, 128, 31).transpose(1, 0, 2).reshape(128, 124))], 1).astype(np.float32)
    xcs, masks = [], []
    for c in range(8):
        a, ma = make_xc(hp, 16 + 2048 * c, LP)
        b, mb = make_xc(hs[c], 16, LS)
        xcs.append(np.stack([a, b], 0))
        masks.append(np.stack([ma, mb], 0))
    P1 = build_l1()
    ims = []
    for c in range(8):
        order = [c] + list(range(8))
        im = dict(ident=ident, xh_p=xh_p, xh_s=np.concatenate([z1, hs[c], z1], 0), valid_p=valid_p, valid_s=valid_s,
                  zpos_p=zpos_table(LP), zpos_s=zpos_table(LS), fw1=f(hy_w1)[0], fw2=f(hy_w2)[0],
                  fw3=np.stack([G[i]["fw3"] for i in order], 0), fcols=np.stack([G[i]["fcols"] for i in order], 0).astype(np.float32),
                  why=np.stack([G[i]["why"] for i in order], 0), brow=np.stack([G[i]["brow"] for i in order], 0),
                  hcols=np.stack([G[i]["hcols"] for i in order], 0), dskip=np.stack([G[i]["dskip"] for i in order], 0),
                  xc=xcs[c], mask=masks[c], wconf=np.ascontiguousarray(win[:, 1536:2560]), ccols=ccols)
        for k, v in tabP.items():
            im["tp_" + k] = v
        for k, v in tabS.items():
            im["ts_" + k] = v
        ims.append(check_inputs(P1, im))
    r1 = run_bass_kernel_spmd(P1.nc, ims, core_ids=list(range(8))).results
    yaP_all = np.concatenate([np.asarray(r1[c]["yaP"]) for c in range(8)], 0)
    P2 = build_l2()
    wqb = f(mla_wq_b)[0].reshape(384, NH, 96)
    WqH = np.concatenate([wqb[:, :, 64:96], np.zeros((384, NH, 32), np.float32), wqb[:, :, 0:64]], -1).reshape(384, NH * 128)
    WqS = np.concatenate([wqb[:, :, 80:96], wqb[:, :, 64:80]], -1).reshape(384, NH * 32)
    wkvb = f(mla_wkv_b)[0].reshape(256, NH, 128)
    WkH = np.concatenate([np.zeros((256, NH, 64), np.float32), wkvb[:, :, 0:64]], -1).reshape(256, NH * 128)
    WvH = np.ascontiguousarray(wkvb[:, :, 64:128]).reshape(256, NH * 64)
    ims = []
    for c in range(8):
        m0 = 16 + 2048 * c
        ya_p = np.concatenate([yaP_all[:, m0:m0 + 2048], yaP_all[:, 0:16]], 1)
        ya_s = np.asarray(r1[c]["yaS"]).reshape(512, LS)
        ya_s = np.concatenate([ya_s[:, 16:], ya_s[:, 0:16]], 1)
        yb = np.asarray(r1[c]["ybT"])
        ycT = np.stack([np.concatenate([ya_p, yb[0]], 0), np.concatenate([ya_s, yb[1]], 0)], 0)
        css, Cqs, Sqs = [], [], []
        for pos in (np.concatenate([np.arange(m0, m0 + 2048), np.arange(16)]), np.concatenate([np.arange(16, LS), np.arange(16)])):
            co, si = rope_cs(pos)
            css.append(np.concatenate([co, si], 1))
            Cqs.append(np.concatenate([co[:2048].T, co[:2048].T], 0))
            Sqs.append(np.concatenate([-si[:2048].T, si[:2048].T], 0))
        im = dict(ident=ident, xc=xcs[c], ycT=ycT, wout=f(ev_w_out)[0], bout=f(ev_b_out)[0:1], ln1g=f(ln1_g)[0:1], ln1b=f(ln1_b)[0:1],
                  ln2g=f(ln2_g)[0:1], ln2b=f(ln2_b)[0:1], w1=f(mlp_w1)[0], w2=f(mlp_w2)[0], wqa=f(mla_wq_a)[0], qg=f(mla_q_norm)[0:1],
                  WqH=WqH, WqS=WqS, wkva=f(mla_wkv_a)[0], kvg=f(mla_kv_norm)[0:1], cs=np.stack(css, 0), Cq=np.stack(Cqs, 0), Sq=np.stack(Sqs, 0))
        ims.append(check_inputs(P2, im))
    r2 = run_bass_kernel_spmd(P2.nc, ims, core_ids=list(range(8))).results
    kvp = np.concatenate([np.asarray(r2[c]["kvlat"])[0, :2048] for c in range(8)] + [np.asarray(r2[0]["kvlat"])[0, 2048:]], 0)
    P3 = build_l3()
    ims = []
    for c in range(8):
        im = dict(ident=ident, h2=np.asarray(r2[c]["h2"]), kvp=kvp, kvs=np.asarray(r2[c]["kvlat"])[1], QT=np.asarray(r2[c]["QT"]), WkH=WkH, WvH=WvH,
                  wo=f(mla_wo)[0], ln1g=f(ln1_g)[1:2], ln1b=f(ln1_b)[1:2], ln2g=f(ln2_g)[1:2], ln2b=f(ln2_b)[1:2], w1=f(mlp_w1)[1], w2=f(mlp_w2)[1])
        ims.append(check_inputs(P3, im))
    r3 = run_bass_kernel_spmd(P3.nc, ims, core_ids=list(range(8))).results
    y_prompt = np.concatenate([np.asarray(r3[c]["out"])[0] for c in range(8)], 0)[None].astype(np.float32)
    y_sample = np.stack([np.asarray(r3[c]["out"])[1] for c in range(8)], 0).astype(np.float32)
    return (y_prompt, y_sample)


U32 = mybir.dt.uint32
YAW = 18432


def build_fused(stop=10 ** 9, trace_steps=None):
    P = Prog()
    step = [0]

    def run(fn, *a):
        if step[0] < stop:
            fn(*a)
        step[0] += 1

    kb = P.kb
    nc = P.nc
    ident = P.din("ident", [128, 128])
    xh_p = P.din("xh_p", [LP + 2, D]); xh_s = P.din("xh_s", [LS + 2, D])
    valid_p = P.din("valid_p", [1, LP + 2]); valid_s = P.din("valid_s", [1, LS + 2])
    zpos_p = P.din("zpos_p", [33, LP]); zpos_s = P.din("zpos_s", [33, LS])
    tabsP, _ = declare_tabs(P, CFG_P, "tp_")
    tabsS, _ = declare_tabs(P, CFG_S, "ts_")
    fw1 = P.din("fw1", [2, 33, 64]); fw2 = P.din("fw2", [2, 64, 64])
    fw3 = P.din("fw3", [9, 2, 64, 64]); fcols = P.din("fcols", [9, 64, 2, 5])
    why = P.din("why", [9, D, 192]); brow = P.din("brow", [9, 1, 192]); hcols = P.din("hcols", [9, 64, 12]); dskip = P.din("dskip", [9, 1, 64])
    xc = P.din("xc", [2, XC, D]); mask = P.din("mask", [2, 1, XC])
    wconf = P.din("wconf", [D, 1024]); ccols = P.din("ccols", [128, 144])
    gidx = P.din("gidx", [128, 4], U32)
    wout = P.din("wout", [D, D]); bout = P.din("bout", [1, D])
    ln1g = P.din("ln1g", [2, 1, D]); ln1b = P.din("ln1b", [2, 1, D]); ln2g = P.din("ln2g", [2, 1, D]); ln2b = P.din("ln2b", [2, 1, D])
    w1 = P.din("w1", [2, D, DFF]); w2 = P.din("w2", [2, DFF, D])
    wqa = P.din("wqa", [D, 384]); qg = P.din("qg", [1, 384]); WqH = P.din("WqH", [384, NH * 128]); WqS = P.din("WqS", [384, NH * 32])
    wkva = P.din("wkva", [D, 288]); kvg = P.din("kvg", [1, 256])
    cs = P.din("cs", [2, LS, 32]); Cq = P.din("Cq", [2, 32, 2048]); Sq = P.din("Sq", [2, 32, 2048])
    WkH = P.din("WkH", [256, NH * 128]); WvH = P.din("WvH", [256, NH * 64]); wo = P.din("wo", [D, D])
    out = P.dout("out", [2, 2048, D])
    yaP = P.scr("yaP", [64, YAW], BF16)
    yaP_all = P.scr("yaP_all", [512, YAW], BF16)
    yaS = P.scr("yaS", [8, 64, LS], BF16)
    ybT = P.scr("ybT", [2, 512, LS], BF16)
    taps_p = P.scr("taps_p", [2, 64, LP]); Hs_p = P.scr("Hs_p", [86, 64 * CFG_P.nq, 2, CFG_P.N1])
    z_p = P.scr("z_p", [64, LP]); x0_p = P.scr("x0_p", [64, LP])
    taps_s = P.scr("taps_s", [8, 2, 64, LS]); Hs_s = P.scr("Hs_s", [8, 86, 64 * CFG_S.nq, 2, CFG_S.N1])
    z_s = P.scr("z_s", [8, 64, LS]); x0_s = P.scr("x0_s", [8, 64, LS])
    h1 = P.scr("h1", [2, LS, D]); h2 = P.scr("h2", [2, LS, D])
    kvlat = P.scr("kvlat", [2, LS, 288]); kv_all = P.scr("kv_all", [8 * LS, 288])
    QT = P.scr("QT", [2, NH, 128, 2048], BF16)
    otok = P.scr("otok", [2, 2048, D]); h3 = P.scr("h3", [2, 2048, D])
    tl = chunk_tiles()
    with ExitStack() as st:
        g = setup_globals(kb, st)
        load_ident(kb, g, ident)
        fwd = lambda i: dict(w1=fw1, w2=fw2, w3=fw3[i], fcols=fcols[i])
        run(phase_hy_filters, kb, g, LP, zpos_p, fwd(0), taps_p)
        run(phase_hy_inproj, kb, g, [dict(xh=xh_p, valid=valid_p, L=LP, z=[z_p], x0=[x0_p])], why[0], brow[0], hcols[0])
        run(phase_hy_conv, kb, g, CFG_P, tabsP, taps_p, Hs_p, [dict(z=z_p, x0=x0_p, ya=yaP[:, 2032:2032 + LP])], dskip[0])
        agd = Dep()
        run(lambda: kb.all_gather(yaP, yaP_all, reads=[], writes=[agd]))
        kb.barrier()
        for gi in range(8):
            run(phase_hy_filters, kb, g, LS, zpos_s, fwd(1 + gi), taps_s[gi])
            run(phase_hy_inproj, kb, g, [dict(xh=xh_s, valid=valid_s, L=LS, z=[z_s[gi]], x0=[x0_s[gi]])], why[1 + gi], brow[1 + gi], hcols[1 + gi])
            run(phase_hy_conv, kb, g, CFG_S, tabsS, taps_s[gi], Hs_s[gi], [dict(z=z_s[gi], x0=x0_s[gi], ya=yaS[gi])], dskip[1 + gi])

        def outf(s_):
            def f(j, bi, cn):
                if bi == 0:
                    return ybT[s_, j * 128:(j + 1) * 128, 2048:2064]
                return ybT[s_, j * 128:(j + 1) * 128, (bi - 1) * 512:bi * 512]
            return f
        run(phase_conf, kb, g, [dict(x=xc[s_], mask=mask[s_], out=outf(s_)) for s_ in range(2)], wconf, ccols)
        with ExitStack() as st2:
            yaG = kb.sb(st2, [128, 4, 2048], BF16, "yaG")
            yaGd = Dep()
            ix = kb.sb(st2, [128, 4], U32, "gix")
            ixd = Dep()
            kb.dma("sp", ix[:, :], gidx[:, :], writes=[ixd])
            rows = yaP_all.rearrange("c (b t) -> (c b) t", t=2048)
            for k in range(4):
                run(lambda k=k: kb.gather_rows(yaG[:, k, :], rows[:, :], ix[:, k:k + 1], reads=[agd, ixd], writes=[yaGd]))
            ybv = ybT.rearrange("s (k p) t -> s p k t", p=128)
            yav_meta = yaP_all.rearrange("(k p) c -> p k c", p=128)
            yas = yaS.rearrange("g c t -> (g c) t").rearrange("(k p) t -> p k t", p=128)
            tiles = []
            for t0, n, xr in tl:
                if n == 128:
                    yl = [(slice(0, 4), yaG[:, :, t0:t0 + n], yaGd), (slice(4, 8), ybv[0, :, :, t0:t0 + n], None)]
                else:
                    yl = [(slice(0, 4), yav_meta[:, :, 2032:2048], agd), (slice(4, 8), ybv[0, :, :, 2048:2064], None)]
                tiles.append((xc[0, xr:xr + n, :], yl, h1[0, t0:t0 + n, :], n))
            for t0, n, xr in tl:
                tok0 = 16 + t0 if n == 128 else 0
                yl = [(slice(0, 4), yas[:, :, tok0:tok0 + n], None), (slice(4, 8), ybv[1, :, :, t0:t0 + n], None)]
                tiles.append((xc[1, xr:xr + n, :], yl, h1[1, t0:t0 + n, :], n))
            run(phase_proj_ln, kb, g, tiles, True, wout, bout, ln1g[0], ln1b[0])
        run(phase_mlp_ln, kb, g, [(h1[s_, t0:t0 + n, :], h2[s_, t0:t0 + n, :], n) for s_ in range(2) for t0, n, xr in tl], w1[0], w2[0], ln2g[0], ln2b[0])
        seqs = []
        for s_ in range(2):
            seqs.append(dict(tiles=[(h2[s_, t0:t0 + n, :], kvlat[s_, t0:t0 + n, :], cs[s_, t0:t0 + n, :], n) for t0, n, xr in tl],
                             CS=(Cq[s_], Sq[s_]), qt=(lambda s_: (lambda h, q0: QT[s_, h, :, q0:q0 + 512]))(s_)))
        run(phase_qkv, kb, g, seqs, wqa, qg, WqH, WqS, wkva, kvg)
        kvd = Dep()
        run(lambda: kb.all_gather(kvlat[0], kv_all, reads=[], writes=[kvd]))
        kb.barrier()
        seqs = []
        pch = [(kv_all[r * LS + t0:r * LS + t0 + 128, :], 128) for r in range(8) for t0 in range(0, 2048, 128)] + [(kv_all[2048:2064, :], 16)]
        sch = [(kvlat[1, t0:min(t0 + 128, LS), :], min(128, LS - t0)) for t0 in range(0, LS, 128)]
        for s_, ch in ((0, pch), (1, sch)):
            otv = otok[s_].rearrange("(a t p) (h c) -> a p t h c", p=128, t=4, c=64)
            seqs.append(dict(kchunks=ch, qt=(lambda s_: (lambda h: QT[s_, h, :, :]))(s_),
                             o=(lambda otv: (lambda qsb, half, h: otv[qsb * 2 + half, :, :, h, :]))(otv)))
        run(phase_attn, kb, g, seqs, WkH, WvH)
        tl2 = [(s_, t0) for s_ in range(2) for t0 in range(0, 2048, 128)]
        run(phase_proj_ln, kb, g, [(h2[s_, t0:t0 + 128, :], otok[s_, t0:t0 + 128, :], h3[s_, t0:t0 + 128, :], 128) for s_, t0 in tl2], False, wo, None, ln1g[1], ln1b[1])
        run(phase_mlp_ln, kb, g, [(h3[s_, t0:t0 + 128, :], out[s_, t0:t0 + 128, :], 128) for s_, t0 in tl2], w1[1], w2[1], ln2g[1], ln2b[1])
        kb.finish_wait()
    P.nsteps = step[0]
    return P


def build_nc():
    P = Prog()
    kb = P.kb
    ident = P.din("ident", [128, 128])
    xpad_p = P.din("xpad_p", [LP + 30, D]); maskpad = P.din("maskpad", [1, LP + 30])
    xh_s = P.din("xh_s", [LS + 2, D]); valid_s = P.din("valid_s", [1, LS + 2])
    xc_s = P.din("xc_s", [XC, D]); mask_s = P.din("mask_s", [1, XC])
    zpos_p = P.din("zpos_p", [33, LP]); zpos_s = P.din("zpos_s", [33, LS])
    tabsP, _ = declare_tabs(P, CFG_P, "tp_")
    tabsS, _ = declare_tabs(P, CFG_S, "ts_")
    fw1 = P.din("fw1", [2, 33, 64]); fw2 = P.din("fw2", [2, 64, 64])
    fw3 = P.din("fw3", [8, 2, 64, 64]); fcols = P.din("fcols", [8, 64, 2, 5])
    why = P.din("why", [D, 8 * 192]); brow = P.din("brow", [1, 8 * 192]); hcols = P.din("hcols", [64, 8 * 12]); dskip = P.din("dskip", [8, 1, 64])
    wconf = P.din("wconf", [D, 1024]); ccols = P.din("ccols", [128, 144])
    tokidx = P.din("tokidx", [128, 16], U32)
    wout = P.din("wout", [D, D]); bout = P.din("bout", [1, D])
    ln1g = P.din("ln1g", [2, 1, D]); ln1b = P.din("ln1b", [2, 1, D]); ln2g = P.din("ln2g", [2, 1, D]); ln2b = P.din("ln2b", [2, 1, D])
    w1 = P.din("w1", [2, D, DFF]); w2 = P.din("w2", [2, DFF, D])
    wqa = P.din("wqa", [D, 384]); qg = P.din("qg", [1, 384]); WqH = P.din("WqH", [384, NH * 128]); WqS = P.din("WqS", [384, NH * 32])
    wkva = P.din("wkva", [D, 288]); kvg = P.din("kvg", [1, 256])
    cs_all = P.din("cs_all", [LP, 32])
    cs = P.din("cs", [2, LS, 32]); Cq = P.din("Cq", [2, 32, 2048]); Sq = P.din("Sq", [2, 32, 2048])
    WkH = P.din("WkH", [256, NH * 128]); WvH = P.din("WvH", [256, NH * 64]); wo = P.din("wo", [D, D])
    out = P.dout("out", [2, 2048, D])
    yaP_all = P.scr("yaP_all", [512, YAW], BF16)
    yaS = P.scr("yaS", [8, 64, LS], BF16)
    ybT_p = P.scr("ybT_p", [512, LP], BF16); ybT_s = P.scr("ybT_s", [512, LS], BF16)
    h2f_p = P.scr("h2f_p", [2, 64, LP]); h2f_s = P.scr("h2f_s", [2, 64, LS])
    taps_p = P.scr("taps_p", [2, 64, LP]); Hs_p = P.scr("Hs_p", [86, 64 * CFG_P.nq, 2, CFG_P.NF])
    z_p = P.scr("z_p", [8, 64, LP]); x0_p = P.scr("x0_p", [8, 64, LP])
    taps_s = P.scr("taps_s", [2, 64, LS]); Hs_s = P.scr("Hs_s", [86, 64 * CFG_S.nq, 2, CFG_S.NF])
    z_s = P.scr("z_s", [8, 64, LS]); x0_s = P.scr("x0_s", [8, 64, LS])
    h1_all = P.scr("h1_all", [LP, D]); h2_all = P.scr("h2_all", [LP, D])
    h1_s = P.scr("h1_s", [LS, D]); h2_s = P.scr("h2_s", [LS, D]); h2_own = P.scr("h2_own", [LS, D])
    kv_all = P.scr("kv_all", [LP, 288]); kv_dummy = P.scr("kv_dummy", [LS, 288]); kvlat_s = P.scr("kvlat_s", [LS, 288])
    QT = P.scr("QT", [2, NH, 128, 2048], BF16)
    otok = P.scr("otok", [2, 2048, D]); h3 = P.scr("h3", [2, 2048, D])
    tl = chunk_tiles()
    with ExitStack() as st:
        g = setup_globals(kb, st)
        load_ident(kb, g, ident)
        fwd = lambda i: dict(w1=fw1, w2=fw2, w3=fw3[i], fcols=fcols[i])
        phase_hy_inproj(kb, g, [dict(xh=xpad_p[14:14 + LP + 2, :], valid=maskpad[:, 14:14 + LP + 2], L=LP,
                                     z=[z_p[gi] for gi in range(8)], x0=[x0_p[gi] for gi in range(8)])], why, brow, hcols, G=8)
        phase_hy_filter_h2(kb, g, LP, zpos_p, fwd(0), h2f_p)
        phase_hy_filter_h2(kb, g, LS, zpos_s, fwd(0), h2f_s)
        for gi in range(8):
            phase_hy_filter_taps(kb, g, LP, zpos_p, h2f_p, fw3[gi], fcols[gi], taps_p)
            phase_hy_conv(kb, g, CFG_P, tabsP, taps_p, Hs_p, [dict(z=z_p[gi], x0=x0_p[gi], ya=yaP_all[gi * 64:(gi + 1) * 64, 2032:2032 + LP])], dskip[gi])
        phase_hy_inproj(kb, g, [dict(xh=xh_s, valid=valid_s, L=LS, z=[z_s[gi] for gi in range(8)], x0=[x0_s[gi] for gi in range(8)])],
                        why, brow, hcols, G=8)
        for gi in range(8):
            phase_hy_filter_taps(kb, g, LS, zpos_s, h2f_s, fw3[gi], fcols[gi], taps_s)
            phase_hy_conv(kb, g, CFG_S, tabsS, taps_s, Hs_s, [dict(z=z_s[gi], x0=x0_s[gi], ya=yaS[gi])], dskip[gi])
        cseqs = []
        for j in range(8):
            r0 = 16 + 2048 * j
            cseqs.append(dict(x=xpad_p[r0:r0 + 2078, :], mask=maskpad[:, r0:r0 + 2078], ncols=2078, blocks=[(15 + 512 * i, 512) for i in range(4)],
                              out=(lambda j: (lambda jj, bi, cn: ybT_p[jj * 128:(jj + 1) * 128, 2048 * j + 512 * bi:2048 * j + 512 * bi + cn]))(j)))
        cseqs.append(dict(x=xpad_p[0:46, :], mask=maskpad[:, 0:46], ncols=46, blocks=[(15, 16)],
                          out=lambda jj, bi, cn: ybT_p[jj * 128:(jj + 1) * 128, 16384:16400]))

        def outf_s(jj, bi, cn):
            if bi == 0:
                return ybT_s[jj * 128:(jj + 1) * 128, 2048:2064]
            return ybT_s[jj * 128:(jj + 1) * 128, (bi - 1) * 512:bi * 512]
        cseqs.append(dict(x=xc_s, mask=mask_s, out=outf_s))
        phase_conf(kb, g, cseqs, wconf, ccols)
        yav = yaP_all.rearrange("(k p) c -> p k c", p=128)
        ybv_p = ybT_p.rearrange("(k p) t -> p k t", p=128)
        ybv_s = ybT_s.rearrange("(k p) t -> p k t", p=128)
        yas = yaS.rearrange("g c t -> (g c) t").rearrange("(k p) t -> p k t", p=128)
        tiles = []
        for j in range(8):
            for t0 in range(0, 2048, 128):
                tok = 16 + 2048 * j + t0
                gr = 2048 * j + t0
                tiles.append((xpad_p[15 + tok:15 + tok + 128, :],
                              [(slice(0, 4), yav[:, :, 2032 + tok:2032 + tok + 128], None), (slice(4, 8), ybv_p[:, :, gr:gr + 128], None)],
                              h1_all[gr:gr + 128, :], 128))
        tiles.append((xpad_p[15:31, :], [(slice(0, 4), yav[:, :, 2032:2048], None), (slice(4, 8), ybv_p[:, :, 16384:16400], None)],
                      h1_all[16384:16400, :], 16))
        for t0, n, xr in tl:
            tok0 = 16 + t0 if n == 128 else 0
            tiles.append((xc_s[xr:xr + n, :], [(slice(0, 4), yas[:, :, tok0:tok0 + n], None), (slice(4, 8), ybv_s[:, :, t0:t0 + n], None)],
                          h1_s[t0:t0 + n, :], n))
        phase_proj_ln(kb, g, tiles, True, wout, bout, ln1g[0], ln1b[0])
        ptl = [(r0, min(128, LP - r0)) for r0 in range(0, LP, 128)]
        phase_mlp_ln(kb, g, [(h1_all[r0:r0 + n, :], h2_all[r0:r0 + n, :], n) for r0, n in ptl] +
                     [(h1_s[t0:t0 + n, :], h2_s[t0:t0 + n, :], n) for t0, n, xr in tl], w1[0], w2[0], ln2g[0], ln2b[0])
        with ExitStack() as st2:
            ix = kb.sb(st2, [128, 16], U32, "tokix")
            ixd = Dep()
            kb.dma("sp", ix[:, :], tokidx[:, :], writes=[ixd])
            gb = [kb.sb(st2, [128, D], F32, "gb") for _ in range(2)]
            gd = [Dep(), Dep()]
            for i in range(16):
                j = i % 2
                kb.gather_rows(gb[j][:, :], h2_all[:, :], ix[:, i:i + 1], reads=[ixd], writes=[gd[j]])
                kb.dma("sp", h2_own[128 * i:128 * i + 128, :], gb[j][:, :], reads=[gd[j]])
            kb.dma("sp", h2_own[2048:2064, :], h2_all[16384:16400, :])
            kb.barrier()
        seqs = [dict(tiles=[(h2_all[r0:r0 + n, :], kv_all[r0:r0 + n, :], cs_all[r0:r0 + n, :], n) for r0, n in ptl], kv_only=True),
                dict(tiles=[(h2_own[t0:t0 + n, :], kv_dummy[t0:t0 + n, :], cs[0, t0:t0 + n, :], n) for t0, n, xr in tl],
                     CS=(Cq[0], Sq[0]), qt=lambda h, q0: QT[0, h, :, q0:q0 + 512]),
                dict(tiles=[(h2_s[t0:t0 + n, :], kvlat_s[t0:t0 + n, :], cs[1, t0:t0 + n, :], n) for t0, n, xr in tl],
                     CS=(Cq[1], Sq[1]), qt=lambda h, q0: QT[1, h, :, q0:q0 + 512])]
        phase_qkv(kb, g, seqs, wqa, qg, WqH, WqS, wkva, kvg)
        aseqs = []
        for s_, ch in ((0, [(kv_all[r0:r0 + n, :], n) for r0, n in ptl]),
                       (1, [(kvlat_s[t0:min(t0 + 128, LS), :], min(128, LS - t0)) for t0 in range(0, LS, 128)])):
            otv = otok[s_].rearrange("(a t p) (h c) -> a p t h c", p=128, t=4, c=64)
            aseqs.append(dict(kchunks=ch, qt=(lambda s_: (lambda h: QT[s_, h, :, :]))(s_),
                              o=(lambda otv: (lambda qsb, half, h: otv[qsb * 2 + half, :, :, h, :]))(otv)))
        phase_attn(kb, g, aseqs, WkH, WvH)
        hres = (h2_own, h2_s)
        tl2 = [(s_, t0) for s_ in range(2) for t0 in range(0, 2048, 128)]
        phase_proj_ln(kb, g, [(hres[s_][t0:t0 + 128, :], otok[s_, t0:t0 + 128, :], h3[s_, t0:t0 + 128, :], 128) for s_, t0 in tl2], False, wo, None, ln1g[1], ln1b[1])
        phase_mlp_ln(kb, g, [(h3[s_, t0:t0 + 128, :], out[s_, t0:t0 + 128, :], 128) for s_, t0 in tl2], w1[1], w2[1], ln2g[1], ln2b[1])
        kb.finish_wait()
    return P


def kernel(x_prompt, x_sample, meta_tokens, ev_w_in, ev_b_in, ev_short_w, ev_short_b,
           hy_w1, hy_b1, hy_freq1, hy_w2, hy_b2, hy_freq2, hy_w3, hy_decay, hy_skip_d,
           cf_dw_w, cf_dw_b, cf_ln_g, cf_ln_b, ev_w_out, ev_b_out,
           mla_wq_a, mla_q_norm, mla_wq_b, mla_wkv_a, mla_kv_norm, mla_wkv_b, mla_wo,
           ln1_g, ln1_b, mlp_w1, mlp_w2, ln2_g, ln2_b):
    f = lambda a: np.asarray(a, dtype=np.float32)
    x_prompt, x_sample, meta = f(x_prompt), f(x_sample), f(meta_tokens)
    win, bin_, sw, sb = f(ev_w_in)[0], f(ev_b_in)[0], f(ev_short_w)[0], f(ev_short_b)[0]
    ident = np.eye(128, dtype=np.float32)
    hp = np.concatenate([meta, x_prompt[0]], 0)
    hs = [np.concatenate([meta, x_sample[c]], 0) for c in range(8)]
    z1 = np.zeros((1, D), np.float32)
    z15 = np.zeros((15, D), np.float32)
    xpad_p = np.concatenate([z15, hp, z15], 0)
    maskpad = np.zeros((1, LP + 30), np.float32); maskpad[0, 15:15 + LP] = 1
    valid_s = np.ones((1, LS + 2), np.float32); valid_s[0, 0] = 0; valid_s[0, -1] = 0
    tabP, tabS = fft_tables(CFG_P), fft_tables(CFG_S)
    gcols = [[np.arange(k * 512 + gi * 64, k * 512 + gi * 64 + 64) for k in range(3)] for gi in range(8)]
    allc = np.concatenate([np.concatenate(gc) for gc in gcols])
    why = np.ascontiguousarray(win[:, allc])
    brow = bin_[allc][None, :].copy()
    hcols = np.concatenate([np.stack([sw[0, c_], sw[1, c_], sw[2, c_], sb[c_]], 1) for gc in gcols for c_ in gc], 1).astype(np.float32)
    fw3 = np.stack([np.ascontiguousarray(f(hy_w3)[0][:, :, gi * 64:gi * 64 + 64]) for gi in range(8)], 0)
    fcols = np.stack([np.stack([f(hy_freq1)[0], f(hy_b1)[0], f(hy_freq2)[0], f(hy_b2)[0], f(hy_decay)[0][:, gi * 64:gi * 64 + 64]], -1).transpose(1, 0, 2)
                      for gi in range(8)], 0).astype(np.float32)
    dskip = np.stack([f(hy_skip_d)[0][gi * 64:gi * 64 + 64][None, :] for gi in range(8)], 0)
    ccols = np.concatenate([colpack(bin_[1536:2048]), colpack(bin_[2048:2560]), colpack(f(cf_dw_b)[0]), colpack(f(cf_ln_g)[0]), colpack(f(cf_ln_b)[0]),
                            np.ascontiguousarray(f(cf_dw_w)[0].T.reshape(4, 128, 31).transpose(1, 0, 2).reshape(128, 124))], 1).astype(np.float32)
    wqb = f(mla_wq_b)[0].reshape(384, NH, 96)
    WqH = np.concatenate([wqb[:, :, 64:96], np.zeros((384, NH, 32), np.float32), wqb[:, :, 0:64]], -1).reshape(384, NH * 128)
    WqS = np.concatenate([wqb[:, :, 80:96], wqb[:, :, 64:80]], -1).reshape(384, NH * 32)
    wkvb = f(mla_wkv_b)[0].reshape(256, NH, 128)
    WkH = np.concatenate([np.zeros((256, NH, 64), np.float32), wkvb[:, :, 0:64]], -1).reshape(256, NH * 128)
    WvH = np.ascontiguousarray(wkvb[:, :, 64:128]).reshape(256, NH * 64)
    zp_p, zp_s = zpos_table(LP), zpos_table(LS)
    co, si = rope_cs(np.concatenate([np.arange(16, LP), np.arange(16)]))
    cs_all = np.concatenate([co, si], 1)
    shared = dict(ident=ident, xpad_p=xpad_p, maskpad=maskpad, valid_s=valid_s, zpos_p=zp_p, zpos_s=zp_s, fw1=f(hy_w1)[0], fw2=f(hy_w2)[0],
                  fw3=fw3, fcols=fcols, why=why, brow=brow, hcols=hcols, dskip=dskip, wconf=np.ascontiguousarray(win[:, 1536:2560]), ccols=ccols,
                  wout=f(ev_w_out)[0], bout=f(ev_b_out)[0:1], ln1g=f(ln1_g)[:, None, :], ln1b=f(ln1_b)[:, None, :],
                  ln2g=f(ln2_g)[:, None, :], ln2b=f(ln2_b)[:, None, :], w1=f(mlp_w1), w2=f(mlp_w2), wqa=f(mla_wq_a)[0], qg=f(mla_q_norm)[0:1],
                  WqH=WqH, WqS=WqS, wkva=f(mla_wkv_a)[0], kvg=f(mla_kv_norm)[0:1], cs_all=cs_all, WkH=WkH, WvH=WvH, wo=f(mla_wo)[0])
    for k, v in tabP.items():
        shared["tp_" + k] = v
    for k, v in tabS.items():
        shared["ts_" + k] = v
    P = build_nc()
    ims = []
    for c in range(8):
        m0 = 16 + 2048 * c
        xb, mb = make_xc(hs[c], 16, LS)
        css, Cqs, Sqs = [], [], []
        for pos in (np.concatenate([np.arange(m0, m0 + 2048), np.arange(16)]), np.concatenate([np.arange(16, LS), np.arange(16)])):
            co, si = rope_cs(pos)
            css.append(np.concatenate([co, si], 1))
            Cqs.append(np.concatenate([co[:2048].T, co[:2048].T], 0))
            Sqs.append(np.concatenate([-si[:2048].T, si[:2048].T], 0))
        tix = (2048 * c + 128 * np.arange(16)[None, :] + np.arange(128)[:, None]).astype(np.uint32)
        im = dict(shared)
        im.update(xh_s=np.concatenate([z1, hs[c], z1], 0), xc_s=xb, mask_s=mb, tokidx=tix,
                  cs=np.stack(css, 0), Cq=np.stack(Cqs, 0), Sq=np.stack(Sqs, 0))
        ims.append(check_inputs(P, im))
    r = run_bass_kernel_spmd(P.nc, ims, core_ids=list(range(8))).results
    y_prompt = np.concatenate([np.asarray(r[c]["out"])[0] for c in range(8)], 0)[None].astype(np.float32)
    y_sample = np.stack([np.asarray(r[c]["out"])[1] for c in range(8)], 0).astype(np.float32)
    return (y_prompt, y_sample)
```

```python
import math
from contextlib import ExitStack
import numpy as np
import ml_dtypes
import concourse.bass as bass
import concourse.mybir as mybir
from concourse.bass_utils import run_bass_kernel_spmd

F32 = mybir.dt.float32
BF16 = mybir.dt.bfloat16
AF = mybir.ActivationFunctionType
ALU = mybir.AluOpType
AX = mybir.AxisListType

D = 1024
NMETA = 16
DFF = 4096
ALPHA = 4 ** 0.25
LN_EPS = 1e-5
RMS_EPS = 1e-6
NH = 16


SEM_MAX = 24000


class Dep:
    __slots__ = ("w", "r")

    def __init__(self):
        self.w = None
        self.r = {}


class KB:
    def __init__(self, nc):
        self.nc = nc
        self.stack = ExitStack()
        self.raw = dict(pe=nc.tensor, act=nc.scalar, dve=nc.vector, pool=nc.gpsimd, sp=nc.sync)
        self.sem = {}
        self.cnt = {}
        self.seen = {e: {} for e in self.raw}
        self.semobj = []
        for e in ("pe", "act", "dve", "pool"):
            self.sem[e] = self._newsem("s_" + e)
            self.cnt[e] = 0
        self.dq = {}
        for q, n in (("sp", 20), ("act", 8), ("pool", 8)):
            self.dq[q] = dict(sems=[self._newsem(f"d_{q}{i}") for i in range(n)], vals=[0] * n, nxt=0)
        self.uid = 0

    def _newsem(self, name):
        s = self.stack.enter_context(self.nc.semaphore(name))
        self.semobj.append(s)
        return len(self.semobj) - 1

    def name(self, p):
        self.uid += 1
        return f"{p}{self.uid}"

    def sb(self, st, shape, dt, name="t"):
        return st.enter_context(self.nc.sbuf_tensor(self.name(name), list(shape), dt))

    def ps(self, st, shape, dt, name="p"):
        return st.enter_context(self.nc.psum_tensor(self.name(name), list(shape), dt))

    def _waits(self, eng, reads, writes, extra=None):
        need = {}

        def add(tok):
            if tok is None:
                return
            s, v, src = tok
            if src == "pe" and eng == "pe":
                return
            if need.get(s, 0) < v:
                need[s] = v

        for d in reads:
            add(d.w)
        for d in writes:
            add(d.w)
            for t in d.r.values():
                add(t)
        if extra:
            for t in extra:
                add(t)
        seen = self.seen[eng]
        for s, v in need.items():
            if seen.get(s, 0) < v:
                self.raw[eng].wait_ge(self.semobj[s], v)
                seen[s] = v

    def op(self, eng, fn, reads=(), writes=()):
        self._waits(eng, reads, writes)
        ins = fn(self.raw[eng])
        if self.cnt[eng] >= SEM_MAX:
            self.sem[eng] = self._newsem(self.name("s_" + eng))
            self.cnt[eng] = 0
        self.cnt[eng] += 1
        ins.then_inc(self.semobj[self.sem[eng]], 1)
        tok = (self.sem[eng], self.cnt[eng], eng)
        for d in reads:
            d.r[tok[0]] = tok
        for d in writes:
            d.w = tok
            d.r = {}
        return ins

    def dma(self, q, out, in_, reads=(), writes=(), **kw):
        dq = self.dq[q]
        i = dq["nxt"]
        dq["nxt"] = (i + 1) % len(dq["sems"])
        s = dq["sems"][i]
        extra = [(s, dq["vals"][i], "dma")] if dq["vals"][i] else None
        self._waits(q, reads, writes, extra)
        ins = self.raw[q].dma_start(out=out, in_=in_, **kw)
        dq["vals"][i] += 16
        ins.then_inc(self.semobj[s], 16)
        tok = (s, dq["vals"][i], "dma")
        for d in reads:
            d.r[s] = tok
        for d in writes:
            d.w = tok
            d.r = {}
        return ins

    def all_gather(self, in_ap, out_ap, reads=(), writes=()):
        if not hasattr(self, "ccsem"):
            self.ccsem = self._newsem("ccsem")
            self.ccval = 0
        self._waits("pool", reads, writes)
        ins = self.raw["pool"].collective_compute("AllGather", ALU.bypass, replica_groups=[list(range(8))],
                                                  ins=[in_ap.opt()], outs=[out_ap.opt()])
        self.ccval += 1
        ins.then_inc(self.semobj[self.ccsem], 1)
        tok = (self.ccsem, self.ccval, "cc")
        for d in reads:
            d.r[self.ccsem] = tok
        for d in writes:
            d.w = tok
            d.r = {}
        return ins

    def gather_rows(self, out, in_rows, idx, reads=(), writes=()):
        dq = self.dq["pool"]
        i = dq["nxt"]
        dq["nxt"] = (i + 1) % len(dq["sems"])
        s = dq["sems"][i]
        extra = [(s, dq["vals"][i], "dma")] if dq["vals"][i] else None
        self._waits("pool", reads, writes, extra)
        ins = self.raw["pool"].indirect_dma_start(out=out, out_offset=None, in_=in_rows,
                                                  in_offset=bass.IndirectOffsetOnAxis(ap=idx, axis=0))
        dq["vals"][i] += 16
        ins.then_inc(self.semobj[s], 16)
        tok = (s, dq["vals"][i], "dma")
        for d in reads:
            d.r[s] = tok
        for d in writes:
            d.w = tok
            d.r = {}
        return ins

    def barrier(self):
        toks = [(self.sem[e], self.cnt[e], e) for e in ("pe", "act", "dve", "pool") if self.cnt[e]]
        for q in self.dq.values():
            for s, v in zip(q["sems"], q["vals"]):
                if v:
                    toks.append((s, v, "dma"))
        if getattr(self, "ccval", 0):
            toks.append((self.ccsem, self.ccval, "cc"))
        for eng in ("pe", "act", "dve", "pool", "sp"):
            seen = self.seen[eng]
            for s, v, src in toks:
                if seen.get(s, 0) < v and not (s == self.sem.get(eng)):
                    self.raw[eng].wait_ge(self.semobj[s], v)
                    seen[s] = v

    def finish_wait(self):
        for q in self.dq.values():
            for s, v in zip(q["sems"], q["vals"]):
                if v and self.seen["sp"].get(s, 0) < v:
                    self.raw["sp"].wait_ge(self.semobj[s], v)
                    self.seen["sp"][s] = v


class Glob:
    pass


def setup_globals(kb, st):
    g = Glob()
    nc = kb.nc
    g.pall = kb.ps(st, [128, 8, 512], F32, "banks")
    g.psum = [g.pall[:, b, :] for b in range(8)]
    g.pd = [Dep() for _ in range(8)]
    g.ident_f = kb.sb(st, [128, 128], F32, "identf")
    g.ident_b = kb.sb(st, [128, 128], BF16, "identb")
    g.ident_d = Dep()
    g.ones_b = kb.sb(st, [128, 128], BF16, "onesb")
    g.ones_d = Dep()
    g.bk = -1
    return g


def load_ident(kb, g, ident_dram):
    kb.dma("sp", g.ident_f[:], ident_dram, writes=[g.ident_d])
    kb.op("dve", lambda e: e.tensor_copy(out=g.ident_b[:], in_=g.ident_f[:]), reads=[g.ident_d], writes=[g.ident_d])
    kb.op("pool", lambda e: e.memset(g.ones_b[:], 1.0), writes=[g.ones_d])


_rr = [0]


def cast_eng():
    _rr[0] += 1
    return ("dve", "pool", "act")[_rr[0] % 3]


def copy_op(kb, eng, out, in_, reads, writes):
    if eng == "act":
        return kb.op("act", lambda e: e.copy(out=out, in_=in_), reads=reads, writes=writes)
    return kb.op(eng, lambda e: e.tensor_copy(out=out, in_=in_), reads=reads, writes=writes)


def load_weight_bf16(kb, st_phase, dst, dst_dep, src, kc, ncols, stage_cols=2048):
    with ExitStack() as st:
        stg = [kb.sb(st, [128, stage_cols], F32, "wstg") for _ in range(3)]
        sd = [Dep() for _ in range(3)]
        i = 0
        for k in range(kc):
            for c0 in range(0, ncols, stage_cols):
                cn = min(stage_cols, ncols - c0)
                j = i % 3
                kb.dma("sp" if i % 2 == 0 else "pool", stg[j][:, :cn], src[k * 128:(k + 1) * 128, c0:c0 + cn], writes=[sd[j]])
                copy_op(kb, ("dve", "act")[i % 2], dst[:, k, c0:c0 + cn], stg[j][:, :cn], [sd[j]], [dst_dep])
                i += 1
        kb.barrier()


def load_bcast(kb, dst, dep, src_row):
    kb.dma("sp", dst, src_row.partition_broadcast(128) if len(src_row.shape) == 1 else src_row.broadcast_to([128, src_row.shape[-1]]), writes=[dep])


def layer_norm_tile(kb, r, rd, n, gt, bt, gbd, out, outd, small, smd, junk, junkd):
    s1, s2 = small[:, 0:1], small[:, 1:2]
    kb.op("act", lambda e: e.activation(out=junk[:n, :], in_=r[:n, :], func=AF.Identity, accum_out=s1[:n, :]), reads=[rd], writes=[junkd, smd])
    kb.op("act", lambda e: e.activation(out=junk[:n, :], in_=r[:n, :], func=AF.Square, accum_out=s2[:n, :]), reads=[rd], writes=[junkd, smd])
    mean, var, rstd = small[:, 2:3], small[:, 3:4], small[:, 4:5]
    kb.op("dve", lambda e: e.tensor_scalar(out=mean[:n, :], in0=s1[:n, :], scalar1=1.0 / D, scalar2=None, op0=ALU.mult), reads=[smd], writes=[smd])
    kb.op("dve", lambda e: e.tensor_tensor(out=var[:n, :], in0=mean[:n, :], in1=mean[:n, :], op=ALU.mult), reads=[smd], writes=[smd])
    kb.op("dve", lambda e: e.scalar_tensor_tensor(out=var[:n, :], in0=s2[:n, :], scalar=1.0 / D, in1=var[:n, :], op0=ALU.mult, op1=ALU.subtract), reads=[smd], writes=[smd])
    kb.op("act", lambda e: e.activation(out=rstd[:n, :], in_=var[:n, :], func=AF.Sqrt, bias=LN_EPS, scale=1.0), reads=[smd], writes=[smd])
    kb.op("dve", lambda e: e.reciprocal(out=rstd[:n, :], in_=rstd[:n, :]), reads=[smd], writes=[smd])
    kb.op("dve", lambda e: e.tensor_scalar(out=r[:n, :], in0=r[:n, :], scalar1=mean[:n, :], scalar2=rstd[:n, :], op0=ALU.subtract, op1=ALU.mult), reads=[smd, rd], writes=[rd])
    kb.op("dve", lambda e: e.tensor_tensor(out=r[:n, :], in0=r[:n, :], in1=gt[:n, :], op=ALU.mult), reads=[rd, gbd], writes=[rd])
    kb.op("pool", lambda e: e.tensor_tensor(out=out[:n, :], in0=r[:n, :], in1=bt[:n, :], op=ALU.add), reads=[rd, gbd], writes=[outd])


def mm(kb, out, lhsT, rhs, start, stop, reads, writes):
    return kb.op("pe", lambda e: e.matmul(out, lhsT=lhsT, rhs=rhs, start=start, stop=stop), reads, writes)


def tt(kb, eng, out, in0, in1, op, reads, writes):
    return kb.op(eng, lambda e: e.tensor_tensor(out=out, in0=in0, in1=in1, op=op), reads, writes)


def ts(kb, eng, out, in0, s1, s2, op0, op1, reads, writes):
    if s2 is None:
        return kb.op(eng, lambda e: e.tensor_scalar(out=out, in0=in0, scalar1=s1, scalar2=None, op0=op0), reads, writes)
    return kb.op(eng, lambda e: e.tensor_scalar(out=out, in0=in0, scalar1=s1, scalar2=s2, op0=op0, op1=op1), reads, writes)


def stt(kb, eng, out, in0, scalar, in1, op0, op1, reads, writes):
    return kb.op("dve", lambda e: e.scalar_tensor_tensor(out=out, in0=in0, scalar=scalar, in1=in1, op0=op0, op1=op1), reads, writes)


def act(kb, out, in_, func, reads, writes, **kw):
    return kb.op("act", lambda e: e.activation(out=out, in_=in_, func=func, **kw), reads, writes)


def nextbank(g):
    g.bk = (g.bk + 1) % 8
    return g.bk


def transpose_tile(kb, g, src, srcd, n, dstT, dstd, col0, kc=8):
    for k0 in range(0, kc, 4):
        b = nextbank(g)
        kn = min(4, kc - k0)
        pv = g.psum[b][:, :].rearrange("p (k t) -> p k t", k=4)
        for k in range(kn):
            kb.op("pe", lambda e, k=k: e.transpose(pv[:, k, :n], src[:n, (k0 + k) * 128:(k0 + k + 1) * 128], g.ident_f[:n, :n]),
                  reads=[srcd, g.ident_d], writes=[g.pd[b]])
        copy_op(kb, ("dve", "act")[b % 2], dstT[:, k0:k0 + kn, col0:col0 + n], pv[:, 0:kn, :n], [g.pd[b]], [dstd])


def phase_proj_ln(kb, g, tiles, fm, W, bias, lng, lnb):
    with ExitStack() as st:
        Wb = kb.sb(st, [128, 8, D], BF16, "Wb")
        Wd = Dep()
        load_weight_bf16(kb, st, Wb, Wd, W, 8, D)
        gt = kb.sb(st, [128, D], F32, "g")
        bt = kb.sb(st, [128, D], F32, "b")
        gbd = Dep()
        load_bcast(kb, gt[:], gbd, lng)
        load_bcast(kb, bt[:], gbd, lnb)
        if bias is not None:
            bi = kb.sb(st, [128, D], F32, "bias")
            load_bcast(kb, bi[:], gbd, bias)
        NB = 4
        hb = [kb.sb(st, [128, D], F32, "h") for _ in range(NB)]
        hd = [Dep() for _ in range(NB)]
        yT = [kb.sb(st, [128, 8, 128], BF16, "yT") for _ in range(NB)]
        yTd = [Dep() for _ in range(NB)]
        if not fm:
            yb = [kb.sb(st, [128, D], F32, "y") for _ in range(NB)]
            yd = [Dep() for _ in range(NB)]
        rb = [kb.sb(st, [128, D], F32, "r") for _ in range(NB)]
        rd = [Dep() for _ in range(NB)]
        junk = kb.sb(st, [128, D], F32, "junk")
        junkd = Dep()
        small = [kb.sb(st, [128, 8], F32, "small") for _ in range(NB)]
        smd = [Dep() for _ in range(NB)]
        for i, (hap, yap, oap, n) in enumerate(tiles):
            j = i % NB
            kb.dma("sp", hb[j][:n, :], hap, writes=[hd[j]])
            if fm:
                for qi, (ksl, src, dep) in enumerate(yap):
                    kb.dma("sp", yT[j][:, ksl, :n], src, reads=[dep] if dep is not None else [], writes=[yTd[j]])
            else:
                kb.dma("sp", yb[j][:n, :], yap, writes=[yd[j]])
                transpose_tile(kb, g, yb[j], yd[j], n, yT[j], yTd[j], 0)
            bks = (nextbank(g), nextbank(g))
            for half, bk in enumerate(bks):
                for k in range(8):
                    mm(kb, g.psum[bk][:n, :], yT[j][:, k, :n], Wb[:, k, half * 512:(half + 1) * 512], k == 0, k == 7,
                       [yTd[j], Wd], [g.pd[bk]])
            for half, bk in enumerate(bks):
                sl = slice(half * 512, (half + 1) * 512)
                if bias is not None:
                    tt(kb, "dve", rb[j][:n, sl], g.psum[bk][:n, :], bi[:n, sl], ALU.add, [g.pd[bk], gbd], [rd[j]])
                else:
                    copy_op(kb, "act", rb[j][:n, sl], g.psum[bk][:n, :], [g.pd[bk]], [rd[j]])
            stt(kb, "pool", rb[j][:n, :], hb[j][:n, :], ALPHA, rb[j][:n, :], ALU.mult, ALU.add, [hd[j], rd[j]], [rd[j]])
            layer_norm_tile(kb, rb[j], rd[j], n, gt, bt, gbd, rb[j], rd[j], small[j], smd[j], junk, junkd)
            kb.dma("pool", oap, rb[j][:n, :], reads=[rd[j]])
        kb.barrier()


def phase_mlp_ln(kb, g, tiles, W1, W2, lng, lnb):
    with ExitStack() as st:
        W1b = kb.sb(st, [128, 8, DFF], BF16, "W1b")
        W2b = kb.sb(st, [128, 32, D], BF16, "W2b")
        Wd = Dep()
        load_weight_bf16(kb, st, W1b, Wd, W1, 8, DFF)
        load_weight_bf16(kb, st, W2b, Wd, W2, 32, D, stage_cols=1024)
        gt = kb.sb(st, [128, D], F32, "g")
        bt = kb.sb(st, [128, D], F32, "b")
        gbd = Dep()
        load_bcast(kb, gt[:], gbd, lng)
        load_bcast(kb, bt[:], gbd, lnb)
        hb = [kb.sb(st, [128, D], F32, "h") for _ in range(4)]
        hd = [Dep() for _ in range(4)]
        hT = kb.sb(st, [128, 8, 512], BF16, "hT")
        hTd = Dep()
        uT = kb.sb(st, [128, 32, 512], BF16, "uT")
        uTd = [Dep() for _ in range(32)]
        rl = [kb.sb(st, [128, 512], F32, "relu") for _ in range(2)]
        rld = [Dep() for _ in range(2)]
        junk = kb.sb(st, [128, D], BF16, "junk")
        junkd = Dep()
        small = [kb.sb(st, [128, 8], F32, "small") for _ in range(4)]
        smd = [Dep() for _ in range(4)]
        for s0 in range(0, len(tiles), 4):
            grp = tiles[s0:s0 + 4]
            offs = []
            tot = 0
            for i, (iap, oap, n) in enumerate(grp):
                kb.dma("sp", hb[i][:n, :], iap, writes=[hd[i]])
                offs.append(tot)
                tot += n
            for i, (iap, oap, n) in enumerate(grp):
                transpose_tile(kb, g, hb[i], hd[i], n, hT, hTd, offs[i])
            for j in range(32):
                bk = nextbank(g)
                for k in range(8):
                    mm(kb, g.psum[bk][:, :tot], W1b[:, k, j * 128:(j + 1) * 128], hT[:, k, :tot], k == 0, k == 7, [Wd, hTd], [g.pd[bk]])
                q = j % 2
                act(kb, rl[q][:, :tot], g.psum[bk][:, :tot], AF.Relu, [g.pd[bk]], [rld[q]])
                tt(kb, "pool" if j % 4 < 3 else "dve", uT[:, j, :tot], rl[q][:, :tot], rl[q][:, :tot], ALU.mult, [rld[q]], [uTd[j]])
            for i, (iap, oap, n) in enumerate(grp):
                bks = (nextbank(g), nextbank(g))
                for half, bk in enumerate(bks):
                    for j in range(32):
                        mm(kb, g.psum[bk][:n, :], uT[:, j, offs[i]:offs[i] + n], W2b[:, j, half * 512:(half + 1) * 512], j == 0, j == 31,
                           [uTd[j], Wd], [g.pd[bk]])
                for half, bk in enumerate(bks):
                    sl = slice(half * 512, (half + 1) * 512)
                    stt(kb, "dve", hb[i][:n, sl], hb[i][:n, sl], ALPHA, g.psum[bk][:n, :], ALU.mult, ALU.add, [hd[i], g.pd[bk]], [hd[i]])
                layer_norm_tile(kb, hb[i], hd[i], n, gt, bt, gbd, hb[i], hd[i], small[i], smd[i], junk, junkd)
                kb.dma("pool", oap, hb[i][:n, :], reads=[hd[i]])
        kb.barrier()


def rms_rstd(kb, src, srcd, n, width, small, smd, junk, junkd, col):
    ss, rs = small[:, col:col + 1], small[:, col + 1:col + 2]
    act(kb, junk[:n, :width], src[:n, :width], AF.Square, [srcd], [junkd, smd], accum_out=ss[:n, :])
    act(kb, rs[:n, :], ss[:n, :], AF.Sqrt, [smd], [smd], bias=RMS_EPS, scale=1.0 / width)
    kb.op("dve", lambda e: e.reciprocal(out=rs[:n, :], in_=rs[:n, :]), [smd], [smd])
    return rs


def phase_qkv(kb, g, seqs, wqa, qg, WqH, WqS, wkva, kvg):
    with ExitStack() as st:
        wqa_b = kb.sb(st, [128, 8, 384], BF16, "wqa")
        wkva_b = kb.sb(st, [128, 8, 288], BF16, "wkva")
        wqh_b = kb.sb(st, [128, 3, NH * 128], BF16, "wqh")
        wqs_b = kb.sb(st, [128, 3, NH * 32], BF16, "wqs")
        Wd = Dep()
        load_weight_bf16(kb, st, wqa_b, Wd, wqa, 8, 384)
        load_weight_bf16(kb, st, wkva_b, Wd, wkva, 8, 288)
        load_weight_bf16(kb, st, wqh_b, Wd, WqH, 3, NH * 128)
        load_weight_bf16(kb, st, wqs_b, Wd, WqS, 3, NH * 32)
        qgt = kb.sb(st, [128, 384], F32, "qg")
        kvgt = kb.sb(st, [128, 256], F32, "kvg")
        gd = Dep()
        load_bcast(kb, qgt[:], gd, qg)
        load_bcast(kb, kvgt[:], gd, kvg)
        hb = [kb.sb(st, [128, D], F32, "h") for _ in range(4)]
        hd = [Dep() for _ in range(4)]
        hT = kb.sb(st, [128, 8, 512], BF16, "hT")
        hTd = Dep()
        cq = [kb.sb(st, [128, 384], F32, "cq") for _ in range(2)]
        cqd = [Dep() for _ in range(2)]
        cqT = kb.sb(st, [128, 3, 512], BF16, "cqT")
        cqTd = Dep()
        kvr = [kb.sb(st, [128, 288], F32, "kvr") for _ in range(2)]
        kvrd = [Dep() for _ in range(2)]
        kvo = [kb.sb(st, [128, 288], F32, "kvo") for _ in range(2)]
        kvod = [Dep() for _ in range(2)]
        cst = [kb.sb(st, [128, 32], F32, "cs") for _ in range(2)]
        csd = [Dep() for _ in range(2)]
        tmp = [kb.sb(st, [128, 64], F32, "tmp") for _ in range(2)]
        tmpd = [Dep() for _ in range(2)]
        junk = kb.sb(st, [128, 384], F32, "junk")
        junkd = Dep()
        small = [kb.sb(st, [128, 8], F32, "small") for _ in range(2)]
        smd = [Dep() for _ in range(2)]
        Ct = kb.sb(st, [32, 2048], F32, "C")
        St = kb.sb(st, [32, 2048], F32, "S")
        CSd = Dep()
        qsw = [kb.sb(st, [32, 512], F32, "qsw") for _ in range(2)]
        qswd = [Dep() for _ in range(2)]
        qo = [kb.sb(st, [128, 512], BF16, "qo") for _ in range(2)]
        qod = [Dep() for _ in range(2)]
        it = 0
        for sq in seqs:
            if not sq.get("kv_only"):
                kb.dma("sp", Ct[:], sq["CS"][0], writes=[CSd])
                kb.dma("sp", St[:], sq["CS"][1], writes=[CSd])
            tiles = sq["tiles"]
            for s0 in range(0, len(tiles), 4):
                grp = tiles[s0:s0 + 4]
                offs, tot = [], 0
                for i, (hap, kvap, csap, n) in enumerate(grp):
                    kb.dma("sp", hb[i][:n, :], hap, writes=[hd[i]])
                    offs.append(tot)
                    tot += n
                for i, (hap, kvap, csap, n) in enumerate(grp):
                    transpose_tile(kb, g, hb[i], hd[i], n, hT, hTd, offs[i])
                is_main = (tot == 512) and not sq.get("kv_only")
                for i, (hap, kvap, csap, n) in enumerate(grp):
                    j = it % 2
                    it += 1
                    kb.dma("sp", cst[j][:n, :], csap, writes=[csd[j]])
                    bk = nextbank(g)
                    for k in range(8):
                        mm(kb, g.psum[bk][:n, :288], hT[:, k, offs[i]:offs[i] + n], wkva_b[:, k, :], k == 0, k == 7, [hTd, Wd], [g.pd[bk]])
                    copy_op(kb, "act", kvr[j][:n, :], g.psum[bk][:n, :288], [g.pd[bk]], [kvrd[j]])
                    rs = rms_rstd(kb, kvr[j], kvrd[j], n, 256, small[j], smd[j], junk, junkd, 0)
                    stt(kb, "dve", kvo[j][:n, 0:256], kvr[j][:n, 0:256], rs[:n, :], kvgt[:n, :], ALU.mult, ALU.mult, [kvrd[j], smd[j], gd], [kvod[j]])
                    x1, x2 = kvr[j][:n, 256:272], kvr[j][:n, 272:288]
                    co, si = cst[j][:n, 0:16], cst[j][:n, 16:32]
                    t = tmp[j]
                    tt(kb, "dve", t[:n, 0:16], x1, co, ALU.mult, [kvrd[j], csd[j]], [tmpd[j]])
                    tt(kb, "dve", t[:n, 16:32], x2, si, ALU.mult, [kvrd[j], csd[j]], [tmpd[j]])
                    tt(kb, "dve", t[:n, 32:48], x1, si, ALU.mult, [kvrd[j], csd[j]], [tmpd[j]])
                    tt(kb, "dve", t[:n, 48:64], x2, co, ALU.mult, [kvrd[j], csd[j]], [tmpd[j]])
                    tt(kb, "dve", kvo[j][:n, 256:272], t[:n, 0:16], t[:n, 16:32], ALU.subtract, [tmpd[j]], [kvod[j]])
                    tt(kb, "dve", kvo[j][:n, 272:288], t[:n, 32:48], t[:n, 48:64], ALU.add, [tmpd[j]], [kvod[j]])
                    kb.dma("pool", kvap, kvo[j][:n, :], reads=[kvod[j]])
                    if not is_main:
                        continue
                    bk = nextbank(g)
                    for k in range(8):
                        mm(kb, g.psum[bk][:n, :384], hT[:, k, offs[i]:offs[i] + n], wqa_b[:, k, :], k == 0, k == 7, [hTd, Wd], [g.pd[bk]])
                    copy_op(kb, "act", cq[j][:n, :], g.psum[bk][:n, :384], [g.pd[bk]], [cqd[j]])
                    rs = rms_rstd(kb, cq[j], cqd[j], n, 384, small[j], smd[j], junk, junkd, 2)
                    stt(kb, "dve", cq[j][:n, :], cq[j][:n, :], rs[:n, :], qgt[:n, :], ALU.mult, ALU.mult, [cqd[j], smd[j], gd], [cqd[j]])
                    transpose_tile(kb, g, cq[j], cqd[j], n, cqT, cqTd, offs[i], kc=3)
                if not is_main:
                    continue
                q0 = (s0 // 4) * 512
                for h in range(NH):
                    j = h % 2
                    bka, bkb = nextbank(g), nextbank(g)
                    for k in range(3):
                        mm(kb, g.psum[bka][:, :], wqh_b[:, k, h * 128:(h + 1) * 128], cqT[:, k, :], k == 0, k == 2, [Wd, cqTd], [g.pd[bka]])
                    for k in range(3):
                        mm(kb, g.psum[bkb][:32, :], wqs_b[:, k, h * 32:(h + 1) * 32], cqT[:, k, :], k == 0, k == 2, [Wd, cqTd], [g.pd[bkb]])
                    tt(kb, "dve", qsw[j][:, :], g.psum[bkb][:32, :], St[:, q0:q0 + 512], ALU.mult, [g.pd[bkb], CSd], [qswd[j]])
                    rope_q(kb, g, qo[j], qod[j], bka, qsw[j], qswd[j], Ct, CSd, q0, st, small)
                    copy_op(kb, "act", qo[j][32:64, :], g.psum[bka][32:64, :], [g.pd[bka]], [qod[j]])
                    copy_op(kb, "act", qo[j][64:128, :], g.psum[bka][64:128, :], [g.pd[bka]], [qod[j]])
                    kb.dma("pool", sq["qt"](h, q0), qo[j][:, :], reads=[qod[j]])
        kb.barrier()


_ropetmp = {}


def rope_q(kb, g, qo, qod, bka, qsw, qswd, Ct, CSd, q0, st, small):
    key = id(st)
    if key not in _ropetmp:
        _ropetmp[key] = (kb.sb(st, [32, 512], F32, "rq"), Dep())
    t, td = _ropetmp[key]
    tt(kb, "dve", t[:, :], g.psum[bka][0:32, :], Ct[:, q0:q0 + 512], ALU.mult, [g.pd[bka], CSd], [td])
    tt(kb, "pool", qo[0:32, :], t[:, :], qsw[:, :], ALU.add, [td, qswd], [qod])


QK_SCALE = 96 ** -0.5


def phase_attn(kb, g, seqs, WkH, WvH):
    NKmax = max(sum(n for _, n in sq["kchunks"]) for sq in seqs)
    NCH = max(len(sq["kchunks"]) for sq in seqs)
    with ExitStack() as st:
        wk_b = kb.sb(st, [128, 2, NH * 128], BF16, "wk")
        wv_b = kb.sb(st, [128, 2, NH * 64], BF16, "wv")
        Wd = Dep()
        load_weight_bf16(kb, st, wk_b, Wd, WkH, 2, NH * 128)
        load_weight_bf16(kb, st, wv_b, Wd, WvH, 2, NH * 64, stage_cols=1024)
        ckvT = kb.sb(st, [128, 3, NKmax], BF16, "ckvT")
        ckvTd = Dep()
        KT = kb.sb(st, [128, NKmax], BF16, "KT")
        KTd = Dep()
        V = kb.sb(st, [128, NCH, 66], BF16, "V")
        Vd = Dep()
        QT = [kb.sb(st, [128, 2048], BF16, "QT") for _ in range(2)]
        QTd = [Dep() for _ in range(2)]
        P = [kb.sb(st, [128, 1024], BF16, "P") for _ in range(2)]
        Pd = [Dep() for _ in range(2)]
        oT = kb.sb(st, [128, 1024], F32, "oT")
        oTd = Dep()
        osm = [kb.sb(st, [128, 4, 64], F32, "osm") for _ in range(2)]
        osmd = [Dep() for _ in range(2)]
        rec = [kb.sb(st, [128, 4, 1], F32, "rec") for _ in range(2)]
        recd = [Dep() for _ in range(2)]
        kvin = [kb.sb(st, [128, 288], F32, "kvin") for _ in range(2)]
        kvind = [Dep() for _ in range(2)]
        kb.op("pool", lambda e: e.memset(V[:, :, 64:66], 1.0), [], [Vd])
        for sq in seqs:
            chunks = sq["kchunks"]
            NK = sum(n for _, n in chunks)
            coff = []
            c0 = 0
            for ci, (kvap, n) in enumerate(chunks):
                j = ci % 2
                kb.dma("sp" if ci % 2 == 0 else "pool", kvin[j][:n, :], kvap, writes=[kvind[j]])
                b = nextbank(g)
                pv = g.psum[b].rearrange("p (k t) -> p k t", k=4)
                for k, w in ((0, 128), (1, 128), (2, 32)):
                    kb.op("pe", lambda e, k=k, w=w: e.transpose(pv[:w, k, :n], kvin[j][:n, k * 128:k * 128 + w], g.ident_f[:n, :n]),
                          [kvind[j], g.ident_d], [g.pd[b]])
                copy_op(kb, "dve", ckvT[:, 0:2, c0:c0 + n], pv[:, 0:2, :n], [g.pd[b]], [ckvTd])
                copy_op(kb, "act", ckvT[0:32, 2, c0:c0 + n], pv[0:32, 2, :n], [g.pd[b]], [ckvTd])
                coff.append(c0)
                c0 += n
            for h in range(NH):
                qj = h % 2
                kb.dma("sp", QT[qj][:, :], sq["qt"](h), writes=[QTd[qj]])
                for bi, k0 in enumerate(range(0, NK, 512)):
                    kn = min(512, NK - k0)
                    b = 6 + bi % 2
                    mm(kb, g.psum[b][:, :kn], wk_b[:, 0, h * 128:(h + 1) * 128], ckvT[:, 0, k0:k0 + kn], True, False, [Wd, ckvTd], [g.pd[b]])
                    mm(kb, g.psum[b][:, :kn], wk_b[:, 1, h * 128:(h + 1) * 128], ckvT[:, 1, k0:k0 + kn], False, False, [Wd, ckvTd], [g.pd[b]])
                    mm(kb, g.psum[b][:, :kn], g.ident_b[0:32, :], ckvT[0:32, 2, k0:k0 + kn], False, True, [g.ident_d, ckvTd], [g.pd[b]])
                    copy_op(kb, ("dve", "pool")[bi % 2] if False else "dve", KT[:, k0:k0 + kn], g.psum[b][:, :kn], [g.pd[b]], [KTd])
                for gi, cg in enumerate(range(0, len(chunks), 8)):
                    cn = min(8, len(chunks) - cg)
                    b = 6 + gi % 2
                    for ci in range(cn):
                        n = chunks[cg + ci][1]
                        o = coff[cg + ci]
                        for k in range(2):
                            mm(kb, g.psum[b][:n, ci * 64:(ci + 1) * 64], ckvT[:, k, o:o + n], wv_b[:, k, h * 64:(h + 1) * 64], k == 0, k == 1,
                               [ckvTd, Wd], [g.pd[b]])
                    copy_op(kb, "act", V[:, cg:cg + cn, 0:64], g.psum[b][:, :cn * 64].rearrange("p (c d) -> p c d", d=64), [g.pd[b]], [Vd])
                for qsb in range(2):
                    for ci, (kvap, n) in enumerate(chunks):
                        o = coff[ci]
                        sb0 = 2 + 2 * (ci % 2)
                        pj = ci % 2
                        for i in range(2):
                            mm(kb, g.psum[sb0 + i][:n, :], KT[:, o:o + n], QT[qj][:, qsb * 1024 + i * 512:qsb * 1024 + (i + 1) * 512], True, True,
                               [KTd, QTd[qj]], [g.pd[sb0 + i]])
                        act(kb, P[pj][:n, :].rearrange("p (a b) -> p a b", a=2), g.pall[:n, sb0:sb0 + 2, :], AF.Exp,
                            [g.pd[sb0], g.pd[sb0 + 1]], [Pd[pj]], scale=QK_SCALE)
                        for i in range(2):
                            mm(kb, g.psum[i][:65, :], V[:n, ci, 0:65], P[pj][:n, i * 512:(i + 1) * 512], ci == 0, ci == len(chunks) - 1,
                               [Vd, Pd[pj]], [g.pd[i]])
                    copy_op(kb, "dve", oT[:65, :].rearrange("p (a b) -> p a b", a=2), g.pall[:65, 0:2, :], [g.pd[0], g.pd[1]], [oTd])
                    for half in range(2):
                        b = 6 + half
                        oj = half
                        pv = g.psum[b][:, 0:4 * 65].rearrange("p (t c) -> p t c", c=65)
                        for t in range(4):
                            q0 = half * 512 + t * 128
                            kb.op("pe", lambda e, t=t, q0=q0: e.transpose(pv[:, t, :], oT[:65, q0:q0 + 128], g.ident_f[:65, :65]),
                                  [oTd, g.ident_d], [g.pd[b]])
                        kb.op("dve", lambda e: e.reciprocal(out=rec[oj][:, :, :], in_=pv[:, :, 64:65]), [g.pd[b]], [recd[oj]])
                        tt(kb, "dve", osm[oj][:, :, :], pv[:, :, 0:64], rec[oj][:, :, :].broadcast_to([128, 4, 64]), ALU.mult,
                           [g.pd[b], recd[oj]], [osmd[oj]])
                        kb.dma("pool", sq["o"](qsb, half, h), osm[oj][:, :, :], reads=[osmd[oj]])
        kb.barrier()


XC = 2124


def phase_conf(kb, g, seqs, w_conf, cols_ap):
    with ExitStack() as st:
        wb = kb.sb(st, [128, 8, 1024], BF16, "wconf")
        Wd = Dep()
        load_weight_bf16(kb, st, wb, Wd, w_conf, 8, 1024)
        cols = kb.sb(st, [128, 20 + 124], F32, "cols")
        cd = Dep()
        kb.dma("sp", cols[:], cols_ap, writes=[cd])
        Dg = kb.sb(st, [128, 4, 31, 128], BF16, "Dg")
        Dgd = Dep()
        for j in range(4):
            for k in range(31):
                ts(kb, ("dve", "pool")[k % 2], Dg[:, j, k, :], g.ident_f[:, :], cols[:, 20 + j * 31 + k:20 + j * 31 + k + 1], None, ALU.mult, None,
                   [g.ident_d, cd], [Dgd])
        xin = [kb.sb(st, [128, D], F32, "xin") for _ in range(2)]
        xind = [Dep() for _ in range(2)]
        xT = kb.sb(st, [128, 8, XC], BF16, "xT")
        xTd = Dep()
        hT = kb.sb(st, [128, 4, XC], BF16, "hT")
        hTd = Dep()
        mask = kb.sb(st, [128, XC], F32, "mask")
        maskd = Dep()
        sg = [kb.sb(st, [128, 512], F32, "sg") for _ in range(2)]
        sgd = [Dep() for _ in range(2)]
        cc = kb.sb(st, [128, 4, 512], F32, "cc")
        ccd = Dep()
        cb = kb.sb(st, [128, 4, 512], BF16, "cb")
        cbd = Dep()
        sq = kb.sb(st, [128, 4, 512], BF16, "sq")
        sqd = Dep()
        mean = kb.sb(st, [128, 512], F32, "mean")
        rstd = kb.sb(st, [128, 512], F32, "rstd")
        std = Dep()
        yt = [kb.sb(st, [128, 512], F32, "yt") for _ in range(2)]
        ytd = [Dep() for _ in range(2)]
        yo = [kb.sb(st, [128, 512], BF16, "yo") for _ in range(2)]
        yod = [Dep() for _ in range(2)]
        for s_ in seqs:
            NC = s_.get("ncols", XC)
            kb.dma("sp", mask[:, :NC], s_["mask"].broadcast_to([128, NC]), writes=[maskd])
            for ti, t0 in enumerate(range(0, NC, 128)):
                n = min(128, NC - t0)
                j = ti % 2
                kb.dma("sp", xin[j][:n, :], s_["x"][t0:t0 + n, :], writes=[xind[j]])
                transpose_tile(kb, g, xin[j], xind[j], n, xT, xTd, t0)
            for bi, c0 in enumerate(range(0, NC, 512)):
                cn = min(512, NC - c0)
                for j in range(4):
                    ba, bg = nextbank(g), nextbank(g)
                    for k in range(8):
                        mm(kb, g.psum[ba][:, :cn], wb[:, k, j * 128:(j + 1) * 128], xT[:, k, c0:c0 + cn], k == 0, k == 7, [Wd, xTd], [g.pd[ba]])
                    for k in range(8):
                        mm(kb, g.psum[bg][:, :cn], wb[:, k, 512 + j * 128:512 + (j + 1) * 128], xT[:, k, c0:c0 + cn], k == 0, k == 7, [Wd, xTd], [g.pd[bg]])
                    q = j % 2
                    act(kb, sg[q][:, :cn], g.psum[bg][:, :cn], AF.Sigmoid, [g.pd[bg], cd], [sgd[q]], bias=cols[:, 4 + j:5 + j], scale=1.0)
                    stt(kb, "dve", sg[q][:, :cn], g.psum[ba][:, :cn], cols[:, j:j + 1], sg[q][:, :cn], ALU.add, ALU.mult, [g.pd[ba], cd, sgd[q]], [sgd[q]])
                    tt(kb, "pool", hT[:, j, c0:c0 + cn], sg[q][:, :cn], mask[:, c0:c0 + cn], ALU.mult, [sgd[q], maskd], [hTd])
            blocks = s_.get("blocks") or ([(15, 16)] + [(61 + 512 * i, 512) for i in range(4)])
            for bi, (c0, cn) in enumerate(blocks):
                for j in range(4):
                    b = nextbank(g)
                    for k in range(31):
                        mm(kb, g.psum[b][:, :cn], Dg[:, j, k, :], hT[:, j, c0 + k - 15:c0 + k - 15 + cn], k == 0, k == 30, [Dgd, hTd], [g.pd[b]])
                    act(kb, cc[:, j, :cn], g.psum[b][:, :cn], AF.Identity, [g.pd[b], cd], [ccd], bias=cols[:, 8 + j:9 + j], scale=1.0)
                    copy_op(kb, "pool", cb[:, j, :cn], cc[:, j, :cn], [ccd], [cbd])
                    tt(kb, "dve", sq[:, j, :cn], cc[:, j, :cn], cc[:, j, :cn], ALU.mult, [ccd], [sqd])
                b1, b2 = nextbank(g), nextbank(g)
                for j in range(4):
                    mm(kb, g.psum[b1][:, :cn], g.ones_b[:, :], cb[:, j, :cn], j == 0, j == 3, [g.ones_d, cbd], [g.pd[b1]])
                for j in range(4):
                    mm(kb, g.psum[b2][:, :cn], g.ones_b[:, :], sq[:, j, :cn], j == 0, j == 3, [g.ones_d, sqd], [g.pd[b2]])
                act(kb, mean[:, :cn], g.psum[b1][:, :cn], AF.Copy, [g.pd[b1]], [std], scale=1.0 / 512)
                tt(kb, "pool", rstd[:, :cn], mean[:, :cn], mean[:, :cn], ALU.mult, [std], [std])
                stt(kb, "dve", rstd[:, :cn], g.psum[b2][:, :cn], 1.0 / 512, rstd[:, :cn], ALU.mult, ALU.subtract, [g.pd[b2], std], [std])
                act(kb, rstd[:, :cn], rstd[:, :cn], AF.Sqrt, [std], [std], bias=LN_EPS, scale=1.0)
                kb.op("dve", lambda e: e.reciprocal(out=rstd[:, :cn], in_=rstd[:, :cn]), [std], [std])
                for j in range(4):
                    q = j % 2
                    tt(kb, "pool", yt[q][:, :cn], cc[:, j, :cn], mean[:, :cn], ALU.subtract, [ccd, std], [ytd[q]])
                    tt(kb, "dve", yt[q][:, :cn], yt[q][:, :cn], rstd[:, :cn], ALU.mult, [ytd[q], std], [ytd[q]])
                    ts(kb, "dve", yt[q][:, :cn], yt[q][:, :cn], cols[:, 12 + j:13 + j], cols[:, 16 + j:17 + j], ALU.mult, ALU.add, [ytd[q], cd], [ytd[q]])
                    act(kb, yo[q][:, :cn], yt[q][:, :cn], AF.Silu, [ytd[q]], [yod[q]])
                    kb.dma("pool", s_["out"](j, bi, cn), yo[q][:, :cn], reads=[yod[q]])
        kb.barrier()


I32 = mybir.dt.int32
TWO_PI = 2.0 * math.pi


class FCfg:
    def __init__(self, L, rows, N1, nq, CB):
        self.L, self.rows, self.N1, self.nq, self.CB = L, rows, N1, nq, CB
        self.N2 = 86 * nq
        self.N = N1 * self.N2
        self.NF = N1 // 2 + 1
        assert self.N >= 2 * L - 1 and rows * self.N2 >= L


CFG_P = FCfg(16400, 64, 128, 3, 8)
CFG_S = FCfg(2064, 24, 48, 1, 32)


def fft_tables(cfg):
    N1, N2, N, rows, nq, NF = cfg.N1, cfg.N2, cfg.N, cfg.rows, cfg.nq, cfg.NF
    n1 = np.arange(rows)[:, None].astype(np.float64)
    k1 = np.arange(NF)[None, :].astype(np.float64)
    a = 2 * np.pi * n1 * k1 / N1
    F1 = np.concatenate([np.cos(a), -np.sin(a)], 1)
    n2 = np.arange(N2)[:, None].astype(np.float64)
    a = 2 * np.pi * n2 * k1 / N
    tw = np.stack([np.cos(a), -np.sin(a)], 1)
    tw = tw.reshape(nq, 86, 2, NF).transpose(1, 0, 2, 3)
    m = np.arange(N2)[None, :].astype(np.float64)
    a = 2 * np.pi * n2 * m / N2
    F2 = np.stack([np.cos(a), -np.sin(a), np.sin(a)], 0)
    F2 = F2.reshape(3, nq, 86, N2).transpose(2, 0, 1, 3)
    kk = np.arange(NF)[:, None].astype(np.float64)
    a = 2 * np.pi * kk * np.arange(N2)[None, :] / N
    twc = np.stack([np.cos(a), np.sin(a)], 1)
    a = 2 * np.pi * kk * np.arange(rows)[None, :] / N1
    wgt = np.full((NF, 1), 2.0)
    wgt[0, 0] = 1.0
    wgt[NF - 1, 0] = 1.0
    G1 = np.stack([wgt * np.cos(a) / N, -wgt * np.sin(a) / N], 1)
    bf = ml_dtypes.bfloat16
    return dict(F1=F1.astype(np.float32).astype(bf), tw=np.ascontiguousarray(tw).astype(np.float32),
                F2=np.ascontiguousarray(F2).astype(np.float32).astype(bf), twc=twc.astype(np.float32),
                G1=G1.astype(np.float32).astype(bf))


class FTab:
    pass


def fft_load_tables(kb, st, cfg, tabs):
    t = FTab()
    t.d = Dep()
    t.F1 = kb.sb(st, [cfg.rows, 2 * cfg.NF], BF16, "F1")
    t.tw = kb.sb(st, [86, cfg.nq, 2, cfg.NF], F32, "tw")
    t.F2 = kb.sb(st, [86, 3, cfg.nq, cfg.N2], BF16, "F2")
    t.twc = kb.sb(st, [cfg.NF, 2, cfg.N2], F32, "twc")
    t.G1 = kb.sb(st, [cfg.NF, 2, cfg.rows], BF16, "G1")
    for nm in ("F1", "tw", "F2", "twc", "G1"):
        kb.dma("sp", getattr(t, nm)[:], tabs[nm], writes=[t.d])
    return t


class FBuf:
    pass


def fft_alloc(kb, st, cfg, nsets=1):
    CB, nq, N1, N2, rows = cfg.CB, cfg.nq, cfg.NF, cfg.N2, cfg.rows
    E = CB * nq * N1
    E2 = CB * N2
    tn = max(E, E2)
    Ab = kb.sb(st, [86, CB * nq, 2, N1], BF16, "Ab")
    Abd = Dep()
    Xs = kb.sb(st, [86, CB * nq, 2, N1], F32, "Xs")
    Xsd = Dep()
    t = [kb.sb(st, [128, tn], F32, "ft") for _ in range(4)]
    td = [Dep() for _ in range(4)]
    sets = []
    for _ in range(nsets):
        b = FBuf()
        b.src_f = kb.sb(st, [rows, CB, N2], F32, "srcf")
        b.src_fd = Dep()
        b.src_b = kb.sb(st, [rows, CB, N2], BF16, "srcb")
        b.src_bd = Dep()
        b.As = kb.sb(st, [86, CB * nq, 2, N1], F32, "As")
        b.Asd = Dep()
        b.Ab, b.Abd, b.Xs, b.Xsd, b.t, b.td = Ab, Abd, Xs, Xsd, t, td
        sets.append(b)
    return sets if nsets > 1 else sets[0]


import os
CMUL_ENG = os.environ.get("CMUL_ENG", "dve,dve,dve,dve,dve,dve").split(",")


def cmul_batched(kb, cfg, b, P, shape, Are, Aim, Br, Bi, out_re, out_im, rdeps, wdep, conj=False):
    n = int(np.prod(shape))
    pat = {2: "p (a b) -> p a b", 3: "p (a b c) -> p a b c"}[len(shape)]
    kw = dict(zip("abc", shape))
    kw.pop("a")
    tv = [b.t[i][:P, :n].rearrange(pat, **kw) for i in range(4)]
    e = CMUL_ENG
    tt(kb, e[0], tv[0], Are, Br, ALU.mult, rdeps, [b.td[0]])
    tt(kb, e[1], tv[1], Aim, Bi, ALU.mult, rdeps, [b.td[1]])
    tt(kb, e[2], tv[2], Are, Bi, ALU.mult, rdeps, [b.td[2]])
    tt(kb, e[3], tv[3], Aim, Br, ALU.mult, rdeps, [b.td[3]])
    tt(kb, e[4], out_re, tv[0], tv[1], ALU.subtract, [b.td[0], b.td[1]], [wdep])
    tt(kb, e[5], out_im, tv[2], tv[3], ALU.add, [b.td[2], b.td[3]], [wdep])


def fft_s1(kb, g, cfg, tb, b, cb):
    nq, N1, N2, rows = cfg.nq, cfg.NF, cfg.N2, cfg.rows
    per = 512 // (2 * N1)
    tot = cb * nq
    for i0 in range(0, tot, per):
        cnt = min(per, tot - i0)
        bk = nextbank(g)
        for i in range(i0, i0 + cnt):
            c, q = divmod(i, nq)
            mm(kb, g.psum[bk][:86, (i - i0) * 2 * N1:(i - i0 + 1) * 2 * N1], b.src_b[:rows, c, q * 86:(q + 1) * 86], tb.F1[:rows, :], True, True,
               [b.src_bd, tb.d], [g.pd[bk]])
        copy_op(kb, "act", b.As[:, i0:i0 + cnt, :, :], g.psum[bk][:86, :cnt * 2 * N1].rearrange("p (i r k) -> p i r k", r=2, k=N1), [g.pd[bk]], [b.Asd])


def fft_s2(kb, g, cfg, tb, b, cb):
    nq, N1, N2, rows = cfg.nq, cfg.NF, cfg.N2, cfg.rows
    per = 512 // (2 * N1)
    tot = cb * nq
    Av = b.As[:, :tot, :, :].rearrange("p (c q) r k -> p c q r k", q=nq)
    Abv = b.Ab[:, :tot, :, :].rearrange("p (c q) r k -> p c q r k", q=nq)
    twr = tb.tw[:, :, 0, :].unsqueeze(1).broadcast_to([86, cb, nq, N1])
    twi = tb.tw[:, :, 1, :].unsqueeze(1).broadcast_to([86, cb, nq, N1])
    cmul_batched(kb, cfg, b, 86, (cb, nq, N1), Av[:, :, :, 0, :], Av[:, :, :, 1, :], twr, twi, Abv[:, :, :, 0, :], Abv[:, :, :, 1, :],
                 [b.Asd, tb.d], b.Abd)
    for i0 in range(0, tot, per):
        cnt = min(per, tot - i0)
        bk = nextbank(g)
        for i in range(i0, i0 + cnt):
            c, p = divmod(i, nq)
            reg = g.psum[bk][:86, (i - i0) * 2 * N1:(i - i0 + 1) * 2 * N1]
            for q in range(nq):
                blk = slice(p * 86, (p + 1) * 86)
                mm(kb, reg, tb.F2[:, 0, q, blk], b.Ab[:, c * nq + q, :, :].rearrange("p r k -> p (r k)"), q == 0, False, [tb.d, b.Abd], [g.pd[bk]])
                mm(kb, reg[:, 0:N1], tb.F2[:, 2, q, blk], b.Ab[:, c * nq + q, 1, :], False, False, [tb.d, b.Abd], [g.pd[bk]])
                mm(kb, reg[:, N1:2 * N1], tb.F2[:, 1, q, blk], b.Ab[:, c * nq + q, 0, :], False, q == nq - 1, [tb.d, b.Abd], [g.pd[bk]])
        copy_op(kb, "act", b.Xs[:, i0:i0 + cnt, :, :], g.psum[bk][:86, :cnt * 2 * N1].rearrange("p (i r k) -> p i r k", r=2, k=N1), [g.pd[bk]], [b.Xsd])


def fft_fwd(kb, g, cfg, tb, b, cb):
    fft_s1(kb, g, cfg, tb, b, cb)
    fft_s2(kb, g, cfg, tb, b, cb)


def pipeline2(items, stage_a, stage_b, depth=2):
    if depth < 2:
        for it in items:
            stage_a(it)
            stage_b(it)
        return
    prev = None
    for it in items:
        stage_a(it)
        if prev is not None:
            stage_b(prev)
        prev = it
    if prev is not None:
        stage_b(prev)


def fft_layout_dma(kb, q, cfg, tile, tiled, dram2d, c0, cb, to_sbuf):
    L, N2, rows = cfg.L, cfg.N2, cfg.rows
    full = L // N2
    rem = L - full * N2
    dv = dram2d[c0:c0 + cb, 0:full * N2].rearrange("c (a b) -> a c b", b=N2)
    if to_sbuf:
        kb.dma(q, tile[:full, :cb, :], dv, writes=[tiled])
        if rem:
            kb.dma(q, tile[full:full + 1, :cb, :rem], dram2d[c0:c0 + cb, full * N2:L].unsqueeze(0), writes=[tiled])
    else:
        kb.dma(q, dv, tile[:full, :cb, :], reads=[tiled])
        if rem:
            kb.dma(q, dram2d[c0:c0 + cb, full * N2:L].unsqueeze(0), tile[full:full + 1, :cb, :rem], reads=[tiled])


def phase_hy_conv(kb, g, cfg, tabs, taps, Hs, seqs, dskip):
    CB, nq, N1, N2, rows, L = cfg.CB, cfg.nq, cfg.NF, cfg.N2, cfg.rows, cfg.L
    with ExitStack() as st:
        tb = fft_load_tables(kb, st, cfg, tabs)
        bs = [fft_alloc(kb, st, cfg, nsets=1)]
        for b in bs:
            kb.op("pool", lambda e, b=b: e.memset(b.src_f[:, :, :], 0.0), [], [b.src_fd])
        X0 = kb.sb(st, [86, CB * nq, 2, N1], F32, "X0")
        X0d = Dep()
        Hb = [kb.sb(st, [86, CB * nq, 2, N1], F32, "Hb") for _ in range(len(bs))]
        Hbd = [Dep() for _ in range(len(bs))]
        items = [(c0, d, bs[i % len(bs)]) for i, (c0, d) in enumerate((c0, d) for c0 in range(0, 64, CB) for d in range(2))]

        def sp_a(it):
            c0, d, b = it
            fft_layout_dma(kb, "sp", cfg, b.src_f, b.src_fd, taps[d], c0, CB, True)
            copy_op(kb, "dve", b.src_b[:, :, :], b.src_f[:, :, :], [b.src_fd], [b.src_bd])
            fft_s1(kb, g, cfg, tb, b, CB)

        def sp_b(it):
            c0, d, b = it
            fft_s2(kb, g, cfg, tb, b, CB)
            if d == 0:
                copy_op(kb, "act", X0[:, :, :, :], b.Xs[:, :, :, :], [b.Xsd], [X0d])
            else:
                tt(kb, "dve", Hb[0][:, :, 0, :], X0[:, :, 0, :], b.Xs[:, :, 0, :], ALU.add, [X0d, b.Xsd], [Hbd[0]])
                tt(kb, "dve", Hb[0][:, :, 1, :], X0[:, :, 1, :], b.Xs[:, :, 1, :], ALU.subtract, [X0d, b.Xsd], [Hbd[0]])
                kb.dma("pool", Hs[:, c0 * nq:(c0 + CB) * nq, :, :], Hb[0][:, :, :, :], reads=[Hbd[0]])
        pipeline2(items, sp_a, sp_b, depth=len(bs))
        kb.barrier()
        Yb = kb.sb(st, [86, CB * nq, 2, N1], BF16, "Yb")
        Ybd = Dep()
        Bs = kb.sb(st, [N1, CB, 2, N2], F32, "Bs")
        Bsd = Dep()
        Bb = kb.sb(st, [N1, CB, 2, N2], BF16, "Bb")
        Bbd = Dep()
        x0f = [kb.sb(st, [rows, CB, N2], F32, "x0f") for _ in range(len(bs))]
        x0d = [Dep() for _ in range(len(bs))]
        cv = kb.sb(st, [rows, CB, N2], F32, "cv")
        cvd = Dep()
        yo = kb.sb(st, [rows, CB, N2], BF16, "yo")
        yod = Dep()
        dsk = kb.sb(st, [128, 64], F32, "dsk")
        dskd = Dep()
        kb.dma("sp", dsk[:, :], dskip.broadcast_to([128, 64]), writes=[dskd])
        perb = 512 // N2
        citems = [(sq, c0, i % len(bs)) for i, (sq, c0) in enumerate((sq, c0) for sq in seqs for c0 in range(0, 64, CB))]

        def cv_a(it):
            sq, c0, k = it
            b = bs[k]
            fft_layout_dma(kb, "sp", cfg, b.src_f, b.src_fd, sq["z"], c0, CB, True)
            fft_layout_dma(kb, "sp", cfg, x0f[k], x0d[k], sq["x0"], c0, CB, True)
            kb.dma("sp", Hb[k][:, :, :, :], Hs[:, c0 * nq:(c0 + CB) * nq, :, :], writes=[Hbd[k]])
            copy_op(kb, "dve", b.src_b[:, :, :], b.src_f[:, :, :], [b.src_fd], [b.src_bd])
            fft_s1(kb, g, cfg, tb, b, CB)

        def cv_b(it):
            sq, c0, k = it
            b = bs[k]
            fft_s2(kb, g, cfg, tb, b, CB)
            cmul_batched(kb, cfg, b, 86, (CB * nq, N1), b.Xs[:, :, 0, :], b.Xs[:, :, 1, :], Hb[k][:, :, 0, :], Hb[k][:, :, 1, :],
                         Yb[:, :, 0, :], Yb[:, :, 1, :], [b.Xsd, Hbd[k]], Ybd)
            tot = CB * 2
            for i0 in range(0, tot, perb):
                cnt = min(perb, tot - i0)
                bk = nextbank(g)
                for i in range(i0, i0 + cnt):
                    c, ri = divmod(i, 2)
                    reg = g.psum[bk][:N1, (i - i0) * N2:(i - i0 + 1) * N2]
                    for p in range(nq):
                        ya_re, ya_im = Yb[:, c * nq + p, 0, :], Yb[:, c * nq + p, 1, :]
                        if ri == 0:
                            mm(kb, reg, ya_re, tb.F2[:, 0, p, :], p == 0, False, [Ybd, tb.d], [g.pd[bk]])
                            mm(kb, reg, ya_im, tb.F2[:, 1, p, :], False, p == nq - 1, [Ybd, tb.d], [g.pd[bk]])
                        else:
                            mm(kb, reg, ya_re, tb.F2[:, 2, p, :], p == 0, False, [Ybd, tb.d], [g.pd[bk]])
                            mm(kb, reg, ya_im, tb.F2[:, 0, p, :], False, p == nq - 1, [Ybd, tb.d], [g.pd[bk]])
                copy_op(kb, "act", Bs[:, :, :, :].rearrange("p c r n -> p (c r) n")[:, i0:i0 + cnt, :],
                        g.psum[bk][:N1, :cnt * N2].rearrange("p (i n) -> p i n", n=N2), [g.pd[bk]], [Bsd])
            twr = tb.twc[:, 0, :].unsqueeze(1).broadcast_to([N1, CB, N2])
            twi = tb.twc[:, 1, :].unsqueeze(1).broadcast_to([N1, CB, N2])
            cmul_batched(kb, cfg, b, N1, (CB, N2), Bs[:, :, 0, :], Bs[:, :, 1, :], twr, twi, Bb[:, :, 0, :], Bb[:, :, 1, :], [Bsd, tb.d], Bbd)
            for i0 in range(0, CB, perb):
                cnt = min(perb, CB - i0)
                bk = nextbank(g)
                for c in range(i0, i0 + cnt):
                    reg = g.psum[bk][:rows, (c - i0) * N2:(c - i0 + 1) * N2]
                    mm(kb, reg, tb.G1[:, 0, :], Bb[:, c, 0, :], True, False, [tb.d, Bbd], [g.pd[bk]])
                    mm(kb, reg, tb.G1[:, 1, :], Bb[:, c, 1, :], False, True, [tb.d, Bbd], [g.pd[bk]])
                copy_op(kb, "act", cv[:, i0:i0 + cnt, :], g.psum[bk][:rows, :cnt * N2].rearrange("p (i n) -> p i n", n=N2), [g.pd[bk]], [cvd])
            tt(kb, "dve", b.src_f[:, :, :], b.src_f[:, :, :], dsk[:rows, c0:c0 + CB].unsqueeze(2).broadcast_to([rows, CB, N2]), ALU.mult,
               [b.src_fd, dskd], [b.src_fd])
            tt(kb, "dve", cv[:, :, :], cv[:, :, :], b.src_f[:, :, :], ALU.add, [cvd, b.src_fd], [cvd])
            tt(kb, "dve", yo[:, :, :], cv[:, :, :], x0f[k][:, :, :], ALU.mult, [cvd, x0d[k]], [yod])
            fft_layout_dma(kb, "pool", cfg, yo, yod, sq["ya"], c0, CB, False)
        pipeline2(citems, cv_a, cv_b, depth=len(bs))
        kb.barrier()


def phase_hy_inproj(kb, g, seqs, w_hy, brow, hcols, G=1):
    with ExitStack() as st:
        wb = kb.sb(st, [128, 8, G * 192], BF16, "why")
        Wd = Dep()
        load_weight_bf16(kb, st, wb, Wd, w_hy, 8, G * 192, stage_cols=1536)
        hc = kb.sb(st, [64, G * 12], F32, "hc")
        hcd = Dep()
        kb.dma("sp", hc[:, :], hcols, writes=[hcd])
        brf = kb.sb(st, [1, G * 192], F32, "brf")
        brb = kb.sb(st, [1, G * 192], BF16, "brb")
        brd = Dep()
        kb.dma("sp", brf[:, :], brow, writes=[brd])
        copy_op(kb, "dve", brb[:, :], brf[:, :], [brd], [brd])
        xin = [kb.sb(st, [128, D], F32, "xin") for _ in range(4)]
        xind = [Dep() for _ in range(4)]
        xT = [kb.sb(st, [128, 8, 512], BF16, "xT") for _ in range(2)]
        xTd = [Dep() for _ in range(2)]
        vf = [kb.sb(st, [1, 512], F32, "vf") for _ in range(2)]
        vb = [kb.sb(st, [1, 512], BF16, "vb") for _ in range(2)]
        vd = [Dep() for _ in range(2)]
        o3 = [[kb.sb(st, [64, 512], F32, "o3") for _ in range(3)] for _ in range(2)]
        o3d = [[Dep() for _ in range(3)] for _ in range(2)]
        bi = 0
        oi = 0
        for sq in seqs:
            L = sq["L"]
            for t0 in range(0, L, 510):
                no = min(510, L - t0)
                ni = no + 2
                j = bi % 2
                bi += 1
                for ti, r0 in enumerate(range(0, ni, 128)):
                    n = min(128, ni - r0)
                    kb.dma("sp", xin[ti][:n, :], sq["xh"][t0 + r0:t0 + r0 + n, :], writes=[xind[ti]])
                    transpose_tile(kb, g, xin[ti], xind[ti], n, xT[j], xTd[j], r0)
                kb.dma("sp", vf[j][:, :ni], sq["valid"][:, t0:t0 + ni], writes=[vd[j]])
                copy_op(kb, "dve", vb[j][:, :ni], vf[j][:, :ni], [vd[j]], [vd[j]])
                for gg in range(G):
                    oj = oi % 2
                    oi += 1
                    for gi in range(3):
                        c0 = gg * 192 + gi * 64
                        h0 = gg * 12 + gi * 4
                        bk = nextbank(g)
                        for k in range(8):
                            mm(kb, g.psum[bk][:64, :ni], wb[:, k, c0:c0 + 64], xT[j][:, k, :ni], k == 0, False, [Wd, xTd[j]], [g.pd[bk]])
                        mm(kb, g.psum[bk][:64, :ni], brb[:, c0:c0 + 64], vb[j][:, :ni], False, True, [brd, vd[j]], [g.pd[bk]])
                        o = o3[oj][gi]
                        od = o3d[oj][gi]
                        act(kb, o[:, :no], g.psum[bk][:64, 1:1 + no], AF.Identity, [g.pd[bk], hcd], [od],
                            scale=hc[:, h0 + 1:h0 + 2], bias=hc[:, h0 + 3:h0 + 4])
                        stt(kb, "dve", o[:, :no], g.psum[bk][:64, 0:no], hc[:, h0:h0 + 1], o[:, :no], ALU.mult, ALU.add, [g.pd[bk], hcd, od], [od])
                        stt(kb, "dve", o[:, :no], g.psum[bk][:64, 2:2 + no], hc[:, h0 + 2:h0 + 3], o[:, :no], ALU.mult, ALU.add, [g.pd[bk], hcd, od], [od])
                    tt(kb, "pool", o3[oj][1][:, :no], o3[oj][1][:, :no], o3[oj][2][:, :no], ALU.mult, [o3d[oj][1], o3d[oj][2]], [o3d[oj][1]])
                    kb.dma("pool", sq["x0"][gg][:, t0:t0 + no], o3[oj][0][:, :no], reads=[o3d[oj][0]])
                    kb.dma("pool", sq["z"][gg][:, t0:t0 + no], o3[oj][1][:, :no], reads=[o3d[oj][1]])
        kb.barrier()


def sin_reduced(kb, out, outd, src_ps, fcol, fbcol, tmps, tmpd, ki, kid, n, reads):
    a, r = tmps
    ts(kb, "dve", a[:, :n], src_ps, fcol, fbcol, ALU.mult, ALU.add, reads, [tmpd[0]])
    ts(kb, "pool", r[:, :n], a[:, :n], 1.0 / TWO_PI, None, ALU.mult, None, [tmpd[0]], [tmpd[1]])
    copy_op(kb, "dve", ki[:, :n], r[:, :n], [tmpd[1]], [kid])
    copy_op(kb, "pool", r[:, :n], ki[:, :n], [kid], [tmpd[1]])
    stt(kb, "dve", r[:, :n], r[:, :n], -TWO_PI, a[:, :n], ALU.mult, ALU.add, [tmpd[0], tmpd[1]], [tmpd[1]])
    ts(kb, "pool", r[:, :n], r[:, :n], -3.1415925, 3.1415925, ALU.max, ALU.min, [tmpd[1]], [tmpd[1]])
    return act(kb, out, r[:, :n], AF.Sin, [tmpd[1]], [outd])


def phase_hy_filters(kb, g, L, zposT, fw, taps_out):
    with ExitStack() as st:
        w1 = kb.sb(st, [33, 2, 64], F32, "fw1")
        w2 = kb.sb(st, [64, 2, 64], F32, "fw2")
        w3 = kb.sb(st, [64, 2, 64], F32, "fw3")
        fc = kb.sb(st, [64, 2, 8], F32, "fc")
        Wd = Dep()
        kb.dma("sp", w1[:, :, :], fw["w1"].rearrange("d e f -> e d f"), writes=[Wd])
        kb.dma("sp", w2[:, :, :], fw["w2"].rearrange("d e f -> e d f"), writes=[Wd])
        kb.dma("sp", w3[:, :, :], fw["w3"].rearrange("d e f -> e d f"), writes=[Wd])
        kb.dma("sp", fc[:, :, 0:5], fw["fcols"], writes=[Wd])
        tt(kb, "dve", fc[:, :, 5:6], fc[:, :, 0:1], fc[:, :, 1:2], ALU.mult, [Wd], [Wd])
        tt(kb, "dve", fc[:, :, 6:7], fc[:, :, 2:3], fc[:, :, 3:4], ALU.mult, [Wd], [Wd])
        ts(kb, "dve", fc[:, :, 7:8], fc[:, :, 4:5], -1.0, None, ALU.mult, None, [Wd], [Wd])
        taps = kb.sb(st, [64, 2, L], F32, "taps")
        tapsd = Dep()
        zp = [kb.sb(st, [33, 512], F32, "zp") for _ in range(2)]
        zpd = [Dep() for _ in range(2)]
        tb_ = [kb.sb(st, [64, 512], F32, "tbc") for _ in range(2)]
        tbd = [Dep() for _ in range(2)]
        tmps = [kb.sb(st, [64, 512], F32, "ftmp") for _ in range(2)]
        tmpd = [Dep(), Dep()]
        ki = kb.sb(st, [64, 512], I32, "ki")
        kid = Dep()
        h1 = kb.sb(st, [64, 512], F32, "h1")
        h1d = Dep()
        h2 = kb.sb(st, [64, 512], F32, "h2")
        h2d = Dep()
        ex = kb.sb(st, [64, 512], F32, "ex")
        exd = Dep()
        ss = kb.sb(st, [64, 2 * ((L + 511) // 512) + 4], F32, "ss")
        ssd = Dep()
        junk = kb.sb(st, [64, 512], F32, "fjunk")
        junkd = Dep()
        nb = (L + 511) // 512
        for bi, l0 in enumerate(range(0, L, 512)):
            n = min(512, L - l0)
            j = bi % 2
            kb.dma("sp", zp[j][:, :n], zposT[:, l0:l0 + n], writes=[zpd[j]])
            kb.dma("pool", tb_[j][:, :n], zposT[0:1, l0:l0 + n].broadcast_to([64, n]), writes=[tbd[j]])
            for d in range(2):
                bk = nextbank(g)
                mm(kb, g.psum[bk][:64, :n], w1[:, d, :], zp[j][:, :n], True, True, [Wd, zpd[j]], [g.pd[bk]])
                sin_reduced(kb, h1[:, :n], h1d, g.psum[bk][:64, :n], fc[:, d, 0:1], fc[:, d, 5:6], tmps, tmpd, ki, kid, n, [g.pd[bk], Wd])
                bk = nextbank(g)
                mm(kb, g.psum[bk][:64, :n], w2[:, d, :], h1[:, :n], True, True, [Wd, h1d], [g.pd[bk]])
                sin_reduced(kb, h2[:, :n], h2d, g.psum[bk][:64, :n], fc[:, d, 2:3], fc[:, d, 6:7], tmps, tmpd, ki, kid, n, [g.pd[bk], Wd])
                bk = nextbank(g)
                mm(kb, g.psum[bk][:64, :n], w3[:, d, :], h2[:, :n], True, True, [Wd, h2d], [g.pd[bk]])
                act(kb, ex[:, :n], tb_[j][:, :n], AF.Exp, [tbd[j], Wd], [exd], scale=fc[:, d, 7:8])
                tt(kb, "dve", taps[:, d, l0:l0 + n], g.psum[bk][:64, :n], ex[:, :n], ALU.mult, [g.pd[bk], exd], [tapsd])
                if d == 1 and l0 == 0:
                    kb.op("pool", lambda e: e.memset(taps[:, 1, 0:1], 0.0), [], [tapsd])
                act(kb, junk[:, :n], taps[:, d, l0:l0 + n], AF.Square, [tapsd], [junkd, ssd], accum_out=ss[:, 2 * bi + d:2 * bi + d + 1])
        tot, nrm = ss[:, 2 * nb:2 * nb + 1], ss[:, 2 * nb + 1:2 * nb + 2]
        kb.op("dve", lambda e: e.tensor_reduce(out=tot, in_=ss[:, 0:2 * nb], axis=AX.X, op=ALU.add), [ssd], [ssd])
        act(kb, nrm, tot, AF.Sqrt, [ssd], [ssd])
        kb.op("dve", lambda e: e.reciprocal(out=nrm, in_=nrm), [ssd], [ssd])
        for d in range(2):
            for l0 in range(0, L, 4096):
                n = min(4096, L - l0)
                ts(kb, ("dve", "pool")[d], taps[:, d, l0:l0 + n], taps[:, d, l0:l0 + n], nrm, None, ALU.mult, None, [tapsd, ssd], [tapsd])
            kb.dma("sp", taps_out[d], taps[:, d, :], reads=[tapsd])
        kb.barrier()


def phase_hy_filter_h2(kb, g, L, zposT, fw, h2_out):
    with ExitStack() as st:
        w1 = kb.sb(st, [33, 2, 64], F32, "fw1")
        w2 = kb.sb(st, [64, 2, 64], F32, "fw2")
        fc = kb.sb(st, [64, 2, 8], F32, "fc")
        Wd = Dep()
        kb.dma("sp", w1[:, :, :], fw["w1"].rearrange("d e f -> e d f"), writes=[Wd])
        kb.dma("sp", w2[:, :, :], fw["w2"].rearrange("d e f -> e d f"), writes=[Wd])
        kb.dma("sp", fc[:, :, 0:5], fw["fcols"], writes=[Wd])
        tt(kb, "dve", fc[:, :, 5:6], fc[:, :, 0:1], fc[:, :, 1:2], ALU.mult, [Wd], [Wd])
        tt(kb, "dve", fc[:, :, 6:7], fc[:, :, 2:3], fc[:, :, 3:4], ALU.mult, [Wd], [Wd])
        zp = [kb.sb(st, [33, 512], F32, "zp") for _ in range(2)]
        zpd = [Dep() for _ in range(2)]
        NQ = 3
        tmps = [[kb.sb(st, [64, 512], F32, "ftmp") for _ in range(2)] for _ in range(NQ)]
        tmpd = [[Dep(), Dep()] for _ in range(NQ)]
        ki = [kb.sb(st, [64, 512], I32, "ki") for _ in range(NQ)]
        kid = [Dep() for _ in range(NQ)]
        h1 = [kb.sb(st, [64, 512], F32, "h1") for _ in range(NQ)]
        h1d = [Dep() for _ in range(NQ)]
        h2 = [kb.sb(st, [64, 512], F32, "h2") for _ in range(NQ)]
        h2d = [Dep() for _ in range(NQ)]
        it = 0
        for bi, l0 in enumerate(range(0, L, 512)):
            n = min(512, L - l0)
            j = bi % 2
            kb.dma("sp", zp[j][:, :n], zposT[:, l0:l0 + n], writes=[zpd[j]])
            for d in range(2):
                q = it % NQ
                it += 1
                bk = nextbank(g)
                mm(kb, g.psum[bk][:64, :n], w1[:, d, :], zp[j][:, :n], True, True, [Wd, zpd[j]], [g.pd[bk]])
                sin_reduced(kb, h1[q][:, :n], h1d[q], g.psum[bk][:64, :n], fc[:, d, 0:1], fc[:, d, 5:6], tmps[q], tmpd[q], ki[q], kid[q], n, [g.pd[bk], Wd])
                bk = nextbank(g)
                mm(kb, g.psum[bk][:64, :n], w2[:, d, :], h1[q][:, :n], True, True, [Wd, h1d[q]], [g.pd[bk]])
                sin_reduced(kb, h2[q][:, :n], h2d[q], g.psum[bk][:64, :n], fc[:, d, 2:3], fc[:, d, 6:7], tmps[q], tmpd[q], ki[q], kid[q], n, [g.pd[bk], Wd])
                kb.dma("pool", h2_out[d][:, l0:l0 + n], h2[q][:, :n], reads=[h2d[q]])
        kb.barrier()


def phase_hy_filter_taps(kb, g, L, zposT, h2_in, w3_ap, fcols_ap, taps_out):
    with ExitStack() as st:
        w3 = kb.sb(st, [64, 2, 64], F32, "fw3")
        fc = kb.sb(st, [64, 2, 8], F32, "fc")
        Wd = Dep()
        kb.dma("sp", w3[:, :, :], w3_ap.rearrange("d e f -> e d f"), writes=[Wd])
        kb.dma("sp", fc[:, :, 0:5], fcols_ap, writes=[Wd])
        ts(kb, "dve", fc[:, :, 7:8], fc[:, :, 4:5], -1.0, None, ALU.mult, None, [Wd], [Wd])
        taps = kb.sb(st, [64, 2, L], F32, "taps")
        tapsd = [Dep(), Dep()]
        NQ = 3
        tb_ = [kb.sb(st, [64, 512], F32, "tbc") for _ in range(2)]
        tbd = [Dep() for _ in range(2)]
        hin = [kb.sb(st, [64, 512], F32, "h2in") for _ in range(NQ)]
        hind = [Dep() for _ in range(NQ)]
        ex = [kb.sb(st, [64, 512], F32, "ex") for _ in range(NQ)]
        exd = [Dep() for _ in range(NQ)]
        junk = [kb.sb(st, [64, 512], F32, "fjunk") for _ in range(2)]
        junkd = [Dep(), Dep()]
        nb = (L + 511) // 512
        ss = kb.sb(st, [64, 2 * nb + 4], F32, "ss")
        ssd = Dep()
        it = 0
        for bi, l0 in enumerate(range(0, L, 512)):
            n = min(512, L - l0)
            j = bi % 2
            kb.dma("pool", tb_[j][:, :n], zposT[0:1, l0:l0 + n].broadcast_to([64, n]), writes=[tbd[j]])
            for d in range(2):
                q = it % NQ
                it += 1
                kb.dma("sp", hin[q][:, :n], h2_in[d][:, l0:l0 + n], writes=[hind[q]])
                bk = nextbank(g)
                mm(kb, g.psum[bk][:64, :n], w3[:, d, :], hin[q][:, :n], True, True, [Wd, hind[q]], [g.pd[bk]])
                act(kb, ex[q][:, :n], tb_[j][:, :n], AF.Exp, [tbd[j], Wd], [exd[q]], scale=fc[:, d, 7:8])
                tt(kb, "dve", taps[:, d, l0:l0 + n], g.psum[bk][:64, :n], ex[q][:, :n], ALU.mult, [g.pd[bk], exd[q]], [tapsd[d]])
                if d == 1 and l0 == 0:
                    kb.op("pool", lambda e: e.memset(taps[:, 1, 0:1], 0.0), [], [tapsd[d]])
                act(kb, junk[d][:, :n], taps[:, d, l0:l0 + n], AF.Square, [tapsd[d]], [junkd[d], ssd], accum_out=ss[:, 2 * bi + d:2 * bi + d + 1])
        tot, nrm = ss[:, 2 * nb:2 * nb + 1], ss[:, 2 * nb + 1:2 * nb + 2]
        kb.op("dve", lambda e: e.tensor_reduce(out=tot, in_=ss[:, 0:2 * nb], axis=AX.X, op=ALU.add), [ssd], [ssd])
        act(kb, nrm, tot, AF.Sqrt, [ssd], [ssd])
        kb.op("dve", lambda e: e.reciprocal(out=nrm, in_=nrm), [ssd], [ssd])
        for d in range(2):
            for l0 in range(0, L, 4096):
                n = min(4096, L - l0)
                ts(kb, "dve", taps[:, d, l0:l0 + n], taps[:, d, l0:l0 + n], nrm, None, ALU.mult, None, [tapsd[d], ssd], [tapsd[d]])
            kb.dma(("sp", "pool")[d], taps_out[d], taps[:, d, :], reads=[tapsd[d]])
        kb.barrier()


LP, LS = 16400, 2064
NCORES = 8
BF = ml_dtypes.bfloat16


class Prog:
    def __init__(self):
        self.nc = bass.Bass("TRN2", target_bir_lowering=False)
        self.kb = KB(self.nc)
        self.ins = {}

    def din(self, name, shape, dt=F32):
        self.ins[name] = (tuple(shape), dt)
        return self.nc.dram_tensor(name, list(shape), dt, kind="ExternalInput").ap()

    def dout(self, name, shape, dt=F32):
        return self.nc.dram_tensor(name, list(shape), dt, kind="ExternalOutput").ap()

    def scr(self, name, shape, dt=F32):
        return self.nc.dram_tensor(name, list(shape), dt).ap()


def chunk_tiles():
    return [(t0, 128, 61 + t0) for t0 in range(0, 2048, 128)] + [(2048, 16, 15)]


def declare_tabs(P, cfg, pre):
    t = fft_tables(cfg)
    return {k: P.din(pre + k, v.shape, F32 if v.dtype == np.float32 else BF16) for k, v in t.items()}, {pre + k: v for k, v in t.items()}


def build_l1():
    P = Prog()
    kb = P.kb
    ident = P.din("ident", [128, 128])
    xh_p = P.din("xh_p", [LP + 2, D])
    xh_s = P.din("xh_s", [LS + 2, D])
    valid_p = P.din("valid_p", [1, LP + 2])
    valid_s = P.din("valid_s", [1, LS + 2])
    zpos_p = P.din("zpos_p", [33, LP])
    zpos_s = P.din("zpos_s", [33, LS])
    tabsP, _ = declare_tabs(P, CFG_P, "tp_")
    tabsS, _ = declare_tabs(P, CFG_S, "ts_")
    fw1 = P.din("fw1", [2, 33, 64])
    fw2 = P.din("fw2", [2, 64, 64])
    fw3 = P.din("fw3", [9, 2, 64, 64])
    fcols = P.din("fcols", [9, 64, 2, 5])
    why = P.din("why", [9, D, 192])
    brow = P.din("brow", [9, 1, 192])
    hcols = P.din("hcols", [9, 64, 12])
    dskip = P.din("dskip", [9, 1, 64])
    xc = P.din("xc", [2, XC, D])
    mask = P.din("mask", [2, 1, XC])
    wconf = P.din("wconf", [D, 1024])
    ccols = P.din("ccols", [128, 144])
    yaP = P.dout("yaP", [64, LP], BF16)
    yaS = P.dout("yaS", [8, 64, LS], BF16)
    ybT = P.dout("ybT", [2, 512, LS], BF16)
    taps_p = P.scr("taps_p", [2, 64, LP])
    Hs_p = P.scr("Hs_p", [86, 64 * CFG_P.nq, 2, CFG_P.N1])
    z_p = P.scr("z_p", [64, LP])
    x0_p = P.scr("x0_p", [64, LP])
    taps_s = P.scr("taps_s", [8, 2, 64, LS])
    Hs_s = P.scr("Hs_s", [8, 86, 64 * CFG_S.nq, 2, CFG_S.N1])
    z_s = P.scr("z_s", [8, 64, LS])
    x0_s = P.scr("x0_s", [8, 64, LS])
    with ExitStack() as st:
        g = setup_globals(kb, st)
        load_ident(kb, g, ident)
        fwd = lambda i: dict(w1=fw1, w2=fw2, w3=fw3[i], fcols=fcols[i])
        phase_hy_filters(kb, g, LP, zpos_p, fwd(0), taps_p)
        phase_hy_inproj(kb, g, [dict(xh=xh_p, valid=valid_p, L=LP, z=[z_p], x0=[x0_p])], why[0], brow[0], hcols[0])
        phase_hy_conv(kb, g, CFG_P, tabsP, taps_p, Hs_p, [dict(z=z_p, x0=x0_p, ya=yaP)], dskip[0])
        for gi in range(8):
            phase_hy_filters(kb, g, LS, zpos_s, fwd(1 + gi), taps_s[gi])
            phase_hy_inproj(kb, g, [dict(xh=xh_s, valid=valid_s, L=LS, z=[z_s[gi]], x0=[x0_s[gi]])], why[1 + gi], brow[1 + gi], hcols[1 + gi])
            phase_hy_conv(kb, g, CFG_S, tabsS, taps_s[gi], Hs_s[gi], [dict(z=z_s[gi], x0=x0_s[gi], ya=yaS[gi])], dskip[1 + gi])

        def outf(s_):
            def f(j, bi, cn):
                if bi == 0:
                    return ybT[s_, j * 128:(j + 1) * 128, 2048:2064]
                return ybT[s_, j * 128:(j + 1) * 128, (bi - 1) * 512:bi * 512]
            return f
        phase_conf(kb, g, [dict(x=xc[s_], mask=mask[s_], out=outf(s_)) for s_ in range(2)], wconf, ccols)
        kb.finish_wait()
    return P


def build_l2():
    P = Prog()
    kb = P.kb
    ident = P.din("ident", [128, 128])
    xc = P.din("xc", [2, XC, D])
    ycT = P.din("ycT", [2, D, LS], BF16)
    wout = P.din("wout", [D, D])
    bout = P.din("bout", [1, D])
    ln1g = P.din("ln1g", [1, D]); ln1b = P.din("ln1b", [1, D]); ln2g = P.din("ln2g", [1, D]); ln2b = P.din("ln2b", [1, D])
    w1 = P.din("w1", [D, DFF]); w2 = P.din("w2", [DFF, D])
    wqa = P.din("wqa", [D, 384]); qg = P.din("qg", [1, 384]); WqH = P.din("WqH", [384, NH * 128]); WqS = P.din("WqS", [384, NH * 32])
    wkva = P.din("wkva", [D, 288]); kvg = P.din("kvg", [1, 256])
    cs = P.din("cs", [2, LS, 32]); Cq = P.din("Cq", [2, 32, 2048]); Sq = P.din("Sq", [2, 32, 2048])
    h2 = P.dout("h2", [2, LS, D])
    kvlat = P.dout("kvlat", [2, LS, 288])
    QT = P.dout("QT", [2, NH, 128, 2048], BF16)
    h1 = P.scr("h1", [2, LS, D])
    tl = chunk_tiles()
    with ExitStack() as st:
        g = setup_globals(kb, st)
        load_ident(kb, g, ident)
        ycv = ycT.rearrange("s (k p) t -> s p k t", p=128)
        phase_proj_ln(kb, g, [(xc[s_, xr:xr + n, :], [(slice(0, 8), ycv[s_, :, :, t0:t0 + n], None)], h1[s_, t0:t0 + n, :], n) for s_ in range(2) for t0, n, xr in tl],
                      True, wout, bout, ln1g, ln1b)
        phase_mlp_ln(kb, g, [(h1[s_, t0:t0 + n, :], h2[s_, t0:t0 + n, :], n) for s_ in range(2) for t0, n, xr in tl], w1, w2, ln2g, ln2b)
        seqs = []
        for s_ in range(2):
            seqs.append(dict(tiles=[(h2[s_, t0:t0 + n, :], kvlat[s_, t0:t0 + n, :], cs[s_, t0:t0 + n, :], n) for t0, n, xr in tl],
                             CS=(Cq[s_], Sq[s_]), qt=(lambda s_: (lambda h, q0: QT[s_, h, :, q0:q0 + 512]))(s_)))
        phase_qkv(kb, g, seqs, wqa, qg, WqH, WqS, wkva, kvg)
        kb.finish_wait()
    return P


def build_l3():
    P = Prog()
    kb = P.kb
    ident = P.din("ident", [128, 128])
    h2 = P.din("h2", [2, LS, D])
    kvp = P.din("kvp", [LP, 288])
    kvs = P.din("kvs", [LS, 288])
    QT = P.din("QT", [2, NH, 128, 2048], BF16)
    WkH = P.din("WkH", [256, NH * 128]); WvH = P.din("WvH", [256, NH * 64])
    wo = P.din("wo", [D, D])
    ln1g = P.din("ln1g", [1, D]); ln1b = P.din("ln1b", [1, D]); ln2g = P.din("ln2g", [1, D]); ln2b = P.din("ln2b", [1, D])
    w1 = P.din("w1", [D, DFF]); w2 = P.din("w2", [DFF, D])
    out = P.dout("out", [2, 2048, D])
    otok = P.scr("otok", [2, 2048, D])
    h3 = P.scr("h3", [2, 2048, D])
    with ExitStack() as st:
        g = setup_globals(kb, st)
        load_ident(kb, g, ident)
        seqs = []
        for s_, kv, L in ((0, kvp, LP), (1, kvs, LS)):
            otv = otok[s_].rearrange("(a t p) (h c) -> a p t h c", p=128, t=4, c=64)
            seqs.append(dict(kchunks=[(kv[t0:min(t0 + 128, L), :], min(128, L - t0)) for t0 in range(0, L, 128)],
                             qt=(lambda s_: (lambda h: QT[s_, h, :, :]))(s_),
                             o=(lambda otv: (lambda qsb, half, h: otv[qsb * 2 + half, :, :, h, :]))(otv)))
        phase_attn(kb, g, seqs, WkH, WvH)
        tl2 = [(s_, t0) for s_ in range(2) for t0 in range(0, 2048, 128)]
        phase_proj_ln(kb, g, [(h2[s_, t0:t0 + 128, :], otok[s_, t0:t0 + 128, :], h3[s_, t0:t0 + 128, :], 128) for s_, t0 in tl2], False, wo, None, ln1g, ln1b)
        phase_mlp_ln(kb, g, [(h3[s_, t0:t0 + 128, :], out[s_, t0:t0 + 128, :], 128) for s_, t0 in tl2], w1, w2, ln2g, ln2b)
        kb.finish_wait()
    return P


def zpos_table(L):
    t = np.arange(L, dtype=np.float32) / max(L - 1, 1)
    freqs = np.linspace(1e-4, 15, 16, dtype=np.float32)
    w = (np.float32(2.0 * math.pi) * np.arange(L, dtype=np.float32) / np.float32(L)).astype(np.float32)
    ang = w[:, None] * freqs[None, :]
    return np.ascontiguousarray(np.concatenate([t[:, None], np.cos(ang), -np.sin(ang)], -1).T.astype(np.float32))


def rope_cs(pos):
    inv = (1.0 / (10000.0 ** (np.arange(0, 32, 2, dtype=np.float32) / 32))).astype(np.float32)
    ang = pos.astype(np.float32)[:, None] * inv[None, :]
    return np.cos(ang).astype(np.float32), np.sin(ang).astype(np.float32)


def make_xc(hfull, m0, L):
    x = np.zeros((XC, D), np.float32)
    mk = np.zeros((1, XC), np.float32)
    x[15:46] = hfull[0:31]
    mk[0, 15:46] = 1
    lo, hi = m0 - 15, min(m0 + 2048 + 15, L)
    x[46:46 + (hi - lo)] = hfull[lo:hi]
    mk[0, 46:46 + (hi - lo)] = 1
    return x, mk


def colpack(v):
    return np.ascontiguousarray(v.reshape(4, 128).T)


def check_inputs(P, im):
    for k, (shape, dt) in P.ins.items():
        assert k in im, k
        assert tuple(im[k].shape) == shape, (k, im[k].shape, shape)
    return {k: np.ascontiguousarray(im[k]) for k in P.ins}


def kernel_unfused(x_prompt, x_sample, meta_tokens, ev_w_in, ev_b_in, ev_short_w, ev_short_b,
           hy_w1, hy_b1, hy_freq1, hy_w2, hy_b2, hy_freq2, hy_w3, hy_decay, hy_skip_d,
           cf_dw_w, cf_dw_b, cf_ln_g, cf_ln_b, ev_w_out, ev_b_out,
           mla_wq_a, mla_q_norm, mla_wq_b, mla_wkv_a, mla_kv_norm, mla_wkv_b, mla_wo,
           ln1_g, ln1_b, mlp_w1, mlp_w2, ln2_g, ln2_b):
    f = lambda a: np.asarray(a, dtype=np.float32)
    x_prompt, x_sample, meta = f(x_prompt), f(x_sample), f(meta_tokens)
    win, bin_, sw, sb = f(ev_w_in)[0], f(ev_b_in)[0], f(ev_short_w)[0], f(ev_short_b)[0]
    ident = np.eye(128, dtype=np.float32)
    hp = np.concatenate([meta, x_prompt[0]], 0)
    hs = [np.concatenate([meta, x_sample[c]], 0) for c in range(8)]
    z1 = np.zeros((1, D), np.float32)
    xh_p = np.concatenate([z1, hp, z1], 0)
    valid_p = np.ones((1, LP + 2), np.float32); valid_p[0, 0] = 0; valid_p[0, -1] = 0
    valid_s = np.ones((1, LS + 2), np.float32); valid_s[0, 0] = 0; valid_s[0, -1] = 0
    tabP, tabS = fft_tables(CFG_P), fft_tables(CFG_S)
    def grp(gi):
        ch = slice(gi * 64, gi * 64 + 64)
        gcols = [np.arange(k * 512 + gi * 64, k * 512 + gi * 64 + 64) for k in range(3)]
        allc = np.concatenate(gcols)
        return dict(fw3=np.ascontiguousarray(f(hy_w3)[0][:, :, ch]),
                    fcols=np.ascontiguousarray(np.stack([f(hy_freq1)[0], f(hy_b1)[0], f(hy_freq2)[0], f(hy_b2)[0], f(hy_decay)[0][:, ch]], -1).transpose(1, 0, 2)),
                    why=np.ascontiguousarray(win[:, allc]), brow=bin_[allc][None, :].copy(),
                    hcols=np.concatenate([np.stack([sw[0, gc], sw[1, gc], sw[2, gc], sb[gc]], 1) for gc in gcols], 1).astype(np.float32),
                    dskip=f(hy_skip_d)[0][ch][None, :].copy())
    G = [grp(gi) for gi in range(8)]
    ccols = np.concatenate([colpack(bin_[1536:2048]), colpack(bin_[2048:2560]), colpack(f(cf_dw_b)[0]), colpack(f(cf_ln_g)[0]), colpack(f(cf_ln_b)[0]),
                            np.ascontiguousarray(f(cf_dw_w)[0].T.reshape(4, 128, 31).transpose(1, 0, 2).reshape(128, 124))], 1).astype(np.float32)
    xcs, masks = [], []
    for c in range(8):
        a, ma = make_xc(hp, 16 + 2048 * c, LP)
        b, mb = make_xc(hs[c], 16, LS)
        xcs.append(np.stack([a, b], 0))
        masks.append(np.stack([ma, mb], 0))
    P1 = build_l1()
    ims = []
    for c in range(8):
        order = [c] + list(range(8))
        im = dict(ident=ident, xh_p=xh_p, xh_s=np.concatenate([z1, hs[c], z1], 0), valid_p=valid_p, valid_s=valid_s,
                  zpos_p=zpos_table(LP), zpos_s=zpos_table(LS), fw1=f(hy_w1)[0], fw2=f(hy_w2)[0],
                  fw3=np.stack([G[i]["fw3"] for i in order], 0), fcols=np.stack([G[i]["fcols"] for i in order], 0).astype(np.float32),
                  why=np.stack([G[i]["why"] for i in order], 0), brow=np.stack([G[i]["brow"] for i in order], 0),
                  hcols=np.stack([G[i]["hcols"] for i in order], 0), dskip=np.stack([G[i]["dskip"] for i in order], 0),
                  xc=xcs[c], mask=masks[c], wconf=np.ascontiguousarray(win[:, 1536:2560]), ccols=ccols)
        for k, v in tabP.items():
            im["tp_" + k] = v
        for k, v in tabS.items():
            im["ts_" + k] = v
        ims.append(check_inputs(P1, im))
    r1 = run_bass_kernel_spmd(P1.nc, ims, core_ids=list(range(8))).results
    yaP_all = np.concatenate([np.asarray(r1[c]["yaP"]) for c in range(8)], 0)
    P2 = build_l2()
    wqb = f(mla_wq_b)[0].reshape(384, NH, 96)
    WqH = np.concatenate([wqb[:, :, 64:96], np.zeros((384, NH, 32), np.float32), wqb[:, :, 0:64]], -1).reshape(384, NH * 128)
    WqS = np.concatenate([wqb[:, :, 80:96], wqb[:, :, 64:80]], -1).reshape(384, NH * 32)
    wkvb = f(mla_wkv_b)[0].reshape(256, NH, 128)
    WkH = np.concatenate([np.zeros((256, NH, 64), np.float32), wkvb[:, :, 0:64]], -1).reshape(256, NH * 128)
    WvH = np.ascontiguousarray(wkvb[:, :, 64:128]).reshape(256, NH * 64)
    ims = []
    for c in range(8):
        m0 = 16 + 2048 * c
        ya_p = np.concatenate([yaP_all[:, m0:m0 + 2048], yaP_all[:, 0:16]], 1)
        ya_s = np.asarray(r1[c]["yaS"]).reshape(512, LS)
        ya_s = np.concatenate([ya_s[:, 16:], ya_s[:, 0:16]], 1)
        yb = np.asarray(r1[c]["ybT"])
        ycT = np.stack([np.concatenate([ya_p, yb[0]], 0), np.concatenate([ya_s, yb[1]], 0)], 0)
        css, Cqs, Sqs = [], [], []
        for pos in (np.concatenate([np.arange(m0, m0 + 2048), np.arange(16)]), np.concatenate([np.arange(16, LS), np.arange(16)])):
            co, si = rope_cs(pos)
            css.append(np.concatenate([co, si], 1))
            Cqs.append(np.concatenate([co[:2048].T, co[:2048].T], 0))
            Sqs.append(np.concatenate([-si[:2048].T, si[:2048].T], 0))
        im = dict(ident=ident, xc=xcs[c], ycT=ycT, wout=f(ev_w_out)[0], bout=f(ev_b_out)[0:1], ln1g=f(ln1_g)[0:1], ln1b=f(ln1_b)[0:1],
                  ln2g=f(ln2_g)[0:1], ln2b=f(ln2_b)[0:1], w1=f(mlp_w1)[0], w2=f(mlp_w2)[0], wqa=f(mla_wq_a)[0], qg=f(mla_q_norm)[0:1],
                  WqH=WqH, WqS=WqS, wkva=f(mla_wkv_a)[0], kvg=f(mla_kv_norm)[0:1], cs=np.stack(css, 0), Cq=np.stack(Cqs, 0), Sq=np.stack(Sqs, 0))
        ims.append(check_inputs(P2, im))
    r2 = run_bass_kernel_spmd(P2.nc, ims, core_ids=list(range(8))).results
    kvp = np.concatenate([np.asarray(r2[c]["kvlat"])[0, :2048] for c in range(8)] + [np.asarray(r2[0]["kvlat"])[0, 2048:]], 0)
    P3 = build_l3()
    ims = []
    for c in range(8):
        im = dict(ident=ident, h2=np.asarray(r2[c]["h2"]), kvp=kvp, kvs=np.asarray(r2[c]["kvlat"])[1], QT=np.asarray(r2[c]["QT"]), WkH=WkH, WvH=WvH,
                  wo=f(mla_wo)[0], ln1g=f(ln1_g)[1:2], ln1b=f(ln1_b)[1:2], ln2g=f(ln2_g)[1:2], ln2b=f(ln2_b)[1:2], w1=f(mlp_w1)[1], w2=f(mlp_w2)[1])
        ims.append(check_inputs(P3, im))
    r3 = run_bass_kernel_spmd(P3.nc, ims, core_ids=list(range(8))).results
    y_prompt = np.concatenate([np.asarray(r3[c]["out"])[0] for c in range(8)], 0)[None].astype(np.float32)
    y_sample = np.stack([np.asarray(r3[c]["out"])[1] for c in range(8)], 0).astype(np.float32)
    return (y_prompt, y_sample)


U32 = mybir.dt.uint32
YAW = 18432


def build_fused(stop=10 ** 9, trace_steps=None):
    P = Prog()
    step = [0]

    def run(fn, *a):
        if step[0] < stop:
            fn(*a)
        step[0] += 1

    kb = P.kb
    nc = P.nc
    ident = P.din("ident", [128, 128])
    xh_p = P.din("xh_p", [LP + 2, D]); xh_s = P.din("xh_s", [LS + 2, D])
    valid_p = P.din("valid_p", [1, LP + 2]); valid_s = P.din("valid_s", [1, LS + 2])
    zpos_p = P.din("zpos_p", [33, LP]); zpos_s = P.din("zpos_s", [33, LS])
    tabsP, _ = declare_tabs(P, CFG_P, "tp_")
    tabsS, _ = declare_tabs(P, CFG_S, "ts_")
    fw1 = P.din("fw1", [2, 33, 64]); fw2 = P.din("fw2", [2, 64, 64])
    fw3 = P.din("fw3", [9, 2, 64, 64]); fcols = P.din("fcols", [9, 64, 2, 5])
    why = P.din("why", [9, D, 192]); brow = P.din("brow", [9, 1, 192]); hcols = P.din("hcols", [9, 64, 12]); dskip = P.din("dskip", [9, 1, 64])
    xc = P.din("xc", [2, XC, D]); mask = P.din("mask", [2, 1, XC])
    wconf = P.din("wconf", [D, 1024]); ccols = P.din("ccols", [128, 144])
    gidx = P.din("gidx", [128, 4], U32)
    wout = P.din("wout", [D, D]); bout = P.din("bout", [1, D])
    ln1g = P.din("ln1g", [2, 1, D]); ln1b = P.din("ln1b", [2, 1, D]); ln2g = P.din("ln2g", [2, 1, D]); ln2b = P.din("ln2b", [2, 1, D])
    w1 = P.din("w1", [2, D, DFF]); w2 = P.din("w2", [2, DFF, D])
    wqa = P.din("wqa", [D, 384]); qg = P.din("qg", [1, 384]); WqH = P.din("WqH", [384, NH * 128]); WqS = P.din("WqS", [384, NH * 32])
    wkva = P.din("wkva", [D, 288]); kvg = P.din("kvg", [1, 256])
    cs = P.din("cs", [2, LS, 32]); Cq = P.din("Cq", [2, 32, 2048]); Sq = P.din("Sq", [2, 32, 2048])
    WkH = P.din("WkH", [256, NH * 128]); WvH = P.din("WvH", [256, NH * 64]); wo = P.din("wo", [D, D])
    out = P.dout("out", [2, 2048, D])
    yaP = P.scr("yaP", [64, YAW], BF16)
    yaP_all = P.scr("yaP_all", [512, YAW], BF16)
    yaS = P.scr("yaS", [8, 64, LS], BF16)
    ybT = P.scr("ybT", [2, 512, LS], BF16)
    taps_p = P.scr("taps_p", [2, 64, LP]); Hs_p = P.scr("Hs_p", [86, 64 * CFG_P.nq, 2, CFG_P.N1])
    z_p = P.scr("z_p", [64, LP]); x0_p = P.scr("x0_p", [64, LP])
    taps_s = P.scr("taps_s", [8, 2, 64, LS]); Hs_s = P.scr("Hs_s", [8, 86, 64 * CFG_S.nq, 2, CFG_S.N1])
    z_s = P.scr("z_s", [8, 64, LS]); x0_s = P.scr("x0_s", [8, 64, LS])
    h1 = P.scr("h1", [2, LS, D]); h2 = P.scr("h2", [2, LS, D])
    kvlat = P.scr("kvlat", [2, LS, 288]); kv_all = P.scr("kv_all", [8 * LS, 288])
    QT = P.scr("QT", [2, NH, 128, 2048], BF16)
    otok = P.scr("otok", [2, 2048, D]); h3 = P.scr("h3", [2, 2048, D])
    tl = chunk_tiles()
    with ExitStack() as st:
        g = setup_globals(kb, st)
        load_ident(kb, g, ident)
        fwd = lambda i: dict(w1=fw1, w2=fw2, w3=fw3[i], fcols=fcols[i])
        run(phase_hy_filters, kb, g, LP, zpos_p, fwd(0), taps_p)
        run(phase_hy_inproj, kb, g, [dict(xh=xh_p, valid=valid_p, L=LP, z=[z_p], x0=[x0_p])], why[0], brow[0], hcols[0])
        run(phase_hy_conv, kb, g, CFG_P, tabsP, taps_p, Hs_p, [dict(z=z_p, x0=x0_p, ya=yaP[:, 2032:2032 + LP])], dskip[0])
        agd = Dep()
        run(lambda: kb.all_gather(yaP, yaP_all, reads=[], writes=[agd]))
        kb.barrier()
        for gi in range(8):
            run(phase_hy_filters, kb, g, LS, zpos_s, fwd(1 + gi), taps_s[gi])
            run(phase_hy_inproj, kb, g, [dict(xh=xh_s, valid=valid_s, L=LS, z=[z_s[gi]], x0=[x0_s[gi]])], why[1 + gi], brow[1 + gi], hcols[1 + gi])
            run(phase_hy_conv, kb, g, CFG_S, tabsS, taps_s[gi], Hs_s[gi], [dict(z=z_s[gi], x0=x0_s[gi], ya=yaS[gi])], dskip[1 + gi])

        def outf(s_):
            def f(j, bi, cn):
                if bi == 0:
                    return ybT[s_, j * 128:(j + 1) * 128, 2048:2064]
                return ybT[s_, j * 128:(j + 1) * 128, (bi - 1) * 512:bi * 512]
            return f
        run(phase_conf, kb, g, [dict(x=xc[s_], mask=mask[s_], out=outf(s_)) for s_ in range(2)], wconf, ccols)
        with ExitStack() as st2:
            yaG = kb.sb(st2, [128, 4, 2048], BF16, "yaG")
            yaGd = Dep()
            ix = kb.sb(st2, [128, 4], U32, "gix")
            ixd = Dep()
            kb.dma("sp", ix[:, :], gidx[:, :], writes=[ixd])
            rows = yaP_all.rearrange("c (b t) -> (c b) t", t=2048)
            for k in range(4):
                run(lambda k=k: kb.gather_rows(yaG[:, k, :], rows[:, :], ix[:, k:k + 1], reads=[agd, ixd], writes=[yaGd]))
            ybv = ybT.rearrange("s (k p) t -> s p k t", p=128)
            yav_meta = yaP_all.rearrange("(k p) c -> p k c", p=128)
            yas = yaS.rearrange("g c t -> (g c) t").rearrange("(k p) t -> p k t", p=128)
            tiles = []
            for t0, n, xr in tl:
                if n == 128:
                    yl = [(slice(0, 4), yaG[:, :, t0:t0 + n], yaGd), (slice(4, 8), ybv[0, :, :, t0:t0 + n], None)]
                else:
                    yl = [(slice(0, 4), yav_meta[:, :, 2032:2048], agd), (slice(4, 8), ybv[0, :, :, 2048:2064], None)]
                tiles.append((xc[0, xr:xr + n, :], yl, h1[0, t0:t0 + n, :], n))
            for t0, n, xr in tl:
                tok0 = 16 + t0 if n == 128 else 0
                yl = [(slice(0, 4), yas[:, :, tok0:tok0 + n], None), (slice(4, 8), ybv[1, :, :, t0:t0 + n], None)]
                tiles.append((xc[1, xr:xr + n, :], yl, h1[1, t0:t0 + n, :], n))
            run(phase_proj_ln, kb, g, tiles, True, wout, bout, ln1g[0], ln1b[0])
        run(phase_mlp_ln, kb, g, [(h1[s_, t0:t0 + n, :], h2[s_, t0:t0 + n, :], n) for s_ in range(2) for t0, n, xr in tl], w1[0], w2[0], ln2g[0], ln2b[0])
        seqs = []
        for s_ in range(2):
            seqs.append(dict(tiles=[(h2[s_, t0:t0 + n, :], kvlat[s_, t0:t0 + n, :], cs[s_, t0:t0 + n, :], n) for t0, n, xr in tl],
                             CS=(Cq[s_], Sq[s_]), qt=(lambda s_: (lambda h, q0: QT[s_, h, :, q0:q0 + 512]))(s_)))
        run(phase_qkv, kb, g, seqs, wqa, qg, WqH, WqS, wkva, kvg)
        kvd = Dep()
        run(lambda: kb.all_gather(kvlat[0], kv_all, reads=[], writes=[kvd]))
        kb.barrier()
        seqs = []
        pch = [(kv_all[r * LS + t0:r * LS + t0 + 128, :], 128) for r in range(8) for t0 in range(0, 2048, 128)] + [(kv_all[2048:2064, :], 16)]
        sch = [(kvlat[1, t0:min(t0 + 128, LS), :], min(128, LS - t0)) for t0 in range(0, LS, 128)]
        for s_, ch in ((0, pch), (1, sch)):
            otv = otok[s_].rearrange("(a t p) (h c) -> a p t h c", p=128, t=4, c=64)
            seqs.append(dict(kchunks=ch, qt=(lambda s_: (lambda h: QT[s_, h, :, :]))(s_),
                             o=(lambda otv: (lambda qsb, half, h: otv[qsb * 2 + half, :, :, h, :]))(otv)))
        run(phase_attn, kb, g, seqs, WkH, WvH)
        tl2 = [(s_, t0) for s_ in range(2) for t0 in range(0, 2048, 128)]
        run(phase_proj_ln, kb, g, [(h2[s_, t0:t0 + 128, :], otok[s_, t0:t0 + 128, :], h3[s_, t0:t0 + 128, :], 128) for s_, t0 in tl2], False, wo, None, ln1g[1], ln1b[1])
        run(phase_mlp_ln, kb, g, [(h3[s_, t0:t0 + 128, :], out[s_, t0:t0 + 128, :], 128) for s_, t0 in tl2], w1[1], w2[1], ln2g[1], ln2b[1])
        kb.finish_wait()
    P.nsteps = step[0]
    return P


def build_nc():
    P = Prog()
    kb = P.kb
    ident = P.din("ident", [128, 128])
    xpad_p = P.din("xpad_p", [LP + 30, D]); maskpad = P.din("maskpad", [1, LP + 30])
    xh_s = P.din("xh_s", [LS + 2, D]); valid_s = P.din("valid_s", [1, LS + 2])
    xc_s = P.din("xc_s", [XC, D]); mask_s = P.din("mask_s", [1, XC])
    zpos_p = P.din("zpos_p", [33, LP]); zpos_s = P.din("zpos_s", [33, LS])
    tabsP, _ = declare_tabs(P, CFG_P, "tp_")
    tabsS, _ = declare_tabs(P, CFG_S, "ts_")
    fw1 = P.din("fw1", [2, 33, 64]); fw2 = P.din("fw2", [2, 64, 64])
    fw3 = P.din("fw3", [8, 2, 64, 64]); fcols = P.din("fcols", [8, 64, 2, 5])
    why = P.din("why", [D, 8 * 192]); brow = P.din("brow", [1, 8 * 192]); hcols = P.din("hcols", [64, 8 * 12]); dskip = P.din("dskip", [8, 1, 64])
    wconf = P.din("wconf", [D, 1024]); ccols = P.din("ccols", [128, 144])
    tokidx = P.din("tokidx", [128, 16], U32)
    wout = P.din("wout", [D, D]); bout = P.din("bout", [1, D])
    ln1g = P.din("ln1g", [2, 1, D]); ln1b = P.din("ln1b", [2, 1, D]); ln2g = P.din("ln2g", [2, 1, D]); ln2b = P.din("ln2b", [2, 1, D])
    w1 = P.din("w1", [2, D, DFF]); w2 = P.din("w2", [2, DFF, D])
    wqa = P.din("wqa", [D, 384]); qg = P.din("qg", [1, 384]); WqH = P.din("WqH", [384, NH * 128]); WqS = P.din("WqS", [384, NH * 32])
    wkva = P.din("wkva", [D, 288]); kvg = P.din("kvg", [1, 256])
    cs_all = P.din("cs_all", [LP, 32])
    cs = P.din("cs", [2, LS, 32]); Cq = P.din("Cq", [2, 32, 2048]); Sq = P.din("Sq", [2, 32, 2048])
    WkH = P.din("WkH", [256, NH * 128]); WvH = P.din("WvH", [256, NH * 64]); wo = P.din("wo", [D, D])
    out = P.dout("out", [2, 2048, D])
    yaP_all = P.scr("yaP_all", [512, YAW], BF16)
    yaS = P.scr("yaS", [8, 64, LS], BF16)
    ybT_p = P.scr("ybT_p", [512, LP], BF16); ybT_s = P.scr("ybT_s", [512, LS], BF16)
    h2f_p = P.scr("h2f_p", [2, 64, LP]); h2f_s = P.scr("h2f_s", [2, 64, LS])
    taps_p = P.scr("taps_p", [2, 64, LP]); Hs_p = P.scr("Hs_p", [86, 64 * CFG_P.nq, 2, CFG_P.NF])
    z_p = P.scr("z_p", [8, 64, LP]); x0_p = P.scr("x0_p", [8, 64, LP])
    taps_s = P.scr("taps_s", [2, 64, LS]); Hs_s = P.scr("Hs_s", [86, 64 * CFG_S.nq, 2, CFG_S.NF])
    z_s = P.scr("z_s", [8, 64, LS]); x0_s = P.scr("x0_s", [8, 64, LS])
    h1_all = P.scr("h1_all", [LP, D]); h2_all = P.scr("h2_all", [LP, D])
    h1_s = P.scr("h1_s", [LS, D]); h2_s = P.scr("h2_s", [LS, D]); h2_own = P.scr("h2_own", [LS, D])
    kv_all = P.scr("kv_all", [LP, 288]); kv_dummy = P.scr("kv_dummy", [LS, 288]); kvlat_s = P.scr("kvlat_s", [LS, 288])
    QT = P.scr("QT", [2, NH, 128, 2048], BF16)
    otok = P.scr("otok", [2, 2048, D]); h3 = P.scr("h3", [2, 2048, D])
    tl = chunk_tiles()
    with ExitStack() as st:
        g = setup_globals(kb, st)
        load_ident(kb, g, ident)
        fwd = lambda i: dict(w1=fw1, w2=fw2, w3=fw3[i], fcols=fcols[i])
        phase_hy_inproj(kb, g, [dict(xh=xpad_p[14:14 + LP + 2, :], valid=maskpad[:, 14:14 + LP + 2], L=LP,
                                     z=[z_p[gi] for gi in range(8)], x0=[x0_p[gi] for gi in range(8)])], why, brow, hcols, G=8)
        phase_hy_filter_h2(kb, g, LP, zpos_p, fwd(0), h2f_p)
        phase_hy_filter_h2(kb, g, LS, zpos_s, fwd(0), h2f_s)
        for gi in range(8):
            phase_hy_filter_taps(kb, g, LP, zpos_p, h2f_p, fw3[gi], fcols[gi], taps_p)
            phase_hy_conv(kb, g, CFG_P, tabsP, taps_p, Hs_p, [dict(z=z_p[gi], x0=x0_p[gi], ya=yaP_all[gi * 64:(gi + 1) * 64, 2032:2032 + LP])], dskip[gi])
        phase_hy_inproj(kb, g, [dict(xh=xh_s, valid=valid_s, L=LS, z=[z_s[gi] for gi in range(8)], x0=[x0_s[gi] for gi in range(8)])],
                        why, brow, hcols, G=8)
        for gi in range(8):
            phase_hy_filter_taps(kb, g, LS, zpos_s, h2f_s, fw3[gi], fcols[gi], taps_s)
            phase_hy_conv(kb, g, CFG_S, tabsS, taps_s, Hs_s, [dict(z=z_s[gi], x0=x0_s[gi], ya=yaS[gi])], dskip[gi])
        cseqs = []
        for j in range(8):
            r0 = 16 + 2048 * j
            cseqs.append(dict(x=xpad_p[r0:r0 + 2078, :], mask=maskpad[:, r0:r0 + 2078], ncols=2078, blocks=[(15 + 512 * i, 512) for i in range(4)],
                              out=(lambda j: (lambda jj, bi, cn: ybT_p[jj * 128:(jj + 1) * 128, 2048 * j + 512 * bi:2048 * j + 512 * bi + cn]))(j)))
        cseqs.append(dict(x=xpad_p[0:46, :], mask=maskpad[:, 0:46], ncols=46, blocks=[(15, 16)],
                          out=lambda jj, bi, cn: ybT_p[jj * 128:(jj + 1) * 128, 16384:16400]))

        def outf_s(jj, bi, cn):
            if bi == 0:
                return ybT_s[jj * 128:(jj + 1) * 128, 2048:2064]
            return ybT_s[jj * 128:(jj + 1) * 128, (bi - 1) * 512:bi * 512]
        cseqs.append(dict(x=xc_s, mask=mask_s, out=outf_s))
        phase_conf(kb, g, cseqs, wconf, ccols)
        yav = yaP_all.rearrange("(k p) c -> p k c", p=128)
        ybv_p = ybT_p.rearrange("(k p) t -> p k t", p=128)
        ybv_s = ybT_s.rearrange("(k p) t -> p k t", p=128)
        yas = yaS.rearrange("g c t -> (g c) t").rearrange("(k p) t -> p k t", p=128)
        tiles = []
        for j in range(8):
            for t0 in range(0, 2048, 128):
                tok = 16 + 2048 * j + t0
                gr = 2048 * j + t0
                tiles.append((xpad_p[15 + tok:15 + tok + 128, :],
                              [(slice(0, 4), yav[:, :, 2032 + tok:2032 + tok + 128], None), (slice(4, 8), ybv_p[:, :, gr:gr + 128], None)],
                              h1_all[gr:gr + 128, :], 128))
        tiles.append((xpad_p[15:31, :], [(slice(0, 4), yav[:, :, 2032:2048], None), (slice(4, 8), ybv_p[:, :, 16384:16400], None)],
                      h1_all[16384:16400, :], 16))
        for t0, n, xr in tl:
            tok0 = 16 + t0 if n == 128 else 0
            tiles.append((xc_s[xr:xr + n, :], [(slice(0, 4), yas[:, :, tok0:tok0 + n], None), (slice(4, 8), ybv_s[:, :, t0:t0 + n], None)],
                          h1_s[t0:t0 + n, :], n))
        phase_proj_ln(kb, g, tiles, True, wout, bout, ln1g[0], ln1b[0])
        ptl = [(r0, min(128, LP - r0)) for r0 in range(0, LP, 128)]
        phase_mlp_ln(kb, g, [(h1_all[r0:r0 + n, :], h2_all[r0:r0 + n, :], n) for r0, n in ptl] +
                     [(h1_s[t0:t0 + n, :], h2_s[t0:t0 + n, :], n) for t0, n, xr in tl], w1[0], w2[0], ln2g[0], ln2b[0])
        with ExitStack() as st2:
            ix = kb.sb(st2, [128, 16], U32, "tokix")
            ixd = Dep()
            kb.dma("sp", ix[:, :], tokidx[:, :], writes=[ixd])
            gb = [kb.sb(st2, [128, D], F32, "gb") for _ in range(2)]
            gd = [Dep(), Dep()]
            for i in range(16):
                j = i % 2
                kb.gather_rows(gb[j][:, :], h2_all[:, :], ix[:, i:i + 1], reads=[ixd], writes=[gd[j]])
                kb.dma("sp", h2_own[128 * i:128 * i + 128, :], gb[j][:, :], reads=[gd[j]])
            kb.dma("sp", h2_own[2048:2064, :], h2_all[16384:16400, :])
            kb.barrier()
        seqs = [dict(tiles=[(h2_all[r0:r0 + n, :], kv_all[r0:r0 + n, :], cs_all[r0:r0 + n, :], n) for r0, n in ptl], kv_only=True),
                dict(tiles=[(h2_own[t0:t0 + n, :], kv_dummy[t0:t0 + n, :], cs[0, t0:t0 + n, :], n) for t0, n, xr in tl],
                     CS=(Cq[0], Sq[0]), qt=lambda h, q0: QT[0, h, :, q0:q0 + 512]),
                dict(tiles=[(h2_s[t0:t0 + n, :], kvlat_s[t0:t0 + n, :], cs[1, t0:t0 + n, :], n) for t0, n, xr in tl],
                     CS=(Cq[1], Sq[1]), qt=lambda h, q0: QT[1, h, :, q0:q0 + 512])]
        phase_qkv(kb, g, seqs, wqa, qg, WqH, WqS, wkva, kvg)
        aseqs = []
        for s_, ch in ((0, [(kv_all[r0:r0 + n, :], n) for r0, n in ptl]),
                       (1, [(kvlat_s[t0:min(t0 + 128, LS), :], min(128, LS - t0)) for t0 in range(0, LS, 128)])):
            otv = otok[s_].rearrange("(a t p) (h c) -> a p t h c", p=128, t=4, c=64)
            aseqs.append(dict(kchunks=ch, qt=(lambda s_: (lambda h: QT[s_, h, :, :]))(s_),
                              o=(lambda otv: (lambda qsb, half, h: otv[qsb * 2 + half, :, :, h, :]))(otv)))
        phase_attn(kb, g, aseqs, WkH, WvH)
        hres = (h2_own, h2_s)
        tl2 = [(s_, t0) for s_ in range(2) for t0 in range(0, 2048, 128)]
        phase_proj_ln(kb, g, [(hres[s_][t0:t0 + 128, :], otok[s_, t0:t0 + 128, :], h3[s_, t0:t0 + 128, :], 128) for s_, t0 in tl2], False, wo, None, ln1g[1], ln1b[1])
        phase_mlp_ln(kb, g, [(h3[s_, t0:t0 + 128, :], out[s_, t0:t0 + 128, :], 128) for s_, t0 in tl2], w1[1], w2[1], ln2g[1], ln2b[1])
        kb.finish_wait()
    return P


def kernel(x_prompt, x_sample, meta_tokens, ev_w_in, ev_b_in, ev_short_w, ev_short_b,
           hy_w1, hy_b1, hy_freq1, hy_w2, hy_b2, hy_freq2, hy_w3, hy_decay, hy_skip_d,
           cf_dw_w, cf_dw_b, cf_ln_g, cf_ln_b, ev_w_out, ev_b_out,
           mla_wq_a, mla_q_norm, mla_wq_b, mla_wkv_a, mla_kv_norm, mla_wkv_b, mla_wo,
           ln1_g, ln1_b, mlp_w1, mlp_w2, ln2_g, ln2_b):
    f = lambda a: np.asarray(a, dtype=np.float32)
    x_prompt, x_sample, meta = f(x_prompt), f(x_sample), f(meta_tokens)
    win, bin_, sw, sb = f(ev_w_in)[0], f(ev_b_in)[0], f(ev_short_w)[0], f(ev_short_b)[0]
    ident = np.eye(128, dtype=np.float32)
    hp = np.concatenate([meta, x_prompt[0]], 0)
    hs = [np.concatenate([meta, x_sample[c]], 0) for c in range(8)]
    z1 = np.zeros((1, D), np.float32)
    z15 = np.zeros((15, D), np.float32)
    xpad_p = np.concatenate([z15, hp, z15], 0)
    maskpad = np.zeros((1, LP + 30), np.float32); maskpad[0, 15:15 + LP] = 1
    valid_s = np.ones((1, LS + 2), np.float32); valid_s[0, 0] = 0; valid_s[0, -1] = 0
    tabP, tabS = fft_tables(CFG_P), fft_tables(CFG_S)
    gcols = [[np.arange(k * 512 + gi * 64, k * 512 + gi * 64 + 64) for k in range(3)] for gi in range(8)]
    allc = np.concatenate([np.concatenate(gc) for gc in gcols])
    why = np.ascontiguousarray(win[:, allc])
    brow = bin_[allc][None, :].copy()
    hcols = np.concatenate([np.stack([sw[0, c_], sw[1, c_], sw[2, c_], sb[c_]], 1) for gc in gcols for c_ in gc], 1).astype(np.float32)
    fw3 = np.stack([np.ascontiguousarray(f(hy_w3)[0][:, :, gi * 64:gi * 64 + 64]) for gi in range(8)], 0)
    fcols = np.stack([np.stack([f(hy_freq1)[0], f(hy_b1)[0], f(hy_freq2)[0], f(hy_b2)[0], f(hy_decay)[0][:, gi * 64:gi * 64 + 64]], -1).transpose(1, 0, 2)
                      for gi in range(8)], 0).astype(np.float32)
    dskip = np.stack([f(hy_skip_d)[0][gi * 64:gi * 64 + 64][None, :] for gi in range(8)], 0)
    ccols = np.concatenate([colpack(bin_[1536:2048]), colpack(bin_[2048:2560]), colpack(f(cf_dw_b)[0]), colpack(f(cf_ln_g)[0]), colpack(f(cf_ln_b)[0]),
                            np.ascontiguousarray(f(cf_dw_w)[0].T.reshape(4, 128, 31).transpose(1, 0, 2).reshape(128, 124))], 1).astype(np.float32)
    wqb = f(mla_wq_b)[0].reshape(384, NH, 96)
    WqH = np.concatenate([wqb[:, :, 64:96], np.zeros((384, NH, 32), np.float32), wqb[:, :, 0:64]], -1).reshape(384, NH * 128)
    WqS = np.concatenate([wqb[:, :, 80:96], wqb[:, :, 64:80]], -1).reshape(384, NH * 32)
    wkvb = f(mla_wkv_b)[0].reshape(256, NH, 128)
    WkH = np.concatenate([np.zeros((256, NH, 64), np.float32), wkvb[:, :, 0:64]], -1).reshape(256, NH * 128)
    WvH = np.ascontiguousarray(wkvb[:, :, 64:128]).reshape(256, NH * 64)
    zp_p, zp_s = zpos_table(LP), zpos_table(LS)
    co, si = rope_cs(np.concatenate([np.arange(16, LP), np.arange(16)]))
    cs_all = np.concatenate([co, si], 1)
    shared = dict(ident=ident, xpad_p=xpad_p, maskpad=maskpad, valid_s=valid_s, zpos_p=zp_p, zpos_s=zp_s, fw1=f(hy_w1)[0], fw2=f(hy_w2)[0],
                  fw3=fw3, fcols=fcols, why=why, brow=brow, hcols=hcols, dskip=dskip, wconf=np.ascontiguousarray(win[:, 1536:2560]), ccols=ccols,
                  wout=f(ev_w_out)[0], bout=f(ev_b_out)[0:1], ln1g=f(ln1_g)[:, None, :], ln1b=f(ln1_b)[:, None, :],
                  ln2g=f(ln2_g)[:, None, :], ln2b=f(ln2_b)[:, None, :], w1=f(mlp_w1), w2=f(mlp_w2), wqa=f(mla_wq_a)[0], qg=f(mla_q_norm)[0:1],
                  WqH=WqH, WqS=WqS, wkva=f(mla_wkv_a)[0], kvg=f(mla_kv_norm)[0:1], cs_all=cs_all, WkH=WkH, WvH=WvH, wo=f(mla_wo)[0])
    for k, v in tabP.items():
        shared["tp_" + k] = v
    for k, v in tabS.items():
        shared["ts_" + k] = v
    P = build_nc()
    ims = []
    for c in range(8):
        m0 = 16 + 2048 * c
        xb, mb = make_xc(hs[c], 16, LS)
        css, Cqs, Sqs = [], [], []
        for pos in (np.concatenate([np.arange(m0, m0 + 2048), np.arange(16)]), np.concatenate([np.arange(16, LS), np.arange(16)])):
            co, si = rope_cs(pos)
            css.append(np.concatenate([co, si], 1))
            Cqs.append(np.concatenate([co[:2048].T, co[:2048].T], 0))
            Sqs.append(np.concatenate([-si[:2048].T, si[:2048].T], 0))
        tix = (2048 * c + 128 * np.arange(16)[None, :] + np.arange(128)[:, None]).astype(np.uint32)
        im = dict(shared)
        im.update(xh_s=np.concatenate([z1, hs[c], z1], 0), xc_s=xb, mask_s=mb, tokidx=tix,
                  cs=np.stack(css, 0), Cq=np.stack(Cqs, 0), Sq=np.stack(Sqs, 0))
        ims.append(check_inputs(P, im))
    r = run_bass_kernel_spmd(P.nc, ims, core_ids=list(range(8))).results
    y_prompt = np.concatenate([np.asarray(r[c]["out"])[0] for c in range(8)], 0)[None].astype(np.float32)
    y_sample = np.stack([np.asarray(r[c]["out"])[1] for c in range(8)], 0).astype(np.float32)
    return (y_prompt, y_sample)
```

```python
import math
from contextlib import ExitStack
import numpy as np
import ml_dtypes
import concourse.bass as bass
import concourse.mybir as mybir
from concourse.bass_utils import run_bass_kernel_spmd

F32 = mybir.dt.float32
BF16 = mybir.dt.bfloat16
AF = mybir.ActivationFunctionType
ALU = mybir.AluOpType
AX = mybir.AxisListType

D = 1024
NMETA = 16
DFF = 4096
ALPHA = 4 ** 0.25
LN_EPS = 1e-5
RMS_EPS = 1e-6
NH = 16


SEM_MAX = 24000


class Dep:
    __slots__ = ("w", "r")

    def __init__(self):
        self.w = None
        self.r = {}


class KB:
    def __init__(self, nc):
        self.nc = nc
        self.stack = ExitStack()
        self.raw = dict(pe=nc.tensor, act=nc.scalar, dve=nc.vector, pool=nc.gpsimd, sp=nc.sync)
        self.sem = {}
        self.cnt = {}
        self.seen = {e: {} for e in self.raw}
        self.semobj = []
        for e in ("pe", "act", "dve", "pool"):
            self.sem[e] = self._newsem("s_" + e)
            self.cnt[e] = 0
        self.dq = {}
        for q, n in (("sp", 20), ("act", 8), ("pool", 8)):
            self.dq[q] = dict(sems=[self._newsem(f"d_{q}{i}") for i in range(n)], vals=[0] * n, nxt=0)
        self.uid = 0

    def _newsem(self, name):
        s = self.stack.enter_context(self.nc.semaphore(name))
        self.semobj.append(s)
        return len(self.semobj) - 1

    def name(self, p):
        self.uid += 1
        return f"{p}{self.uid}"

    def sb(self, st, shape, dt, name="t"):
        return st.enter_context(self.nc.sbuf_tensor(self.name(name), list(shape), dt))

    def ps(self, st, shape, dt, name="p"):
        return st.enter_context(self.nc.psum_tensor(self.name(name), list(shape), dt))

    def _waits(self, eng, reads, writes, extra=None):
        need = {}

        def add(tok):
            if tok is None:
                return
            s, v, src = tok
            if src == "pe" and eng == "pe":
                return
            if need.get(s, 0) < v:
                need[s] = v

        for d in reads:
            add(d.w)
        for d in writes:
            add(d.w)
            for t in d.r.values():
                add(t)
        if extra:
            for t in extra:
                add(t)
        seen = self.seen[eng]
        for s, v in need.items():
            if seen.get(s, 0) < v:
                self.raw[eng].wait_ge(self.semobj[s], v)
                seen[s] = v

    def op(self, eng, fn, reads=(), writes=()):
        self._waits(eng, reads, writes)
        ins = fn(self.raw[eng])
        if self.cnt[eng] >= SEM_MAX:
            self.sem[eng] = self._newsem(self.name("s_" + eng))
            self.cnt[eng] = 0
        self.cnt[eng] += 1
        ins.then_inc(self.semobj[self.sem[eng]], 1)
        tok = (self.sem[eng], self.cnt[eng], eng)
        for d in reads:
            d.r[tok[0]] = tok
        for d in writes:
            d.w = tok
            d.r = {}
        return ins

    def dma(self, q, out, in_, reads=(), writes=(), **kw):
        dq = self.dq[q]
        i = dq["nxt"]
        dq["nxt"] = (i + 1) % len(dq["sems"])
        s = dq["sems"][i]
        extra = [(s, dq["vals"][i], "dma")] if dq["vals"][i] else None
        self._waits(q, reads, writes, extra)
        ins = self.raw[q].dma_start(out=out, in_=in_, **kw)
        dq["vals"][i] += 16
        ins.then_inc(self.semobj[s], 16)
        tok = (s, dq["vals"][i], "dma")
        for d in reads:
            d.r[s] = tok
        for d in writes:
            d.w = tok
            d.r = {}
        return ins

    def all_gather(self, in_ap, out_ap, reads=(), writes=()):
        if not hasattr(self, "ccsem"):
            self.ccsem = self._newsem("ccsem")
            self.ccval = 0
        self._waits("pool", reads, writes)
        ins = self.raw["pool"].collective_compute("AllGather", ALU.bypass, replica_groups=[list(range(8))],
                                                  ins=[in_ap.opt()], outs=[out_ap.opt()])
        self.ccval += 1
        ins.then_inc(self.semobj[self.ccsem], 1)
        tok = (self.ccsem, self.ccval, "cc")
        for d in reads:
            d.r[self.ccsem] = tok
        for d in writes:
            d.w = tok
            d.r = {}
        return ins

    def gather_rows(self, out, in_rows, idx, reads=(), writes=()):
        dq = self.dq["pool"]
        i = dq["nxt"]
        dq["nxt"] = (i + 1) % len(dq["sems"])
        s = dq["sems"][i]
        extra = [(s, dq["vals"][i], "dma")] if dq["vals"][i] else None
        self._waits("pool", reads, writes, extra)
        ins = self.raw["pool"].indirect_dma_start(out=out, out_offset=None, in_=in_rows,
                                                  in_offset=bass.IndirectOffsetOnAxis(ap=idx, axis=0))
        dq["vals"][i] += 16
        ins.then_inc(self.semobj[s], 16)
        tok = (s, dq["vals"][i], "dma")
        for d in reads:
            d.r[s] = tok
        for d in writes:
            d.w = tok
            d.r = {}
        return ins

    def barrier(self):
        toks = [(self.sem[e], self.cnt[e], e) for e in ("pe", "act", "dve", "pool") if self.cnt[e]]
        for q in self.dq.values():
            for s, v in zip(q["sems"], q["vals"]):
                if v:
                    toks.append((s, v, "dma"))
        if getattr(self, "ccval", 0):
            toks.append((self.ccsem, self.ccval, "cc"))
        for eng in ("pe", "act", "dve", "pool", "sp"):
            seen = self.seen[eng]
            for s, v, src in toks:
                if seen.get(s, 0) < v and not (s == self.sem.get(eng)):
                    self.raw[eng].wait_ge(self.semobj[s], v)
                    seen[s] = v

    def finish_wait(self):
        for q in self.dq.values():
            for s, v in zip(q["sems"], q["vals"]):
                if v and self.seen["sp"].get(s, 0) < v:
                    self.raw["sp"].wait_ge(self.semobj[s], v)
                    self.seen["sp"][s] = v


class Glob:
    pass


def setup_globals(kb, st):
    g = Glob()
    nc = kb.nc
    g.pall = kb.ps(st, [128, 8, 512], F32, "banks")
    g.psum = [g.pall[:, b, :] for b in range(8)]
    g.pd = [Dep() for _ in range(8)]
    g.ident_f = kb.sb(st, [128, 128], F32, "identf")
    g.ident_b = kb.sb(st, [128, 128], BF16, "identb")
    g.ident_d = Dep()
    g.ones_b = kb.sb(st, [128, 128], BF16, "onesb")
    g.ones_d = Dep()
    g.bk = -1
    return g


def load_ident(kb, g, ident_dram):
    kb.dma("sp", g.ident_f[:], ident_dram, writes=[g.ident_d])
    kb.op("dve", lambda e: e.tensor_copy(out=g.ident_b[:], in_=g.ident_f[:]), reads=[g.ident_d], writes=[g.ident_d])
    kb.op("pool", lambda e: e.memset(g.ones_b[:], 1.0), writes=[g.ones_d])


_rr = [0]


def cast_eng():
    _rr[0] += 1
    return ("dve", "pool", "act")[_rr[0] % 3]


def copy_op(kb, eng, out, in_, reads, writes):
    if eng == "act":
        return kb.op("act", lambda e: e.copy(out=out, in_=in_), reads=reads, writes=writes)
    return kb.op(eng, lambda e: e.tensor_copy(out=out, in_=in_), reads=reads, writes=writes)


def load_weight_bf16(kb, st_phase, dst, dst_dep, src, kc, ncols, stage_cols=2048):
    with ExitStack() as st:
        stg = [kb.sb(st, [128, stage_cols], F32, "wstg") for _ in range(3)]
        sd = [Dep() for _ in range(3)]
        i = 0
        for k in range(kc):
            for c0 in range(0, ncols, stage_cols):
                cn = min(stage_cols, ncols - c0)
                j = i % 3
                kb.dma("sp" if i % 2 == 0 else "pool", stg[j][:, :cn], src[k * 128:(k + 1) * 128, c0:c0 + cn], writes=[sd[j]])
                copy_op(kb, ("dve", "act")[i % 2], dst[:, k, c0:c0 + cn], stg[j][:, :cn], [sd[j]], [dst_dep])
                i += 1
        kb.barrier()


def load_bcast(kb, dst, dep, src_row):
    kb.dma("sp", dst, src_row.partition_broadcast(128) if len(src_row.shape) == 1 else src_row.broadcast_to([128, src_row.shape[-1]]), writes=[dep])


def layer_norm_tile(kb, r, rd, n, gt, bt, gbd, out, outd, small, smd, junk, junkd):
    s1, s2 = small[:, 0:1], small[:, 1:2]
    kb.op("act", lambda e: e.activation(out=junk[:n, :], in_=r[:n, :], func=AF.Identity, accum_out=s1[:n, :]), reads=[rd], writes=[junkd, smd])
    kb.op("act", lambda e: e.activation(out=junk[:n, :], in_=r[:n, :], func=AF.Square, accum_out=s2[:n, :]), reads=[rd], writes=[junkd, smd])
    mean, var, rstd = small[:, 2:3], small[:, 3:4], small[:, 4:5]
    kb.op("dve", lambda e: e.tensor_scalar(out=mean[:n, :], in0=s1[:n, :], scalar1=1.0 / D, scalar2=None, op0=ALU.mult), reads=[smd], writes=[smd])
    kb.op("dve", lambda e: e.tensor_tensor(out=var[:n, :], in0=mean[:n, :], in1=mean[:n, :], op=ALU.mult), reads=[smd], writes=[smd])
    kb.op("dve", lambda e: e.scalar_tensor_tensor(out=var[:n, :], in0=s2[:n, :], scalar=1.0 / D, in1=var[:n, :], op0=ALU.mult, op1=ALU.subtract), reads=[smd], writes=[smd])
    kb.op("act", lambda e: e.activation(out=rstd[:n, :], in_=var[:n, :], func=AF.Sqrt, bias=LN_EPS, scale=1.0), reads=[smd], writes=[smd])
    kb.op("dve", lambda e: e.reciprocal(out=rstd[:n, :], in_=rstd[:n, :]), reads=[smd], writes=[smd])
    kb.op("dve", lambda e: e.tensor_scalar(out=r[:n, :], in0=r[:n, :], scalar1=mean[:n, :], scalar2=rstd[:n, :], op0=ALU.subtract, op1=ALU.mult), reads=[smd, rd], writes=[rd])
    kb.op("dve", lambda e: e.tensor_tensor(out=r[:n, :], in0=r[:n, :], in1=gt[:n, :], op=ALU.mult), reads=[rd, gbd], writes=[rd])
    kb.op("pool", lambda e: e.tensor_tensor(out=out[:n, :], in0=r[:n, :], in1=bt[:n, :], op=ALU.add), reads=[rd, gbd], writes=[outd])


def mm(kb, out, lhsT, rhs, start, stop, reads, writes):
    return kb.op("pe", lambda e: e.matmul(out, lhsT=lhsT, rhs=rhs, start=start, stop=stop), reads, writes)


def tt(kb, eng, out, in0, in1, op, reads, writes):
    return kb.op(eng, lambda e: e.tensor_tensor(out=out, in0=in0, in1=in1, op=op), reads, writes)


def ts(kb, eng, out, in0, s1, s2, op0, op1, reads, writes):
    if s2 is None:
        return kb.op(eng, lambda e: e.tensor_scalar(out=out, in0=in0, scalar1=s1, scalar2=None, op0=op0), reads, writes)
    return kb.op(eng, lambda e: e.tensor_scalar(out=out, in0=in0, scalar1=s1, scalar2=s2, op0=op0, op1=op1), reads, writes)


def stt(kb, eng, out, in0, scalar, in1, op0, op1, reads, writes):
    return kb.op("dve", lambda e: e.scalar_tensor_tensor(out=out, in0=in0, scalar=scalar, in1=in1, op0=op0, op1=op1), reads, writes)


def act(kb, out, in_, func, reads, writes, **kw):
    return kb.op("act", lambda e: e.activation(out=out, in_=in_, func=func, **kw), reads, writes)


def nextbank(g):
    g.bk = (g.bk + 1) % 8
    return g.bk


def transpose_tile(kb, g, src, srcd, n, dstT, dstd, col0, kc=8):
    for k0 in range(0, kc, 4):
        b = nextbank(g)
        kn = min(4, kc - k0)
        pv = g.psum[b][:, :].rearrange("p (k t) -> p k t", k=4)
        for k in range(kn):
            kb.op("pe", lambda e, k=k: e.transpose(pv[:, k, :n], src[:n, (k0 + k) * 128:(k0 + k + 1) * 128], g.ident_f[:n, :n]),
                  reads=[srcd, g.ident_d], writes=[g.pd[b]])
        copy_op(kb, ("dve", "act")[b % 2], dstT[:, k0:k0 + kn, col0:col0 + n], pv[:, 0:kn, :n], [g.pd[b]], [dstd])


def phase_proj_ln(kb, g, tiles, fm, W, bias, lng, lnb):
    with ExitStack() as st:
        Wb = kb.sb(st, [128, 8, D], BF16, "Wb")
        Wd = Dep()
        load_weight_bf16(kb, st, Wb, Wd, W, 8, D)
        gt = kb.sb(st, [128, D], F32, "g")
        bt = kb.sb(st, [128, D], F32, "b")
        gbd = Dep()
        load_bcast(kb, gt[:], gbd, lng)
        load_bcast(kb, bt[:], gbd, lnb)
        if bias is not None:
            bi = kb.sb(st, [128, D], F32, "bias")
            load_bcast(kb, bi[:], gbd, bias)
        NB = 4
        hb = [kb.sb(st, [128, D], F32, "h") for _ in range(NB)]
        hd = [Dep() for _ in range(NB)]
        yT = [kb.sb(st, [128, 8, 128], BF16, "yT") for _ in range(NB)]
        yTd = [Dep() for _ in range(NB)]
        if not fm:
            yb = [kb.sb(st, [128, D], F32, "y") for _ in range(NB)]
            yd = [Dep() for _ in range(NB)]
        rb = [kb.sb(st, [128, D], F32, "r") for _ in range(NB)]
        rd = [Dep() for _ in range(NB)]
        junk = kb.sb(st, [128, D], F32, "junk")
        junkd = Dep()
        small = [kb.sb(st, [128, 8], F32, "small") for _ in range(NB)]
        smd = [Dep() for _ in range(NB)]
        for i, (hap, yap, oap, n) in enumerate(tiles):
            j = i % NB
            kb.dma("sp", hb[j][:n, :], hap, writes=[hd[j]])
            if fm:
                for qi, (ksl, src, dep) in enumerate(yap):
                    kb.dma("sp", yT[j][:, ksl, :n], src, reads=[dep] if dep is not None else [], writes=[yTd[j]])
            else:
                kb.dma("sp", yb[j][:n, :], yap, writes=[yd[j]])
                transpose_tile(kb, g, yb[j], yd[j], n, yT[j], yTd[j], 0)
            bks = (nextbank(g), nextbank(g))
            for half, bk in enumerate(bks):
                for k in range(8):
                    mm(kb, g.psum[bk][:n, :], yT[j][:, k, :n], Wb[:, k, half * 512:(half + 1) * 512], k == 0, k == 7,
                       [yTd[j], Wd], [g.pd[bk]])
            for half, bk in enumerate(bks):
                sl = slice(half * 512, (half + 1) * 512)
                if bias is not None:
                    tt(kb, "dve", rb[j][:n, sl], g.psum[bk][:n, :], bi[:n, sl], ALU.add, [g.pd[bk], gbd], [rd[j]])
                else:
                    copy_op(kb, "act", rb[j][:n, sl], g.psum[bk][:n, :], [g.pd[bk]], [rd[j]])
            stt(kb, "pool", rb[j][:n, :], hb[j][:n, :], ALPHA, rb[j][:n, :], ALU.mult, ALU.add, [hd[j], rd[j]], [rd[j]])
            layer_norm_tile(kb, rb[j], rd[j], n, gt, bt, gbd, rb[j], rd[j], small[j], smd[j], junk, junkd)
            kb.dma("pool", oap, rb[j][:n, :], reads=[rd[j]])
        kb.barrier()


def phase_mlp_ln(kb, g, tiles, W1, W2, lng, lnb):
    with ExitStack() as st:
        W1b = kb.sb(st, [128, 8, DFF], BF16, "W1b")
        W2b = kb.sb(st, [128, 32, D], BF16, "W2b")
        Wd = Dep()
        load_weight_bf16(kb, st, W1b, Wd, W1, 8, DFF)
        load_weight_bf16(kb, st, W2b, Wd, W2, 32, D, stage_cols=1024)
        gt = kb.sb(st, [128, D], F32, "g")
        bt = kb.sb(st, [128, D], F32, "b")
        gbd = Dep()
        load_bcast(kb, gt[:], gbd, lng)
        load_bcast(kb, bt[:], gbd, lnb)
        hb = [kb.sb(st, [128, D], F32, "h") for _ in range(4)]
        hd = [Dep() for _ in range(4)]
        hT = kb.sb(st, [128, 8, 512], BF16, "hT")
        hTd = Dep()
        uT = kb.sb(st, [128, 32, 512], BF16, "uT")
        uTd = [Dep() for _ in range(32)]
        rl = [kb.sb(st, [128, 512], F32, "relu") for _ in range(2)]
        rld = [Dep() for _ in range(2)]
        junk = kb.sb(st, [128, D], BF16, "junk")
        junkd = Dep()
        small = [kb.sb(st, [128, 8], F32, "small") for _ in range(4)]
        smd = [Dep() for _ in range(4)]
        for s0 in range(0, len(tiles), 4):
            grp = tiles[s0:s0 + 4]
            offs = []
            tot = 0
            for i, (iap, oap, n) in enumerate(grp):
                kb.dma("sp", hb[i][:n, :], iap, writes=[hd[i]])
                offs.append(tot)
                tot += n
            for i, (iap, oap, n) in enumerate(grp):
                transpose_tile(kb, g, hb[i], hd[i], n, hT, hTd, offs[i])
            for j in range(32):
                bk = nextbank(g)
                for k in range(8):
                    mm(kb, g.psum[bk][:, :tot], W1b[:, k, j * 128:(j + 1) * 128], hT[:, k, :tot], k == 0, k == 7, [Wd, hTd], [g.pd[bk]])
                q = j % 2
                act(kb, rl[q][:, :tot], g.psum[bk][:, :tot], AF.Relu, [g.pd[bk]], [rld[q]])
                tt(kb, "pool" if j % 4 == 0 else "dve", uT[:, j, :tot], rl[q][:, :tot], rl[q][:, :tot], ALU.mult, [rld[q]], [uTd[j]])
            for i, (iap, oap, n) in enumerate(grp):
                bks = (nextbank(g), nextbank(g))
                for half, bk in enumerate(bks):
                    for j in range(32):
                        mm(kb, g.psum[bk][:n, :], uT[:, j, offs[i]:offs[i] + n], W2b[:, j, half * 512:(half + 1) * 512], j == 0, j == 31,
                           [uTd[j], Wd], [g.pd[bk]])
                for half, bk in enumerate(bks):
                    sl = slice(half * 512, (half + 1) * 512)
                    stt(kb, "dve", hb[i][:n, sl], hb[i][:n, sl], ALPHA, g.psum[bk][:n, :], ALU.mult, ALU.add, [hd[i], g.pd[bk]], [hd[i]])
                layer_norm_tile(kb, hb[i], hd[i], n, gt, bt, gbd, hb[i], hd[i], small[i], smd[i], junk, junkd)
                kb.dma("pool", oap, hb[i][:n, :], reads=[hd[i]])
        kb.barrier()


def rms_rstd(kb, src, srcd, n, width, small, smd, junk, junkd, col):
    ss, rs = small[:, col:col + 1], small[:, col + 1:col + 2]
    act(kb, junk[:n, :width], src[:n, :width], AF.Square, [srcd], [junkd, smd], accum_out=ss[:n, :])
    act(kb, rs[:n, :], ss[:n, :], AF.Sqrt, [smd], [smd], bias=RMS_EPS, scale=1.0 / width)
    kb.op("dve", lambda e: e.reciprocal(out=rs[:n, :], in_=rs[:n, :]), [smd], [smd])
    return rs


def phase_qkv(kb, g, seqs, wqa, qg, WqH, WqS, wkva, kvg):
    with ExitStack() as st:
        wqa_b = kb.sb(st, [128, 8, 384], BF16, "wqa")
        wkva_b = kb.sb(st, [128, 8, 288], BF16, "wkva")
        wqh_b = kb.sb(st, [128, 3, NH * 128], BF16, "wqh")
        wqs_b = kb.sb(st, [128, 3, NH * 32], BF16, "wqs")
        Wd = Dep()
        load_weight_bf16(kb, st, wqa_b, Wd, wqa, 8, 384)
        load_weight_bf16(kb, st, wkva_b, Wd, wkva, 8, 288)
        load_weight_bf16(kb, st, wqh_b, Wd, WqH, 3, NH * 128)
        load_weight_bf16(kb, st, wqs_b, Wd, WqS, 3, NH * 32)
        qgt = kb.sb(st, [128, 384], F32, "qg")
        kvgt = kb.sb(st, [128, 256], F32, "kvg")
        gd = Dep()
        load_bcast(kb, qgt[:], gd, qg)
        load_bcast(kb, kvgt[:], gd, kvg)
        hb = [kb.sb(st, [128, D], F32, "h") for _ in range(4)]
        hd = [Dep() for _ in range(4)]
        hT = kb.sb(st, [128, 8, 512], BF16, "hT")
        hTd = Dep()
        cq = [kb.sb(st, [128, 384], F32, "cq") for _ in range(2)]
        cqd = [Dep() for _ in range(2)]
        cqT = kb.sb(st, [128, 3, 512], BF16, "cqT")
        cqTd = Dep()
        kvr = [kb.sb(st, [128, 288], F32, "kvr") for _ in range(2)]
        kvrd = [Dep() for _ in range(2)]
        kvo = [kb.sb(st, [128, 288], F32, "kvo") for _ in range(2)]
        kvod = [Dep() for _ in range(2)]
        cst = [kb.sb(st, [128, 32], F32, "cs") for _ in range(2)]
        csd = [Dep() for _ in range(2)]
        tmp = [kb.sb(st, [128, 64], F32, "tmp") for _ in range(2)]
        tmpd = [Dep() for _ in range(2)]
        junk = kb.sb(st, [128, 384], F32, "junk")
        junkd = Dep()
        small = [kb.sb(st, [128, 8], F32, "small") for _ in range(2)]
        smd = [Dep() for _ in range(2)]
        Ct = kb.sb(st, [32, 2048], F32, "C")
        St = kb.sb(st, [32, 2048], F32, "S")
        CSd = Dep()
        qsw = [kb.sb(st, [32, 512], F32, "qsw") for _ in range(2)]
        qswd = [Dep() for _ in range(2)]
        qo = [kb.sb(st, [128, 512], BF16, "qo") for _ in range(2)]
        qod = [Dep() for _ in range(2)]
        it = 0
        for sq in seqs:
            if not sq.get("kv_only"):
                kb.dma("sp", Ct[:], sq["CS"][0], writes=[CSd])
                kb.dma("sp", St[:], sq["CS"][1], writes=[CSd])
            tiles = sq["tiles"]
            for s0 in range(0, len(tiles), 4):
                grp = tiles[s0:s0 + 4]
                offs, tot = [], 0
                for i, (hap, kvap, csap, n) in enumerate(grp):
                    kb.dma("sp", hb[i][:n, :], hap, writes=[hd[i]])
                    offs.append(tot)
                    tot += n
                for i, (hap, kvap, csap, n) in enumerate(grp):
                    transpose_tile(kb, g, hb[i], hd[i], n, hT, hTd, offs[i])
                is_main = (tot == 512) and not sq.get("kv_only")
                for i, (hap, kvap, csap, n) in enumerate(grp):
                    j = it % 2
                    it += 1
                    kb.dma("sp", cst[j][:n, :], csap, writes=[csd[j]])
                    bk = nextbank(g)
                    for k in range(8):
                        mm(kb, g.psum[bk][:n, :288], hT[:, k, offs[i]:offs[i] + n], wkva_b[:, k, :], k == 0, k == 7, [hTd, Wd], [g.pd[bk]])
                    copy_op(kb, "act", kvr[j][:n, :], g.psum[bk][:n, :288], [g.pd[bk]], [kvrd[j]])
                    rs = rms_rstd(kb, kvr[j], kvrd[j], n, 256, small[j], smd[j], junk, junkd, 0)
                    stt(kb, "dve", kvo[j][:n, 0:256], kvr[j][:n, 0:256], rs[:n, :], kvgt[:n, :], ALU.mult, ALU.mult, [kvrd[j], smd[j], gd], [kvod[j]])
                    x1, x2 = kvr[j][:n, 256:272], kvr[j][:n, 272:288]
                    co, si = cst[j][:n, 0:16], cst[j][:n, 16:32]
                    t = tmp[j]
                    tt(kb, "dve", t[:n, 0:16], x1, co, ALU.mult, [kvrd[j], csd[j]], [tmpd[j]])
                    tt(kb, "dve", t[:n, 16:32], x2, si, ALU.mult, [kvrd[j], csd[j]], [tmpd[j]])
                    tt(kb, "dve", t[:n, 32:48], x1, si, ALU.mult, [kvrd[j], csd[j]], [tmpd[j]])
                    tt(kb, "dve", t[:n, 48:64], x2, co, ALU.mult, [kvrd[j], csd[j]], [tmpd[j]])
                    tt(kb, "dve", kvo[j][:n, 256:272], t[:n, 0:16], t[:n, 16:32], ALU.subtract, [tmpd[j]], [kvod[j]])
                    tt(kb, "dve", kvo[j][:n, 272:288], t[:n, 32:48], t[:n, 48:64], ALU.add, [tmpd[j]], [kvod[j]])
                    kb.dma("pool", kvap, kvo[j][:n, :], reads=[kvod[j]])
                    if not is_main:
                        continue
                    bk = nextbank(g)
                    for k in range(8):
                        mm(kb, g.psum[bk][:n, :384], hT[:, k, offs[i]:offs[i] + n], wqa_b[:, k, :], k == 0, k == 7, [hTd, Wd], [g.pd[bk]])
                    copy_op(kb, "act", cq[j][:n, :], g.psum[bk][:n, :384], [g.pd[bk]], [cqd[j]])
                    rs = rms_rstd(kb, cq[j], cqd[j], n, 384, small[j], smd[j], junk, junkd, 2)
                    stt(kb, "dve", cq[j][:n, :], cq[j][:n, :], rs[:n, :], qgt[:n, :], ALU.mult, ALU.mult, [cqd[j], smd[j], gd], [cqd[j]])
                    transpose_tile(kb, g, cq[j], cqd[j], n, cqT, cqTd, offs[i], kc=3)
                if not is_main:
                    continue
                q0 = (s0 // 4) * 512
                for h in range(NH):
                    j = h % 2
                    bka, bkb = nextbank(g), nextbank(g)
                    for k in range(3):
                        mm(kb, g.psum[bka][:, :], wqh_b[:, k, h * 128:(h + 1) * 128], cqT[:, k, :], k == 0, k == 2, [Wd, cqTd], [g.pd[bka]])
                    for k in range(3):
                        mm(kb, g.psum[bkb][:32, :], wqs_b[:, k, h * 32:(h + 1) * 32], cqT[:, k, :], k == 0, k == 2, [Wd, cqTd], [g.pd[bkb]])
                    tt(kb, "dve", qsw[j][:, :], g.psum[bkb][:32, :], St[:, q0:q0 + 512], ALU.mult, [g.pd[bkb], CSd], [qswd[j]])
                    rope_q(kb, g, qo[j], qod[j], bka, qsw[j], qswd[j], Ct, CSd, q0, st, small)
                    copy_op(kb, "act", qo[j][32:64, :], g.psum[bka][32:64, :], [g.pd[bka]], [qod[j]])
                    copy_op(kb, "act", qo[j][64:128, :], g.psum[bka][64:128, :], [g.pd[bka]], [qod[j]])
                    kb.dma("pool", sq["qt"](h, q0), qo[j][:, :], reads=[qod[j]])
        kb.barrier()


_ropetmp = {}


def rope_q(kb, g, qo, qod, bka, qsw, qswd, Ct, CSd, q0, st, small):
    key = id(st)
    if key not in _ropetmp:
        _ropetmp[key] = (kb.sb(st, [32, 512], F32, "rq"), Dep())
    t, td = _ropetmp[key]
    tt(kb, "dve", t[:, :], g.psum[bka][0:32, :], Ct[:, q0:q0 + 512], ALU.mult, [g.pd[bka], CSd], [td])
    tt(kb, "pool", qo[0:32, :], t[:, :], qsw[:, :], ALU.add, [td, qswd], [qod])


QK_SCALE = 96 ** -0.5


def phase_attn(kb, g, seqs, WkH, WvH):
    NKmax = max(sum(n for _, n in sq["kchunks"]) for sq in seqs)
    NCH = max(len(sq["kchunks"]) for sq in seqs)
    with ExitStack() as st:
        wk_b = kb.sb(st, [128, 2, NH * 128], BF16, "wk")
        wv_b = kb.sb(st, [128, 2, NH * 64], BF16, "wv")
        Wd = Dep()
        load_weight_bf16(kb, st, wk_b, Wd, WkH, 2, NH * 128)
        load_weight_bf16(kb, st, wv_b, Wd, WvH, 2, NH * 64, stage_cols=1024)
        ckvT = kb.sb(st, [128, 3, NKmax], BF16, "ckvT")
        ckvTd = Dep()
        KT = kb.sb(st, [128, NKmax], BF16, "KT")
        KTd = Dep()
        V = kb.sb(st, [128, NCH, 66], BF16, "V")
        Vd = Dep()
        QT = [kb.sb(st, [128, 2048], BF16, "QT") for _ in range(2)]
        QTd = [Dep() for _ in range(2)]
        P = [kb.sb(st, [128, 1024], BF16, "P") for _ in range(2)]
        Pd = [Dep() for _ in range(2)]
        oT = kb.sb(st, [128, 1024], F32, "oT")
        oTd = Dep()
        osm = [kb.sb(st, [128, 4, 64], F32, "osm") for _ in range(2)]
        osmd = [Dep() for _ in range(2)]
        rec = [kb.sb(st, [128, 4, 1], F32, "rec") for _ in range(2)]
        recd = [Dep() for _ in range(2)]
        kvin = [kb.sb(st, [128, 288], F32, "kvin") for _ in range(2)]
        kvind = [Dep() for _ in range(2)]
        kb.op("pool", lambda e: e.memset(V[:, :, 64:66], 1.0), [], [Vd])
        for sq in seqs:
            chunks = sq["kchunks"]
            NK = sum(n for _, n in chunks)
            coff = []
            c0 = 0
            for ci, (kvap, n) in enumerate(chunks):
                j = ci % 2
                kb.dma("sp" if ci % 2 == 0 else "pool", kvin[j][:n, :], kvap, writes=[kvind[j]])
                b = nextbank(g)
                pv = g.psum[b].rearrange("p (k t) -> p k t", k=4)
                for k, w in ((0, 128), (1, 128), (2, 32)):
                    kb.op("pe", lambda e, k=k, w=w: e.transpose(pv[:w, k, :n], kvin[j][:n, k * 128:k * 128 + w], g.ident_f[:n, :n]),
                          [kvind[j], g.ident_d], [g.pd[b]])
                copy_op(kb, "dve", ckvT[:, 0:2, c0:c0 + n], pv[:, 0:2, :n], [g.pd[b]], [ckvTd])
                copy_op(kb, "act", ckvT[0:32, 2, c0:c0 + n], pv[0:32, 2, :n], [g.pd[b]], [ckvTd])
                coff.append(c0)
                c0 += n
            for h in range(NH):
                qj = h % 2
                kb.dma("sp", QT[qj][:, :], sq["qt"](h), writes=[QTd[qj]])
                for bi, k0 in enumerate(range(0, NK, 512)):
                    kn = min(512, NK - k0)
                    b = 6 + bi % 2
                    mm(kb, g.psum[b][:, :kn], wk_b[:, 0, h * 128:(h + 1) * 128], ckvT[:, 0, k0:k0 + kn], True, False, [Wd, ckvTd], [g.pd[b]])
                    mm(kb, g.psum[b][:, :kn], wk_b[:, 1, h * 128:(h + 1) * 128], ckvT[:, 1, k0:k0 + kn], False, False, [Wd, ckvTd], [g.pd[b]])
                    mm(kb, g.psum[b][:, :kn], g.ident_b[0:32, :], ckvT[0:32, 2, k0:k0 + kn], False, True, [g.ident_d, ckvTd], [g.pd[b]])
                    copy_op(kb, ("dve", "pool")[bi % 2] if False else "dve", KT[:, k0:k0 + kn], g.psum[b][:, :kn], [g.pd[b]], [KTd])
                for gi, cg in enumerate(range(0, len(chunks), 8)):
                    cn = min(8, len(chunks) - cg)
                    b = 6 + gi % 2
                    for ci in range(cn):
                        n = chunks[cg + ci][1]
                        o = coff[cg + ci]
                        for k in range(2):
                            mm(kb, g.psum[b][:n, ci * 64:(ci + 1) * 64], ckvT[:, k, o:o + n], wv_b[:, k, h * 64:(h + 1) * 64], k == 0, k == 1,
                               [ckvTd, Wd], [g.pd[b]])
                    copy_op(kb, "act", V[:, cg:cg + cn, 0:64], g.psum[b][:, :cn * 64].rearrange("p (c d) -> p c d", d=64), [g.pd[b]], [Vd])
                for qsb in range(2):
                    for ci, (kvap, n) in enumerate(chunks):
                        o = coff[ci]
                        sb0 = 2 + 2 * (ci % 2)
                        pj = ci % 2
                        for i in range(2):
                            mm(kb, g.psum[sb0 + i][:n, :], KT[:, o:o + n], QT[qj][:, qsb * 1024 + i * 512:qsb * 1024 + (i + 1) * 512], True, True,
                               [KTd, QTd[qj]], [g.pd[sb0 + i]])
                        act(kb, P[pj][:n, :].rearrange("p (a b) -> p a b", a=2), g.pall[:n, sb0:sb0 + 2, :], AF.Exp,
                            [g.pd[sb0], g.pd[sb0 + 1]], [Pd[pj]], scale=QK_SCALE)
                        for i in range(2):
                            mm(kb, g.psum[i][:65, :], V[:n, ci, 0:65], P[pj][:n, i * 512:(i + 1) * 512], ci == 0, ci == len(chunks) - 1,
                               [Vd, Pd[pj]], [g.pd[i]])
                    copy_op(kb, "dve", oT[:65, :].rearrange("p (a b) -> p a b", a=2), g.pall[:65, 0:2, :], [g.pd[0], g.pd[1]], [oTd])
                    for half in range(2):
                        b = 6 + half
                        oj = half
                        pv = g.psum[b][:, 0:4 * 65].rearrange("p (t c) -> p t c", c=65)
                        for t in range(4):
                            q0 = half * 512 + t * 128
                            kb.op("pe", lambda e, t=t, q0=q0: e.transpose(pv[:, t, :], oT[:65, q0:q0 + 128], g.ident_f[:65, :65]),
                                  [oTd, g.ident_d], [g.pd[b]])
                        kb.op("dve", lambda e: e.reciprocal(out=rec[oj][:, :, :], in_=pv[:, :, 64:65]), [g.pd[b]], [recd[oj]])
                        tt(kb, "dve", osm[oj][:, :, :], pv[:, :, 0:64], rec[oj][:, :, :].broadcast_to([128, 4, 64]), ALU.mult,
                           [g.pd[b], recd[oj]], [osmd[oj]])
                        kb.dma("pool", sq["o"](qsb, half, h), osm[oj][:, :, :], reads=[osmd[oj]])
        kb.barrier()


XC = 2124


def phase_conf(kb, g, seqs, w_conf, cols_ap):
    with ExitStack() as st:
        wb = kb.sb(st, [128, 8, 1024], BF16, "wconf")
        Wd = Dep()
        load_weight_bf16(kb, st, wb, Wd, w_conf, 8, 1024)
        cols = kb.sb(st, [128, 20 + 124], F32, "cols")
        cd = Dep()
        kb.dma("sp", cols[:], cols_ap, writes=[cd])
        Dg = kb.sb(st, [128, 4, 31, 128], BF16, "Dg")
        Dgd = Dep()
        for j in range(4):
            for k in range(31):
                ts(kb, ("dve", "pool")[k % 2], Dg[:, j, k, :], g.ident_f[:, :], cols[:, 20 + j * 31 + k:20 + j * 31 + k + 1], None, ALU.mult, None,
                   [g.ident_d, cd], [Dgd])
        xin = [kb.sb(st, [128, D], F32, "xin") for _ in range(2)]
        xind = [Dep() for _ in range(2)]
        xT = kb.sb(st, [128, 8, XC], BF16, "xT")
        xTd = Dep()
        hT = kb.sb(st, [128, 4, XC], BF16, "hT")
        hTd = Dep()
        mask = kb.sb(st, [128, XC], F32, "mask")
        maskd = Dep()
        sg = [kb.sb(st, [128, 512], F32, "sg") for _ in range(2)]
        sgd = [Dep() for _ in range(2)]
        cc2 = [kb.sb(st, [128, 4, 512], F32, "cc") for _ in range(2)]
        ccd2 = [Dep(), Dep()]
        cb2 = [kb.sb(st, [128, 4, 512], BF16, "cb") for _ in range(2)]
        cbd2 = [Dep(), Dep()]
        sq2 = [kb.sb(st, [128, 4, 512], BF16, "sq") for _ in range(2)]
        sqd2 = [Dep(), Dep()]
        mean2 = [kb.sb(st, [128, 512], F32, "mean") for _ in range(2)]
        rstd2 = [kb.sb(st, [128, 512], F32, "rstd") for _ in range(2)]
        std2 = [Dep(), Dep()]
        blk_i = 0
        yt = [kb.sb(st, [128, 512], F32, "yt") for _ in range(2)]
        ytd = [Dep() for _ in range(2)]
        yo = [kb.sb(st, [128, 512], BF16, "yo") for _ in range(2)]
        yod = [Dep() for _ in range(2)]
        for s_ in seqs:
            NC = s_.get("ncols", XC)
            kb.dma("sp", mask[:, :NC], s_["mask"].broadcast_to([128, NC]), writes=[maskd])
            for ti, t0 in enumerate(range(0, NC, 128)):
                n = min(128, NC - t0)
                j = ti % 2
                kb.dma("sp", xin[j][:n, :], s_["x"][t0:t0 + n, :], writes=[xind[j]])
                transpose_tile(kb, g, xin[j], xind[j], n, xT, xTd, t0)
            for bi, c0 in enumerate(range(0, NC, 512)):
                cn = min(512, NC - c0)
                for j in range(4):
                    ba, bg = nextbank(g), nextbank(g)
                    for k in range(8):
                        mm(kb, g.psum[ba][:, :cn], wb[:, k, j * 128:(j + 1) * 128], xT[:, k, c0:c0 + cn], k == 0, k == 7, [Wd, xTd], [g.pd[ba]])
                    for k in range(8):
                        mm(kb, g.psum[bg][:, :cn], wb[:, k, 512 + j * 128:512 + (j + 1) * 128], xT[:, k, c0:c0 + cn], k == 0, k == 7, [Wd, xTd], [g.pd[bg]])
                    q = j % 2
                    act(kb, sg[q][:, :cn], g.psum[bg][:, :cn], AF.Sigmoid, [g.pd[bg], cd], [sgd[q]], bias=cols[:, 4 + j:5 + j], scale=1.0)
                    stt(kb, "dve", sg[q][:, :cn], g.psum[ba][:, :cn], cols[:, j:j + 1], sg[q][:, :cn], ALU.add, ALU.mult, [g.pd[ba], cd, sgd[q]], [sgd[q]])
                    tt(kb, "pool", hT[:, j, c0:c0 + cn], sg[q][:, :cn], mask[:, c0:c0 + cn], ALU.mult, [sgd[q], maskd], [hTd])
            blocks = s_.get("blocks") or ([(15, 16)] + [(61 + 512 * i, 512) for i in range(4)])
            for bi, (c0, cn) in enumerate(blocks):
                pp = blk_i % 2
                blk_i += 1
                cc, ccd, cb, cbd, sq, sqd = cc2[pp], ccd2[pp], cb2[pp], cbd2[pp], sq2[pp], sqd2[pp]
                mean, rstd, std = mean2[pp], rstd2[pp], std2[pp]
                for j in range(4):
                    b = nextbank(g)
                    for k in range(31):
                        mm(kb, g.psum[b][:, :cn], Dg[:, j, k, :], hT[:, j, c0 + k - 15:c0 + k - 15 + cn], k == 0, k == 30, [Dgd, hTd], [g.pd[b]])
                    act(kb, cc[:, j, :cn], g.psum[b][:, :cn], AF.Identity, [g.pd[b], cd], [ccd], bias=cols[:, 8 + j:9 + j], scale=1.0)
                    copy_op(kb, "dve", cb[:, j, :cn], cc[:, j, :cn], [ccd], [cbd])
                    tt(kb, "dve", sq[:, j, :cn], cc[:, j, :cn], cc[:, j, :cn], ALU.mult, [ccd], [sqd])
                b1, b2 = nextbank(g), nextbank(g)
                for j in range(4):
                    mm(kb, g.psum[b1][:, :cn], g.ones_b[:, :], cb[:, j, :cn], j == 0, j == 3, [g.ones_d, cbd], [g.pd[b1]])
                for j in range(4):
                    mm(kb, g.psum[b2][:, :cn], g.ones_b[:, :], sq[:, j, :cn], j == 0, j == 3, [g.ones_d, sqd], [g.pd[b2]])
                act(kb, mean[:, :cn], g.psum[b1][:, :cn], AF.Copy, [g.pd[b1]], [std], scale=1.0 / 512)
                tt(kb, "pool", rstd[:, :cn], mean[:, :cn], mean[:, :cn], ALU.mult, [std], [std])
                stt(kb, "dve", rstd[:, :cn], g.psum[b2][:, :cn], 1.0 / 512, rstd[:, :cn], ALU.mult, ALU.subtract, [g.pd[b2], std], [std])
                act(kb, rstd[:, :cn], rstd[:, :cn], AF.Sqrt, [std], [std], bias=LN_EPS, scale=1.0)
                kb.op("dve", lambda e: e.reciprocal(out=rstd[:, :cn], in_=rstd[:, :cn]), [std], [std])
                for j in range(4):
                    q = j % 2
                    tt(kb, "dve", yt[q][:, :cn], cc[:, j, :cn], mean[:, :cn], ALU.subtract, [ccd, std], [ytd[q]])
                    tt(kb, "dve", yt[q][:, :cn], yt[q][:, :cn], rstd[:, :cn], ALU.mult, [ytd[q], std], [ytd[q]])
                    ts(kb, "dve", yt[q][:, :cn], yt[q][:, :cn], cols[:, 12 + j:13 + j], cols[:, 16 + j:17 + j], ALU.mult, ALU.add, [ytd[q], cd], [ytd[q]])
                    act(kb, yo[q][:, :cn], yt[q][:, :cn], AF.Silu, [ytd[q]], [yod[q]])
                    kb.dma("pool", s_["out"](j, bi, cn), yo[q][:, :cn], reads=[yod[q]])
        kb.barrier()


I32 = mybir.dt.int32
TWO_PI = 2.0 * math.pi


class FCfg:
    def __init__(self, L, rows, N1, nq, CB):
        self.L, self.rows, self.N1, self.nq, self.CB = L, rows, N1, nq, CB
        self.N2 = 86 * nq
        self.N = N1 * self.N2
        self.NF = N1 // 2 + 1
        assert self.N >= 2 * L - 1 and rows * self.N2 >= L


CFG_P = FCfg(16400, 64, 128, 3, 8)
CFG_S = FCfg(2064, 24, 48, 1, 32)


def fft_tables(cfg):
    N1, N2, N, rows, nq, NF = cfg.N1, cfg.N2, cfg.N, cfg.rows, cfg.nq, cfg.NF
    n1 = np.arange(rows)[:, None].astype(np.float64)
    k1 = np.arange(NF)[None, :].astype(np.float64)
    a = 2 * np.pi * n1 * k1 / N1
    F1 = np.concatenate([np.cos(a), -np.sin(a)], 1)
    n2 = np.arange(N2)[:, None].astype(np.float64)
    a = 2 * np.pi * n2 * k1 / N
    tw = np.stack([np.cos(a), -np.sin(a)], 1)
    tw = tw.reshape(nq, 86, 2, NF).transpose(1, 0, 2, 3)
    m = np.arange(N2)[None, :].astype(np.float64)
    a = 2 * np.pi * n2 * m / N2
    F2 = np.stack([np.cos(a), -np.sin(a), np.sin(a)], 0)
    F2 = F2.reshape(3, nq, 86, N2).transpose(2, 0, 1, 3)
    kk = np.arange(NF)[:, None].astype(np.float64)
    a = 2 * np.pi * kk * np.arange(N2)[None, :] / N
    twc = np.stack([np.cos(a), np.sin(a)], 1)
    a = 2 * np.pi * kk * np.arange(rows)[None, :] / N1
    wgt = np.full((NF, 1), 2.0)
    wgt[0, 0] = 1.0
    wgt[NF - 1, 0] = 1.0
    G1 = np.stack([wgt * np.cos(a) / N, -wgt * np.sin(a) / N], 1)
    bf = ml_dtypes.bfloat16
    return dict(F1=F1.astype(np.float32).astype(bf), tw=np.ascontiguousarray(tw).astype(np.float32),
                F2=np.ascontiguousarray(F2).astype(np.float32).astype(bf), twc=twc.astype(np.float32),
                G1=G1.astype(np.float32).astype(bf))


class FTab:
    pass


def fft_load_tables(kb, st, cfg, tabs):
    t = FTab()
    t.d = Dep()
    t.F1 = kb.sb(st, [cfg.rows, 2 * cfg.NF], BF16, "F1")
    t.tw = kb.sb(st, [86, cfg.nq, 2, cfg.NF], F32, "tw")
    t.F2 = kb.sb(st, [86, 3, cfg.nq, cfg.N2], BF16, "F2")
    t.twc = kb.sb(st, [cfg.NF, 2, cfg.N2], F32, "twc")
    t.G1 = kb.sb(st, [cfg.NF, 2, cfg.rows], BF16, "G1")
    for nm in ("F1", "tw", "F2", "twc", "G1"):
        kb.dma("sp", getattr(t, nm)[:], tabs[nm], writes=[t.d])
    return t


class FBuf:
    pass


def fft_alloc(kb, st, cfg, nsets=1):
    CB, nq, N1, N2, rows = cfg.CB, cfg.nq, cfg.NF, cfg.N2, cfg.rows
    E = CB * nq * N1
    E2 = CB * N2
    tn = max(E, E2)
    Ab = kb.sb(st, [86, CB * nq, 2, N1], BF16, "Ab")
    Abd = Dep()
    Xs = kb.sb(st, [86, CB * nq, 2, N1], F32, "Xs")
    Xsd = Dep()
    t = [kb.sb(st, [128, tn], F32, "ft") for _ in range(4)]
    td = [Dep() for _ in range(4)]
    sets = []
    for _ in range(nsets):
        b = FBuf()
        b.src_f = kb.sb(st, [rows, CB, N2], F32, "srcf")
        b.src_fd = Dep()
        b.src_b = kb.sb(st, [rows, CB, N2], BF16, "srcb")
        b.src_bd = Dep()
        b.As = kb.sb(st, [86, CB * nq, 2, N1], F32, "As")
        b.Asd = Dep()
        b.Ab, b.Abd, b.Xs, b.Xsd, b.t, b.td = Ab, Abd, Xs, Xsd, t, td
        sets.append(b)
    return sets if nsets > 1 else sets[0]


import os
CMUL_ENG = os.environ.get("CMUL_ENG", "dve,dve,dve,dve,dve,dve").split(",")


def cmul_batched(kb, cfg, b, P, shape, Are, Aim, Br, Bi, out_re, out_im, rdeps, wdep, conj=False):
    n = int(np.prod(shape))
    pat = {2: "p (a b) -> p a b", 3: "p (a b c) -> p a b c"}[len(shape)]
    kw = dict(zip("abc", shape))
    kw.pop("a")
    tv = [b.t[i][:P, :n].rearrange(pat, **kw) for i in range(4)]
    e = CMUL_ENG
    tt(kb, e[0], tv[0], Are, Br, ALU.mult, rdeps, [b.td[0]])
    tt(kb, e[1], tv[1], Aim, Bi, ALU.mult, rdeps, [b.td[1]])
    tt(kb, e[2], tv[2], Are, Bi, ALU.mult, rdeps, [b.td[2]])
    tt(kb, e[3], tv[3], Aim, Br, ALU.mult, rdeps, [b.td[3]])
    tt(kb, e[4], out_re, tv[0], tv[1], ALU.subtract, [b.td[0], b.td[1]], [wdep])
    tt(kb, e[5], out_im, tv[2], tv[3], ALU.add, [b.td[2], b.td[3]], [wdep])


def fft_s1(kb, g, cfg, tb, b, cb):
    nq, N1, N2, rows = cfg.nq, cfg.NF, cfg.N2, cfg.rows
    per = 512 // (2 * N1)
    tot = cb * nq
    for i0 in range(0, tot, per):
        cnt = min(per, tot - i0)
        bk = nextbank(g)
        for i in range(i0, i0 + cnt):
            c, q = divmod(i, nq)
            mm(kb, g.psum[bk][:86, (i - i0) * 2 * N1:(i - i0 + 1) * 2 * N1], b.src_b[:rows, c, q * 86:(q + 1) * 86], tb.F1[:rows, :], True, True,
               [b.src_bd, tb.d], [g.pd[bk]])
        copy_op(kb, "act", b.As[:, i0:i0 + cnt, :, :], g.psum[bk][:86, :cnt * 2 * N1].rearrange("p (i r k) -> p i r k", r=2, k=N1), [g.pd[bk]], [b.Asd])


def fft_s2(kb, g, cfg, tb, b, cb):
    nq, N1, N2, rows = cfg.nq, cfg.NF, cfg.N2, cfg.rows
    per = 512 // (2 * N1)
    tot = cb * nq
    Av = b.As[:, :tot, :, :].rearrange("p (c q) r k -> p c q r k", q=nq)
    Abv = b.Ab[:, :tot, :, :].rearrange("p (c q) r k -> p c q r k", q=nq)
    twr = tb.tw[:, :, 0, :].unsqueeze(1).broadcast_to([86, cb, nq, N1])
    twi = tb.tw[:, :, 1, :].unsqueeze(1).broadcast_to([86, cb, nq, N1])
    cmul_batched(kb, cfg, b, 86, (cb, nq, N1), Av[:, :, :, 0, :], Av[:, :, :, 1, :], twr, twi, Abv[:, :, :, 0, :], Abv[:, :, :, 1, :],
                 [b.Asd, tb.d], b.Abd)
    for i0 in range(0, tot, per):
        cnt = min(per, tot - i0)
        bk = nextbank(g)
        for i in range(i0, i0 + cnt):
            c, p = divmod(i, nq)
            reg = g.psum[bk][:86, (i - i0) * 2 * N1:(i - i0 + 1) * 2 * N1]
            for q in range(nq):
                blk = slice(p * 86, (p + 1) * 86)
                mm(kb, reg, tb.F2[:, 0, q, blk], b.Ab[:, c * nq + q, :, :].rearrange("p r k -> p (r k)"), q == 0, False, [tb.d, b.Abd], [g.pd[bk]])
                mm(kb, reg[:, 0:N1], tb.F2[:, 2, q, blk], b.Ab[:, c * nq + q, 1, :], False, False, [tb.d, b.Abd], [g.pd[bk]])
                mm(kb, reg[:, N1:2 * N1], tb.F2[:, 1, q, blk], b.Ab[:, c * nq + q, 0, :], False, q == nq - 1, [tb.d, b.Abd], [g.pd[bk]])
        copy_op(kb, "act", b.Xs[:, i0:i0 + cnt, :, :], g.psum[bk][:86, :cnt * 2 * N1].rearrange("p (i r k) -> p i r k", r=2, k=N1), [g.pd[bk]], [b.Xsd])


def fft_fwd(kb, g, cfg, tb, b, cb):
    fft_s1(kb, g, cfg, tb, b, cb)
    fft_s2(kb, g, cfg, tb, b, cb)


def pipeline2(items, stage_a, stage_b, depth=2):
    if depth < 2:
        for it in items:
            stage_a(it)
            stage_b(it)
        return
    prev = None
    for it in items:
        stage_a(it)
        if prev is not None:
            stage_b(prev)
        prev = it
    if prev is not None:
        stage_b(prev)


def fft_layout_dma(kb, q, cfg, tile, tiled, dram2d, c0, cb, to_sbuf):
    L, N2, rows = cfg.L, cfg.N2, cfg.rows
    full = L // N2
    rem = L - full * N2
    dv = dram2d[c0:c0 + cb, 0:full * N2].rearrange("c (a b) -> a c b", b=N2)
    if to_sbuf:
        kb.dma(q, tile[:full, :cb, :], dv, writes=[tiled])
        if rem:
            kb.dma(q, tile[full:full + 1, :cb, :rem], dram2d[c0:c0 + cb, full * N2:L].unsqueeze(0), writes=[tiled])
    else:
        kb.dma(q, dv, tile[:full, :cb, :], reads=[tiled])
        if rem:
            kb.dma(q, dram2d[c0:c0 + cb, full * N2:L].unsqueeze(0), tile[full:full + 1, :cb, :rem], reads=[tiled])


def phase_hy_conv(kb, g, cfg, tabs, taps, Hs, seqs, dskip):
    CB, nq, N1, N2, rows, L = cfg.CB, cfg.nq, cfg.NF, cfg.N2, cfg.rows, cfg.L
    with ExitStack() as st:
        tb = fft_load_tables(kb, st, cfg, tabs)
        bs = [fft_alloc(kb, st, cfg, nsets=1)]
        for b in bs:
            kb.op("pool", lambda e, b=b: e.memset(b.src_f[:, :, :], 0.0), [], [b.src_fd])
        X0 = kb.sb(st, [86, CB * nq, 2, N1], F32, "X0")
        X0d = Dep()
        Hb = [kb.sb(st, [86, CB * nq, 2, N1], F32, "Hb") for _ in range(len(bs))]
        Hbd = [Dep() for _ in range(len(bs))]
        items = [(c0, d, bs[i % len(bs)]) for i, (c0, d) in enumerate((c0, d) for c0 in range(0, 64, CB) for d in range(2))]

        def sp_a(it):
            c0, d, b = it
            fft_layout_dma(kb, "sp", cfg, b.src_f, b.src_fd, taps[d], c0, CB, True)
            copy_op(kb, "dve", b.src_b[:, :, :], b.src_f[:, :, :], [b.src_fd], [b.src_bd])
            fft_s1(kb, g, cfg, tb, b, CB)

        def sp_b(it):
            c0, d, b = it
            fft_s2(kb, g, cfg, tb, b, CB)
            if d == 0:
                copy_op(kb, "act", X0[:, :, :, :], b.Xs[:, :, :, :], [b.Xsd], [X0d])
            else:
                tt(kb, "dve", Hb[0][:, :, 0, :], X0[:, :, 0, :], b.Xs[:, :, 0, :], ALU.add, [X0d, b.Xsd], [Hbd[0]])
                tt(kb, "dve", Hb[0][:, :, 1, :], X0[:, :, 1, :], b.Xs[:, :, 1, :], ALU.subtract, [X0d, b.Xsd], [Hbd[0]])
                kb.dma("pool", Hs[:, c0 * nq:(c0 + CB) * nq, :, :], Hb[0][:, :, :, :], reads=[Hbd[0]])
        pipeline2(items, sp_a, sp_b, depth=len(bs))
        kb.barrier()
        Yb = kb.sb(st, [86, CB * nq, 2, N1], BF16, "Yb")
        Ybd = Dep()
        Bs = kb.sb(st, [N1, CB, 2, N2], F32, "Bs")
        Bsd = Dep()
        Bb = kb.sb(st, [N1, CB, 2, N2], BF16, "Bb")
        Bbd = Dep()
        x0f = [kb.sb(st, [rows, CB, N2], F32, "x0f") for _ in range(len(bs))]
        x0d = [Dep() for _ in range(len(bs))]
        cv = kb.sb(st, [rows, CB, N2], F32, "cv")
        cvd = Dep()
        yo = kb.sb(st, [rows, CB, N2], BF16, "yo")
        yod = Dep()
        dsk = kb.sb(st, [128, 64], F32, "dsk")
        dskd = Dep()
        kb.dma("sp", dsk[:, :], dskip.broadcast_to([128, 64]), writes=[dskd])
        perb = 512 // N2
        citems = [(sq, c0, i % len(bs)) for i, (sq, c0) in enumerate((sq, c0) for sq in seqs for c0 in range(0, 64, CB))]

        def cv_a(it):
            sq, c0, k = it
            b = bs[k]
            fft_layout_dma(kb, "sp", cfg, b.src_f, b.src_fd, sq["z"], c0, CB, True)
            fft_layout_dma(kb, "sp", cfg, x0f[k], x0d[k], sq["x0"], c0, CB, True)
            kb.dma("sp", Hb[k][:, :, :, :], Hs[:, c0 * nq:(c0 + CB) * nq, :, :], writes=[Hbd[k]])
            copy_op(kb, "dve", b.src_b[:, :, :], b.src_f[:, :, :], [b.src_fd], [b.src_bd])
            fft_s1(kb, g, cfg, tb, b, CB)

        def cv_b(it):
            sq, c0, k = it
            b = bs[k]
            fft_s2(kb, g, cfg, tb, b, CB)
            cmul_batched(kb, cfg, b, 86, (CB * nq, N1), b.Xs[:, :, 0, :], b.Xs[:, :, 1, :], Hb[k][:, :, 0, :], Hb[k][:, :, 1, :],
                         Yb[:, :, 0, :], Yb[:, :, 1, :], [b.Xsd, Hbd[k]], Ybd)
            tot = CB * 2
            for i0 in range(0, tot, perb):
                cnt = min(perb, tot - i0)
                bk = nextbank(g)
                for i in range(i0, i0 + cnt):
                    c, ri = divmod(i, 2)
                    reg = g.psum[bk][:N1, (i - i0) * N2:(i - i0 + 1) * N2]
                    for p in range(nq):
                        ya_re, ya_im = Yb[:, c * nq + p, 0, :], Yb[:, c * nq + p, 1, :]
                        if ri == 0:
                            mm(kb, reg, ya_re, tb.F2[:, 0, p, :], p == 0, False, [Ybd, tb.d], [g.pd[bk]])
                            mm(kb, reg, ya_im, tb.F2[:, 1, p, :], False, p == nq - 1, [Ybd, tb.d], [g.pd[bk]])
                        else:
                            mm(kb, reg, ya_re, tb.F2[:, 2, p, :], p == 0, False, [Ybd, tb.d], [g.pd[bk]])
                            mm(kb, reg, ya_im, tb.F2[:, 0, p, :], False, p == nq - 1, [Ybd, tb.d], [g.pd[bk]])
                copy_op(kb, "act", Bs[:, :, :, :].rearrange("p c r n -> p (c r) n")[:, i0:i0 + cnt, :],
                        g.psum[bk][:N1, :cnt * N2].rearrange("p (i n) -> p i n", n=N2), [g.pd[bk]], [Bsd])
            twr = tb.twc[:, 0, :].unsqueeze(1).broadcast_to([N1, CB, N2])
            twi = tb.twc[:, 1, :].unsqueeze(1).broadcast_to([N1, CB, N2])
            cmul_batched(kb, cfg, b, N1, (CB, N2), Bs[:, :, 0, :], Bs[:, :, 1, :], twr, twi, Bb[:, :, 0, :], Bb[:, :, 1, :], [Bsd, tb.d], Bbd)
            for i0 in range(0, CB, perb):
                cnt = min(perb, CB - i0)
                bk = nextbank(g)
                for c in range(i0, i0 + cnt):
                    reg = g.psum[bk][:rows, (c - i0) * N2:(c - i0 + 1) * N2]
                    mm(kb, reg, tb.G1[:, 0, :], Bb[:, c, 0, :], True, False, [tb.d, Bbd], [g.pd[bk]])
                    mm(kb, reg, tb.G1[:, 1, :], Bb[:, c, 1, :], False, True, [tb.d, Bbd], [g.pd[bk]])
                copy_op(kb, "act", cv[:, i0:i0 + cnt, :], g.psum[bk][:rows, :cnt * N2].rearrange("p (i n) -> p i n", n=N2), [g.pd[bk]], [cvd])
            tt(kb, "dve", b.src_f[:, :, :], b.src_f[:, :, :], dsk[:rows, c0:c0 + CB].unsqueeze(2).broadcast_to([rows, CB, N2]), ALU.mult,
               [b.src_fd, dskd], [b.src_fd])
            tt(kb, "dve", cv[:, :, :], cv[:, :, :], b.src_f[:, :, :], ALU.add, [cvd, b.src_fd], [cvd])
            tt(kb, "dve", yo[:, :, :], cv[:, :, :], x0f[k][:, :, :], ALU.mult, [cvd, x0d[k]], [yod])
            fft_layout_dma(kb, "pool", cfg, yo, yod, sq["ya"], c0, CB, False)
        pipeline2(citems, cv_a, cv_b, depth=len(bs))
        kb.barrier()


def phase_hy_inproj(kb, g, seqs, w_hy, brow, hcols, G=1):
    with ExitStack() as st:
        wb = kb.sb(st, [128, 8, G * 192], BF16, "why")
        Wd = Dep()
        load_weight_bf16(kb, st, wb, Wd, w_hy, 8, G * 192, stage_cols=1536)
        hc = kb.sb(st, [64, G * 12], F32, "hc")
        hcd = Dep()
        kb.dma("sp", hc[:, :], hcols, writes=[hcd])
        brf = kb.sb(st, [1, G * 192], F32, "brf")
        brb = kb.sb(st, [1, G * 192], BF16, "brb")
        brd = Dep()
        kb.dma("sp", brf[:, :], brow, writes=[brd])
        copy_op(kb, "dve", brb[:, :], brf[:, :], [brd], [brd])
        xin = [kb.sb(st, [128, D], F32, "xin") for _ in range(4)]
        xind = [Dep() for _ in range(4)]
        xT = [kb.sb(st, [128, 8, 512], BF16, "xT") for _ in range(2)]
        xTd = [Dep() for _ in range(2)]
        vf = [kb.sb(st, [1, 512], F32, "vf") for _ in range(2)]
        vb = [kb.sb(st, [1, 512], BF16, "vb") for _ in range(2)]
        vd = [Dep() for _ in range(2)]
        o3 = [[kb.sb(st, [64, 512], F32, "o3") for _ in range(3)] for _ in range(2)]
        o3d = [[Dep() for _ in range(3)] for _ in range(2)]
        bi = 0
        oi = 0
        for sq in seqs:
            L = sq["L"]
            for t0 in range(0, L, 510):
                no = min(510, L - t0)
                ni = no + 2
                j = bi % 2
                bi += 1
                for ti, r0 in enumerate(range(0, ni, 128)):
                    n = min(128, ni - r0)
                    kb.dma("sp", xin[ti][:n, :], sq["xh"][t0 + r0:t0 + r0 + n, :], writes=[xind[ti]])
                    transpose_tile(kb, g, xin[ti], xind[ti], n, xT[j], xTd[j], r0)
                kb.dma("sp", vf[j][:, :ni], sq["valid"][:, t0:t0 + ni], writes=[vd[j]])
                copy_op(kb, "dve", vb[j][:, :ni], vf[j][:, :ni], [vd[j]], [vd[j]])
                for gg in range(G):
                    oj = oi % 2
                    oi += 1
                    for gi in range(3):
                        c0 = gg * 192 + gi * 64
                        h0 = gg * 12 + gi * 4
                        bk = nextbank(g)
                        for k in range(8):
                            mm(kb, g.psum[bk][:64, :ni], wb[:, k, c0:c0 + 64], xT[j][:, k, :ni], k == 0, False, [Wd, xTd[j]], [g.pd[bk]])
                        mm(kb, g.psum[bk][:64, :ni], brb[:, c0:c0 + 64], vb[j][:, :ni], False, True, [brd, vd[j]], [g.pd[bk]])
                        o = o3[oj][gi]
                        od = o3d[oj][gi]
                        act(kb, o[:, :no], g.psum[bk][:64, 1:1 + no], AF.Identity, [g.pd[bk], hcd], [od],
                            scale=hc[:, h0 + 1:h0 + 2], bias=hc[:, h0 + 3:h0 + 4])
                        stt(kb, "dve", o[:, :no], g.psum[bk][:64, 0:no], hc[:, h0:h0 + 1], o[:, :no], ALU.mult, ALU.add, [g.pd[bk], hcd, od], [od])
                        stt(kb, "dve", o[:, :no], g.psum[bk][:64, 2:2 + no], hc[:, h0 + 2:h0 + 3], o[:, :no], ALU.mult, ALU.add, [g.pd[bk], hcd, od], [od])
                    tt(kb, "pool", o3[oj][1][:, :no], o3[oj][1][:, :no], o3[oj][2][:, :no], ALU.mult, [o3d[oj][1], o3d[oj][2]], [o3d[oj][1]])
                    kb.dma("pool", sq["x0"][gg][:, t0:t0 + no], o3[oj][0][:, :no], reads=[o3d[oj][0]])
                    kb.dma("pool", sq["z"][gg][:, t0:t0 + no], o3[oj][1][:, :no], reads=[o3d[oj][1]])
        kb.barrier()


def sin_reduced(kb, out, outd, src_ps, fcol, fbcol, tmps, tmpd, ki, kid, n, reads):
    a, r = tmps
    ts(kb, "dve", a[:, :n], src_ps, fcol, fbcol, ALU.mult, ALU.add, reads, [tmpd[0]])
    ts(kb, "pool", r[:, :n], a[:, :n], 1.0 / TWO_PI, None, ALU.mult, None, [tmpd[0]], [tmpd[1]])
    copy_op(kb, "dve", ki[:, :n], r[:, :n], [tmpd[1]], [kid])
    copy_op(kb, "pool", r[:, :n], ki[:, :n], [kid], [tmpd[1]])
    stt(kb, "dve", r[:, :n], r[:, :n], -TWO_PI, a[:, :n], ALU.mult, ALU.add, [tmpd[0], tmpd[1]], [tmpd[1]])
    ts(kb, "pool", r[:, :n], r[:, :n], -3.1415925, 3.1415925, ALU.max, ALU.min, [tmpd[1]], [tmpd[1]])
    return act(kb, out, r[:, :n], AF.Sin, [tmpd[1]], [outd])


def phase_hy_filters(kb, g, L, zposT, fw, taps_out):
    with ExitStack() as st:
        w1 = kb.sb(st, [33, 2, 64], F32, "fw1")
        w2 = kb.sb(st, [64, 2, 64], F32, "fw2")
        w3 = kb.sb(st, [64, 2, 64], F32, "fw3")
        fc = kb.sb(st, [64, 2, 8], F32, "fc")
        Wd = Dep()
        kb.dma("sp", w1[:, :, :], fw["w1"].rearrange("d e f -> e d f"), writes=[Wd])
        kb.dma("sp", w2[:, :, :], fw["w2"].rearrange("d e f -> e d f"), writes=[Wd])
        kb.dma("sp", w3[:, :, :], fw["w3"].rearrange("d e f -> e d f"), writes=[Wd])
        kb.dma("sp", fc[:, :, 0:5], fw["fcols"], writes=[Wd])
        tt(kb, "dve", fc[:, :, 5:6], fc[:, :, 0:1], fc[:, :, 1:2], ALU.mult, [Wd], [Wd])
        tt(kb, "dve", fc[:, :, 6:7], fc[:, :, 2:3], fc[:, :, 3:4], ALU.mult, [Wd], [Wd])
        ts(kb, "dve", fc[:, :, 7:8], fc[:, :, 4:5], -1.0, None, ALU.mult, None, [Wd], [Wd])
        taps = kb.sb(st, [64, 2, L], F32, "taps")
        tapsd = Dep()
        zp = [kb.sb(st, [33, 512], F32, "zp") for _ in range(2)]
        zpd = [Dep() for _ in range(2)]
        tb_ = [kb.sb(st, [64, 512], F32, "tbc") for _ in range(2)]
        tbd = [Dep() for _ in range(2)]
        tmps = [kb.sb(st, [64, 512], F32, "ftmp") for _ in range(2)]
        tmpd = [Dep(), Dep()]
        ki = kb.sb(st, [64, 512], I32, "ki")
        kid = Dep()
        h1 = kb.sb(st, [64, 512], F32, "h1")
        h1d = Dep()
        h2 = kb.sb(st, [64, 512], F32, "h2")
        h2d = Dep()
        ex = kb.sb(st, [64, 512], F32, "ex")
        exd = Dep()
        ss = kb.sb(st, [64, 2 * ((L + 511) // 512) + 4], F32, "ss")
        ssd = Dep()
        junk = kb.sb(st, [64, 512], F32, "fjunk")
        junkd = Dep()
        nb = (L + 511) // 512
        for bi, l0 in enumerate(range(0, L, 512)):
            n = min(512, L - l0)
            j = bi % 2
            kb.dma("sp", zp[j][:, :n], zposT[:, l0:l0 + n], writes=[zpd[j]])
            kb.dma("pool", tb_[j][:, :n], zposT[0:1, l0:l0 + n].broadcast_to([64, n]), writes=[tbd[j]])
            for d in range(2):
                bk = nextbank(g)
                mm(kb, g.psum[bk][:64, :n], w1[:, d, :], zp[j][:, :n], True, True, [Wd, zpd[j]], [g.pd[bk]])
                sin_reduced(kb, h1[:, :n], h1d, g.psum[bk][:64, :n], fc[:, d, 0:1], fc[:, d, 5:6], tmps, tmpd, ki, kid, n, [g.pd[bk], Wd])
                bk = nextbank(g)
                mm(kb, g.psum[bk][:64, :n], w2[:, d, :], h1[:, :n], True, True, [Wd, h1d], [g.pd[bk]])
                sin_reduced(kb, h2[:, :n], h2d, g.psum[bk][:64, :n], fc[:, d, 2:3], fc[:, d, 6:7], tmps, tmpd, ki, kid, n, [g.pd[bk], Wd])
                bk = nextbank(g)
                mm(kb, g.psum[bk][:64, :n], w3[:, d, :], h2[:, :n], True, True, [Wd, h2d], [g.pd[bk]])
                act(kb, ex[:, :n], tb_[j][:, :n], AF.Exp, [tbd[j], Wd], [exd], scale=fc[:, d, 7:8])
                tt(kb, "dve", taps[:, d, l0:l0 + n], g.psum[bk][:64, :n], ex[:, :n], ALU.mult, [g.pd[bk], exd], [tapsd])
                if d == 1 and l0 == 0:
                    kb.op("pool", lambda e: e.memset(taps[:, 1, 0:1], 0.0), [], [tapsd])
                act(kb, junk[:, :n], taps[:, d, l0:l0 + n], AF.Square, [tapsd], [junkd, ssd], accum_out=ss[:, 2 * bi + d:2 * bi + d + 1])
        tot, nrm = ss[:, 2 * nb:2 * nb + 1], ss[:, 2 * nb + 1:2 * nb + 2]
        kb.op("dve", lambda e: e.tensor_reduce(out=tot, in_=ss[:, 0:2 * nb], axis=AX.X, op=ALU.add), [ssd], [ssd])
        act(kb, nrm, tot, AF.Sqrt, [ssd], [ssd])
        kb.op("dve", lambda e: e.reciprocal(out=nrm, in_=nrm), [ssd], [ssd])
        for d in range(2):
            for l0 in range(0, L, 4096):
                n = min(4096, L - l0)
                ts(kb, ("dve", "pool")[d], taps[:, d, l0:l0 + n], taps[:, d, l0:l0 + n], nrm, None, ALU.mult, None, [tapsd, ssd], [tapsd])
            kb.dma("sp", taps_out[d], taps[:, d, :], reads=[tapsd])
        kb.barrier()


def phase_hy_filter_h2(kb, g, L, zposT, fw, h2_out):
    with ExitStack() as st:
        w1 = kb.sb(st, [33, 2, 64], F32, "fw1")
        w2 = kb.sb(st, [64, 2, 64], F32, "fw2")
        fc = kb.sb(st, [64, 2, 8], F32, "fc")
        Wd = Dep()
        kb.dma("sp", w1[:, :, :], fw["w1"].rearrange("d e f -> e d f"), writes=[Wd])
        kb.dma("sp", w2[:, :, :], fw["w2"].rearrange("d e f -> e d f"), writes=[Wd])
        kb.dma("sp", fc[:, :, 0:5], fw["fcols"], writes=[Wd])
        tt(kb, "dve", fc[:, :, 5:6], fc[:, :, 0:1], fc[:, :, 1:2], ALU.mult, [Wd], [Wd])
        tt(kb, "dve", fc[:, :, 6:7], fc[:, :, 2:3], fc[:, :, 3:4], ALU.mult, [Wd], [Wd])
        zp = [kb.sb(st, [33, 512], F32, "zp") for _ in range(2)]
        zpd = [Dep() for _ in range(2)]
        NQ = 3
        tmps = [[kb.sb(st, [64, 512], F32, "ftmp") for _ in range(2)] for _ in range(NQ)]
        tmpd = [[Dep(), Dep()] for _ in range(NQ)]
        ki = [kb.sb(st, [64, 512], I32, "ki") for _ in range(NQ)]
        kid = [Dep() for _ in range(NQ)]
        h1 = [kb.sb(st, [64, 512], F32, "h1") for _ in range(NQ)]
        h1d = [Dep() for _ in range(NQ)]
        h2 = [kb.sb(st, [64, 512], F32, "h2") for _ in range(NQ)]
        h2d = [Dep() for _ in range(NQ)]
        it = 0
        for bi, l0 in enumerate(range(0, L, 512)):
            n = min(512, L - l0)
            j = bi % 2
            kb.dma("sp", zp[j][:, :n], zposT[:, l0:l0 + n], writes=[zpd[j]])
            for d in range(2):
                q = it % NQ
                it += 1
                bk = nextbank(g)
                mm(kb, g.psum[bk][:64, :n], w1[:, d, :], zp[j][:, :n], True, True, [Wd, zpd[j]], [g.pd[bk]])
                sin_reduced(kb, h1[q][:, :n], h1d[q], g.psum[bk][:64, :n], fc[:, d, 0:1], fc[:, d, 5:6], tmps[q], tmpd[q], ki[q], kid[q], n, [g.pd[bk], Wd])
                bk = nextbank(g)
                mm(kb, g.psum[bk][:64, :n], w2[:, d, :], h1[q][:, :n], True, True, [Wd, h1d[q]], [g.pd[bk]])
                sin_reduced(kb, h2[q][:, :n], h2d[q], g.psum[bk][:64, :n], fc[:, d, 2:3], fc[:, d, 6:7], tmps[q], tmpd[q], ki[q], kid[q], n, [g.pd[bk], Wd])
                kb.dma("pool", h2_out[d][:, l0:l0 + n], h2[q][:, :n], reads=[h2d[q]])
        kb.barrier()


def phase_hy_filter_taps(kb, g, L, zposT, h2_in, w3_ap, fcols_ap, taps_out):
    with ExitStack() as st:
        w3 = kb.sb(st, [64, 2, 64], F32, "fw3")
        fc = kb.sb(st, [64, 2, 8], F32, "fc")
        Wd = Dep()
        kb.dma("sp", w3[:, :, :], w3_ap.rearrange("d e f -> e d f"), writes=[Wd])
        kb.dma("sp", fc[:, :, 0:5], fcols_ap, writes=[Wd])
        ts(kb, "dve", fc[:, :, 7:8], fc[:, :, 4:5], -1.0, None, ALU.mult, None, [Wd], [Wd])
        taps = kb.sb(st, [64, 2, L], F32, "taps")
        tapsd = [Dep(), Dep()]
        NQ = 3
        tb_ = [kb.sb(st, [64, 512], F32, "tbc") for _ in range(2)]
        tbd = [Dep() for _ in range(2)]
        hin = [kb.sb(st, [64, 512], F32, "h2in") for _ in range(NQ)]
        hind = [Dep() for _ in range(NQ)]
        ex = [kb.sb(st, [64, 512], F32, "ex") for _ in range(NQ)]
        exd = [Dep() for _ in range(NQ)]
        junk = [kb.sb(st, [64, 512], F32, "fjunk") for _ in range(2)]
        junkd = [Dep(), Dep()]
        nb = (L + 511) // 512
        ss = kb.sb(st, [64, 2 * nb + 4], F32, "ss")
        ssd = Dep()
        it = 0
        for bi, l0 in enumerate(range(0, L, 512)):
            n = min(512, L - l0)
            j = bi % 2
            kb.dma("pool", tb_[j][:, :n], zposT[0:1, l0:l0 + n].broadcast_to([64, n]), writes=[tbd[j]])
            for d in range(2):
                q = it % NQ
                it += 1
                kb.dma("sp", hin[q][:, :n], h2_in[d][:, l0:l0 + n], writes=[hind[q]])
                bk = nextbank(g)
                mm(kb, g.psum[bk][:64, :n], w3[:, d, :], hin[q][:, :n], True, True, [Wd, hind[q]], [g.pd[bk]])
                act(kb, ex[q][:, :n], tb_[j][:, :n], AF.Exp, [tbd[j], Wd], [exd[q]], scale=fc[:, d, 7:8])
                tt(kb, "dve", taps[:, d, l0:l0 + n], g.psum[bk][:64, :n], ex[q][:, :n], ALU.mult, [g.pd[bk], exd[q]], [tapsd[d]])
                if d == 1 and l0 == 0:
                    kb.op("pool", lambda e: e.memset(taps[:, 1, 0:1], 0.0), [], [tapsd[d]])
                act(kb, junk[d][:, :n], taps[:, d, l0:l0 + n], AF.Square, [tapsd[d]], [junkd[d], ssd], accum_out=ss[:, 2 * bi + d:2 * bi + d + 1])
        tot, nrm = ss[:, 2 * nb:2 * nb + 1], ss[:, 2 * nb + 1:2 * nb + 2]
        kb.op("dve", lambda e: e.tensor_reduce(out=tot, in_=ss[:, 0:2 * nb], axis=AX.X, op=ALU.add), [ssd], [ssd])
        act(kb, nrm, tot, AF.Sqrt, [ssd], [ssd])
        kb.op("dve", lambda e: e.reciprocal(out=nrm, in_=nrm), [ssd], [ssd])
        for d in range(2):
            for l0 in range(0, L, 4096):
                n = min(4096, L - l0)
                ts(kb, "dve", taps[:, d, l0:l0 + n], taps[:, d, l0:l0 + n], nrm, None, ALU.mult, None, [tapsd[d], ssd], [tapsd[d]])
            kb.dma(("sp", "pool")[d], taps_out[d], taps[:, d, :], reads=[tapsd[d]])
        kb.barrier()


LP, LS = 16400, 2064
NCORES = 8
BF = ml_dtypes.bfloat16


class Prog:
    def __init__(self):
        self.nc = bass.Bass("TRN2", target_bir_lowering=False)
        self.kb = KB(self.nc)
        self.ins = {}

    def din(self, name, shape, dt=F32):
        self.ins[name] = (tuple(shape), dt)
        return self.nc.dram_tensor(name, list(shape), dt, kind="ExternalInput").ap()

    def dout(self, name, shape, dt=F32):
        return self.nc.dram_tensor(name, list(shape), dt, kind="ExternalOutput").ap()

    def scr(self, name, shape, dt=F32):
        return self.nc.dram_tensor(name, list(shape), dt).ap()


def chunk_tiles():
    return [(t0, 128, 61 + t0) for t0 in range(0, 2048, 128)] + [(2048, 16, 15)]


def declare_tabs(P, cfg, pre):
    t = fft_tables(cfg)
    return {k: P.din(pre + k, v.shape, F32 if v.dtype == np.float32 else BF16) for k, v in t.items()}, {pre + k: v for k, v in t.items()}


def build_l1():
    P = Prog()
    kb = P.kb
    ident = P.din("ident", [128, 128])
    xh_p = P.din("xh_p", [LP + 2, D])
    xh_s = P.din("xh_s", [LS + 2, D])
    valid_p = P.din("valid_p", [1, LP + 2])
    valid_s = P.din("valid_s", [1, LS + 2])
    zpos_p = P.din("zpos_p", [33, LP])
    zpos_s = P.din("zpos_s", [33, LS])
    tabsP, _ = declare_tabs(P, CFG_P, "tp_")
    tabsS, _ = declare_tabs(P, CFG_S, "ts_")
    fw1 = P.din("fw1", [2, 33, 64])
    fw2 = P.din("fw2", [2, 64, 64])
    fw3 = P.din("fw3", [9, 2, 64, 64])
    fcols = P.din("fcols", [9, 64, 2, 5])
    why = P.din("why", [9, D, 192])
    brow = P.din("brow", [9, 1, 192])
    hcols = P.din("hcols", [9, 64, 12])
    dskip = P.din("dskip", [9, 1, 64])
    xc = P.din("xc", [2, XC, D])
    mask = P.din("mask", [2, 1, XC])
    wconf = P.din("wconf", [D, 1024])
    ccols = P.din("ccols", [128, 144])
    yaP = P.dout("yaP", [64, LP], BF16)
    yaS = P.dout("yaS", [8, 64, LS], BF16)
    ybT = P.dout("ybT", [2, 512, LS], BF16)
    taps_p = P.scr("taps_p", [2, 64, LP])
    Hs_p = P.scr("Hs_p", [86, 64 * CFG_P.nq, 2, CFG_P.N1])
    z_p = P.scr("z_p", [64, LP])
    x0_p = P.scr("x0_p", [64, LP])
    taps_s = P.scr("taps_s", [8, 2, 64, LS])
    Hs_s = P.scr("Hs_s", [8, 86, 64 * CFG_S.nq, 2, CFG_S.N1])
    z_s = P.scr("z_s", [8, 64, LS])
    x0_s = P.scr("x0_s", [8, 64, LS])
    with ExitStack() as st:
        g = setup_globals(kb, st)
        load_ident(kb, g, ident)
        fwd = lambda i: dict(w1=fw1, w2=fw2, w3=fw3[i], fcols=fcols[i])
        phase_hy_filters(kb, g, LP, zpos_p, fwd(0), taps_p)
        phase_hy_inproj(kb, g, [dict(xh=xh_p, valid=valid_p, L=LP, z=[z_p], x0=[x0_p])], why[0], brow[0], hcols[0])
        phase_hy_conv(kb, g, CFG_P, tabsP, taps_p, Hs_p, [dict(z=z_p, x0=x0_p, ya=yaP)], dskip[0])
        for gi in range(8):
            phase_hy_filters(kb, g, LS, zpos_s, fwd(1 + gi), taps_s[gi])
            phase_hy_inproj(kb, g, [dict(xh=xh_s, valid=valid_s, L=LS, z=[z_s[gi]], x0=[x0_s[gi]])], why[1 + gi], brow[1 + gi], hcols[1 + gi])
            phase_hy_conv(kb, g, CFG_S, tabsS, taps_s[gi], Hs_s[gi], [dict(z=z_s[gi], x0=x0_s[gi], ya=yaS[gi])], dskip[1 + gi])

        def outf(s_):
            def f(j, bi, cn):
                if bi == 0:
                    return ybT[s_, j * 128:(j + 1) * 128, 2048:2064]
                return ybT[s_, j * 128:(j + 1) * 128, (bi - 1) * 512:bi * 512]
            return f
        phase_conf(kb, g, [dict(x=xc[s_], mask=mask[s_], out=outf(s_)) for s_ in range(2)], wconf, ccols)
        kb.finish_wait()
    return P


def build_l2():
    P = Prog()
    kb = P.kb
    ident = P.din("ident", [128, 128])
    xc = P.din("xc", [2, XC, D])
    ycT = P.din("ycT", [2, D, LS], BF16)
    wout = P.din("wout", [D, D])
    bout = P.din("bout", [1, D])
    ln1g = P.din("ln1g", [1, D]); ln1b = P.din("ln1b", [1, D]); ln2g = P.din("ln2g", [1, D]); ln2b = P.din("ln2b", [1, D])
    w1 = P.din("w1", [D, DFF]); w2 = P.din("w2", [DFF, D])
    wqa = P.din("wqa", [D, 384]); qg = P.din("qg", [1, 384]); WqH = P.din("WqH", [384, NH * 128]); WqS = P.din("WqS", [384, NH * 32])
    wkva = P.din("wkva", [D, 288]); kvg = P.din("kvg", [1, 256])
    cs = P.din("cs", [2, LS, 32]); Cq = P.din("Cq", [2, 32, 2048]); Sq = P.din("Sq", [2, 32, 2048])
    h2 = P.dout("h2", [2, LS, D])
    kvlat = P.dout("kvlat", [2, LS, 288])
    QT = P.dout("QT", [2, NH, 128, 2048], BF16)
    h1 = P.scr("h1", [2, LS, D])
    tl = chunk_tiles()
    with ExitStack() as st:
        g = setup_globals(kb, st)
        load_ident(kb, g, ident)
        ycv = ycT.rearrange("s (k p) t -> s p k t", p=128)
        phase_proj_ln(kb, g, [(xc[s_, xr:xr + n, :], [(slice(0, 8), ycv[s_, :, :, t0:t0 + n], None)], h1[s_, t0:t0 + n, :], n) for s_ in range(2) for t0, n, xr in tl],
                      True, wout, bout, ln1g, ln1b)
        phase_mlp_ln(kb, g, [(h1[s_, t0:t0 + n, :], h2[s_, t0:t0 + n, :], n) for s_ in range(2) for t0, n, xr in tl], w1, w2, ln2g, ln2b)
        seqs = []
        for s_ in range(2):
            seqs.append(dict(tiles=[(h2[s_, t0:t0 + n, :], kvlat[s_, t0:t0 + n, :], cs[s_, t0:t0 + n, :], n) for t0, n, xr in tl],
                             CS=(Cq[s_], Sq[s_]), qt=(lambda s_: (lambda h, q0: QT[s_, h, :, q0:q0 + 512]))(s_)))
        phase_qkv(kb, g, seqs, wqa, qg, WqH, WqS, wkva, kvg)
        kb.finish_wait()
    return P


def build_l3():
    P = Prog()
    kb = P.kb
    ident = P.din("ident", [128, 128])
    h2 = P.din("h2", [2, LS, D])
    kvp = P.din("kvp", [LP, 288])
    kvs = P.din("kvs", [LS, 288])
    QT = P.din("QT", [2, NH, 128, 2048], BF16)
    WkH = P.din("WkH", [256, NH * 128]); WvH = P.din("WvH", [256, NH * 64])
    wo = P.din("wo", [D, D])
    ln1g = P.din("ln1g", [1, D]); ln1b = P.din("ln1b", [1, D]); ln2g = P.din("ln2g", [1, D]); ln2b = P.din("ln2b", [1, D])
    w1 = P.din("w1", [D, DFF]); w2 = P.din("w2", [DFF, D])
    out = P.dout("out", [2, 2048, D])
    otok = P.scr("otok", [2, 2048, D])
    h3 = P.scr("h3", [2, 2048, D])
    with ExitStack() as st:
        g = setup_globals(kb, st)
        load_ident(kb, g, ident)
        seqs = []
        for s_, kv, L in ((0, kvp, LP), (1, kvs, LS)):
            otv = otok[s_].rearrange("(a t p) (h c) -> a p t h c", p=128, t=4, c=64)
            seqs.append(dict(kchunks=[(kv[t0:min(t0 + 128, L), :], min(128, L - t0)) for t0 in range(0, L, 128)],
                             qt=(lambda s_: (lambda h: QT[s_, h, :, :]))(s_),
                             o=(lambda otv: (lambda qsb, half, h: otv[qsb * 2 + half, :, :, h, :]))(otv)))
        phase_attn(kb, g, seqs, WkH, WvH)
        tl2 = [(s_, t0) for s_ in range(2) for t0 in range(0, 2048, 128)]
        phase_proj_ln(kb, g, [(h2[s_, t0:t0 + 128, :], otok[s_, t0:t0 + 128, :], h3[s_, t0:t0 + 128, :], 128) for s_, t0 in tl2], False, wo, None, ln1g, ln1b)
        phase_mlp_ln(kb, g, [(h3[s_, t0:t0 + 128, :], out[s_, t0:t0 + 128, :], 128) for s_, t0 in tl2], w1, w2, ln2g, ln2b)
        kb.finish_wait()
    return P


def zpos_table(L):
    t = np.arange(L, dtype=np.float32) / max(L - 1, 1)
    freqs = np.linspace(1e-4, 15, 16, dtype=np.float32)
    w = (np.float32(2.0 * math.pi) * np.arange(L, dtype=np.float32) / np.float32(L)).astype(np.float32)
    ang = w[:, None] * freqs[None, :]
    return np.ascontiguousarray(np.concatenate([t[:, None], np.cos(ang), -np.sin(ang)], -1).T.astype(np.float32))


def rope_cs(pos):
    inv = (1.0 / (10000.0 ** (np.arange(0, 32, 2, dtype=np.float32) / 32))).astype(np.float32)
    ang = pos.astype(np.float32)[:, None] * inv[None, :]
    return np.cos(ang).astype(np.float32), np.sin(ang).astype(np.float32)


def make_xc(hfull, m0, L):
    x = np.zeros((XC, D), np.float32)
    mk = np.zeros((1, XC), np.float32)
    x[15:46] = hfull[0:31]
    mk[0, 15:46] = 1
    lo, hi = m0 - 15, min(m0 + 2048 + 15, L)
    x[46:46 + (hi - lo)] = hfull[lo:hi]
    mk[0, 46:46 + (hi - lo)] = 1
    return x, mk


def colpack(v):
    return np.ascontiguousarray(v.reshape(4, 128).T)


def check_inputs(P, im):
    for k, (shape, dt) in P.ins.items():
        assert k in im, k
        assert tuple(im[k].shape) == shape, (k, im[k].shape, shape)
    return {k: np.ascontiguousarray(im[k]) for k in P.ins}


def kernel_unfused(x_prompt, x_sample, meta_tokens, ev_w_in, ev_b_in, ev_short_w, ev_short_b,
           hy_w1, hy_b1, hy_freq1, hy_w2, hy_b2, hy_freq2, hy_w3, hy_decay, hy_skip_d,
           cf_dw_w, cf_dw_b, cf_ln_g, cf_ln_b, ev_w_out, ev_b_out,
           mla_wq_a, mla_q_norm, mla_wq_b, mla_wkv_a, mla_kv_norm, mla_wkv_b, mla_wo,
           ln1_g, ln1_b, mlp_w1, mlp_w2, ln2_g, ln2_b):
    f = lambda a: np.asarray(a, dtype=np.float32)
    x_prompt, x_sample, meta = f(x_prompt), f(x_sample), f(meta_tokens)
    win, bin_, sw, sb = f(ev_w_in)[0], f(ev_b_in)[0], f(ev_short_w)[0], f(ev_short_b)[0]
    ident = np.eye(128, dtype=np.float32)
    hp = np.concatenate([meta, x_prompt[0]], 0)
    hs = [np.concatenate([meta, x_sample[c]], 0) for c in range(8)]
    z1 = np.zeros((1, D), np.float32)
    xh_p = np.concatenate([z1, hp, z1], 0)
    valid_p = np.ones((1, LP + 2), np.float32); valid_p[0, 0] = 0; valid_p[0, -1] = 0
    valid_s = np.ones((1, LS + 2), np.float32); valid_s[0, 0] = 0; valid_s[0, -1] = 0
    tabP, tabS = fft_tables(CFG_P), fft_tables(CFG_S)
    def grp(gi):
        ch = slice(gi * 64, gi * 64 + 64)
        gcols = [np.arange(k * 512 + gi * 64, k * 512 + gi * 64 + 64) for k in range(3)]
        allc = np.concatenate(gcols)
        return dict(fw3=np.ascontiguousarray(f(hy_w3)[0][:, :, ch]),
                    fcols=np.ascontiguousarray(np.stack([f(hy_freq1)[0], f(hy_b1)[0], f(hy_freq2)[0], f(hy_b2)[0], f(hy_decay)[0][:, ch]], -1).transpose(1, 0, 2)),
                    why=np.ascontiguousarray(win[:, allc]), brow=bin_[allc][None, :].copy(),
                    hcols=np.concatenate([np.stack([sw[0, gc], sw[1, gc], sw[2, gc], sb[gc]], 1) for gc in gcols], 1).astype(np.float32),
                    dskip=f(hy_skip_d)[0][ch][None, :].copy())
    G = [grp(gi) for gi in range(8)]
    ccols = np.concatenate([colpack(bin_[1536:2048]), colpack(bin_[2048:2560]), colpack(f(cf_dw_b)[0]), colpack(f(cf_ln_g)[0]), colpack(f(cf_ln_b)[0]),
                            np.ascontiguousarray(f(cf_dw_w)[0].T.reshape(4, 128, 31).transpose(1, 0, 2).reshape(128, 124))], 1).astype(np.float32)
    xcs, masks = [], []
    for c in range(8):
        a, ma = make_xc(hp, 16 + 2048 * c, LP)
        b, mb = make_xc(hs[c], 16, LS)
        xcs.append(np.stack([a, b], 0))
        masks.append(np.stack([ma, mb], 0))
    P1 = build_l1()
    ims = []
    for c in range(8):
        order = [c] + list(range(8))
        im = dict(ident=ident, xh_p=xh_p, xh_s=np.concatenate([z1, hs[c], z1], 0), valid_p=valid_p, valid_s=valid_s,
                  zpos_p=zpos_table(LP), zpos_s=zpos_table(LS), fw1=f(hy_w1)[0], fw2=f(hy_w2)[0],
                  fw3=np.stack([G[i]["fw3"] for i in order], 0), fcols=np.stack([G[i]["fcols"] for i in order], 0).astype(np.float32),
                  why=np.stack([G[i]["why"] for i in order], 0), brow=np.stack([G[i]["brow"] for i in order], 0),
                  hcols=np.stack([G[i]["hcols"] for i in order], 0), dskip=np.stack([G[i]["dskip"] for i in order], 0),
                  xc=xcs[c], mask=masks[c], wconf=np.ascontiguousarray(win[:, 1536:2560]), ccols=ccols)
        for k, v in tabP.items():
            im["tp_" + k] = v
        for k, v in tabS.items():
            im["ts_" + k] = v
        ims.append(check_inputs(P1, im))
    r1 = run_bass_kernel_spmd(P1.nc, ims, core_ids=list(range(8))).results
    yaP_all = np.concatenate([np.asarray(r1[c]["yaP"]) for c in range(8)], 0)
    P2 = build_l2()
    wqb = f(mla_wq_b)[0].reshape(384, NH, 96)
    WqH = np.concatenate([wqb[:, :, 64:96], np.zeros((384, NH, 32), np.float32), wqb[:, :, 0:64]], -1).reshape(384, NH * 128)
    WqS = np.concatenate([wqb[:, :, 80:96], wqb[:, :, 64:80]], -1).reshape(384, NH * 32)
    wkvb = f(mla_wkv_b)[0].reshape(256, NH, 128)
    WkH = np.concatenate([np.zeros((256, NH, 64), np.float32), wkvb[:, :, 0:64]], -1).reshape(256, NH * 128)
    WvH = np.ascontiguousarray(wkvb[:, :, 64:128]).reshape(256, NH * 64)
    ims = []
    for c in range(8):
        m0 = 16 + 2048 * c
        ya_p = np.concatenate([yaP_all[:, m0:m0 + 2048], yaP_all[:, 0:16]], 1)
        ya_s = np.asarray(r1[c]["yaS"]).reshape(512, LS)
        ya_s = np.concatenate([ya_s[:, 16:], ya_s[:, 0:16]], 1)
        yb = np.asarray(r1[c]["ybT"])
        ycT = np.stack([np.concatenate([ya_p, yb[0]], 0), np.concatenate([ya_s, yb[1]], 0)], 0)
        css, Cqs, Sqs = [], [], []
        for pos in (np.concatenate([np.arange(m0, m0 + 2048), np.arange(16)]), np.concatenate([np.arange(16, LS), np.arange(16)])):
            co, si = rope_cs(pos)
            css.append(np.concatenate([co, si], 1))
            Cqs.append(np.concatenate([co[:2048].T, co[:2048].T], 0))
            Sqs.append(np.concatenate([-si[:2048].T, si[:2048].T], 0))
        im = dict(ident=ident, xc=xcs[c], ycT=ycT, wout=f(ev_w_out)[0], bout=f(ev_b_out)[0:1], ln1g=f(ln1_g)[0:1], ln1b=f(ln1_b)[0:1],
                  ln2g=f(ln2_g)[0:1], ln2b=f(ln2_b)[0:1], w1=f(mlp_w1)[0], w2=f(mlp_w2)[0], wqa=f(mla_wq_a)[0], qg=f(mla_q_norm)[0:1],
                  WqH=WqH, WqS=WqS, wkva=f(mla_wkv_a)[0], kvg=f(mla_kv_norm)[0:1], cs=np.stack(css, 0), Cq=np.stack(Cqs, 0), Sq=np.stack(Sqs, 0))
        ims.append(check_inputs(P2, im))
    r2 = run_bass_kernel_spmd(P2.nc, ims, core_ids=list(range(8))).results
    kvp = np.concatenate([np.asarray(r2[c]["kvlat"])[0, :2048] for c in range(8)] + [np.asarray(r2[0]["kvlat"])[0, 2048:]], 0)
    P3 = build_l3()
    ims = []
    for c in range(8):
        im = dict(ident=ident, h2=np.asarray(r2[c]["h2"]), kvp=kvp, kvs=np.asarray(r2[c]["kvlat"])[1], QT=np.asarray(r2[c]["QT"]), WkH=WkH, WvH=WvH,
                  wo=f(mla_wo)[0], ln1g=f(ln1_g)[1:2], ln1b=f(ln1_b)[1:2], ln2g=f(ln2_g)[1:2], ln2b=f(ln2_b)[1:2], w1=f(mlp_w1)[1], w2=f(mlp_w2)[1])
        ims.append(check_inputs(P3, im))
    r3 = run_bass_kernel_spmd(P3.nc, ims, core_ids=list(range(8))).results
    y_prompt = np.concatenate([np.asarray(r3[c]["out"])[0] for c in range(8)], 0)[None].astype(np.float32)
    y_sample = np.stack([np.asarray(r3[c]["out"])[1] for c in range(8)], 0).astype(np.float32)
    return (y_prompt, y_sample)


U32 = mybir.dt.uint32
YAW = 18432


def build_fused(stop=10 ** 9, trace_steps=None):
    P = Prog()
    step = [0]

    def run(fn, *a):
        if step[0] < stop:
            fn(*a)
        step[0] += 1

    kb = P.kb
    nc = P.nc
    ident = P.din("ident", [128, 128])
    xh_p = P.din("xh_p", [LP + 2, D]); xh_s = P.din("xh_s", [LS + 2, D])
    valid_p = P.din("valid_p", [1, LP + 2]); valid_s = P.din("valid_s", [1, LS + 2])
    zpos_p = P.din("zpos_p", [33, LP]); zpos_s = P.din("zpos_s", [33, LS])
    tabsP, _ = declare_tabs(P, CFG_P, "tp_")
    tabsS, _ = declare_tabs(P, CFG_S, "ts_")
    fw1 = P.din("fw1", [2, 33, 64]); fw2 = P.din("fw2", [2, 64, 64])
    fw3 = P.din("fw3", [9, 2, 64, 64]); fcols = P.din("fcols", [9, 64, 2, 5])
    why = P.din("why", [9, D, 192]); brow = P.din("brow", [9, 1, 192]); hcols = P.din("hcols", [9, 64, 12]); dskip = P.din("dskip", [9, 1, 64])
    xc = P.din("xc", [2, XC, D]); mask = P.din("mask", [2, 1, XC])
    wconf = P.din("wconf", [D, 1024]); ccols = P.din("ccols", [128, 144])
    gidx = P.din("gidx", [128, 4], U32)
    wout = P.din("wout", [D, D]); bout = P.din("bout", [1, D])
    ln1g = P.din("ln1g", [2, 1, D]); ln1b = P.din("ln1b", [2, 1, D]); ln2g = P.din("ln2g", [2, 1, D]); ln2b = P.din("ln2b", [2, 1, D])
    w1 = P.din("w1", [2, D, DFF]); w2 = P.din("w2", [2, DFF, D])
    wqa = P.din("wqa", [D, 384]); qg = P.din("qg", [1, 384]); WqH = P.din("WqH", [384, NH * 128]); WqS = P.din("WqS", [384, NH * 32])
    wkva = P.din("wkva", [D, 288]); kvg = P.din("kvg", [1, 256])
    cs = P.din("cs", [2, LS, 32]); Cq = P.din("Cq", [2, 32, 2048]); Sq = P.din("Sq", [2, 32, 2048])
    WkH = P.din("WkH", [256, NH * 128]); WvH = P.din("WvH", [256, NH * 64]); wo = P.din("wo", [D, D])
    out = P.dout("out", [2, 2048, D])
    yaP = P.scr("yaP", [64, YAW], BF16)
    yaP_all = P.scr("yaP_all", [512, YAW], BF16)
    yaS = P.scr("yaS", [8, 64, LS], BF16)
    ybT = P.scr("ybT", [2, 512, LS], BF16)
    taps_p = P.scr("taps_p", [2, 64, LP]); Hs_p = P.scr("Hs_p", [86, 64 * CFG_P.nq, 2, CFG_P.N1])
    z_p = P.scr("z_p", [64, LP]); x0_p = P.scr("x0_p", [64, LP])
    taps_s = P.scr("taps_s", [8, 2, 64, LS]); Hs_s = P.scr("Hs_s", [8, 86, 64 * CFG_S.nq, 2, CFG_S.N1])
    z_s = P.scr("z_s", [8, 64, LS]); x0_s = P.scr("x0_s", [8, 64, LS])
    h1 = P.scr("h1", [2, LS, D]); h2 = P.scr("h2", [2, LS, D])
    kvlat = P.scr("kvlat", [2, LS, 288]); kv_all = P.scr("kv_all", [8 * LS, 288])
    QT = P.scr("QT", [2, NH, 128, 2048], BF16)
    otok = P.scr("otok", [2, 2048, D]); h3 = P.scr("h3", [2, 2048, D])
    tl = chunk_tiles()
    with ExitStack() as st:
        g = setup_globals(kb, st)
        load_ident(kb, g, ident)
        fwd = lambda i: dict(w1=fw1, w2=fw2, w3=fw3[i], fcols=fcols[i])
        run(phase_hy_filters, kb, g, LP, zpos_p, fwd(0), taps_p)
        run(phase_hy_inproj, kb, g, [dict(xh=xh_p, valid=valid_p, L=LP, z=[z_p], x0=[x0_p])], why[0], brow[0], hcols[0])
        run(phase_hy_conv, kb, g, CFG_P, tabsP, taps_p, Hs_p, [dict(z=z_p, x0=x0_p, ya=yaP[:, 2032:2032 + LP])], dskip[0])
        agd = Dep()
        run(lambda: kb.all_gather(yaP, yaP_all, reads=[], writes=[agd]))
        kb.barrier()
        for gi in range(8):
            run(phase_hy_filters, kb, g, LS, zpos_s, fwd(1 + gi), taps_s[gi])
            run(phase_hy_inproj, kb, g, [dict(xh=xh_s, valid=valid_s, L=LS, z=[z_s[gi]], x0=[x0_s[gi]])], why[1 + gi], brow[1 + gi], hcols[1 + gi])
            run(phase_hy_conv, kb, g, CFG_S, tabsS, taps_s[gi], Hs_s[gi], [dict(z=z_s[gi], x0=x0_s[gi], ya=yaS[gi])], dskip[1 + gi])

        def outf(s_):
            def f(j, bi, cn):
                if bi == 0:
                    return ybT[s_, j * 128:(j + 1) * 128, 2048:2064]
                return ybT[s_, j * 128:(j + 1) * 128, (bi - 1) * 512:bi * 512]
            return f
        run(phase_conf, kb, g, [dict(x=xc[s_], mask=mask[s_], out=outf(s_)) for s_ in range(2)], wconf, ccols)
        with ExitStack() as st2:
            yaG = kb.sb(st2, [128, 4, 2048], BF16, "yaG")
            yaGd = Dep()
            ix = kb.sb(st2, [128, 4], U32, "gix")
            ixd = Dep()
            kb.dma("sp", ix[:, :], gidx[:, :], writes=[ixd])
            rows = yaP_all.rearrange("c (b t) -> (c b) t", t=2048)
            for k in range(4):
                run(lambda k=k: kb.gather_rows(yaG[:, k, :], rows[:, :], ix[:, k:k + 1], reads=[agd, ixd], writes=[yaGd]))
            ybv = ybT.rearrange("s (k p) t -> s p k t", p=128)
            yav_meta = yaP_all.rearrange("(k p) c -> p k c", p=128)
            yas = yaS.rearrange("g c t -> (g c) t").rearrange("(k p) t -> p k t", p=128)
            tiles = []
            for t0, n, xr in tl:
                if n == 128:
                    yl = [(slice(0, 4), yaG[:, :, t0:t0 + n], yaGd), (slice(4, 8), ybv[0, :, :, t0:t0 + n], None)]
                else:
                    yl = [(slice(0, 4), yav_meta[:, :, 2032:2048], agd), (slice(4, 8), ybv[0, :, :, 2048:2064], None)]
                tiles.append((xc[0, xr:xr + n, :], yl, h1[0, t0:t0 + n, :], n))
            for t0, n, xr in tl:
                tok0 = 16 + t0 if n == 128 else 0
                yl = [(slice(0, 4), yas[:, :, tok0:tok0 + n], None), (slice(4, 8), ybv[1, :, :, t0:t0 + n], None)]
                tiles.append((xc[1, xr:xr + n, :], yl, h1[1, t0:t0 + n, :], n))
            run(phase_proj_ln, kb, g, tiles, True, wout, bout, ln1g[0], ln1b[0])
        run(phase_mlp_ln, kb, g, [(h1[s_, t0:t0 + n, :], h2[s_, t0:t0 + n, :], n) for s_ in range(2) for t0, n, xr in tl], w1[0], w2[0], ln2g[0], ln2b[0])
        seqs = []
        for s_ in range(2):
            seqs.append(dict(tiles=[(h2[s_, t0:t0 + n, :], kvlat[s_, t0:t0 + n, :], cs[s_, t0:t0 + n, :], n) for t0, n, xr in tl],
                             CS=(Cq[s_], Sq[s_]), qt=(lambda s_: (lambda h, q0: QT[s_, h, :, q0:q0 + 512]))(s_)))
        run(phase_qkv, kb, g, seqs, wqa, qg, WqH, WqS, wkva, kvg)
        kvd = Dep()
        run(lambda: kb.all_gather(kvlat[0], kv_all, reads=[], writes=[kvd]))
        kb.barrier()
        seqs = []
        pch = [(kv_all[r * LS + t0:r * LS + t0 + 128, :], 128) for r in range(8) for t0 in range(0, 2048, 128)] + [(kv_all[2048:2064, :], 16)]
        sch = [(kvlat[1, t0:min(t0 + 128, LS), :], min(128, LS - t0)) for t0 in range(0, LS, 128)]
        for s_, ch in ((0, pch), (1, sch)):
            otv = otok[s_].rearrange("(a t p) (h c) -> a p t h c", p=128, t=4, c=64)
            seqs.append(dict(kchunks=ch, qt=(lambda s_: (lambda h: QT[s_, h, :, :]))(s_),
                             o=(lambda otv: (lambda qsb, half, h: otv[qsb * 2 + half, :, :, h, :]))(otv)))
        run(phase_attn, kb, g, seqs, WkH, WvH)
        tl2 = [(s_, t0) for s_ in range(2) for t0 in range(0, 2048, 128)]
        run(phase_proj_ln, kb, g, [(h2[s_, t0:t0 + 128, :], otok[s_, t0:t0 + 128, :], h3[s_, t0:t0 + 128, :], 128) for s_, t0 in tl2], False, wo, None, ln1g[1], ln1b[1])
        run(phase_mlp_ln, kb, g, [(h3[s_, t0:t0 + 128, :], out[s_, t0:t0 + 128, :], 128) for s_, t0 in tl2], w1[1], w2[1], ln2g[1], ln2b[1])
        kb.finish_wait()
    P.nsteps = step[0]
    return P


def build_nc():
    P = Prog()
    kb = P.kb
    ident = P.din("ident", [128, 128])
    xpad_p = P.din("xpad_p", [LP + 30, D]); maskpad = P.din("maskpad", [1, LP + 30])
    xh_s = P.din("xh_s", [LS + 2, D]); valid_s = P.din("valid_s", [1, LS + 2])
    xc_s = P.din("xc_s", [XC, D]); mask_s = P.din("mask_s", [1, XC])
    zpos_p = P.din("zpos_p", [33, LP]); zpos_s = P.din("zpos_s", [33, LS])
    tabsP, _ = declare_tabs(P, CFG_P, "tp_")
    tabsS, _ = declare_tabs(P, CFG_S, "ts_")
    fw1 = P.din("fw1", [2, 33, 64]); fw2 = P.din("fw2", [2, 64, 64])
    fw3 = P.din("fw3", [8, 2, 64, 64]); fcols = P.din("fcols", [8, 64, 2, 5])
    why = P.din("why", [D, 8 * 192]); brow = P.din("brow", [1, 8 * 192]); hcols = P.din("hcols", [64, 8 * 12]); dskip = P.din("dskip", [8, 1, 64])
    wconf = P.din("wconf", [D, 1024]); ccols = P.din("ccols", [128, 144])
    tokidx = P.din("tokidx", [128, 16], U32)
    wout = P.din("wout", [D, D]); bout = P.din("bout", [1, D])
    ln1g = P.din("ln1g", [2, 1, D]); ln1b = P.din("ln1b", [2, 1, D]); ln2g = P.din("ln2g", [2, 1, D]); ln2b = P.din("ln2b", [2, 1, D])
    w1 = P.din("w1", [2, D, DFF]); w2 = P.din("w2", [2, DFF, D])
    wqa = P.din("wqa", [D, 384]); qg = P.din("qg", [1, 384]); WqH = P.din("WqH", [384, NH * 128]); WqS = P.din("WqS", [384, NH * 32])
    wkva = P.din("wkva", [D, 288]); kvg = P.din("kvg", [1, 256])
    cs_all = P.din("cs_all", [LP, 32])
    cs = P.din("cs", [2, LS, 32]); Cq = P.din("Cq", [2, 32, 2048]); Sq = P.din("Sq", [2, 32, 2048])
    WkH = P.din("WkH", [256, NH * 128]); WvH = P.din("WvH", [256, NH * 64]); wo = P.din("wo", [D, D])
    out = P.dout("out", [2, 2048, D])
    yaP_all = P.scr("yaP_all", [512, YAW], BF16)
    yaS = P.scr("yaS", [8, 64, LS], BF16)
    ybT_p = P.scr("ybT_p", [512, LP], BF16); ybT_s = P.scr("ybT_s", [512, LS], BF16)
    h2f_p = P.scr("h2f_p", [2, 64, LP]); h2f_s = P.scr("h2f_s", [2, 64, LS])
    taps_p = P.scr("taps_p", [2, 64, LP]); Hs_p = P.scr("Hs_p", [86, 64 * CFG_P.nq, 2, CFG_P.NF])
    z_p = P.scr("z_p", [8, 64, LP]); x0_p = P.scr("x0_p", [8, 64, LP])
    taps_s = P.scr("taps_s", [2, 64, LS]); Hs_s = P.scr("Hs_s", [86, 64 * CFG_S.nq, 2, CFG_S.NF])
    z_s = P.scr("z_s", [8, 64, LS]); x0_s = P.scr("x0_s", [8, 64, LS])
    h1_all = P.scr("h1_all", [LP, D]); h2_all = P.scr("h2_all", [LP, D])
    h1_s = P.scr("h1_s", [LS, D]); h2_s = P.scr("h2_s", [LS, D]); h2_own = P.scr("h2_own", [LS, D])
    kv_all = P.scr("kv_all", [LP, 288]); kv_dummy = P.scr("kv_dummy", [LS, 288]); kvlat_s = P.scr("kvlat_s", [LS, 288])
    QT = P.scr("QT", [2, NH, 128, 2048], BF16)
    otok = P.scr("otok", [2, 2048, D]); h3 = P.scr("h3", [2, 2048, D])
    tl = chunk_tiles()
    with ExitStack() as st:
        g = setup_globals(kb, st)
        load_ident(kb, g, ident)
        fwd = lambda i: dict(w1=fw1, w2=fw2, w3=fw3[i], fcols=fcols[i])
        phase_hy_inproj(kb, g, [dict(xh=xpad_p[14:14 + LP + 2, :], valid=maskpad[:, 14:14 + LP + 2], L=LP,
                                     z=[z_p[gi] for gi in range(8)], x0=[x0_p[gi] for gi in range(8)])], why, brow, hcols, G=8)
        phase_hy_filter_h2(kb, g, LP, zpos_p, fwd(0), h2f_p)
        phase_hy_filter_h2(kb, g, LS, zpos_s, fwd(0), h2f_s)
        for gi in range(8):
            phase_hy_filter_taps(kb, g, LP, zpos_p, h2f_p, fw3[gi], fcols[gi], taps_p)
            phase_hy_conv(kb, g, CFG_P, tabsP, taps_p, Hs_p, [dict(z=z_p[gi], x0=x0_p[gi], ya=yaP_all[gi * 64:(gi + 1) * 64, 2032:2032 + LP])], dskip[gi])
        phase_hy_inproj(kb, g, [dict(xh=xh_s, valid=valid_s, L=LS, z=[z_s[gi] for gi in range(8)], x0=[x0_s[gi] for gi in range(8)])],
                        why, brow, hcols, G=8)
        for gi in range(8):
            phase_hy_filter_taps(kb, g, LS, zpos_s, h2f_s, fw3[gi], fcols[gi], taps_s)
            phase_hy_conv(kb, g, CFG_S, tabsS, taps_s, Hs_s, [dict(z=z_s[gi], x0=x0_s[gi], ya=yaS[gi])], dskip[gi])
        cseqs = []
        for j in range(8):
            r0 = 16 + 2048 * j
            cseqs.append(dict(x=xpad_p[r0:r0 + 2078, :], mask=maskpad[:, r0:r0 + 2078], ncols=2078, blocks=[(15 + 512 * i, 512) for i in range(4)],
                              out=(lambda j: (lambda jj, bi, cn: ybT_p[jj * 128:(jj + 1) * 128, 2048 * j + 512 * bi:2048 * j + 512 * bi + cn]))(j)))
        cseqs.append(dict(x=xpad_p[0:46, :], mask=maskpad[:, 0:46], ncols=46, blocks=[(15, 16)],
                          out=lambda jj, bi, cn: ybT_p[jj * 128:(jj + 1) * 128, 16384:16400]))

        def outf_s(jj, bi, cn):
            if bi == 0:
                return ybT_s[jj * 128:(jj + 1) * 128, 2048:2064]
            return ybT_s[jj * 128:(jj + 1) * 128, (bi - 1) * 512:bi * 512]
        cseqs.append(dict(x=xc_s, mask=mask_s, out=outf_s))
        phase_conf(kb, g, cseqs, wconf, ccols)
        yav = yaP_all.rearrange("(k p) c -> p k c", p=128)
        ybv_p = ybT_p.rearrange("(k p) t -> p k t", p=128)
        ybv_s = ybT_s.rearrange("(k p) t -> p k t", p=128)
        yas = yaS.rearrange("g c t -> (g c) t").rearrange("(k p) t -> p k t", p=128)
        tiles = []
        for j in range(8):
            for t0 in range(0, 2048, 128):
                tok = 16 + 2048 * j + t0
                gr = 2048 * j + t0
                tiles.append((xpad_p[15 + tok:15 + tok + 128, :],
                              [(slice(0, 4), yav[:, :, 2032 + tok:2032 + tok + 128], None), (slice(4, 8), ybv_p[:, :, gr:gr + 128], None)],
                              h1_all[gr:gr + 128, :], 128))
        tiles.append((xpad_p[15:31, :], [(slice(0, 4), yav[:, :, 2032:2048], None), (slice(4, 8), ybv_p[:, :, 16384:16400], None)],
                      h1_all[16384:16400, :], 16))
        for t0, n, xr in tl:
            tok0 = 16 + t0 if n == 128 else 0
            tiles.append((xc_s[xr:xr + n, :], [(slice(0, 4), yas[:, :, tok0:tok0 + n], None), (slice(4, 8), ybv_s[:, :, t0:t0 + n], None)],
                          h1_s[t0:t0 + n, :], n))
        phase_proj_ln(kb, g, tiles, True, wout, bout, ln1g[0], ln1b[0])
        ptl = [(r0, min(128, LP - r0)) for r0 in range(0, LP, 128)]
        phase_mlp_ln(kb, g, [(h1_all[r0:r0 + n, :], h2_all[r0:r0 + n, :], n) for r0, n in ptl] +
                     [(h1_s[t0:t0 + n, :], h2_s[t0:t0 + n, :], n) for t0, n, xr in tl], w1[0], w2[0], ln2g[0], ln2b[0])
        with ExitStack() as st2:
            ix = kb.sb(st2, [128, 16], U32, "tokix")
            ixd = Dep()
            kb.dma("sp", ix[:, :], tokidx[:, :], writes=[ixd])
            gb = [kb.sb(st2, [128, D], F32, "gb") for _ in range(2)]
            gd = [Dep(), Dep()]
            for i in range(16):
                j = i % 2
                kb.gather_rows(gb[j][:, :], h2_all[:, :], ix[:, i:i + 1], reads=[ixd], writes=[gd[j]])
                kb.dma("sp", h2_own[128 * i:128 * i + 128, :], gb[j][:, :], reads=[gd[j]])
            kb.dma("sp", h2_own[2048:2064, :], h2_all[16384:16400, :])
            kb.barrier()
        seqs = [dict(tiles=[(h2_all[r0:r0 + n, :], kv_all[r0:r0 + n, :], cs_all[r0:r0 + n, :], n) for r0, n in ptl], kv_only=True),
                dict(tiles=[(h2_own[t0:t0 + n, :], kv_dummy[t0:t0 + n, :], cs[0, t0:t0 + n, :], n) for t0, n, xr in tl],
                     CS=(Cq[0], Sq[0]), qt=lambda h, q0: QT[0, h, :, q0:q0 + 512]),
                dict(tiles=[(h2_s[t0:t0 + n, :], kvlat_s[t0:t0 + n, :], cs[1, t0:t0 + n, :], n) for t0, n, xr in tl],
                     CS=(Cq[1], Sq[1]), qt=lambda h, q0: QT[1, h, :, q0:q0 + 512])]
        phase_qkv(kb, g, seqs, wqa, qg, WqH, WqS, wkva, kvg)
        aseqs = []
        for s_, ch in ((0, [(kv_all[r0:r0 + n, :], n) for r0, n in ptl]),
                       (1, [(kvlat_s[t0:min(t0 + 128, LS), :], min(128, LS - t0)) for t0 in range(0, LS, 128)])):
            otv = otok[s_].rearrange("(a t p) (h c) -> a p t h c", p=128, t=4, c=64)
            aseqs.append(dict(kchunks=ch, qt=(lambda s_: (lambda h: QT[s_, h, :, :]))(s_),
                              o=(lambda otv: (lambda qsb, half, h: otv[qsb * 2 + half, :, :, h, :]))(otv)))
        phase_attn(kb, g, aseqs, WkH, WvH)
        hres = (h2_own, h2_s)
        tl2 = [(s_, t0) for s_ in range(2) for t0 in range(0, 2048, 128)]
        phase_proj_ln(kb, g, [(hres[s_][t0:t0 + 128, :], otok[s_, t0:t0 + 128, :], h3[s_, t0:t0 + 128, :], 128) for s_, t0 in tl2], False, wo, None, ln1g[1], ln1b[1])
        phase_mlp_ln(kb, g, [(h3[s_, t0:t0 + 128, :], out[s_, t0:t0 + 128, :], 128) for s_, t0 in tl2], w1[1], w2[1], ln2g[1], ln2b[1])
        kb.finish_wait()
    return P


def kernel(x_prompt, x_sample, meta_tokens, ev_w_in, ev_b_in, ev_short_w, ev_short_b,
           hy_w1, hy_b1, hy_freq1, hy_w2, hy_b2, hy_freq2, hy_w3, hy_decay, hy_skip_d,
           cf_dw_w, cf_dw_b, cf_ln_g, cf_ln_b, ev_w_out, ev_b_out,
           mla_wq_a, mla_q_norm, mla_wq_b, mla_wkv_a, mla_kv_norm, mla_wkv_b, mla_wo,
           ln1_g, ln1_b, mlp_w1, mlp_w2, ln2_g, ln2_b):
    f = lambda a: np.asarray(a, dtype=np.float32)
    x_prompt, x_sample, meta = f(x_prompt), f(x_sample), f(meta_tokens)
    win, bin_, sw, sb = f(ev_w_in)[0], f(ev_b_in)[0], f(ev_short_w)[0], f(ev_short_b)[0]
    ident = np.eye(128, dtype=np.float32)
    hp = np.concatenate([meta, x_prompt[0]], 0)
    hs = [np.concatenate([meta, x_sample[c]], 0) for c in range(8)]
    z1 = np.zeros((1, D), np.float32)
    z15 = np.zeros((15, D), np.float32)
    xpad_p = np.concatenate([z15, hp, z15], 0)
    maskpad = np.zeros((1, LP + 30), np.float32); maskpad[0, 15:15 + LP] = 1
    valid_s = np.ones((1, LS + 2), np.float32); valid_s[0, 0] = 0; valid_s[0, -1] = 0
    tabP, tabS = fft_tables(CFG_P), fft_tables(CFG_S)
    gcols = [[np.arange(k * 512 + gi * 64, k * 512 + gi * 64 + 64) for k in range(3)] for gi in range(8)]
    allc = np.concatenate([np.concatenate(gc) for gc in gcols])
    why = np.ascontiguousarray(win[:, allc])
    brow = bin_[allc][None, :].copy()
    hcols = np.concatenate([np.stack([sw[0, c_], sw[1, c_], sw[2, c_], sb[c_]], 1) for gc in gcols for c_ in gc], 1).astype(np.float32)
    fw3 = np.stack([np.ascontiguousarray(f(hy_w3)[0][:, :, gi * 64:gi * 64 + 64]) for gi in range(8)], 0)
    fcols = np.stack([np.stack([f(hy_freq1)[0], f(hy_b1)[0], f(hy_freq2)[0], f(hy_b2)[0], f(hy_decay)[0][:, gi * 64:gi * 64 + 64]], -1).transpose(1, 0, 2)
                      for gi in range(8)], 0).astype(np.float32)
    dskip = np.stack([f(hy_skip_d)[0][gi * 64:gi * 64 + 64][None, :] for gi in range(8)], 0)
    ccols = np.concatenate([colpack(bin_[1536:2048]), colpack(bin_[2048:2560]), colpack(f(cf_dw_b)[0]), colpack(f(cf_ln_g)[0]), colpack(f(cf_ln_b)[0]),
                            np.ascontiguousarray(f(cf_dw_w)[0].T.reshape(4, 128, 31).transpose(1, 0, 2).reshape(128, 124))], 1).astype(np.float32)
    wqb = f(mla_wq_b)[0].reshape(384, NH, 96)
    WqH = np.concatenate([wqb[:, :, 64:96], np.zeros((384, NH, 32), np.float32), wqb[:, :, 0:64]], -1).reshape(384, NH * 128)
    WqS = np.concatenate([wqb[:, :, 80:96], wqb[:, :, 64:80]], -1).reshape(384, NH * 32)
    wkvb = f(mla_wkv_b)[0].reshape(256, NH, 128)
    WkH = np.concatenate([np.zeros((256, NH, 64), np.float32), wkvb[:, :, 0:64]], -1).reshape(256, NH * 128)
    WvH = np.ascontiguousarray(wkvb[:, :, 64:128]).reshape(256, NH * 64)
    zp_p, zp_s = zpos_table(LP), zpos_table(LS)
    co, si = rope_cs(np.concatenate([np.arange(16, LP), np.arange(16)]))
    cs_all = np.concatenate([co, si], 1)
    shared = dict(ident=ident, xpad_p=xpad_p, maskpad=maskpad, valid_s=valid_s, zpos_p=zp_p, zpos_s=zp_s, fw1=f(hy_w1)[0], fw2=f(hy_w2)[0],
                  fw3=fw3, fcols=fcols, why=why, brow=brow, hcols=hcols, dskip=dskip, wconf=np.ascontiguousarray(win[:, 1536:2560]), ccols=ccols,
                  wout=f(ev_w_out)[0], bout=f(ev_b_out)[0:1], ln1g=f(ln1_g)[:, None, :], ln1b=f(ln1_b)[:, None, :],
                  ln2g=f(ln2_g)[:, None, :], ln2b=f(ln2_b)[:, None, :], w1=f(mlp_w1), w2=f(mlp_w2), wqa=f(mla_wq_a)[0], qg=f(mla_q_norm)[0:1],
                  WqH=WqH, WqS=WqS, wkva=f(mla_wkv_a)[0], kvg=f(mla_kv_norm)[0:1], cs_all=cs_all, WkH=WkH, WvH=WvH, wo=f(mla_wo)[0])
    for k, v in tabP.items():
        shared["tp_" + k] = v
    for k, v in tabS.items():
        shared["ts_" + k] = v
    P = build_nc()
    ims = []
    for c in range(8):
        m0 = 16 + 2048 * c
        xb, mb = make_xc(hs[c], 16, LS)
        css, Cqs, Sqs = [], [], []
        for pos in (np.concatenate([np.arange(m0, m0 + 2048), np.arange(16)]), np.concatenate([np.arange(16, LS), np.arange(16)])):
            co, si = rope_cs(pos)
            css.append(np.concatenate([co, si], 1))
            Cqs.append(np.concatenate([co[:2048].T, co[:2048].T], 0))
            Sqs.append(np.concatenate([-si[:2048].T, si[:2048].T], 0))
        tix = (2048 * c + 128 * np.arange(16)[None, :] + np.arange(128)[:, None]).astype(np.uint32)
        im = dict(shared)
        im.update(xh_s=np.concatenate([z1, hs[c], z1], 0), xc_s=xb, mask_s=mb, tokidx=tix,
                  cs=np.stack(css, 0), Cq=np.stack(Cqs, 0), Sq=np.stack(Sqs, 0))
        ims.append(check_inputs(P, im))
    r = run_bass_kernel_spmd(P.nc, ims, core_ids=list(range(8))).results
    y_prompt = np.concatenate([np.asarray(r[c]["out"])[0] for c in range(8)], 0)[None].astype(np.float32)
    y_sample = np.stack([np.asarray(r[c]["out"])[1] for c in range(8)], 0).astype(np.float32)
    return (y_prompt, y_sample)
```

```python
import math
from contextlib import ExitStack
import numpy as np
import ml_dtypes
import concourse.bass as bass
import concourse.mybir as mybir
from concourse.bass_utils import run_bass_kernel_spmd

F32 = mybir.dt.float32
BF16 = mybir.dt.bfloat16
AF = mybir.ActivationFunctionType
ALU = mybir.AluOpType
AX = mybir.AxisListType

D = 1024
NMETA = 16
DFF = 4096
ALPHA = 4 ** 0.25
LN_EPS = 1e-5
RMS_EPS = 1e-6
NH = 16


SEM_MAX = 24000


class Dep:
    __slots__ = ("w", "r")

    def __init__(self):
        self.w = None
        self.r = {}


class KB:
    def __init__(self, nc):
        self.nc = nc
        self.stack = ExitStack()
        self.raw = dict(pe=nc.tensor, act=nc.scalar, dve=nc.vector, pool=nc.gpsimd, sp=nc.sync)
        self.sem = {}
        self.cnt = {}
        self.seen = {e: {} for e in self.raw}
        self.semobj = []
        for e in ("pe", "act", "dve", "pool"):
            self.sem[e] = self._newsem("s_" + e)
            self.cnt[e] = 0
        self.dq = {}
        for q, n in (("sp", 20), ("act", 8), ("pool", 8)):
            self.dq[q] = dict(sems=[self._newsem(f"d_{q}{i}") for i in range(n)], vals=[0] * n, nxt=0)
        self.uid = 0

    def _newsem(self, name):
        s = self.stack.enter_context(self.nc.semaphore(name))
        self.semobj.append(s)
        return len(self.semobj) - 1

    def name(self, p):
        self.uid += 1
        return f"{p}{self.uid}"

    def sb(self, st, shape, dt, name="t"):
        return st.enter_context(self.nc.sbuf_tensor(self.name(name), list(shape), dt))

    def ps(self, st, shape, dt, name="p"):
        return st.enter_context(self.nc.psum_tensor(self.name(name), list(shape), dt))

    def _waits(self, eng, reads, writes, extra=None):
        need = {}

        def add(tok):
            if tok is None:
                return
            s, v, src = tok
            if src == "pe" and eng == "pe":
                return
            if need.get(s, 0) < v:
                need[s] = v

        for d in reads:
            add(d.w)
        for d in writes:
            add(d.w)
            for t in d.r.values():
                add(t)
        if extra:
            for t in extra:
                add(t)
        seen = self.seen[eng]
        for s, v in need.items():
            if seen.get(s, 0) < v:
                self.raw[eng].wait_ge(self.semobj[s], v)
                seen[s] = v

    def op(self, eng, fn, reads=(), writes=()):
        self._waits(eng, reads, writes)
        ins = fn(self.raw[eng])
        if self.cnt[eng] >= SEM_MAX:
            self.sem[eng] = self._newsem(self.name("s_" + eng))
            self.cnt[eng] = 0
        self.cnt[eng] += 1
        ins.then_inc(self.semobj[self.sem[eng]], 1)
        tok = (self.sem[eng], self.cnt[eng], eng)
        for d in reads:
            d.r[tok[0]] = tok
        for d in writes:
            d.w = tok
            d.r = {}
        return ins

    def dma(self, q, out, in_, reads=(), writes=(), **kw):
        dq = self.dq[q]
        i = dq["nxt"]
        dq["nxt"] = (i + 1) % len(dq["sems"])
        s = dq["sems"][i]
        extra = [(s, dq["vals"][i], "dma")] if dq["vals"][i] else None
        self._waits(q, reads, writes, extra)
        ins = self.raw[q].dma_start(out=out, in_=in_, **kw)
        dq["vals"][i] += 16
        ins.then_inc(self.semobj[s], 16)
        tok = (s, dq["vals"][i], "dma")
        for d in reads:
            d.r[s] = tok
        for d in writes:
            d.w = tok
            d.r = {}
        return ins

    def all_gather(self, in_ap, out_ap, reads=(), writes=()):
        if not hasattr(self, "ccsem"):
            self.ccsem = self._newsem("ccsem")
            self.ccval = 0
        self._waits("pool", reads, writes)
        ins = self.raw["pool"].collective_compute("AllGather", ALU.bypass, replica_groups=[list(range(8))],
                                                  ins=[in_ap.opt()], outs=[out_ap.opt()])
        self.ccval += 1
        ins.then_inc(self.semobj[self.ccsem], 1)
        tok = (self.ccsem, self.ccval, "cc")
        for d in reads:
            d.r[self.ccsem] = tok
        for d in writes:
            d.w = tok
            d.r = {}
        return ins

    def gather_rows(self, out, in_rows, idx, reads=(), writes=()):
        dq = self.dq["pool"]
        i = dq["nxt"]
        dq["nxt"] = (i + 1) % len(dq["sems"])
        s = dq["sems"][i]
        extra = [(s, dq["vals"][i], "dma")] if dq["vals"][i] else None
        self._waits("pool", reads, writes, extra)
        ins = self.raw["pool"].indirect_dma_start(out=out, out_offset=None, in_=in_rows,
                                                  in_offset=bass.IndirectOffsetOnAxis(ap=idx, axis=0))
        dq["vals"][i] += 16
        ins.then_inc(self.semobj[s], 16)
        tok = (s, dq["vals"][i], "dma")
        for d in reads:
            d.r[s] = tok
        for d in writes:
            d.w = tok
            d.r = {}
        return ins

    def barrier(self):
        toks = [(self.sem[e], self.cnt[e], e) for e in ("pe", "act", "dve", "pool") if self.cnt[e]]
        for q in self.dq.values():
            for s, v in zip(q["sems"], q["vals"]):
                if v:
                    toks.append((s, v, "dma"))
        if getattr(self, "ccval", 0):
            toks.append((self.ccsem, self.ccval, "cc"))
        for eng in ("pe", "act", "dve", "pool", "sp"):
            seen = self.seen[eng]
            for s, v, src in toks:
                if seen.get(s, 0) < v and not (s == self.sem.get(eng)):
                    self.raw[eng].wait_ge(self.semobj[s], v)
                    seen[s] = v

    def finish_wait(self):
        for q in self.dq.values():
            for s, v in zip(q["sems"], q["vals"]):
                if v and self.seen["sp"].get(s, 0) < v:
                    self.raw["sp"].wait_ge(self.semobj[s], v)
                    self.seen["sp"][s] = v


class Glob:
    pass


def setup_globals(kb, st):
    g = Glob()
    nc = kb.nc
    g.pall = kb.ps(st, [128, 8, 512], F32, "banks")
    g.psum = [g.pall[:, b, :] for b in range(8)]
    g.pd = [Dep() for _ in range(8)]
    g.ident_f = kb.sb(st, [128, 128], F32, "identf")
    g.ident_b = kb.sb(st, [128, 128], BF16, "identb")
    g.ident_d = Dep()
    g.ones_b = kb.sb(st, [128, 128], BF16, "onesb")
    g.ones_d = Dep()
    g.bk = -1
    return g


def load_ident(kb, g, ident_dram):
    kb.dma("sp", g.ident_f[:], ident_dram, writes=[g.ident_d])
    kb.op("dve", lambda e: e.tensor_copy(out=g.ident_b[:], in_=g.ident_f[:]), reads=[g.ident_d], writes=[g.ident_d])
    kb.op("pool", lambda e: e.memset(g.ones_b[:], 1.0), writes=[g.ones_d])


_rr = [0]


def cast_eng():
    _rr[0] += 1
    return ("dve", "pool", "act")[_rr[0] % 3]


def copy_op(kb, eng, out, in_, reads, writes):
    if eng == "act":
        return kb.op("act", lambda e: e.copy(out=out, in_=in_), reads=reads, writes=writes)
    return kb.op(eng, lambda e: e.tensor_copy(out=out, in_=in_), reads=reads, writes=writes)


def load_weight_bf16(kb, st_phase, dst, dst_dep, src, kc, ncols, stage_cols=2048):
    with ExitStack() as st:
        stg = [kb.sb(st, [128, stage_cols], F32, "wstg") for _ in range(3)]
        sd = [Dep() for _ in range(3)]
        i = 0
        for k in range(kc):
            for c0 in range(0, ncols, stage_cols):
                cn = min(stage_cols, ncols - c0)
                j = i % 3
                kb.dma("sp" if i % 2 == 0 else "pool", stg[j][:, :cn], src[k * 128:(k + 1) * 128, c0:c0 + cn], writes=[sd[j]])
                copy_op(kb, ("dve", "act")[i % 2], dst[:, k, c0:c0 + cn], stg[j][:, :cn], [sd[j]], [dst_dep])
                i += 1
        kb.barrier()


def load_bcast(kb, dst, dep, src_row):
    kb.dma("sp", dst, src_row.partition_broadcast(128) if len(src_row.shape) == 1 else src_row.broadcast_to([128, src_row.shape[-1]]), writes=[dep])


def layer_norm_tile(kb, r, rd, n, gt, bt, gbd, out, outd, small, smd, junk, junkd):
    s1, s2 = small[:, 0:1], small[:, 1:2]
    kb.op("act", lambda e: e.activation(out=junk[:n, :], in_=r[:n, :], func=AF.Identity, accum_out=s1[:n, :]), reads=[rd], writes=[junkd, smd])
    kb.op("act", lambda e: e.activation(out=junk[:n, :], in_=r[:n, :], func=AF.Square, accum_out=s2[:n, :]), reads=[rd], writes=[junkd, smd])
    mean, var, rstd = small[:, 2:3], small[:, 3:4], small[:, 4:5]
    kb.op("dve", lambda e: e.tensor_scalar(out=mean[:n, :], in0=s1[:n, :], scalar1=1.0 / D, scalar2=None, op0=ALU.mult), reads=[smd], writes=[smd])
    kb.op("dve", lambda e: e.tensor_tensor(out=var[:n, :], in0=mean[:n, :], in1=mean[:n, :], op=ALU.mult), reads=[smd], writes=[smd])
    kb.op("dve", lambda e: e.scalar_tensor_tensor(out=var[:n, :], in0=s2[:n, :], scalar=1.0 / D, in1=var[:n, :], op0=ALU.mult, op1=ALU.subtract), reads=[smd], writes=[smd])
    kb.op("act", lambda e: e.activation(out=rstd[:n, :], in_=var[:n, :], func=AF.Sqrt, bias=LN_EPS, scale=1.0), reads=[smd], writes=[smd])
    kb.op("dve", lambda e: e.reciprocal(out=rstd[:n, :], in_=rstd[:n, :]), reads=[smd], writes=[smd])
    kb.op("dve", lambda e: e.tensor_scalar(out=r[:n, :], in0=r[:n, :], scalar1=mean[:n, :], scalar2=rstd[:n, :], op0=ALU.subtract, op1=ALU.mult), reads=[smd, rd], writes=[rd])
    kb.op("dve", lambda e: e.tensor_tensor(out=r[:n, :], in0=r[:n, :], in1=gt[:n, :], op=ALU.mult), reads=[rd, gbd], writes=[rd])
    kb.op("pool", lambda e: e.tensor_tensor(out=out[:n, :], in0=r[:n, :], in1=bt[:n, :], op=ALU.add), reads=[rd, gbd], writes=[outd])


def mm(kb, out, lhsT, rhs, start, stop, reads, writes):
    return kb.op("pe", lambda e: e.matmul(out, lhsT=lhsT, rhs=rhs, start=start, stop=stop), reads, writes)


def tt(kb, eng, out, in0, in1, op, reads, writes):
    return kb.op(eng, lambda e: e.tensor_tensor(out=out, in0=in0, in1=in1, op=op), reads, writes)


def ts(kb, eng, out, in0, s1, s2, op0, op1, reads, writes):
    if s2 is None:
        return kb.op(eng, lambda e: e.tensor_scalar(out=out, in0=in0, scalar1=s1, scalar2=None, op0=op0), reads, writes)
    return kb.op(eng, lambda e: e.tensor_scalar(out=out, in0=in0, scalar1=s1, scalar2=s2, op0=op0, op1=op1), reads, writes)


def stt(kb, eng, out, in0, scalar, in1, op0, op1, reads, writes):
    return kb.op("dve", lambda e: e.scalar_tensor_tensor(out=out, in0=in0, scalar=scalar, in1=in1, op0=op0, op1=op1), reads, writes)


def act(kb, out, in_, func, reads, writes, **kw):
    return kb.op("act", lambda e: e.activation(out=out, in_=in_, func=func, **kw), reads, writes)


def nextbank(g):
    g.bk = (g.bk + 1) % 8
    return g.bk


def transpose_tile(kb, g, src, srcd, n, dstT, dstd, col0, kc=8):
    for k0 in range(0, kc, 4):
        b = nextbank(g)
        kn = min(4, kc - k0)
        pv = g.psum[b][:, :].rearrange("p (k t) -> p k t", k=4)
        for k in range(kn):
            kb.op("pe", lambda e, k=k: e.transpose(pv[:, k, :n], src[:n, (k0 + k) * 128:(k0 + k + 1) * 128], g.ident_f[:n, :n]),
                  reads=[srcd, g.ident_d], writes=[g.pd[b]])
        copy_op(kb, ("dve", "act")[b % 2], dstT[:, k0:k0 + kn, col0:col0 + n], pv[:, 0:kn, :n], [g.pd[b]], [dstd])


def phase_proj_ln(kb, g, tiles, fm, W, bias, lng, lnb):
    with ExitStack() as st:
        Wb = kb.sb(st, [128, 8, D], BF16, "Wb")
        Wd = Dep()
        load_weight_bf16(kb, st, Wb, Wd, W, 8, D)
        gt = kb.sb(st, [128, D], F32, "g")
        bt = kb.sb(st, [128, D], F32, "b")
        gbd = Dep()
        load_bcast(kb, gt[:], gbd, lng)
        load_bcast(kb, bt[:], gbd, lnb)
        if bias is not None:
            bi = kb.sb(st, [128, D], F32, "bias")
            load_bcast(kb, bi[:], gbd, bias)
        NB = 4
        hb = [kb.sb(st, [128, D], F32, "h") for _ in range(NB)]
        hd = [Dep() for _ in range(NB)]
        yT = [kb.sb(st, [128, 8, 128], BF16, "yT") for _ in range(NB)]
        yTd = [Dep() for _ in range(NB)]
        if not fm:
            yb = [kb.sb(st, [128, D], F32, "y") for _ in range(NB)]
            yd = [Dep() for _ in range(NB)]
        rb = [kb.sb(st, [128, D], F32, "r") for _ in range(NB)]
        rd = [Dep() for _ in range(NB)]
        junk = kb.sb(st, [128, D], F32, "junk")
        junkd = Dep()
        small = [kb.sb(st, [128, 8], F32, "small") for _ in range(NB)]
        smd = [Dep() for _ in range(NB)]
        for i, (hap, yap, oap, n) in enumerate(tiles):
            j = i % NB
            kb.dma("sp", hb[j][:n, :], hap, writes=[hd[j]])
            if fm:
                for qi, (ksl, src, dep) in enumerate(yap):
                    kb.dma("sp", yT[j][:, ksl, :n], src, reads=[dep] if dep is not None else [], writes=[yTd[j]])
            else:
                kb.dma("sp", yb[j][:n, :], yap, writes=[yd[j]])
                transpose_tile(kb, g, yb[j], yd[j], n, yT[j], yTd[j], 0)
            bks = (nextbank(g), nextbank(g))
            for half, bk in enumerate(bks):
                for k in range(8):
                    mm(kb, g.psum[bk][:n, :], yT[j][:, k, :n], Wb[:, k, half * 512:(half + 1) * 512], k == 0, k == 7,
                       [yTd[j], Wd], [g.pd[bk]])
            for half, bk in enumerate(bks):
                sl = slice(half * 512, (half + 1) * 512)
                if bias is not None:
                    tt(kb, "dve", rb[j][:n, sl], g.psum[bk][:n, :], bi[:n, sl], ALU.add, [g.pd[bk], gbd], [rd[j]])
                else:
                    copy_op(kb, "act", rb[j][:n, sl], g.psum[bk][:n, :], [g.pd[bk]], [rd[j]])
            stt(kb, "pool", rb[j][:n, :], hb[j][:n, :], ALPHA, rb[j][:n, :], ALU.mult, ALU.add, [hd[j], rd[j]], [rd[j]])
            layer_norm_tile(kb, rb[j], rd[j], n, gt, bt, gbd, rb[j], rd[j], small[j], smd[j], junk, junkd)
            kb.dma("pool", oap, rb[j][:n, :], reads=[rd[j]])
        kb.barrier()


def phase_mlp_ln(kb, g, tiles, W1, W2, lng, lnb):
    with ExitStack() as st:
        W1b = kb.sb(st, [128, 8, DFF], BF16, "W1b")
        W2b = kb.sb(st, [128, 32, D], BF16, "W2b")
        Wd = Dep()
        load_weight_bf16(kb, st, W1b, Wd, W1, 8, DFF)
        load_weight_bf16(kb, st, W2b, Wd, W2, 32, D, stage_cols=1024)
        gt = kb.sb(st, [128, D], F32, "g")
        bt = kb.sb(st, [128, D], F32, "b")
        gbd = Dep()
        load_bcast(kb, gt[:], gbd, lng)
        load_bcast(kb, bt[:], gbd, lnb)
        hb = [kb.sb(st, [128, D], F32, "h") for _ in range(4)]
        hd = [Dep() for _ in range(4)]
        hT = kb.sb(st, [128, 8, 512], BF16, "hT")
        hTd = Dep()
        uT = kb.sb(st, [128, 32, 512], BF16, "uT")
        uTd = [Dep() for _ in range(32)]
        rl = [kb.sb(st, [128, 512], F32, "relu") for _ in range(2)]
        rld = [Dep() for _ in range(2)]
        junk = kb.sb(st, [128, D], BF16, "junk")
        junkd = Dep()
        small = [kb.sb(st, [128, 8], F32, "small") for _ in range(4)]
        smd = [Dep() for _ in range(4)]
        for s0 in range(0, len(tiles), 4):
            grp = tiles[s0:s0 + 4]
            offs = []
            tot = 0
            for i, (iap, oap, n) in enumerate(grp):
                kb.dma("sp", hb[i][:n, :], iap, writes=[hd[i]])
                offs.append(tot)
                tot += n
            for i, (iap, oap, n) in enumerate(grp):
                transpose_tile(kb, g, hb[i], hd[i], n, hT, hTd, offs[i])
            for j in range(32):
                bk = nextbank(g)
                for k in range(8):
                    mm(kb, g.psum[bk][:, :tot], W1b[:, k, j * 128:(j + 1) * 128], hT[:, k, :tot], k == 0, k == 7, [Wd, hTd], [g.pd[bk]])
                q = j % 2
                act(kb, rl[q][:, :tot], g.psum[bk][:, :tot], AF.Relu, [g.pd[bk]], [rld[q]])
                tt(kb, "pool" if j % 4 == 0 else "dve", uT[:, j, :tot], rl[q][:, :tot], rl[q][:, :tot], ALU.mult, [rld[q]], [uTd[j]])
            for i, (iap, oap, n) in enumerate(grp):
                bks = (nextbank(g), nextbank(g))
                for half, bk in enumerate(bks):
                    for j in range(32):
                        mm(kb, g.psum[bk][:n, :], uT[:, j, offs[i]:offs[i] + n], W2b[:, j, half * 512:(half + 1) * 512], j == 0, j == 31,
                           [uTd[j], Wd], [g.pd[bk]])
                for half, bk in enumerate(bks):
                    sl = slice(half * 512, (half + 1) * 512)
                    stt(kb, "dve", hb[i][:n, sl], hb[i][:n, sl], ALPHA, g.psum[bk][:n, :], ALU.mult, ALU.add, [hd[i], g.pd[bk]], [hd[i]])
                layer_norm_tile(kb, hb[i], hd[i], n, gt, bt, gbd, hb[i], hd[i], small[i], smd[i], junk, junkd)
                kb.dma("pool", oap, hb[i][:n, :], reads=[hd[i]])
        kb.barrier()


def rms_rstd(kb, src, srcd, n, width, small, smd, junk, junkd, col):
    ss, rs = small[:, col:col + 1], small[:, col + 1:col + 2]
    act(kb, junk[:n, :width], src[:n, :width], AF.Square, [srcd], [junkd, smd], accum_out=ss[:n, :])
    act(kb, rs[:n, :], ss[:n, :], AF.Sqrt, [smd], [smd], bias=RMS_EPS, scale=1.0 / width)
    kb.op("dve", lambda e: e.reciprocal(out=rs[:n, :], in_=rs[:n, :]), [smd], [smd])
    return rs


def phase_qkv(kb, g, seqs, wqa, qg, WqH, WqS, wkva, kvg):
    with ExitStack() as st:
        wqa_b = kb.sb(st, [128, 8, 384], BF16, "wqa")
        wkva_b = kb.sb(st, [128, 8, 288], BF16, "wkva")
        wqh_b = kb.sb(st, [128, 3, NH * 128], BF16, "wqh")
        wqs_b = kb.sb(st, [128, 3, NH * 32], BF16, "wqs")
        Wd = Dep()
        load_weight_bf16(kb, st, wqa_b, Wd, wqa, 8, 384)
        load_weight_bf16(kb, st, wkva_b, Wd, wkva, 8, 288)
        load_weight_bf16(kb, st, wqh_b, Wd, WqH, 3, NH * 128)
        load_weight_bf16(kb, st, wqs_b, Wd, WqS, 3, NH * 32)
        qgt = kb.sb(st, [128, 384], F32, "qg")
        kvgt = kb.sb(st, [128, 256], F32, "kvg")
        gd = Dep()
        load_bcast(kb, qgt[:], gd, qg)
        load_bcast(kb, kvgt[:], gd, kvg)
        hb = [kb.sb(st, [128, D], F32, "h") for _ in range(4)]
        hd = [Dep() for _ in range(4)]
        hT = kb.sb(st, [128, 8, 512], BF16, "hT")
        hTd = Dep()
        cq = [kb.sb(st, [128, 384], F32, "cq") for _ in range(2)]
        cqd = [Dep() for _ in range(2)]
        cqT = kb.sb(st, [128, 3, 512], BF16, "cqT")
        cqTd = Dep()
        kvr = [kb.sb(st, [128, 288], F32, "kvr") for _ in range(2)]
        kvrd = [Dep() for _ in range(2)]
        kvo = [kb.sb(st, [128, 288], F32, "kvo") for _ in range(2)]
        kvod = [Dep() for _ in range(2)]
        cst = [kb.sb(st, [128, 32], F32, "cs") for _ in range(2)]
        csd = [Dep() for _ in range(2)]
        tmp = [kb.sb(st, [128, 64], F32, "tmp") for _ in range(2)]
        tmpd = [Dep() for _ in range(2)]
        junk = kb.sb(st, [128, 384], F32, "junk")
        junkd = Dep()
        small = [kb.sb(st, [128, 8], F32, "small") for _ in range(2)]
        smd = [Dep() for _ in range(2)]
        Ct = kb.sb(st, [32, 2048], F32, "C")
        St = kb.sb(st, [32, 2048], F32, "S")
        CSd = Dep()
        qsw = [kb.sb(st, [32, 512], F32, "qsw") for _ in range(2)]
        qswd = [Dep() for _ in range(2)]
        qo = [kb.sb(st, [128, 512], BF16, "qo") for _ in range(2)]
        qod = [Dep() for _ in range(2)]
        it = 0
        for sq in seqs:
            if not sq.get("kv_only"):
                kb.dma("sp", Ct[:], sq["CS"][0], writes=[CSd])
                kb.dma("sp", St[:], sq["CS"][1], writes=[CSd])
            tiles = sq["tiles"]
            for s0 in range(0, len(tiles), 4):
                grp = tiles[s0:s0 + 4]
                offs, tot = [], 0
                for i, (hap, kvap, csap, n) in enumerate(grp):
                    kb.dma("sp", hb[i][:n, :], hap, writes=[hd[i]])
                    offs.append(tot)
                    tot += n
                for i, (hap, kvap, csap, n) in enumerate(grp):
                    transpose_tile(kb, g, hb[i], hd[i], n, hT, hTd, offs[i])
                is_main = (tot == 512) and not sq.get("kv_only")
                for i, (hap, kvap, csap, n) in enumerate(grp):
                    j = it % 2
                    it += 1
                    kb.dma("sp", cst[j][:n, :], csap, writes=[csd[j]])
                    bk = nextbank(g)
                    for k in range(8):
                        mm(kb, g.psum[bk][:n, :288], hT[:, k, offs[i]:offs[i] + n], wkva_b[:, k, :], k == 0, k == 7, [hTd, Wd], [g.pd[bk]])
                    copy_op(kb, "act", kvr[j][:n, :], g.psum[bk][:n, :288], [g.pd[bk]], [kvrd[j]])
                    rs = rms_rstd(kb, kvr[j], kvrd[j], n, 256, small[j], smd[j], junk, junkd, 0)
                    stt(kb, "dve", kvo[j][:n, 0:256], kvr[j][:n, 0:256], rs[:n, :], kvgt[:n, :], ALU.mult, ALU.mult, [kvrd[j], smd[j], gd], [kvod[j]])
                    x1, x2 = kvr[j][:n, 256:272], kvr[j][:n, 272:288]
                    co, si = cst[j][:n, 0:16], cst[j][:n, 16:32]
                    t = tmp[j]
                    tt(kb, "dve", t[:n, 0:16], x1, co, ALU.mult, [kvrd[j], csd[j]], [tmpd[j]])
                    tt(kb, "dve", t[:n, 16:32], x2, si, ALU.mult, [kvrd[j], csd[j]], [tmpd[j]])
                    tt(kb, "dve", t[:n, 32:48], x1, si, ALU.mult, [kvrd[j], csd[j]], [tmpd[j]])
                    tt(kb, "dve", t[:n, 48:64], x2, co, ALU.mult, [kvrd[j], csd[j]], [tmpd[j]])
                    tt(kb, "dve", kvo[j][:n, 256:272], t[:n, 0:16], t[:n, 16:32], ALU.subtract, [tmpd[j]], [kvod[j]])
                    tt(kb, "dve", kvo[j][:n, 272:288], t[:n, 32:48], t[:n, 48:64], ALU.add, [tmpd[j]], [kvod[j]])
                    kb.dma("pool", kvap, kvo[j][:n, :], reads=[kvod[j]])
                    if not is_main:
                        continue
                    bk = nextbank(g)
                    for k in range(8):
                        mm(kb, g.psum[bk][:n, :384], hT[:, k, offs[i]:offs[i] + n], wqa_b[:, k, :], k == 0, k == 7, [hTd, Wd], [g.pd[bk]])
                    copy_op(kb, "act", cq[j][:n, :], g.psum[bk][:n, :384], [g.pd[bk]], [cqd[j]])
                    rs = rms_rstd(kb, cq[j], cqd[j], n, 384, small[j], smd[j], junk, junkd, 2)
                    stt(kb, "dve", cq[j][:n, :], cq[j][:n, :], rs[:n, :], qgt[:n, :], ALU.mult, ALU.mult, [cqd[j], smd[j], gd], [cqd[j]])
                    transpose_tile(kb, g, cq[j], cqd[j], n, cqT, cqTd, offs[i], kc=3)
                if not is_main:
                    continue
                q0 = (s0 // 4) * 512
                for h in range(NH):
                    j = h % 2
                    bka, bkb = nextbank(g), nextbank(g)
                    for k in range(3):
                        mm(kb, g.psum[bka][:, :], wqh_b[:, k, h * 128:(h + 1) * 128], cqT[:, k, :], k == 0, k == 2, [Wd, cqTd], [g.pd[bka]])
                    for k in range(3):
                        mm(kb, g.psum[bkb][:32, :], wqs_b[:, k, h * 32:(h + 1) * 32], cqT[:, k, :], k == 0, k == 2, [Wd, cqTd], [g.pd[bkb]])
                    tt(kb, "dve", qsw[j][:, :], g.psum[bkb][:32, :], St[:, q0:q0 + 512], ALU.mult, [g.pd[bkb], CSd], [qswd[j]])
                    rope_q(kb, g, qo[j], qod[j], bka, qsw[j], qswd[j], Ct, CSd, q0, st, small)
                    copy_op(kb, "act", qo[j][32:64, :], g.psum[bka][32:64, :], [g.pd[bka]], [qod[j]])
                    copy_op(kb, "act", qo[j][64:128, :], g.psum[bka][64:128, :], [g.pd[bka]], [qod[j]])
                    kb.dma("pool", sq["qt"](h, q0), qo[j][:, :], reads=[qod[j]])
        kb.barrier()


_ropetmp = {}


def rope_q(kb, g, qo, qod, bka, qsw, qswd, Ct, CSd, q0, st, small):
    key = id(st)
    if key not in _ropetmp:
        _ropetmp[key] = (kb.sb(st, [32, 512], F32, "rq"), Dep())
    t, td = _ropetmp[key]
    tt(kb, "dve", t[:, :], g.psum[bka][0:32, :], Ct[:, q0:q0 + 512], ALU.mult, [g.pd[bka], CSd], [td])
    tt(kb, "pool", qo[0:32, :], t[:, :], qsw[:, :], ALU.add, [td, qswd], [qod])


QK_SCALE = 96 ** -0.5


def phase_attn(kb, g, seqs, WkH, WvH):
    NKmax = max(sum(n for _, n in sq["kchunks"]) for sq in seqs)
    NCH = max(len(sq["kchunks"]) for sq in seqs)
    with ExitStack() as st:
        wk_b = kb.sb(st, [128, 2, NH * 128], BF16, "wk")
        wv_b = kb.sb(st, [128, 2, NH * 64], BF16, "wv")
        Wd = Dep()
        load_weight_bf16(kb, st, wk_b, Wd, WkH, 2, NH * 128)
        load_weight_bf16(kb, st, wv_b, Wd, WvH, 2, NH * 64, stage_cols=1024)
        ckvT = kb.sb(st, [128, 3, NKmax], BF16, "ckvT")
        ckvTd = Dep()
        KT = kb.sb(st, [128, NKmax], BF16, "KT")
        KTd = Dep()
        V = kb.sb(st, [128, NCH, 66], BF16, "V")
        Vd = Dep()
        QT = [kb.sb(st, [128, 2048], BF16, "QT") for _ in range(2)]
        QTd = [Dep() for _ in range(2)]
        P = [kb.sb(st, [128, 1024], BF16, "P") for _ in range(2)]
        Pd = [Dep() for _ in range(2)]
        oT = kb.sb(st, [128, 1024], F32, "oT")
        oTd = Dep()
        osm = [kb.sb(st, [128, 4, 64], F32, "osm") for _ in range(2)]
        osmd = [Dep() for _ in range(2)]
        rec = [kb.sb(st, [128, 4, 1], F32, "rec") for _ in range(2)]
        recd = [Dep() for _ in range(2)]
        kvin = [kb.sb(st, [128, 288], F32, "kvin") for _ in range(2)]
        kvind = [Dep() for _ in range(2)]
        kb.op("pool", lambda e: e.memset(V[:, :, 64:66], 1.0), [], [Vd])
        for sq in seqs:
            chunks = sq["kchunks"]
            NK = sum(n for _, n in chunks)
            coff = []
            c0 = 0
            for ci, (kvap, n) in enumerate(chunks):
                j = ci % 2
                kb.dma("sp" if ci % 2 == 0 else "pool", kvin[j][:n, :], kvap, writes=[kvind[j]])
                b = nextbank(g)
                pv = g.psum[b].rearrange("p (k t) -> p k t", k=4)
                for k, w in ((0, 128), (1, 128), (2, 32)):
                    kb.op("pe", lambda e, k=k, w=w: e.transpose(pv[:w, k, :n], kvin[j][:n, k * 128:k * 128 + w], g.ident_f[:n, :n]),
                          [kvind[j], g.ident_d], [g.pd[b]])
                copy_op(kb, "dve", ckvT[:, 0:2, c0:c0 + n], pv[:, 0:2, :n], [g.pd[b]], [ckvTd])
                copy_op(kb, "act", ckvT[0:32, 2, c0:c0 + n], pv[0:32, 2, :n], [g.pd[b]], [ckvTd])
                coff.append(c0)
                c0 += n
            for h in range(NH):
                qj = h % 2
                kb.dma("sp", QT[qj][:, :], sq["qt"](h), writes=[QTd[qj]])
                for bi, k0 in enumerate(range(0, NK, 512)):
                    kn = min(512, NK - k0)
                    b = 6 + bi % 2
                    mm(kb, g.psum[b][:, :kn], wk_b[:, 0, h * 128:(h + 1) * 128], ckvT[:, 0, k0:k0 + kn], True, False, [Wd, ckvTd], [g.pd[b]])
                    mm(kb, g.psum[b][:, :kn], wk_b[:, 1, h * 128:(h + 1) * 128], ckvT[:, 1, k0:k0 + kn], False, False, [Wd, ckvTd], [g.pd[b]])
                    mm(kb, g.psum[b][:, :kn], g.ident_b[0:32, :], ckvT[0:32, 2, k0:k0 + kn], False, True, [g.ident_d, ckvTd], [g.pd[b]])
                    copy_op(kb, ("dve", "pool")[bi % 2] if False else "dve", KT[:, k0:k0 + kn], g.psum[b][:, :kn], [g.pd[b]], [KTd])
                for gi, cg in enumerate(range(0, len(chunks), 8)):
                    cn = min(8, len(chunks) - cg)
                    b = 6 + gi % 2
                    for ci in range(cn):
                        n = chunks[cg + ci][1]
                        o = coff[cg + ci]
                        for k in range(2):
                            mm(kb, g.psum[b][:n, ci * 64:(ci + 1) * 64], ckvT[:, k, o:o + n], wv_b[:, k, h * 64:(h + 1) * 64], k == 0, k == 1,
                               [ckvTd, Wd], [g.pd[b]])
                    copy_op(kb, "dve", V[:, cg:cg + cn, 0:64], g.psum[b][:, :cn * 64].rearrange("p (c d) -> p c d", d=64), [g.pd[b]], [Vd])
                for qsb in range(2):
                    for ci, (kvap, n) in enumerate(chunks):
                        o = coff[ci]
                        sb0 = 2 + 2 * (ci % 2)
                        pj = ci % 2
                        for i in range(2):
                            mm(kb, g.psum[sb0 + i][:n, :], KT[:, o:o + n], QT[qj][:, qsb * 1024 + i * 512:qsb * 1024 + (i + 1) * 512], True, True,
                               [KTd, QTd[qj]], [g.pd[sb0 + i]])
                        act(kb, P[pj][:n, :].rearrange("p (a b) -> p a b", a=2), g.pall[:n, sb0:sb0 + 2, :], AF.Exp,
                            [g.pd[sb0], g.pd[sb0 + 1]], [Pd[pj]], scale=QK_SCALE)
                        for i in range(2):
                            mm(kb, g.psum[i][:65, :], V[:n, ci, 0:65], P[pj][:n, i * 512:(i + 1) * 512], ci == 0, ci == len(chunks) - 1,
                               [Vd, Pd[pj]], [g.pd[i]])
                    copy_op(kb, "dve", oT[:65, :].rearrange("p (a b) -> p a b", a=2), g.pall[:65, 0:2, :], [g.pd[0], g.pd[1]], [oTd])
                    for half in range(2):
                        b = 6 + half
                        oj = half
                        pv = g.psum[b][:, 0:4 * 65].rearrange("p (t c) -> p t c", c=65)
                        for t in range(4):
                            q0 = half * 512 + t * 128
                            kb.op("pe", lambda e, t=t, q0=q0: e.transpose(pv[:, t, :], oT[:65, q0:q0 + 128], g.ident_f[:65, :65]),
                                  [oTd, g.ident_d], [g.pd[b]])
                        kb.op("dve", lambda e: e.reciprocal(out=rec[oj][:, :, :], in_=pv[:, :, 64:65]), [g.pd[b]], [recd[oj]])
                        tt(kb, "dve", osm[oj][:, :, :], pv[:, :, 0:64], rec[oj][:, :, :].broadcast_to([128, 4, 64]), ALU.mult,
                           [g.pd[b], recd[oj]], [osmd[oj]])
                        kb.dma("pool", sq["o"](qsb, half, h), osm[oj][:, :, :], reads=[osmd[oj]])
        kb.barrier()


XC = 2124


def phase_conf(kb, g, seqs, w_conf, cols_ap):
    with ExitStack() as st:
        wb = kb.sb(st, [128, 8, 1024], BF16, "wconf")
        Wd = Dep()
        load_weight_bf16(kb, st, wb, Wd, w_conf, 8, 1024)
        cols = kb.sb(st, [128, 20 + 124], F32, "cols")
        cd = Dep()
        kb.dma("sp", cols[:], cols_ap, writes=[cd])
        Dg = kb.sb(st, [128, 4, 31, 128], BF16, "Dg")
        Dgd = Dep()
        for j in range(4):
            for k in range(31):
                ts(kb, ("dve", "pool")[k % 2], Dg[:, j, k, :], g.ident_f[:, :], cols[:, 20 + j * 31 + k:20 + j * 31 + k + 1], None, ALU.mult, None,
                   [g.ident_d, cd], [Dgd])
        xin = [kb.sb(st, [128, D], F32, "xin") for _ in range(2)]
        xind = [Dep() for _ in range(2)]
        xT = kb.sb(st, [128, 8, XC], BF16, "xT")
        xTd = Dep()
        hT = kb.sb(st, [128, 4, XC], BF16, "hT")
        hTd = Dep()
        mask = kb.sb(st, [128, XC], F32, "mask")
        maskd = Dep()
        sg = [kb.sb(st, [128, 512], F32, "sg") for _ in range(2)]
        sgd = [Dep() for _ in range(2)]
        cc2 = [kb.sb(st, [128, 4, 512], F32, "cc") for _ in range(2)]
        ccd2 = [Dep(), Dep()]
        cb2 = [kb.sb(st, [128, 4, 512], BF16, "cb") for _ in range(2)]
        cbd2 = [Dep(), Dep()]
        sq2 = [kb.sb(st, [128, 4, 512], BF16, "sq") for _ in range(2)]
        sqd2 = [Dep(), Dep()]
        mean2 = [kb.sb(st, [128, 512], F32, "mean") for _ in range(2)]
        rstd2 = [kb.sb(st, [128, 512], F32, "rstd") for _ in range(2)]
        std2 = [Dep(), Dep()]
        blk_i = 0
        yt = [kb.sb(st, [128, 512], F32, "yt") for _ in range(2)]
        ytd = [Dep() for _ in range(2)]
        yo = [kb.sb(st, [128, 512], BF16, "yo") for _ in range(2)]
        yod = [Dep() for _ in range(2)]
        for s_ in seqs:
            NC = s_.get("ncols", XC)
            kb.dma("sp", mask[:, :NC], s_["mask"].broadcast_to([128, NC]), writes=[maskd])
            for ti, t0 in enumerate(range(0, NC, 128)):
                n = min(128, NC - t0)
                j = ti % 2
                kb.dma("sp", xin[j][:n, :], s_["x"][t0:t0 + n, :], writes=[xind[j]])
                transpose_tile(kb, g, xin[j], xind[j], n, xT, xTd, t0)
            for bi, c0 in enumerate(range(0, NC, 512)):
                cn = min(512, NC - c0)
                for j in range(4):
                    ba, bg = nextbank(g), nextbank(g)
                    for k in range(8):
                        mm(kb, g.psum[ba][:, :cn], wb[:, k, j * 128:(j + 1) * 128], xT[:, k, c0:c0 + cn], k == 0, k == 7, [Wd, xTd], [g.pd[ba]])
                    for k in range(8):
                        mm(kb, g.psum[bg][:, :cn], wb[:, k, 512 + j * 128:512 + (j + 1) * 128], xT[:, k, c0:c0 + cn], k == 0, k == 7, [Wd, xTd], [g.pd[bg]])
                    q = j % 2
                    act(kb, sg[q][:, :cn], g.psum[bg][:, :cn], AF.Sigmoid, [g.pd[bg], cd], [sgd[q]], bias=cols[:, 4 + j:5 + j], scale=1.0)
                    stt(kb, "dve", sg[q][:, :cn], g.psum[ba][:, :cn], cols[:, j:j + 1], sg[q][:, :cn], ALU.add, ALU.mult, [g.pd[ba], cd, sgd[q]], [sgd[q]])
                    tt(kb, "pool", hT[:, j, c0:c0 + cn], sg[q][:, :cn], mask[:, c0:c0 + cn], ALU.mult, [sgd[q], maskd], [hTd])
            blocks = s_.get("blocks") or ([(15, 16)] + [(61 + 512 * i, 512) for i in range(4)])
            for bi, (c0, cn) in enumerate(blocks):
                pp = blk_i % 2
                blk_i += 1
                cc, ccd, cb, cbd, sq, sqd = cc2[pp], ccd2[pp], cb2[pp], cbd2[pp], sq2[pp], sqd2[pp]
                mean, rstd, std = mean2[pp], rstd2[pp], std2[pp]
                for j in range(4):
                    b = nextbank(g)
                    for k in range(31):
                        mm(kb, g.psum[b][:, :cn], Dg[:, j, k, :], hT[:, j, c0 + k - 15:c0 + k - 15 + cn], k == 0, k == 30, [Dgd, hTd], [g.pd[b]])
                    act(kb, cc[:, j, :cn], g.psum[b][:, :cn], AF.Identity, [g.pd[b], cd], [ccd], bias=cols[:, 8 + j:9 + j], scale=1.0)
                    copy_op(kb, "dve", cb[:, j, :cn], cc[:, j, :cn], [ccd], [cbd])
                    tt(kb, "dve", sq[:, j, :cn], cc[:, j, :cn], cc[:, j, :cn], ALU.mult, [ccd], [sqd])
                b1, b2 = nextbank(g), nextbank(g)
                for j in range(4):
                    mm(kb, g.psum[b1][:, :cn], g.ones_b[:, :], cb[:, j, :cn], j == 0, j == 3, [g.ones_d, cbd], [g.pd[b1]])
                for j in range(4):
                    mm(kb, g.psum[b2][:, :cn], g.ones_b[:, :], sq[:, j, :cn], j == 0, j == 3, [g.ones_d, sqd], [g.pd[b2]])
                act(kb, mean[:, :cn], g.psum[b1][:, :cn], AF.Copy, [g.pd[b1]], [std], scale=1.0 / 512)
                tt(kb, "pool", rstd[:, :cn], mean[:, :cn], mean[:, :cn], ALU.mult, [std], [std])
                stt(kb, "dve", rstd[:, :cn], g.psum[b2][:, :cn], 1.0 / 512, rstd[:, :cn], ALU.mult, ALU.subtract, [g.pd[b2], std], [std])
                act(kb, rstd[:, :cn], rstd[:, :cn], AF.Sqrt, [std], [std], bias=LN_EPS, scale=1.0)
                kb.op("dve", lambda e: e.reciprocal(out=rstd[:, :cn], in_=rstd[:, :cn]), [std], [std])
                for j in range(4):
                    q = j % 2
                    tt(kb, "dve", yt[q][:, :cn], cc[:, j, :cn], mean[:, :cn], ALU.subtract, [ccd, std], [ytd[q]])
                    tt(kb, "dve", yt[q][:, :cn], yt[q][:, :cn], rstd[:, :cn], ALU.mult, [ytd[q], std], [ytd[q]])
                    ts(kb, "dve", yt[q][:, :cn], yt[q][:, :cn], cols[:, 12 + j:13 + j], cols[:, 16 + j:17 + j], ALU.mult, ALU.add, [ytd[q], cd], [ytd[q]])
                    act(kb, yo[q][:, :cn], yt[q][:, :cn], AF.Silu, [ytd[q]], [yod[q]])
                    kb.dma("pool", s_["out"](j, bi, cn), yo[q][:, :cn], reads=[yod[q]])
        kb.barrier()


I32 = mybir.dt.int32
TWO_PI = 2.0 * math.pi


class FCfg:
    def __init__(self, L, rows, N1, nq, CB):
        self.L, self.rows, self.N1, self.nq, self.CB = L, rows, N1, nq, CB
        self.N2 = 86 * nq
        self.N = N1 * self.N2
        self.NF = N1 // 2 + 1
        assert self.N >= 2 * L - 1 and rows * self.N2 >= L


CFG_P = FCfg(16400, 64, 128, 3, 8)
CFG_S = FCfg(2064, 24, 48, 1, 32)


def fft_tables(cfg):
    N1, N2, N, rows, nq, NF = cfg.N1, cfg.N2, cfg.N, cfg.rows, cfg.nq, cfg.NF
    n1 = np.arange(rows)[:, None].astype(np.float64)
    k1 = np.arange(NF)[None, :].astype(np.float64)
    a = 2 * np.pi * n1 * k1 / N1
    F1 = np.concatenate([np.cos(a), -np.sin(a)], 1)
    n2 = np.arange(N2)[:, None].astype(np.float64)
    a = 2 * np.pi * n2 * k1 / N
    tw = np.stack([np.cos(a), -np.sin(a)], 1)
    tw = tw.reshape(nq, 86, 2, NF).transpose(1, 0, 2, 3)
    m = np.arange(N2)[None, :].astype(np.float64)
    a = 2 * np.pi * n2 * m / N2
    F2 = np.stack([np.cos(a), -np.sin(a), np.sin(a)], 0)
    F2 = F2.reshape(3, nq, 86, N2).transpose(2, 0, 1, 3)
    kk = np.arange(NF)[:, None].astype(np.float64)
    a = 2 * np.pi * kk * np.arange(N2)[None, :] / N
    twc = np.stack([np.cos(a), np.sin(a)], 1)
    a = 2 * np.pi * kk * np.arange(rows)[None, :] / N1
    wgt = np.full((NF, 1), 2.0)
    wgt[0, 0] = 1.0
    wgt[NF - 1, 0] = 1.0
    G1 = np.stack([wgt * np.cos(a) / N, -wgt * np.sin(a) / N], 1)
    bf = ml_dtypes.bfloat16
    return dict(F1=F1.astype(np.float32).astype(bf), tw=np.ascontiguousarray(tw).astype(np.float32),
                F2=np.ascontiguousarray(F2).astype(np.float32).astype(bf), twc=twc.astype(np.float32),
                G1=G1.astype(np.float32).astype(bf))


class FTab:
    pass


def fft_load_tables(kb, st, cfg, tabs):
    t = FTab()
    t.d = Dep()
    t.F1 = kb.sb(st, [cfg.rows, 2 * cfg.NF], BF16, "F1")
    t.tw = kb.sb(st, [86, cfg.nq, 2, cfg.NF], F32, "tw")
    t.F2 = kb.sb(st, [86, 3, cfg.nq, cfg.N2], BF16, "F2")
    t.twc = kb.sb(st, [cfg.NF, 2, cfg.N2], F32, "twc")
    t.G1 = kb.sb(st, [cfg.NF, 2, cfg.rows], BF16, "G1")
    for nm in ("F1", "tw", "F2", "twc", "G1"):
        kb.dma("sp", getattr(t, nm)[:], tabs[nm], writes=[t.d])
    return t


class FBuf:
    pass


def fft_alloc(kb, st, cfg, nsets=1):
    CB, nq, N1, N2, rows = cfg.CB, cfg.nq, cfg.NF, cfg.N2, cfg.rows
    E = CB * nq * N1
    E2 = CB * N2
    tn = max(E, E2)
    Ab = kb.sb(st, [86, CB * nq, 2, N1], BF16, "Ab")
    Abd = Dep()
    Xs = kb.sb(st, [86, CB * nq, 2, N1], F32, "Xs")
    Xsd = Dep()
    t = [kb.sb(st, [128, tn], F32, "ft") for _ in range(4)]
    td = [Dep() for _ in range(4)]
    sets = []
    for _ in range(nsets):
        b = FBuf()
        b.src_f = kb.sb(st, [rows, CB, N2], F32, "srcf")
        b.src_fd = Dep()
        b.src_b = kb.sb(st, [rows, CB, N2], BF16, "srcb")
        b.src_bd = Dep()
        b.As = kb.sb(st, [86, CB * nq, 2, N1], F32, "As")
        b.Asd = Dep()
        b.Ab, b.Abd, b.Xs, b.Xsd, b.t, b.td = Ab, Abd, Xs, Xsd, t, td
        sets.append(b)
    return sets if nsets > 1 else sets[0]


import os
CMUL_ENG = os.environ.get("CMUL_ENG", "dve,dve,dve,dve,dve,dve").split(",")


def cmul_batched(kb, cfg, b, P, shape, Are, Aim, Br, Bi, out_re, out_im, rdeps, wdep, conj=False):
    n = int(np.prod(shape))
    pat = {2: "p (a b) -> p a b", 3: "p (a b c) -> p a b c"}[len(shape)]
    kw = dict(zip("abc", shape))
    kw.pop("a")
    tv = [b.t[i][:P, :n].rearrange(pat, **kw) for i in range(4)]
    e = CMUL_ENG
    tt(kb, e[0], tv[0], Are, Br, ALU.mult, rdeps, [b.td[0]])
    tt(kb, e[1], tv[1], Aim, Bi, ALU.mult, rdeps, [b.td[1]])
    tt(kb, e[2], tv[2], Are, Bi, ALU.mult, rdeps, [b.td[2]])
    tt(kb, e[3], tv[3], Aim, Br, ALU.mult, rdeps, [b.td[3]])
    tt(kb, e[4], out_re, tv[0], tv[1], ALU.subtract, [b.td[0], b.td[1]], [wdep])
    tt(kb, e[5], out_im, tv[2], tv[3], ALU.add, [b.td[2], b.td[3]], [wdep])


def fft_s1(kb, g, cfg, tb, b, cb):
    nq, N1, N2, rows = cfg.nq, cfg.NF, cfg.N2, cfg.rows
    per = 512 // (2 * N1)
    tot = cb * nq
    for i0 in range(0, tot, per):
        cnt = min(per, tot - i0)
        bk = nextbank(g)
        for i in range(i0, i0 + cnt):
            c, q = divmod(i, nq)
            mm(kb, g.psum[bk][:86, (i - i0) * 2 * N1:(i - i0 + 1) * 2 * N1], b.src_b[:rows, c, q * 86:(q + 1) * 86], tb.F1[:rows, :], True, True,
               [b.src_bd, tb.d], [g.pd[bk]])
        copy_op(kb, "act", b.As[:, i0:i0 + cnt, :, :], g.psum[bk][:86, :cnt * 2 * N1].rearrange("p (i r k) -> p i r k", r=2, k=N1), [g.pd[bk]], [b.Asd])


def fft_s2(kb, g, cfg, tb, b, cb):
    nq, N1, N2, rows = cfg.nq, cfg.NF, cfg.N2, cfg.rows
    per = 512 // (2 * N1)
    tot = cb * nq
    Av = b.As[:, :tot, :, :].rearrange("p (c q) r k -> p c q r k", q=nq)
    Abv = b.Ab[:, :tot, :, :].rearrange("p (c q) r k -> p c q r k", q=nq)
    twr = tb.tw[:, :, 0, :].unsqueeze(1).broadcast_to([86, cb, nq, N1])
    twi = tb.tw[:, :, 1, :].unsqueeze(1).broadcast_to([86, cb, nq, N1])
    cmul_batched(kb, cfg, b, 86, (cb, nq, N1), Av[:, :, :, 0, :], Av[:, :, :, 1, :], twr, twi, Abv[:, :, :, 0, :], Abv[:, :, :, 1, :],
                 [b.Asd, tb.d], b.Abd)
    for i0 in range(0, tot, per):
        cnt = min(per, tot - i0)
        bk = nextbank(g)
        for i in range(i0, i0 + cnt):
            c, p = divmod(i, nq)
            reg = g.psum[bk][:86, (i - i0) * 2 * N1:(i - i0 + 1) * 2 * N1]
            for q in range(nq):
                blk = slice(p * 86, (p + 1) * 86)
                mm(kb, reg, tb.F2[:, 0, q, blk], b.Ab[:, c * nq + q, :, :].rearrange("p r k -> p (r k)"), q == 0, False, [tb.d, b.Abd], [g.pd[bk]])
                mm(kb, reg[:, 0:N1], tb.F2[:, 2, q, blk], b.Ab[:, c * nq + q, 1, :], False, False, [tb.d, b.Abd], [g.pd[bk]])
                mm(kb, reg[:, N1:2 * N1], tb.F2[:, 1, q, blk], b.Ab[:, c * nq + q, 0, :], False, q == nq - 1, [tb.d, b.Abd], [g.pd[bk]])
        copy_op(kb, "act", b.Xs[:, i0:i0 + cnt, :, :], g.psum[bk][:86, :cnt * 2 * N1].rearrange("p (i r k) -> p i r k", r=2, k=N1), [g.pd[bk]], [b.Xsd])


def fft_fwd(kb, g, cfg, tb, b, cb):
    fft_s1(kb, g, cfg, tb, b, cb)
    fft_s2(kb, g, cfg, tb, b, cb)


def pipeline2(items, stage_a, stage_b, depth=2):
    if depth < 2:
        for it in items:
            stage_a(it)
            stage_b(it)
        return
    prev = None
    for it in items:
        stage_a(it)
        if prev is not None:
            stage_b(prev)
        prev = it
    if prev is not None:
        stage_b(prev)


def fft_layout_dma(kb, q, cfg, tile, tiled, dram2d, c0, cb, to_sbuf):
    L, N2, rows = cfg.L, cfg.N2, cfg.rows
    full = L // N2
    rem = L - full * N2
    dv = dram2d[c0:c0 + cb, 0:full * N2].rearrange("c (a b) -> a c b", b=N2)
    if to_sbuf:
        kb.dma(q, tile[:full, :cb, :], dv, writes=[tiled])
        if rem:
            kb.dma(q, tile[full:full + 1, :cb, :rem], dram2d[c0:c0 + cb, full * N2:L].unsqueeze(0), writes=[tiled])
    else:
        kb.dma(q, dv, tile[:full, :cb, :], reads=[tiled])
        if rem:
            kb.dma(q, dram2d[c0:c0 + cb, full * N2:L].unsqueeze(0), tile[full:full + 1, :cb, :rem], reads=[tiled])


def phase_hy_conv(kb, g, cfg, tabs, taps, Hs, seqs, dskip):
    CB, nq, N1, N2, rows, L = cfg.CB, cfg.nq, cfg.NF, cfg.N2, cfg.rows, cfg.L
    with ExitStack() as st:
        tb = fft_load_tables(kb, st, cfg, tabs)
        bs = [fft_alloc(kb, st, cfg, nsets=1)]
        for b in bs:
            kb.op("pool", lambda e, b=b: e.memset(b.src_f[:, :, :], 0.0), [], [b.src_fd])
        X0 = kb.sb(st, [86, CB * nq, 2, N1], F32, "X0")
        X0d = Dep()
        Hb = [kb.sb(st, [86, CB * nq, 2, N1], F32, "Hb") for _ in range(len(bs))]
        Hbd = [Dep() for _ in range(len(bs))]
        items = [(c0, d, bs[i % len(bs)]) for i, (c0, d) in enumerate((c0, d) for c0 in range(0, 64, CB) for d in range(2))]

        def sp_a(it):
            c0, d, b = it
            fft_layout_dma(kb, "sp", cfg, b.src_f, b.src_fd, taps[d], c0, CB, True)
            copy_op(kb, "dve", b.src_b[:, :, :], b.src_f[:, :, :], [b.src_fd], [b.src_bd])
            fft_s1(kb, g, cfg, tb, b, CB)

        def sp_b(it):
            c0, d, b = it
            fft_s2(kb, g, cfg, tb, b, CB)
            if d == 0:
                copy_op(kb, "act", X0[:, :, :, :], b.Xs[:, :, :, :], [b.Xsd], [X0d])
            else:
                tt(kb, "dve", Hb[0][:, :, 0, :], X0[:, :, 0, :], b.Xs[:, :, 0, :], ALU.add, [X0d, b.Xsd], [Hbd[0]])
                tt(kb, "dve", Hb[0][:, :, 1, :], X0[:, :, 1, :], b.Xs[:, :, 1, :], ALU.subtract, [X0d, b.Xsd], [Hbd[0]])
                kb.dma("pool", Hs[:, c0 * nq:(c0 + CB) * nq, :, :], Hb[0][:, :, :, :], reads=[Hbd[0]])
        pipeline2(items, sp_a, sp_b, depth=len(bs))
        kb.barrier()
        Yb = kb.sb(st, [86, CB * nq, 2, N1], BF16, "Yb")
        Ybd = Dep()
        Bs = kb.sb(st, [N1, CB, 2, N2], F32, "Bs")
        Bsd = Dep()
        Bb = kb.sb(st, [N1, CB, 2, N2], BF16, "Bb")
        Bbd = Dep()
        x0f = [kb.sb(st, [rows, CB, N2], F32, "x0f") for _ in range(len(bs))]
        x0d = [Dep() for _ in range(len(bs))]
        cv = kb.sb(st, [rows, CB, N2], F32, "cv")
        cvd = Dep()
        yo = kb.sb(st, [rows, CB, N2], BF16, "yo")
        yod = Dep()
        dsk = kb.sb(st, [128, 64], F32, "dsk")
        dskd = Dep()
        kb.dma("sp", dsk[:, :], dskip.broadcast_to([128, 64]), writes=[dskd])
        perb = 512 // N2
        citems = [(sq, c0, i % len(bs)) for i, (sq, c0) in enumerate((sq, c0) for sq in seqs for c0 in range(0, 64, CB))]

        def cv_a(it):
            sq, c0, k = it
            b = bs[k]
            fft_layout_dma(kb, "sp", cfg, b.src_f, b.src_fd, sq["z"], c0, CB, True)
            fft_layout_dma(kb, "sp", cfg, x0f[k], x0d[k], sq["x0"], c0, CB, True)
            kb.dma("sp", Hb[k][:, :, :, :], Hs[:, c0 * nq:(c0 + CB) * nq, :, :], writes=[Hbd[k]])
            copy_op(kb, "dve", b.src_b[:, :, :], b.src_f[:, :, :], [b.src_fd], [b.src_bd])
            fft_s1(kb, g, cfg, tb, b, CB)

        def cv_b(it):
            sq, c0, k = it
            b = bs[k]
            fft_s2(kb, g, cfg, tb, b, CB)
            cmul_batched(kb, cfg, b, 86, (CB * nq, N1), b.Xs[:, :, 0, :], b.Xs[:, :, 1, :], Hb[k][:, :, 0, :], Hb[k][:, :, 1, :],
                         Yb[:, :, 0, :], Yb[:, :, 1, :], [b.Xsd, Hbd[k]], Ybd)
            tot = CB * 2
            for i0 in range(0, tot, perb):
                cnt = min(perb, tot - i0)
                bk = nextbank(g)
                for i in range(i0, i0 + cnt):
                    c, ri = divmod(i, 2)
                    reg = g.psum[bk][:N1, (i - i0) * N2:(i - i0 + 1) * N2]
                    for p in range(nq):
                        ya_re, ya_im = Yb[:, c * nq + p, 0, :], Yb[:, c * nq + p, 1, :]
                        if ri == 0:
                            mm(kb, reg, ya_re, tb.F2[:, 0, p, :], p == 0, False, [Ybd, tb.d], [g.pd[bk]])
                            mm(kb, reg, ya_im, tb.F2[:, 1, p, :], False, p == nq - 1, [Ybd, tb.d], [g.pd[bk]])
                        else:
                            mm(kb, reg, ya_re, tb.F2[:, 2, p, :], p == 0, False, [Ybd, tb.d], [g.pd[bk]])
                            mm(kb, reg, ya_im, tb.F2[:, 0, p, :], False, p == nq - 1, [Ybd, tb.d], [g.pd[bk]])
                copy_op(kb, "act", Bs[:, :, :, :].rearrange("p c r n -> p (c r) n")[:, i0:i0 + cnt, :],
                        g.psum[bk][:N1, :cnt * N2].rearrange("p (i n) -> p i n", n=N2), [g.pd[bk]], [Bsd])
            twr = tb.twc[:, 0, :].unsqueeze(1).broadcast_to([N1, CB, N2])
            twi = tb.twc[:, 1, :].unsqueeze(1).broadcast_to([N1, CB, N2])
            cmul_batched(kb, cfg, b, N1, (CB, N2), Bs[:, :, 0, :], Bs[:, :, 1, :], twr, twi, Bb[:, :, 0, :], Bb[:, :, 1, :], [Bsd, tb.d], Bbd)
            for i0 in range(0, CB, perb):
                cnt = min(perb, CB - i0)
                bk = nextbank(g)
                for c in range(i0, i0 + cnt):
                    reg = g.psum[bk][:rows, (c - i0) * N2:(c - i0 + 1) * N2]
                    mm(kb, reg, tb.G1[:, 0, :], Bb[:, c, 0, :], True, False, [tb.d, Bbd], [g.pd[bk]])
                    mm(kb, reg, tb.G1[:, 1, :], Bb[:, c, 1, :], False, True, [tb.d, Bbd], [g.pd[bk]])
                copy_op(kb, "act", cv[:, i0:i0 + cnt, :], g.psum[bk][:rows, :cnt * N2].rearrange("p (i n) -> p i n", n=N2), [g.pd[bk]], [cvd])
            tt(kb, "dve", b.src_f[:, :, :], b.src_f[:, :, :], dsk[:rows, c0:c0 + CB].unsqueeze(2).broadcast_to([rows, CB, N2]), ALU.mult,
               [b.src_fd, dskd], [b.src_fd])
            tt(kb, "dve", cv[:, :, :], cv[:, :, :], b.src_f[:, :, :], ALU.add, [cvd, b.src_fd], [cvd])
            tt(kb, "dve", yo[:, :, :], cv[:, :, :], x0f[k][:, :, :], ALU.mult, [cvd, x0d[k]], [yod])
            fft_layout_dma(kb, "pool", cfg, yo, yod, sq["ya"], c0, CB, False)
        pipeline2(citems, cv_a, cv_b, depth=len(bs))
        kb.barrier()


def phase_hy_inproj(kb, g, seqs, w_hy, brow, hcols, G=1):
    with ExitStack() as st:
        wb = kb.sb(st, [128, 8, G * 192], BF16, "why")
        Wd = Dep()
        load_weight_bf16(kb, st, wb, Wd, w_hy, 8, G * 192, stage_cols=1536)
        hc = kb.sb(st, [64, G * 12], F32, "hc")
        hcd = Dep()
        kb.dma("sp", hc[:, :], hcols, writes=[hcd])
        brf = kb.sb(st, [1, G * 192], F32, "brf")
        brb = kb.sb(st, [1, G * 192], BF16, "brb")
        brd = Dep()
        kb.dma("sp", brf[:, :], brow, writes=[brd])
        copy_op(kb, "dve", brb[:, :], brf[:, :], [brd], [brd])
        xin = [kb.sb(st, [128, D], F32, "xin") for _ in range(4)]
        xind = [Dep() for _ in range(4)]
        xT = [kb.sb(st, [128, 8, 512], BF16, "xT") for _ in range(2)]
        xTd = [Dep() for _ in range(2)]
        vf = [kb.sb(st, [1, 512], F32, "vf") for _ in range(2)]
        vb = [kb.sb(st, [1, 512], BF16, "vb") for _ in range(2)]
        vd = [Dep() for _ in range(2)]
        o3 = [[kb.sb(st, [64, 512], F32, "o3") for _ in range(3)] for _ in range(2)]
        o3d = [[Dep() for _ in range(3)] for _ in range(2)]
        bi = 0
        oi = 0
        for sq in seqs:
            L = sq["L"]
            for t0 in range(0, L, 510):
                no = min(510, L - t0)
                ni = no + 2
                j = bi % 2
                bi += 1
                for ti, r0 in enumerate(range(0, ni, 128)):
                    n = min(128, ni - r0)
                    kb.dma("sp", xin[ti][:n, :], sq["xh"][t0 + r0:t0 + r0 + n, :], writes=[xind[ti]])
                    transpose_tile(kb, g, xin[ti], xind[ti], n, xT[j], xTd[j], r0)
                kb.dma("sp", vf[j][:, :ni], sq["valid"][:, t0:t0 + ni], writes=[vd[j]])
                copy_op(kb, "dve", vb[j][:, :ni], vf[j][:, :ni], [vd[j]], [vd[j]])
                for gg in range(G):
                    oj = oi % 2
                    oi += 1
                    for gi in range(3):
                        c0 = gg * 192 + gi * 64
                        h0 = gg * 12 + gi * 4
                        bk = nextbank(g)
                        for k in range(8):
                            mm(kb, g.psum[bk][:64, :ni], wb[:, k, c0:c0 + 64], xT[j][:, k, :ni], k == 0, False, [Wd, xTd[j]], [g.pd[bk]])
                        mm(kb, g.psum[bk][:64, :ni], brb[:, c0:c0 + 64], vb[j][:, :ni], False, True, [brd, vd[j]], [g.pd[bk]])
                        o = o3[oj][gi]
                        od = o3d[oj][gi]
                        act(kb, o[:, :no], g.psum[bk][:64, 1:1 + no], AF.Identity, [g.pd[bk], hcd], [od],
                            scale=hc[:, h0 + 1:h0 + 2], bias=hc[:, h0 + 3:h0 + 4])
                        stt(kb, "dve", o[:, :no], g.psum[bk][:64, 0:no], hc[:, h0:h0 + 1], o[:, :no], ALU.mult, ALU.add, [g.pd[bk], hcd, od], [od])
                        stt(kb, "dve", o[:, :no], g.psum[bk][:64, 2:2 + no], hc[:, h0 + 2:h0 + 3], o[:, :no], ALU.mult, ALU.add, [g.pd[bk], hcd, od], [od])
                    tt(kb, "pool", o3[oj][1][:, :no], o3[oj][1][:, :no], o3[oj][2][:, :no], ALU.mult, [o3d[oj][1], o3d[oj][2]], [o3d[oj][1]])
                    kb.dma("pool", sq["x0"][gg][:, t0:t0 + no], o3[oj][0][:, :no], reads=[o3d[oj][0]])
                    kb.dma("pool", sq["z"][gg][:, t0:t0 + no], o3[oj][1][:, :no], reads=[o3d[oj][1]])
        kb.barrier()


def sin_reduced(kb, out, outd, src_ps, fcol, fbcol, tmps, tmpd, ki, kid, n, reads):
    a, r = tmps
    ts(kb, "dve", a[:, :n], src_ps, fcol, fbcol, ALU.mult, ALU.add, reads, [tmpd[0]])
    ts(kb, "dve", r[:, :n], a[:, :n], 1.0 / TWO_PI, None, ALU.mult, None, [tmpd[0]], [tmpd[1]])
    copy_op(kb, "dve", ki[:, :n], r[:, :n], [tmpd[1]], [kid])
    copy_op(kb, "pool", r[:, :n], ki[:, :n], [kid], [tmpd[1]])
    stt(kb, "dve", r[:, :n], r[:, :n], -TWO_PI, a[:, :n], ALU.mult, ALU.add, [tmpd[0], tmpd[1]], [tmpd[1]])
    ts(kb, "dve", r[:, :n], r[:, :n], -3.1415925, 3.1415925, ALU.max, ALU.min, [tmpd[1]], [tmpd[1]])
    return act(kb, out, r[:, :n], AF.Sin, [tmpd[1]], [outd])


def phase_hy_filters(kb, g, L, zposT, fw, taps_out):
    with ExitStack() as st:
        w1 = kb.sb(st, [33, 2, 64], F32, "fw1")
        w2 = kb.sb(st, [64, 2, 64], F32, "fw2")
        w3 = kb.sb(st, [64, 2, 64], F32, "fw3")
        fc = kb.sb(st, [64, 2, 8], F32, "fc")
        Wd = Dep()
        kb.dma("sp", w1[:, :, :], fw["w1"].rearrange("d e f -> e d f"), writes=[Wd])
        kb.dma("sp", w2[:, :, :], fw["w2"].rearrange("d e f -> e d f"), writes=[Wd])
        kb.dma("sp", w3[:, :, :], fw["w3"].rearrange("d e f -> e d f"), writes=[Wd])
        kb.dma("sp", fc[:, :, 0:5], fw["fcols"], writes=[Wd])
        tt(kb, "dve", fc[:, :, 5:6], fc[:, :, 0:1], fc[:, :, 1:2], ALU.mult, [Wd], [Wd])
        tt(kb, "dve", fc[:, :, 6:7], fc[:, :, 2:3], fc[:, :, 3:4], ALU.mult, [Wd], [Wd])
        ts(kb, "dve", fc[:, :, 7:8], fc[:, :, 4:5], -1.0, None, ALU.mult, None, [Wd], [Wd])
        taps = kb.sb(st, [64, 2, L], F32, "taps")
        tapsd = Dep()
        zp = [kb.sb(st, [33, 512], F32, "zp") for _ in range(2)]
        zpd = [Dep() for _ in range(2)]
        tb_ = [kb.sb(st, [64, 512], F32, "tbc") for _ in range(2)]
        tbd = [Dep() for _ in range(2)]
        tmps = [kb.sb(st, [64, 512], F32, "ftmp") for _ in range(2)]
        tmpd = [Dep(), Dep()]
        ki = kb.sb(st, [64, 512], I32, "ki")
        kid = Dep()
        h1 = kb.sb(st, [64, 512], F32, "h1")
        h1d = Dep()
        h2 = kb.sb(st, [64, 512], F32, "h2")
        h2d = Dep()
        ex = kb.sb(st, [64, 512], F32, "ex")
        exd = Dep()
        ss = kb.sb(st, [64, 2 * ((L + 511) // 512) + 4], F32, "ss")
        ssd = Dep()
        junk = kb.sb(st, [64, 512], F32, "fjunk")
        junkd = Dep()
        nb = (L + 511) // 512
        for bi, l0 in enumerate(range(0, L, 512)):
            n = min(512, L - l0)
            j = bi % 2
            kb.dma("sp", zp[j][:, :n], zposT[:, l0:l0 + n], writes=[zpd[j]])
            kb.dma("pool", tb_[j][:, :n], zposT[0:1, l0:l0 + n].broadcast_to([64, n]), writes=[tbd[j]])
            for d in range(2):
                bk = nextbank(g)
                mm(kb, g.psum[bk][:64, :n], w1[:, d, :], zp[j][:, :n], True, True, [Wd, zpd[j]], [g.pd[bk]])
                sin_reduced(kb, h1[:, :n], h1d, g.psum[bk][:64, :n], fc[:, d, 0:1], fc[:, d, 5:6], tmps, tmpd, ki, kid, n, [g.pd[bk], Wd])
                bk = nextbank(g)
                mm(kb, g.psum[bk][:64, :n], w2[:, d, :], h1[:, :n], True, True, [Wd, h1d], [g.pd[bk]])
                sin_reduced(kb, h2[:, :n], h2d, g.psum[bk][:64, :n], fc[:, d, 2:3], fc[:, d, 6:7], tmps, tmpd, ki, kid, n, [g.pd[bk], Wd])
                bk = nextbank(g)
                mm(kb, g.psum[bk][:64, :n], w3[:, d, :], h2[:, :n], True, True, [Wd, h2d], [g.pd[bk]])
                act(kb, ex[:, :n], tb_[j][:, :n], AF.Exp, [tbd[j], Wd], [exd], scale=fc[:, d, 7:8])
                tt(kb, "dve", taps[:, d, l0:l0 + n], g.psum[bk][:64, :n], ex[:, :n], ALU.mult, [g.pd[bk], exd], [tapsd])
                if d == 1 and l0 == 0:
                    kb.op("pool", lambda e: e.memset(taps[:, 1, 0:1], 0.0), [], [tapsd])
                act(kb, junk[:, :n], taps[:, d, l0:l0 + n], AF.Square, [tapsd], [junkd, ssd], accum_out=ss[:, 2 * bi + d:2 * bi + d + 1])
        tot, nrm = ss[:, 2 * nb:2 * nb + 1], ss[:, 2 * nb + 1:2 * nb + 2]
        kb.op("dve", lambda e: e.tensor_reduce(out=tot, in_=ss[:, 0:2 * nb], axis=AX.X, op=ALU.add), [ssd], [ssd])
        act(kb, nrm, tot, AF.Sqrt, [ssd], [ssd])
        kb.op("dve", lambda e: e.reciprocal(out=nrm, in_=nrm), [ssd], [ssd])
        for d in range(2):
            for l0 in range(0, L, 4096):
                n = min(4096, L - l0)
                ts(kb, ("dve", "pool")[d], taps[:, d, l0:l0 + n], taps[:, d, l0:l0 + n], nrm, None, ALU.mult, None, [tapsd, ssd], [tapsd])
            kb.dma("sp", taps_out[d], taps[:, d, :], reads=[tapsd])
        kb.barrier()


def phase_hy_filter_h2(kb, g, L, zposT, fw, h2_out):
    with ExitStack() as st:
        w1 = kb.sb(st, [33, 2, 64], F32, "fw1")
        w2 = kb.sb(st, [64, 2, 64], F32, "fw2")
        fc = kb.sb(st, [64, 2, 8], F32, "fc")
        Wd = Dep()
        kb.dma("sp", w1[:, :, :], fw["w1"].rearrange("d e f -> e d f"), writes=[Wd])
        kb.dma("sp", w2[:, :, :], fw["w2"].rearrange("d e f -> e d f"), writes=[Wd])
        kb.dma("sp", fc[:, :, 0:5], fw["fcols"], writes=[Wd])
        tt(kb, "dve", fc[:, :, 5:6], fc[:, :, 0:1], fc[:, :, 1:2], ALU.mult, [Wd], [Wd])
        tt(kb, "dve", fc[:, :, 6:7], fc[:, :, 2:3], fc[:, :, 3:4], ALU.mult, [Wd], [Wd])
        zp = [kb.sb(st, [33, 512], F32, "zp") for _ in range(2)]
        zpd = [Dep() for _ in range(2)]
        NQ = 3
        tmps = [[kb.sb(st, [64, 512], F32, "ftmp") for _ in range(2)] for _ in range(NQ)]
        tmpd = [[Dep(), Dep()] for _ in range(NQ)]
        ki = [kb.sb(st, [64, 512], I32, "ki") for _ in range(NQ)]
        kid = [Dep() for _ in range(NQ)]
        h1 = [kb.sb(st, [64, 512], F32, "h1") for _ in range(NQ)]
        h1d = [Dep() for _ in range(NQ)]
        h2 = [kb.sb(st, [64, 512], F32, "h2") for _ in range(NQ)]
        h2d = [Dep() for _ in range(NQ)]
        it = 0
        for bi, l0 in enumerate(range(0, L, 512)):
            n = min(512, L - l0)
            j = bi % 2
            kb.dma("sp", zp[j][:, :n], zposT[:, l0:l0 + n], writes=[zpd[j]])
            for d in range(2):
                q = it % NQ
                it += 1
                bk = nextbank(g)
                mm(kb, g.psum[bk][:64, :n], w1[:, d, :], zp[j][:, :n], True, True, [Wd, zpd[j]], [g.pd[bk]])
                sin_reduced(kb, h1[q][:, :n], h1d[q], g.psum[bk][:64, :n], fc[:, d, 0:1], fc[:, d, 5:6], tmps[q], tmpd[q], ki[q], kid[q], n, [g.pd[bk], Wd])
                bk = nextbank(g)
                mm(kb, g.psum[bk][:64, :n], w2[:, d, :], h1[q][:, :n], True, True, [Wd, h1d[q]], [g.pd[bk]])
                sin_reduced(kb, h2[q][:, :n], h2d[q], g.psum[bk][:64, :n], fc[:, d, 2:3], fc[:, d, 6:7], tmps[q], tmpd[q], ki[q], kid[q], n, [g.pd[bk], Wd])
                kb.dma("pool", h2_out[d][:, l0:l0 + n], h2[q][:, :n], reads=[h2d[q]])
        kb.barrier()


def phase_hy_filter_taps(kb, g, L, zposT, h2_in, w3_ap, fcols_ap, taps_out):
    with ExitStack() as st:
        w3 = kb.sb(st, [64, 2, 64], F32, "fw3")
        fc = kb.sb(st, [64, 2, 8], F32, "fc")
        Wd = Dep()
        kb.dma("sp", w3[:, :, :], w3_ap.rearrange("d e f -> e d f"), writes=[Wd])
        kb.dma("sp", fc[:, :, 0:5], fcols_ap, writes=[Wd])
        ts(kb, "dve", fc[:, :, 7:8], fc[:, :, 4:5], -1.0, None, ALU.mult, None, [Wd], [Wd])
        taps = kb.sb(st, [64, 2, L], F32, "taps")
        tapsd = [Dep(), Dep()]
        NQ = 3
        tb_ = [kb.sb(st, [64, 512], F32, "tbc") for _ in range(2)]
        tbd = [Dep() for _ in range(2)]
        hin = [kb.sb(st, [64, 512], F32, "h2in") for _ in range(NQ)]
        hind = [Dep() for _ in range(NQ)]
        ex = [kb.sb(st, [64, 512], F32, "ex") for _ in range(NQ)]
        exd = [Dep() for _ in range(NQ)]
        junk = [kb.sb(st, [64, 512], F32, "fjunk") for _ in range(2)]
        junkd = [Dep(), Dep()]
        nb = (L + 511) // 512
        ss = kb.sb(st, [64, 2 * nb + 4], F32, "ss")
        ssd = Dep()
        it = 0
        for bi, l0 in enumerate(range(0, L, 512)):
            n = min(512, L - l0)
            j = bi % 2
            kb.dma("pool", tb_[j][:, :n], zposT[0:1, l0:l0 + n].broadcast_to([64, n]), writes=[tbd[j]])
            for d in range(2):
                q = it % NQ
                it += 1
                kb.dma("sp", hin[q][:, :n], h2_in[d][:, l0:l0 + n], writes=[hind[q]])
                bk = nextbank(g)
                mm(kb, g.psum[bk][:64, :n], w3[:, d, :], hin[q][:, :n], True, True, [Wd, hind[q]], [g.pd[bk]])
                act(kb, ex[q][:, :n], tb_[j][:, :n], AF.Exp, [tbd[j], Wd], [exd[q]], scale=fc[:, d, 7:8])
                tt(kb, "dve", taps[:, d, l0:l0 + n], g.psum[bk][:64, :n], ex[q][:, :n], ALU.mult, [g.pd[bk], exd[q]], [tapsd[d]])
                if d == 1 and l0 == 0:
                    kb.op("pool", lambda e: e.memset(taps[:, 1, 0:1], 0.0), [], [tapsd[d]])
                act(kb, junk[d][:, :n], taps[:, d, l0:l0 + n], AF.Square, [tapsd[d]], [junkd[d], ssd], accum_out=ss[:, 2 * bi + d:2 * bi + d + 1])
        tot, nrm = ss[:, 2 * nb:2 * nb + 1], ss[:, 2 * nb + 1:2 * nb + 2]
        kb.op("dve", lambda e: e.tensor_reduce(out=tot, in_=ss[:, 0:2 * nb], axis=AX.X, op=ALU.add), [ssd], [ssd])
        act(kb, nrm, tot, AF.Sqrt, [ssd], [ssd])
        kb.op("dve", lambda e: e.reciprocal(out=nrm, in_=nrm), [ssd], [ssd])
        for d in range(2):
            for l0 in range(0, L, 4096):
                n = min(4096, L - l0)
                ts(kb, "dve", taps[:, d, l0:l0 + n], taps[:, d, l0:l0 + n], nrm, None, ALU.mult, None, [tapsd[d], ssd], [tapsd[d]])
            kb.dma(("sp", "pool")[d], taps_out[d], taps[:, d, :], reads=[tapsd[d]])
        kb.barrier()


LP, LS = 16400, 2064
NCORES = 8
BF = ml_dtypes.bfloat16


class Prog:
    def __init__(self):
        self.nc = bass.Bass("TRN2", target_bir_lowering=False)
        self.kb = KB(self.nc)
        self.ins = {}

    def din(self, name, shape, dt=F32):
        self.ins[name] = (tuple(shape), dt)
        return self.nc.dram_tensor(name, list(shape), dt, kind="ExternalInput").ap()

    def dout(self, name, shape, dt=F32):
        return self.nc.dram_tensor(name, list(shape), dt, kind="ExternalOutput").ap()

    def scr(self, name, shape, dt=F32):
        return self.nc.dram_tensor(name, list(shape), dt).ap()


def chunk_tiles():
    return [(t0, 128, 61 + t0) for t0 in range(0, 2048, 128)] + [(2048, 16, 15)]


def declare_tabs(P, cfg, pre):
    t = fft_tables(cfg)
    return {k: P.din(pre + k, v.shape, F32 if v.dtype == np.float32 else BF16) for k, v in t.items()}, {pre + k: v for k, v in t.items()}


def build_l1():
    P = Prog()
    kb = P.kb
    ident = P.din("ident", [128, 128])
    xh_p = P.din("xh_p", [LP + 2, D])
    xh_s = P.din("xh_s", [LS + 2, D])
    valid_p = P.din("valid_p", [1, LP + 2])
    valid_s = P.din("valid_s", [1, LS + 2])
    zpos_p = P.din("zpos_p", [33, LP])
    zpos_s = P.din("zpos_s", [33, LS])
    tabsP, _ = declare_tabs(P, CFG_P, "tp_")
    tabsS, _ = declare_tabs(P, CFG_S, "ts_")
    fw1 = P.din("fw1", [2, 33, 64])
    fw2 = P.din("fw2", [2, 64, 64])
    fw3 = P.din("fw3", [9, 2, 64, 64])
    fcols = P.din("fcols", [9, 64, 2, 5])
    why = P.din("why", [9, D, 192])
    brow = P.din("brow", [9, 1, 192])
    hcols = P.din("hcols", [9, 64, 12])
    dskip = P.din("dskip", [9, 1, 64])
    xc = P.din("xc", [2, XC, D])
    mask = P.din("mask", [2, 1, XC])
    wconf = P.din("wconf", [D, 1024])
    ccols = P.din("ccols", [128, 144])
    yaP = P.dout("yaP", [64, LP], BF16)
    yaS = P.dout("yaS", [8, 64, LS], BF16)
    ybT = P.dout("ybT", [2, 512, LS], BF16)
    taps_p = P.scr("taps_p", [2, 64, LP])
    Hs_p = P.scr("Hs_p", [86, 64 * CFG_P.nq, 2, CFG_P.N1])
    z_p = P.scr("z_p", [64, LP])
    x0_p = P.scr("x0_p", [64, LP])
    taps_s = P.scr("taps_s", [8, 2, 64, LS])
    Hs_s = P.scr("Hs_s", [8, 86, 64 * CFG_S.nq, 2, CFG_S.N1])
    z_s = P.scr("z_s", [8, 64, LS])
    x0_s = P.scr("x0_s", [8, 64, LS])
    with ExitStack() as st:
        g = setup_globals(kb, st)
        load_ident(kb, g, ident)
        fwd = lambda i: dict(w1=fw1, w2=fw2, w3=fw3[i], fcols=fcols[i])
        phase_hy_filters(kb, g, LP, zpos_p, fwd(0), taps_p)
        phase_hy_inproj(kb, g, [dict(xh=xh_p, valid=valid_p, L=LP, z=[z_p], x0=[x0_p])], why[0], brow[0], hcols[0])
        phase_hy_conv(kb, g, CFG_P, tabsP, taps_p, Hs_p, [dict(z=z_p, x0=x0_p, ya=yaP)], dskip[0])
        for gi in range(8):
            phase_hy_filters(kb, g, LS, zpos_s, fwd(1 + gi), taps_s[gi])
            phase_hy_inproj(kb, g, [dict(xh=xh_s, valid=valid_s, L=LS, z=[z_s[gi]], x0=[x0_s[gi]])], why[1 + gi], brow[1 + gi], hcols[1 + gi])
            phase_hy_conv(kb, g, CFG_S, tabsS, taps_s[gi], Hs_s[gi], [dict(z=z_s[gi], x0=x0_s[gi], ya=yaS[gi])], dskip[1 + gi])

        def outf(s_):
            def f(j, bi, cn):
                if bi == 0:
                    return ybT[s_, j * 128:(j + 1) * 128, 2048:2064]
                return ybT[s_, j * 128:(j + 1) * 128, (bi - 1) * 512:bi * 512]
            return f
        phase_conf(kb, g, [dict(x=xc[s_], mask=mask[s_], out=outf(s_)) for s_ in range(2)], wconf, ccols)
        kb.finish_wait()
    return P


def build_l2():
    P = Prog()
    kb = P.kb
    ident = P.din("ident", [128, 128])
    xc = P.din("xc", [2, XC, D])
    ycT = P.din("ycT", [2, D, LS], BF16)
    wout = P.din("wout", [D, D])
    bout = P.din("bout", [1, D])
    ln1g = P.din("ln1g", [1, D]); ln1b = P.din("ln1b", [1, D]); ln2g = P.din("ln2g", [1, D]); ln2b = P.din("ln2b", [1, D])
    w1 = P.din("w1", [D, DFF]); w2 = P.din("w2", [DFF, D])
    wqa = P.din("wqa", [D, 384]); qg = P.din("qg", [1, 384]); WqH = P.din("WqH", [384, NH * 128]); WqS = P.din("WqS", [384, NH * 32])
    wkva = P.din("wkva", [D, 288]); kvg = P.din("kvg", [1, 256])
    cs = P.din("cs", [2, LS, 32]); Cq = P.din("Cq", [2, 32, 2048]); Sq = P.din("Sq", [2, 32, 2048])
    h2 = P.dout("h2", [2, LS, D])
    kvlat = P.dout("kvlat", [2, LS, 288])
    QT = P.dout("QT", [2, NH, 128, 2048], BF16)
    h1 = P.scr("h1", [2, LS, D])
    tl = chunk_tiles()
    with ExitStack() as st:
        g = setup_globals(kb, st)
        load_ident(kb, g, ident)
        ycv = ycT.rearrange("s (k p) t -> s p k t", p=128)
        phase_proj_ln(kb, g, [(xc[s_, xr:xr + n, :], [(slice(0, 8), ycv[s_, :, :, t0:t0 + n], None)], h1[s_, t0:t0 + n, :], n) for s_ in range(2) for t0, n, xr in tl],
                      True, wout, bout, ln1g, ln1b)
        phase_mlp_ln(kb, g, [(h1[s_, t0:t0 + n, :], h2[s_, t0:t0 + n, :], n) for s_ in range(2) for t0, n, xr in tl], w1, w2, ln2g, ln2b)
        seqs = []
        for s_ in range(2):
            seqs.append(dict(tiles=[(h2[s_, t0:t0 + n, :], kvlat[s_, t0:t0 + n, :], cs[s_, t0:t0 + n, :], n) for t0, n, xr in tl],
                             CS=(Cq[s_], Sq[s_]), qt=(lambda s_: (lambda h, q0: QT[s_, h, :, q0:q0 + 512]))(s_)))
        phase_qkv(kb, g, seqs, wqa, qg, WqH, WqS, wkva, kvg)
        kb.finish_wait()
    return P


def build_l3():
    P = Prog()
    kb = P.kb
    ident = P.din("ident", [128, 128])
    h2 = P.din("h2", [2, LS, D])
    kvp = P.din("kvp", [LP, 288])
    kvs = P.din("kvs", [LS, 288])
    QT = P.din("QT", [2, NH, 128, 2048], BF16)
    WkH = P.din("WkH", [256, NH * 128]); WvH = P.din("WvH", [256, NH * 64])
    wo = P.din("wo", [D, D])
    ln1g = P.din("ln1g", [1, D]); ln1b = P.din("ln1b", [1, D]); ln2g = P.din("ln2g", [1, D]); ln2b = P.din("ln2b", [1, D])
    w1 = P.din("w1", [D, DFF]); w2 = P.din("w2", [DFF, D])
    out = P.dout("out", [2, 2048, D])
    otok = P.scr("otok", [2, 2048, D])
    h3 = P.scr("h3", [2, 2048, D])
    with ExitStack() as st:
        g = setup_globals(kb, st)
        load_ident(kb, g, ident)
        seqs = []
        for s_, kv, L in ((0, kvp, LP), (1, kvs, LS)):
            otv = otok[s_].rearrange("(a t p) (h c) -> a p t h c", p=128, t=4, c=64)
            seqs.append(dict(kchunks=[(kv[t0:min(t0 + 128, L), :], min(128, L - t0)) for t0 in range(0, L, 128)],
                             qt=(lambda s_: (lambda h: QT[s_, h, :, :]))(s_),
                             o=(lambda otv: (lambda qsb, half, h: otv[qsb * 2 + half, :, :, h, :]))(otv)))
        phase_attn(kb, g, seqs, WkH, WvH)
        tl2 = [(s_, t0) for s_ in range(2) for t0 in range(0, 2048, 128)]
        phase_proj_ln(kb, g, [(h2[s_, t0:t0 + 128, :], otok[s_, t0:t0 + 128, :], h3[s_, t0:t0 + 128, :], 128) for s_, t0 in tl2], False, wo, None, ln1g, ln1b)
        phase_mlp_ln(kb, g, [(h3[s_, t0:t0 + 128, :], out[s_, t0:t0 + 128, :], 128) for s_, t0 in tl2], w1, w2, ln2g, ln2b)
        kb.finish_wait()
    return P


def zpos_table(L):
    t = np.arange(L, dtype=np.float32) / max(L - 1, 1)
    freqs = np.linspace(1e-4, 15, 16, dtype=np.float32)
    w = (np.float32(2.0 * math.pi) * np.arange(L, dtype=np.float32) / np.float32(L)).astype(np.float32)
    ang = w[:, None] * freqs[None, :]
    return np.ascontiguousarray(np.concatenate([t[:, None], np.cos(ang), -np.sin(ang)], -1).T.astype(np.float32))


def rope_cs(pos):
    inv = (1.0 / (10000.0 ** (np.arange(0, 32, 2, dtype=np.float32) / 32))).astype(np.float32)
    ang = pos.astype(np.float32)[:, None] * inv[None, :]
    return np.cos(ang).astype(np.float32), np.sin(ang).astype(np.float32)


def make_xc(hfull, m0, L):
    x = np.zeros((XC, D), np.float32)
    mk = np.zeros((1, XC), np.float32)
    x[15:46] = hfull[0:31]
    mk[0, 15:46] = 1
    lo, hi = m0 - 15, min(m0 + 2048 + 15, L)
    x[46:46 + (hi - lo)] = hfull[lo:hi]
    mk[0, 46:46 + (hi - lo)] = 1
    return x, mk


def colpack(v):
    return np.ascontiguousarray(v.reshape(4, 128).T)


def check_inputs(P, im):
    for k, (shape, dt) in P.ins.items():
        assert k in im, k
        assert tuple(im[k].shape) == shape, (k, im[k].shape, shape)
    return {k: np.ascontiguousarray(im[k]) for k in P.ins}


def kernel_unfused(x_prompt, x_sample, meta_tokens, ev_w_in, ev_b_in, ev_short_w, ev_short_b,
           hy_w1, hy_b1, hy_freq1, hy_w2, hy_b2, hy_freq2, hy_w3, hy_decay, hy_skip_d,
           cf_dw_w, cf_dw_b, cf_ln_g, cf_ln_b, ev_w_out, ev_b_out,
           mla_wq_a, mla_q_norm, mla_wq_b, mla_wkv_a, mla_kv_norm, mla_wkv_b, mla_wo,
           ln1_g, ln1_b, mlp_w1, mlp_w2, ln2_g, ln2_b):
    f = lambda a: np.asarray(a, dtype=np.float32)
    x_prompt, x_sample, meta = f(x_prompt), f(x_sample), f(meta_tokens)
    win, bin_, sw, sb = f(ev_w_in)[0], f(ev_b_in)[0], f(ev_short_w)[0], f(ev_short_b)[0]
    ident = np.eye(128, dtype=np.float32)
    hp = np.concatenate([meta, x_prompt[0]], 0)
    hs = [np.concatenate([meta, x_sample[c]], 0) for c in range(8)]
    z1 = np.zeros((1, D), np.float32)
    xh_p = np.concatenate([z1, hp, z1], 0)
    valid_p = np.ones((1, LP + 2), np.float32); valid_p[0, 0] = 0; valid_p[0, -1] = 0
    valid_s = np.ones((1, LS + 2), np.float32); valid_s[0, 0] = 0; valid_s[0, -1] = 0
    tabP, tabS = fft_tables(CFG_P), fft_tables(CFG_S)
    def grp(gi):
        ch = slice(gi * 64, gi * 64 + 64)
        gcols = [np.arange(k * 512 + gi * 64, k * 512 + gi * 64 + 64) for k in range(3)]
        allc = np.concatenate(gcols)
        return dict(fw3=np.ascontiguousarray(f(hy_w3)[0][:, :, ch]),
                    fcols=np.ascontiguousarray(np.stack([f(hy_freq1)[0], f(hy_b1)[0], f(hy_freq2)[0], f(hy_b2)[0], f(hy_decay)[0][:, ch]], -1).transpose(1, 0, 2)),
                    why=np.ascontiguousarray(win[:, allc]), brow=bin_[allc][None, :].copy(),
                    hcols=np.concatenate([np.stack([sw[0, gc], sw[1, gc], sw[2, gc], sb[gc]], 1) for gc in gcols], 1).astype(np.float32),
                    dskip=f(hy_skip_d)[0][ch][None, :].copy())
    G = [grp(gi) for gi in range(8)]
    ccols = np.concatenate([colpack(bin_[1536:2048]), colpack(bin_[2048:2560]), colpack(f(cf_dw_b)[0]), colpack(f(cf_ln_g)[0]), colpack(f(cf_ln_b)[0]),
                            np.ascontiguousarray(f(cf_dw_w)[0].T.reshape(4, 128, 31).transpose(1, 0, 2).reshape(128, 124))], 1).astype(np.float32)
    xcs, masks = [], []
    for c in range(8):
        a, ma = make_xc(hp, 16 + 2048 * c, LP)
        b, mb = make_xc(hs[c], 16, LS)
        xcs.append(np.stack([a, b], 0))
        masks.append(np.stack([ma, mb], 0))
    P1 = build_l1()
    ims = []
    for c in range(8):
        order = [c] + list(range(8))
        im = dict(ident=ident, xh_p=xh_p, xh_s=np.concatenate([z1, hs[c], z1], 0), valid_p=valid_p, valid_s=valid_s,
                  zpos_p=zpos_table(LP), zpos_s=zpos_table(LS), fw1=f(hy_w1)[0], fw2=f(hy_w2)[0],
                  fw3=np.stack([G[i]["fw3"] for i in order], 0), fcols=np.stack([G[i]["fcols"] for i in order], 0).astype(np.float32),
                  why=np.stack([G[i]["why"] for i in order], 0), brow=np.stack([G[i]["brow"] for i in order], 0),
                  hcols=np.stack([G[i]["hcols"] for i in order], 0), dskip=np.stack([G[i]["dskip"] for i in order], 0),
                  xc=xcs[c], mask=masks[c], wconf=np.ascontiguousarray(win[:, 1536:2560]), ccols=ccols)
        for k, v in tabP.items():
            im["tp_" + k] = v
        for k, v in tabS.items():
            im["ts_" + k] = v
        ims.append(check_inputs(P1, im))
    r1 = run_bass_kernel_spmd(P1.nc, ims, core_ids=list(range(8))).results
    yaP_all = np.concatenate([np.asarray(r1[c]["yaP"]) for c in range(8)], 0)
    P2 = build_l2()
    wqb = f(mla_wq_b)[0].reshape(384, NH, 96)
    WqH = np.concatenate([wqb[:, :, 64:96], np.zeros((384, NH, 32), np.float32), wqb[:, :, 0:64]], -1).reshape(384, NH * 128)
    WqS = np.concatenate([wqb[:, :, 80:96], wqb[:, :, 64:80]], -1).reshape(384, NH * 32)
    wkvb = f(mla_wkv_b)[0].reshape(256, NH, 128)
    WkH = np.concatenate([np.zeros((256, NH, 64), np.float32), wkvb[:, :, 0:64]], -1).reshape(256, NH * 128)
    WvH = np.ascontiguousarray(wkvb[:, :, 64:128]).reshape(256, NH * 64)
    ims = []
    for c in range(8):
        m0 = 16 + 2048 * c
        ya_p = np.concatenate([yaP_all[:, m0:m0 + 2048], yaP_all[:, 0:16]], 1)
        ya_s = np.asarray(r1[c]["yaS"]).reshape(512, LS)
        ya_s = np.concatenate([ya_s[:, 16:], ya_s[:, 0:16]], 1)
        yb = np.asarray(r1[c]["ybT"])
        ycT = np.stack([np.concatenate([ya_p, yb[0]], 0), np.concatenate([ya_s, yb[1]], 0)], 0)
        css, Cqs, Sqs = [], [], []
        for pos in (np.concatenate([np.arange(m0, m0 + 2048), np.arange(16)]), np.concatenate([np.arange(16, LS), np.arange(16)])):
            co, si = rope_cs(pos)
            css.append(np.concatenate([co, si], 1))
            Cqs.append(np.concatenate([co[:2048].T, co[:2048].T], 0))
            Sqs.append(np.concatenate([-si[:2048].T, si[:2048].T], 0))
        im = dict(ident=ident, xc=xcs[c], ycT=ycT, wout=f(ev_w_out)[0], bout=f(ev_b_out)[0:1], ln1g=f(ln1_g)[0:1], ln1b=f(ln1_b)[0:1],
                  ln2g=f(ln2_g)[0:1], ln2b=f(ln2_b)[0:1], w1=f(mlp_w1)[0], w2=f(mlp_w2)[0], wqa=f(mla_wq_a)[0], qg=f(mla_q_norm)[0:1],
                  WqH=WqH, WqS=WqS, wkva=f(mla_wkv_a)[0], kvg=f(mla_kv_norm)[0:1], cs=np.stack(css, 0), Cq=np.stack(Cqs, 0), Sq=np.stack(Sqs, 0))
        ims.append(check_inputs(P2, im))
    r2 = run_bass_kernel_spmd(P2.nc, ims, core_ids=list(range(8))).results
    kvp = np.concatenate([np.asarray(r2[c]["kvlat"])[0, :2048] for c in range(8)] + [np.asarray(r2[0]["kvlat"])[0, 2048:]], 0)
    P3 = build_l3()
    ims = []
    for c in range(8):
        im = dict(ident=ident, h2=np.asarray(r2[c]["h2"]), kvp=kvp, kvs=np.asarray(r2[c]["kvlat"])[1], QT=np.asarray(r2[c]["QT"]), WkH=WkH, WvH=WvH,
                  wo=f(mla_wo)[0], ln1g=f(ln1_g)[1:2], ln1b=f(ln1_b)[1:2], ln2g=f(ln2_g)[1:2], ln2b=f(ln2_b)[1:2], w1=f(mlp_w1)[1], w2=f(mlp_w2)[1])
        ims.append(check_inputs(P3, im))
    r3 = run_bass_kernel_spmd(P3.nc, ims, core_ids=list(range(8))).results
    y_prompt = np.concatenate([np.asarray(r3[c]["out"])[0] for c in range(8)], 0)[None].astype(np.float32)
    y_sample = np.stack([np.asarray(r3[c]["out"])[1] for c in range(8)], 0).astype(np.float32)
    return (y_prompt, y_sample)


U32 = mybir.dt.uint32
YAW = 18432


def build_fused(stop=10 ** 9, trace_steps=None):
    P = Prog()
    step = [0]

    def run(fn, *a):
        if step[0] < stop:
            fn(*a)
        step[0] += 1

    kb = P.kb
    nc = P.nc
    ident = P.din("ident", [128, 128])
    xh_p = P.din("xh_p", [LP + 2, D]); xh_s = P.din("xh_s", [LS + 2, D])
    valid_p = P.din("valid_p", [1, LP + 2]); valid_s = P.din("valid_s", [1, LS + 2])
    zpos_p = P.din("zpos_p", [33, LP]); zpos_s = P.din("zpos_s", [33, LS])
    tabsP, _ = declare_tabs(P, CFG_P, "tp_")
    tabsS, _ = declare_tabs(P, CFG_S, "ts_")
    fw1 = P.din("fw1", [2, 33, 64]); fw2 = P.din("fw2", [2, 64, 64])
    fw3 = P.din("fw3", [9, 2, 64, 64]); fcols = P.din("fcols", [9, 64, 2, 5])
    why = P.din("why", [9, D, 192]); brow = P.din("brow", [9, 1, 192]); hcols = P.din("hcols", [9, 64, 12]); dskip = P.din("dskip", [9, 1, 64])
    xc = P.din("xc", [2, XC, D]); mask = P.din("mask", [2, 1, XC])
    wconf = P.din("wconf", [D, 1024]); ccols = P.din("ccols", [128, 144])
    gidx = P.din("gidx", [128, 4], U32)
    wout = P.din("wout", [D, D]); bout = P.din("bout", [1, D])
    ln1g = P.din("ln1g", [2, 1, D]); ln1b = P.din("ln1b", [2, 1, D]); ln2g = P.din("ln2g", [2, 1, D]); ln2b = P.din("ln2b", [2, 1, D])
    w1 = P.din("w1", [2, D, DFF]); w2 = P.din("w2", [2, DFF, D])
    wqa = P.din("wqa", [D, 384]); qg = P.din("qg", [1, 384]); WqH = P.din("WqH", [384, NH * 128]); WqS = P.din("WqS", [384, NH * 32])
    wkva = P.din("wkva", [D, 288]); kvg = P.din("kvg", [1, 256])
    cs = P.din("cs", [2, LS, 32]); Cq = P.din("Cq", [2, 32, 2048]); Sq = P.din("Sq", [2, 32, 2048])
    WkH = P.din("WkH", [256, NH * 128]); WvH = P.din("WvH", [256, NH * 64]); wo = P.din("wo", [D, D])
    out = P.dout("out", [2, 2048, D])
    yaP = P.scr("yaP", [64, YAW], BF16)
    yaP_all = P.scr("yaP_all", [512, YAW], BF16)
    yaS = P.scr("yaS", [8, 64, LS], BF16)
    ybT = P.scr("ybT", [2, 512, LS], BF16)
    taps_p = P.scr("taps_p", [2, 64, LP]); Hs_p = P.scr("Hs_p", [86, 64 * CFG_P.nq, 2, CFG_P.N1])
    z_p = P.scr("z_p", [64, LP]); x0_p = P.scr("x0_p", [64, LP])
    taps_s = P.scr("taps_s", [8, 2, 64, LS]); Hs_s = P.scr("Hs_s", [8, 86, 64 * CFG_S.nq, 2, CFG_S.N1])
    z_s = P.scr("z_s", [8, 64, LS]); x0_s = P.scr("x0_s", [8, 64, LS])
    h1 = P.scr("h1", [2, LS, D]); h2 = P.scr("h2", [2, LS, D])
    kvlat = P.scr("kvlat", [2, LS, 288]); kv_all = P.scr("kv_all", [8 * LS, 288])
    QT = P.scr("QT", [2, NH, 128, 2048], BF16)
    otok = P.scr("otok", [2, 2048, D]); h3 = P.scr("h3", [2, 2048, D])
    tl = chunk_tiles()
    with ExitStack() as st:
        g = setup_globals(kb, st)
        load_ident(kb, g, ident)
        fwd = lambda i: dict(w1=fw1, w2=fw2, w3=fw3[i], fcols=fcols[i])
        run(phase_hy_filters, kb, g, LP, zpos_p, fwd(0), taps_p)
        run(phase_hy_inproj, kb, g, [dict(xh=xh_p, valid=valid_p, L=LP, z=[z_p], x0=[x0_p])], why[0], brow[0], hcols[0])
        run(phase_hy_conv, kb, g, CFG_P, tabsP, taps_p, Hs_p, [dict(z=z_p, x0=x0_p, ya=yaP[:, 2032:2032 + LP])], dskip[0])
        agd = Dep()
        run(lambda: kb.all_gather(yaP, yaP_all, reads=[], writes=[agd]))
        kb.barrier()
        for gi in range(8):
            run(phase_hy_filters, kb, g, LS, zpos_s, fwd(1 + gi), taps_s[gi])
            run(phase_hy_inproj, kb, g, [dict(xh=xh_s, valid=valid_s, L=LS, z=[z_s[gi]], x0=[x0_s[gi]])], why[1 + gi], brow[1 + gi], hcols[1 + gi])
            run(phase_hy_conv, kb, g, CFG_S, tabsS, taps_s[gi], Hs_s[gi], [dict(z=z_s[gi], x0=x0_s[gi], ya=yaS[gi])], dskip[1 + gi])

        def outf(s_):
            def f(j, bi, cn):
                if bi == 0:
                    return ybT[s_, j * 128:(j + 1) * 128, 2048:2064]
                return ybT[s_, j * 128:(j + 1) * 128, (bi - 1) * 512:bi * 512]
            return f
        run(phase_conf, kb, g, [dict(x=xc[s_], mask=mask[s_], out=outf(s_)) for s_ in range(2)], wconf, ccols)
        with ExitStack() as st2:
            yaG = kb.sb(st2, [128, 4, 2048], BF16, "yaG")
            yaGd = Dep()
            ix = kb.sb(st2, [128, 4], U32, "gix")
            ixd = Dep()
            kb.dma("sp", ix[:, :], gidx[:, :], writes=[ixd])
            rows = yaP_all.rearrange("c (b t) -> (c b) t", t=2048)
            for k in range(4):
                run(lambda k=k: kb.gather_rows(yaG[:, k, :], rows[:, :], ix[:, k:k + 1], reads=[agd, ixd], writes=[yaGd]))
            ybv = ybT.rearrange("s (k p) t -> s p k t", p=128)
            yav_meta = yaP_all.rearrange("(k p) c -> p k c", p=128)
            yas = yaS.rearrange("g c t -> (g c) t").rearrange("(k p) t -> p k t", p=128)
            tiles = []
            for t0, n, xr in tl:
                if n == 128:
                    yl = [(slice(0, 4), yaG[:, :, t0:t0 + n], yaGd), (slice(4, 8), ybv[0, :, :, t0:t0 + n], None)]
                else:
                    yl = [(slice(0, 4), yav_meta[:, :, 2032:2048], agd), (slice(4, 8), ybv[0, :, :, 2048:2064], None)]
                tiles.append((xc[0, xr:xr + n, :], yl, h1[0, t0:t0 + n, :], n))
            for t0, n, xr in tl:
                tok0 = 16 + t0 if n == 128 else 0
                yl = [(slice(0, 4), yas[:, :, tok0:tok0 + n], None), (slice(4, 8), ybv[1, :, :, t0:t0 + n], None)]
                tiles.append((xc[1, xr:xr + n, :], yl, h1[1, t0:t0 + n, :], n))
            run(phase_proj_ln, kb, g, tiles, True, wout, bout, ln1g[0], ln1b[0])
        run(phase_mlp_ln, kb, g, [(h1[s_, t0:t0 + n, :], h2[s_, t0:t0 + n, :], n) for s_ in range(2) for t0, n, xr in tl], w1[0], w2[0], ln2g[0], ln2b[0])
        seqs = []
        for s_ in range(2):
            seqs.append(dict(tiles=[(h2[s_, t0:t0 + n, :], kvlat[s_, t0:t0 + n, :], cs[s_, t0:t0 + n, :], n) for t0, n, xr in tl],
                             CS=(Cq[s_], Sq[s_]), qt=(lambda s_: (lambda h, q0: QT[s_, h, :, q0:q0 + 512]))(s_)))
        run(phase_qkv, kb, g, seqs, wqa, qg, WqH, WqS, wkva, kvg)
        kvd = Dep()
        run(lambda: kb.all_gather(kvlat[0], kv_all, reads=[], writes=[kvd]))
        kb.barrier()
        seqs = []
        pch = [(kv_all[r * LS + t0:r * LS + t0 + 128, :], 128) for r in range(8) for t0 in range(0, 2048, 128)] + [(kv_all[2048:2064, :], 16)]
        sch = [(kvlat[1, t0:min(t0 + 128, LS), :], min(128, LS - t0)) for t0 in range(0, LS, 128)]
        for s_, ch in ((0, pch), (1, sch)):
            otv = otok[s_].rearrange("(a t p) (h c) -> a p t h c", p=128, t=4, c=64)
            seqs.append(dict(kchunks=ch, qt=(lambda s_: (lambda h: QT[s_, h, :, :]))(s_),
                             o=(lambda otv: (lambda qsb, half, h: otv[qsb * 2 + half, :, :, h, :]))(otv)))
        run(phase_attn, kb, g, seqs, WkH, WvH)
        tl2 = [(s_, t0) for s_ in range(2) for t0 in range(0, 2048, 128)]
        run(phase_proj_ln, kb, g, [(h2[s_, t0:t0 + 128, :], otok[s_, t0:t0 + 128, :], h3[s_, t0:t0 + 128, :], 128) for s_, t0 in tl2], False, wo, None, ln1g[1], ln1b[1])
        run(phase_mlp_ln, kb, g, [(h3[s_, t0:t0 + 128, :], out[s_, t0:t0 + 128, :], 128) for s_, t0 in tl2], w1[1], w2[1], ln2g[1], ln2b[1])
        kb.finish_wait()
    P.nsteps = step[0]
    return P


def build_nc():
    P = Prog()
    kb = P.kb
    ident = P.din("ident", [128, 128])
    xpad_p = P.din("xpad_p", [LP + 30, D]); maskpad = P.din("maskpad", [1, LP + 30])
    xh_s = P.din("xh_s", [LS + 2, D]); valid_s = P.din("valid_s", [1, LS + 2])
    xc_s = P.din("xc_s", [XC, D]); mask_s = P.din("mask_s", [1, XC])
    zpos_p = P.din("zpos_p", [33, LP]); zpos_s = P.din("zpos_s", [33, LS])
    tabsP, _ = declare_tabs(P, CFG_P, "tp_")
    tabsS, _ = declare_tabs(P, CFG_S, "ts_")
    fw1 = P.din("fw1", [2, 33, 64]); fw2 = P.din("fw2", [2, 64, 64])
    fw3 = P.din("fw3", [8, 2, 64, 64]); fcols = P.din("fcols", [8, 64, 2, 5])
    why = P.din("why", [D, 8 * 192]); brow = P.din("brow", [1, 8 * 192]); hcols = P.din("hcols", [64, 8 * 12]); dskip = P.din("dskip", [8, 1, 64])
    wconf = P.din("wconf", [D, 1024]); ccols = P.din("ccols", [128, 144])
    tokidx = P.din("tokidx", [128, 16], U32)
    wout = P.din("wout", [D, D]); bout = P.din("bout", [1, D])
    ln1g = P.din("ln1g", [2, 1, D]); ln1b = P.din("ln1b", [2, 1, D]); ln2g = P.din("ln2g", [2, 1, D]); ln2b = P.din("ln2b", [2, 1, D])
    w1 = P.din("w1", [2, D, DFF]); w2 = P.din("w2", [2, DFF, D])
    wqa = P.din("wqa", [D, 384]); qg = P.din("qg", [1, 384]); WqH = P.din("WqH", [384, NH * 128]); WqS = P.din("WqS", [384, NH * 32])
    wkva = P.din("wkva", [D, 288]); kvg = P.din("kvg", [1, 256])
    cs_all = P.din("cs_all", [LP, 32])
    cs = P.din("cs", [2, LS, 32]); Cq = P.din("Cq", [2, 32, 2048]); Sq = P.din("Sq", [2, 32, 2048])
    WkH = P.din("WkH", [256, NH * 128]); WvH = P.din("WvH", [256, NH * 64]); wo = P.din("wo", [D, D])
    out = P.dout("out", [2, 2048, D])
    yaP_all = P.scr("yaP_all", [512, YAW], BF16)
    yaS = P.scr("yaS", [8, 64, LS], BF16)
    ybT_p = P.scr("ybT_p", [512, LP], BF16); ybT_s = P.scr("ybT_s", [512, LS], BF16)
    h2f_p = P.scr("h2f_p", [2, 64, LP]); h2f_s = P.scr("h2f_s", [2, 64, LS])
    taps_p = P.scr("taps_p", [2, 64, LP]); Hs_p = P.scr("Hs_p", [86, 64 * CFG_P.nq, 2, CFG_P.NF])
    z_p = P.scr("z_p", [8, 64, LP]); x0_p = P.scr("x0_p", [8, 64, LP])
    taps_s = P.scr("taps_s", [2, 64, LS]); Hs_s = P.scr("Hs_s", [86, 64 * CFG_S.nq, 2, CFG_S.NF])
    z_s = P.scr("z_s", [8, 64, LS]); x0_s = P.scr("x0_s", [8, 64, LS])
    h1_all = P.scr("h1_all", [LP, D]); h2_all = P.scr("h2_all", [LP, D])
    h1_s = P.scr("h1_s", [LS, D]); h2_s = P.scr("h2_s", [LS, D]); h2_own = P.scr("h2_own", [LS, D])
    kv_all = P.scr("kv_all", [LP, 288]); kv_dummy = P.scr("kv_dummy", [LS, 288]); kvlat_s = P.scr("kvlat_s", [LS, 288])
    QT = P.scr("QT", [2, NH, 128, 2048], BF16)
    otok = P.scr("otok", [2, 2048, D]); h3 = P.scr("h3", [2, 2048, D])
    tl = chunk_tiles()
    with ExitStack() as st:
        g = setup_globals(kb, st)
        load_ident(kb, g, ident)
        fwd = lambda i: dict(w1=fw1, w2=fw2, w3=fw3[i], fcols=fcols[i])
        phase_hy_inproj(kb, g, [dict(xh=xpad_p[14:14 + LP + 2, :], valid=maskpad[:, 14:14 + LP + 2], L=LP,
                                     z=[z_p[gi] for gi in range(8)], x0=[x0_p[gi] for gi in range(8)])], why, brow, hcols, G=8)
        phase_hy_filter_h2(kb, g, LP, zpos_p, fwd(0), h2f_p)
        phase_hy_filter_h2(kb, g, LS, zpos_s, fwd(0), h2f_s)
        for gi in range(8):
            phase_hy_filter_taps(kb, g, LP, zpos_p, h2f_p, fw3[gi], fcols[gi], taps_p)
            phase_hy_conv(kb, g, CFG_P, tabsP, taps_p, Hs_p, [dict(z=z_p[gi], x0=x0_p[gi], ya=yaP_all[gi * 64:(gi + 1) * 64, 2032:2032 + LP])], dskip[gi])
        phase_hy_inproj(kb, g, [dict(xh=xh_s, valid=valid_s, L=LS, z=[z_s[gi] for gi in range(8)], x0=[x0_s[gi] for gi in range(8)])],
                        why, brow, hcols, G=8)
        for gi in range(8):
            phase_hy_filter_taps(kb, g, LS, zpos_s, h2f_s, fw3[gi], fcols[gi], taps_s)
            phase_hy_conv(kb, g, CFG_S, tabsS, taps_s, Hs_s, [dict(z=z_s[gi], x0=x0_s[gi], ya=yaS[gi])], dskip[gi])
        cseqs = []
        for j in range(8):
            r0 = 16 + 2048 * j
            cseqs.append(dict(x=xpad_p[r0:r0 + 2078, :], mask=maskpad[:, r0:r0 + 2078], ncols=2078, blocks=[(15 + 512 * i, 512) for i in range(4)],
                              out=(lambda j: (lambda jj, bi, cn: ybT_p[jj * 128:(jj + 1) * 128, 2048 * j + 512 * bi:2048 * j + 512 * bi + cn]))(j)))
        cseqs.append(dict(x=xpad_p[0:46, :], mask=maskpad[:, 0:46], ncols=46, blocks=[(15, 16)],
                          out=lambda jj, bi, cn: ybT_p[jj * 128:(jj + 1) * 128, 16384:16400]))

        def outf_s(jj, bi, cn):
            if bi == 0:
                return ybT_s[jj * 128:(jj + 1) * 128, 2048:2064]
            return ybT_s[jj * 128:(jj + 1) * 128, (bi - 1) * 512:bi * 512]
        cseqs.append(dict(x=xc_s, mask=mask_s, out=outf_s))
        phase_conf(kb, g, cseqs, wconf, ccols)
        yav = yaP_all.rearrange("(k p) c -> p k c", p=128)
        ybv_p = ybT_p.rearrange("(k p) t -> p k t", p=128)
        ybv_s = ybT_s.rearrange("(k p) t -> p k t", p=128)
        yas = yaS.rearrange("g c t -> (g c) t").rearrange("(k p) t -> p k t", p=128)
        tiles = []
        for j in range(8):
            for t0 in range(0, 2048, 128):
                tok = 16 + 2048 * j + t0
                gr = 2048 * j + t0
                tiles.append((xpad_p[15 + tok:15 + tok + 128, :],
                              [(slice(0, 4), yav[:, :, 2032 + tok:2032 + tok + 128], None), (slice(4, 8), ybv_p[:, :, gr:gr + 128], None)],
                              h1_all[gr:gr + 128, :], 128))
        tiles.append((xpad_p[15:31, :], [(slice(0, 4), yav[:, :, 2032:2048], None), (slice(4, 8), ybv_p[:, :, 16384:16400], None)],
                      h1_all[16384:16400, :], 16))
        for t0, n, xr in tl:
            tok0 = 16 + t0 if n == 128 else 0
            tiles.append((xc_s[xr:xr + n, :], [(slice(0, 4), yas[:, :, tok0:tok0 + n], None), (slice(4, 8), ybv_s[:, :, t0:t0 + n], None)],
                          h1_s[t0:t0 + n, :], n))
        phase_proj_ln(kb, g, tiles, True, wout, bout, ln1g[0], ln1b[0])
        ptl = [(r0, min(128, LP - r0)) for r0 in range(0, LP, 128)]
        phase_mlp_ln(kb, g, [(h1_all[r0:r0 + n, :], h2_all[r0:r0 + n, :], n) for r0, n in ptl] +
                     [(h1_s[t0:t0 + n, :], h2_s[t0:t0 + n, :], n) for t0, n, xr in tl], w1[0], w2[0], ln2g[0], ln2b[0])
        with ExitStack() as st2:
            ix = kb.sb(st2, [128, 16], U32, "tokix")
            ixd = Dep()
            kb.dma("sp", ix[:, :], tokidx[:, :], writes=[ixd])
            gb = [kb.sb(st2, [128, D], F32, "gb") for _ in range(2)]
            gd = [Dep(), Dep()]
            for i in range(16):
                j = i % 2
                kb.gather_rows(gb[j][:, :], h2_all[:, :], ix[:, i:i + 1], reads=[ixd], writes=[gd[j]])
                kb.dma("sp", h2_own[128 * i:128 * i + 128, :], gb[j][:, :], reads=[gd[j]])
            kb.dma("sp", h2_own[2048:2064, :], h2_all[16384:16400, :])
            kb.barrier()
        seqs = [dict(tiles=[(h2_all[r0:r0 + n, :], kv_all[r0:r0 + n, :], cs_all[r0:r0 + n, :], n) for r0, n in ptl], kv_only=True),
                dict(tiles=[(h2_own[t0:t0 + n, :], kv_dummy[t0:t0 + n, :], cs[0, t0:t0 + n, :], n) for t0, n, xr in tl],
                     CS=(Cq[0], Sq[0]), qt=lambda h, q0: QT[0, h, :, q0:q0 + 512]),
                dict(tiles=[(h2_s[t0:t0 + n, :], kvlat_s[t0:t0 + n, :], cs[1, t0:t0 + n, :], n) for t0, n, xr in tl],
                     CS=(Cq[1], Sq[1]), qt=lambda h, q0: QT[1, h, :, q0:q0 + 512])]
        phase_qkv(kb, g, seqs, wqa, qg, WqH, WqS, wkva, kvg)
        aseqs = []
        for s_, ch in ((0, [(kv_all[r0:r0 + n, :], n) for r0, n in ptl]),
                       (1, [(kvlat_s[t0:min(t0 + 128, LS), :], min(128, LS - t0)) for t0 in range(0, LS, 128)])):
            otv = otok[s_].rearrange("(a t p) (h c) -> a p t h c", p=128, t=4, c=64)
            aseqs.append(dict(kchunks=ch, qt=(lambda s_: (lambda h: QT[s_, h, :, :]))(s_),
                              o=(lambda otv: (lambda qsb, half, h: otv[qsb * 2 + half, :, :, h, :]))(otv)))
        phase_attn(kb, g, aseqs, WkH, WvH)
        hres = (h2_own, h2_s)
        tl2 = [(s_, t0) for s_ in range(2) for t0 in range(0, 2048, 128)]
        phase_proj_ln(kb, g, [(hres[s_][t0:t0 + 128, :], otok[s_, t0:t0 + 128, :], h3[s_, t0:t0 + 128, :], 128) for s_, t0 in tl2], False, wo, None, ln1g[1], ln1b[1])
        phase_mlp_ln(kb, g, [(h3[s_, t0:t0 + 128, :], out[s_, t0:t0 + 128, :], 128) for s_, t0 in tl2], w1[1], w2[1], ln2g[1], ln2b[1])
        kb.finish_wait()
    return P


def kernel(x_prompt, x_sample, meta_tokens, ev_w_in, ev_b_in, ev_short_w, ev_short_b,
           hy_w1, hy_b1, hy_freq1, hy_w2, hy_b2, hy_freq2, hy_w3, hy_decay, hy_skip_d,
           cf_dw_w, cf_dw_b, cf_ln_g, cf_ln_b, ev_w_out, ev_b_out,
           mla_wq_a, mla_q_norm, mla_wq_b, mla_wkv_a, mla_kv_norm, mla_wkv_b, mla_wo,
           ln1_g, ln1_b, mlp_w1, mlp_w2, ln2_g, ln2_b):
    f = lambda a: np.asarray(a, dtype=np.float32)
    x_prompt, x_sample, meta = f(x_prompt), f(x_sample), f(meta_tokens)
    win, bin_, sw, sb = f(ev_w_in)[0], f(ev_b_in)[0], f(ev_short_w)[0], f(ev_short_b)[0]
    ident = np.eye(128, dtype=np.float32)
    hp = np.concatenate([meta, x_prompt[0]], 0)
    hs = [np.concatenate([meta, x_sample[c]], 0) for c in range(8)]
    z1 = np.zeros((1, D), np.float32)
    z15 = np.zeros((15, D), np.float32)
    xpad_p = np.concatenate([z15, hp, z15], 0)
    maskpad = np.zeros((1, LP + 30), np.float32); maskpad[0, 15:15 + LP] = 1
    valid_s = np.ones((1, LS + 2), np.float32); valid_s[0, 0] = 0; valid_s[0, -1] = 0
    tabP, tabS = fft_tables(CFG_P), fft_tables(CFG_S)
    gcols = [[np.arange(k * 512 + gi * 64, k * 512 + gi * 64 + 64) for k in range(3)] for gi in range(8)]
    allc = np.concatenate([np.concatenate(gc) for gc in gcols])
    why = np.ascontiguousarray(win[:, allc])
    brow = bin_[allc][None, :].copy()
    hcols = np.concatenate([np.stack([sw[0, c_], sw[1, c_], sw[2, c_], sb[c_]], 1) for gc in gcols for c_ in gc], 1).astype(np.float32)
    fw3 = np.stack([np.ascontiguousarray(f(hy_w3)[0][:, :, gi * 64:gi * 64 + 64]) for gi in range(8)], 0)
    fcols = np.stack([np.stack([f(hy_freq1)[0], f(hy_b1)[0], f(hy_freq2)[0], f(hy_b2)[0], f(hy_decay)[0][:, gi * 64:gi * 64 + 64]], -1).transpose(1, 0, 2)
                      for gi in range(8)], 0).astype(np.float32)
    dskip = np.stack([f(hy_skip_d)[0][gi * 64:gi * 64 + 64][None, :] for gi in range(8)], 0)
    ccols = np.concatenate([colpack(bin_[1536:2048]), colpack(bin_[2048:2560]), colpack(f(cf_dw_b)[0]), colpack(f(cf_ln_g)[0]), colpack(f(cf_ln_b)[0]),
                            np.ascontiguousarray(f(cf_dw_w)[0].T.reshape(4, 128, 31).transpose(1, 0, 2).reshape(128, 124))], 1).astype(np.float32)
    wqb = f(mla_wq_b)[0].reshape(384, NH, 96)
    WqH = np.concatenate([wqb[:, :, 64:96], np.zeros((384, NH, 32), np.float32), wqb[:, :, 0:64]], -1).reshape(384, NH * 128)
    WqS = np.concatenate([wqb[:, :, 80:96], wqb[:, :, 64:80]], -1).reshape(384, NH * 32)
    wkvb = f(mla_wkv_b)[0].reshape(256, NH, 128)
    WkH = np.concatenate([np.zeros((256, NH, 64), np.float32), wkvb[:, :, 0:64]], -1).reshape(256, NH * 128)
    WvH = np.ascontiguousarray(wkvb[:, :, 64:128]).reshape(256, NH * 64)
    zp_p, zp_s = zpos_table(LP), zpos_table(LS)
    co, si = rope_cs(np.concatenate([np.arange(16, LP), np.arange(16)]))
    cs_all = np.concatenate([co, si], 1)
    shared = dict(ident=ident, xpad_p=xpad_p, maskpad=maskpad, valid_s=valid_s, zpos_p=zp_p, zpos_s=zp_s, fw1=f(hy_w1)[0], fw2=f(hy_w2)[0],
                  fw3=fw3, fcols=fcols, why=why, brow=brow, hcols=hcols, dskip=dskip, wconf=np.ascontiguousarray(win[:, 1536:2560]), ccols=ccols,
                  wout=f(ev_w_out)[0], bout=f(ev_b_out)[0:1], ln1g=f(ln1_g)[:, None, :], ln1b=f(ln1_b)[:, None, :],
                  ln2g=f(ln2_g)[:, None, :], ln2b=f(ln2_b)[:, None, :], w1=f(mlp_w1), w2=f(mlp_w2), wqa=f(mla_wq_a)[0], qg=f(mla_q_norm)[0:1],
                  WqH=WqH, WqS=WqS, wkva=f(mla_wkv_a)[0], kvg=f(mla_kv_norm)[0:1], cs_all=cs_all, WkH=WkH, WvH=WvH, wo=f(mla_wo)[0])
    for k, v in tabP.items():
        shared["tp_" + k] = v
    for k, v in tabS.items():
        shared["ts_" + k] = v
    P = build_nc()
    ims = []
    for c in range(8):
        m0 = 16 + 2048 * c
        xb, mb = make_xc(hs[c], 16, LS)
        css, Cqs, Sqs = [], [], []
        for pos in (np.concatenate([np.arange(m0, m0 + 2048), np.arange(16)]), np.concatenate([np.arange(16, LS), np.arange(16)])):
            co, si = rope_cs(pos)
            css.append(np.concatenate([co, si], 1))
            Cqs.append(np.concatenate([co[:2048].T, co[:2048].T], 0))
            Sqs.append(np.concatenate([-si[:2048].T, si[:2048].T], 0))
        tix = (2048 * c + 128 * np.arange(16)[None, :] + np.arange(128)[:, None]).astype(np.uint32)
        im = dict(shared)
        im.update(xh_s=np.concatenate([z1, hs[c], z1], 0), xc_s=xb, mask_s=mb, tokidx=tix,
                  cs=np.stack(css, 0), Cq=np.stack(Cqs, 0), Sq=np.stack(Sqs, 0))
        ims.append(check_inputs(P, im))
    r = run_bass_kernel_spmd(P.nc, ims, core_ids=list(range(8))).results
    y_prompt = np.concatenate([np.asarray(r[c]["out"])[0] for c in range(8)], 0)[None].astype(np.float32)
    y_sample = np.stack([np.asarray(r[c]["out"])[1] for c in range(8)], 0).astype(np.float32)
    return (y_prompt, y_sample)
```

```python
import math
from contextlib import ExitStack
import numpy as np
import ml_dtypes
import concourse.bass as bass
import concourse.mybir as mybir
from concourse.bass_utils import run_bass_kernel_spmd

F32 = mybir.dt.float32
BF16 = mybir.dt.bfloat16
AF = mybir.ActivationFunctionType
ALU = mybir.AluOpType
AX = mybir.AxisListType

D = 1024
NMETA = 16
DFF = 4096
ALPHA = 4 ** 0.25
LN_EPS = 1e-5
RMS_EPS = 1e-6
NH = 16


SEM_MAX = 24000


class Dep:
    __slots__ = ("w", "r")

    def __init__(self):
        self.w = None
        self.r = {}


class KB:
    def __init__(self, nc):
        self.nc = nc
        self.stack = ExitStack()
        self.raw = dict(pe=nc.tensor, act=nc.scalar, dve=nc.vector, pool=nc.gpsimd, sp=nc.sync)
        self.sem = {}
        self.cnt = {}
        self.seen = {e: {} for e in self.raw}
        self.semobj = []
        for e in ("pe", "act", "dve", "pool"):
            self.sem[e] = self._newsem("s_" + e)
            self.cnt[e] = 0
        self.dq = {}
        for q, n in (("sp", 20), ("act", 8), ("pool", 8)):
            self.dq[q] = dict(sems=[self._newsem(f"d_{q}{i}") for i in range(n)], vals=[0] * n, nxt=0)
        self.uid = 0

    def _newsem(self, name):
        s = self.stack.enter_context(self.nc.semaphore(name))
        self.semobj.append(s)
        return len(self.semobj) - 1

    def name(self, p):
        self.uid += 1
        return f"{p}{self.uid}"

    def sb(self, st, shape, dt, name="t"):
        return st.enter_context(self.nc.sbuf_tensor(self.name(name), list(shape), dt))

    def ps(self, st, shape, dt, name="p"):
        return st.enter_context(self.nc.psum_tensor(self.name(name), list(shape), dt))

    def _waits(self, eng, reads, writes, extra=None):
        need = {}

        def add(tok):
            if tok is None:
                return
            s, v, src = tok
            if src == "pe" and eng == "pe":
                return
            if need.get(s, 0) < v:
                need[s] = v

        for d in reads:
            add(d.w)
        for d in writes:
            add(d.w)
            for t in d.r.values():
                add(t)
        if extra:
            for t in extra:
                add(t)
        seen = self.seen[eng]
        for s, v in need.items():
            if seen.get(s, 0) < v:
                self.raw[eng].wait_ge(self.semobj[s], v)
                seen[s] = v

    def op(self, eng, fn, reads=(), writes=()):
        self._waits(eng, reads, writes)
        ins = fn(self.raw[eng])
        if self.cnt[eng] >= SEM_MAX:
            self.sem[eng] = self._newsem(self.name("s_" + eng))
            self.cnt[eng] = 0
        self.cnt[eng] += 1
        ins.then_inc(self.semobj[self.sem[eng]], 1)
        tok = (self.sem[eng], self.cnt[eng], eng)
        for d in reads:
            d.r[tok[0]] = tok
        for d in writes:
            d.w = tok
            d.r = {}
        return ins

    def dma(self, q, out, in_, reads=(), writes=(), **kw):
        dq = self.dq[q]
        i = dq["nxt"]
        dq["nxt"] = (i + 1) % len(dq["sems"])
        s = dq["sems"][i]
        extra = [(s, dq["vals"][i], "dma")] if dq["vals"][i] else None
        self._waits(q, reads, writes, extra)
        ins = self.raw[q].dma_start(out=out, in_=in_, **kw)
        dq["vals"][i] += 16
        ins.then_inc(self.semobj[s], 16)
        tok = (s, dq["vals"][i], "dma")
        for d in reads:
            d.r[s] = tok
        for d in writes:
            d.w = tok
            d.r = {}
        return ins

    def all_gather(self, in_ap, out_ap, reads=(), writes=()):
        if not hasattr(self, "ccsem"):
            self.ccsem = self._newsem("ccsem")
            self.ccval = 0
        self._waits("pool", reads, writes)
        ins = self.raw["pool"].collective_compute("AllGather", ALU.bypass, replica_groups=[list(range(8))],
                                                  ins=[in_ap.opt()], outs=[out_ap.opt()])
        self.ccval += 1
        ins.then_inc(self.semobj[self.ccsem], 1)
        tok = (self.ccsem, self.ccval, "cc")
        for d in reads:
            d.r[self.ccsem] = tok
        for d in writes:
            d.w = tok
            d.r = {}
        return ins

    def gather_rows(self, out, in_rows, idx, reads=(), writes=()):
        dq = self.dq["pool"]
        i = dq["nxt"]
        dq["nxt"] = (i + 1) % len(dq["sems"])
        s = dq["sems"][i]
        extra = [(s, dq["vals"][i], "dma")] if dq["vals"][i] else None
        self._waits("pool", reads, writes, extra)
        ins = self.raw["pool"].indirect_dma_start(out=out, out_offset=None, in_=in_rows,
                                                  in_offset=bass.IndirectOffsetOnAxis(ap=idx, axis=0))
        dq["vals"][i] += 16
        ins.then_inc(self.semobj[s], 16)
        tok = (s, dq["vals"][i], "dma")
        for d in reads:
            d.r[s] = tok
        for d in writes:
            d.w = tok
            d.r = {}
        return ins

    def barrier(self):
        toks = [(self.sem[e], self.cnt[e], e) for e in ("pe", "act", "dve", "pool") if self.cnt[e]]
        for q in self.dq.values():
            for s, v in zip(q["sems"], q["vals"]):
                if v:
                    toks.append((s, v, "dma"))
        if getattr(self, "ccval", 0):
            toks.append((self.ccsem, self.ccval, "cc"))
        for eng in ("pe", "act", "dve", "pool", "sp"):
            seen = self.seen[eng]
            for s, v, src in toks:
                if seen.get(s, 0) < v and not (s == self.sem.get(eng)):
                    self.raw[eng].wait_ge(self.semobj[s], v)
                    seen[s] = v

    def finish_wait(self):
        for q in self.dq.values():
            for s, v in zip(q["sems"], q["vals"]):
                if v and self.seen["sp"].get(s, 0) < v:
                    self.raw["sp"].wait_ge(self.semobj[s], v)
                    self.seen["sp"][s] = v


class Glob:
    pass


def setup_globals(kb, st):
    g = Glob()
    nc = kb.nc
    g.pall = kb.ps(st, [128, 8, 512], F32, "banks")
    g.psum = [g.pall[:, b, :] for b in range(8)]
    g.pd = [Dep() for _ in range(8)]
    g.ident_f = kb.sb(st, [128, 128], F32, "identf")
    g.ident_b = kb.sb(st, [128, 128], BF16, "identb")
    g.ident_d = Dep()
    g.ones_b = kb.sb(st, [128, 128], BF16, "onesb")
    g.ones_d = Dep()
    g.bk = -1
    return g


def load_ident(kb, g, ident_dram):
    kb.dma("sp", g.ident_f[:], ident_dram, writes=[g.ident_d])
    kb.op("dve", lambda e: e.tensor_copy(out=g.ident_b[:], in_=g.ident_f[:]), reads=[g.ident_d], writes=[g.ident_d])
    kb.op("pool", lambda e: e.memset(g.ones_b[:], 1.0), writes=[g.ones_d])


_rr = [0]


def cast_eng():
    _rr[0] += 1
    return ("dve", "pool", "act")[_rr[0] % 3]


def copy_op(kb, eng, out, in_, reads, writes):
    if eng == "act":
        return kb.op("act", lambda e: e.copy(out=out, in_=in_), reads=reads, writes=writes)
    return kb.op(eng, lambda e: e.tensor_copy(out=out, in_=in_), reads=reads, writes=writes)


def load_weight_bf16(kb, st_phase, dst, dst_dep, src, kc, ncols, stage_cols=2048):
    with ExitStack() as st:
        stg = [kb.sb(st, [128, stage_cols], F32, "wstg") for _ in range(3)]
        sd = [Dep() for _ in range(3)]
        i = 0
        for k in range(kc):
            for c0 in range(0, ncols, stage_cols):
                cn = min(stage_cols, ncols - c0)
                j = i % 3
                kb.dma("sp" if i % 2 == 0 else "pool", stg[j][:, :cn], src[k * 128:(k + 1) * 128, c0:c0 + cn], writes=[sd[j]])
                copy_op(kb, ("dve", "act")[i % 2], dst[:, k, c0:c0 + cn], stg[j][:, :cn], [sd[j]], [dst_dep])
                i += 1
        kb.barrier()


def load_bcast(kb, dst, dep, src_row):
    kb.dma("sp", dst, src_row.partition_broadcast(128) if len(src_row.shape) == 1 else src_row.broadcast_to([128, src_row.shape[-1]]), writes=[dep])


def layer_norm_tile(kb, r, rd, n, gt, bt, gbd, out, outd, small, smd, junk, junkd):
    s1, s2 = small[:, 0:1], small[:, 1:2]
    kb.op("act", lambda e: e.activation(out=junk[:n, :], in_=r[:n, :], func=AF.Identity, accum_out=s1[:n, :]), reads=[rd], writes=[junkd, smd])
    kb.op("act", lambda e: e.activation(out=junk[:n, :], in_=r[:n, :], func=AF.Square, accum_out=s2[:n, :]), reads=[rd], writes=[junkd, smd])
    mean, var, rstd = small[:, 2:3], small[:, 3:4], small[:, 4:5]
    kb.op("dve", lambda e: e.tensor_scalar(out=mean[:n, :], in0=s1[:n, :], scalar1=1.0 / D, scalar2=None, op0=ALU.mult), reads=[smd], writes=[smd])
    kb.op("dve", lambda e: e.tensor_tensor(out=var[:n, :], in0=mean[:n, :], in1=mean[:n, :], op=ALU.mult), reads=[smd], writes=[smd])
    kb.op("dve", lambda e: e.scalar_tensor_tensor(out=var[:n, :], in0=s2[:n, :], scalar=1.0 / D, in1=var[:n, :], op0=ALU.mult, op1=ALU.subtract), reads=[smd], writes=[smd])
    kb.op("act", lambda e: e.activation(out=rstd[:n, :], in_=var[:n, :], func=AF.Sqrt, bias=LN_EPS, scale=1.0), reads=[smd], writes=[smd])
    kb.op("dve", lambda e: e.reciprocal(out=rstd[:n, :], in_=rstd[:n, :]), reads=[smd], writes=[smd])
    kb.op("dve", lambda e: e.tensor_scalar(out=r[:n, :], in0=r[:n, :], scalar1=mean[:n, :], scalar2=rstd[:n, :], op0=ALU.subtract, op1=ALU.mult), reads=[smd, rd], writes=[rd])
    kb.op("dve", lambda e: e.tensor_tensor(out=r[:n, :], in0=r[:n, :], in1=gt[:n, :], op=ALU.mult), reads=[rd, gbd], writes=[rd])
    kb.op("pool", lambda e: e.tensor_tensor(out=out[:n, :], in0=r[:n, :], in1=bt[:n, :], op=ALU.add), reads=[rd, gbd], writes=[outd])


def mm(kb, out, lhsT, rhs, start, stop, reads, writes):
    return kb.op("pe", lambda e: e.matmul(out, lhsT=lhsT, rhs=rhs, start=start, stop=stop), reads, writes)


def tt(kb, eng, out, in0, in1, op, reads, writes):
    return kb.op(eng, lambda e: e.tensor_tensor(out=out, in0=in0, in1=in1, op=op), reads, writes)


def ts(kb, eng, out, in0, s1, s2, op0, op1, reads, writes):
    if s2 is None:
        return kb.op(eng, lambda e: e.tensor_scalar(out=out, in0=in0, scalar1=s1, scalar2=None, op0=op0), reads, writes)
    return kb.op(eng, lambda e: e.tensor_scalar(out=out, in0=in0, scalar1=s1, scalar2=s2, op0=op0, op1=op1), reads, writes)


def stt(kb, eng, out, in0, scalar, in1, op0, op1, reads, writes):
    return kb.op("dve", lambda e: e.scalar_tensor_tensor(out=out, in0=in0, scalar=scalar, in1=in1, op0=op0, op1=op1), reads, writes)


def act(kb, out, in_, func, reads, writes, **kw):
    return kb.op("act", lambda e: e.activation(out=out, in_=in_, func=func, **kw), reads, writes)


def nextbank(g):
    g.bk = (g.bk + 1) % 8
    return g.bk


def transpose_tile(kb, g, src, srcd, n, dstT, dstd, col0, kc=8):
    for k0 in range(0, kc, 4):
        b = nextbank(g)
        kn = min(4, kc - k0)
        pv = g.psum[b][:, :].rearrange("p (k t) -> p k t", k=4)
        for k in range(kn):
            kb.op("pe", lambda e, k=k: e.transpose(pv[:, k, :n], src[:n, (k0 + k) * 128:(k0 + k + 1) * 128], g.ident_f[:n, :n]),
                  reads=[srcd, g.ident_d], writes=[g.pd[b]])
        copy_op(kb, ("dve", "act")[b % 2], dstT[:, k0:k0 + kn, col0:col0 + n], pv[:, 0:kn, :n], [g.pd[b]], [dstd])


def phase_proj_ln(kb, g, tiles, fm, W, bias, lng, lnb):
    with ExitStack() as st:
        Wb = kb.sb(st, [128, 8, D], BF16, "Wb")
        Wd = Dep()
        load_weight_bf16(kb, st, Wb, Wd, W, 8, D)
        gt = kb.sb(st, [128, D], F32, "g")
        bt = kb.sb(st, [128, D], F32, "b")
        gbd = Dep()
        load_bcast(kb, gt[:], gbd, lng)
        load_bcast(kb, bt[:], gbd, lnb)
        if bias is not None:
            bi = kb.sb(st, [128, D], F32, "bias")
            load_bcast(kb, bi[:], gbd, bias)
        NB = 4
        hb = [kb.sb(st, [128, D], F32, "h") for _ in range(NB)]
        hd = [Dep() for _ in range(NB)]
        yT = [kb.sb(st, [128, 8, 128], BF16, "yT") for _ in range(NB)]
        yTd = [Dep() for _ in range(NB)]
        if not fm:
            yb = [kb.sb(st, [128, D], F32, "y") for _ in range(NB)]
            yd = [Dep() for _ in range(NB)]
        rb = [kb.sb(st, [128, D], F32, "r") for _ in range(NB)]
        rd = [Dep() for _ in range(NB)]
        junk = kb.sb(st, [128, D], F32, "junk")
        junkd = Dep()
        small = [kb.sb(st, [128, 8], F32, "small") for _ in range(NB)]
        smd = [Dep() for _ in range(NB)]
        for i, (hap, yap, oap, n) in enumerate(tiles):
            j = i % NB
            kb.dma("sp", hb[j][:n, :], hap, writes=[hd[j]])
            if fm:
                for qi, (ksl, src, dep) in enumerate(yap):
                    kb.dma("sp", yT[j][:, ksl, :n], src, reads=[dep] if dep is not None else [], writes=[yTd[j]])
            else:
                kb.dma("sp", yb[j][:n, :], yap, writes=[yd[j]])
                transpose_tile(kb, g, yb[j], yd[j], n, yT[j], yTd[j], 0)
            bks = (nextbank(g), nextbank(g))
            for half, bk in enumerate(bks):
                for k in range(8):
                    mm(kb, g.psum[bk][:n, :], yT[j][:, k, :n], Wb[:, k, half * 512:(half + 1) * 512], k == 0, k == 7,
                       [yTd[j], Wd], [g.pd[bk]])
            for half, bk in enumerate(bks):
                sl = slice(half * 512, (half + 1) * 512)
                if bias is not None:
                    tt(kb, "dve", rb[j][:n, sl], g.psum[bk][:n, :], bi[:n, sl], ALU.add, [g.pd[bk], gbd], [rd[j]])
                else:
                    copy_op(kb, "act", rb[j][:n, sl], g.psum[bk][:n, :], [g.pd[bk]], [rd[j]])
            stt(kb, "pool", rb[j][:n, :], hb[j][:n, :], ALPHA, rb[j][:n, :], ALU.mult, ALU.add, [hd[j], rd[j]], [rd[j]])
            layer_norm_tile(kb, rb[j], rd[j], n, gt, bt, gbd, rb[j], rd[j], small[j], smd[j], junk, junkd)
            kb.dma("pool", oap, rb[j][:n, :], reads=[rd[j]])
        kb.barrier()


def phase_mlp_ln(kb, g, tiles, W1, W2, lng, lnb):
    with ExitStack() as st:
        W1b = kb.sb(st, [128, 8, DFF], BF16, "W1b")
        W2b = kb.sb(st, [128, 32, D], BF16, "W2b")
        Wd = Dep()
        load_weight_bf16(kb, st, W1b, Wd, W1, 8, DFF)
        load_weight_bf16(kb, st, W2b, Wd, W2, 32, D, stage_cols=1024)
        gt = kb.sb(st, [128, D], F32, "g")
        bt = kb.sb(st, [128, D], F32, "b")
        gbd = Dep()
        load_bcast(kb, gt[:], gbd, lng)
        load_bcast(kb, bt[:], gbd, lnb)
        hb = [kb.sb(st, [128, D], F32, "h") for _ in range(4)]
        hd = [Dep() for _ in range(4)]
        hT = kb.sb(st, [128, 8, 512], BF16, "hT")
        hTd = Dep()
        uT = kb.sb(st, [128, 32, 512], BF16, "uT")
        uTd = [Dep() for _ in range(32)]
        rl = [kb.sb(st, [128, 512], F32, "relu") for _ in range(2)]
        rld = [Dep() for _ in range(2)]
        junk = kb.sb(st, [128, D], BF16, "junk")
        junkd = Dep()
        small = [kb.sb(st, [128, 8], F32, "small") for _ in range(4)]
        smd = [Dep() for _ in range(4)]
        for s0 in range(0, len(tiles), 4):
            grp = tiles[s0:s0 + 4]
            offs = []
            tot = 0
            for i, (iap, oap, n) in enumerate(grp):
                kb.dma("sp", hb[i][:n, :], iap, writes=[hd[i]])
                offs.append(tot)
                tot += n
            for i, (iap, oap, n) in enumerate(grp):
                transpose_tile(kb, g, hb[i], hd[i], n, hT, hTd, offs[i])
            for j in range(32):
                bk = nextbank(g)
                for k in range(8):
                    mm(kb, g.psum[bk][:, :tot], W1b[:, k, j * 128:(j + 1) * 128], hT[:, k, :tot], k == 0, k == 7, [Wd, hTd], [g.pd[bk]])
                q = j % 2
                act(kb, rl[q][:, :tot], g.psum[bk][:, :tot], AF.Relu, [g.pd[bk]], [rld[q]])
                tt(kb, "dve", uT[:, j, :tot], rl[q][:, :tot], rl[q][:, :tot], ALU.mult, [rld[q]], [uTd[j]])
            for i, (iap, oap, n) in enumerate(grp):
                bks = (nextbank(g), nextbank(g))
                for half, bk in enumerate(bks):
                    for j in range(32):
                        mm(kb, g.psum[bk][:n, :], uT[:, j, offs[i]:offs[i] + n], W2b[:, j, half * 512:(half + 1) * 512], j == 0, j == 31,
                           [uTd[j], Wd], [g.pd[bk]])
                for half, bk in enumerate(bks):
                    sl = slice(half * 512, (half + 1) * 512)
                    stt(kb, "dve", hb[i][:n, sl], hb[i][:n, sl], ALPHA, g.psum[bk][:n, :], ALU.mult, ALU.add, [hd[i], g.pd[bk]], [hd[i]])
                layer_norm_tile(kb, hb[i], hd[i], n, gt, bt, gbd, hb[i], hd[i], small[i], smd[i], junk, junkd)
                kb.dma("pool", oap, hb[i][:n, :], reads=[hd[i]])
        kb.barrier()


def rms_rstd(kb, src, srcd, n, width, small, smd, junk, junkd, col):
    ss, rs = small[:, col:col + 1], small[:, col + 1:col + 2]
    act(kb, junk[:n, :width], src[:n, :width], AF.Square, [srcd], [junkd, smd], accum_out=ss[:n, :])
    act(kb, rs[:n, :], ss[:n, :], AF.Sqrt, [smd], [smd], bias=RMS_EPS, scale=1.0 / width)
    kb.op("dve", lambda e: e.reciprocal(out=rs[:n, :], in_=rs[:n, :]), [smd], [smd])
    return rs


def phase_qkv(kb, g, seqs, wqa, qg, WqH, WqS, wkva, kvg):
    with ExitStack() as st:
        wqa_b = kb.sb(st, [128, 8, 384], BF16, "wqa")
        wkva_b = kb.sb(st, [128, 8, 288], BF16, "wkva")
        wqh_b = kb.sb(st, [128, 3, NH * 128], BF16, "wqh")
        wqs_b = kb.sb(st, [128, 3, NH * 32], BF16, "wqs")
        Wd = Dep()
        load_weight_bf16(kb, st, wqa_b, Wd, wqa, 8, 384)
        load_weight_bf16(kb, st, wkva_b, Wd, wkva, 8, 288)
        load_weight_bf16(kb, st, wqh_b, Wd, WqH, 3, NH * 128)
        load_weight_bf16(kb, st, wqs_b, Wd, WqS, 3, NH * 32)
        qgt = kb.sb(st, [128, 384], F32, "qg")
        kvgt = kb.sb(st, [128, 256], F32, "kvg")
        gd = Dep()
        load_bcast(kb, qgt[:], gd, qg)
        load_bcast(kb, kvgt[:], gd, kvg)
        hb = [kb.sb(st, [128, D], F32, "h") for _ in range(4)]
        hd = [Dep() for _ in range(4)]
        hT = kb.sb(st, [128, 8, 512], BF16, "hT")
        hTd = Dep()
        cq = [kb.sb(st, [128, 384], F32, "cq") for _ in range(2)]
        cqd = [Dep() for _ in range(2)]
        cqT = kb.sb(st, [128, 3, 512], BF16, "cqT")
        cqTd = Dep()
        kvr = [kb.sb(st, [128, 288], F32, "kvr") for _ in range(2)]
        kvrd = [Dep() for _ in range(2)]
        kvo = [kb.sb(st, [128, 288], F32, "kvo") for _ in range(2)]
        kvod = [Dep() for _ in range(2)]
        cst = [kb.sb(st, [128, 32], F32, "cs") for _ in range(2)]
        csd = [Dep() for _ in range(2)]
        tmp = [kb.sb(st, [128, 64], F32, "tmp") for _ in range(2)]
        tmpd = [Dep() for _ in range(2)]
        junk = kb.sb(st, [128, 384], F32, "junk")
        junkd = Dep()
        small = [kb.sb(st, [128, 8], F32, "small") for _ in range(2)]
        smd = [Dep() for _ in range(2)]
        Ct = kb.sb(st, [32, 2048], F32, "C")
        St = kb.sb(st, [32, 2048], F32, "S")
        CSd = Dep()
        qsw = [kb.sb(st, [32, 512], F32, "qsw") for _ in range(2)]
        qswd = [Dep() for _ in range(2)]
        qo = [kb.sb(st, [128, 512], BF16, "qo") for _ in range(2)]
        qod = [Dep() for _ in range(2)]
        it = 0
        for sq in seqs:
            if not sq.get("kv_only"):
                kb.dma("sp", Ct[:], sq["CS"][0], writes=[CSd])
                kb.dma("sp", St[:], sq["CS"][1], writes=[CSd])
            tiles = sq["tiles"]
            for s0 in range(0, len(tiles), 4):
                grp = tiles[s0:s0 + 4]
                offs, tot = [], 0
                for i, (hap, kvap, csap, n) in enumerate(grp):
                    kb.dma("sp", hb[i][:n, :], hap, writes=[hd[i]])
                    offs.append(tot)
                    tot += n
                for i, (hap, kvap, csap, n) in enumerate(grp):
                    transpose_tile(kb, g, hb[i], hd[i], n, hT, hTd, offs[i])
                is_main = (tot == 512) and not sq.get("kv_only")
                for i, (hap, kvap, csap, n) in enumerate(grp):
                    j = it % 2
                    it += 1
                    kb.dma("sp", cst[j][:n, :], csap, writes=[csd[j]])
                    bk = nextbank(g)
                    for k in range(8):
                        mm(kb, g.psum[bk][:n, :288], hT[:, k, offs[i]:offs[i] + n], wkva_b[:, k, :], k == 0, k == 7, [hTd, Wd], [g.pd[bk]])
                    copy_op(kb, "act", kvr[j][:n, :], g.psum[bk][:n, :288], [g.pd[bk]], [kvrd[j]])
                    rs = rms_rstd(kb, kvr[j], kvrd[j], n, 256, small[j], smd[j], junk, junkd, 0)
                    stt(kb, "dve", kvo[j][:n, 0:256], kvr[j][:n, 0:256], rs[:n, :], kvgt[:n, :], ALU.mult, ALU.mult, [kvrd[j], smd[j], gd], [kvod[j]])
                    x1, x2 = kvr[j][:n, 256:272], kvr[j][:n, 272:288]
                    co, si = cst[j][:n, 0:16], cst[j][:n, 16:32]
                    t = tmp[j]
                    tt(kb, "dve", t[:n, 0:16], x1, co, ALU.mult, [kvrd[j], csd[j]], [tmpd[j]])
                    tt(kb, "dve", t[:n, 16:32], x2, si, ALU.mult, [kvrd[j], csd[j]], [tmpd[j]])
                    tt(kb, "dve", t[:n, 32:48], x1, si, ALU.mult, [kvrd[j], csd[j]], [tmpd[j]])
                    tt(kb, "dve", t[:n, 48:64], x2, co, ALU.mult, [kvrd[j], csd[j]], [tmpd[j]])
                    tt(kb, "dve", kvo[j][:n, 256:272], t[:n, 0:16], t[:n, 16:32], ALU.subtract, [tmpd[j]], [kvod[j]])
                    tt(kb, "dve", kvo[j][:n, 272:288], t[:n, 32:48], t[:n, 48:64], ALU.add, [tmpd[j]], [kvod[j]])
                    kb.dma("pool", kvap, kvo[j][:n, :], reads=[kvod[j]])
                    if not is_main:
                        continue
                    bk = nextbank(g)
                    for k in range(8):
                        mm(kb, g.psum[bk][:n, :384], hT[:, k, offs[i]:offs[i] + n], wqa_b[:, k, :], k == 0, k == 7, [hTd, Wd], [g.pd[bk]])
                    copy_op(kb, "act", cq[j][:n, :], g.psum[bk][:n, :384], [g.pd[bk]], [cqd[j]])
                    rs = rms_rstd(kb, cq[j], cqd[j], n, 384, small[j], smd[j], junk, junkd, 2)
                    stt(kb, "dve", cq[j][:n, :], cq[j][:n, :], rs[:n, :], qgt[:n, :], ALU.mult, ALU.mult, [cqd[j], smd[j], gd], [cqd[j]])
                    transpose_tile(kb, g, cq[j], cqd[j], n, cqT, cqTd, offs[i], kc=3)
                if not is_main:
                    continue
                q0 = (s0 // 4) * 512
                for h in range(NH):
                    j = h % 2
                    bka, bkb = nextbank(g), nextbank(g)
                    for k in range(3):
                        mm(kb, g.psum[bka][:, :], wqh_b[:, k, h * 128:(h + 1) * 128], cqT[:, k, :], k == 0, k == 2, [Wd, cqTd], [g.pd[bka]])
                    for k in range(3):
                        mm(kb, g.psum[bkb][:32, :], wqs_b[:, k, h * 32:(h + 1) * 32], cqT[:, k, :], k == 0, k == 2, [Wd, cqTd], [g.pd[bkb]])
                    tt(kb, "dve", qsw[j][:, :], g.psum[bkb][:32, :], St[:, q0:q0 + 512], ALU.mult, [g.pd[bkb], CSd], [qswd[j]])
                    rope_q(kb, g, qo[j], qod[j], bka, qsw[j], qswd[j], Ct, CSd, q0, st, small)
                    copy_op(kb, "act", qo[j][32:64, :], g.psum[bka][32:64, :], [g.pd[bka]], [qod[j]])
                    copy_op(kb, "act", qo[j][64:128, :], g.psum[bka][64:128, :], [g.pd[bka]], [qod[j]])
                    kb.dma("pool", sq["qt"](h, q0), qo[j][:, :], reads=[qod[j]])
        kb.barrier()


_ropetmp = {}


def rope_q(kb, g, qo, qod, bka, qsw, qswd, Ct, CSd, q0, st, small):
    key = id(st)
    if key not in _ropetmp:
        _ropetmp[key] = (kb.sb(st, [32, 512], F32, "rq"), Dep())
    t, td = _ropetmp[key]
    tt(kb, "dve", t[:, :], g.psum[bka][0:32, :], Ct[:, q0:q0 + 512], ALU.mult, [g.pd[bka], CSd], [td])
    tt(kb, "dve", qo[0:32, :], t[:, :], qsw[:, :], ALU.add, [td, qswd], [qod])


QK_SCALE = 96 ** -0.5


def phase_attn(kb, g, seqs, WkH, WvH):
    NKmax = max(sum(n for _, n in sq["kchunks"]) for sq in seqs)
    NCH = max(len(sq["kchunks"]) for sq in seqs)
    with ExitStack() as st:
        wk_b = kb.sb(st, [128, 2, NH * 128], BF16, "wk")
        wv_b = kb.sb(st, [128, 2, NH * 64], BF16, "wv")
        Wd = Dep()
        load_weight_bf16(kb, st, wk_b, Wd, WkH, 2, NH * 128)
        load_weight_bf16(kb, st, wv_b, Wd, WvH, 2, NH * 64, stage_cols=1024)
        ckvT = kb.sb(st, [128, 3, NKmax], BF16, "ckvT")
        ckvTd = Dep()
        KT = kb.sb(st, [128, NKmax], BF16, "KT")
        KTd = Dep()
        V = kb.sb(st, [128, NCH, 66], BF16, "V")
        Vd = Dep()
        QT = [kb.sb(st, [128, 2048], BF16, "QT") for _ in range(2)]
        QTd = [Dep() for _ in range(2)]
        P = [kb.sb(st, [128, 1024], BF16, "P") for _ in range(2)]
        Pd = [Dep() for _ in range(2)]
        oT = kb.sb(st, [128, 1024], F32, "oT")
        oTd = Dep()
        osm = [kb.sb(st, [128, 4, 64], F32, "osm") for _ in range(2)]
        osmd = [Dep() for _ in range(2)]
        rec = [kb.sb(st, [128, 4, 1], F32, "rec") for _ in range(2)]
        recd = [Dep() for _ in range(2)]
        kvin = [kb.sb(st, [128, 288], F32, "kvin") for _ in range(2)]
        kvind = [Dep() for _ in range(2)]
        kb.op("pool", lambda e: e.memset(V[:, :, 64:66], 1.0), [], [Vd])
        for sq in seqs:
            chunks = sq["kchunks"]
            NK = sum(n for _, n in chunks)
            coff = []
            c0 = 0
            for ci, (kvap, n) in enumerate(chunks):
                j = ci % 2
                kb.dma("sp" if ci % 2 == 0 else "pool", kvin[j][:n, :], kvap, writes=[kvind[j]])
                b = nextbank(g)
                pv = g.psum[b].rearrange("p (k t) -> p k t", k=4)
                for k, w in ((0, 128), (1, 128), (2, 32)):
                    kb.op("pe", lambda e, k=k, w=w: e.transpose(pv[:w, k, :n], kvin[j][:n, k * 128:k * 128 + w], g.ident_f[:n, :n]),
                          [kvind[j], g.ident_d], [g.pd[b]])
                copy_op(kb, "dve", ckvT[:, 0:2, c0:c0 + n], pv[:, 0:2, :n], [g.pd[b]], [ckvTd])
                copy_op(kb, "act", ckvT[0:32, 2, c0:c0 + n], pv[0:32, 2, :n], [g.pd[b]], [ckvTd])
                coff.append(c0)
                c0 += n
            for h in range(NH):
                qj = h % 2
                kb.dma("sp", QT[qj][:, :], sq["qt"](h), writes=[QTd[qj]])
                for bi, k0 in enumerate(range(0, NK, 512)):
                    kn = min(512, NK - k0)
                    b = 6 + bi % 2
                    mm(kb, g.psum[b][:, :kn], wk_b[:, 0, h * 128:(h + 1) * 128], ckvT[:, 0, k0:k0 + kn], True, False, [Wd, ckvTd], [g.pd[b]])
                    mm(kb, g.psum[b][:, :kn], wk_b[:, 1, h * 128:(h + 1) * 128], ckvT[:, 1, k0:k0 + kn], False, False, [Wd, ckvTd], [g.pd[b]])
                    mm(kb, g.psum[b][:, :kn], g.ident_b[0:32, :], ckvT[0:32, 2, k0:k0 + kn], False, True, [g.ident_d, ckvTd], [g.pd[b]])
                    copy_op(kb, ("dve", "pool")[bi % 2] if False else "dve", KT[:, k0:k0 + kn], g.psum[b][:, :kn], [g.pd[b]], [KTd])
                for gi, cg in enumerate(range(0, len(chunks), 8)):
                    cn = min(8, len(chunks) - cg)
                    b = 6 + gi % 2
                    for ci in range(cn):
                        n = chunks[cg + ci][1]
                        o = coff[cg + ci]
                        for k in range(2):
                            mm(kb, g.psum[b][:n, ci * 64:(ci + 1) * 64], ckvT[:, k, o:o + n], wv_b[:, k, h * 64:(h + 1) * 64], k == 0, k == 1,
                               [ckvTd, Wd], [g.pd[b]])
                    copy_op(kb, "dve", V[:, cg:cg + cn, 0:64], g.psum[b][:, :cn * 64].rearrange("p (c d) -> p c d", d=64), [g.pd[b]], [Vd])
                for qsb in range(2):
                    for ci, (kvap, n) in enumerate(chunks):
                        o = coff[ci]
                        sb0 = 2 + 2 * (ci % 2)
                        pj = ci % 2
                        for i in range(2):
                            mm(kb, g.psum[sb0 + i][:n, :], KT[:, o:o + n], QT[qj][:, qsb * 1024 + i * 512:qsb * 1024 + (i + 1) * 512], True, True,
                               [KTd, QTd[qj]], [g.pd[sb0 + i]])
                        act(kb, P[pj][:n, :].rearrange("p (a b) -> p a b", a=2), g.pall[:n, sb0:sb0 + 2, :], AF.Exp,
                            [g.pd[sb0], g.pd[sb0 + 1]], [Pd[pj]], scale=QK_SCALE)
                        for i in range(2):
                            mm(kb, g.psum[i][:65, :], V[:n, ci, 0:65], P[pj][:n, i * 512:(i + 1) * 512], ci == 0, ci == len(chunks) - 1,
                               [Vd, Pd[pj]], [g.pd[i]])
                    copy_op(kb, "dve", oT[:65, :].rearrange("p (a b) -> p a b", a=2), g.pall[:65, 0:2, :], [g.pd[0], g.pd[1]], [oTd])
                    for half in range(2):
                        b = 6 + half
                        oj = half
                        pv = g.psum[b][:, 0:4 * 65].rearrange("p (t c) -> p t c", c=65)
                        for t in range(4):
                            q0 = half * 512 + t * 128
                            kb.op("pe", lambda e, t=t, q0=q0: e.transpose(pv[:, t, :], oT[:65, q0:q0 + 128], g.ident_f[:65, :65]),
                                  [oTd, g.ident_d], [g.pd[b]])
                        kb.op("dve", lambda e: e.reciprocal(out=rec[oj][:, :, :], in_=pv[:, :, 64:65]), [g.pd[b]], [recd[oj]])
                        tt(kb, "dve", osm[oj][:, :, :], pv[:, :, 0:64], rec[oj][:, :, :].broadcast_to([128, 4, 64]), ALU.mult,
                           [g.pd[b], recd[oj]], [osmd[oj]])
                        kb.dma("pool", sq["o"](qsb, half, h), osm[oj][:, :, :], reads=[osmd[oj]])
        kb.barrier()


XC = 2124


def phase_conf(kb, g, seqs, w_conf, cols_ap):
    with ExitStack() as st:
        wb = kb.sb(st, [128, 8, 1024], BF16, "wconf")
        Wd = Dep()
        load_weight_bf16(kb, st, wb, Wd, w_conf, 8, 1024)
        cols = kb.sb(st, [128, 20 + 124], F32, "cols")
        cd = Dep()
        kb.dma("sp", cols[:], cols_ap, writes=[cd])
        Dg = kb.sb(st, [128, 4, 31, 128], BF16, "Dg")
        Dgd = Dep()
        for j in range(4):
            for k in range(31):
                ts(kb, ("dve", "pool")[k % 2], Dg[:, j, k, :], g.ident_f[:, :], cols[:, 20 + j * 31 + k:20 + j * 31 + k + 1], None, ALU.mult, None,
                   [g.ident_d, cd], [Dgd])
        xin = [kb.sb(st, [128, D], F32, "xin") for _ in range(2)]
        xind = [Dep() for _ in range(2)]
        xT = kb.sb(st, [128, 8, XC], BF16, "xT")
        xTd = Dep()
        hT = kb.sb(st, [128, 4, XC], BF16, "hT")
        hTd = Dep()
        mask = kb.sb(st, [128, XC], F32, "mask")
        maskd = Dep()
        sg = [kb.sb(st, [128, 512], F32, "sg") for _ in range(2)]
        sgd = [Dep() for _ in range(2)]
        cc2 = [kb.sb(st, [128, 4, 512], F32, "cc") for _ in range(2)]
        ccd2 = [Dep(), Dep()]
        cb2 = [kb.sb(st, [128, 4, 512], BF16, "cb") for _ in range(2)]
        cbd2 = [Dep(), Dep()]
        sq2 = [kb.sb(st, [128, 4, 512], BF16, "sq") for _ in range(2)]
        sqd2 = [Dep(), Dep()]
        mean2 = [kb.sb(st, [128, 512], F32, "mean") for _ in range(2)]
        rstd2 = [kb.sb(st, [128, 512], F32, "rstd") for _ in range(2)]
        std2 = [Dep(), Dep()]
        blk_i = 0
        yt = [kb.sb(st, [128, 512], F32, "yt") for _ in range(2)]
        ytd = [Dep() for _ in range(2)]
        yo = [kb.sb(st, [128, 512], BF16, "yo") for _ in range(2)]
        yod = [Dep() for _ in range(2)]
        for s_ in seqs:
            NC = s_.get("ncols", XC)
            kb.dma("sp", mask[:, :NC], s_["mask"].broadcast_to([128, NC]), writes=[maskd])
            for ti, t0 in enumerate(range(0, NC, 128)):
                n = min(128, NC - t0)
                j = ti % 2
                kb.dma("sp", xin[j][:n, :], s_["x"][t0:t0 + n, :], writes=[xind[j]])
                transpose_tile(kb, g, xin[j], xind[j], n, xT, xTd, t0)
            for bi, c0 in enumerate(range(0, NC, 512)):
                cn = min(512, NC - c0)
                for j in range(4):
                    ba, bg = nextbank(g), nextbank(g)
                    for k in range(8):
                        mm(kb, g.psum[ba][:, :cn], wb[:, k, j * 128:(j + 1) * 128], xT[:, k, c0:c0 + cn], k == 0, k == 7, [Wd, xTd], [g.pd[ba]])
                    for k in range(8):
                        mm(kb, g.psum[bg][:, :cn], wb[:, k, 512 + j * 128:512 + (j + 1) * 128], xT[:, k, c0:c0 + cn], k == 0, k == 7, [Wd, xTd], [g.pd[bg]])
                    q = j % 2
                    act(kb, sg[q][:, :cn], g.psum[bg][:, :cn], AF.Sigmoid, [g.pd[bg], cd], [sgd[q]], bias=cols[:, 4 + j:5 + j], scale=1.0)
                    stt(kb, "dve", sg[q][:, :cn], g.psum[ba][:, :cn], cols[:, j:j + 1], sg[q][:, :cn], ALU.add, ALU.mult, [g.pd[ba], cd, sgd[q]], [sgd[q]])
                    tt(kb, "dve", hT[:, j, c0:c0 + cn], sg[q][:, :cn], mask[:, c0:c0 + cn], ALU.mult, [sgd[q], maskd], [hTd])
            blocks = s_.get("blocks") or ([(15, 16)] + [(61 + 512 * i, 512) for i in range(4)])
            for bi, (c0, cn) in enumerate(blocks):
                pp = blk_i % 2
                blk_i += 1
                cc, ccd, cb, cbd, sq, sqd = cc2[pp], ccd2[pp], cb2[pp], cbd2[pp], sq2[pp], sqd2[pp]
                mean, rstd, std = mean2[pp], rstd2[pp], std2[pp]
                for j in range(4):
                    b = nextbank(g)
                    for k in range(31):
                        mm(kb, g.psum[b][:, :cn], Dg[:, j, k, :], hT[:, j, c0 + k - 15:c0 + k - 15 + cn], k == 0, k == 30, [Dgd, hTd], [g.pd[b]])
                    act(kb, cc[:, j, :cn], g.psum[b][:, :cn], AF.Identity, [g.pd[b], cd], [ccd], bias=cols[:, 8 + j:9 + j], scale=1.0)
                    copy_op(kb, "dve", cb[:, j, :cn], cc[:, j, :cn], [ccd], [cbd])
                    tt(kb, "dve", sq[:, j, :cn], cc[:, j, :cn], cc[:, j, :cn], ALU.mult, [ccd], [sqd])
                b1, b2 = nextbank(g), nextbank(g)
                for j in range(4):
                    mm(kb, g.psum[b1][:, :cn], g.ones_b[:, :], cb[:, j, :cn], j == 0, j == 3, [g.ones_d, cbd], [g.pd[b1]])
                for j in range(4):
                    mm(kb, g.psum[b2][:, :cn], g.ones_b[:, :], sq[:, j, :cn], j == 0, j == 3, [g.ones_d, sqd], [g.pd[b2]])
                act(kb, mean[:, :cn], g.psum[b1][:, :cn], AF.Copy, [g.pd[b1]], [std], scale=1.0 / 512)
                tt(kb, "pool", rstd[:, :cn], mean[:, :cn], mean[:, :cn], ALU.mult, [std], [std])
                stt(kb, "dve", rstd[:, :cn], g.psum[b2][:, :cn], 1.0 / 512, rstd[:, :cn], ALU.mult, ALU.subtract, [g.pd[b2], std], [std])
                act(kb, rstd[:, :cn], rstd[:, :cn], AF.Sqrt, [std], [std], bias=LN_EPS, scale=1.0)
                kb.op("dve", lambda e: e.reciprocal(out=rstd[:, :cn], in_=rstd[:, :cn]), [std], [std])
                for j in range(4):
                    q = j % 2
                    tt(kb, "dve", yt[q][:, :cn], cc[:, j, :cn], mean[:, :cn], ALU.subtract, [ccd, std], [ytd[q]])
                    tt(kb, "dve", yt[q][:, :cn], yt[q][:, :cn], rstd[:, :cn], ALU.mult, [ytd[q], std], [ytd[q]])
                    ts(kb, "dve", yt[q][:, :cn], yt[q][:, :cn], cols[:, 12 + j:13 + j], cols[:, 16 + j:17 + j], ALU.mult, ALU.add, [ytd[q], cd], [ytd[q]])
                    act(kb, yo[q][:, :cn], yt[q][:, :cn], AF.Silu, [ytd[q]], [yod[q]])
                    kb.dma("pool", s_["out"](j, bi, cn), yo[q][:, :cn], reads=[yod[q]])
        kb.barrier()


I32 = mybir.dt.int32
TWO_PI = 2.0 * math.pi


class FCfg:
    def __init__(self, L, rows, N1, nq, CB):
        self.L, self.rows, self.N1, self.nq, self.CB = L, rows, N1, nq, CB
        self.N2 = 86 * nq
        self.N = N1 * self.N2
        self.NF = N1 // 2 + 1
        assert self.N >= 2 * L - 1 and rows * self.N2 >= L


CFG_P = FCfg(16400, 64, 128, 3, 8)
CFG_S = FCfg(2064, 24, 48, 1, 32)


def fft_tables(cfg):
    N1, N2, N, rows, nq, NF = cfg.N1, cfg.N2, cfg.N, cfg.rows, cfg.nq, cfg.NF
    n1 = np.arange(rows)[:, None].astype(np.float64)
    k1 = np.arange(NF)[None, :].astype(np.float64)
    a = 2 * np.pi * n1 * k1 / N1
    F1 = np.concatenate([np.cos(a), -np.sin(a)], 1)
    n2 = np.arange(N2)[:, None].astype(np.float64)
    a = 2 * np.pi * n2 * k1 / N
    tw = np.stack([np.cos(a), -np.sin(a)], 1)
    tw = tw.reshape(nq, 86, 2, NF).transpose(1, 0, 2, 3)
    m = np.arange(N2)[None, :].astype(np.float64)
    a = 2 * np.pi * n2 * m / N2
    F2 = np.stack([np.cos(a), -np.sin(a), np.sin(a)], 0)
    F2 = F2.reshape(3, nq, 86, N2).transpose(2, 0, 1, 3)
    kk = np.arange(NF)[:, None].astype(np.float64)
    a = 2 * np.pi * kk * np.arange(N2)[None, :] / N
    twc = np.stack([np.cos(a), np.sin(a)], 1)
    a = 2 * np.pi * kk * np.arange(rows)[None, :] / N1
    wgt = np.full((NF, 1), 2.0)
    wgt[0, 0] = 1.0
    wgt[NF - 1, 0] = 1.0
    G1 = np.stack([wgt * np.cos(a) / N, -wgt * np.sin(a) / N], 1)
    bf = ml_dtypes.bfloat16
    return dict(F1=F1.astype(np.float32).astype(bf), tw=np.ascontiguousarray(tw).astype(np.float32),
                F2=np.ascontiguousarray(F2).astype(np.float32).astype(bf), twc=twc.astype(np.float32),
                G1=G1.astype(np.float32).astype(bf))


class FTab:
    pass


def fft_load_tables(kb, st, cfg, tabs):
    t = FTab()
    t.d = Dep()
    t.F1 = kb.sb(st, [cfg.rows, 2 * cfg.NF], BF16, "F1")
    t.tw = kb.sb(st, [86, cfg.nq, 2, cfg.NF], F32, "tw")
    t.F2 = kb.sb(st, [86, 3, cfg.nq, cfg.N2], BF16, "F2")
    t.twc = kb.sb(st, [cfg.NF, 2, cfg.N2], F32, "twc")
    t.G1 = kb.sb(st, [cfg.NF, 2, cfg.rows], BF16, "G1")
    for nm in ("F1", "tw", "F2", "twc", "G1"):
        kb.dma("sp", getattr(t, nm)[:], tabs[nm], writes=[t.d])
    return t


class FBuf:
    pass


def fft_alloc(kb, st, cfg, nsets=1):
    CB, nq, N1, N2, rows = cfg.CB, cfg.nq, cfg.NF, cfg.N2, cfg.rows
    E = CB * nq * N1
    E2 = CB * N2
    tn = max(E, E2)
    Ab = kb.sb(st, [86, CB * nq, 2, N1], BF16, "Ab")
    Abd = Dep()
    Xs = kb.sb(st, [86, CB * nq, 2, N1], F32, "Xs")
    Xsd = Dep()
    t = [kb.sb(st, [128, tn], F32, "ft") for _ in range(4)]
    td = [Dep() for _ in range(4)]
    sets = []
    for _ in range(nsets):
        b = FBuf()
        b.src_f = kb.sb(st, [rows, CB, N2], F32, "srcf")
        b.src_fd = Dep()
        b.src_b = kb.sb(st, [rows, CB, N2], BF16, "srcb")
        b.src_bd = Dep()
        b.As = kb.sb(st, [86, CB * nq, 2, N1], F32, "As")
        b.Asd = Dep()
        b.Ab, b.Abd, b.Xs, b.Xsd, b.t, b.td = Ab, Abd, Xs, Xsd, t, td
        sets.append(b)
    return sets if nsets > 1 else sets[0]


import os
CMUL_ENG = os.environ.get("CMUL_ENG", "dve,dve,dve,dve,dve,dve").split(",")


def cmul_batched(kb, cfg, b, P, shape, Are, Aim, Br, Bi, out_re, out_im, rdeps, wdep, conj=False):
    n = int(np.prod(shape))
    pat = {2: "p (a b) -> p a b", 3: "p (a b c) -> p a b c"}[len(shape)]
    kw = dict(zip("abc", shape))
    kw.pop("a")
    tv = [b.t[i][:P, :n].rearrange(pat, **kw) for i in range(4)]
    e = CMUL_ENG
    tt(kb, e[0], tv[0], Are, Br, ALU.mult, rdeps, [b.td[0]])
    tt(kb, e[1], tv[1], Aim, Bi, ALU.mult, rdeps, [b.td[1]])
    tt(kb, e[2], tv[2], Are, Bi, ALU.mult, rdeps, [b.td[2]])
    tt(kb, e[3], tv[3], Aim, Br, ALU.mult, rdeps, [b.td[3]])
    tt(kb, e[4], out_re, tv[0], tv[1], ALU.subtract, [b.td[0], b.td[1]], [wdep])
    tt(kb, e[5], out_im, tv[2], tv[3], ALU.add, [b.td[2], b.td[3]], [wdep])


def fft_s1(kb, g, cfg, tb, b, cb):
    nq, N1, N2, rows = cfg.nq, cfg.NF, cfg.N2, cfg.rows
    per = 512 // (2 * N1)
    tot = cb * nq
    for i0 in range(0, tot, per):
        cnt = min(per, tot - i0)
        bk = nextbank(g)
        for i in range(i0, i0 + cnt):
            c, q = divmod(i, nq)
            mm(kb, g.psum[bk][:86, (i - i0) * 2 * N1:(i - i0 + 1) * 2 * N1], b.src_b[:rows, c, q * 86:(q + 1) * 86], tb.F1[:rows, :], True, True,
               [b.src_bd, tb.d], [g.pd[bk]])
        copy_op(kb, "act", b.As[:, i0:i0 + cnt, :, :], g.psum[bk][:86, :cnt * 2 * N1].rearrange("p (i r k) -> p i r k", r=2, k=N1), [g.pd[bk]], [b.Asd])


def fft_s2(kb, g, cfg, tb, b, cb):
    nq, N1, N2, rows = cfg.nq, cfg.NF, cfg.N2, cfg.rows
    per = 512 // (2 * N1)
    tot = cb * nq
    Av = b.As[:, :tot, :, :].rearrange("p (c q) r k -> p c q r k", q=nq)
    Abv = b.Ab[:, :tot, :, :].rearrange("p (c q) r k -> p c q r k", q=nq)
    twr = tb.tw[:, :, 0, :].unsqueeze(1).broadcast_to([86, cb, nq, N1])
    twi = tb.tw[:, :, 1, :].unsqueeze(1).broadcast_to([86, cb, nq, N1])
    cmul_batched(kb, cfg, b, 86, (cb, nq, N1), Av[:, :, :, 0, :], Av[:, :, :, 1, :], twr, twi, Abv[:, :, :, 0, :], Abv[:, :, :, 1, :],
                 [b.Asd, tb.d], b.Abd)
    for i0 in range(0, tot, per):
        cnt = min(per, tot - i0)
        bk = nextbank(g)
        for i in range(i0, i0 + cnt):
            c, p = divmod(i, nq)
            reg = g.psum[bk][:86, (i - i0) * 2 * N1:(i - i0 + 1) * 2 * N1]
            for q in range(nq):
                blk = slice(p * 86, (p + 1) * 86)
                mm(kb, reg, tb.F2[:, 0, q, blk], b.Ab[:, c * nq + q, :, :].rearrange("p r k -> p (r k)"), q == 0, False, [tb.d, b.Abd], [g.pd[bk]])
                mm(kb, reg[:, 0:N1], tb.F2[:, 2, q, blk], b.Ab[:, c * nq + q, 1, :], False, False, [tb.d, b.Abd], [g.pd[bk]])
                mm(kb, reg[:, N1:2 * N1], tb.F2[:, 1, q, blk], b.Ab[:, c * nq + q, 0, :], False, q == nq - 1, [tb.d, b.Abd], [g.pd[bk]])
        copy_op(kb, "act", b.Xs[:, i0:i0 + cnt, :, :], g.psum[bk][:86, :cnt * 2 * N1].rearrange("p (i r k) -> p i r k", r=2, k=N1), [g.pd[bk]], [b.Xsd])


def fft_fwd(kb, g, cfg, tb, b, cb):
    fft_s1(kb, g, cfg, tb, b, cb)
    fft_s2(kb, g, cfg, tb, b, cb)


def pipeline2(items, stage_a, stage_b, depth=2):
    if depth < 2:
        for it in items:
            stage_a(it)
            stage_b(it)
        return
    prev = None
    for it in items:
        stage_a(it)
        if prev is not None:
            stage_b(prev)
        prev = it
    if prev is not None:
        stage_b(prev)


def fft_layout_dma(kb, q, cfg, tile, tiled, dram2d, c0, cb, to_sbuf):
    L, N2, rows = cfg.L, cfg.N2, cfg.rows
    full = L // N2
    rem = L - full * N2
    dv = dram2d[c0:c0 + cb, 0:full * N2].rearrange("c (a b) -> a c b", b=N2)
    if to_sbuf:
        kb.dma(q, tile[:full, :cb, :], dv, writes=[tiled])
        if rem:
            kb.dma(q, tile[full:full + 1, :cb, :rem], dram2d[c0:c0 + cb, full * N2:L].unsqueeze(0), writes=[tiled])
    else:
        kb.dma(q, dv, tile[:full, :cb, :], reads=[tiled])
        if rem:
            kb.dma(q, dram2d[c0:c0 + cb, full * N2:L].unsqueeze(0), tile[full:full + 1, :cb, :rem], reads=[tiled])


def phase_hy_conv(kb, g, cfg, tabs, taps, Hs, seqs, dskip):
    CB, nq, N1, N2, rows, L = cfg.CB, cfg.nq, cfg.NF, cfg.N2, cfg.rows, cfg.L
    with ExitStack() as st:
        tb = fft_load_tables(kb, st, cfg, tabs)
        bs = [fft_alloc(kb, st, cfg, nsets=1)]
        for b in bs:
            kb.op("pool", lambda e, b=b: e.memset(b.src_f[:, :, :], 0.0), [], [b.src_fd])
        X0 = kb.sb(st, [86, CB * nq, 2, N1], F32, "X0")
        X0d = Dep()
        Hb = [kb.sb(st, [86, CB * nq, 2, N1], F32, "Hb") for _ in range(len(bs))]
        Hbd = [Dep() for _ in range(len(bs))]
        items = [(c0, d, bs[i % len(bs)]) for i, (c0, d) in enumerate((c0, d) for c0 in range(0, 64, CB) for d in range(2))]

        def sp_a(it):
            c0, d, b = it
            fft_layout_dma(kb, "sp", cfg, b.src_f, b.src_fd, taps[d], c0, CB, True)
            copy_op(kb, "dve", b.src_b[:, :, :], b.src_f[:, :, :], [b.src_fd], [b.src_bd])
            fft_s1(kb, g, cfg, tb, b, CB)

        def sp_b(it):
            c0, d, b = it
            fft_s2(kb, g, cfg, tb, b, CB)
            if d == 0:
                copy_op(kb, "act", X0[:, :, :, :], b.Xs[:, :, :, :], [b.Xsd], [X0d])
            else:
                tt(kb, "dve", Hb[0][:, :, 0, :], X0[:, :, 0, :], b.Xs[:, :, 0, :], ALU.add, [X0d, b.Xsd], [Hbd[0]])
                tt(kb, "dve", Hb[0][:, :, 1, :], X0[:, :, 1, :], b.Xs[:, :, 1, :], ALU.subtract, [X0d, b.Xsd], [Hbd[0]])
                kb.dma("pool", Hs[:, c0 * nq:(c0 + CB) * nq, :, :], Hb[0][:, :, :, :], reads=[Hbd[0]])
        pipeline2(items, sp_a, sp_b, depth=len(bs))
        kb.barrier()
        Yb = kb.sb(st, [86, CB * nq, 2, N1], BF16, "Yb")
        Ybd = Dep()
        Bs = kb.sb(st, [N1, CB, 2, N2], F32, "Bs")
        Bsd = Dep()
        Bb = kb.sb(st, [N1, CB, 2, N2], BF16, "Bb")
        Bbd = Dep()
        x0f = [kb.sb(st, [rows, CB, N2], F32, "x0f") for _ in range(len(bs))]
        x0d = [Dep() for _ in range(len(bs))]
        cv = kb.sb(st, [rows, CB, N2], F32, "cv")
        cvd = Dep()
        yo = kb.sb(st, [rows, CB, N2], BF16, "yo")
        yod = Dep()
        dsk = kb.sb(st, [128, 64], F32, "dsk")
        dskd = Dep()
        kb.dma("sp", dsk[:, :], dskip.broadcast_to([128, 64]), writes=[dskd])
        perb = 512 // N2
        citems = [(sq, c0, i % len(bs)) for i, (sq, c0) in enumerate((sq, c0) for sq in seqs for c0 in range(0, 64, CB))]

        def cv_a(it):
            sq, c0, k = it
            b = bs[k]
            fft_layout_dma(kb, "sp", cfg, b.src_f, b.src_fd, sq["z"], c0, CB, True)
            fft_layout_dma(kb, "sp", cfg, x0f[k], x0d[k], sq["x0"], c0, CB, True)
            kb.dma("sp", Hb[k][:, :, :, :], Hs[:, c0 * nq:(c0 + CB) * nq, :, :], writes=[Hbd[k]])
            copy_op(kb, "dve", b.src_b[:, :, :], b.src_f[:, :, :], [b.src_fd], [b.src_bd])
            fft_s1(kb, g, cfg, tb, b, CB)

        def cv_b(it):
            sq, c0, k = it
            b = bs[k]
            fft_s2(kb, g, cfg, tb, b, CB)
            cmul_batched(kb, cfg, b, 86, (CB * nq, N1), b.Xs[:, :, 0, :], b.Xs[:, :, 1, :], Hb[k][:, :, 0, :], Hb[k][:, :, 1, :],
                         Yb[:, :, 0, :], Yb[:, :, 1, :], [b.Xsd, Hbd[k]], Ybd)
            tot = CB * 2
            for i0 in range(0, tot, perb):
                cnt = min(perb, tot - i0)
                bk = nextbank(g)
                for i in range(i0, i0 + cnt):
                    c, ri = divmod(i, 2)
                    reg = g.psum[bk][:N1, (i - i0) * N2:(i - i0 + 1) * N2]
                    for p in range(nq):
                        ya_re, ya_im = Yb[:, c * nq + p, 0, :], Yb[:, c * nq + p, 1, :]
                        if ri == 0:
                            mm(kb, reg, ya_re, tb.F2[:, 0, p, :], p == 0, False, [Ybd, tb.d], [g.pd[bk]])
                            mm(kb, reg, ya_im, tb.F2[:, 1, p, :], False, p == nq - 1, [Ybd, tb.d], [g.pd[bk]])
                        else:
                            mm(kb, reg, ya_re, tb.F2[:, 2, p, :], p == 0, False, [Ybd, tb.d], [g.pd[bk]])
                            mm(kb, reg, ya_im, tb.F2[:, 0, p, :], False, p == nq - 1, [Ybd, tb.d], [g.pd[bk]])
                copy_op(kb, "act", Bs[:, :, :, :].rearrange("p c r n -> p (c r) n")[:, i0:i0 + cnt, :],
                        g.psum[bk][:N1, :cnt * N2].rearrange("p (i n) -> p i n", n=N2), [g.pd[bk]], [Bsd])
            twr = tb.twc[:, 0, :].unsqueeze(1).broadcast_to([N1, CB, N2])
            twi = tb.twc[:, 1, :].unsqueeze(1).broadcast_to([N1, CB, N2])
            cmul_batched(kb, cfg, b, N1, (CB, N2), Bs[:, :, 0, :], Bs[:, :, 1, :], twr, twi, Bb[:, :, 0, :], Bb[:, :, 1, :], [Bsd, tb.d], Bbd)
            for i0 in range(0, CB, perb):
                cnt = min(perb, CB - i0)
                bk = nextbank(g)
                for c in range(i0, i0 + cnt):
                    reg = g.psum[bk][:rows, (c - i0) * N2:(c - i0 + 1) * N2]
                    mm(kb, reg, tb.G1[:, 0, :], Bb[:, c, 0, :], True, False, [tb.d, Bbd], [g.pd[bk]])
                    mm(kb, reg, tb.G1[:, 1, :], Bb[:, c, 1, :], False, True, [tb.d, Bbd], [g.pd[bk]])
                copy_op(kb, "act", cv[:, i0:i0 + cnt, :], g.psum[bk][:rows, :cnt * N2].rearrange("p (i n) -> p i n", n=N2), [g.pd[bk]], [cvd])
            tt(kb, "dve", b.src_f[:, :, :], b.src_f[:, :, :], dsk[:rows, c0:c0 + CB].unsqueeze(2).broadcast_to([rows, CB, N2]), ALU.mult,
               [b.src_fd, dskd], [b.src_fd])
            tt(kb, "dve", cv[:, :, :], cv[:, :, :], b.src_f[:, :, :], ALU.add, [cvd, b.src_fd], [cvd])
            tt(kb, "dve", yo[:, :, :], cv[:, :, :], x0f[k][:, :, :], ALU.mult, [cvd, x0d[k]], [yod])
            fft_layout_dma(kb, "pool", cfg, yo, yod, sq["ya"], c0, CB, False)
        pipeline2(citems, cv_a, cv_b, depth=len(bs))
        kb.barrier()


def phase_hy_inproj(kb, g, seqs, w_hy, brow, hcols, G=1):
    with ExitStack() as st:
        wb = kb.sb(st, [128, 8, G * 192], BF16, "why")
        Wd = Dep()
        load_weight_bf16(kb, st, wb, Wd, w_hy, 8, G * 192, stage_cols=1536)
        hc = kb.sb(st, [64, G * 12], F32, "hc")
        hcd = Dep()
        kb.dma("sp", hc[:, :], hcols, writes=[hcd])
        brf = kb.sb(st, [1, G * 192], F32, "brf")
        brb = kb.sb(st, [1, G * 192], BF16, "brb")
        brd = Dep()
        kb.dma("sp", brf[:, :], brow, writes=[brd])
        copy_op(kb, "dve", brb[:, :], brf[:, :], [brd], [brd])
        xin = [kb.sb(st, [128, D], F32, "xin") for _ in range(4)]
        xind = [Dep() for _ in range(4)]
        xT = [kb.sb(st, [128, 8, 512], BF16, "xT") for _ in range(2)]
        xTd = [Dep() for _ in range(2)]
        vf = [kb.sb(st, [1, 512], F32, "vf") for _ in range(2)]
        vb = [kb.sb(st, [1, 512], BF16, "vb") for _ in range(2)]
        vd = [Dep() for _ in range(2)]
        o3 = [[kb.sb(st, [64, 512], F32, "o3") for _ in range(3)] for _ in range(2)]
        o3d = [[Dep() for _ in range(3)] for _ in range(2)]
        bi = 0
        oi = 0
        for sq in seqs:
            L = sq["L"]
            for t0 in range(0, L, 510):
                no = min(510, L - t0)
                ni = no + 2
                j = bi % 2
                bi += 1
                for ti, r0 in enumerate(range(0, ni, 128)):
                    n = min(128, ni - r0)
                    kb.dma("sp", xin[ti][:n, :], sq["xh"][t0 + r0:t0 + r0 + n, :], writes=[xind[ti]])
                    transpose_tile(kb, g, xin[ti], xind[ti], n, xT[j], xTd[j], r0)
                kb.dma("sp", vf[j][:, :ni], sq["valid"][:, t0:t0 + ni], writes=[vd[j]])
                copy_op(kb, "dve", vb[j][:, :ni], vf[j][:, :ni], [vd[j]], [vd[j]])
                for gg in range(G):
                    oj = oi % 2
                    oi += 1
                    for gi in range(3):
                        c0 = gg * 192 + gi * 64
                        h0 = gg * 12 + gi * 4
                        bk = nextbank(g)
                        for k in range(8):
                            mm(kb, g.psum[bk][:64, :ni], wb[:, k, c0:c0 + 64], xT[j][:, k, :ni], k == 0, False, [Wd, xTd[j]], [g.pd[bk]])
                        mm(kb, g.psum[bk][:64, :ni], brb[:, c0:c0 + 64], vb[j][:, :ni], False, True, [brd, vd[j]], [g.pd[bk]])
                        o = o3[oj][gi]
                        od = o3d[oj][gi]
                        act(kb, o[:, :no], g.psum[bk][:64, 1:1 + no], AF.Identity, [g.pd[bk], hcd], [od],
                            scale=hc[:, h0 + 1:h0 + 2], bias=hc[:, h0 + 3:h0 + 4])
                        stt(kb, "dve", o[:, :no], g.psum[bk][:64, 0:no], hc[:, h0:h0 + 1], o[:, :no], ALU.mult, ALU.add, [g.pd[bk], hcd, od], [od])
                        stt(kb, "dve", o[:, :no], g.psum[bk][:64, 2:2 + no], hc[:, h0 + 2:h0 + 3], o[:, :no], ALU.mult, ALU.add, [g.pd[bk], hcd, od], [od])
                    tt(kb, "dve", o3[oj][1][:, :no], o3[oj][1][:, :no], o3[oj][2][:, :no], ALU.mult, [o3d[oj][1], o3d[oj][2]], [o3d[oj][1]])
                    kb.dma("pool", sq["x0"][gg][:, t0:t0 + no], o3[oj][0][:, :no], reads=[o3d[oj][0]])
                    kb.dma("pool", sq["z"][gg][:, t0:t0 + no], o3[oj][1][:, :no], reads=[o3d[oj][1]])
        kb.barrier()


def sin_reduced(kb, out, outd, src_ps, fcol, fbcol, tmps, tmpd, ki, kid, n, reads):
    a, r = tmps
    ts(kb, "dve", a[:, :n], src_ps, fcol, fbcol, ALU.mult, ALU.add, reads, [tmpd[0]])
    ts(kb, "dve", r[:, :n], a[:, :n], 1.0 / TWO_PI, None, ALU.mult, None, [tmpd[0]], [tmpd[1]])
    copy_op(kb, "dve", ki[:, :n], r[:, :n], [tmpd[1]], [kid])
    copy_op(kb, "dve", r[:, :n], ki[:, :n], [kid], [tmpd[1]])
    stt(kb, "dve", r[:, :n], r[:, :n], -TWO_PI, a[:, :n], ALU.mult, ALU.add, [tmpd[0], tmpd[1]], [tmpd[1]])
    ts(kb, "dve", r[:, :n], r[:, :n], -3.1415925, 3.1415925, ALU.max, ALU.min, [tmpd[1]], [tmpd[1]])
    return act(kb, out, r[:, :n], AF.Sin, [tmpd[1]], [outd])


def phase_hy_filters(kb, g, L, zposT, fw, taps_out):
    with ExitStack() as st:
        w1 = kb.sb(st, [33, 2, 64], F32, "fw1")
        w2 = kb.sb(st, [64, 2, 64], F32, "fw2")
        w3 = kb.sb(st, [64, 2, 64], F32, "fw3")
        fc = kb.sb(st, [64, 2, 8], F32, "fc")
        Wd = Dep()
        kb.dma("sp", w1[:, :, :], fw["w1"].rearrange("d e f -> e d f"), writes=[Wd])
        kb.dma("sp", w2[:, :, :], fw["w2"].rearrange("d e f -> e d f"), writes=[Wd])
        kb.dma("sp", w3[:, :, :], fw["w3"].rearrange("d e f -> e d f"), writes=[Wd])
        kb.dma("sp", fc[:, :, 0:5], fw["fcols"], writes=[Wd])
        tt(kb, "dve", fc[:, :, 5:6], fc[:, :, 0:1], fc[:, :, 1:2], ALU.mult, [Wd], [Wd])
        tt(kb, "dve", fc[:, :, 6:7], fc[:, :, 2:3], fc[:, :, 3:4], ALU.mult, [Wd], [Wd])
        ts(kb, "dve", fc[:, :, 7:8], fc[:, :, 4:5], -1.0, None, ALU.mult, None, [Wd], [Wd])
        taps = kb.sb(st, [64, 2, L], F32, "taps")
        tapsd = Dep()
        zp = [kb.sb(st, [33, 512], F32, "zp") for _ in range(2)]
        zpd = [Dep() for _ in range(2)]
        tb_ = [kb.sb(st, [64, 512], F32, "tbc") for _ in range(2)]
        tbd = [Dep() for _ in range(2)]
        tmps = [kb.sb(st, [64, 512], F32, "ftmp") for _ in range(2)]
        tmpd = [Dep(), Dep()]
        ki = kb.sb(st, [64, 512], I32, "ki")
        kid = Dep()
        h1 = kb.sb(st, [64, 512], F32, "h1")
        h1d = Dep()
        h2 = kb.sb(st, [64, 512], F32, "h2")
        h2d = Dep()
        ex = kb.sb(st, [64, 512], F32, "ex")
        exd = Dep()
        ss = kb.sb(st, [64, 2 * ((L + 511) // 512) + 4], F32, "ss")
        ssd = Dep()
        junk = kb.sb(st, [64, 512], F32, "fjunk")
        junkd = Dep()
        nb = (L + 511) // 512
        for bi, l0 in enumerate(range(0, L, 512)):
            n = min(512, L - l0)
            j = bi % 2
            kb.dma("sp", zp[j][:, :n], zposT[:, l0:l0 + n], writes=[zpd[j]])
            kb.dma("pool", tb_[j][:, :n], zposT[0:1, l0:l0 + n].broadcast_to([64, n]), writes=[tbd[j]])
            for d in range(2):
                bk = nextbank(g)
                mm(kb, g.psum[bk][:64, :n], w1[:, d, :], zp[j][:, :n], True, True, [Wd, zpd[j]], [g.pd[bk]])
                sin_reduced(kb, h1[:, :n], h1d, g.psum[bk][:64, :n], fc[:, d, 0:1], fc[:, d, 5:6], tmps, tmpd, ki, kid, n, [g.pd[bk], Wd])
                bk = nextbank(g)
                mm(kb, g.psum[bk][:64, :n], w2[:, d, :], h1[:, :n], True, True, [Wd, h1d], [g.pd[bk]])
                sin_reduced(kb, h2[:, :n], h2d, g.psum[bk][:64, :n], fc[:, d, 2:3], fc[:, d, 6:7], tmps, tmpd, ki, kid, n, [g.pd[bk], Wd])
                bk = nextbank(g)
                mm(kb, g.psum[bk][:64, :n], w3[:, d, :], h2[:, :n], True, True, [Wd, h2d], [g.pd[bk]])
                act(kb, ex[:, :n], tb_[j][:, :n], AF.Exp, [tbd[j], Wd], [exd], scale=fc[:, d, 7:8])
                tt(kb, "dve", taps[:, d, l0:l0 + n], g.psum[bk][:64, :n], ex[:, :n], ALU.mult, [g.pd[bk], exd], [tapsd])
                if d == 1 and l0 == 0:
                    kb.op("pool", lambda e: e.memset(taps[:, 1, 0:1], 0.0), [], [tapsd])
                act(kb, junk[:, :n], taps[:, d, l0:l0 + n], AF.Square, [tapsd], [junkd, ssd], accum_out=ss[:, 2 * bi + d:2 * bi + d + 1])
        tot, nrm = ss[:, 2 * nb:2 * nb + 1], ss[:, 2 * nb + 1:2 * nb + 2]
        kb.op("dve", lambda e: e.tensor_reduce(out=tot, in_=ss[:, 0:2 * nb], axis=AX.X, op=ALU.add), [ssd], [ssd])
        act(kb, nrm, tot, AF.Sqrt, [ssd], [ssd])
        kb.op("dve", lambda e: e.reciprocal(out=nrm, in_=nrm), [ssd], [ssd])
        for d in range(2):
            for l0 in range(0, L, 4096):
                n = min(4096, L - l0)
                ts(kb, ("dve", "pool")[d], taps[:, d, l0:l0 + n], taps[:, d, l0:l0 + n], nrm, None, ALU.mult, None, [tapsd, ssd], [tapsd])
            kb.dma("sp", taps_out[d], taps[:, d, :], reads=[tapsd])
        kb.barrier()


def phase_hy_filter_h2(kb, g, L, zposT, fw, h2_out):
    with ExitStack() as st:
        w1 = kb.sb(st, [33, 2, 64], F32, "fw1")
        w2 = kb.sb(st, [64, 2, 64], F32, "fw2")
        fc = kb.sb(st, [64, 2, 8], F32, "fc")
        Wd = Dep()
        kb.dma("sp", w1[:, :, :], fw["w1"].rearrange("d e f -> e d f"), writes=[Wd])
        kb.dma("sp", w2[:, :, :], fw["w2"].rearrange("d e f -> e d f"), writes=[Wd])
        kb.dma("sp", fc[:, :, 0:5], fw["fcols"], writes=[Wd])
        tt(kb, "dve", fc[:, :, 5:6], fc[:, :, 0:1], fc[:, :, 1:2], ALU.mult, [Wd], [Wd])
        tt(kb, "dve", fc[:, :, 6:7], fc[:, :, 2:3], fc[:, :, 3:4], ALU.mult, [Wd], [Wd])
        zp = [kb.sb(st, [33, 512], F32, "zp") for _ in range(2)]
        zpd = [Dep() for _ in range(2)]
        NQ = 3
        tmps = [[kb.sb(st, [64, 512], F32, "ftmp") for _ in range(2)] for _ in range(NQ)]
        tmpd = [[Dep(), Dep()] for _ in range(NQ)]
        ki = [kb.sb(st, [64, 512], I32, "ki") for _ in range(NQ)]
        kid = [Dep() for _ in range(NQ)]
        h1 = [kb.sb(st, [64, 512], F32, "h1") for _ in range(NQ)]
        h1d = [Dep() for _ in range(NQ)]
        h2 = [kb.sb(st, [64, 512], F32, "h2") for _ in range(NQ)]
        h2d = [Dep() for _ in range(NQ)]
        it = 0
        for bi, l0 in enumerate(range(0, L, 512)):
            n = min(512, L - l0)
            j = bi % 2
            kb.dma("sp", zp[j][:, :n], zposT[:, l0:l0 + n], writes=[zpd[j]])
            for d in range(2):
                q = it % NQ
                it += 1
                bk = nextbank(g)
                mm(kb, g.psum[bk][:64, :n], w1[:, d, :], zp[j][:, :n], True, True, [Wd, zpd[j]], [g.pd[bk]])
                sin_reduced(kb, h1[q][:, :n], h1d[q], g.psum[bk][:64, :n], fc[:, d, 0:1], fc[:, d, 5:6], tmps[q], tmpd[q], ki[q], kid[q], n, [g.pd[bk], Wd])
                bk = nextbank(g)
                mm(kb, g.psum[bk][:64, :n], w2[:, d, :], h1[q][:, :n], True, True, [Wd, h1d[q]], [g.pd[bk]])
                sin_reduced(kb, h2[q][:, :n], h2d[q], g.psum[bk][:64, :n], fc[:, d, 2:3], fc[:, d, 6:7], tmps[q], tmpd[q], ki[q], kid[q], n, [g.pd[bk], Wd])
                kb.dma("pool", h2_out[d][:, l0:l0 + n], h2[q][:, :n], reads=[h2d[q]])
        kb.barrier()


def phase_hy_filter_taps(kb, g, L, zposT, h2_in, w3_ap, fcols_ap, taps_out):
    with ExitStack() as st:
        w3 = kb.sb(st, [64, 2, 64], F32, "fw3")
        fc = kb.sb(st, [64, 2, 8], F32, "fc")
        Wd = Dep()
        kb.dma("sp", w3[:, :, :], w3_ap.rearrange("d e f -> e d f"), writes=[Wd])
        kb.dma("sp", fc[:, :, 0:5], fcols_ap, writes=[Wd])
        ts(kb, "dve", fc[:, :, 7:8], fc[:, :, 4:5], -1.0, None, ALU.mult, None, [Wd], [Wd])
        taps = kb.sb(st, [64, 2, L], F32, "taps")
        tapsd = [Dep(), Dep()]
        NQ = 3
        tb_ = [kb.sb(st, [64, 512], F32, "tbc") for _ in range(2)]
        tbd = [Dep() for _ in range(2)]
        hin = [kb.sb(st, [64, 512], F32, "h2in") for _ in range(NQ)]
        hind = [Dep() for _ in range(NQ)]
        ex = [kb.sb(st, [64, 512], F32, "ex") for _ in range(NQ)]
        exd = [Dep() for _ in range(NQ)]
        junk = [kb.sb(st, [64, 512], F32, "fjunk") for _ in range(2)]
        junkd = [Dep(), Dep()]
        nb = (L + 511) // 512
        ss = kb.sb(st, [64, 2 * nb + 4], F32, "ss")
        ssd = Dep()
        it = 0
        for bi, l0 in enumerate(range(0, L, 512)):
            n = min(512, L - l0)
            j = bi % 2
            kb.dma("pool", tb_[j][:, :n], zposT[0:1, l0:l0 + n].broadcast_to([64, n]), writes=[tbd[j]])
            for d in range(2):
                q = it % NQ
                it += 1
                kb.dma("sp", hin[q][:, :n], h2_in[d][:, l0:l0 + n], writes=[hind[q]])
                bk = nextbank(g)
                mm(kb, g.psum[bk][:64, :n], w3[:, d, :], hin[q][:, :n], True, True, [Wd, hind[q]], [g.pd[bk]])
                act(kb, ex[q][:, :n], tb_[j][:, :n], AF.Exp, [tbd[j], Wd], [exd[q]], scale=fc[:, d, 7:8])
                tt(kb, "dve", taps[:, d, l0:l0 + n], g.psum[bk][:64, :n], ex[q][:, :n], ALU.mult, [g.pd[bk], exd[q]], [tapsd[d]])
                if d == 1 and l0 == 0:
                    kb.op("pool", lambda e: e.memset(taps[:, 1, 0:1], 0.0), [], [tapsd[d]])
                act(kb, junk[d][:, :n], taps[:, d, l0:l0 + n], AF.Square, [tapsd[d]], [junkd[d], ssd], accum_out=ss[:, 2 * bi + d:2 * bi + d + 1])
        tot, nrm = ss[:, 2 * nb:2 * nb + 1], ss[:, 2 * nb + 1:2 * nb + 2]
        kb.op("dve", lambda e: e.tensor_reduce(out=tot, in_=ss[:, 0:2 * nb], axis=AX.X, op=ALU.add), [ssd], [ssd])
        act(kb, nrm, tot, AF.Sqrt, [ssd], [ssd])
        kb.op("dve", lambda e: e.reciprocal(out=nrm, in_=nrm), [ssd], [ssd])
        for d in range(2):
            for l0 in range(0, L, 4096):
                n = min(4096, L - l0)
                ts(kb, "dve", taps[:, d, l0:l0 + n], taps[:, d, l0:l0 + n], nrm, None, ALU.mult, None, [tapsd[d], ssd], [tapsd[d]])
            kb.dma(("sp", "pool")[d], taps_out[d], taps[:, d, :], reads=[tapsd[d]])
        kb.barrier()


LP, LS = 16400, 2064
NCORES = 8
BF = ml_dtypes.bfloat16


class Prog:
    def __init__(self):
        self.nc = bass.Bass("TRN2", target_bir_lowering=False)
        self.kb = KB(self.nc)
        self.ins = {}

    def din(self, name, shape, dt=F32):
        self.ins[name] = (tuple(shape), dt)
        return self.nc.dram_tensor(name, list(shape), dt, kind="ExternalInput").ap()

    def dout(self, name, shape, dt=F32):
        return self.nc.dram_tensor(name, list(shape), dt, kind="ExternalOutput").ap()

    def scr(self, name, shape, dt=F32):
        return self.nc.dram_tensor(name, list(shape), dt).ap()


def chunk_tiles():
    return [(t0, 128, 61 + t0) for t0 in range(0, 2048, 128)] + [(2048, 16, 15)]


def declare_tabs(P, cfg, pre):
    t = fft_tables(cfg)
    return {k: P.din(pre + k, v.shape, F32 if v.dtype == np.float32 else BF16) for k, v in t.items()}, {pre + k: v for k, v in t.items()}


def build_l1():
    P = Prog()
    kb = P.kb
    ident = P.din("ident", [128, 128])
    xh_p = P.din("xh_p", [LP + 2, D])
    xh_s = P.din("xh_s", [LS + 2, D])
    valid_p = P.din("valid_p", [1, LP + 2])
    valid_s = P.din("valid_s", [1, LS + 2])
    zpos_p = P.din("zpos_p", [33, LP])
    zpos_s = P.din("zpos_s", [33, LS])
    tabsP, _ = declare_tabs(P, CFG_P, "tp_")
    tabsS, _ = declare_tabs(P, CFG_S, "ts_")
    fw1 = P.din("fw1", [2, 33, 64])
    fw2 = P.din("fw2", [2, 64, 64])
    fw3 = P.din("fw3", [9, 2, 64, 64])
    fcols = P.din("fcols", [9, 64, 2, 5])
    why = P.din("why", [9, D, 192])
    brow = P.din("brow", [9, 1, 192])
    hcols = P.din("hcols", [9, 64, 12])
    dskip = P.din("dskip", [9, 1, 64])
    xc = P.din("xc", [2, XC, D])
    mask = P.din("mask", [2, 1, XC])
    wconf = P.din("wconf", [D, 1024])
    ccols = P.din("ccols", [128, 144])
    yaP = P.dout("yaP", [64, LP], BF16)
    yaS = P.dout("yaS", [8, 64, LS], BF16)
    ybT = P.dout("ybT", [2, 512, LS], BF16)
    taps_p = P.scr("taps_p", [2, 64, LP])
    Hs_p = P.scr("Hs_p", [86, 64 * CFG_P.nq, 2, CFG_P.N1])
    z_p = P.scr("z_p", [64, LP])
    x0_p = P.scr("x0_p", [64, LP])
    taps_s = P.scr("taps_s", [8, 2, 64, LS])
    Hs_s = P.scr("Hs_s", [8, 86, 64 * CFG_S.nq, 2, CFG_S.N1])
    z_s = P.scr("z_s", [8, 64, LS])
    x0_s = P.scr("x0_s", [8, 64, LS])
    with ExitStack() as st:
        g = setup_globals(kb, st)
        load_ident(kb, g, ident)
        fwd = lambda i: dict(w1=fw1, w2=fw2, w3=fw3[i], fcols=fcols[i])
        phase_hy_filters(kb, g, LP, zpos_p, fwd(0), taps_p)
        phase_hy_inproj(kb, g, [dict(xh=xh_p, valid=valid_p, L=LP, z=[z_p], x0=[x0_p])], why[0], brow[0], hcols[0])
        phase_hy_conv(kb, g, CFG_P, tabsP, taps_p, Hs_p, [dict(z=z_p, x0=x0_p, ya=yaP)], dskip[0])
        for gi in range(8):
            phase_hy_filters(kb, g, LS, zpos_s, fwd(1 + gi), taps_s[gi])
            phase_hy_inproj(kb, g, [dict(xh=xh_s, valid=valid_s, L=LS, z=[z_s[gi]], x0=[x0_s[gi]])], why[1 + gi], brow[1 + gi], hcols[1 + gi])
            phase_hy_conv(kb, g, CFG_S, tabsS, taps_s[gi], Hs_s[gi], [dict(z=z_s[gi], x0=x0_s[gi], ya=yaS[gi])], dskip[1 + gi])

        def outf(s_):
            def f(j, bi, cn):
                if bi == 0:
                    return ybT[s_, j * 128:(j + 1) * 128, 2048:2064]
                return ybT[s_, j * 128:(j + 1) * 128, (bi - 1) * 512:bi * 512]
            return f
        phase_conf(kb, g, [dict(x=xc[s_], mask=mask[s_], out=outf(s_)) for s_ in range(2)], wconf, ccols)
        kb.finish_wait()
    return P


def build_l2():
    P = Prog()
    kb = P.kb
    ident = P.din("ident", [128, 128])
    xc = P.din("xc", [2, XC, D])
    ycT = P.din("ycT", [2, D, LS], BF16)
    wout = P.din("wout", [D, D])
    bout = P.din("bout", [1, D])
    ln1g = P.din("ln1g", [1, D]); ln1b = P.din("ln1b", [1, D]); ln2g = P.din("ln2g", [1, D]); ln2b = P.din("ln2b", [1, D])
    w1 = P.din("w1", [D, DFF]); w2 = P.din("w2", [DFF, D])
    wqa = P.din("wqa", [D, 384]); qg = P.din("qg", [1, 384]); WqH = P.din("WqH", [384, NH * 128]); WqS = P.din("WqS", [384, NH * 32])
    wkva = P.din("wkva", [D, 288]); kvg = P.din("kvg", [1, 256])
    cs = P.din("cs", [2, LS, 32]); Cq = P.din("Cq", [2, 32, 2048]); Sq = P.din("Sq", [2, 32, 2048])
    h2 = P.dout("h2", [2, LS, D])
    kvlat = P.dout("kvlat", [2, LS, 288])
    QT = P.dout("QT", [2, NH, 128, 2048], BF16)
    h1 = P.scr("h1", [2, LS, D])
    tl = chunk_tiles()
    with ExitStack() as st:
        g = setup_globals(kb, st)
        load_ident(kb, g, ident)
        ycv = ycT.rearrange("s (k p) t -> s p k t", p=128)
        phase_proj_ln(kb, g, [(xc[s_, xr:xr + n, :], [(slice(0, 8), ycv[s_, :, :, t0:t0 + n], None)], h1[s_, t0:t0 + n, :], n) for s_ in range(2) for t0, n, xr in tl],
                      True, wout, bout, ln1g, ln1b)
        phase_mlp_ln(kb, g, [(h1[s_, t0:t0 + n, :], h2[s_, t0:t0 + n, :], n) for s_ in range(2) for t0, n, xr in tl], w1, w2, ln2g, ln2b)
        seqs = []
        for s_ in range(2):
            seqs.append(dict(tiles=[(h2[s_, t0:t0 + n, :], kvlat[s_, t0:t0 + n, :], cs[s_, t0:t0 + n, :], n) for t0, n, xr in tl],
                             CS=(Cq[s_], Sq[s_]), qt=(lambda s_: (lambda h, q0: QT[s_, h, :, q0:q0 + 512]))(s_)))
        phase_qkv(kb, g, seqs, wqa, qg, WqH, WqS, wkva, kvg)
        kb.finish_wait()
    return P


def build_l3():
    P = Prog()
    kb = P.kb
    ident = P.din("ident", [128, 128])
    h2 = P.din("h2", [2, LS, D])
    kvp = P.din("kvp", [LP, 288])
    kvs = P.din("kvs", [LS, 288])
    QT = P.din("QT", [2, NH, 128, 2048], BF16)
    WkH = P.din("WkH", [256, NH * 128]); WvH = P.din("WvH", [256, NH * 64])
    wo = P.din("wo", [D, D])
    ln1g = P.din("ln1g", [1, D]); ln1b = P.din("ln1b", [1, D]); ln2g = P.din("ln2g", [1, D]); ln2b = P.din("ln2b", [1, D])
    w1 = P.din("w1", [D, DFF]); w2 = P.din("w2", [DFF, D])
    out = P.dout("out", [2, 2048, D])
    otok = P.scr("otok", [2, 2048, D])
    h3 = P.scr("h3", [2, 2048, D])
    with ExitStack() as st:
        g = setup_globals(kb, st)
        load_ident(kb, g, ident)
        seqs = []
        for s_, kv, L in ((0, kvp, LP), (1, kvs, LS)):
            otv = otok[s_].rearrange("(a t p) (h c) -> a p t h c", p=128, t=4, c=64)
            seqs.append(dict(kchunks=[(kv[t0:min(t0 + 128, L), :], min(128, L - t0)) for t0 in range(0, L, 128)],
                             qt=(lambda s_: (lambda h: QT[s_, h, :, :]))(s_),
                             o=(lambda otv: (lambda qsb, half, h: otv[qsb * 2 + half, :, :, h, :]))(otv)))
        phase_attn(kb, g, seqs, WkH, WvH)
        tl2 = [(s_, t0) for s_ in range(2) for t0 in range(0, 2048, 128)]
        phase_proj_ln(kb, g, [(h2[s_, t0:t0 + 128, :], otok[s_, t0:t0 + 128, :], h3[s_, t0:t0 + 128, :], 128) for s_, t0 in tl2], False, wo, None, ln1g, ln1b)
        phase_mlp_ln(kb, g, [(h3[s_, t0:t0 + 128, :], out[s_, t0:t0 + 128, :], 128) for s_, t0 in tl2], w1, w2, ln2g, ln2b)
        kb.finish_wait()
    return P


def zpos_table(L):
    t = np.arange(L, dtype=np.float32) / max(L - 1, 1)
    freqs = np.linspace(1e-4, 15, 16, dtype=np.float32)
    w = (np.float32(2.0 * math.pi) * np.arange(L, dtype=np.float32) / np.float32(L)).astype(np.float32)
    ang = w[:, None] * freqs[None, :]
    return np.ascontiguousarray(np.concatenate([t[:, None], np.cos(ang), -np.sin(ang)], -1).T.astype(np.float32))


def rope_cs(pos):
    inv = (1.0 / (10000.0 ** (np.arange(0, 32, 2, dtype=np.float32) / 32))).astype(np.float32)
    ang = pos.astype(np.float32)[:, None] * inv[None, :]
    return np.cos(ang).astype(np.float32), np.sin(ang).astype(np.float32)


def make_xc(hfull, m0, L):
    x = np.zeros((XC, D), np.float32)
    mk = np.zeros((1, XC), np.float32)
    x[15:46] = hfull[0:31]
    mk[0, 15:46] = 1
    lo, hi = m0 - 15, min(m0 + 2048 + 15, L)
    x[46:46 + (hi - lo)] = hfull[lo:hi]
    mk[0, 46:46 + (hi - lo)] = 1
    return x, mk


def colpack(v):
    return np.ascontiguousarray(v.reshape(4, 128).T)


def check_inputs(P, im):
    for k, (shape, dt) in P.ins.items():
        assert k in im, k
        assert tuple(im[k].shape) == shape, (k, im[k].shape, shape)
    return {k: np.ascontiguousarray(im[k]) for k in P.ins}


def kernel_unfused(x_prompt, x_sample, meta_tokens, ev_w_in, ev_b_in, ev_short_w, ev_short_b,
           hy_w1, hy_b1, hy_freq1, hy_w2, hy_b2, hy_freq2, hy_w3, hy_decay, hy_skip_d,
           cf_dw_w, cf_dw_b, cf_ln_g, cf_ln_b, ev_w_out, ev_b_out,
           mla_wq_a, mla_q_norm, mla_wq_b, mla_wkv_a, mla_kv_norm, mla_wkv_b, mla_wo,
           ln1_g, ln1_b, mlp_w1, mlp_w2, ln2_g, ln2_b):
    f = lambda a: np.asarray(a, dtype=np.float32)
    x_prompt, x_sample, meta = f(x_prompt), f(x_sample), f(meta_tokens)
    win, bin_, sw, sb = f(ev_w_in)[0], f(ev_b_in)[0], f(ev_short_w)[0], f(ev_short_b)[0]
    ident = np.eye(128, dtype=np.float32)
    hp = np.concatenate([meta, x_prompt[0]], 0)
    hs = [np.concatenate([meta, x_sample[c]], 0) for c in range(8)]
    z1 = np.zeros((1, D), np.float32)
    xh_p = np.concatenate([z1, hp, z1], 0)
    valid_p = np.ones((1, LP + 2), np.float32); valid_p[0, 0] = 0; valid_p[0, -1] = 0
    valid_s = np.ones((1, LS + 2), np.float32); valid_s[0, 0] = 0; valid_s[0, -1] = 0
    tabP, tabS = fft_tables(CFG_P), fft_tables(CFG_S)
    def grp(gi):
        ch = slice(gi * 64, gi * 64 + 64)
        gcols = [np.arange(k * 512 + gi * 64, k * 512 + gi * 64 + 64) for k in range(3)]
        allc = np.concatenate(gcols)
        return dict(fw3=np.ascontiguousarray(f(hy_w3)[0][:, :, ch]),
                    fcols=np.ascontiguousarray(np.stack([f(hy_freq1)[0], f(hy_b1)[0], f(hy_freq2)[0], f(hy_b2)[0], f(hy_decay)[0][:, ch]], -1).transpose(1, 0, 2)),
                    why=np.ascontiguousarray(win[:, allc]), brow=bin_[allc][None, :].copy(),
                    hcols=np.concatenate([np.stack([sw[0, gc], sw[1, gc], sw[2, gc], sb[gc]], 1) for gc in gcols], 1).astype(np.float32),
                    dskip=f(hy_skip_d)[0][ch][None, :].copy())
    G = [grp(gi) for gi in range(8)]
    ccols = np.concatenate([colpack(bin_[1536:2048]), colpack(bin_[2048:2560]), colpack(f(cf_dw_b)[0]), colpack(f(cf_ln_g)[0]), colpack(f(cf_ln_b)[0]),
                            np.ascontiguousarray(f(cf_dw_w)[0].T.reshape(4, 128, 31).transpose(1, 0, 2).reshape(128, 124))], 1).astype(np.float32)
    xcs, masks = [], []
    for c in range(8):
        a, ma = make_xc(hp, 16 + 2048 * c, LP)
        b, mb = make_xc(hs[c], 16, LS)
        xcs.append(np.stack([a, b], 0))
        masks.append(np.stack([ma, mb], 0))
    P1 = build_l1()
    ims = []
    for c in range(8):
        order = [c] + list(range(8))
        im = dict(ident=ident, xh_p=xh_p, xh_s=np.concatenate([z1, hs[c], z1], 0), valid_p=valid_p, valid_s=valid_s,
                  zpos_p=zpos_table(LP), zpos_s=zpos_table(LS), fw1=f(hy_w1)[0], fw2=f(hy_w2)[0],
                  fw3=np.stack([G[i]["fw3"] for i in order], 0), fcols=np.stack([G[i]["fcols"] for i in order], 0).astype(np.float32),
                  why=np.stack([G[i]["why"] for i in order], 0), brow=np.stack([G[i]["brow"] for i in order], 0),
                  hcols=np.stack([G[i]["hcols"] for i in order], 0), dskip=np.stack([G[i]["dskip"] for i in order], 0),
                  xc=xcs[c], mask=masks[c], wconf=np.ascontiguousarray(win[:, 1536:2560]), ccols=ccols)
        for k, v in tabP.items():
            im["tp_" + k] = v
        for k, v in tabS.items():
            im["ts_" + k] = v
        ims.append(check_inputs(P1, im))
    r1 = run_bass_kernel_spmd(P1.nc, ims, core_ids=list(range(8))).results
    yaP_all = np.concatenate([np.asarray(r1[c]["yaP"]) for c in range(8)], 0)
    P2 = build_l2()
    wqb = f(mla_wq_b)[0].reshape(384, NH, 96)
    WqH = np.concatenate([wqb[:, :, 64:96], np.zeros((384, NH, 32), np.float32), wqb[:, :, 0:64]], -1).reshape(384, NH * 128)
    WqS = np.concatenate([wqb[:, :, 80:96], wqb[:, :, 64:80]], -1).reshape(384, NH * 32)
    wkvb = f(mla_wkv_b)[0].reshape(256, NH, 128)
    WkH = np.concatenate([np.zeros((256, NH, 64), np.float32), wkvb[:, :, 0:64]], -1).reshape(256, NH * 128)
    WvH = np.ascontiguousarray(wkvb[:, :, 64:128]).reshape(256, NH * 64)
    ims = []
    for c in range(8):
        m0 = 16 + 2048 * c
        ya_p = np.concatenate([yaP_all[:, m0:m0 + 2048], yaP_all[:, 0:16]], 1)
        ya_s = np.asarray(r1[c]["yaS"]).reshape(512, LS)
        ya_s = np.concatenate([ya_s[:, 16:], ya_s[:, 0:16]], 1)
        yb = np.asarray(r1[c]["ybT"])
        ycT = np.stack([np.concatenate([ya_p, yb[0]], 0), np.concatenate([ya_s, yb[1]], 0)], 0)
        css, Cqs, Sqs = [], [], []
        for pos in (np.concatenate([np.arange(m0, m0 + 2048), np.arange(16)]), np.concatenate([np.arange(16, LS), np.arange(16)])):
            co, si = rope_cs(pos)
            css.append(np.concatenate([co, si], 1))
            Cqs.append(np.concatenate([co[:2048].T, co[:2048].T], 0))
            Sqs.append(np.concatenate([-si[:2048].T, si[:2048].T], 0))
        im = dict(ident=ident, xc=xcs[c], ycT=ycT, wout=f(ev_w_out)[0], bout=f(ev_b_out)[0:1], ln1g=f(ln1_g)[0:1], ln1b=f(ln1_b)[0:1],
                  ln2g=f(ln2_g)[0:1], ln2b=f(ln2_b)[0:1], w1=f(mlp_w1)[0], w2=f(mlp_w2)[0], wqa=f(mla_wq_a)[0], qg=f(mla_q_norm)[0:1],
                  WqH=WqH, WqS=WqS, wkva=f(mla_wkv_a)[0], kvg=f(mla_kv_norm)[0:1], cs=np.stack(css, 0), Cq=np.stack(Cqs, 0), Sq=np.stack(Sqs, 0))
        ims.append(check_inputs(P2, im))
    r2 = run_bass_kernel_spmd(P2.nc, ims, core_ids=list(range(8))).results
    kvp = np.concatenate([np.asarray(r2[c]["kvlat"])[0, :2048] for c in range(8)] + [np.asarray(r2[0]["kvlat"])[0, 2048:]], 0)
    P3 = build_l3()
    ims = []
    for c in range(8):
        im = dict(ident=ident, h2=np.asarray(r2[c]["h2"]), kvp=kvp, kvs=np.asarray(r2[c]["kvlat"])[1], QT=np.asarray(r2[c]["QT"]), WkH=WkH, WvH=WvH,
                  wo=f(mla_wo)[0], ln1g=f(ln1_g)[1:2], ln1b=f(ln1_b)[1:2], ln2g=f(ln2_g)[1:2], ln2b=f(ln2_b)[1:2], w1=f(mlp_w1)[1], w2=f(mlp_w2)[1])
        ims.append(check_inputs(P3, im))
    r3 = run_bass_kernel_spmd(P3.nc, ims, core_ids=list(range(8))).results
    y_prompt = np.concatenate([np.asarray(r3[c]["out"])[0] for c in range(8)], 0)[None].astype(np.float32)
    y_sample = np.stack([np.asarray(r3[c]["out"])[1] for c in range(8)], 0).astype(np.float32)
    return (y_prompt, y_sample)


U32 = mybir.dt.uint32
YAW = 18432


def build_fused(stop=10 ** 9, trace_steps=None):
    P = Prog()
    step = [0]

    def run(fn, *a):
        if step[0] < stop:
            fn(*a)
        step[0] += 1

    kb = P.kb
    nc = P.nc
    ident = P.din("ident", [128, 128])
    xh_p = P.din("xh_p", [LP + 2, D]); xh_s = P.din("xh_s", [LS + 2, D])
    valid_p = P.din("valid_p", [1, LP + 2]); valid_s = P.din("valid_s", [1, LS + 2])
    zpos_p = P.din("zpos_p", [33, LP]); zpos_s = P.din("zpos_s", [33, LS])
    tabsP, _ = declare_tabs(P, CFG_P, "tp_")
    tabsS, _ = declare_tabs(P, CFG_S, "ts_")
    fw1 = P.din("fw1", [2, 33, 64]); fw2 = P.din("fw2", [2, 64, 64])
    fw3 = P.din("fw3", [9, 2, 64, 64]); fcols = P.din("fcols", [9, 64, 2, 5])
    why = P.din("why", [9, D, 192]); brow = P.din("brow", [9, 1, 192]); hcols = P.din("hcols", [9, 64, 12]); dskip = P.din("dskip", [9, 1, 64])
    xc = P.din("xc", [2, XC, D]); mask = P.din("mask", [2, 1, XC])
    wconf = P.din("wconf", [D, 1024]); ccols = P.din("ccols", [128, 144])
    gidx = P.din("gidx", [128, 4], U32)
    wout = P.din("wout", [D, D]); bout = P.din("bout", [1, D])
    ln1g = P.din("ln1g", [2, 1, D]); ln1b = P.din("ln1b", [2, 1, D]); ln2g = P.din("ln2g", [2, 1, D]); ln2b = P.din("ln2b", [2, 1, D])
    w1 = P.din("w1", [2, D, DFF]); w2 = P.din("w2", [2, DFF, D])
    wqa = P.din("wqa", [D, 384]); qg = P.din("qg", [1, 384]); WqH = P.din("WqH", [384, NH * 128]); WqS = P.din("WqS", [384, NH * 32])
    wkva = P.din("wkva", [D, 288]); kvg = P.din("kvg", [1, 256])
    cs = P.din("cs", [2, LS, 32]); Cq = P.din("Cq", [2, 32, 2048]); Sq = P.din("Sq", [2, 32, 2048])
    WkH = P.din("WkH", [256, NH * 128]); WvH = P.din("WvH", [256, NH * 64]); wo = P.din("wo", [D, D])
    out = P.dout("out", [2, 2048, D])
    yaP = P.scr("yaP", [64, YAW], BF16)
    yaP_all = P.scr("yaP_all", [512, YAW], BF16)
    yaS = P.scr("yaS", [8, 64, LS], BF16)
    ybT = P.scr("ybT", [2, 512, LS], BF16)
    taps_p = P.scr("taps_p", [2, 64, LP]); Hs_p = P.scr("Hs_p", [86, 64 * CFG_P.nq, 2, CFG_P.N1])
    z_p = P.scr("z_p", [64, LP]); x0_p = P.scr("x0_p", [64, LP])
    taps_s = P.scr("taps_s", [8, 2, 64, LS]); Hs_s = P.scr("Hs_s", [8, 86, 64 * CFG_S.nq, 2, CFG_S.N1])
    z_s = P.scr("z_s", [8, 64, LS]); x0_s = P.scr("x0_s", [8, 64, LS])
    h1 = P.scr("h1", [2, LS, D]); h2 = P.scr("h2", [2, LS, D])
    kvlat = P.scr("kvlat", [2, LS, 288]); kv_all = P.scr("kv_all", [8 * LS, 288])
    QT = P.scr("QT", [2, NH, 128, 2048], BF16)
    otok = P.scr("otok", [2, 2048, D]); h3 = P.scr("h3", [2, 2048, D])
    tl = chunk_tiles()
    with ExitStack() as st:
        g = setup_globals(kb, st)
        load_ident(kb, g, ident)
        fwd = lambda i: dict(w1=fw1, w2=fw2, w3=fw3[i], fcols=fcols[i])
        run(phase_hy_filters, kb, g, LP, zpos_p, fwd(0), taps_p)
        run(phase_hy_inproj, kb, g, [dict(xh=xh_p, valid=valid_p, L=LP, z=[z_p], x0=[x0_p])], why[0], brow[0], hcols[0])
        run(phase_hy_conv, kb, g, CFG_P, tabsP, taps_p, Hs_p, [dict(z=z_p, x0=x0_p, ya=yaP[:, 2032:2032 + LP])], dskip[0])
        agd = Dep()
        run(lambda: kb.all_gather(yaP, yaP_all, reads=[], writes=[agd]))
        kb.barrier()
        for gi in range(8):
            run(phase_hy_filters, kb, g, LS, zpos_s, fwd(1 + gi), taps_s[gi])
            run(phase_hy_inproj, kb, g, [dict(xh=xh_s, valid=valid_s, L=LS, z=[z_s[gi]], x0=[x0_s[gi]])], why[1 + gi], brow[1 + gi], hcols[1 + gi])
            run(phase_hy_conv, kb, g, CFG_S, tabsS, taps_s[gi], Hs_s[gi], [dict(z=z_s[gi], x0=x0_s[gi], ya=yaS[gi])], dskip[1 + gi])

        def outf(s_):
            def f(j, bi, cn):
                if bi == 0:
                    return ybT[s_, j * 128:(j + 1) * 128, 2048:2064]
                return ybT[s_, j * 128:(j + 1) * 128, (bi - 1) * 512:bi * 512]
            return f
        run(phase_conf, kb, g, [dict(x=xc[s_], mask=mask[s_], out=outf(s_)) for s_ in range(2)], wconf, ccols)
        with ExitStack() as st2:
            yaG = kb.sb(st2, [128, 4, 2048], BF16, "yaG")
            yaGd = Dep()
            ix = kb.sb(st2, [128, 4], U32, "gix")
            ixd = Dep()
            kb.dma("sp", ix[:, :], gidx[:, :], writes=[ixd])
            rows = yaP_all.rearrange("c (b t) -> (c b) t", t=2048)
            for k in range(4):
                run(lambda k=k: kb.gather_rows(yaG[:, k, :], rows[:, :], ix[:, k:k + 1], reads=[agd, ixd], writes=[yaGd]))
            ybv = ybT.rearrange("s (k p) t -> s p k t", p=128)
            yav_meta = yaP_all.rearrange("(k p) c -> p k c", p=128)
            yas = yaS.rearrange("g c t -> (g c) t").rearrange("(k p) t -> p k t", p=128)
            tiles = []
            for t0, n, xr in tl:
                if n == 128:
                    yl = [(slice(0, 4), yaG[:, :, t0:t0 + n], yaGd), (slice(4, 8), ybv[0, :, :, t0:t0 + n], None)]
                else:
                    yl = [(slice(0, 4), yav_meta[:, :, 2032:2048], agd), (slice(4, 8), ybv[0, :, :, 2048:2064], None)]
                tiles.append((xc[0, xr:xr + n, :], yl, h1[0, t0:t0 + n, :], n))
            for t0, n, xr in tl:
                tok0 = 16 + t0 if n == 128 else 0
                yl = [(slice(0, 4), yas[:, :, tok0:tok0 + n], None), (slice(4, 8), ybv[1, :, :, t0:t0 + n], None)]
                tiles.append((xc[1, xr:xr + n, :], yl, h1[1, t0:t0 + n, :], n))
            run(phase_proj_ln, kb, g, tiles, True, wout, bout, ln1g[0], ln1b[0])
        run(phase_mlp_ln, kb, g, [(h1[s_, t0:t0 + n, :], h2[s_, t0:t0 + n, :], n) for s_ in range(2) for t0, n, xr in tl], w1[0], w2[0], ln2g[0], ln2b[0])
        seqs = []
        for s_ in range(2):
            seqs.append(dict(tiles=[(h2[s_, t0:t0 + n, :], kvlat[s_, t0:t0 + n, :], cs[s_, t0:t0 + n, :], n) for t0, n, xr in tl],
                             CS=(Cq[s_], Sq[s_]), qt=(lambda s_: (lambda h, q0: QT[s_, h, :, q0:q0 + 512]))(s_)))
        run(phase_qkv, kb, g, seqs, wqa, qg, WqH, WqS, wkva, kvg)
        kvd = Dep()
        run(lambda: kb.all_gather(kvlat[0], kv_all, reads=[], writes=[kvd]))
        kb.barrier()
        seqs = []
        pch = [(kv_all[r * LS + t0:r * LS + t0 + 128, :], 128) for r in range(8) for t0 in range(0, 2048, 128)] + [(kv_all[2048:2064, :], 16)]
        sch = [(kvlat[1, t0:min(t0 + 128, LS), :], min(128, LS - t0)) for t0 in range(0, LS, 128)]
        for s_, ch in ((0, pch), (1, sch)):
            otv = otok[s_].rearrange("(a t p) (h c) -> a p t h c", p=128, t=4, c=64)
            seqs.append(dict(kchunks=ch, qt=(lambda s_: (lambda h: QT[s_, h, :, :]))(s_),
                             o=(lambda otv: (lambda qsb, half, h: otv[qsb * 2 + half, :, :, h, :]))(otv)))
        run(phase_attn, kb, g, seqs, WkH, WvH)
        tl2 = [(s_, t0) for s_ in range(2) for t0 in range(0, 2048, 128)]
        run(phase_proj_ln, kb, g, [(h2[s_, t0:t0 + 128, :], otok[s_, t0:t0 + 128, :], h3[s_, t0:t0 + 128, :], 128) for s_, t0 in tl2], False, wo, None, ln1g[1], ln1b[1])
        run(phase_mlp_ln, kb, g, [(h3[s_, t0:t0 + 128, :], out[s_, t0:t0 + 128, :], 128) for s_, t0 in tl2], w1[1], w2[1], ln2g[1], ln2b[1])
        kb.finish_wait()
    P.nsteps = step[0]
    return P


def build_nc():
    P = Prog()
    kb = P.kb
    ident = P.din("ident", [128, 128])
    xpad_p = P.din("xpad_p", [LP + 30, D]); maskpad = P.din("maskpad", [1, LP + 30])
    xh_s = P.din("xh_s", [LS + 2, D]); valid_s = P.din("valid_s", [1, LS + 2])
    xc_s = P.din("xc_s", [XC, D]); mask_s = P.din("mask_s", [1, XC])
    zpos_p = P.din("zpos_p", [33, LP]); zpos_s = P.din("zpos_s", [33, LS])
    tabsP, _ = declare_tabs(P, CFG_P, "tp_")
    tabsS, _ = declare_tabs(P, CFG_S, "ts_")
    fw1 = P.din("fw1", [2, 33, 64]); fw2 = P.din("fw2", [2, 64, 64])
    fw3 = P.din("fw3", [8, 2, 64, 64]); fcols = P.din("fcols", [8, 64, 2, 5])
    why = P.din("why", [D, 8 * 192]); brow = P.din("brow", [1, 8 * 192]); hcols = P.din("hcols", [64, 8 * 12]); dskip = P.din("dskip", [8, 1, 64])
    wconf = P.din("wconf", [D, 1024]); ccols = P.din("ccols", [128, 144])
    tokidx = P.din("tokidx", [128, 16], U32)
    wout = P.din("wout", [D, D]); bout = P.din("bout", [1, D])
    ln1g = P.din("ln1g", [2, 1, D]); ln1b = P.din("ln1b", [2, 1, D]); ln2g = P.din("ln2g", [2, 1, D]); ln2b = P.din("ln2b", [2, 1, D])
    w1 = P.din("w1", [2, D, DFF]); w2 = P.din("w2", [2, DFF, D])
    wqa = P.din("wqa", [D, 384]); qg = P.din("qg", [1, 384]); WqH = P.din("WqH", [384, NH * 128]); WqS = P.din("WqS", [384, NH * 32])
    wkva = P.din("wkva", [D, 288]); kvg = P.din("kvg", [1, 256])
    cs_all = P.din("cs_all", [LP, 32])
    cs = P.din("cs", [2, LS, 32]); Cq = P.din("Cq", [2, 32, 2048]); Sq = P.din("Sq", [2, 32, 2048])
    WkH = P.din("WkH", [256, NH * 128]); WvH = P.din("WvH", [256, NH * 64]); wo = P.din("wo", [D, D])
    out = P.dout("out", [2, 2048, D])
    yaP_all = P.scr("yaP_all", [512, YAW], BF16)
    yaS = P.scr("yaS", [8, 64, LS], BF16)
    ybT_p = P.scr("ybT_p", [512, LP], BF16); ybT_s = P.scr("ybT_s", [512, LS], BF16)
    h2f_p = P.scr("h2f_p", [2, 64, LP]); h2f_s = P.scr("h2f_s", [2, 64, LS])
    taps_p = P.scr("taps_p", [2, 64, LP]); Hs_p = P.scr("Hs_p", [86, 64 * CFG_P.nq, 2, CFG_P.NF])
    z_p = P.scr("z_p", [8, 64, LP]); x0_p = P.scr("x0_p", [8, 64, LP])
    taps_s = P.scr("taps_s", [2, 64, LS]); Hs_s = P.scr("Hs_s", [86, 64 * CFG_S.nq, 2, CFG_S.NF])
    z_s = P.scr("z_s", [8, 64, LS]); x0_s = P.scr("x0_s", [8, 64, LS])
    h1_all = P.scr("h1_all", [LP, D]); h2_all = P.scr("h2_all", [LP, D])
    h1_s = P.scr("h1_s", [LS, D]); h2_s = P.scr("h2_s", [LS, D]); h2_own = P.scr("h2_own", [LS, D])
    kv_all = P.scr("kv_all", [LP, 288]); kv_dummy = P.scr("kv_dummy", [LS, 288]); kvlat_s = P.scr("kvlat_s", [LS, 288])
    QT = P.scr("QT", [2, NH, 128, 2048], BF16)
    otok = P.scr("otok", [2, 2048, D]); h3 = P.scr("h3", [2, 2048, D])
    tl = chunk_tiles()
    with ExitStack() as st:
        g = setup_globals(kb, st)
        load_ident(kb, g, ident)
        fwd = lambda i: dict(w1=fw1, w2=fw2, w3=fw3[i], fcols=fcols[i])
        phase_hy_inproj(kb, g, [dict(xh=xpad_p[14:14 + LP + 2, :], valid=maskpad[:, 14:14 + LP + 2], L=LP,
                                     z=[z_p[gi] for gi in range(8)], x0=[x0_p[gi] for gi in range(8)])], why, brow, hcols, G=8)
        phase_hy_filter_h2(kb, g, LP, zpos_p, fwd(0), h2f_p)
        phase_hy_filter_h2(kb, g, LS, zpos_s, fwd(0), h2f_s)
        for gi in range(8):
            phase_hy_filter_taps(kb, g, LP, zpos_p, h2f_p, fw3[gi], fcols[gi], taps_p)
            phase_hy_conv(kb, g, CFG_P, tabsP, taps_p, Hs_p, [dict(z=z_p[gi], x0=x0_p[gi], ya=yaP_all[gi * 64:(gi + 1) * 64, 2032:2032 + LP])], dskip[gi])
        phase_hy_inproj(kb, g, [dict(xh=xh_s, valid=valid_s, L=LS, z=[z_s[gi] for gi in range(8)], x0=[x0_s[gi] for gi in range(8)])],
                        why, brow, hcols, G=8)
        for gi in range(8):
            phase_hy_filter_taps(kb, g, LS, zpos_s, h2f_s, fw3[gi], fcols[gi], taps_s)
            phase_hy_conv(kb, g, CFG_S, tabsS, taps_s, Hs_s, [dict(z=z_s[gi], x0=x0_s[gi], ya=yaS[gi])], dskip[gi])
        cseqs = []
        for j in range(8):
            r0 = 16 + 2048 * j
            cseqs.append(dict(x=xpad_p[r0:r0 + 2078, :], mask=maskpad[:, r0:r0 + 2078], ncols=2078, blocks=[(15 + 512 * i, 512) for i in range(4)],
                              out=(lambda j: (lambda jj, bi, cn: ybT_p[jj * 128:(jj + 1) * 128, 2048 * j + 512 * bi:2048 * j + 512 * bi + cn]))(j)))
        cseqs.append(dict(x=xpad_p[0:46, :], mask=maskpad[:, 0:46], ncols=46, blocks=[(15, 16)],
                          out=lambda jj, bi, cn: ybT_p[jj * 128:(jj + 1) * 128, 16384:16400]))

        def outf_s(jj, bi, cn):
            if bi == 0:
                return ybT_s[jj * 128:(jj + 1) * 128, 2048:2064]
            return ybT_s[jj * 128:(jj + 1) * 128, (bi - 1) * 512:bi * 512]
        cseqs.append(dict(x=xc_s, mask=mask_s, out=outf_s))
        phase_conf(kb, g, cseqs, wconf, ccols)
        yav = yaP_all.rearrange("(k p) c -> p k c", p=128)
        ybv_p = ybT_p.rearrange("(k p) t -> p k t", p=128)
        ybv_s = ybT_s.rearrange("(k p) t -> p k t", p=128)
        yas = yaS.rearrange("g c t -> (g c) t").rearrange("(k p) t -> p k t", p=128)
        tiles = []
        for j in range(8):
            for t0 in range(0, 2048, 128):
                tok = 16 + 2048 * j + t0
                gr = 2048 * j + t0
                tiles.append((xpad_p[15 + tok:15 + tok + 128, :],
                              [(slice(0, 4), yav[:, :, 2032 + tok:2032 + tok + 128], None), (slice(4, 8), ybv_p[:, :, gr:gr + 128], None)],
                              h1_all[gr:gr + 128, :], 128))
        tiles.append((xpad_p[15:31, :], [(slice(0, 4), yav[:, :, 2032:2048], None), (slice(4, 8), ybv_p[:, :, 16384:16400], None)],
                      h1_all[16384:16400, :], 16))
        for t0, n, xr in tl:
            tok0 = 16 + t0 if n == 128 else 0
            tiles.append((xc_s[xr:xr + n, :], [(slice(0, 4), yas[:, :, tok0:tok0 + n], None), (slice(4, 8), ybv_s[:, :, t0:t0 + n], None)],
                          h1_s[t0:t0 + n, :], n))
        phase_proj_ln(kb, g, tiles, True, wout, bout, ln1g[0], ln1b[0])
        ptl = [(r0, min(128, LP - r0)) for r0 in range(0, LP, 128)]
        phase_mlp_ln(kb, g, [(h1_all[r0:r0 + n, :], h2_all[r0:r0 + n, :], n) for r0, n in ptl] +
                     [(h1_s[t0:t0 + n, :], h2_s[t0:t0 + n, :], n) for t0, n, xr in tl], w1[0], w2[0], ln2g[0], ln2b[0])
        with ExitStack() as st2:
            ix = kb.sb(st2, [128, 16], U32, "tokix")
            ixd = Dep()
            kb.dma("sp", ix[:, :], tokidx[:, :], writes=[ixd])
            gb = [kb.sb(st2, [128, D], F32, "gb") for _ in range(2)]
            gd = [Dep(), Dep()]
            for i in range(16):
                j = i % 2
                kb.gather_rows(gb[j][:, :], h2_all[:, :], ix[:, i:i + 1], reads=[ixd], writes=[gd[j]])
                kb.dma("sp", h2_own[128 * i:128 * i + 128, :], gb[j][:, :], reads=[gd[j]])
            kb.dma("sp", h2_own[2048:2064, :], h2_all[16384:16400, :])
            kb.barrier()
        seqs = [dict(tiles=[(h2_all[r0:r0 + n, :], kv_all[r0:r0 + n, :], cs_all[r0:r0 + n, :], n) for r0, n in ptl], kv_only=True),
                dict(tiles=[(h2_own[t0:t0 + n, :], kv_dummy[t0:t0 + n, :], cs[0, t0:t0 + n, :], n) for t0, n, xr in tl],
                     CS=(Cq[0], Sq[0]), qt=lambda h, q0: QT[0, h, :, q0:q0 + 512]),
                dict(tiles=[(h2_s[t0:t0 + n, :], kvlat_s[t0:t0 + n, :], cs[1, t0:t0 + n, :], n) for t0, n, xr in tl],
                     CS=(Cq[1], Sq[1]), qt=lambda h, q0: QT[1, h, :, q0:q0 + 512])]
        phase_qkv(kb, g, seqs, wqa, qg, WqH, WqS, wkva, kvg)
        aseqs = []
        for s_, ch in ((0, [(kv_all[r0:r0 + n, :], n) for r0, n in ptl]),
                       (1, [(kvlat_s[t0:min(t0 + 128, LS), :], min(128, LS - t0)) for t0 in range(0, LS, 128)])):
            otv = otok[s_].rearrange("(a t p) (h c) -> a p t h c", p=128, t=4, c=64)
            aseqs.append(dict(kchunks=ch, qt=(lambda s_: (lambda h: QT[s_, h, :, :]))(s_),
                              o=(lambda otv: (lambda qsb, half, h: otv[qsb * 2 + half, :, :, h, :]))(otv)))
        phase_attn(kb, g, aseqs, WkH, WvH)
        hres = (h2_own, h2_s)
        tl2 = [(s_, t0) for s_ in range(2) for t0 in range(0, 2048, 128)]
        phase_proj_ln(kb, g, [(hres[s_][t0:t0 + 128, :], otok[s_, t0:t0 + 128, :], h3[s_, t0:t0 + 128, :], 128) for s_, t0 in tl2], False, wo, None, ln1g[1], ln1b[1])
        phase_mlp_ln(kb, g, [(h3[s_, t0:t0 + 128, :], out[s_, t0:t0 + 128, :], 128) for s_, t0 in tl2], w1[1], w2[1], ln2g[1], ln2b[1])
        kb.finish_wait()
    return P


def kernel(x_prompt, x_sample, meta_tokens, ev_w_in, ev_b_in, ev_short_w, ev_short_b,
           hy_w1, hy_b1, hy_freq1, hy_w2, hy_b2, hy_freq2, hy_w3, hy_decay, hy_skip_d,
           cf_dw_w, cf_dw_b, cf_ln_g, cf_ln_b, ev_w_out, ev_b_out,
           mla_wq_a, mla_q_norm, mla_wq_b, mla_wkv_a, mla_kv_norm, mla_wkv_b, mla_wo,
           ln1_g, ln1_b, mlp_w1, mlp_w2, ln2_g, ln2_b):
    f = lambda a: np.asarray(a, dtype=np.float32)
    x_prompt, x_sample, meta = f(x_prompt), f(x_sample), f(meta_tokens)
    win, bin_, sw, sb = f(ev_w_in)[0], f(ev_b_in)[0], f(ev_short_w)[0], f(ev_short_b)[0]
    ident = np.eye(128, dtype=np.float32)
    hp = np.concatenate([meta, x_prompt[0]], 0)
    hs = [np.concatenate([meta, x_sample[c]], 0) for c in range(8)]
    z1 = np.zeros((1, D), np.float32)
    z15 = np.zeros((15, D), np.float32)
    xpad_p = np.concatenate([z15, hp, z15], 0)
    maskpad = np.zeros((1, LP + 30), np.float32); maskpad[0, 15:15 + LP] = 1
    valid_s = np.ones((1, LS + 2), np.float32); valid_s[0, 0] = 0; valid_s[0, -1] = 0
    tabP, tabS = fft_tables(CFG_P), fft_tables(CFG_S)
    gcols = [[np.arange(k * 512 + gi * 64, k * 512 + gi * 64 + 64) for k in range(3)] for gi in range(8)]
    allc = np.concatenate([np.concatenate(gc) for gc in gcols])
    why = np.ascontiguousarray(win[:, allc])
    brow = bin_[allc][None, :].copy()
    hcols = np.concatenate([np.stack([sw[0, c_], sw[1, c_], sw[2, c_], sb[c_]], 1) for gc in gcols for c_ in gc], 1).astype(np.float32)
    fw3 = np.stack([np.ascontiguousarray(f(hy_w3)[0][:, :, gi * 64:gi * 64 + 64]) for gi in range(8)], 0)
    fcols = np.stack([np.stack([f(hy_freq1)[0], f(hy_b1)[0], f(hy_freq2)[0], f(hy_b2)[0], f(hy_decay)[0][:, gi * 64:gi * 64 + 64]], -1).transpose(1, 0, 2)
                      for gi in range(8)], 0).astype(np.float32)
    dskip = np.stack([f(hy_skip_d)[0][gi * 64:gi * 64 + 64][None, :] for gi in range(8)], 0)
    ccols = np.concatenate([colpack(bin_[1536:2048]), colpack(bin_[2048:2560]), colpack(f(cf_dw_b)[0]), colpack(f(cf_ln_g)[0]), colpack(f(cf_ln_b)[0]),
                            np.ascontiguousarray(f(cf_dw_w)[0].T.reshape(4, 128, 31).transpose(1, 0, 2).reshape(128, 124))], 1).astype(np.float32)
    wqb = f(mla_wq_b)[0].reshape(384, NH, 96)
    WqH = np.concatenate([wqb[:, :, 64:96], np.zeros((384, NH, 32), np.float32), wqb[:, :, 0:64]], -1).reshape(384, NH * 128)
    WqS = np.concatenate([wqb[:, :, 80:96], wqb[:, :, 64:80]], -1).reshape(384, NH * 32)
    wkvb = f(mla_wkv_b)[0].reshape(256, NH, 128)
    WkH = np.concatenate([np.zeros((256, NH, 64), np.float32), wkvb[:, :, 0:64]], -1).reshape(256, NH * 128)
    WvH = np.ascontiguousarray(wkvb[:, :, 64:128]).reshape(256, NH * 64)
    zp_p, zp_s = zpos_table(LP), zpos_table(LS)
    co, si = rope_cs(np.concatenate([np.arange(16, LP), np.arange(16)]))
    cs_all = np.concatenate([co, si], 1)
    shared = dict(ident=ident, xpad_p=xpad_p, maskpad=maskpad, valid_s=valid_s, zpos_p=zp_p, zpos_s=zp_s, fw1=f(hy_w1)[0], fw2=f(hy_w2)[0],
                  fw3=fw3, fcols=fcols, why=why, brow=brow, hcols=hcols, dskip=dskip, wconf=np.ascontiguousarray(win[:, 1536:2560]), ccols=ccols,
                  wout=f(ev_w_out)[0], bout=f(ev_b_out)[0:1], ln1g=f(ln1_g)[:, None, :], ln1b=f(ln1_b)[:, None, :],
                  ln2g=f(ln2_g)[:, None, :], ln2b=f(ln2_b)[:, None, :], w1=f(mlp_w1), w2=f(mlp_w2), wqa=f(mla_wq_a)[0], qg=f(mla_q_norm)[0:1],
                  WqH=WqH, WqS=WqS, wkva=f(mla_wkv_a)[0], kvg=f(mla_kv_norm)[0:1], cs_all=cs_all, WkH=WkH, WvH=WvH, wo=f(mla_wo)[0])
    for k, v in tabP.items():
        shared["tp_" + k] = v
    for k, v in tabS.items():
        shared["ts_" + k] = v
    P = build_nc()
    ims = []
    for c in range(8):
        m0 = 16 + 2048 * c
        xb, mb = make_xc(hs[c], 16, LS)
        css, Cqs, Sqs = [], [], []
        for pos in (np.concatenate([np.arange(m0, m0 + 2048), np.arange(16)]), np.concatenate([np.arange(16, LS), np.arange(16)])):
            co, si = rope_cs(pos)
            css.append(np.concatenate([co, si], 1))
            Cqs.append(np.concatenate([co[:2048].T, co[:2048].T], 0))
            Sqs.append(np.concatenate([-si[:2048].T, si[:2048].T], 0))
        tix = (2048 * c + 128 * np.arange(16)[None, :] + np.arange(128)[:, None]).astype(np.uint32)
        im = dict(shared)
        im.update(xh_s=np.concatenate([z1, hs[c], z1], 0), xc_s=xb, mask_s=mb, tokidx=tix,
                  cs=np.stack(css, 0), Cq=np.stack(Cqs, 0), Sq=np.stack(Sqs, 0))
        ims.append(check_inputs(P, im))
    r = run_bass_kernel_spmd(P.nc, ims, core_ids=list(range(8))).results
    y_prompt = np.concatenate([np.asarray(r[c]["out"])[0] for c in range(8)], 0)[None].astype(np.float32)
    y_sample = np.stack([np.asarray(r[c]["out"])[1] for c in range(8)], 0).astype(np.float32)
    return (y_prompt, y_sample)
```
